# Optimizing a Trainium2 kernel written in Bass

```python
import math
import jax, jax.numpy as jnp
from jax import lax
import numpy as np

D_MODEL = 1024
BATCH = 2
SEQ = 8192
DEPTH = 4

N_A = DEPTH // 2
N_B = DEPTH - N_A
HEAD_DIM = 64
MIX_WIDTH = D_MODEL
MEM_HEADS = 4
MEM_WIDTH = MEM_HEADS * HEAD_DIM
MEM_LEN = 256
CONV_CH = MIX_WIDTH - MEM_WIDTH
CONV_K = 3
NSA_HEADS = CONV_CH // HEAD_DIM
NSA_KV_GROUPS = 2
NSA_HPG = NSA_HEADS // NSA_KV_GROUPS
N_BRANCH = 3
NSA_Q_WIDTH = NSA_HEADS * HEAD_DIM
NSA_GATE_WIDTH = NSA_HEADS * N_BRANCH
CMP_LEN = 32
CMP_STRIDE = 16
CMP_HIDDEN = 256
SEL_BLOCK = 64
SEL_TOPK = 16
N_LOCAL_SEL = 2
WINDOW = 512
Q_BLOCK = 128
REL_BUCKETS = 32
REL_MAX_DIST = 1024
FF_DIM = 4 * D_MODEL
EPS = 1e-6
NEG = -1e30
FORCE_SCORE = 1e4

kernel_name = "yoco_shortconv_nsa_hybrid"


def rmsnorm(x, g):
    xf = x.astype(jnp.float32)
    y = xf * lax.rsqrt(jnp.mean(xf * xf, axis=-1, keepdims=True) + EPS)
    return (y * g.astype(jnp.float32)).astype(x.dtype)


def rel_bucket(dist):
    n = jnp.maximum(dist, 0)
    max_exact = REL_BUCKETS // 2
    nf = jnp.maximum(n, 1).astype(jnp.float32)
    large = max_exact + (jnp.log(nf / max_exact) / math.log(REL_MAX_DIST / max_exact)
                         * (REL_BUCKETS - max_exact)).astype(jnp.int32)
    large = jnp.minimum(large, REL_BUCKETS - 1)
    return jnp.where(n < max_exact, n, large)


def masked_softmax(logits, mask):
    p = jax.nn.softmax(jnp.where(mask, logits, NEG), axis=-1)
    return jnp.where(mask, p, 0.0)


def selection_overlap(n_cmp, n_sel):
    c0 = np.arange(n_cmp)[:, None] * CMP_STRIDE
    s0 = np.arange(n_sel)[None, :] * SEL_BLOCK
    ov = np.minimum(c0 + CMP_LEN, s0 + SEL_BLOCK) - np.maximum(c0, s0)
    return np.clip(ov, 0, None).astype(np.float32) / CMP_LEN


def short_conv_mixer(z, conv_w):
    gate_b, gate_c, u = jnp.split(z, 3, axis=-1)
    v = gate_c * u
    y = lax.conv_general_dilated(
        v, conv_w[:, None, :].astype(v.dtype), window_strides=(1,),
        padding=[(CONV_K - 1, 0)], dimension_numbers=('NWC', 'WIO', 'NWC'),
        feature_group_count=CONV_CH)
    return gate_b * y


def memory_attention(q_raw, mem_k, mem_v, q_gain):
    b, s, _ = q_raw.shape
    q = rmsnorm(q_raw.reshape(b, s, MEM_HEADS, HEAD_DIM), q_gain) * (HEAD_DIM ** -0.5)
    lg = jnp.einsum('bshd,bmhd->bhsm', q, mem_k).astype(jnp.float32)
    p = jax.nn.softmax(lg, axis=-1).astype(mem_v.dtype)
    return jnp.einsum('bhsm,bmhd->bshd', p, mem_v).reshape(b, s, MEM_WIDTH)


def compress(u, pos, w1, w2):
    b, s, g, d = u.shape
    ch = u.reshape(b, s // CMP_STRIDE, CMP_STRIDE, g, d)
    blk = jnp.concatenate([ch[:, :-1], ch[:, 1:]], axis=2) + pos[:, None, :]
    flat = blk.transpose(0, 1, 3, 2, 4).reshape(b, -1, g, CMP_LEN * d)
    return jax.nn.gelu(flat @ w1) @ w2


def build_shared_kv(x, kv_norm, w_kv_shared, k_gain, cmp_pos, cmp_w1, cmp_w2):
    b, s, _ = x.shape
    h = rmsnorm(x, kv_norm)
    kv = (h @ w_kv_shared).reshape(b, s, 6, NSA_KV_GROUPS, HEAD_DIM)
    kc = rmsnorm(compress(kv[:, :, 0], cmp_pos[0], cmp_w1[0], cmp_w2[0]), k_gain[0])
    vc = compress(kv[:, :, 1], cmp_pos[1], cmp_w1[1], cmp_w2[1])
    n_sel = s // SEL_BLOCK
    def to_blocks(u):
        return u.reshape(b, n_sel, SEL_BLOCK, NSA_KV_GROUPS, HEAD_DIM).transpose(0, 3, 1, 2, 4)
    ks = to_blocks(rmsnorm(kv[:, :, 2], k_gain[1]))
    vs = to_blocks(kv[:, :, 3])
    def pad_front(u):
        return jnp.pad(u, ((0, 0), (WINDOW, 0), (0, 0), (0, 0)))
    kw = pad_front(rmsnorm(kv[:, :, 4], k_gain[2]))
    vw = pad_front(kv[:, :, 5])
    return (kc, vc, ks, vs, kw, vw)


def nsa_mixer(q_raw, gate_logits, q_gain, kc, vc, ks, vs, kw, vw, rel_bias):
    b, s, _ = q_raw.shape
    G, Hg, hd = NSA_KV_GROUPS, NSA_HPG, HEAD_DIM
    q = rmsnorm(q_raw.reshape(b, s, G, Hg, hd), q_gain) * (hd ** -0.5)
    gates = jax.nn.sigmoid(gate_logits.reshape(b, s, G, Hg, N_BRANCH))
    n_cmp = kc.shape[1]
    n_sel = ks.shape[2]
    topk = min(SEL_TOPK, n_sel)
    overlap = jnp.asarray(selection_overlap(n_cmp, n_sel), dtype=jnp.float32)
    cmp_end = jnp.arange(n_cmp) * CMP_STRIDE + CMP_LEN - 1
    sel_idx = jnp.arange(n_sel)
    table = rel_bias.astype(jnp.float32)
    table_g = table.reshape(REL_BUCKETS, G, Hg).transpose(1, 0, 2)
    bi = jnp.arange(b)[:, None, None, None]
    gi = jnp.arange(G)[None, :, None, None]
    kw_len = WINDOW + Q_BLOCK

    def block(i):
        s0 = i * Q_BLOCK
        tq = s0 + jnp.arange(Q_BLOCK)
        qb = lax.dynamic_slice_in_dim(q, s0, Q_BLOCK, axis=1)
        gb = lax.dynamic_slice_in_dim(gates, s0, Q_BLOCK, axis=1)
        d_c = tq[:, None] - cmp_end[None, :]
        bias_c = table[rel_bucket(d_c)].reshape(Q_BLOCK, n_cmp, G, Hg).transpose(2, 3, 0, 1)
        lg = jnp.einsum('bqghd,bcgd->bghqc', qb, kc).astype(jnp.float32) + bias_c
        p_c = masked_softmax(lg, d_c >= 0)
        o_c = jnp.einsum('bghqc,bcgd->bqghd', p_c.astype(vc.dtype), vc)
        imp = jnp.einsum('bghqc,cj->bgqj', p_c, overlap)
        back = (tq // SEL_BLOCK)[:, None] - sel_idx[None, :]
        forced = (sel_idx[None, :] == 0) | ((back >= 0) & (back < N_LOCAL_SEL))
        score = jnp.where(back >= 0, jnp.where(forced, FORCE_SCORE, imp), NEG)
        _, idx = lax.top_k(score, topk)
        k_sel = ks[bi, gi, idx].reshape(b, G, Q_BLOCK, topk * SEL_BLOCK, hd)
        v_sel = vs[bi, gi, idx].reshape(b, G, Q_BLOCK, topk * SEL_BLOCK, hd)
        pos_s = (idx[..., None] * SEL_BLOCK + jnp.arange(SEL_BLOCK)).reshape(b, G, Q_BLOCK, topk * SEL_BLOCK)
        d_s = tq[None, None, :, None] - pos_s
        bias_s = table_g[gi, rel_bucket(d_s)].transpose(0, 1, 4, 2, 3)
        lg = jnp.einsum('bqghd,bgqkd->bghqk', qb, k_sel).astype(jnp.float32) + bias_s
        p_s = masked_softmax(lg, (d_s >= 0)[:, :, None])
        o_s = jnp.einsum('bghqk,bgqkd->bqghd', p_s.astype(v_sel.dtype), v_sel)
        k_w = lax.dynamic_slice_in_dim(kw, s0, kw_len, axis=1)
        v_w = lax.dynamic_slice_in_dim(vw, s0, kw_len, axis=1)
        pos_w = s0 - WINDOW + jnp.arange(kw_len)
        d_w = tq[:, None] - pos_w[None, :]
        mask_w = (d_w >= 0) & (d_w < WINDOW) & (pos_w[None, :] >= 0)
        bias_w = table[rel_bucket(d_w)].reshape(Q_BLOCK, kw_len, G, Hg).transpose(2, 3, 0, 1)
        lg = jnp.einsum('bqghd,bkgd->bghqk', qb, k_w).astype(jnp.float32) + bias_w
        p_w = masked_softmax(lg, mask_w)
        o_w = jnp.einsum('bghqk,bkgd->bqghd', p_w.astype(v_w.dtype), v_w)
        return gb[..., 0:1] * o_c + gb[..., 1:2] * o_s + gb[..., 2:3] * o_w

    out = lax.map(block, jnp.arange(s // Q_BLOCK))
    return jnp.moveaxis(out, 0, 1).reshape(b, s, NSA_Q_WIDTH)


def setup_inputs(seed: int = 0) -> dict:
    key = jax.random.key(seed)
    ks = jax.random.split(key, 22)
    def nrm(k, shape, scale):
        return jax.random.normal(k, shape, jnp.float32) * scale
    def gain(k, shape):
        return 1.0 + 0.1 * jax.random.normal(k, shape, jnp.float32)
    a_in = 3 * CONV_CH + MEM_WIDTH
    b_in = NSA_Q_WIDTH + NSA_GATE_WIDTH + MEM_WIDTH
    return {
        "x": nrm(ks[0], (BATCH, SEQ, D_MODEL), 1.0),
        "mem": nrm(ks[1], (BATCH, MEM_LEN, D_MODEL), 1.0),
        "mix_norm": gain(ks[2], (DEPTH, D_MODEL)),
        "a_w_in": nrm(ks[3], (N_A, D_MODEL, a_in), D_MODEL ** -0.5),
        "a_conv_w": nrm(ks[4], (N_A, CONV_K, CONV_CH), CONV_K ** -0.5),
        "b_w_in": nrm(ks[5], (N_B, D_MODEL, b_in), D_MODEL ** -0.5),
        "b_q_gain": gain(ks[6], (N_B, HEAD_DIM)),
        "kv_norm": gain(ks[7], (D_MODEL,)),
        "w_kv_shared": nrm(ks[8], (D_MODEL, 6 * NSA_KV_GROUPS * HEAD_DIM), D_MODEL ** -0.5),
        "k_gain": gain(ks[9], (N_BRANCH, HEAD_DIM)),
        "cmp_pos": nrm(ks[10], (2, CMP_LEN, HEAD_DIM), 0.1),
        "cmp_w1": nrm(ks[11], (2, CMP_LEN * HEAD_DIM, CMP_HIDDEN), (CMP_LEN * HEAD_DIM) ** -0.5),
        "cmp_w2": nrm(ks[12], (2, CMP_HIDDEN, HEAD_DIM), CMP_HIDDEN ** -0.5),
        "rel_bias": nrm(ks[13], (REL_BUCKETS, NSA_HEADS), 0.5),
        "mem_norm": gain(ks[14], (D_MODEL,)),
        "mem_w_kv": nrm(ks[15], (DEPTH, D_MODEL, 2 * MEM_WIDTH), D_MODEL ** -0.5),
        "mem_q_gain": gain(ks[16], (DEPTH, HEAD_DIM)),
        "mem_k_gain": gain(ks[17], (DEPTH, HEAD_DIM)),
        "w_out": nrm(ks[18], (DEPTH, MIX_WIDTH, D_MODEL), MIX_WIDTH ** -0.5),
        "mlp_norm": gain(ks[19], (DEPTH, D_MODEL)),
        "w_up": nrm(ks[20], (DEPTH, D_MODEL, FF_DIM), D_MODEL ** -0.5),
        "w_down": nrm(ks[21], (DEPTH, FF_DIM, D_MODEL), FF_DIM ** -0.5),
    }


def reference(x, mem, mix_norm, a_w_in, a_conv_w, b_w_in, b_q_gain, kv_norm, w_kv_shared,
              k_gain, cmp_pos, cmp_w1, cmp_w2, rel_bias, mem_norm, mem_w_kv, mem_q_gain,
              mem_k_gain, w_out, mlp_norm, w_up, w_down):
    b, m, _ = mem.shape
    mem_h = rmsnorm(mem, mem_norm)
    shared = None
    for layer in range(DEPTH):
        h = rmsnorm(x, mix_norm[layer])
        mkv = (mem_h @ mem_w_kv[layer]).reshape(b, m, 2, MEM_HEADS, HEAD_DIM)
        mem_k = rmsnorm(mkv[:, :, 0], mem_k_gain[layer])
        mem_v = mkv[:, :, 1]
        if layer < N_A:
            z = h @ a_w_in[layer]
            y_tok = short_conv_mixer(z[..., :3 * CONV_CH], a_conv_w[layer])
            q_mem = z[..., 3 * CONV_CH:]
        else:
            j = layer - N_A
            z = h @ b_w_in[j]
            y_tok = nsa_mixer(z[..., :NSA_Q_WIDTH],
                              z[..., NSA_Q_WIDTH:NSA_Q_WIDTH + NSA_GATE_WIDTH],
                              b_q_gain[j], *shared, rel_bias)
            q_mem = z[..., NSA_Q_WIDTH + NSA_GATE_WIDTH:]
        y_mem = memory_attention(q_mem, mem_k, mem_v, mem_q_gain[layer])
        x = x + jnp.concatenate([y_tok, y_mem], axis=-1) @ w_out[layer]
        hm = rmsnorm(x, mlp_norm[layer])
        x = x + jnp.square(jax.nn.relu(hm @ w_up[layer])) @ w_down[layer]
        if layer == N_A - 1:
            shared = build_shared_kv(x, kv_norm, w_kv_shared, k_gain, cmp_pos, cmp_w1, cmp_w2)
    return x
```

```python
import numpy as np
import concourse.bass as bass
import concourse.mybir as mybir
from contextlib import ExitStack

F32 = mybir.dt.float32
BF16 = mybir.dt.bfloat16
AF = mybir.ActivationFunctionType
ALU = mybir.AluOpType
AX = mybir.AxisListType

ENGS = ("pe", "act", "dve", "pool", "sp")


class T:
    __slots__ = ("h", "name", "lw", "rd", "dsem")

    def __init__(self, h, name):
        self.h = h
        self.name = name
        self.lw = None
        self.rd = []
        self.dsem = None

    def __getitem__(self, k):
        return self.h[k]


class Prog:
    def __init__(self, nc):
        self.nc = nc
        self.es = ExitStack()
        self.ins = {e: [] for e in ENGS}
        self.nsem = 0
        self.dsems = []
        self.free_dsems = []
        self.marked = set()
        self.rw = {e: {} for e in ENGS}
        self.cur = self.es
        self.uid = 0
        self.last_tok = {e: None for e in ENGS}
        self.epoch = 0
        self.ep_of = {}

    def _prune(self, eng, deps):
        out = []
        w = self.rw[eng]
        for d in deps:
            key = (d[0], d[1]); val = d[2]
            if w.get(key, -1) >= val:
                continue
            w[key] = val
            out.append(d)
        return out

    def sb(self, name, shape, dt):
        self.uid += 1
        name = "%s_%d" % (name, self.uid)
        return T(self.cur.enter_context(self.nc.sbuf_tensor(name, list(shape), dt)), name)

    def ps(self, name, shape, dt=F32):
        self.uid += 1
        name = "%s_%d" % (name, self.uid)
        return T(self.cur.enter_context(self.nc.psum_tensor(name, list(shape), dt)), name)

    def push_scope(self):
        self.cur = ExitStack()

    def pop_scope(self):
        self.barrier()
        self.cur.close()
        self.cur = self.es
        self.epoch += 1

    def barrier(self):
        for e in ENGS:
            deps = [t for e2, t in self.last_tok.items() if e2 != e and t is not None]
            deps += [("d", s, c[1]) for s, c in enumerate(self.dsems) if c[1] > 0]
            deps = self._prune(e, deps)
            self.ins[e].append(dict(deps=deps, fn=None, dma=None))

    def dram(self, name, shape, dt, kind="Internal"):
        return T(self.nc.dram_tensor(name, list(shape), dt, kind=kind), name)

    def _new_dsem(self):
        h = self.es.enter_context(self.nc.semaphore("d%d" % len(self.dsems)))
        self.dsems.append([h, 0])
        return len(self.dsems) - 1

    def _deps(self, eng, reads, writes):
        deps = []
        for t in reads:
            if t.lw is not None:
                deps.append(t.lw)
        for t in writes:
            if t.lw is not None:
                deps.append(t.lw)
            deps.extend(t.rd)
        out = []
        for d in deps:
            if d[0] == "e":
                if d[1] == eng:
                    continue
            out.append(d)
        if eng != "pe":
            for t in reads:
                if t.lw is not None and t.lw[0] == "e" and t.lw[1] == eng:
                    out.append(t.lw)
        return out

    def op(self, eng, fn, reads=(), writes=()):
        deps = self._prune(eng, self._deps(eng, reads, writes))
        idx = len(self.ins[eng])
        self.ins[eng].append(dict(deps=deps, fn=fn, dma=None))
        tok = ("e", eng, idx)
        self.last_tok[eng] = tok
        self.ep_of[(eng, idx)] = self.epoch
        for t in reads:
            t.rd = [r for r in t.rd if not (r[0] == "e" and r[1] == eng)]
            t.rd.append(tok)
        for t in writes:
            t.lw = tok
            t.rd = []
        return tok

    def dma(self, eng, out_t, out_ap, in_t, in_ap, nodep_out=False, **kw):
        reads = [in_t] if in_t is not None else []
        writes = [out_t] if out_t is not None else []
        deps = []
        for t in reads:
            if t.lw is not None:
                deps.append(t.lw)
        for t in writes:
            if nodep_out:
                continue
            if t.lw is not None:
                deps.append(t.lw)
            deps.extend(t.rd)
        owner = out_t if (out_t is not None and out_t.dsem is not None) else None
        if owner is None:
            owner = out_t if out_t is not None else in_t
        if owner.dsem is None:
            owner.dsem = self._new_dsem()
        s = owner.dsem
        self.dsems[s][1] += 16
        tok = ("d", s, self.dsems[s][1])
        deps = self._prune(eng, deps)
        self.ins[eng].append(dict(deps=deps, fn=None, dma=(out_ap, in_ap, s, kw)))
        for t in reads:
            t.rd.append(tok)
        for t in writes:
            t.lw = tok
            t.rd = []
        return tok

    def wait_all(self, eng, tiles):
        deps = []
        for t in tiles:
            if t.lw is not None:
                deps.append(t.lw)
            deps.extend(t.rd)
        deps = self._prune(eng, deps)
        self.ins[eng].append(dict(deps=deps, fn=None, dma=None))

    def finalize(self):
        nc = self.nc
        for e in ENGS:
            for rec in self.ins[e]:
                for d in rec["deps"]:
                    if d[0] == "e":
                        self.marked.add((d[1], d[2]))
        count_at = {}
        for e in ENGS:
            c = {}
            for i, rec in enumerate(self.ins[e]):
                if (e, i) in self.marked:
                    ep = self.ep_of[(e, i)]
                    c[ep] = c.get(ep, 0) + 1
                    count_at[(e, i)] = c[ep]
        esem = {(e, ep): self.es.enter_context(nc.semaphore("s_%s_%d" % (e, ep)))
                for e in ENGS for ep in range(self.epoch + 1)}
        engobj = {"pe": "tensor", "act": "scalar", "dve": "vector", "pool": "gpsimd", "sp": "sync"}
        block = self.es.enter_context(nc.Block())
        stats = {}

        def make_body(e):
            def body(eng):
                waited = {}
                nwait = 0
                for i, rec in enumerate(self.ins[e]):
                    need = {}
                    for d in rec["deps"]:
                        if d[0] == "e":
                            key = ("e", (d[1], self.ep_of[(d[1], d[2])])); val = count_at[(d[1], d[2])]
                        else:
                            key = ("d", d[1]); val = d[2]
                        if waited.get(key, 0) >= val:
                            continue
                        if need.get(key, 0) < val:
                            need[key] = val
                    for key, val in need.items():
                        h = esem[key[1]] if key[0] == "e" else self.dsems[key[1]][0]
                        eng.wait_ge(h, val)
                        waited[key] = val
                        nwait += 1
                    if rec.get("cc") is not None:
                        rec["cc"](eng).then_inc(self.dsems[rec["ccsem"]][0], 16)
                    elif rec["dma"] is not None:
                        out_ap, in_ap, s, kw = rec["dma"]
                        try:
                            eng.dma_start(out=out_ap, in_=in_ap, **kw).then_inc(self.dsems[s][0], 16)
                        except Exception:
                            print("DMA FAILED:", e, "out=", out_ap, "in=", in_ap)
                            raise
                    elif rec["fn"] is not None:
                        ins = rec["fn"](eng)
                        if (e, i) in self.marked:
                            ins.then_inc(esem[(e, self.ep_of[(e, i)])], 1)
                stats[e] = (len(self.ins[e]), nwait)
            return body

        for e in ENGS:
            if self.ins[e]:
                getattr(block, engobj[e])(make_body(e))
        self.stats = stats
        self.es.close()


NTOK = 2048
HALO = 4
NCOL = NTOK + HALO
RANGES = [(0, 1028), (1028, 1024)]
EPS = 1e-6


def tiles_of(n):
    out = []
    o = 0
    while o < n:
        w = min(512, n - o)
        out.append((o, w))
        o += w
    return out


class TokStage:
    def __init__(self, P, mode):
        self.P = P
        self.mode = mode
        nc = P.nc
        D = lambda n, s: P.dram(n, s, F32, kind="ExternalInput")
        O = lambda n, s: P.dram(n, s, F32, kind="ExternalOutput")
        self.xT = D("xT", [1024, NCOL])
        self.memT = D("memT", [1024, 256])
        self.gains = D("gains", [128, 8 * 16])
        self.hg = D("hg", [128, 16])
        self.flag = D("flag", [128, 1])
        self.mem_w_kv = D("mem_w_kv", [4, 1024, 512])
        self.w_out = D("w_out", [4, 1024, 1024])
        self.w_up = D("w_up", [4, 1024, 4096])
        self.w_down = D("w_down", [4, 4096, 1024])
        self.b_w_in = D("b_w_in", [2, 1024, 1060])
        if mode == "L1":
            self.a_w_in = D("a_w_in", [2, 1024, 2560])
            self.convw = D("convw", [128, 2 * 6 * 3])
            self.w_kv = D("w_kv", [1024, 768])
            self.kvT_o = O("kvT_o", [768, NCOL])
        else:
            self.ytokT = D("ytokT", [768, NCOL])
        self.xT_o = O("xT_o", [1024, NCOL])
        if mode != "L5":
            self.qT_o = O("qT_o", [768, NCOL])
            self.gT_o = O("gT_o", [36, NCOL])

        self.alloc()

    def alloc(self):
        P = self.P
        sb = P.sb
        self.x = sb("x", [128, 8, NCOL], F32)
        self.h = sb("h", [128, 8, 1028], BF16)
        self.y = sb("y", [128, 8, 1028], BF16)
        self.act = sb("act", [128, 8, 1028], BF16)
        self.u = sb("u", [128, 2, 1028], F32)
        self.v = sb("v", [128, 2, 1030], F32)
        self.qm = sb("qm", [128, 2, 1028], F32)
        self.carry = sb("carry", [128, 6, 2], F32)
        self.rstd = sb("rstd", [128, 512], F32)
        self.sq = [sb("sq%d" % i, [128, 512], BF16) for i in range(2)]
        self.tmp = [sb("tmp%d" % i, [128, 512], F32) for i in range(2)]
        self.ost = [sb("ost%d" % i, [128, 512], F32) for i in range(2)]
        self.wf = [sb("wf%d" % i, [128, 8, 256], F32) for i in range(2)]
        self.wb = [sb("wb%d" % i, [128, 8, 256], BF16) for i in range(2)]
        self.memh = sb("memh", [128, 8, 256], BF16)
        self.kraw = sb("kraw", [128, 2, 256], F32)
        self.memk = sb("memk", [128, 2, 256], BF16)
        self.vaug = sb("vaug", [128, 2, 4, 128], BF16)
        self.qn = sb("qn", [128, 512], BF16)
        self.eT = [sb("eT%d" % i, [128, 512], BF16) for i in range(2)]
        self.rd = sb("rd", [64, 512], F32)
        self.g = sb("g", [128, 8 * 16], F32)
        self.hgs = sb("hgs", [128, 16], F32)
        self.flg = sb("flg", [128, 1], F32)
        self.cw = sb("cw", [128, 36], F32)
        self.ones = sb("ones", [128, 128], BF16)
        self.bd = sb("bd", [128, 128], BF16)
        self.epsT = sb("epsT", [128, 1], F32)
        self.pp = [P.ps("pp%d" % i, [128, 512]) for i in range(4)]
        self.pl = [P.ps("pl%d" % i, [128, 512]) for i in range(2)]
        self.po = P.ps("po", [128, 512])
        self.pn = P.ps("pn", [128, 512])
        self.cnt = dict(wf=0, wb=0, pp=0, pl=0, sq=0, tmp=0, ost=0, eT=0)

    def rot(self, name, lst):
        i = self.cnt[name]
        self.cnt[name] += 1
        return lst[i % len(lst)]

    def stream(self, wT, w_ap, ncols):
        P = self.P
        wf = self.rot("wf", self.wf)
        wb = self.rot("wb", self.wb)
        P.dma("sp", wf, wf[:, :, 0:ncols], wT, w_ap.rearrange("(c p) n -> p c n", p=128))
        P.op("pool", lambda e, wf=wf, wb=wb: e.tensor_copy(out=wb[:, :, 0:ncols], in_=wf[:, :, 0:ncols]), [wf], [wb])
        return wb

    def proj(self, wb, sub, msz, rhs_t, rhs_fn, tiles, consumer):
        P = self.P
        for (t0, w) in tiles:
            ps = self.rot("pp", self.pp)
            for kc in range(8):
                P.op("pe", lambda e, ps=ps, kc=kc, t0=t0, w=w: e.matmul(
                    ps[0:msz, 0:w], lhsT=wb[:, kc, sub * 128: sub * 128 + msz], rhs=rhs_fn(kc, t0, w),
                    start=(kc == 0), stop=(kc == 7)), [wb, rhs_t], [ps])
            consumer(ps, t0, w)

    def setup(self):
        P = self.P
        P.dma("sp", self.g, self.g[:], self.gains, self.gains[:])
        P.dma("sp", self.hgs, self.hgs[:], self.hg, self.hg[:])
        P.dma("sp", self.flg, self.flg[:], self.flag, self.flag[:])
        if self.mode == "L1":
            P.dma("sp", self.cw, self.cw[:], self.convw, self.convw[:])
        for c in range(8):
            P.dma("sp", self.x, self.x[:, c, :], self.xT, self.xT[c * 128:(c + 1) * 128, :], nodep_out=True)
            P.dma("sp", self.u, self.memx(c, 0, 256), self.memT, self.memT[c * 128:(c + 1) * 128, :], nodep_out=True)
        P.op("pool", lambda e: e.memset(self.ones[:], 1.0), [], [self.ones])
        P.op("pool", lambda e: e.memset(self.bd[:], 0.0), [], [self.bd])
        P.op("pool", lambda e: e.memset(self.bd[0:64, 0:64], 1.0), [], [self.bd])
        P.op("pool", lambda e: e.memset(self.bd[64:128, 64:128], 1.0), [], [self.bd])
        P.op("pool", lambda e: e.memset(self.epsT[:], EPS), [], [self.epsT])
        P.op("pool", lambda e: e.memset(self.vaug[:], 1.0), [], [self.vaug])
        P.op("pool", lambda e: e.memset(self.carry[:], 0.0), [], [self.carry])
        self.norm(self.u, self.memx, 256, [(0, 256)], 15, self.memh, 0)

    def gcol(self, which, c):
        return self.g[:, which * 8 + c: which * 8 + c + 1]

    def memx(self, c, t0, w):
        return self.u[:, c // 4, (c % 4) * 256 + t0: (c % 4) * 256 + t0 + w]

    def xs(self, c0):
        return lambda c, t0, w: self.x[:, c, c0 + t0: c0 + t0 + w]

    def norm(self, src, src_fn, n, tiles, which, dst, d0):
        P = self.P
        for (t0, w) in tiles:
            for c in range(8):
                sq = self.rot("sq", self.sq)
                P.op("act", lambda e, sq=sq, c=c, t0=t0, w=w: e.activation(
                    out=sq[:, 0:w], in_=src_fn(c, t0, w), func=AF.Square), [src], [sq])
                P.op("pe", lambda e, sq=sq, c=c, w=w: e.matmul(
                    self.pn[:, 0:w], lhsT=self.ones[:], rhs=sq[:, 0:w], start=(c == 0), stop=(c == 7)),
                    [self.ones, sq], [self.pn])
            P.op("act", lambda e, w=w: e.activation(out=self.rstd[:, 0:w], in_=self.pn[:, 0:w], func=AF.Sqrt,
                                                    bias=self.epsT[:], scale=1.0 / 1024.0),
                 [self.pn, self.epsT], [self.rstd])
            P.op("dve", lambda e, w=w: e.reciprocal(out=self.rstd[:, 0:w], in_=self.rstd[:, 0:w]),
                 [self.rstd], [self.rstd])
            for c in range(8):
                eng = "dve"
                P.op(eng, lambda e, c=c, t0=t0, w=w: e.scalar_tensor_tensor(
                    out=dst[:, c, d0 + t0: d0 + t0 + w], in0=src_fn(c, t0, w),
                    scalar=self.gcol(which, c), in1=self.rstd[:, 0:w], op0=ALU.mult, op1=ALU.mult),
                    [src, self.g, self.rstd], [dst])

    def headnorm(self, src_t, src_ap_fn, w, gain_ap, dst_t, dst_ap):
        P = self.P
        sq = self.rot("sq", self.sq)
        P.op("act", lambda e, sq=sq: e.activation(out=sq[:, 0:w], in_=src_ap_fn(), func=AF.Square), [src_t], [sq])
        P.op("pe", lambda e, sq=sq: e.matmul(self.pn[:, 0:w], lhsT=self.bd[:], rhs=sq[:, 0:w], start=True, stop=True),
             [self.bd, sq], [self.pn])
        P.op("act", lambda e: e.activation(out=self.rstd[:, 0:w], in_=self.pn[:, 0:w], func=AF.Sqrt,
                                           bias=self.epsT[:], scale=1.0 / 64.0), [self.pn, self.epsT], [self.rstd])
        P.op("dve", lambda e: e.reciprocal(out=self.rstd[:, 0:w], in_=self.rstd[:, 0:w]), [self.rstd], [self.rstd])
        P.op("dve", lambda e: e.scalar_tensor_tensor(out=dst_ap, in0=src_ap_fn(), scalar=gain_ap,
                                                     in1=self.rstd[:, 0:w], op0=ALU.mult, op1=ALU.mult),
             [src_t, self.hgs, self.rstd], [dst_t])

    def mem_kv(self, l):
        P = self.P
        wb = self.stream(self.mem_w_kv, self.mem_w_kv[l, :, 0:256], 256)
        for sub in range(2):
            def cons(ps, t0, w, sub=sub):
                P.op("act", lambda e: e.activation(out=self.kraw[:, sub, 0:256], in_=ps[:, 0:256], func=AF.Copy),
                     [ps], [self.kraw])
            self.proj(wb, sub, 128, self.memh, lambda kc, t0, w: self.memh[:, kc, t0:t0 + w], [(0, 256)], cons)
            self.headnorm(self.kraw, lambda sub=sub: self.kraw[:, sub, 0:256], 256,
                          self.hgs[:, 4 + l: 5 + l], self.memk, self.memk[:, sub, 0:256])
        wb = self.stream(self.mem_w_kv, self.mem_w_kv[l, :, 256:512], 256)
        for mc in range(2):
            ps = self.rot("pp", self.pp)
            for kc in range(8):
                P.op("pe", lambda e, ps=ps, kc=kc, mc=mc: e.matmul(
                    ps[:, 0:256], lhsT=self.memh[:, kc, mc * 128:(mc + 1) * 128], rhs=wb[:, kc, 0:256],
                    start=(kc == 0), stop=(kc == 7)), [self.memh, wb], [ps])
            for hh in range(4):
                P.op("act", lambda e, ps=ps, mc=mc, hh=hh: e.activation(
                    out=self.vaug[:, mc, hh, 0:64], in_=ps[:, hh * 64:(hh + 1) * 64], func=AF.Copy), [ps], [self.vaug])

    def mem_attn(self, l, n, tiles):
        P = self.P
        for (t0, w) in tiles:
            for j in range(2):
                self.headnorm(self.qm, lambda j=j, t0=t0, w=w: self.qm[:, j, t0:t0 + w], w,
                              self.hgs[:, l: l + 1], self.qn, self.qn[:, 0:w])
                for hh in range(2):
                    b0 = 64 * hh
                    hd = 2 * j + hh
                    ets = []
                    for mc in range(2):
                        pl = self.rot("pl", self.pl)
                        P.op("pe", lambda e, pl=pl, mc=mc, b0=b0, j=j, w=w: e.matmul(
                            pl[:, 0:w], lhsT=self.memk[b0:b0 + 64, j, mc * 128:(mc + 1) * 128],
                            rhs=self.qn[b0:b0 + 64, 0:w], start=True, stop=True), [self.memk, self.qn], [pl])
                        eT = self.rot("eT", self.eT)
                        P.op("act", lambda e, pl=pl, eT=eT, w=w: e.activation(
                            out=eT[:, 0:w], in_=pl[:, 0:w], func=AF.Exp, scale=0.125), [pl], [eT])
                        ets.append(eT)
                    for mc in range(2):
                        P.op("pe", lambda e, mc=mc, hd=hd, w=w, eT=ets[mc]: e.matmul(
                            self.po[:, 0:w], lhsT=self.vaug[:, mc, hd, :], rhs=eT[:, 0:w],
                            start=(mc == 0), stop=(mc == 1)), [self.vaug, ets[mc]], [self.po])
                    P.op("dve", lambda e, w=w: e.reciprocal(out=self.rd[0:64, 0:w], in_=self.po[64:128, 0:w]),
                         [self.po], [self.rd])
                    P.op("dve", lambda e, w=w, b0=b0, j=j, t0=t0: e.tensor_tensor(
                        out=self.y[b0:b0 + 64, 6 + j, t0:t0 + w], in0=self.po[0:64, 0:w], in1=self.rd[0:64, 0:w],
                        op=ALU.mult), [self.po, self.rd], [self.y])

    def add_into_x(self, c0):
        P = self.P

        def mk(ch):
            def cons(ps, t0, w):
                P.op("dve", lambda e: e.tensor_tensor(
                    out=self.x[:, ch, c0 + t0: c0 + t0 + w], in0=ps[:, 0:w], in1=self.x[:, ch, c0 + t0: c0 + t0 + w],
                    op=ALU.add), [ps, self.x], [self.x])
            return cons
        return mk

    def out_proj(self, l, c0, n, tiles):
        mk = self.add_into_x(c0)
        for blk in range(4):
            wb = self.stream(self.w_out, self.w_out[l, :, blk * 256:(blk + 1) * 256], 256)
            for sub in range(2):
                self.proj(wb, sub, 128, self.y, lambda kc, t0, w: self.y[:, kc, t0:t0 + w], tiles, mk(blk * 2 + sub))

    def mlp(self, l, c0, n, tiles):
        P = self.P
        self.norm(self.x, self.xs(c0), n, tiles, 4 + l, self.h, 0)
        mk = self.add_into_x(c0)
        for sg in range(4):
            for blk in range(4):
                wb = self.stream(self.w_up, self.w_up[l, :, sg * 1024 + blk * 256: sg * 1024 + (blk + 1) * 256], 256)
                for sub in range(2):
                    def cons(ps, t0, w, fc=blk * 2 + sub):
                        tmp = self.rot("tmp", self.tmp)
                        P.op("act", lambda e: e.activation(out=tmp[:, 0:w], in_=ps[:, 0:w], func=AF.Relu), [ps], [tmp])
                        P.op("pool", lambda e: e.tensor_tensor(out=self.act[:, fc, t0:t0 + w], in0=tmp[:, 0:w],
                                                               in1=tmp[:, 0:w], op=ALU.mult), [tmp], [self.act])
                    self.proj(wb, sub, 128, self.h, lambda kc, t0, w: self.h[:, kc, t0:t0 + w], tiles, cons)
            for blk in range(4):
                wb = self.stream(self.w_down, self.w_down[l, sg * 1024:(sg + 1) * 1024, blk * 256:(blk + 1) * 256], 256)
                for sub in range(2):
                    self.proj(wb, sub, 128, self.act, lambda kc, t0, w: self.act[:, kc, t0:t0 + w], tiles,
                              mk(blk * 2 + sub))

    def mix_a(self, l, ri, c0, n, tiles):
        P = self.P
        W = self.a_w_in
        hrhs = lambda kc, t0, w: self.h[:, kc, t0:t0 + w]
        for i in range(3):
            wb = self.stream(W, W[l, :, 1536 + 256 * i: 1536 + 256 * (i + 1)], 256)
            for sub in range(2):
                def cons(ps, t0, w, sub=sub):
                    P.op("act", lambda e: e.activation(out=self.u[:, sub, t0:t0 + w], in_=ps[:, 0:w], func=AF.Copy),
                         [ps], [self.u])
                self.proj(wb, sub, 128, self.h, hrhs, tiles, cons)
            wb = self.stream(W, W[l, :, 768 + 256 * i: 768 + 256 * (i + 1)], 256)
            for sub in range(2):
                def cons(ps, t0, w, sub=sub):
                    P.op("dve", lambda e: e.tensor_tensor(out=self.v[:, sub, 2 + t0: 2 + t0 + w], in0=ps[:, 0:w],
                                                          in1=self.u[:, sub, t0:t0 + w], op=ALU.mult),
                         [ps, self.u], [self.v])
                self.proj(wb, sub, 128, self.h, hrhs, tiles, cons)
            for sub in range(2):
                ch = 2 * i + sub
                if ri == 0:
                    P.op("pool", lambda e, sub=sub: e.memset(self.v[:, sub, 0:2], 0.0), [], [self.v])
                    self.halo_fix(sub)
                    P.op("dve", lambda e, sub=sub, ch=ch: e.tensor_copy(out=self.carry[:, ch, :],
                                                                        in_=self.v[:, sub, n:n + 2]),
                         [self.v], [self.carry])
                else:
                    P.op("dve", lambda e, sub=sub, ch=ch: e.tensor_copy(out=self.v[:, sub, 0:2],
                                                                        in_=self.carry[:, ch, :]),
                         [self.carry], [self.v])
                cwc = lambda k, ch=ch: self.cw[:, l * 18 + ch * 3 + k: l * 18 + ch * 3 + k + 1]
                P.op("pool", lambda e, sub=sub, cwc=cwc: e.tensor_scalar(
                    out=self.u[:, sub, 0:n], in0=self.v[:, sub, 0:n], scalar1=cwc(0), scalar2=None, op0=ALU.mult),
                    [self.v, self.cw], [self.u])
                for k in (1, 2):
                    P.op("dve", lambda e, sub=sub, cwc=cwc, k=k: e.scalar_tensor_tensor(
                        out=self.u[:, sub, 0:n], in0=self.v[:, sub, k:k + n], scalar=cwc(k), in1=self.u[:, sub, 0:n],
                        op0=ALU.mult, op1=ALU.add), [self.v, self.cw, self.u], [self.u])
            wb = self.stream(W, W[l, :, 256 * i: 256 * (i + 1)], 256)
            for sub in range(2):
                def cons(ps, t0, w, sub=sub, ch=2 * i + sub):
                    P.op("dve", lambda e: e.tensor_tensor(out=self.y[:, ch, t0:t0 + w], in0=ps[:, 0:w],
                                                          in1=self.u[:, sub, t0:t0 + w], op=ALU.mult),
                         [ps, self.u], [self.y])
                self.proj(wb, sub, 128, self.h, hrhs, tiles, cons)
        wb = self.stream(W, W[l, :, 2304:2560], 256)
        self.qmem_proj(wb, tiles)

    def halo_fix(self, sub):
        self.P.op("dve", lambda e: e.tensor_scalar(
            out=self.v[:, sub, 2:2 + HALO], in0=self.v[:, sub, 2:2 + HALO], scalar1=self.flg[:, 0:1],
            scalar2=None, op0=ALU.mult), [self.v, self.flg], [self.v])

    def store_tile(self, dstT, r, msz, c0, t0, w, ost):
        self.P.dma("sp", dstT, dstT[r:r + msz, c0 + t0: c0 + t0 + w], ost, ost[0:msz, 0:w], nodep_out=True)

    def ytok_src(self, ch, c0, t0, w):
        return self.ytokT[ch * 128:(ch + 1) * 128, c0 + t0: c0 + t0 + w]

    def qmem_proj(self, wb, tiles):
        P = self.P
        for sub in range(2):
            def cons(ps, t0, w, sub=sub):
                P.op("act", lambda e: e.activation(out=self.qm[:, sub, t0:t0 + w], in_=ps[:, 0:w], func=AF.Copy),
                     [ps], [self.qm])
            self.proj(wb, sub, 128, self.h, lambda kc, t0, w: self.h[:, kc, t0:t0 + w], tiles, cons)

    def store_proj(self, wT, w_ap, ncols, dstT, row0, c0, tiles, func=AF.Copy):
        P = self.P
        wb = self.stream(wT, w_ap, ncols)
        nsub = (ncols + 127) // 128
        for sub in range(nsub):
            msz = min(128, ncols - sub * 128)

            def cons(ps, t0, w, sub=sub, msz=msz):
                ost = self.rot("ost", self.ost)
                P.op("act", lambda e: e.activation(out=ost[0:msz, 0:w], in_=ps[0:msz, 0:w], func=func), [ps], [ost])
                r = row0 + sub * 128
                self.store_tile(dstT, r, msz, c0, t0, w, ost)
            self.proj(wb, sub, msz, self.h, lambda kc, t0, w: self.h[:, kc, t0:t0 + w], tiles, cons)

    def pre_b(self, j, c0, n, tiles):
        W = self.b_w_in
        self.norm(self.x, self.xs(c0), n, tiles, 2 + j, self.h, 0)
        for i in range(3):
            self.store_proj(W, W[j, :, 256 * i:256 * (i + 1)], 256, self.qT_o, 256 * i, c0, tiles)
        self.store_proj(W, W[j, :, 768:804], 36, self.gT_o, 0, c0, tiles, func=AF.Sigmoid)

    def post_b(self, j, c0, n, tiles):
        P = self.P
        W = self.b_w_in
        l = 2 + j
        self.norm(self.x, self.xs(c0), n, tiles, l, self.h, 0)
        wb = self.stream(W, W[j, :, 804:1060], 256)
        self.qmem_proj(wb, tiles)
        for ch in range(6):
            for (t0, w) in tiles:
                tmp = self.rot("tmp", self.tmp)
                P.dma("sp", tmp, tmp[:, 0:w], self.ytokT, self.ytok_src(ch, c0, t0, w))
                P.op("pool", lambda e, tmp=tmp, ch=ch, t0=t0, w=w: e.tensor_copy(out=self.y[:, ch, t0:t0 + w],
                                                                                 in_=tmp[:, 0:w]), [tmp], [self.y])
        self.mem_attn(l, n, tiles)
        self.out_proj(l, c0, n, tiles)
        self.mlp(l, c0, n, tiles)

    def store_x(self):
        P = self.P
        for c in range(8):
            P.dma("sp", self.xT_o, self.xT_o[c * 128:(c + 1) * 128, :], self.x, self.x[:, c, :], nodep_out=True)

    def build(self):
        P = self.P
        self.setup()
        if self.mode == "L1":
            for l in range(2):
                self.mem_kv(l)
                for ri, (c0, n) in enumerate(RANGES):
                    tiles = tiles_of(n)
                    self.norm(self.x, self.xs(c0), n, tiles, l, self.h, 0)
                    self.mix_a(l, ri, c0, n, tiles)
                    self.mem_attn(l, n, tiles)
                    self.out_proj(l, c0, n, tiles)
                    self.mlp(l, c0, n, tiles)
            for ri, (c0, n) in enumerate(RANGES):
                tiles = tiles_of(n)
                self.norm(self.x, self.xs(c0), n, tiles, 8, self.h, 0)
                for i in range(3):
                    self.store_proj(self.w_kv, self.w_kv[:, 256 * i:256 * (i + 1)], 256, self.kvT_o, 256 * i, c0, tiles)
                self.pre_b(0, c0, n, tiles)
            self.store_x()
            outs = [self.kvT_o, self.qT_o, self.gT_o, self.xT_o]
        elif self.mode == "L3":
            self.mem_kv(2)
            for ri, (c0, n) in enumerate(RANGES):
                tiles = tiles_of(n)
                self.post_b(0, c0, n, tiles)
            for ri, (c0, n) in enumerate(RANGES):
                tiles = tiles_of(n)
                self.pre_b(1, c0, n, tiles)
            self.store_x()
            outs = [self.qT_o, self.gT_o, self.xT_o]
        else:
            self.mem_kv(3)
            for ri, (c0, n) in enumerate(RANGES):
                tiles = tiles_of(n)
                self.post_b(1, c0, n, tiles)
            self.store_x()
            outs = [self.xT_o]
        P.wait_all("sp", outs)
        P.finalize()


def build_tok(mode):
    nc = bass.Bass("TRN2", target_bir_lowering=False)
    P = Prog(nc)
    st = TokStage(P, mode)
    st.build()
    return nc, P


def host_gains(inp):
    g = np.zeros((16, 1024), np.float32)
    g[0:4] = inp["mix_norm"]
    g[4:8] = inp["mlp_norm"]
    g[8] = inp["kv_norm"]
    g[15] = inp["mem_norm"]
    return np.ascontiguousarray(g.reshape(16, 8, 128).transpose(2, 0, 1).reshape(128, 128))


def host_hg(inp):
    hg = np.zeros((128, 16), np.float32)
    for l in range(4):
        hg[:, l] = np.tile(inp["mem_q_gain"][l], 2)
        hg[:, 4 + l] = np.tile(inp["mem_k_gain"][l], 2)
    return hg


def host_convw(inp):
    w = inp["a_conv_w"].reshape(2, 3, 6, 128)
    return np.ascontiguousarray(w.transpose(3, 0, 2, 1).reshape(128, 36))

import math

SEQ = 8192
QT = 512
NQT = SEQ // QT
DC, LC = 3040, 5104
DW, LW = 1023, 1536
DS, LS = 1407, 1920
LTOT = LC + LW + LS
LU, LTW, LTS = 3072, 1408, 1792
MASKV = -30000.0


def rel_bucket_np(d):
    n = np.maximum(d, 0)
    nf = np.maximum(n, 1).astype(np.float32)
    large = 16 + (np.log(nf / np.float32(16)) / np.float32(math.log(1024 / 16)) * np.float32(16)).astype(np.int32)
    large = np.minimum(large, 31)
    return np.where(n < 16, n, large)


def host_consts():
    c = {}
    oh = np.zeros((33, LTOT), np.float32)
    x = np.arange(LC); d = DC - x
    b = rel_bucket_np(d)
    oh[b[d >= 0], x[d >= 0]] = 1.0
    oh[32, x[d < 0]] = 1.0
    x = np.arange(LW); d = DW - x; ok = (d >= 0) & (d < 512)
    b = rel_bucket_np(d)
    oh[b[ok], LC + x[ok]] = 1.0
    oh[32, LC + x[~ok]] = 1.0
    x = np.arange(LS); d = DS - x; ok = d >= 0
    b = rel_bucket_np(d)
    oh[b[ok], LC + LW + x[ok]] = 1.0
    oh[32, LC + LW + x[~ok]] = 1.0
    c["OH"] = oh
    c["ident"] = np.eye(128, dtype=np.float32)
    k = np.arange(SEQ)
    r = 2 * ((k // 128) % 32) + ((k % 128) >= 64)
    ee = np.zeros((64, SEQ), np.float32)
    ee[r, k] = 30000.0
    c["EE"] = ee
    i = np.arange(512)[:, None] * 16
    s = np.arange(128)[None, :] * 64
    ov = np.clip(np.minimum(i + 32, s + 64) - np.maximum(i, s), 0, None).astype(np.float32) / 32.0
    ov[511] = 0.0
    c["ovl"] = np.ascontiguousarray(ov.reshape(4, 128, 128).transpose(1, 0, 2))
    tc = np.zeros((NQT, 128, 4, 2, 128), np.float32)
    j = np.arange(128)[None, :]
    for qt in range(NQT):
        for ss in range(4):
            t = qt * QT + 511 - (128 * ss + np.arange(128))
            back = (t // 64)[:, None] - j
            forced = (j == 0) | ((back >= 0) & (back < 2))
            A = ((back >= 0) & ~forced).astype(np.float32)
            B = np.where(back >= 0, np.where(forced, 1e4, 0.0), -1e30).astype(np.float32)
            tc[qt, :, ss, 0] = A
            tc[qt, :, ss, 1] = B
    c["topc"] = tc.reshape(NQT, 128, 1024)
    return c


class Attn:
    def __init__(self, P):
        self.P = P
        D = lambda n, s: P.dram(n, s, F32, kind="ExternalInput")
        self.qT = D("qT", [384, SEQ])
        self.gB = D("gB", [9, SEQ])
        self.kvT = D("kvT", [384, SEQ])
        self.vtok = D("vtok", [SEQ, 128])
        self.blk = D("blk", [2, 64, 32, 512])
        self.hg = D("hg", [128, 8])
        self.pos = D("pos", [2, 64, 32])
        self.w1 = D("w1", [2, 2048, 256])
        self.w2 = D("w2", [2, 256, 64])
        self.rb6 = D("rb6", [32, 6])
        self.OH = D("OH", [33, LTOT])
        self.identD = D("ident", [128, 128])
        self.EE = D("EE", [64, SEQ])
        self.ovlD = D("ovl", [128, 4, 128])
        self.topc = D("topc", [NQT, 128, 1024])
        self.yT = P.dram("yT", [192, SEQ], F32, kind="ExternalOutput")
        self.Rrow = P.dram("Rrow", [6, LTOT], F32, kind="Internal")
        self.alloc()

    def alloc(self):
        P = self.P
        sb = P.sb
        self.ksaug = sb("ksaug", [128, SEQ], BF16)
        self.kwT = sb("kwT", [64, SEQ], BF16)
        self.vsaug = sb("vsaug", [128, 64, 128], BF16)
        self.vwaug = sb("vwaug", [128, 64, 128], BF16)
        self.kcT = sb("kcT", [64, 512], BF16)
        self.vcaug = sb("vcaug", [128, 4, 128], BF16)
        self.Ubig = sb("Ubig", [128, 6 * LU], BF16)
        self.TTw = [sb("TTw%d" % h, [128, LTW], BF16) for h in range(3)]
        self.TTs = [sb("TTs%d" % h, [128, LTS], BF16) for h in range(3)]
        self.ident = sb("identb", [128, 128], BF16)
        self.ones = sb("ones", [128, 128], BF16)
        self.ovl = sb("ovlb", [128, 4, 128], BF16)
        self.hgs = sb("hgs", [128, 8], F32)
        self.cf = sb("cf", [128, 6], F32)
        self.tab = sb("tab", [33, 6], F32)
        self.epsT = sb("epsT", [128, 1], F32)
        self.st = [sb("st%d" % i, [128, 1024], F32) for i in range(2)]
        self.sq = [sb("sq%d" % i, [128, 512], BF16) for i in range(2)]
        self.rq = sb("rq", [128, 512], F32)
        self.qraw = sb("qraw", [64, 6, 512], F32)
        self.qn = sb("qn", [64, 6, 512], BF16)
        self.qaug = sb("qaug", [128, 3, 2, 512], BF16)
        self.pT = [sb("pT%d" % i, [128, 512], BF16) for i in range(6)]
        self.pTn = [sb("pTn%d" % i, [128, 512], BF16) for i in range(4)]
        self.rdb = sb("rdb", [128, 512], F32)
        self.impacc = sb("impacc", [128, 512], F32)
        self.tcs = sb("tcs", [128, 1024], F32)
        self.sc = sb("sc", [128, 128], F32)
        self.sc2 = sb("sc2", [128, 128], F32)
        self.mx = sb("mx", [128, 16], F32)
        self.mneg = sb("mneg", [128, 128], BF16)
        self.gt = [sb("gt%d" % i, [64, 512], F32) for i in range(2)]
        self.rdt = sb("rdt", [64, 512], F32)
        self.wt = sb("wt", [64, 512], F32)
        self.tm = sb("tm", [64, 512], F32)
        self.y = sb("y", [64, 3, 512], F32)
        self.w1b = sb("w1b", [64, 4, 256], BF16)
        self.blkb = sb("blkb", [64, 4, 512], BF16)
        self.posT = sb("posT", [64, 32], F32)
        self.w2b = sb("w2b", [128, 2, 64], BF16)
        self.gx = self.rdb
        self.gy = self.impacc
        self.gT = [sb("gT%d" % i, [128, 512], BF16) for i in range(2)]
        self.A = [P.ps("A%d" % i, [128, 512]) for i in range(3)]
        self.S = [P.ps("S%d" % i, [128, 512]) for i in range(2)]
        self.X0 = P.ps("X0", [128, 512])
        self.X1 = P.ps("X1", [128, 512])
        self.XT = P.ps("XT", [128, 512], BF16)
        self.cnt = {}

    def uslice(self, h, mo):
        return self.Ubig[:, h * LU + mo: h * LU + mo + 512]

    def rot(self, name, lst):
        i = self.cnt.get(name, 0)
        self.cnt[name] = i + 1
        return lst[i % len(lst)]

    def norm64(self, src_t, src_ap, w, gain_ap, dst_t, dst_ap):
        P = self.P
        sq = self.rot("sq", self.sq)
        P.op("act", lambda e: e.activation(out=sq[0:64, 0:w], in_=src_ap, func=AF.Square), [src_t], [sq])
        P.op("pe", lambda e: e.matmul(self.X0[0:64, 0:w], lhsT=self.ones[0:64, 0:64], rhs=sq[0:64, 0:w],
                                      start=True, stop=True), [self.ones, sq], [self.X0])
        P.op("act", lambda e: e.activation(out=self.rq[0:64, 0:w], in_=self.X0[0:64, 0:w], func=AF.Sqrt,
                                           bias=self.epsT[0:64, :], scale=1.0 / 64.0), [self.X0, self.epsT], [self.rq])
        P.op("dve", lambda e: e.reciprocal(out=self.rq[0:64, 0:w], in_=self.rq[0:64, 0:w]), [self.rq], [self.rq])
        P.op("dve", lambda e: e.scalar_tensor_tensor(out=dst_ap, in0=src_ap, scalar=gain_ap, in1=self.rq[0:64, 0:w],
                                                     op0=ALU.mult, op1=ALU.mult), [src_t, self.hgs, self.rq], [dst_t])

    def setup(self):
        P = self.P
        st = self.st
        P.dma("sp", self.hgs, self.hgs[:], self.hg, self.hg[:])
        P.op("pool", lambda e: e.memset(self.ones[:], 1.0), [], [self.ones])
        P.op("pool", lambda e: e.memset(self.epsT[:], 1e-6), [], [self.epsT])
        P.op("pool", lambda e: e.memset(self.vsaug[:], 1.0), [], [self.vsaug])
        P.op("pool", lambda e: e.memset(self.vwaug[:], 1.0), [], [self.vwaug])
        P.op("pool", lambda e: e.memset(self.vcaug[:], 1.0), [], [self.vcaug])
        s = self.rot("st", st)
        P.dma("sp", s, s[:, 0:128], self.identD, self.identD[:])
        P.op("pool", lambda e, s=s: e.tensor_copy(out=self.ident[:], in_=s[:, 0:128]), [s], [self.ident])
        s = self.rot("st", st)
        P.dma("sp", s, s[:, 0:512], self.ovlD, self.ovlD[:].rearrange("p c j -> p (c j)"))
        P.op("pool", lambda e, s=s: e.tensor_copy(out=self.ovl[:].rearrange("p c j -> p (c j)"), in_=s[:, 0:512]),
             [s], [self.ovl])
        self.setup_tables()
        self.setup_kv()

    def load_rb(self):
        P = self.P
        rb = self.rb6[:]
        P.dma("sp", self.cf, self.cf[:], self.rb6, bass.AP(tensor=rb.tensor, offset=31 * 6, ap=[[0, 128], [1, 6]]))
        P.dma("sp", self.tab, self.tab[0:32, :], self.rb6, self.rb6[:])

    def setup_tables(self):
        P = self.P
        st = self.st
        P.op("pool", lambda e: e.memset(self.tab[:], MASKV), [], [self.tab])
        self.load_rb()
        P.op("dve", lambda e: e.tensor_scalar(out=self.tab[0:32, :], in0=self.tab[0:32, :], scalar1=8.0, scalar2=None,
                                              op0=ALU.mult), [self.tab], [self.tab])
        o = 0
        while o < LTOT:
            w = min(512, LTOT - o)
            s = self.rot("st", st)
            P.dma("sp", s, s[0:33, 0:w], self.OH, self.OH[:, o:o + w])
            P.op("pe", lambda e, s=s, w=w: e.matmul(self.X1[0:6, 0:w], lhsT=self.tab[:, :], rhs=s[0:33, 0:w],
                                                    start=True, stop=True), [self.tab, s], [self.X1])
            s2 = self.rot("st", st)
            P.op("act", lambda e, s2=s2, w=w: e.activation(out=s2[0:6, 0:w], in_=self.X1[0:6, 0:w], func=AF.Copy),
                 [self.X1], [s2])
            P.dma("sp", self.Rrow, self.Rrow[:, o:o + w], s2, s2[0:6, 0:w])
            o += w
        rr = self.Rrow[:]

        def expand(dst, d0, h, base, pstep, L):
            o = 0
            while o < L:
                w = min(1024, L - o)
                s = self.rot("st", st)
                P.dma("sp", s, s[:, 0:w], self.Rrow,
                      bass.AP(tensor=rr.tensor, offset=h * LTOT + base + o, ap=[[pstep, 128], [1, w]]))
                P.op("pool", lambda e, s=s, o=o, w=w: e.tensor_copy(out=dst[:, d0 + o:d0 + o + w], in_=s[:, 0:w]),
                     [s], [dst])
                o += w
        for h in range(6):
            expand(self.Ubig, h * LU, h, 0, 16, LU)
        for h in range(3):
            expand(self.TTw[h], 0, h, LC, 1, LTW)
            expand(self.TTs[h], 0, h, LC + LW, 1, LTS)

    def load_kv_piece(self, s, slot, c0):
        self.P.dma("sp", s, s[0:64, :], self.kvT, self.kvT[slot * 64:(slot + 1) * 64, c0:c0 + 1024])

    def load_v_piece(self, piece):
        P = self.P
        c0 = piece * 1024
        s = self.rot("st", self.st)
        P.dma("sp", s, s[:, :].rearrange("p (c f) -> p c f", f=128), self.vtok,
              self.vtok[c0:c0 + 1024, :].rearrange("(c p) f -> p c f", p=128))
        sv = s[:, :].rearrange("p (c f) -> p c f", f=128)
        P.op("pool", lambda e, sv=sv, piece=piece: e.tensor_copy(
            out=self.vsaug[:, piece * 8:(piece + 1) * 8, 0:64], in_=sv[:, :, 0:64]), [s], [self.vsaug])
        P.op("pool", lambda e, sv=sv, piece=piece: e.tensor_copy(
            out=self.vwaug[:, piece * 8:(piece + 1) * 8, 0:64], in_=sv[:, :, 64:128]), [s], [self.vwaug])

    def setup_kv(self):
        P = self.P
        st = self.st
        for piece in range(8):
            c0 = piece * 1024
            for slot, gcol, dst in ((2, 2, self.ksaug), (4, 3, self.kwT)):
                s = self.rot("st", st)
                self.load_kv_piece(s, slot, c0)
                for hf in range(2):
                    a = hf * 512
                    self.norm64(s, s[0:64, a:a + 512], 512, self.hgs[0:64, gcol:gcol + 1], dst,
                                dst[0:64, c0 + a:c0 + a + 512])
            s = self.rot("st", st)
            P.dma("sp", s, s[64:128, :], self.EE, self.EE[:, c0:c0 + 1024])
            P.op("pool", lambda e, s=s, c0=c0: e.tensor_copy(out=self.ksaug[64:128, c0:c0 + 1024], in_=s[64:128, :]),
                 [s], [self.ksaug])
            self.load_v_piece(piece)
        self.compress()

    def compress_rhs(self, kvi, tg):
        P = self.P
        for th in range(2):
            s = self.rot("st", self.st)
            t0 = tg * 4 + th * 2
            P.dma("sp", s, s[0:64, :].rearrange("p (t n) -> p t n", n=512), self.blk,
                  self.blk[kvi, :, t0:t0 + 2, :])
            for tt in range(2):
                t = t0 + tt
                P.op("dve", lambda e, s=s, tt=tt, th=th, t=t: e.tensor_scalar(
                    out=self.blkb[:, th * 2 + tt, :], in0=s[0:64, tt * 512:(tt + 1) * 512],
                    scalar1=self.posT[:, t:t + 1], scalar2=None, op0=ALU.add), [s, self.posT], [self.blkb])

    def compress_begin(self, kvi):
        pass

    def compress(self):
        P = self.P
        st = self.st
        H = [self.A[0], self.A[1]]
        for kvi in range(2):
            self.compress_begin(kvi)
            P.dma("sp", self.posT, self.posT[:], self.pos, self.pos[kvi])
            s = self.rot("st", st)
            P.dma("sp", s, s[:, 0:128].rearrange("p (c d) -> p c d", d=64), self.w2,
                  self.w2[kvi].rearrange("(c p) d -> p c d", p=128))
            P.op("pool", lambda e, s=s: e.tensor_copy(out=self.w2b[:].rearrange("p c d -> p (c d)"), in_=s[:, 0:128]),
                 [s], [self.w2b])
            for tg in range(8):
                s = self.rot("st", st)
                P.dma("sp", s, s[0:64, :].rearrange("p (t j) -> p t j", j=256), self.w1,
                      self.w1[kvi, tg * 256:(tg + 1) * 256, :].rearrange("(t d) j -> d t j", d=64))
                P.op("pool", lambda e, s=s: e.tensor_copy(out=self.w1b[:].rearrange("p t j -> p (t j)"),
                                                          in_=s[0:64, :]), [s], [self.w1b])
                self.compress_rhs(kvi, tg)
                for tl in range(4):
                    t = tg * 4 + tl
                    for jc in range(2):
                        P.op("pe", lambda e, tl=tl, jc=jc, t=t: e.matmul(
                            H[jc][:, :], lhsT=self.w1b[:, tl, jc * 128:(jc + 1) * 128], rhs=self.blkb[:, tl, :],
                            start=(t == 0), stop=(t == 31)), [self.w1b, self.blkb], [H[jc]])
            for jc in range(2):
                P.op("act", lambda e, jc=jc: e.activation(out=self.gx[:], in_=H[jc][:, :], func=AF.Copy),
                     [H[jc]], [self.gx])
                P.op("pool", lambda e: e.tensor_tensor(out=self.gy[:], in0=self.gx[:], in1=self.gx[:], op=ALU.mult),
                     [self.gx], [self.gy])
                P.op("dve", lambda e: e.tensor_scalar(out=self.gy[:], in0=self.gy[:], scalar1=0.044715, scalar2=1.0,
                                                      op0=ALU.mult, op1=ALU.add), [self.gy], [self.gy])
                P.op("pool", lambda e: e.tensor_tensor(out=self.gy[:], in0=self.gy[:], in1=self.gx[:], op=ALU.mult),
                     [self.gy, self.gx], [self.gy])
                P.op("act", lambda e: e.activation(out=self.gy[:], in_=self.gy[:], func=AF.Sigmoid,
                                                   scale=1.5957691216057308), [self.gy], [self.gy])
                P.op("dve", lambda e, jc=jc: e.tensor_tensor(out=self.gT[jc][:], in0=self.gx[:], in1=self.gy[:],
                                                             op=ALU.mult), [self.gx, self.gy], [self.gT[jc]])
            if kvi == 0:
                for jc in range(2):
                    P.op("pe", lambda e, jc=jc: e.matmul(self.X1[0:64, :], lhsT=self.w2b[:, jc, :], rhs=self.gT[jc][:],
                                                         start=(jc == 0), stop=(jc == 1)),
                         [self.w2b, self.gT[jc]], [self.X1])
                s = self.rot("st", st)
                P.op("act", lambda e, s=s: e.activation(out=s[0:64, 0:512], in_=self.X1[0:64, :], func=AF.Copy),
                     [self.X1], [s])
                self.norm64(s, s[0:64, 0:512], 512, self.hgs[0:64, 1:2], self.kcT, self.kcT[:, :])
            else:
                for cc in range(4):
                    for jc in range(2):
                        P.op("pe", lambda e, jc=jc, cc=cc: e.matmul(
                            self.X1[:, cc * 64:(cc + 1) * 64], lhsT=self.gT[jc][:, cc * 128:(cc + 1) * 128],
                            rhs=self.w2b[:, jc, :], start=(jc == 0), stop=(jc == 1)),
                            [self.w2b, self.gT[jc]], [self.X1])
                    P.op("act", lambda e, cc=cc: e.activation(out=self.vcaug[:, cc, 0:64],
                                                              in_=self.X1[:, cc * 64:(cc + 1) * 64], func=AF.Copy),
                         [self.X1], [self.vcaug])

    def combine(self, qt, h, br, first):
        P = self.P
        A = self.A[h]
        s0 = qt * QT
        gt = self.load_gate(qt, h, br)
        P.op("dve", lambda e: e.tensor_scalar(out=self.rdt[:], in0=A[64:128, :], scalar1=1e-30, scalar2=None,
                                              op0=ALU.max), [A], [self.rdt])
        P.op("dve", lambda e: e.reciprocal(out=self.rdt[:], in_=self.rdt[:]), [self.rdt], [self.rdt])
        P.op("pool", lambda e: e.tensor_tensor(out=self.wt[:], in0=self.rdt[:], in1=gt[:], op=ALU.mult),
             [self.rdt, gt], [self.wt])
        if first:
            P.op("dve", lambda e: e.tensor_tensor(out=self.y[:, h, :], in0=A[0:64, :], in1=self.wt[:], op=ALU.mult),
                 [A, self.wt], [self.y])
        else:
            P.op("dve", lambda e: e.tensor_tensor(out=self.tm[:], in0=A[0:64, :], in1=self.wt[:], op=ALU.mult),
                 [A, self.wt], [self.tm])
            P.op("pool", lambda e: e.tensor_tensor(out=self.y[:, h, :], in0=self.y[:, h, :], in1=self.tm[:],
                                                   op=ALU.add), [self.y, self.tm], [self.y])

    def exp(self, S, pT, bias):
        P = self.P
        if bias is None:
            P.op("act", lambda e: e.activation(out=pT[:], in_=S[:, :], func=AF.Exp, scale=0.125), [S], [pT])
        else:
            bap, bt = bias
            P.op("act", lambda e: e.activation(out=pT[:], in_=S[:, :], func=AF.Exp, scale=0.125, bias=bap),
                 [S, bt], [pT])

    def cmp_bias(self, cc, h, mixed):
        return None if mixed else (self.cf[:, h:h + 1], self.cf)

    def sel_bias(self, kc, h, near):
        return None if near else (self.cf[:, h:h + 1], self.cf)

    def win_bias(self, c):
        return None

    def topc_ap(self, qt):
        return self.topc[qt]

    def load_q(self, qt):
        s0 = qt * QT
        self.P.dma("sp", self.qraw, self.qraw[:], self.qT, self.qT[:, s0:s0 + QT].rearrange("(h d) t -> d h t", d=64))

    def load_gate(self, qt, h, br):
        s0 = qt * QT
        gt = self.rot("gt", self.gt)
        g = self.gB[:]
        self.P.dma("sp", gt, gt[:], self.gB,
                   bass.AP(tensor=g.tensor, offset=(h * 3 + br) * SEQ + s0, ap=[[0, 64], [1, 512]]))
        return gt

    def store_y(self, qt):
        s0 = qt * QT
        self.P.dma("sp", self.yT, self.yT[:, s0:s0 + QT].rearrange("(h d) t -> d h t", d=64), self.y, self.y[:],
                   nodep_out=True)

    def qtile(self, qt):
        P = self.P
        s0 = qt * QT
        self.load_q(qt)
        P.dma("sp", self.tcs, self.tcs[:], self.topc, self.topc_ap(qt))
        for h in range(6):
            self.norm64(self.qraw, self.qraw[:, h, :], 512, self.hgs[0:64, 0:1], self.qn, self.qn[:, h, :])
        for h in range(3):
            for half in range(2):
                P.op("pool", lambda e, h=h, half=half: e.tensor_copy(out=self.qaug[0:64, h, half, :],
                                                                     in_=self.qn[:, h, :]), [self.qn], [self.qaug])
        for h in range(6):
            chunks = [cc for cc in range(4) if qt - 4 * cc >= 0]
            pts = []
            for cc in chunks:
                k = qt - 4 * cc
                mixed = k <= 5
                S = self.rot("S", self.S)
                P.op("pe", lambda e, S=S, cc=cc, h=h, mixed=mixed: e.matmul(
                    S[:, :], lhsT=self.kcT[:, cc * 128:(cc + 1) * 128], rhs=self.qn[:, h, :], start=True,
                    stop=(not mixed)), [self.kcT, self.qn], [S])
                if mixed:
                    mo = 512 * (5 - k)
                    P.op("pe", lambda e, S=S, h=h, mo=mo: e.matmul(S[:, :], lhsT=self.ident[:],
                                                                   rhs=self.uslice(h, mo), start=False, stop=True),
                         [self.ident, self.Ubig], [S])
                pT = self.rot("pT", self.pT)
                self.exp(S, pT, self.cmp_bias(cc, h, mixed))
                pts.append(pT)
            n = len(chunks)
            for i, pT in enumerate(pts):
                P.op("pe", lambda e, pT=pT, i=i: e.matmul(self.X0[:, :], lhsT=self.ones[:], rhs=pT[:], start=(i == 0),
                                                          stop=(i == n - 1)), [self.ones, pT], [self.X0])
            if h < 3:
                for i, (cc, pT) in enumerate(zip(chunks, pts)):
                    P.op("pe", lambda e, pT=pT, i=i, cc=cc, h=h: e.matmul(
                        self.A[h][:, :], lhsT=self.vcaug[:, cc, :], rhs=pT[:], start=(i == 0), stop=(i == n - 1)),
                        [self.vcaug, pT], [self.A[h]])
            P.op("dve", lambda e: e.tensor_scalar(out=self.rdb[:], in0=self.X0[:, :], scalar1=1e-30, scalar2=None,
                                                  op0=ALU.max), [self.X0], [self.rdb])
            P.op("dve", lambda e: e.reciprocal(out=self.rdb[:], in_=self.rdb[:]), [self.rdb], [self.rdb])
            ptn = []
            for pT in pts:
                pn = self.rot("pTn", self.pTn)
                P.op("pool", lambda e, pT=pT, pn=pn: e.tensor_tensor(out=pn[:], in0=pT[:], in1=self.rdb[:], op=ALU.mult),
                     [pT, self.rdb], [pn])
                ptn.append(pn)
            for ss in range(4):
                for i, (cc, pn) in enumerate(zip(chunks, ptn)):
                    P.op("pe", lambda e, pn=pn, i=i, cc=cc, ss=ss: e.matmul(
                        self.X1[:, ss * 128:(ss + 1) * 128], lhsT=pn[:, ss * 128:(ss + 1) * 128], rhs=self.ovl[:, cc, :],
                        start=(i == 0), stop=(i == n - 1)), [pn, self.ovl], [self.X1])
            if h == 0:
                P.op("dve", lambda e: e.tensor_copy(out=self.impacc[:], in_=self.X1[:, :]), [self.X1], [self.impacc])
            else:
                P.op("dve", lambda e: e.tensor_tensor(out=self.impacc[:], in0=self.X1[:, :], in1=self.impacc[:],
                                                      op=ALU.add), [self.X1, self.impacc], [self.impacc])
            if h < 3:
                self.combine(qt, h, 0, True)
        for ss in range(4):
            tA = self.tcs[:, ss * 256: ss * 256 + 128]
            tB = self.tcs[:, ss * 256 + 128: ss * 256 + 256]
            P.op("dve", lambda e, ss=ss, tA=tA: e.tensor_tensor(out=self.sc[:], in0=self.impacc[:, ss * 128:(ss + 1) * 128],
                                                                in1=tA, op=ALU.mult), [self.impacc, self.tcs], [self.sc])
            P.op("dve", lambda e, tB=tB: e.tensor_tensor(out=self.sc[:], in0=self.sc[:], in1=tB, op=ALU.add),
                 [self.sc, self.tcs], [self.sc])
            P.op("dve", lambda e: e.max(out=self.mx[:, 0:8], in_=self.sc[:]), [self.sc], [self.mx])
            P.op("dve", lambda e: e.match_replace(out=self.sc2[:], in_to_replace=self.mx[:, 0:8], in_values=self.sc[:],
                                                  imm_value=-1e30), [self.sc, self.mx], [self.sc2])
            P.op("dve", lambda e: e.max(out=self.mx[:, 8:16], in_=self.sc2[:]), [self.sc2], [self.mx])
            P.op("dve", lambda e: e.tensor_scalar(out=self.mneg[:], in0=self.sc[:], scalar1=self.mx[:, 15:16], scalar2=1.0,
                                                  op0=ALU.is_ge, op1=ALU.subtract), [self.sc, self.mx], [self.mneg])
            P.op("pe", lambda e, ss=ss: e.transpose(self.XT[:, ss * 128:(ss + 1) * 128], self.mneg[:], self.ident[:]),
                 [self.mneg, self.ident], [self.XT])
        for h in range(3):
            for half in range(2):
                eng = "act" if (h + half) % 2 == 0 else "dve"
                if eng == "act":
                    P.op("act", lambda e, h=h, half=half: e.activation(out=self.qaug[64:128, h, half, :],
                                                                       in_=self.XT[half * 64:(half + 1) * 64, :],
                                                                       func=AF.Copy), [self.XT], [self.qaug])
                else:
                    P.op("dve", lambda e, h=h, half=half: e.tensor_copy(out=self.qaug[64:128, h, half, :],
                                                                        in_=self.XT[half * 64:(half + 1) * 64, :]),
                         [self.XT], [self.qaug])
        nkc = 4 * qt + 4
        for kc in range(nkc):
            k0 = 128 * kc
            ep = s0 + 511 - k0
            near = ep <= DS
            half = kc // 32
            for h in range(3):
                S = self.rot("S", self.S)
                P.op("pe", lambda e, S=S, k0=k0, h=h, half=half, near=near: e.matmul(
                    S[:, :], lhsT=self.ksaug[:, k0:k0 + 128], rhs=self.qaug[:, h, half, :], start=True, stop=(not near)),
                    [self.ksaug, self.qaug], [S])
                if near:
                    mo = DS - ep
                    P.op("pe", lambda e, S=S, h=h, mo=mo: e.matmul(S[:, :], lhsT=self.ident[:],
                                                                   rhs=self.TTs[h][:, mo:mo + 512], start=False, stop=True),
                         [self.ident, self.TTs[h]], [S])
                pT = self.rot("pT", self.pT)
                self.exp(S, pT, self.sel_bias(kc, h, near))
                P.op("pe", lambda e, pT=pT, kc=kc, h=h: e.matmul(self.A[h][:, :], lhsT=self.vsaug[:, kc, :], rhs=pT[:],
                                                                 start=(kc == 0), stop=(kc == nkc - 1)),
                     [self.vsaug, pT], [self.A[h]])
        for h in range(3):
            self.combine(qt, h, 1, False)
        rs = [r for r in range(8) if s0 - 512 + 128 * r >= 0]
        for i, r in enumerate(rs):
            k0 = s0 - 512 + 128 * r
            mo = 128 * r
            for h in range(3):
                S = self.rot("S", self.S)
                P.op("pe", lambda e, S=S, k0=k0, h=h: e.matmul(S[:, :], lhsT=self.kwT[:, k0:k0 + 128],
                                                               rhs=self.qn[:, h, :], start=True, stop=False),
                     [self.kwT, self.qn], [S])
                P.op("pe", lambda e, S=S, h=h, mo=mo: e.matmul(S[:, :], lhsT=self.ident[:], rhs=self.TTw[h][:, mo:mo + 512],
                                                               start=False, stop=True), [self.ident, self.TTw[h]], [S])
                pT = self.rot("pT", self.pT)
                self.exp(S, pT, self.win_bias(k0 // 128))
                P.op("pe", lambda e, pT=pT, k0=k0, h=h, i=i: e.matmul(self.A[h][:, :], lhsT=self.vwaug[:, k0 // 128, :],
                                                                      rhs=pT[:], start=(i == 0), stop=(i == len(rs) - 1)),
                     [self.vwaug, pT], [self.A[h]])
        for h in range(3):
            self.combine(qt, h, 2, False)
        self.store_y(qt)

    def build(self, nqt=NQT):
        self.setup()
        for qt in range(nqt):
            self.qtile(qt)
        self.P.wait_all("sp", [self.yT])
        self.P.finalize()


def build_attn(nqt=NQT):
    nc = bass.Bass("TRN2", target_bir_lowering=False)
    P = Prog(nc)
    a = Attn(P)
    a.build(nqt)
    return nc, P


XPAD = SEQ + HALO


def rev_ap(ap, n):
    dims = [list(d) for d in ap.ap]
    dims[-1] = [-1, n]
    return bass.AP(tensor=ap.tensor, offset=ap.offset + (n - 1), ap=dims)


class FTok(TokStage):
    def __init__(self, P, mode, dr):
        self.P = P
        self.mode = mode
        self.dr = dr
        for k in ("memT", "gains", "hg", "mem_w_kv", "w_out", "w_up", "w_down", "b_w_in", "a_w_in", "convw", "w_kv"):
            setattr(self, k, dr[k])
        self.xsrc = dr["xin"] if mode == "L1" else dr["X"]
        self.kvT_o, self.qT_o, self.gT_o = dr["KV"], dr["Q"], dr["G"]
        self.ytokT = dr["Y"]
        self.chunk = 0
        self.alloc()

    def halo_fix(self, sub):
        if self.chunk == 0:
            self.P.op("pool", lambda e: e.memset(self.v[:, sub, 2:2 + HALO], 0.0), [], [self.v])

    def store_tile(self, dstT, r, msz, c0, t0, w, ost):
        a = c0 + t0
        lo = HALO if a == 0 else 0
        g0 = NTOK * self.chunk + a + lo - HALO
        self.P.dma("sp", dstT, dstT[r:r + msz, g0:g0 + (w - lo)], ost, ost[0:msz, lo:w], nodep_out=True)

    def ytok_src(self, ch, c0, t0, w):
        g0 = NTOK * self.chunk + c0 + t0
        return self.ytokT[ch * 128:(ch + 1) * 128, g0:g0 + w]

    def setup_consts(self):
        P = self.P
        P.dma("sp", self.g, self.g[:], self.gains, self.gains[:])
        P.dma("sp", self.hgs, self.hgs[:], self.hg, self.hg[:])
        if self.mode == "L1":
            P.dma("sp", self.cw, self.cw[:], self.convw, self.convw[:])
        for c in range(8):
            P.dma("sp", self.u, self.memx(c, 0, 256), self.memT, self.memT[c * 128:(c + 1) * 128, :], nodep_out=True)
        P.op("pool", lambda e: e.memset(self.ones[:], 1.0), [], [self.ones])
        P.op("pool", lambda e: e.memset(self.bd[:], 0.0), [], [self.bd])
        P.op("pool", lambda e: e.memset(self.bd[0:64, 0:64], 1.0), [], [self.bd])
        P.op("pool", lambda e: e.memset(self.bd[64:128, 64:128], 1.0), [], [self.bd])
        P.op("pool", lambda e: e.memset(self.epsT[:], EPS), [], [self.epsT])
        P.op("pool", lambda e: e.memset(self.vaug[:], 1.0), [], [self.vaug])
        self.norm(self.u, self.memx, 256, [(0, 256)], 15, self.memh, 0)

    def load_x(self):
        P = self.P
        g0 = NTOK * self.chunk
        for c in range(8):
            P.dma("sp", self.x, self.x[:, c, :], self.xsrc, self.xsrc[c * 128:(c + 1) * 128, g0:g0 + NCOL],
                  nodep_out=(c > 0))

    def store_x(self):
        P = self.P
        dst = self.dr["out"] if self.mode == "L5" else self.dr["X"]
        off = 0 if self.mode == "L5" else HALO
        g0 = NTOK * self.chunk + off
        for c in range(8):
            P.dma("sp", dst, dst[c * 128:(c + 1) * 128, g0:g0 + NTOK], self.x, self.x[:, c, HALO:NCOL], nodep_out=True)

    def run(self):
        self.setup_consts()
        for chunk in range(4):
            self.chunk = chunk
            self.load_x()
            if self.mode == "L1":
                for l in range(2):
                    self.mem_kv(l)
                    for ri, (c0, n) in enumerate(RANGES):
                        tiles = tiles_of(n)
                        self.norm(self.x, self.xs(c0), n, tiles, l, self.h, 0)
                        self.mix_a(l, ri, c0, n, tiles)
                        self.mem_attn(l, n, tiles)
                        self.out_proj(l, c0, n, tiles)
                        self.mlp(l, c0, n, tiles)
                for ri, (c0, n) in enumerate(RANGES):
                    tiles = tiles_of(n)
                    self.norm(self.x, self.xs(c0), n, tiles, 8, self.h, 0)
                    for i in range(3):
                        self.store_proj(self.w_kv, self.w_kv[:, 256 * i:256 * (i + 1)], 256, self.kvT_o, 256 * i, c0, tiles)
                    self.pre_b(0, c0, n, tiles)
            else:
                j = 0 if self.mode == "L3" else 1
                self.mem_kv(2 + j)
                for ri, (c0, n) in enumerate(RANGES):
                    self.post_b(j, c0, n, tiles_of(n))
                if j == 0:
                    for ri, (c0, n) in enumerate(RANGES):
                        self.pre_b(1, c0, n, tiles_of(n))
            self.store_x()


class FAttn(Attn):
    def __init__(self, P, dr, jl):
        self.P = P
        self.dr = dr
        self.jl = jl
        self.Q, self.G, self.KV, self.Y = dr["Q"], dr["G"], dr["KV"], dr["Y"]
        self.hg = dr["ahg%d" % jl]
        self.pos, self.w1, self.w2 = dr["pos"], dr["cmp_w1"], dr["cmp_w2"]
        self.rel = dr["rel_bias"]
        self.OH, self.identD, self.EE, self.ovlD, self.topc = dr["OH"], dr["ident"], dr["EE"], dr["ovl"], dr["topc"]
        self.Rrow = dr["Rrow"]
        self.alloc()

    def alloc(self):
        Attn.alloc(self)
        self.ub_view = self.Ubig[0:64, 0:16 * 513].rearrange("p (r n) -> p r n", n=513)

    def set_combo(self, g, tr):
        self.g, self.tr = g, tr
        own = [3 * tr + i for i in range(3)]
        oth = [3 * (1 - tr) + i for i in range(3)]
        self.heads = own + oth

    def load_rb(self):
        P = self.P
        rel = self.rel[:]
        for i, hh in enumerate(self.heads):
            col = self.g * 6 + hh
            P.dma("sp", self.cf, self.cf[:, i:i + 1], self.rel,
                  bass.AP(tensor=rel.tensor, offset=31 * 12 + col, ap=[[0, 128], [1, 1]]))
            P.dma("sp", self.tab, self.tab[0:32, i:i + 1], self.rel, self.rel[:, col:col + 1],
                  allow_slow_non_contiguous=True)

    def load_kv_piece(self, s, slot, c0):
        r0 = slot * 128 + self.g * 64
        self.P.dma("sp", s, s[0:64, :], self.KV, self.KV[r0:r0 + 64, c0:c0 + 1024])

    def load_v_piece(self, piece):
        P = self.P
        c0 = piece * 1024
        for slot, dst in ((3, self.vsaug), (5, self.vwaug)):
            s = self.rot("st", self.st)
            self.load_kv_piece(s, slot, c0)
            for hf in range(2):
                sq = self.rot("sq", self.sq)
                P.op("pool", lambda e, s=s, sq=sq, hf=hf: e.tensor_copy(out=sq[0:64, :], in_=s[0:64, hf * 512:(hf + 1) * 512]),
                     [s], [sq])
                for c in range(4):
                    P.op("pe", lambda e, sq=sq, c=c: e.transpose(self.XT[:, c * 64:(c + 1) * 64],
                                                                 sq[0:64, c * 128:(c + 1) * 128], self.ident[0:64, 0:64]),
                         [sq, self.ident], [self.XT])
                ch0 = piece * 8 + hf * 4
                P.op("act", lambda e, dst=dst, ch0=ch0: e.activation(
                    out=dst[:, ch0:ch0 + 4, 0:64], in_=self.XT[:, 0:256].rearrange("p (c d) -> p c d", d=64),
                    func=AF.Copy), [self.XT], [dst])

    def compress_begin(self, kvi):
        P = self.P
        P.op("pool", lambda e: e.memset(self.ub_view[:, :, 512:513], 0.0), [], [self.Ubig])
        for piece in range(8):
            s = self.rot("st", self.st)
            self.load_kv_piece(s, kvi, piece * 1024)
            P.op("pool", lambda e, s=s, piece=piece: e.tensor_copy(
                out=self.ub_view[:, :, piece * 64:(piece + 1) * 64],
                in_=s[0:64, :].rearrange("p (n r) -> p r n", r=16)), [s], [self.Ubig])

    def compress_rhs(self, kvi, tg):
        P = self.P
        for tl in range(4):
            t = tg * 4 + tl
            r, off = t % 16, t // 16
            P.op("dve", lambda e, tl=tl, t=t, r=r, off=off: e.tensor_scalar(
                out=self.blkb[:, tl, :], in0=self.ub_view[:, r, off:off + 512], scalar1=self.posT[:, t:t + 1],
                scalar2=None, op0=ALU.add), [self.Ubig, self.posT], [self.blkb])

    def load_q(self, qt):
        P = self.P
        s0 = qt * QT
        for i2 in range(3):
            s = self.rot("st", self.st)
            for k in range(2):
                hh = self.heads[2 * i2 + k]
                r0 = (self.g * 6 + hh) * 64
                P.dma("sp", s, s[0:64, k * 512:(k + 1) * 512], self.Q, self.Q[r0:r0 + 64, s0:s0 + QT], nodep_out=(k > 0))
            for k in range(2):
                P.op("dve", lambda e, s=s, k=k, i2=i2: e.tensor_copy(
                    out=self.qraw[:, 2 * i2 + k, :], in_=rev_ap(s[0:64, k * 512:(k + 1) * 512], 512)), [s], [self.qraw])

    def load_gate(self, qt, h, br):
        P = self.P
        s0 = qt * QT
        gt = self.rot("gt", self.gt)
        g = self.G[:]
        row = (self.g * 6 + 3 * self.tr + h) * 3 + br
        P.dma("sp", gt, gt[:], self.G, bass.AP(tensor=g.tensor, offset=row * SEQ + s0, ap=[[0, 64], [1, 512]]))
        gr = self.rot("gtr", self.gtr)
        P.op("dve", lambda e: e.tensor_copy(out=gr[:], in_=rev_ap(gt[:], 512)), [gt], [gr])
        return gr

    def store_y(self, qt):
        P = self.P
        s0 = qt * QT
        bufs = [self.rdt, self.wt, self.tm]
        for h in range(3):
            b = bufs[h]
            P.op("dve", lambda e, b=b, h=h: e.tensor_copy(out=b[:], in_=rev_ap(self.y[:, h, :], 512)), [self.y], [b])
            r0 = (self.g * 6 + 3 * self.tr + h) * 64
            P.dma("sp", self.Y, self.Y[r0:r0 + 64, HALO + s0:HALO + s0 + QT], b, b[:], nodep_out=True)

    def run(self):
        P = self.P
        st = self.st
        self.gtr = [P.sb("gtr%d" % i, [64, 512], F32) for i in range(2)]
        P.dma("sp", self.hgs, self.hgs[:], self.hg, self.hg[:])
        P.op("pool", lambda e: e.memset(self.ones[:], 1.0), [], [self.ones])
        P.op("pool", lambda e: e.memset(self.epsT[:], 1e-6), [], [self.epsT])
        P.op("pool", lambda e: e.memset(self.vsaug[:], 1.0), [], [self.vsaug])
        P.op("pool", lambda e: e.memset(self.vwaug[:], 1.0), [], [self.vwaug])
        P.op("pool", lambda e: e.memset(self.vcaug[:], 1.0), [], [self.vcaug])
        s = self.rot("st", st)
        P.dma("sp", s, s[:, 0:128], self.identD, self.identD[:])
        P.op("pool", lambda e, s=s: e.tensor_copy(out=self.ident[:], in_=s[:, 0:128]), [s], [self.ident])
        s = self.rot("st", st)
        P.dma("sp", s, s[:, 0:512], self.ovlD, self.ovlD[:].rearrange("p c j -> p (c j)"))
        P.op("pool", lambda e, s=s: e.tensor_copy(out=self.ovl[:].rearrange("p c j -> p (c j)"), in_=s[:, 0:512]),
             [s], [self.ovl])
        for g in range(2):
            self.set_combo(g, 0)
            P.barrier()
            self.setup_kv()
            for tr in range(2):
                self.set_combo(g, tr)
                P.barrier()
                self.setup_tables()
                for qt in range(NQT):
                    self.qtile(qt)


def fused_dram(P, r3=False):
    dr = {}
    D = lambda n, s: P.dram(n, s, F32, kind="ExternalInput")
    dr["xin"] = D("xin", [1024, XPAD])
    dr["memT"] = D("memT", [1024, 256])
    dr["gains"] = D("gains", [128, 128])
    dr["hg"] = D("hg", [128, 16])
    dr["convw"] = D("convw", [128, 36])
    dr["mem_w_kv"] = D("mem_w_kv", [4, 1024, 512])
    dr["w_out"] = D("w_out", [4, 1024, 1024])
    dr["w_up"] = D("w_up", [4, 1024, 4096])
    dr["w_down"] = D("w_down", [4, 4096, 1024])
    dr["b_w_in"] = D("b_w_in", [2, 1024, 1060])
    dr["a_w_in"] = D("a_w_in", [2, 1024, 2560])
    dr["w_kv"] = D("w_kv", [1024, 768])
    dr["ahg0"] = D("ahg0", [128, 8])
    dr["ahg1"] = D("ahg1", [128, 8])
    dr["pos"] = D("pos", [2, 64, 32])
    dr["cmp_w1"] = D("cmp_w1", [2, 2048, 256])
    dr["cmp_w2"] = D("cmp_w2", [2, 256, 64])
    dr["rel_bias"] = D("rel_bias", [32, 12])
    dr["OH"] = D("OH", [33, LTOT])
    dr["ident"] = D("ident", [128, 128])
    dr["EE"] = D("EE", [64, SEQ])
    dr["ovl"] = D("ovl", [128, 4, 128])
    if not r3:
        dr["topc"] = D("topc", [NQT, 128, 1024])
    if not r3:
        dr["out"] = P.dram("xT_out", [1024, SEQ], F32, kind="ExternalOutput")
    I = lambda n, s: P.dram(n, s, F32, kind="Internal")
    dr["X"] = I("Xs", [1024, XPAD])
    if not r3:
        dr["KV"] = I("KVs", [768, SEQ])
    dr["Q"] = I("Qs", [768, SEQ])
    dr["G"] = I("Gs", [36, SEQ])
    dr["Y"] = I("Ys", [768, XPAD])
    dr["Rrow"] = I("Rrow", [6, LTOT])
    return dr


def build_fused():
    nc = bass.Bass("TRN2", target_bir_lowering=False)
    P = Prog(nc)
    dr = fused_dram(P)
    P.push_scope()
    FTok(P, "L1", dr).run()
    P.pop_scope()
    for j in range(2):
        P.push_scope()
        FAttn(P, dr, j).run()
        P.pop_scope()
        P.push_scope()
        FTok(P, "L3" if j == 0 else "L5", dr).run()
        P.pop_scope()
    P.wait_all("sp", [dr["out"]])
    P.finalize()
    return nc, P


KPAD = 1536
QTILES = (3, 7, 11, 15)


def host_core_tables(k):
    tc = np.zeros((4, 128, 4, 2, 128), np.float32)
    jj = np.arange(128)[None, :]
    for j, qt in enumerate(QTILES):
        for ss in range(4):
            ta = qt * QT + 511 - (128 * ss + np.arange(128))
            back = (ta // 64)[:, None] - jj
            J = jj - 24 + 8 * k
            valid = (J >= 0) & (back >= 0)
            forced = (J == 0) | ((back >= 0) & (back < 2))
            tc[j, :, ss, 0] = (valid & ~forced).astype(np.float32)
            tc[j, :, ss, 1] = np.where(valid, np.where(forced, 1e4, 0.0), -1e30).astype(np.float32)
    kval = np.zeros((128, 16), np.float32)
    first = KPAD - 512 * k
    for c in range(12):
        a = 128 * c + np.arange(128)
        kval[:, c] = np.where(a >= first, 0.0, MASKV)
    kval[:, 12] = np.where(16 * np.arange(128) >= first, 0.0, MASKV)
    return tc.reshape(4, 128, 1024), kval


class FTok2(FTok):
    def __init__(self, P, mode, dr):
        FTok.__init__(self, P, mode, dr)
        if mode != "L1":
            self.xsrc = dr["Xown"]
            self.ytokT = dr["Yown"]
            self.qT_o, self.gT_o = dr["Qown"], dr["Gown"]

    def store_tile(self, dstT, r, msz, c0, t0, w, ost):
        if self.mode == "L1":
            off = KPAD if dstT is self.dr["KV"] else 0
            a = c0 + t0
            lo = HALO if a == 0 else 0
            g0 = NTOK * self.chunk + a + lo - HALO + off
            self.P.dma("sp", dstT, dstT[r:r + msz, g0:g0 + (w - lo)], ost, ost[0:msz, lo:w], nodep_out=True)
        else:
            self.P.dma("sp", dstT, dstT[r:r + msz, c0 + t0:c0 + t0 + w], ost, ost[0:msz, 0:w], nodep_out=True)

    def ytok_src(self, ch, c0, t0, w):
        if self.mode == "L1":
            return FTok.ytok_src(self, ch, c0, t0, w)
        return self.ytokT[ch * 128:(ch + 1) * 128, c0 + t0:c0 + t0 + w]

    def load_x(self):
        if self.mode == "L1":
            return FTok.load_x(self)
        P = self.P
        for c in range(8):
            P.dma("sp", self.x, self.x[:, c, 0:NTOK], self.xsrc, self.xsrc[c * 128:(c + 1) * 128, :], nodep_out=(c > 0))

    def store_x(self):
        if self.mode == "L1":
            return FTok.store_x(self)
        P = self.P
        dst = self.dr["out"] if self.mode == "L5" else self.dr["Xown"]
        for c in range(8):
            P.dma("sp", dst, dst[c * 128:(c + 1) * 128, :], self.x, self.x[:, c, 0:NTOK], nodep_out=True)

    def zero_kv_pad(self):
        P = self.P
        z = self.ost[0]
        P.op("pool", lambda e: e.memset(z[:], 0.0), [], [z])
        KV = self.dr["KV"]
        for r in range(6):
            for cb in range(3):
                P.dma("sp", KV, KV[r * 128:(r + 1) * 128, cb * 512:(cb + 1) * 512], z, z[:], nodep_out=True)

    def run(self):
        if self.mode == "L1":
            self.zero_kv_pad()
            return FTok.run(self)
        self.setup_consts()
        self.chunk = 0
        self.load_x()
        j = 0 if self.mode == "L3" else 1
        self.mem_kv(2 + j)
        R2 = [(0, 1024), (1024, 1024)]
        for (c0, n) in R2:
            self.post_b(j, c0, n, tiles_of(n))
        if j == 0:
            for (c0, n) in R2:
                self.pre_b(1, c0, n, tiles_of(n))
        self.store_x()


def extract_own(P, dr):
    kd = dr["kd"]

    def dyn3(src_t, r0, r1, pad, ncols):
        base = src_t[r0:r1, pad:pad + KPAD + 6144 + 512][:, bass.ds(kd, 6144 + 512)]
        return bass.AP(tensor=base.tensor, offset=base.offset, ap=[[ncols, r1 - r0], [2048, 4], [1, 512]])
    KVp, KVw = dr["KV"], dr["KVwin"]
    for rb in range(3):
        P.dma("sp", KVw, KVw[rb * 256:(rb + 1) * 256, :], KVp,
              KVp[rb * 256:(rb + 1) * 256, 0:KPAD + SEQ][:, bass.ds(kd, SEQ)], nodep_out=True)
    for rb in range(4):
        P.dma("sp", dr["Xown"], dr["Xown"][rb * 256:(rb + 1) * 256, :].rearrange("r (s c) -> r s c", c=512), dr["X"],
              dyn3(dr["X"], rb * 256, (rb + 1) * 256, HALO, XPAD), nodep_out=True)
    for rb in range(3):
        P.dma("sp", dr["Qown"], dr["Qown"][rb * 256:(rb + 1) * 256, :].rearrange("r (s c) -> r s c", c=512), dr["Q"],
              dyn3(dr["Q"], rb * 256, (rb + 1) * 256, 0, SEQ), nodep_out=True)
    P.dma("sp", dr["Gown"], dr["Gown"][:, :].rearrange("r (s c) -> r s c", c=512), dr["G"],
          dyn3(dr["G"], 0, 36, 0, SEQ), nodep_out=True)


class FAttn2(FAttn):
    def __init__(self, P, dr, jl):
        FAttn.__init__(self, P, dr, jl)
        self.KV = dr["KVwin"]
        self.Q, self.G, self.Y = dr["Qown"], dr["Gown"], dr["Yown"]
        self.kvalD = dr["kval"]

    def load_q(self, qt):
        P = self.P
        c0 = ((qt - 3) // 4) * 512
        for i2 in range(3):
            s = self.rot("st", self.st)
            for k in range(2):
                hh = self.heads[2 * i2 + k]
                r0 = (self.g * 6 + hh) * 64
                P.dma("sp", s, s[0:64, k * 512:(k + 1) * 512], self.Q, self.Q[r0:r0 + 64, c0:c0 + 512], nodep_out=(k > 0))
            for k in range(2):
                P.op("dve", lambda e, s=s, k=k, i2=i2: e.tensor_copy(
                    out=self.qraw[:, 2 * i2 + k, :], in_=rev_ap(s[0:64, k * 512:(k + 1) * 512], 512)), [s], [self.qraw])

    def load_gate(self, qt, h, br):
        P = self.P
        c0 = ((qt - 3) // 4) * 512
        gt = self.rot("gt", self.gt)
        g = self.G[:]
        row = (self.g * 6 + 3 * self.tr + h) * 3 + br
        P.dma("sp", gt, gt[:], self.G, bass.AP(tensor=g.tensor, offset=row * NTOK + c0, ap=[[0, 64], [1, 512]]))
        gr = self.rot("gtr", self.gtr)
        P.op("dve", lambda e: e.tensor_copy(out=gr[:], in_=rev_ap(gt[:], 512)), [gt], [gr])
        return gr

    def store_y(self, qt):
        P = self.P
        c0 = ((qt - 3) // 4) * 512
        bufs = [self.rdt, self.wt, self.tm]
        for h in range(3):
            b = bufs[h]
            P.op("dve", lambda e, b=b, h=h: e.tensor_copy(out=b[:], in_=rev_ap(self.y[:, h, :], 512)), [self.y], [b])
            r0 = (self.g * 6 + 3 * self.tr + h) * 64
            P.dma("sp", self.Y, self.Y[r0:r0 + 64, c0:c0 + 512], b, b[:], nodep_out=True)

    def topc_ap(self, qt):
        return self.topc[(qt - 3) // 4]

    def cmp_bias(self, cc, h, mixed):
        if cc == 0:
            return (self.kval[:, 12:13], self.kval) if mixed else (self.cfc[:, h:h + 1], self.cfc)
        return None if mixed else (self.cf[:, h:h + 1], self.cf)

    def sel_bias(self, kc, h, near):
        if kc < 12:
            return (self.kval[:, kc:kc + 1], self.kval) if near else (self.vbf[:, kc, h:h + 1], self.vbf)
        return None if near else (self.cf[:, h:h + 1], self.cf)

    def win_bias(self, c):
        return (self.kval[:, c:c + 1], self.kval) if c < 12 else None

    def setup_tables(self):
        FAttn.setup_tables(self)
        P = self.P
        for c in range(12):
            P.op("dve", lambda e, c=c: e.tensor_scalar(out=self.vbf[:, c, :], in0=self.cf[:, 0:3],
                                                       scalar1=self.kval[:, c:c + 1], scalar2=None, op0=ALU.add),
                 [self.cf, self.kval], [self.vbf])
        P.op("dve", lambda e: e.tensor_scalar(out=self.cfc[:], in0=self.cf[:], scalar1=self.kval[:, 12:13], scalar2=None,
                                              op0=ALU.add), [self.cf, self.kval], [self.cfc])

    def run(self):
        P = self.P
        st = self.st
        self.gtr = [P.sb("gtr%d" % i, [64, 512], F32) for i in range(2)]
        self.kval = P.sb("kval", [128, 16], F32)
        self.vbf = P.sb("vbf", [128, 12, 3], F32)
        self.cfc = P.sb("cfc", [128, 6], F32)
        P.dma("sp", self.kval, self.kval[:], self.kvalD, self.kvalD[:])
        P.dma("sp", self.hgs, self.hgs[:], self.hg, self.hg[:])
        P.op("pool", lambda e: e.memset(self.ones[:], 1.0), [], [self.ones])
        P.op("pool", lambda e: e.memset(self.epsT[:], 1e-6), [], [self.epsT])
        P.op("pool", lambda e: e.memset(self.vsaug[:], 1.0), [], [self.vsaug])
        P.op("pool", lambda e: e.memset(self.vwaug[:], 1.0), [], [self.vwaug])
        P.op("pool", lambda e: e.memset(self.vcaug[:], 1.0), [], [self.vcaug])
        s = self.rot("st", st)
        P.dma("sp", s, s[:, 0:128], self.identD, self.identD[:])
        P.op("pool", lambda e, s=s: e.tensor_copy(out=self.ident[:], in_=s[:, 0:128]), [s], [self.ident])
        s = self.rot("st", st)
        P.dma("sp", s, s[:, 0:512], self.ovlD, self.ovlD[:].rearrange("p c j -> p (c j)"))
        P.op("pool", lambda e, s=s: e.tensor_copy(out=self.ovl[:].rearrange("p c j -> p (c j)"), in_=s[:, 0:512]),
             [s], [self.ovl])
        for g in range(2):
            self.set_combo(g, 0)
            self.setup_kv()
            for tr in range(2):
                self.set_combo(g, tr)
                self.setup_tables()
                for qt in QTILES:
                    self.qtile(qt)


def build_fused2():
    nc = bass.Bass("TRN2", target_bir_lowering=False)
    P = Prog(nc)
    dr = fused_dram(P, r3=True)
    I = lambda n, s: P.dram(n, s, F32, kind="Internal")
    dr["KV"] = I("KVp", [768, KPAD + SEQ])
    dr["KVwin"] = I("KVwin", [768, SEQ])
    dr["Xown"] = I("Xown", [1024, NTOK])
    dr["Qown"] = I("Qown", [768, NTOK])
    dr["Gown"] = I("Gown", [36, NTOK])
    dr["Yown"] = I("Yown", [768, NTOK])
    dr["topc"] = P.dram("topc4", [4, 128, 1024], F32, kind="ExternalInput")
    dr["kval"] = P.dram("kval", [128, 16], F32, kind="ExternalInput")
    dr["out"] = P.dram("xT_own", [1024, NTOK], F32, kind="ExternalOutput")
    dr["kd"] = (nc.partition_id() % 4) * 512
    P.push_scope()
    FTok2(P, "L1", dr).run()
    P.pop_scope()
    extract_own(P, dr)
    for j in range(2):
        P.push_scope()
        FAttn2(P, dr, j).run()
        P.pop_scope()
        P.push_scope()
        FTok2(P, "L3" if j == 0 else "L5", dr).run()
        P.pop_scope()
    P.wait_all("sp", [dr["out"]])
    P.finalize()
    return nc, P


from concourse.bass_utils import run_bass_kernel_spmd


def attn_hg(inp, jlayer):
    hg = np.zeros((128, 8), np.float32)
    hg[:, 0] = np.tile(inp["b_q_gain"][jlayer], 2)
    for i in range(3):
        hg[:, 1 + i] = np.tile(inp["k_gain"][i], 2)
    return hg


def kernel(**inputs):
    inp = {k: np.ascontiguousarray(np.asarray(v, dtype=np.float32)) for k, v in inputs.items()}
    common = dict(host_consts())
    del common["topc"]
    common.update(gains=host_gains(inp), hg=host_hg(inp), convw=host_convw(inp), mem_w_kv=inp["mem_w_kv"],
                  w_out=inp["w_out"], w_up=inp["w_up"], w_down=inp["w_down"], b_w_in=inp["b_w_in"],
                  a_w_in=inp["a_w_in"], w_kv=inp["w_kv_shared"], ahg0=attn_hg(inp, 0), ahg1=attn_hg(inp, 1),
                  pos=np.ascontiguousarray(inp["cmp_pos"].transpose(0, 2, 1)), cmp_w1=inp["cmp_w1"],
                  cmp_w2=inp["cmp_w2"], rel_bias=inp["rel_bias"])
    per_b = []
    for b in range(2):
        xin = np.zeros((1024, XPAD), np.float32)
        xin[:, HALO:] = inp["x"][b].T
        per_b.append(dict(xin=xin, memT=np.ascontiguousarray(inp["mem"][b].T)))
    per_k = []
    for k in range(4):
        tc, kval = host_core_tables(k)
        per_k.append(dict(topc4=tc, kval=kval))
    maps = []
    for c in range(8):
        m = dict(common)
        m.update(per_b[c // 4])
        m.update(per_k[c % 4])
        maps.append(m)
    nc = build_fused2()[0]
    r = run_bass_kernel_spmd(nc, maps, core_ids=list(range(8))).results
    out = np.zeros((2, SEQ, 1024), np.float32)
    for c in range(8):
        b, k = c // 4, c % 4
        o = r[c]["xT_own"]
        for j in range(4):
            t0 = 512 * (4 * j + k)
            out[b, t0:t0 + 512, :] = o[:, j * 512:(j + 1) * 512].T
    return out
```

```python
import numpy as np
import concourse.bass as bass
import concourse.mybir as mybir
from contextlib import ExitStack

F32 = mybir.dt.float32
BF16 = mybir.dt.bfloat16
AF = mybir.ActivationFunctionType
ALU = mybir.AluOpType
AX = mybir.AxisListType

ENGS = ("pe", "act", "dve", "pool", "sp")


class T:
    __slots__ = ("h", "name", "lw", "rd", "dsem")

    def __init__(self, h, name):
        self.h = h
        self.name = name
        self.lw = None
        self.rd = []
        self.dsem = None

    def __getitem__(self, k):
        return self.h[k]


class Prog:
    def __init__(self, nc):
        self.nc = nc
        self.es = ExitStack()
        self.ins = {e: [] for e in ENGS}
        self.nsem = 0
        self.dsems = []
        self.free_dsems = []
        self.marked = set()
        self.rw = {e: {} for e in ENGS}
        self.cur = self.es
        self.uid = 0
        self.last_tok = {e: None for e in ENGS}
        self.epoch = 0
        self.ep_of = {}

    def _prune(self, eng, deps):
        out = []
        w = self.rw[eng]
        for d in deps:
            key = (d[0], d[1]); val = d[2]
            if w.get(key, -1) >= val:
                continue
            w[key] = val
            out.append(d)
        return out

    def sb(self, name, shape, dt):
        self.uid += 1
        name = "%s_%d" % (name, self.uid)
        return T(self.cur.enter_context(self.nc.sbuf_tensor(name, list(shape), dt)), name)

    def ps(self, name, shape, dt=F32):
        self.uid += 1
        name = "%s_%d" % (name, self.uid)
        return T(self.cur.enter_context(self.nc.psum_tensor(name, list(shape), dt)), name)

    def push_scope(self):
        self.cur = ExitStack()

    def pop_scope(self):
        self.barrier()
        self.cur.close()
        self.cur = self.es
        self.epoch += 1

    def barrier(self):
        for e in ENGS:
            deps = [t for e2, t in self.last_tok.items() if e2 != e and t is not None]
            deps += [("d", s, c[1]) for s, c in enumerate(self.dsems) if c[1] > 0]
            deps = self._prune(e, deps)
            self.ins[e].append(dict(deps=deps, fn=None, dma=None))

    def dram(self, name, shape, dt, kind="Internal"):
        return T(self.nc.dram_tensor(name, list(shape), dt, kind=kind), name)

    def _new_dsem(self):
        h = self.es.enter_context(self.nc.semaphore("d%d" % len(self.dsems)))
        self.dsems.append([h, 0])
        return len(self.dsems) - 1

    def _deps(self, eng, reads, writes):
        deps = []
        for t in reads:
            if t.lw is not None:
                deps.append(t.lw)
        for t in writes:
            if t.lw is not None:
                deps.append(t.lw)
            deps.extend(t.rd)
        out = []
        for d in deps:
            if d[0] == "e":
                if d[1] == eng:
                    continue
            out.append(d)
        if eng != "pe":
            for t in reads:
                if t.lw is not None and t.lw[0] == "e" and t.lw[1] == eng:
                    out.append(t.lw)
        return out

    def op(self, eng, fn, reads=(), writes=()):
        deps = self._prune(eng, self._deps(eng, reads, writes))
        idx = len(self.ins[eng])
        self.ins[eng].append(dict(deps=deps, fn=fn, dma=None))
        tok = ("e", eng, idx)
        self.last_tok[eng] = tok
        self.ep_of[(eng, idx)] = self.epoch
        for t in reads:
            t.rd = [r for r in t.rd if not (r[0] == "e" and r[1] == eng)]
            t.rd.append(tok)
        for t in writes:
            t.lw = tok
            t.rd = []
        return tok

    def dma(self, eng, out_t, out_ap, in_t, in_ap, nodep_out=False, **kw):
        reads = [in_t] if in_t is not None else []
        writes = [out_t] if out_t is not None else []
        deps = []
        for t in reads:
            if t.lw is not None:
                deps.append(t.lw)
        for t in writes:
            if nodep_out:
                continue
            if t.lw is not None:
                deps.append(t.lw)
            deps.extend(t.rd)
        owner = out_t if (out_t is not None and out_t.dsem is not None) else None
        if owner is None:
            owner = out_t if out_t is not None else in_t
        if owner.dsem is None:
            owner.dsem = self._new_dsem()
        s = owner.dsem
        self.dsems[s][1] += 16
        tok = ("d", s, self.dsems[s][1])
        deps = self._prune(eng, deps)
        self.ins[eng].append(dict(deps=deps, fn=None, dma=(out_ap, in_ap, s, kw)))
        for t in reads:
            t.rd.append(tok)
        for t in writes:
            t.lw = tok
            t.rd = []
        return tok

    def wait_all(self, eng, tiles):
        deps = []
        for t in tiles:
            if t.lw is not None:
                deps.append(t.lw)
            deps.extend(t.rd)
        deps = self._prune(eng, deps)
        self.ins[eng].append(dict(deps=deps, fn=None, dma=None))

    def finalize(self):
        nc = self.nc
        for e in ENGS:
            for rec in self.ins[e]:
                for d in rec["deps"]:
                    if d[0] == "e":
                        self.marked.add((d[1], d[2]))
        count_at = {}
        for e in ENGS:
            c = {}
            for i, rec in enumerate(self.ins[e]):
                if (e, i) in self.marked:
                    ep = self.ep_of[(e, i)]
                    c[ep] = c.get(ep, 0) + 1
                    count_at[(e, i)] = c[ep]
        esem = {(e, ep): self.es.enter_context(nc.semaphore("s_%s_%d" % (e, ep)))
                for e in ENGS for ep in range(self.epoch + 1)}
        engobj = {"pe": "tensor", "act": "scalar", "dve": "vector", "pool": "gpsimd", "sp": "sync"}
        block = self.es.enter_context(nc.Block())
        stats = {}

        def make_body(e):
            def body(eng):
                waited = {}
                nwait = 0
                for i, rec in enumerate(self.ins[e]):
                    need = {}
                    for d in rec["deps"]:
                        if d[0] == "e":
                            key = ("e", (d[1], self.ep_of[(d[1], d[2])])); val = count_at[(d[1], d[2])]
                        else:
                            key = ("d", d[1]); val = d[2]
                        if waited.get(key, 0) >= val:
                            continue
                        if need.get(key, 0) < val:
                            need[key] = val
                    for key, val in need.items():
                        h = esem[key[1]] if key[0] == "e" else self.dsems[key[1]][0]
                        eng.wait_ge(h, val)
                        waited[key] = val
                        nwait += 1
                    if rec.get("cc") is not None:
                        rec["cc"](eng).then_inc(self.dsems[rec["ccsem"]][0], 16)
                    elif rec["dma"] is not None:
                        out_ap, in_ap, s, kw = rec["dma"]
                        try:
                            eng.dma_start(out=out_ap, in_=in_ap, **kw).then_inc(self.dsems[s][0], 16)
                        except Exception:
                            print("DMA FAILED:", e, "out=", out_ap, "in=", in_ap)
                            raise
                    elif rec["fn"] is not None:
                        ins = rec["fn"](eng)
                        if (e, i) in self.marked:
                            ins.then_inc(esem[(e, self.ep_of[(e, i)])], 1)
                stats[e] = (len(self.ins[e]), nwait)
            return body

        for e in ENGS:
            if self.ins[e]:
                getattr(block, engobj[e])(make_body(e))
        self.stats = stats
        self.es.close()


NTOK = 2048
HALO = 4
NCOL = NTOK + HALO
RANGES = [(0, 1028), (1028, 1024)]
EPS = 1e-6


def tiles_of(n):
    out = []
    o = 0
    while o < n:
        w = min(512, n - o)
        out.append((o, w))
        o += w
    return out


class TokStage:
    def __init__(self, P, mode):
        self.P = P
        self.mode = mode
        nc = P.nc
        D = lambda n, s: P.dram(n, s, F32, kind="ExternalInput")
        O = lambda n, s: P.dram(n, s, F32, kind="ExternalOutput")
        self.xT = D("xT", [1024, NCOL])
        self.memT = D("memT", [1024, 256])
        self.gains = D("gains", [128, 8 * 16])
        self.hg = D("hg", [128, 16])
        self.flag = D("flag", [128, 1])
        self.mem_w_kv = D("mem_w_kv", [4, 1024, 512])
        self.w_out = D("w_out", [4, 1024, 1024])
        self.w_up = D("w_up", [4, 1024, 4096])
        self.w_down = D("w_down", [4, 4096, 1024])
        self.b_w_in = D("b_w_in", [2, 1024, 1060])
        if mode == "L1":
            self.a_w_in = D("a_w_in", [2, 1024, 2560])
            self.convw = D("convw", [128, 2 * 6 * 3])
            self.w_kv = D("w_kv", [1024, 768])
            self.kvT_o = O("kvT_o", [768, NCOL])
        else:
            self.ytokT = D("ytokT", [768, NCOL])
        self.xT_o = O("xT_o", [1024, NCOL])
        if mode != "L5":
            self.qT_o = O("qT_o", [768, NCOL])
            self.gT_o = O("gT_o", [36, NCOL])

        self.alloc()

    def alloc(self):
        P = self.P
        sb = P.sb
        self.x = sb("x", [128, 8, NCOL], F32)
        self.h = sb("h", [128, 8, 1028], BF16)
        self.y = sb("y", [128, 8, 1028], BF16)
        self.act = sb("act", [128, 8, 1028], BF16)
        self.u = sb("u", [128, 2, 1028], F32)
        self.v = sb("v", [128, 2, 1030], F32)
        self.qm = sb("qm", [128, 2, 1028], F32)
        self.carry = sb("carry", [128, 6, 2], F32)
        self.rstd = sb("rstd", [128, 512], F32)
        self.sq = [sb("sq%d" % i, [128, 512], BF16) for i in range(2)]
        self.tmp = [sb("tmp%d" % i, [128, 512], F32) for i in range(3)]
        self.ost = [sb("ost%d" % i, [128, 512], F32) for i in range(2)]
        self.wb = [sb("wb%d" % i, [128, 8, 256], BF16) for i in range(4)]
        self.memh = sb("memh", [128, 8, 256], BF16)
        self.kraw = sb("kraw", [128, 2, 256], F32)
        self.memk = sb("memk", [128, 2, 256], BF16)
        self.vaug = sb("vaug", [128, 2, 4, 128], BF16)
        self.qn = sb("qn", [128, 512], BF16)
        self.eT = [sb("eT%d" % i, [128, 512], BF16) for i in range(2)]
        self.rd = sb("rd", [64, 512], F32)
        self.g = sb("g", [128, 8 * 16], F32)
        self.hgs = sb("hgs", [128, 16], F32)
        self.flg = sb("flg", [128, 1], F32)
        self.cw = sb("cw", [128, 36], F32)
        self.ones = sb("ones", [128, 128], BF16)
        self.bd = sb("bd", [128, 128], BF16)
        self.epsT = sb("epsT", [128, 1], F32)
        self.pp = [P.ps("pp%d" % i, [128, 512]) for i in range(4)]
        self.pl = [P.ps("pl%d" % i, [128, 512]) for i in range(2)]
        self.po = P.ps("po", [128, 512])
        self.pn = P.ps("pn", [128, 512])
        self.cnt = dict(wf=0, wb=0, pp=0, pl=0, sq=0, tmp=0, ost=0, eT=0)

    def rot(self, name, lst):
        i = self.cnt[name]
        self.cnt[name] += 1
        return lst[i % len(lst)]

    def stream(self, wT, w_ap, ncols):
        P = self.P
        wb = self.rot("wb", self.wb)
        P.dma("pool", wb, wb[:, :, 0:ncols], wT, w_ap.rearrange("(c p) n -> p c n", p=128))
        return wb

    def proj(self, wb, sub, msz, rhs_t, rhs_fn, tiles, consumer):
        P = self.P
        for (t0, w) in tiles:
            ps = self.rot("pp", self.pp)
            for kc in range(8):
                P.op("pe", lambda e, ps=ps, kc=kc, t0=t0, w=w: e.matmul(
                    ps[0:msz, 0:w], lhsT=wb[:, kc, sub * 128: sub * 128 + msz], rhs=rhs_fn(kc, t0, w),
                    start=(kc == 0), stop=(kc == 7)), [wb, rhs_t], [ps])
            consumer(ps, t0, w)

    def setup(self):
        P = self.P
        P.dma("sp", self.g, self.g[:], self.gains, self.gains[:])
        P.dma("sp", self.hgs, self.hgs[:], self.hg, self.hg[:])
        P.dma("sp", self.flg, self.flg[:], self.flag, self.flag[:])
        if self.mode == "L1":
            P.dma("sp", self.cw, self.cw[:], self.convw, self.convw[:])
        for c in range(8):
            P.dma("sp", self.x, self.x[:, c, :], self.xT, self.xT[c * 128:(c + 1) * 128, :], nodep_out=True)
            P.dma("sp", self.u, self.memx(c, 0, 256), self.memT, self.memT[c * 128:(c + 1) * 128, :], nodep_out=True)
        P.op("pool", lambda e: e.memset(self.ones[:], 1.0), [], [self.ones])
        P.op("pool", lambda e: e.memset(self.bd[:], 0.0), [], [self.bd])
        P.op("pool", lambda e: e.memset(self.bd[0:64, 0:64], 1.0), [], [self.bd])
        P.op("pool", lambda e: e.memset(self.bd[64:128, 64:128], 1.0), [], [self.bd])
        P.op("pool", lambda e: e.memset(self.epsT[:], EPS), [], [self.epsT])
        P.op("pool", lambda e: e.memset(self.vaug[:], 1.0), [], [self.vaug])
        P.op("pool", lambda e: e.memset(self.carry[:], 0.0), [], [self.carry])
        self.norm(self.u, self.memx, 256, [(0, 256)], 15, self.memh, 0)

    def gcol(self, which, c):
        return self.g[:, which * 8 + c: which * 8 + c + 1]

    def memx(self, c, t0, w):
        return self.u[:, c // 4, (c % 4) * 256 + t0: (c % 4) * 256 + t0 + w]

    def xs(self, c0):
        return lambda c, t0, w: self.x[:, c, c0 + t0: c0 + t0 + w]

    def norm(self, src, src_fn, n, tiles, which, dst, d0):
        P = self.P
        for (t0, w) in tiles:
            for c in range(8):
                sq = self.rot("sq", self.sq)
                P.op("act", lambda e, sq=sq, c=c, t0=t0, w=w: e.activation(
                    out=sq[:, 0:w], in_=src_fn(c, t0, w), func=AF.Square), [src], [sq])
                P.op("pe", lambda e, sq=sq, c=c, w=w: e.matmul(
                    self.pn[:, 0:w], lhsT=self.ones[:], rhs=sq[:, 0:w], start=(c == 0), stop=(c == 7)),
                    [self.ones, sq], [self.pn])
            P.op("act", lambda e, w=w: e.activation(out=self.rstd[:, 0:w], in_=self.pn[:, 0:w], func=AF.Sqrt,
                                                    bias=self.epsT[:], scale=1.0 / 1024.0),
                 [self.pn, self.epsT], [self.rstd])
            P.op("dve", lambda e, w=w: e.reciprocal(out=self.rstd[:, 0:w], in_=self.rstd[:, 0:w]),
                 [self.rstd], [self.rstd])
            for c in range(8):
                eng = "dve"
                P.op(eng, lambda e, c=c, t0=t0, w=w: e.scalar_tensor_tensor(
                    out=dst[:, c, d0 + t0: d0 + t0 + w], in0=src_fn(c, t0, w),
                    scalar=self.gcol(which, c), in1=self.rstd[:, 0:w], op0=ALU.mult, op1=ALU.mult),
                    [src, self.g, self.rstd], [dst])

    def headnorm(self, src_t, src_ap_fn, w, gain_ap, dst_t, dst_ap):
        P = self.P
        sq = self.rot("sq", self.sq)
        P.op("act", lambda e, sq=sq: e.activation(out=sq[:, 0:w], in_=src_ap_fn(), func=AF.Square), [src_t], [sq])
        P.op("pe", lambda e, sq=sq: e.matmul(self.pn[:, 0:w], lhsT=self.bd[:], rhs=sq[:, 0:w], start=True, stop=True),
             [self.bd, sq], [self.pn])
        P.op("act", lambda e: e.activation(out=self.rstd[:, 0:w], in_=self.pn[:, 0:w], func=AF.Sqrt,
                                           bias=self.epsT[:], scale=1.0 / 64.0), [self.pn, self.epsT], [self.rstd])
        P.op("dve", lambda e: e.reciprocal(out=self.rstd[:, 0:w], in_=self.rstd[:, 0:w]), [self.rstd], [self.rstd])
        P.op("dve", lambda e: e.scalar_tensor_tensor(out=dst_ap, in0=src_ap_fn(), scalar=gain_ap,
                                                     in1=self.rstd[:, 0:w], op0=ALU.mult, op1=ALU.mult),
             [src_t, self.hgs, self.rstd], [dst_t])

    def mem_kv(self, l):
        P = self.P
        wb = self.stream(self.mem_w_kv, self.mem_w_kv[l, :, 0:256], 256)
        for sub in range(2):
            def cons(ps, t0, w, sub=sub):
                P.op("act", lambda e: e.activation(out=self.kraw[:, sub, 0:256], in_=ps[:, 0:256], func=AF.Copy),
                     [ps], [self.kraw])
            self.proj(wb, sub, 128, self.memh, lambda kc, t0, w: self.memh[:, kc, t0:t0 + w], [(0, 256)], cons)
            self.headnorm(self.kraw, lambda sub=sub: self.kraw[:, sub, 0:256], 256,
                          self.hgs[:, 4 + l: 5 + l], self.memk, self.memk[:, sub, 0:256])
        wb = self.stream(self.mem_w_kv, self.mem_w_kv[l, :, 256:512], 256)
        for mc in range(2):
            ps = self.rot("pp", self.pp)
            for kc in range(8):
                P.op("pe", lambda e, ps=ps, kc=kc, mc=mc: e.matmul(
                    ps[:, 0:256], lhsT=self.memh[:, kc, mc * 128:(mc + 1) * 128], rhs=wb[:, kc, 0:256],
                    start=(kc == 0), stop=(kc == 7)), [self.memh, wb], [ps])
            for hh in range(4):
                P.op("act", lambda e, ps=ps, mc=mc, hh=hh: e.activation(
                    out=self.vaug[:, mc, hh, 0:64], in_=ps[:, hh * 64:(hh + 1) * 64], func=AF.Copy), [ps], [self.vaug])

    def mem_attn(self, l, n, tiles):
        P = self.P
        for (t0, w) in tiles:
            for j in range(2):
                self.headnorm(self.qm, lambda j=j, t0=t0, w=w: self.qm[:, j, t0:t0 + w], w,
                              self.hgs[:, l: l + 1], self.qn, self.qn[:, 0:w])
                for hh in range(2):
                    b0 = 64 * hh
                    hd = 2 * j + hh
                    ets = []
                    for mc in range(2):
                        pl = self.rot("pl", self.pl)
                        P.op("pe", lambda e, pl=pl, mc=mc, b0=b0, j=j, w=w: e.matmul(
                            pl[:, 0:w], lhsT=self.memk[b0:b0 + 64, j, mc * 128:(mc + 1) * 128],
                            rhs=self.qn[b0:b0 + 64, 0:w], start=True, stop=True), [self.memk, self.qn], [pl])
                        eT = self.rot("eT", self.eT)
                        P.op("act", lambda e, pl=pl, eT=eT, w=w: e.activation(
                            out=eT[:, 0:w], in_=pl[:, 0:w], func=AF.Exp, scale=0.125), [pl], [eT])
                        ets.append(eT)
                    for mc in range(2):
                        P.op("pe", lambda e, mc=mc, hd=hd, w=w, eT=ets[mc]: e.matmul(
                            self.po[:, 0:w], lhsT=self.vaug[:, mc, hd, :], rhs=eT[:, 0:w],
                            start=(mc == 0), stop=(mc == 1)), [self.vaug, ets[mc]], [self.po])
                    P.op("dve", lambda e, w=w: e.reciprocal(out=self.rd[0:64, 0:w], in_=self.po[64:128, 0:w]),
                         [self.po], [self.rd])
                    P.op("dve", lambda e, w=w, b0=b0, j=j, t0=t0: e.tensor_tensor(
                        out=self.y[b0:b0 + 64, 6 + j, t0:t0 + w], in0=self.po[0:64, 0:w], in1=self.rd[0:64, 0:w],
                        op=ALU.mult), [self.po, self.rd], [self.y])

    def add_into_x(self, c0):
        P = self.P

        def mk(ch):
            def cons(ps, t0, w):
                P.op("dve", lambda e: e.tensor_tensor(
                    out=self.x[:, ch, c0 + t0: c0 + t0 + w], in0=ps[:, 0:w], in1=self.x[:, ch, c0 + t0: c0 + t0 + w],
                    op=ALU.add), [ps, self.x], [self.x])
            return cons
        return mk

    def out_proj(self, l, c0, n, tiles):
        mk = self.add_into_x(c0)
        for blk in range(4):
            wb = self.stream(self.w_out, self.w_out[l, :, blk * 256:(blk + 1) * 256], 256)
            for sub in range(2):
                self.proj(wb, sub, 128, self.y, lambda kc, t0, w: self.y[:, kc, t0:t0 + w], tiles, mk(blk * 2 + sub))

    def mlp(self, l, c0, n, tiles):
        P = self.P
        self.norm(self.x, self.xs(c0), n, tiles, 4 + l, self.h, 0)
        mk = self.add_into_x(c0)
        for sg in range(4):
            for blk in range(4):
                wb = self.stream(self.w_up, self.w_up[l, :, sg * 1024 + blk * 256: sg * 1024 + (blk + 1) * 256], 256)
                for sub in range(2):
                    def cons(ps, t0, w, fc=blk * 2 + sub):
                        tmp = self.rot("tmp", self.tmp)
                        P.op("act", lambda e: e.activation(out=tmp[:, 0:w], in_=ps[:, 0:w], func=AF.Relu), [ps], [tmp])
                        P.op("dve", lambda e: e.tensor_tensor(out=self.act[:, fc, t0:t0 + w], in0=tmp[:, 0:w],
                                                              in1=tmp[:, 0:w], op=ALU.mult), [tmp], [self.act])
                    self.proj(wb, sub, 128, self.h, lambda kc, t0, w: self.h[:, kc, t0:t0 + w], tiles, cons)
            for blk in range(4):
                wb = self.stream(self.w_down, self.w_down[l, sg * 1024:(sg + 1) * 1024, blk * 256:(blk + 1) * 256], 256)
                for sub in range(2):
                    self.proj(wb, sub, 128, self.act, lambda kc, t0, w: self.act[:, kc, t0:t0 + w], tiles,
                              mk(blk * 2 + sub))

    def mix_a(self, l, ri, c0, n, tiles):
        P = self.P
        W = self.a_w_in
        hrhs = lambda kc, t0, w: self.h[:, kc, t0:t0 + w]
        for i in range(3):
            wb = self.stream(W, W[l, :, 1536 + 256 * i: 1536 + 256 * (i + 1)], 256)
            for sub in range(2):
                def cons(ps, t0, w, sub=sub):
                    P.op("act", lambda e: e.activation(out=self.u[:, sub, t0:t0 + w], in_=ps[:, 0:w], func=AF.Copy),
                         [ps], [self.u])
                self.proj(wb, sub, 128, self.h, hrhs, tiles, cons)
            wb = self.stream(W, W[l, :, 768 + 256 * i: 768 + 256 * (i + 1)], 256)
            for sub in range(2):
                def cons(ps, t0, w, sub=sub):
                    P.op("dve", lambda e: e.tensor_tensor(out=self.v[:, sub, 2 + t0: 2 + t0 + w], in0=ps[:, 0:w],
                                                          in1=self.u[:, sub, t0:t0 + w], op=ALU.mult),
                         [ps, self.u], [self.v])
                self.proj(wb, sub, 128, self.h, hrhs, tiles, cons)
            for sub in range(2):
                ch = 2 * i + sub
                if ri == 0:
                    P.op("pool", lambda e, sub=sub: e.memset(self.v[:, sub, 0:2], 0.0), [], [self.v])
                    self.halo_fix(sub)
                    P.op("dve", lambda e, sub=sub, ch=ch: e.tensor_copy(out=self.carry[:, ch, :],
                                                                        in_=self.v[:, sub, n:n + 2]),
                         [self.v], [self.carry])
                else:
                    P.op("dve", lambda e, sub=sub, ch=ch: e.tensor_copy(out=self.v[:, sub, 0:2],
                                                                        in_=self.carry[:, ch, :]),
                         [self.carry], [self.v])
                cwc = lambda k, ch=ch: self.cw[:, l * 18 + ch * 3 + k: l * 18 + ch * 3 + k + 1]
                P.op("pool", lambda e, sub=sub, cwc=cwc: e.tensor_scalar(
                    out=self.u[:, sub, 0:n], in0=self.v[:, sub, 0:n], scalar1=cwc(0), scalar2=None, op0=ALU.mult),
                    [self.v, self.cw], [self.u])
                for k in (1, 2):
                    P.op("dve", lambda e, sub=sub, cwc=cwc, k=k: e.scalar_tensor_tensor(
                        out=self.u[:, sub, 0:n], in0=self.v[:, sub, k:k + n], scalar=cwc(k), in1=self.u[:, sub, 0:n],
                        op0=ALU.mult, op1=ALU.add), [self.v, self.cw, self.u], [self.u])
            wb = self.stream(W, W[l, :, 256 * i: 256 * (i + 1)], 256)
            for sub in range(2):
                def cons(ps, t0, w, sub=sub, ch=2 * i + sub):
                    P.op("dve", lambda e: e.tensor_tensor(out=self.y[:, ch, t0:t0 + w], in0=ps[:, 0:w],
                                                          in1=self.u[:, sub, t0:t0 + w], op=ALU.mult),
                         [ps, self.u], [self.y])
                self.proj(wb, sub, 128, self.h, hrhs, tiles, cons)
        wb = self.stream(W, W[l, :, 2304:2560], 256)
        self.qmem_proj(wb, tiles)

    def halo_fix(self, sub):
        self.P.op("dve", lambda e: e.tensor_scalar(
            out=self.v[:, sub, 2:2 + HALO], in0=self.v[:, sub, 2:2 + HALO], scalar1=self.flg[:, 0:1],
            scalar2=None, op0=ALU.mult), [self.v, self.flg], [self.v])

    def store_tile(self, dstT, r, msz, c0, t0, w, ost):
        self.P.dma("sp", dstT, dstT[r:r + msz, c0 + t0: c0 + t0 + w], ost, ost[0:msz, 0:w], nodep_out=True)

    def ytok_src(self, ch, c0, t0, w):
        return self.ytokT[ch * 128:(ch + 1) * 128, c0 + t0: c0 + t0 + w]

    def qmem_proj(self, wb, tiles):
        P = self.P
        for sub in range(2):
            def cons(ps, t0, w, sub=sub):
                P.op("act", lambda e: e.activation(out=self.qm[:, sub, t0:t0 + w], in_=ps[:, 0:w], func=AF.Copy),
                     [ps], [self.qm])
            self.proj(wb, sub, 128, self.h, lambda kc, t0, w: self.h[:, kc, t0:t0 + w], tiles, cons)

    def store_proj(self, wT, w_ap, ncols, dstT, row0, c0, tiles, func=AF.Copy):
        P = self.P
        wb = self.stream(wT, w_ap, ncols)
        nsub = (ncols + 127) // 128
        for sub in range(nsub):
            msz = min(128, ncols - sub * 128)

            def cons(ps, t0, w, sub=sub, msz=msz):
                ost = self.rot("ost", self.ost)
                P.op("act", lambda e: e.activation(out=ost[0:msz, 0:w], in_=ps[0:msz, 0:w], func=func), [ps], [ost])
                r = row0 + sub * 128
                self.store_tile(dstT, r, msz, c0, t0, w, ost)
            self.proj(wb, sub, msz, self.h, lambda kc, t0, w: self.h[:, kc, t0:t0 + w], tiles, cons)

    def pre_b(self, j, c0, n, tiles):
        W = self.b_w_in
        self.norm(self.x, self.xs(c0), n, tiles, 2 + j, self.h, 0)
        for i in range(3):
            self.store_proj(W, W[j, :, 256 * i:256 * (i + 1)], 256, self.qT_o, 256 * i, c0, tiles)
        self.store_proj(W, W[j, :, 768:804], 36, self.gT_o, 0, c0, tiles, func=AF.Sigmoid)

    def post_b(self, j, c0, n, tiles):
        P = self.P
        W = self.b_w_in
        l = 2 + j
        self.norm(self.x, self.xs(c0), n, tiles, l, self.h, 0)
        wb = self.stream(W, W[j, :, 804:1060], 256)
        self.qmem_proj(wb, tiles)
        for ch in range(6):
            for (t0, w) in tiles:
                tmp = self.rot("tmp", self.tmp)
                P.dma("sp", tmp, tmp[:, 0:w], self.ytokT, self.ytok_src(ch, c0, t0, w))
                P.op("pool", lambda e, tmp=tmp, ch=ch, t0=t0, w=w: e.tensor_copy(out=self.y[:, ch, t0:t0 + w],
                                                                                 in_=tmp[:, 0:w]), [tmp], [self.y])
        self.mem_attn(l, n, tiles)
        self.out_proj(l, c0, n, tiles)
        self.mlp(l, c0, n, tiles)

    def store_x(self):
        P = self.P
        for c in range(8):
            P.dma("sp", self.xT_o, self.xT_o[c * 128:(c + 1) * 128, :], self.x, self.x[:, c, :], nodep_out=True)

    def build(self):
        P = self.P
        self.setup()
        if self.mode == "L1":
            for l in range(2):
                self.mem_kv(l)
                for ri, (c0, n) in enumerate(RANGES):
                    tiles = tiles_of(n)
                    self.norm(self.x, self.xs(c0), n, tiles, l, self.h, 0)
                    self.mix_a(l, ri, c0, n, tiles)
                    self.mem_attn(l, n, tiles)
                    self.out_proj(l, c0, n, tiles)
                    self.mlp(l, c0, n, tiles)
            for ri, (c0, n) in enumerate(RANGES):
                tiles = tiles_of(n)
                self.norm(self.x, self.xs(c0), n, tiles, 8, self.h, 0)
                for i in range(3):
                    self.store_proj(self.w_kv, self.w_kv[:, 256 * i:256 * (i + 1)], 256, self.kvT_o, 256 * i, c0, tiles)
                self.pre_b(0, c0, n, tiles)
            self.store_x()
            outs = [self.kvT_o, self.qT_o, self.gT_o, self.xT_o]
        elif self.mode == "L3":
            self.mem_kv(2)
            for ri, (c0, n) in enumerate(RANGES):
                tiles = tiles_of(n)
                self.post_b(0, c0, n, tiles)
            for ri, (c0, n) in enumerate(RANGES):
                tiles = tiles_of(n)
                self.pre_b(1, c0, n, tiles)
            self.store_x()
            outs = [self.qT_o, self.gT_o, self.xT_o]
        else:
            self.mem_kv(3)
            for ri, (c0, n) in enumerate(RANGES):
                tiles = tiles_of(n)
                self.post_b(1, c0, n, tiles)
            self.store_x()
            outs = [self.xT_o]
        P.wait_all("sp", outs)
        P.finalize()


def build_tok(mode):
    nc = bass.Bass("TRN2", target_bir_lowering=False)
    P = Prog(nc)
    st = TokStage(P, mode)
    st.build()
    return nc, P


def host_gains(inp):
    g = np.zeros((16, 1024), np.float32)
    g[0:4] = inp["mix_norm"]
    g[4:8] = inp["mlp_norm"]
    g[8] = inp["kv_norm"]
    g[15] = inp["mem_norm"]
    return np.ascontiguousarray(g.reshape(16, 8, 128).transpose(2, 0, 1).reshape(128, 128))


def host_hg(inp):
    hg = np.zeros((128, 16), np.float32)
    for l in range(4):
        hg[:, l] = np.tile(inp["mem_q_gain"][l], 2)
        hg[:, 4 + l] = np.tile(inp["mem_k_gain"][l], 2)
    return hg


def host_convw(inp):
    w = inp["a_conv_w"].reshape(2, 3, 6, 128)
    return np.ascontiguousarray(w.transpose(3, 0, 2, 1).reshape(128, 36))

import math

SEQ = 8192
QT = 512
NQT = SEQ // QT
DC, LC = 3040, 5104
DW, LW = 1023, 1536
DS, LS = 1407, 1920
LTOT = LC + LW + LS
LU, LTW, LTS = 3072, 1408, 1792
MASKV = -30000.0


def rel_bucket_np(d):
    n = np.maximum(d, 0)
    nf = np.maximum(n, 1).astype(np.float32)
    large = 16 + (np.log(nf / np.float32(16)) / np.float32(math.log(1024 / 16)) * np.float32(16)).astype(np.int32)
    large = np.minimum(large, 31)
    return np.where(n < 16, n, large)


def host_consts():
    c = {}
    oh = np.zeros((33, LTOT), np.float32)
    x = np.arange(LC); d = DC - x
    b = rel_bucket_np(d)
    oh[b[d >= 0], x[d >= 0]] = 1.0
    oh[32, x[d < 0]] = 1.0
    x = np.arange(LW); d = DW - x; ok = (d >= 0) & (d < 512)
    b = rel_bucket_np(d)
    oh[b[ok], LC + x[ok]] = 1.0
    oh[32, LC + x[~ok]] = 1.0
    x = np.arange(LS); d = DS - x; ok = d >= 0
    b = rel_bucket_np(d)
    oh[b[ok], LC + LW + x[ok]] = 1.0
    oh[32, LC + LW + x[~ok]] = 1.0
    c["OH"] = oh
    c["ident"] = np.eye(128, dtype=np.float32)
    k = np.arange(SEQ)
    r = 2 * ((k // 128) % 32) + ((k % 128) >= 64)
    ee = np.zeros((64, SEQ), np.float32)
    ee[r, k] = 30000.0
    c["EE"] = ee
    i = np.arange(512)[:, None] * 16
    s = np.arange(128)[None, :] * 64
    ov = np.clip(np.minimum(i + 32, s + 64) - np.maximum(i, s), 0, None).astype(np.float32) / 32.0
    ov[511] = 0.0
    c["ovl"] = np.ascontiguousarray(ov.reshape(4, 128, 128).transpose(1, 0, 2))
    tc = np.zeros((NQT, 128, 4, 2, 128), np.float32)
    j = np.arange(128)[None, :]
    for qt in range(NQT):
        for ss in range(4):
            t = qt * QT + 511 - (128 * ss + np.arange(128))
            back = (t // 64)[:, None] - j
            forced = (j == 0) | ((back >= 0) & (back < 2))
            A = ((back >= 0) & ~forced).astype(np.float32)
            B = np.where(back >= 0, np.where(forced, 1e4, 0.0), -1e30).astype(np.float32)
            tc[qt, :, ss, 0] = A
            tc[qt, :, ss, 1] = B
    c["topc"] = tc.reshape(NQT, 128, 1024)
    return c


class Attn:
    def __init__(self, P):
        self.P = P
        D = lambda n, s: P.dram(n, s, F32, kind="ExternalInput")
        self.qT = D("qT", [384, SEQ])
        self.gB = D("gB", [9, SEQ])
        self.kvT = D("kvT", [384, SEQ])
        self.vtok = D("vtok", [SEQ, 128])
        self.blk = D("blk", [2, 64, 32, 512])
        self.hg = D("hg", [128, 8])
        self.pos = D("pos", [2, 64, 32])
        self.w1 = D("w1", [2, 2048, 256])
        self.w2 = D("w2", [2, 256, 64])
        self.rb6 = D("rb6", [32, 6])
        self.OH = D("OH", [33, LTOT])
        self.identD = D("ident", [128, 128])
        self.EE = D("EE", [64, SEQ])
        self.ovlD = D("ovl", [128, 4, 128])
        self.topc = D("topc", [NQT, 128, 1024])
        self.yT = P.dram("yT", [192, SEQ], F32, kind="ExternalOutput")
        self.Rrow = P.dram("Rrow", [6, LTOT], F32, kind="Internal")
        self.alloc()

    def alloc(self):
        P = self.P
        sb = P.sb
        self.ksaug = sb("ksaug", [128, SEQ], BF16)
        self.kwT = sb("kwT", [64, SEQ], BF16)
        self.vsaug = sb("vsaug", [128, 64, 128], BF16)
        self.vwaug = sb("vwaug", [128, 64, 128], BF16)
        self.kcT = sb("kcT", [64, 512], BF16)
        self.vcaug = sb("vcaug", [128, 4, 128], BF16)
        self.Ubig = sb("Ubig", [128, 6 * LU], BF16)
        self.TTw = [sb("TTw%d" % h, [128, LTW], BF16) for h in range(3)]
        self.TTs = [sb("TTs%d" % h, [128, LTS], BF16) for h in range(3)]
        self.ident = sb("identb", [128, 128], BF16)
        self.ones = sb("ones", [128, 128], BF16)
        self.ovl = sb("ovlb", [128, 4, 128], BF16)
        self.hgs = sb("hgs", [128, 8], F32)
        self.cf = sb("cf", [128, 6], F32)
        self.tab = sb("tab", [33, 6], F32)
        self.epsT = sb("epsT", [128, 1], F32)
        self.st = [sb("st%d" % i, [128, 1024], F32) for i in range(2)]
        self.sq = [sb("sq%d" % i, [128, 512], BF16) for i in range(2)]
        self.rq = sb("rq", [128, 512], F32)
        self.qraw = sb("qraw", [64, 6, 512], F32)
        self.qn = sb("qn", [64, 6, 512], BF16)
        self.qaug = sb("qaug", [128, 3, 2, 512], BF16)
        self.pT = [sb("pT%d" % i, [128, 512], BF16) for i in range(6)]
        self.pTn = [sb("pTn%d" % i, [128, 512], BF16) for i in range(4)]
        self.rdb = sb("rdb", [128, 512], F32)
        self.impacc = sb("impacc", [128, 512], F32)
        self.tcs = sb("tcs", [128, 1024], F32)
        self.sc = sb("sc", [128, 128], F32)
        self.sc2 = sb("sc2", [128, 128], F32)
        self.mx = sb("mx", [128, 16], F32)
        self.mneg = sb("mneg", [128, 128], BF16)
        self.gt = [sb("gt%d" % i, [64, 512], F32) for i in range(2)]
        self.rdt = sb("rdt", [64, 512], F32)
        self.wt = sb("wt", [64, 512], F32)
        self.tm = sb("tm", [64, 512], F32)
        self.y = sb("y", [64, 3, 512], F32)
        self.w1b = sb("w1b", [64, 4, 256], BF16)
        self.blkb = sb("blkb", [64, 4, 512], BF16)
        self.posT = sb("posT", [64, 32], F32)
        self.w2b = sb("w2b", [128, 2, 64], BF16)
        self.gx = self.rdb
        self.gy = self.impacc
        self.gT = [sb("gT%d" % i, [128, 512], BF16) for i in range(2)]
        self.A = [P.ps("A%d" % i, [128, 512]) for i in range(3)]
        self.S = [P.ps("S%d" % i, [128, 512]) for i in range(2)]
        self.X0 = P.ps("X0", [128, 512])
        self.X1 = P.ps("X1", [128, 512])
        self.XT = P.ps("XT", [128, 512], BF16)
        self.cnt = {}

    def uslice(self, h, mo):
        return self.Ubig[:, h * LU + mo: h * LU + mo + 512]

    def rot(self, name, lst):
        i = self.cnt.get(name, 0)
        self.cnt[name] = i + 1
        return lst[i % len(lst)]

    def norm64(self, src_t, src_ap, w, gain_ap, dst_t, dst_ap):
        P = self.P
        sq = self.rot("sq", self.sq)
        P.op("act", lambda e: e.activation(out=sq[0:64, 0:w], in_=src_ap, func=AF.Square), [src_t], [sq])
        P.op("pe", lambda e: e.matmul(self.X0[0:64, 0:w], lhsT=self.ones[0:64, 0:64], rhs=sq[0:64, 0:w],
                                      start=True, stop=True), [self.ones, sq], [self.X0])
        P.op("act", lambda e: e.activation(out=self.rq[0:64, 0:w], in_=self.X0[0:64, 0:w], func=AF.Sqrt,
                                           bias=self.epsT[0:64, :], scale=1.0 / 64.0), [self.X0, self.epsT], [self.rq])
        P.op("dve", lambda e: e.reciprocal(out=self.rq[0:64, 0:w], in_=self.rq[0:64, 0:w]), [self.rq], [self.rq])
        P.op("dve", lambda e: e.scalar_tensor_tensor(out=dst_ap, in0=src_ap, scalar=gain_ap, in1=self.rq[0:64, 0:w],
                                                     op0=ALU.mult, op1=ALU.mult), [src_t, self.hgs, self.rq], [dst_t])

    def setup(self):
        P = self.P
        st = self.st
        P.dma("sp", self.hgs, self.hgs[:], self.hg, self.hg[:])
        P.op("pool", lambda e: e.memset(self.ones[:], 1.0), [], [self.ones])
        P.op("pool", lambda e: e.memset(self.epsT[:], 1e-6), [], [self.epsT])
        P.op("pool", lambda e: e.memset(self.vsaug[:], 1.0), [], [self.vsaug])
        P.op("pool", lambda e: e.memset(self.vwaug[:], 1.0), [], [self.vwaug])
        P.op("pool", lambda e: e.memset(self.vcaug[:], 1.0), [], [self.vcaug])
        s = self.rot("st", st)
        P.dma("sp", s, s[:, 0:128], self.identD, self.identD[:])
        P.op("pool", lambda e, s=s: e.tensor_copy(out=self.ident[:], in_=s[:, 0:128]), [s], [self.ident])
        s = self.rot("st", st)
        P.dma("sp", s, s[:, 0:512], self.ovlD, self.ovlD[:].rearrange("p c j -> p (c j)"))
        P.op("pool", lambda e, s=s: e.tensor_copy(out=self.ovl[:].rearrange("p c j -> p (c j)"), in_=s[:, 0:512]),
             [s], [self.ovl])
        self.setup_tables()
        self.setup_kv()

    def load_rb(self):
        P = self.P
        rb = self.rb6[:]
        P.dma("sp", self.cf, self.cf[:], self.rb6, bass.AP(tensor=rb.tensor, offset=31 * 6, ap=[[0, 128], [1, 6]]))
        P.dma("sp", self.tab, self.tab[0:32, :], self.rb6, self.rb6[:])

    def setup_tables(self):
        P = self.P
        st = self.st
        P.op("pool", lambda e: e.memset(self.tab[:], MASKV), [], [self.tab])
        self.load_rb()
        P.op("dve", lambda e: e.tensor_scalar(out=self.tab[0:32, :], in0=self.tab[0:32, :], scalar1=8.0, scalar2=None,
                                              op0=ALU.mult), [self.tab], [self.tab])
        o = 0
        while o < LTOT:
            w = min(512, LTOT - o)
            s = self.rot("st", st)
            P.dma("sp", s, s[0:33, 0:w], self.OH, self.OH[:, o:o + w])
            P.op("pe", lambda e, s=s, w=w: e.matmul(self.X1[0:6, 0:w], lhsT=self.tab[:, :], rhs=s[0:33, 0:w],
                                                    start=True, stop=True), [self.tab, s], [self.X1])
            s2 = self.rot("st", st)
            P.op("act", lambda e, s2=s2, w=w: e.activation(out=s2[0:6, 0:w], in_=self.X1[0:6, 0:w], func=AF.Copy),
                 [self.X1], [s2])
            P.dma("sp", self.Rrow, self.Rrow[:, o:o + w], s2, s2[0:6, 0:w])
            o += w
        rr = self.Rrow[:]

        def expand(dst, d0, h, base, pstep, L):
            o = 0
            while o < L:
                w = min(1024, L - o)
                s = self.rot("st", st)
                P.dma("sp", s, s[:, 0:w], self.Rrow,
                      bass.AP(tensor=rr.tensor, offset=h * LTOT + base + o, ap=[[pstep, 128], [1, w]]))
                P.op("pool", lambda e, s=s, o=o, w=w: e.tensor_copy(out=dst[:, d0 + o:d0 + o + w], in_=s[:, 0:w]),
                     [s], [dst])
                o += w
        for h in range(6):
            expand(self.Ubig, h * LU, h, 0, 16, LU)
        for h in range(3):
            expand(self.TTw[h], 0, h, LC, 1, LTW)
            expand(self.TTs[h], 0, h, LC + LW, 1, LTS)

    def load_kv_piece(self, s, slot, c0):
        self.P.dma("sp", s, s[0:64, :], self.kvT, self.kvT[slot * 64:(slot + 1) * 64, c0:c0 + 1024])

    def load_v_piece(self, piece):
        P = self.P
        c0 = piece * 1024
        s = self.rot("st", self.st)
        P.dma("sp", s, s[:, :].rearrange("p (c f) -> p c f", f=128), self.vtok,
              self.vtok[c0:c0 + 1024, :].rearrange("(c p) f -> p c f", p=128))
        sv = s[:, :].rearrange("p (c f) -> p c f", f=128)
        P.op("pool", lambda e, sv=sv, piece=piece: e.tensor_copy(
            out=self.vsaug[:, piece * 8:(piece + 1) * 8, 0:64], in_=sv[:, :, 0:64]), [s], [self.vsaug])
        P.op("pool", lambda e, sv=sv, piece=piece: e.tensor_copy(
            out=self.vwaug[:, piece * 8:(piece + 1) * 8, 0:64], in_=sv[:, :, 64:128]), [s], [self.vwaug])

    def setup_kv(self):
        P = self.P
        st = self.st
        for piece in range(8):
            c0 = piece * 1024
            for slot, gcol, dst in ((2, 2, self.ksaug), (4, 3, self.kwT)):
                s = self.rot("st", st)
                self.load_kv_piece(s, slot, c0)
                for hf in range(2):
                    a = hf * 512
                    self.norm64(s, s[0:64, a:a + 512], 512, self.hgs[0:64, gcol:gcol + 1], dst,
                                dst[0:64, c0 + a:c0 + a + 512])
            s = self.rot("st", st)
            P.dma("sp", s, s[64:128, :], self.EE, self.EE[:, c0:c0 + 1024])
            P.op("pool", lambda e, s=s, c0=c0: e.tensor_copy(out=self.ksaug[64:128, c0:c0 + 1024], in_=s[64:128, :]),
                 [s], [self.ksaug])
            self.load_v_piece(piece)
        self.compress()

    def compress_rhs(self, kvi, tg):
        P = self.P
        for th in range(2):
            s = self.rot("st", self.st)
            t0 = tg * 4 + th * 2
            P.dma("sp", s, s[0:64, :].rearrange("p (t n) -> p t n", n=512), self.blk,
                  self.blk[kvi, :, t0:t0 + 2, :])
            for tt in range(2):
                t = t0 + tt
                P.op("dve", lambda e, s=s, tt=tt, th=th, t=t: e.tensor_scalar(
                    out=self.blkb[:, th * 2 + tt, :], in0=s[0:64, tt * 512:(tt + 1) * 512],
                    scalar1=self.posT[:, t:t + 1], scalar2=None, op0=ALU.add), [s, self.posT], [self.blkb])

    def compress_begin(self, kvi):
        pass

    def compress(self):
        P = self.P
        st = self.st
        H = [self.A[0], self.A[1]]
        for kvi in range(2):
            self.compress_begin(kvi)
            P.dma("sp", self.posT, self.posT[:], self.pos, self.pos[kvi])
            s = self.rot("st", st)
            P.dma("sp", s, s[:, 0:128].rearrange("p (c d) -> p c d", d=64), self.w2,
                  self.w2[kvi].rearrange("(c p) d -> p c d", p=128))
            P.op("pool", lambda e, s=s: e.tensor_copy(out=self.w2b[:].rearrange("p c d -> p (c d)"), in_=s[:, 0:128]),
                 [s], [self.w2b])
            for tg in range(8):
                s = self.rot("st", st)
                P.dma("sp", s, s[0:64, :].rearrange("p (t j) -> p t j", j=256), self.w1,
                      self.w1[kvi, tg * 256:(tg + 1) * 256, :].rearrange("(t d) j -> d t j", d=64))
                P.op("pool", lambda e, s=s: e.tensor_copy(out=self.w1b[:].rearrange("p t j -> p (t j)"),
                                                          in_=s[0:64, :]), [s], [self.w1b])
                self.compress_rhs(kvi, tg)
                for tl in range(4):
                    t = tg * 4 + tl
                    for jc in range(2):
                        P.op("pe", lambda e, tl=tl, jc=jc, t=t: e.matmul(
                            H[jc][:, :], lhsT=self.w1b[:, tl, jc * 128:(jc + 1) * 128], rhs=self.blkb[:, tl, :],
                            start=(t == 0), stop=(t == 31)), [self.w1b, self.blkb], [H[jc]])
            for jc in range(2):
                P.op("act", lambda e, jc=jc: e.activation(out=self.gx[:], in_=H[jc][:, :], func=AF.Copy),
                     [H[jc]], [self.gx])
                P.op("pool", lambda e: e.tensor_tensor(out=self.gy[:], in0=self.gx[:], in1=self.gx[:], op=ALU.mult),
                     [self.gx], [self.gy])
                P.op("dve", lambda e: e.tensor_scalar(out=self.gy[:], in0=self.gy[:], scalar1=0.044715, scalar2=1.0,
                                                      op0=ALU.mult, op1=ALU.add), [self.gy], [self.gy])
                P.op("pool", lambda e: e.tensor_tensor(out=self.gy[:], in0=self.gy[:], in1=self.gx[:], op=ALU.mult),
                     [self.gy, self.gx], [self.gy])
                P.op("act", lambda e: e.activation(out=self.gy[:], in_=self.gy[:], func=AF.Sigmoid,
                                                   scale=1.5957691216057308), [self.gy], [self.gy])
                P.op("dve", lambda e, jc=jc: e.tensor_tensor(out=self.gT[jc][:], in0=self.gx[:], in1=self.gy[:],
                                                             op=ALU.mult), [self.gx, self.gy], [self.gT[jc]])
            if kvi == 0:
                for jc in range(2):
                    P.op("pe", lambda e, jc=jc: e.matmul(self.X1[0:64, :], lhsT=self.w2b[:, jc, :], rhs=self.gT[jc][:],
                                                         start=(jc == 0), stop=(jc == 1)),
                         [self.w2b, self.gT[jc]], [self.X1])
                s = self.rot("st", st)
                P.op("act", lambda e, s=s: e.activation(out=s[0:64, 0:512], in_=self.X1[0:64, :], func=AF.Copy),
                     [self.X1], [s])
                self.norm64(s, s[0:64, 0:512], 512, self.hgs[0:64, 1:2], self.kcT, self.kcT[:, :])
            else:
                for cc in range(4):
                    for jc in range(2):
                        P.op("pe", lambda e, jc=jc, cc=cc: e.matmul(
                            self.X1[:, cc * 64:(cc + 1) * 64], lhsT=self.gT[jc][:, cc * 128:(cc + 1) * 128],
                            rhs=self.w2b[:, jc, :], start=(jc == 0), stop=(jc == 1)),
                            [self.w2b, self.gT[jc]], [self.X1])
                    P.op("act", lambda e, cc=cc: e.activation(out=self.vcaug[:, cc, 0:64],
                                                              in_=self.X1[:, cc * 64:(cc + 1) * 64], func=AF.Copy),
                         [self.X1], [self.vcaug])

    def combine(self, qt, h, br, first):
        P = self.P
        A = self.A[h]
        s0 = qt * QT
        gt = self.load_gate(qt, h, br)
        P.op("dve", lambda e: e.tensor_scalar(out=self.rdt[:], in0=A[64:128, :], scalar1=1e-30, scalar2=None,
                                              op0=ALU.max), [A], [self.rdt])
        P.op("dve", lambda e: e.reciprocal(out=self.rdt[:], in_=self.rdt[:]), [self.rdt], [self.rdt])
        P.op("pool", lambda e: e.tensor_tensor(out=self.wt[:], in0=self.rdt[:], in1=gt[:], op=ALU.mult),
             [self.rdt, gt], [self.wt])
        if first:
            P.op("dve", lambda e: e.tensor_tensor(out=self.y[:, h, :], in0=A[0:64, :], in1=self.wt[:], op=ALU.mult),
                 [A, self.wt], [self.y])
        else:
            P.op("dve", lambda e: e.tensor_tensor(out=self.tm[:], in0=A[0:64, :], in1=self.wt[:], op=ALU.mult),
                 [A, self.wt], [self.tm])
            P.op("pool", lambda e: e.tensor_tensor(out=self.y[:, h, :], in0=self.y[:, h, :], in1=self.tm[:],
                                                   op=ALU.add), [self.y, self.tm], [self.y])

    def exp(self, S, pT, bias):
        P = self.P
        if bias is None:
            P.op("act", lambda e: e.activation(out=pT[:], in_=S[:, :], func=AF.Exp, scale=0.125), [S], [pT])
        else:
            bap, bt = bias
            P.op("act", lambda e: e.activation(out=pT[:], in_=S[:, :], func=AF.Exp, scale=0.125, bias=bap),
                 [S, bt], [pT])

    def cmp_bias(self, cc, h, mixed):
        return None if mixed else (self.cf[:, h:h + 1], self.cf)

    def sel_bias(self, kc, h, near):
        return None if near else (self.cf[:, h:h + 1], self.cf)

    def win_bias(self, c):
        return None

    def topc_ap(self, qt):
        return self.topc[qt]

    def load_q(self, qt):
        s0 = qt * QT
        self.P.dma("sp", self.qraw, self.qraw[:], self.qT, self.qT[:, s0:s0 + QT].rearrange("(h d) t -> d h t", d=64))

    def load_gate(self, qt, h, br):
        s0 = qt * QT
        gt = self.rot("gt", self.gt)
        g = self.gB[:]
        self.P.dma("sp", gt, gt[:], self.gB,
                   bass.AP(tensor=g.tensor, offset=(h * 3 + br) * SEQ + s0, ap=[[0, 64], [1, 512]]))
        return gt

    def store_y(self, qt):
        s0 = qt * QT
        self.P.dma("sp", self.yT, self.yT[:, s0:s0 + QT].rearrange("(h d) t -> d h t", d=64), self.y, self.y[:],
                   nodep_out=True)

    def qtile(self, qt):
        P = self.P
        s0 = qt * QT
        self.load_q(qt)
        P.dma("sp", self.tcs, self.tcs[:], self.topc, self.topc_ap(qt))
        for h in range(6):
            self.norm64(self.qraw, self.qraw[:, h, :], 512, self.hgs[0:64, 0:1], self.qn, self.qn[:, h, :])
        for h in range(3):
            for half in range(2):
                P.op("pool", lambda e, h=h, half=half: e.tensor_copy(out=self.qaug[0:64, h, half, :],
                                                                     in_=self.qn[:, h, :]), [self.qn], [self.qaug])
        for h in range(6):
            chunks = [cc for cc in range(4) if qt - 4 * cc >= 0]
            pts = []
            for cc in chunks:
                k = qt - 4 * cc
                mixed = k <= 5
                S = self.rot("S", self.S)
                P.op("pe", lambda e, S=S, cc=cc, h=h, mixed=mixed: e.matmul(
                    S[:, :], lhsT=self.kcT[:, cc * 128:(cc + 1) * 128], rhs=self.qn[:, h, :], start=True,
                    stop=(not mixed)), [self.kcT, self.qn], [S])
                if mixed:
                    mo = 512 * (5 - k)
                    P.op("pe", lambda e, S=S, h=h, mo=mo: e.matmul(S[:, :], lhsT=self.ident[:],
                                                                   rhs=self.uslice(h, mo), start=False, stop=True),
                         [self.ident, self.Ubig], [S])
                pT = self.rot("pT", self.pT)
                self.exp(S, pT, self.cmp_bias(cc, h, mixed))
                pts.append(pT)
            n = len(chunks)
            for i, pT in enumerate(pts):
                P.op("pe", lambda e, pT=pT, i=i: e.matmul(self.X0[:, :], lhsT=self.ones[:], rhs=pT[:], start=(i == 0),
                                                          stop=(i == n - 1)), [self.ones, pT], [self.X0])
            if h < 3:
                for i, (cc, pT) in enumerate(zip(chunks, pts)):
                    P.op("pe", lambda e, pT=pT, i=i, cc=cc, h=h: e.matmul(
                        self.A[h][:, :], lhsT=self.vcaug[:, cc, :], rhs=pT[:], start=(i == 0), stop=(i == n - 1)),
                        [self.vcaug, pT], [self.A[h]])
            P.op("dve", lambda e: e.tensor_scalar(out=self.rdb[:], in0=self.X0[:, :], scalar1=1e-30, scalar2=None,
                                                  op0=ALU.max), [self.X0], [self.rdb])
            P.op("dve", lambda e: e.reciprocal(out=self.rdb[:], in_=self.rdb[:]), [self.rdb], [self.rdb])
            ptn = []
            for pT in pts:
                pn = self.rot("pTn", self.pTn)
                P.op("pool", lambda e, pT=pT, pn=pn: e.tensor_tensor(out=pn[:], in0=pT[:], in1=self.rdb[:], op=ALU.mult),
                     [pT, self.rdb], [pn])
                ptn.append(pn)
            for ss in range(4):
                for i, (cc, pn) in enumerate(zip(chunks, ptn)):
                    P.op("pe", lambda e, pn=pn, i=i, cc=cc, ss=ss: e.matmul(
                        self.X1[:, ss * 128:(ss + 1) * 128], lhsT=pn[:, ss * 128:(ss + 1) * 128], rhs=self.ovl[:, cc, :],
                        start=(i == 0), stop=(i == n - 1)), [pn, self.ovl], [self.X1])
            if h == 0:
                P.op("dve", lambda e: e.tensor_copy(out=self.impacc[:], in_=self.X1[:, :]), [self.X1], [self.impacc])
            else:
                P.op("dve", lambda e: e.tensor_tensor(out=self.impacc[:], in0=self.X1[:, :], in1=self.impacc[:],
                                                      op=ALU.add), [self.X1, self.impacc], [self.impacc])
            if h < 3:
                self.combine(qt, h, 0, True)
        for ss in range(4):
            tA = self.tcs[:, ss * 256: ss * 256 + 128]
            tB = self.tcs[:, ss * 256 + 128: ss * 256 + 256]
            P.op("dve", lambda e, ss=ss, tA=tA: e.tensor_tensor(out=self.sc[:], in0=self.impacc[:, ss * 128:(ss + 1) * 128],
                                                                in1=tA, op=ALU.mult), [self.impacc, self.tcs], [self.sc])
            P.op("dve", lambda e, tB=tB: e.tensor_tensor(out=self.sc[:], in0=self.sc[:], in1=tB, op=ALU.add),
                 [self.sc, self.tcs], [self.sc])
            P.op("dve", lambda e: e.max(out=self.mx[:, 0:8], in_=self.sc[:]), [self.sc], [self.mx])
            P.op("dve", lambda e: e.match_replace(out=self.sc2[:], in_to_replace=self.mx[:, 0:8], in_values=self.sc[:],
                                                  imm_value=-1e30), [self.sc, self.mx], [self.sc2])
            P.op("dve", lambda e: e.max(out=self.mx[:, 8:16], in_=self.sc2[:]), [self.sc2], [self.mx])
            P.op("dve", lambda e: e.tensor_scalar(out=self.mneg[:], in0=self.sc[:], scalar1=self.mx[:, 15:16], scalar2=1.0,
                                                  op0=ALU.is_ge, op1=ALU.subtract), [self.sc, self.mx], [self.mneg])
            P.op("pe", lambda e, ss=ss: e.transpose(self.XT[:, ss * 128:(ss + 1) * 128], self.mneg[:], self.ident[:]),
                 [self.mneg, self.ident], [self.XT])
        for h in range(3):
            for half in range(2):
                eng = "act" if (h + half) % 2 == 0 else "dve"
                if eng == "act":
                    P.op("act", lambda e, h=h, half=half: e.activation(out=self.qaug[64:128, h, half, :],
                                                                       in_=self.XT[half * 64:(half + 1) * 64, :],
                                                                       func=AF.Copy), [self.XT], [self.qaug])
                else:
                    P.op("dve", lambda e, h=h, half=half: e.tensor_copy(out=self.qaug[64:128, h, half, :],
                                                                        in_=self.XT[half * 64:(half + 1) * 64, :]),
                         [self.XT], [self.qaug])
        nkc = 4 * qt + 4

        def sel_qk(kc, h):
            k0 = 128 * kc
            ep = s0 + 511 - k0
            near = ep <= DS
            half = kc // 32
            S = self.rot("S", self.S)
            P.op("pe", lambda e: e.matmul(S[:, :], lhsT=self.ksaug[:, k0:k0 + 128], rhs=self.qaug[:, h, half, :],
                                          start=True, stop=(not near)), [self.ksaug, self.qaug], [S])
            if near:
                mo = DS - ep
                P.op("pe", lambda e: e.matmul(S[:, :], lhsT=self.ident[:], rhs=self.TTs[h][:, mo:mo + 512],
                                              start=False, stop=True), [self.ident, self.TTs[h]], [S])
            return (kc, h, near, S)

        def sel_rest(kc, h, near, S):
            pT = self.rot("pT", self.pT)
            self.exp(S, pT, self.sel_bias(kc, h, near))
            P.op("pe", lambda e: e.matmul(self.A[h][:, :], lhsT=self.vsaug[:, kc, :], rhs=pT[:],
                                          start=(kc == 0), stop=(kc == nkc - 1)), [self.vsaug, pT], [self.A[h]])

        prev = None
        for kc in range(nkc):
            for h in range(3):
                cur = sel_qk(kc, h)
                if prev is not None:
                    sel_rest(*prev)
                prev = cur
        sel_rest(*prev)
        for h in range(3):
            self.combine(qt, h, 1, False)
        rs = [r for r in range(8) if s0 - 512 + 128 * r >= 0]

        def win_qk(i, r, h):
            k0 = s0 - 512 + 128 * r
            mo = 128 * r
            S = self.rot("S", self.S)
            P.op("pe", lambda e: e.matmul(S[:, :], lhsT=self.kwT[:, k0:k0 + 128], rhs=self.qn[:, h, :],
                                          start=True, stop=False), [self.kwT, self.qn], [S])
            P.op("pe", lambda e: e.matmul(S[:, :], lhsT=self.ident[:], rhs=self.TTw[h][:, mo:mo + 512],
                                          start=False, stop=True), [self.ident, self.TTw[h]], [S])
            return (i, k0, h, S)

        def win_rest(i, k0, h, S):
            pT = self.rot("pT", self.pT)
            self.exp(S, pT, self.win_bias(k0 // 128))
            P.op("pe", lambda e: e.matmul(self.A[h][:, :], lhsT=self.vwaug[:, k0 // 128, :], rhs=pT[:],
                                          start=(i == 0), stop=(i == len(rs) - 1)), [self.vwaug, pT], [self.A[h]])

        prev = None
        for i, r in enumerate(rs):
            for h in range(3):
                cur = win_qk(i, r, h)
                if prev is not None:
                    win_rest(*prev)
                prev = cur
        win_rest(*prev)
        for h in range(3):
            self.combine(qt, h, 2, False)
        self.store_y(qt)

    def build(self, nqt=NQT):
        self.setup()
        for qt in range(nqt):
            self.qtile(qt)
        self.P.wait_all("sp", [self.yT])
        self.P.finalize()


def build_attn(nqt=NQT):
    nc = bass.Bass("TRN2", target_bir_lowering=False)
    P = Prog(nc)
    a = Attn(P)
    a.build(nqt)
    return nc, P


XPAD = SEQ + HALO


def rev_ap(ap, n):
    dims = [list(d) for d in ap.ap]
    dims[-1] = [-1, n]
    return bass.AP(tensor=ap.tensor, offset=ap.offset + (n - 1), ap=dims)


class FTok(TokStage):
    def __init__(self, P, mode, dr):
        self.P = P
        self.mode = mode
        self.dr = dr
        for k in ("memT", "gains", "hg", "mem_w_kv", "w_out", "w_up", "w_down", "b_w_in", "a_w_in", "convw", "w_kv"):
            setattr(self, k, dr[k])
        self.xsrc = dr["xin"] if mode == "L1" else dr["X"]
        self.kvT_o, self.qT_o, self.gT_o = dr["KV"], dr["Q"], dr["G"]
        self.ytokT = dr["Y"]
        self.chunk = 0
        self.alloc()

    def halo_fix(self, sub):
        if self.chunk == 0:
            self.P.op("pool", lambda e: e.memset(self.v[:, sub, 2:2 + HALO], 0.0), [], [self.v])

    def store_tile(self, dstT, r, msz, c0, t0, w, ost):
        a = c0 + t0
        lo = HALO if a == 0 else 0
        g0 = NTOK * self.chunk + a + lo - HALO
        self.P.dma("sp", dstT, dstT[r:r + msz, g0:g0 + (w - lo)], ost, ost[0:msz, lo:w], nodep_out=True)

    def ytok_src(self, ch, c0, t0, w):
        g0 = NTOK * self.chunk + c0 + t0
        return self.ytokT[ch * 128:(ch + 1) * 128, g0:g0 + w]

    def setup_consts(self):
        P = self.P
        P.dma("sp", self.g, self.g[:], self.gains, self.gains[:])
        P.dma("sp", self.hgs, self.hgs[:], self.hg, self.hg[:])
        if self.mode == "L1":
            P.dma("sp", self.cw, self.cw[:], self.convw, self.convw[:])
        for c in range(8):
            P.dma("sp", self.u, self.memx(c, 0, 256), self.memT, self.memT[c * 128:(c + 1) * 128, :], nodep_out=True)
        P.op("pool", lambda e: e.memset(self.ones[:], 1.0), [], [self.ones])
        P.op("pool", lambda e: e.memset(self.bd[:], 0.0), [], [self.bd])
        P.op("pool", lambda e: e.memset(self.bd[0:64, 0:64], 1.0), [], [self.bd])
        P.op("pool", lambda e: e.memset(self.bd[64:128, 64:128], 1.0), [], [self.bd])
        P.op("pool", lambda e: e.memset(self.epsT[:], EPS), [], [self.epsT])
        P.op("pool", lambda e: e.memset(self.vaug[:], 1.0), [], [self.vaug])
        self.norm(self.u, self.memx, 256, [(0, 256)], 15, self.memh, 0)

    def load_x(self):
        P = self.P
        g0 = NTOK * self.chunk
        for c in range(8):
            P.dma("sp", self.x, self.x[:, c, :], self.xsrc, self.xsrc[c * 128:(c + 1) * 128, g0:g0 + NCOL],
                  nodep_out=(c > 0))

    def store_x(self):
        P = self.P
        dst = self.dr["out"] if self.mode == "L5" else self.dr["X"]
        off = 0 if self.mode == "L5" else HALO
        g0 = NTOK * self.chunk + off
        for c in range(8):
            P.dma("sp", dst, dst[c * 128:(c + 1) * 128, g0:g0 + NTOK], self.x, self.x[:, c, HALO:NCOL], nodep_out=True)

    def run(self):
        self.setup_consts()
        for chunk in range(4):
            self.chunk = chunk
            self.load_x()
            if self.mode == "L1":
                for l in range(2):
                    self.mem_kv(l)
                    for ri, (c0, n) in enumerate(RANGES):
                        tiles = tiles_of(n)
                        self.norm(self.x, self.xs(c0), n, tiles, l, self.h, 0)
                        self.mix_a(l, ri, c0, n, tiles)
                        self.mem_attn(l, n, tiles)
                        self.out_proj(l, c0, n, tiles)
                        self.mlp(l, c0, n, tiles)
                for ri, (c0, n) in enumerate(RANGES):
                    tiles = tiles_of(n)
                    self.norm(self.x, self.xs(c0), n, tiles, 8, self.h, 0)
                    for i in range(3):
                        self.store_proj(self.w_kv, self.w_kv[:, 256 * i:256 * (i + 1)], 256, self.kvT_o, 256 * i, c0, tiles)
                    self.pre_b(0, c0, n, tiles)
            else:
                j = 0 if self.mode == "L3" else 1
                self.mem_kv(2 + j)
                for ri, (c0, n) in enumerate(RANGES):
                    self.post_b(j, c0, n, tiles_of(n))
                if j == 0:
                    for ri, (c0, n) in enumerate(RANGES):
                        self.pre_b(1, c0, n, tiles_of(n))
            self.store_x()


class FAttn(Attn):
    def __init__(self, P, dr, jl):
        self.P = P
        self.dr = dr
        self.jl = jl
        self.Q, self.G, self.KV, self.Y = dr["Q"], dr["G"], dr["KV"], dr["Y"]
        self.hg = dr["ahg%d" % jl]
        self.pos, self.w1, self.w2 = dr["pos"], dr["cmp_w1"], dr["cmp_w2"]
        self.rel = dr["rel_bias"]
        self.OH, self.identD, self.EE, self.ovlD, self.topc = dr["OH"], dr["ident"], dr["EE"], dr["ovl"], dr["topc"]
        self.Rrow = dr["Rrow"]
        self.alloc()

    def alloc(self):
        Attn.alloc(self)
        self.ub_view = self.Ubig[0:64, 0:16 * 513].rearrange("p (r n) -> p r n", n=513)

    def set_combo(self, g, tr):
        self.g, self.tr = g, tr
        own = [3 * tr + i for i in range(3)]
        oth = [3 * (1 - tr) + i for i in range(3)]
        self.heads = own + oth

    def load_rb(self):
        P = self.P
        rel = self.rel[:]
        for i, hh in enumerate(self.heads):
            col = self.g * 6 + hh
            P.dma("sp", self.cf, self.cf[:, i:i + 1], self.rel,
                  bass.AP(tensor=rel.tensor, offset=31 * 12 + col, ap=[[0, 128], [1, 1]]))
            P.dma("sp", self.tab, self.tab[0:32, i:i + 1], self.rel, self.rel[:, col:col + 1],
                  allow_slow_non_contiguous=True)

    def load_kv_piece(self, s, slot, c0):
        r0 = slot * 128 + self.g * 64
        self.P.dma("sp", s, s[0:64, :], self.KV, self.KV[r0:r0 + 64, c0:c0 + 1024])

    def load_v_piece(self, piece):
        P = self.P
        c0 = piece * 1024
        for slot, dst in ((3, self.vsaug), (5, self.vwaug)):
            s = self.rot("st", self.st)
            self.load_kv_piece(s, slot, c0)
            for hf in range(2):
                sq = self.rot("sq", self.sq)
                P.op("pool", lambda e, s=s, sq=sq, hf=hf: e.tensor_copy(out=sq[0:64, :], in_=s[0:64, hf * 512:(hf + 1) * 512]),
                     [s], [sq])
                for c in range(4):
                    P.op("pe", lambda e, sq=sq, c=c: e.transpose(self.XT[:, c * 64:(c + 1) * 64],
                                                                 sq[0:64, c * 128:(c + 1) * 128], self.ident[0:64, 0:64]),
                         [sq, self.ident], [self.XT])
                ch0 = piece * 8 + hf * 4
                P.op("act", lambda e, dst=dst, ch0=ch0: e.activation(
                    out=dst[:, ch0:ch0 + 4, 0:64], in_=self.XT[:, 0:256].rearrange("p (c d) -> p c d", d=64),
                    func=AF.Copy), [self.XT], [dst])

    def compress_begin(self, kvi):
        P = self.P
        P.op("pool", lambda e: e.memset(self.ub_view[:, :, 512:513], 0.0), [], [self.Ubig])
        for piece in range(8):
            s = self.rot("st", self.st)
            self.load_kv_piece(s, kvi, piece * 1024)
            P.op("pool", lambda e, s=s, piece=piece: e.tensor_copy(
                out=self.ub_view[:, :, piece * 64:(piece + 1) * 64],
                in_=s[0:64, :].rearrange("p (n r) -> p r n", r=16)), [s], [self.Ubig])

    def compress_rhs(self, kvi, tg):
        P = self.P
        for tl in range(4):
            t = tg * 4 + tl
            r, off = t % 16, t // 16
            P.op("dve", lambda e, tl=tl, t=t, r=r, off=off: e.tensor_scalar(
                out=self.blkb[:, tl, :], in0=self.ub_view[:, r, off:off + 512], scalar1=self.posT[:, t:t + 1],
                scalar2=None, op0=ALU.add), [self.Ubig, self.posT], [self.blkb])

    def load_q(self, qt):
        P = self.P
        s0 = qt * QT
        for i2 in range(3):
            s = self.rot("st", self.st)
            for k in range(2):
                hh = self.heads[2 * i2 + k]
                r0 = (self.g * 6 + hh) * 64
                P.dma("sp", s, s[0:64, k * 512:(k + 1) * 512], self.Q, self.Q[r0:r0 + 64, s0:s0 + QT], nodep_out=(k > 0))
            for k in range(2):
                P.op("dve", lambda e, s=s, k=k, i2=i2: e.tensor_copy(
                    out=self.qraw[:, 2 * i2 + k, :], in_=rev_ap(s[0:64, k * 512:(k + 1) * 512], 512)), [s], [self.qraw])

    def load_gate(self, qt, h, br):
        P = self.P
        s0 = qt * QT
        gt = self.rot("gt", self.gt)
        g = self.G[:]
        row = (self.g * 6 + 3 * self.tr + h) * 3 + br
        P.dma("sp", gt, gt[:], self.G, bass.AP(tensor=g.tensor, offset=row * SEQ + s0, ap=[[0, 64], [1, 512]]))
        gr = self.rot("gtr", self.gtr)
        P.op("dve", lambda e: e.tensor_copy(out=gr[:], in_=rev_ap(gt[:], 512)), [gt], [gr])
        return gr

    def store_y(self, qt):
        P = self.P
        s0 = qt * QT
        bufs = [self.rdt, self.wt, self.tm]
        for h in range(3):
            b = bufs[h]
            P.op("dve", lambda e, b=b, h=h: e.tensor_copy(out=b[:], in_=rev_ap(self.y[:, h, :], 512)), [self.y], [b])
            r0 = (self.g * 6 + 3 * self.tr + h) * 64
            P.dma("sp", self.Y, self.Y[r0:r0 + 64, HALO + s0:HALO + s0 + QT], b, b[:], nodep_out=True)

    def run(self):
        P = self.P
        st = self.st
        self.gtr = [P.sb("gtr%d" % i, [64, 512], F32) for i in range(2)]
        P.dma("sp", self.hgs, self.hgs[:], self.hg, self.hg[:])
        P.op("pool", lambda e: e.memset(self.ones[:], 1.0), [], [self.ones])
        P.op("pool", lambda e: e.memset(self.epsT[:], 1e-6), [], [self.epsT])
        P.op("pool", lambda e: e.memset(self.vsaug[:], 1.0), [], [self.vsaug])
        P.op("pool", lambda e: e.memset(self.vwaug[:], 1.0), [], [self.vwaug])
        P.op("pool", lambda e: e.memset(self.vcaug[:], 1.0), [], [self.vcaug])
        s = self.rot("st", st)
        P.dma("sp", s, s[:, 0:128], self.identD, self.identD[:])
        P.op("pool", lambda e, s=s: e.tensor_copy(out=self.ident[:], in_=s[:, 0:128]), [s], [self.ident])
        s = self.rot("st", st)
        P.dma("sp", s, s[:, 0:512], self.ovlD, self.ovlD[:].rearrange("p c j -> p (c j)"))
        P.op("pool", lambda e, s=s: e.tensor_copy(out=self.ovl[:].rearrange("p c j -> p (c j)"), in_=s[:, 0:512]),
             [s], [self.ovl])
        for g in range(2):
            self.set_combo(g, 0)
            P.barrier()
            self.setup_kv()
            for tr in range(2):
                self.set_combo(g, tr)
                P.barrier()
                self.setup_tables()
                for qt in range(NQT):
                    self.qtile(qt)


def fused_dram(P, r3=False):
    dr = {}
    D = lambda n, s: P.dram(n, s, F32, kind="ExternalInput")
    dr["xin"] = D("xin", [1024, XPAD])
    dr["memT"] = D("memT", [1024, 256])
    dr["gains"] = D("gains", [128, 128])
    dr["hg"] = D("hg", [128, 16])
    dr["convw"] = D("convw", [128, 36])
    dr["mem_w_kv"] = D("mem_w_kv", [4, 1024, 512])
    dr["w_out"] = D("w_out", [4, 1024, 1024])
    dr["w_up"] = D("w_up", [4, 1024, 4096])
    dr["w_down"] = D("w_down", [4, 4096, 1024])
    dr["b_w_in"] = D("b_w_in", [2, 1024, 1060])
    dr["a_w_in"] = D("a_w_in", [2, 1024, 2560])
    dr["w_kv"] = D("w_kv", [1024, 768])
    dr["ahg0"] = D("ahg0", [128, 8])
    dr["ahg1"] = D("ahg1", [128, 8])
    dr["pos"] = D("pos", [2, 64, 32])
    dr["cmp_w1"] = D("cmp_w1", [2, 2048, 256])
    dr["cmp_w2"] = D("cmp_w2", [2, 256, 64])
    dr["rel_bias"] = D("rel_bias", [32, 12])
    dr["OH"] = D("OH", [33, LTOT])
    dr["ident"] = D("ident", [128, 128])
    dr["EE"] = D("EE", [64, SEQ])
    dr["ovl"] = D("ovl", [128, 4, 128])
    if not r3:
        dr["topc"] = D("topc", [NQT, 128, 1024])
    if not r3:
        dr["out"] = P.dram("xT_out", [1024, SEQ], F32, kind="ExternalOutput")
    I = lambda n, s: P.dram(n, s, F32, kind="Internal")
    dr["X"] = I("Xs", [1024, XPAD])
    if not r3:
        dr["KV"] = I("KVs", [768, SEQ])
    dr["Q"] = I("Qs", [768, SEQ])
    dr["G"] = I("Gs", [36, SEQ])
    dr["Y"] = I("Ys", [768, XPAD])
    dr["Rrow"] = I("Rrow", [6, LTOT])
    return dr


def build_fused():
    nc = bass.Bass("TRN2", target_bir_lowering=False)
    P = Prog(nc)
    dr = fused_dram(P)
    P.push_scope()
    FTok(P, "L1", dr).run()
    P.pop_scope()
    for j in range(2):
        P.push_scope()
        FAttn(P, dr, j).run()
        P.pop_scope()
        P.push_scope()
        FTok(P, "L3" if j == 0 else "L5", dr).run()
        P.pop_scope()
    P.wait_all("sp", [dr["out"]])
    P.finalize()
    return nc, P


KPAD = 1536
QTILES = (3, 7, 11, 15)


def host_core_tables(k):
    tc = np.zeros((4, 128, 4, 2, 128), np.float32)
    jj = np.arange(128)[None, :]
    for j, qt in enumerate(QTILES):
        for ss in range(4):
            ta = qt * QT + 511 - (128 * ss + np.arange(128))
            back = (ta // 64)[:, None] - jj
            J = jj - 24 + 8 * k
            valid = (J >= 0) & (back >= 0)
            forced = (J == 0) | ((back >= 0) & (back < 2))
            tc[j, :, ss, 0] = (valid & ~forced).astype(np.float32)
            tc[j, :, ss, 1] = np.where(valid, np.where(forced, 1e4, 0.0), -1e30).astype(np.float32)
    kval = np.zeros((128, 16), np.float32)
    first = KPAD - 512 * k
    for c in range(12):
        a = 128 * c + np.arange(128)
        kval[:, c] = np.where(a >= first, 0.0, MASKV)
    kval[:, 12] = np.where(16 * np.arange(128) >= first, 0.0, MASKV)
    return tc.reshape(4, 128, 1024), kval


class FTok2(FTok):
    def __init__(self, P, mode, dr):
        FTok.__init__(self, P, mode, dr)
        if mode != "L1":
            self.xsrc = dr["Xown"]
            self.ytokT = dr["Yown"]
            self.qT_o, self.gT_o = dr["Qown"], dr["Gown"]

    def store_tile(self, dstT, r, msz, c0, t0, w, ost):
        if self.mode == "L1":
            off = KPAD if dstT is self.dr["KV"] else 0
            a = c0 + t0
            lo = HALO if a == 0 else 0
            g0 = NTOK * self.chunk + a + lo - HALO + off
            self.P.dma("sp", dstT, dstT[r:r + msz, g0:g0 + (w - lo)], ost, ost[0:msz, lo:w], nodep_out=True)
        else:
            self.P.dma("sp", dstT, dstT[r:r + msz, c0 + t0:c0 + t0 + w], ost, ost[0:msz, 0:w], nodep_out=True)

    def ytok_src(self, ch, c0, t0, w):
        if self.mode == "L1":
            return FTok.ytok_src(self, ch, c0, t0, w)
        return self.ytokT[ch * 128:(ch + 1) * 128, c0 + t0:c0 + t0 + w]

    def load_x(self):
        if self.mode == "L1":
            return FTok.load_x(self)
        P = self.P
        for c in range(8):
            P.dma("sp", self.x, self.x[:, c, 0:NTOK], self.xsrc, self.xsrc[c * 128:(c + 1) * 128, :], nodep_out=(c > 0))

    def store_x(self):
        if self.mode == "L1":
            return FTok.store_x(self)
        P = self.P
        dst = self.dr["out"] if self.mode == "L5" else self.dr["Xown"]
        for c in range(8):
            P.dma("sp", dst, dst[c * 128:(c + 1) * 128, :], self.x, self.x[:, c, 0:NTOK], nodep_out=True)

    def zero_kv_pad(self):
        P = self.P
        z = self.ost[0]
        P.op("pool", lambda e: e.memset(z[:], 0.0), [], [z])
        KV = self.dr["KV"]
        for r in range(6):
            for cb in range(3):
                P.dma("sp", KV, KV[r * 128:(r + 1) * 128, cb * 512:(cb + 1) * 512], z, z[:], nodep_out=True)

    def run(self):
        if self.mode == "L1":
            self.zero_kv_pad()
            return FTok.run(self)
        self.setup_consts()
        self.chunk = 0
        self.load_x()
        j = 0 if self.mode == "L3" else 1
        self.mem_kv(2 + j)
        R2 = [(0, 1024), (1024, 1024)]
        for (c0, n) in R2:
            self.post_b(j, c0, n, tiles_of(n))
        if j == 0:
            for (c0, n) in R2:
                self.pre_b(1, c0, n, tiles_of(n))
        self.store_x()


def extract_own(P, dr):
    kd = dr["kd"]

    def dyn3(src_t, r0, r1, pad, ncols):
        base = src_t[r0:r1, pad:pad + KPAD + 6144 + 512][:, bass.ds(kd, 6144 + 512)]
        return bass.AP(tensor=base.tensor, offset=base.offset, ap=[[ncols, r1 - r0], [2048, 4], [1, 512]])
    KVp, KVw = dr["KV"], dr["KVwin"]
    for rb in range(3):
        P.dma("sp", KVw, KVw[rb * 256:(rb + 1) * 256, :], KVp,
              KVp[rb * 256:(rb + 1) * 256, 0:KPAD + SEQ][:, bass.ds(kd, SEQ)], nodep_out=True)
    for rb in range(4):
        P.dma("sp", dr["Xown"], dr["Xown"][rb * 256:(rb + 1) * 256, :].rearrange("r (s c) -> r s c", c=512), dr["X"],
              dyn3(dr["X"], rb * 256, (rb + 1) * 256, HALO, XPAD), nodep_out=True)
    for rb in range(3):
        P.dma("sp", dr["Qown"], dr["Qown"][rb * 256:(rb + 1) * 256, :].rearrange("r (s c) -> r s c", c=512), dr["Q"],
              dyn3(dr["Q"], rb * 256, (rb + 1) * 256, 0, SEQ), nodep_out=True)
    P.dma("sp", dr["Gown"], dr["Gown"][:, :].rearrange("r (s c) -> r s c", c=512), dr["G"],
          dyn3(dr["G"], 0, 36, 0, SEQ), nodep_out=True)


class FAttn2(FAttn):
    def __init__(self, P, dr, jl):
        FAttn.__init__(self, P, dr, jl)
        self.KV = dr["KVwin"]
        self.Q, self.G, self.Y = dr["Qown"], dr["Gown"], dr["Yown"]
        self.kvalD = dr["kval"]

    def load_q(self, qt):
        P = self.P
        c0 = ((qt - 3) // 4) * 512
        for i2 in range(3):
            s = self.rot("st", self.st)
            for k in range(2):
                hh = self.heads[2 * i2 + k]
                r0 = (self.g * 6 + hh) * 64
                P.dma("sp", s, s[0:64, k * 512:(k + 1) * 512], self.Q, self.Q[r0:r0 + 64, c0:c0 + 512], nodep_out=(k > 0))
            for k in range(2):
                P.op("dve", lambda e, s=s, k=k, i2=i2: e.tensor_copy(
                    out=self.qraw[:, 2 * i2 + k, :], in_=rev_ap(s[0:64, k * 512:(k + 1) * 512], 512)), [s], [self.qraw])

    def load_gate(self, qt, h, br):
        P = self.P
        c0 = ((qt - 3) // 4) * 512
        gt = self.rot("gt", self.gt)
        g = self.G[:]
        row = (self.g * 6 + 3 * self.tr + h) * 3 + br
        P.dma("sp", gt, gt[:], self.G, bass.AP(tensor=g.tensor, offset=row * NTOK + c0, ap=[[0, 64], [1, 512]]))
        gr = self.rot("gtr", self.gtr)
        P.op("dve", lambda e: e.tensor_copy(out=gr[:], in_=rev_ap(gt[:], 512)), [gt], [gr])
        return gr

    def store_y(self, qt):
        P = self.P
        c0 = ((qt - 3) // 4) * 512
        bufs = [self.rdt, self.wt, self.tm]
        for h in range(3):
            b = bufs[h]
            P.op("dve", lambda e, b=b, h=h: e.tensor_copy(out=b[:], in_=rev_ap(self.y[:, h, :], 512)), [self.y], [b])
            r0 = (self.g * 6 + 3 * self.tr + h) * 64
            P.dma("sp", self.Y, self.Y[r0:r0 + 64, c0:c0 + 512], b, b[:], nodep_out=True)

    def topc_ap(self, qt):
        return self.topc[(qt - 3) // 4]

    def cmp_bias(self, cc, h, mixed):
        if cc == 0:
            return (self.kval[:, 12:13], self.kval) if mixed else (self.cfc[:, h:h + 1], self.cfc)
        return None if mixed else (self.cf[:, h:h + 1], self.cf)

    def sel_bias(self, kc, h, near):
        if kc < 12:
            return (self.kval[:, kc:kc + 1], self.kval) if near else (self.vbf[:, kc, h:h + 1], self.vbf)
        return None if near else (self.cf[:, h:h + 1], self.cf)

    def win_bias(self, c):
        return (self.kval[:, c:c + 1], self.kval) if c < 12 else None

    def setup_tables(self):
        FAttn.setup_tables(self)
        P = self.P
        for c in range(12):
            P.op("dve", lambda e, c=c: e.tensor_scalar(out=self.vbf[:, c, :], in0=self.cf[:, 0:3],
                                                       scalar1=self.kval[:, c:c + 1], scalar2=None, op0=ALU.add),
                 [self.cf, self.kval], [self.vbf])
        P.op("dve", lambda e: e.tensor_scalar(out=self.cfc[:], in0=self.cf[:], scalar1=self.kval[:, 12:13], scalar2=None,
                                              op0=ALU.add), [self.cf, self.kval], [self.cfc])

    def run(self):
        P = self.P
        st = self.st
        self.gtr = [P.sb("gtr%d" % i, [64, 512], F32) for i in range(2)]
        self.kval = P.sb("kval", [128, 16], F32)
        self.vbf = P.sb("vbf", [128, 12, 3], F32)
        self.cfc = P.sb("cfc", [128, 6], F32)
        P.dma("sp", self.kval, self.kval[:], self.kvalD, self.kvalD[:])
        P.dma("sp", self.hgs, self.hgs[:], self.hg, self.hg[:])
        P.op("pool", lambda e: e.memset(self.ones[:], 1.0), [], [self.ones])
        P.op("pool", lambda e: e.memset(self.epsT[:], 1e-6), [], [self.epsT])
        P.op("pool", lambda e: e.memset(self.vsaug[:], 1.0), [], [self.vsaug])
        P.op("pool", lambda e: e.memset(self.vwaug[:], 1.0), [], [self.vwaug])
        P.op("pool", lambda e: e.memset(self.vcaug[:], 1.0), [], [self.vcaug])
        s = self.rot("st", st)
        P.dma("sp", s, s[:, 0:128], self.identD, self.identD[:])
        P.op("pool", lambda e, s=s: e.tensor_copy(out=self.ident[:], in_=s[:, 0:128]), [s], [self.ident])
        s = self.rot("st", st)
        P.dma("sp", s, s[:, 0:512], self.ovlD, self.ovlD[:].rearrange("p c j -> p (c j)"))
        P.op("pool", lambda e, s=s: e.tensor_copy(out=self.ovl[:].rearrange("p c j -> p (c j)"), in_=s[:, 0:512]),
             [s], [self.ovl])
        for g in range(2):
            self.set_combo(g, 0)
            self.setup_kv()
            for tr in range(2):
                self.set_combo(g, tr)
                self.setup_tables()
                for qt in QTILES:
                    self.qtile(qt)


def build_fused2():
    nc = bass.Bass("TRN2", target_bir_lowering=False)
    P = Prog(nc)
    dr = fused_dram(P, r3=True)
    I = lambda n, s: P.dram(n, s, F32, kind="Internal")
    dr["KV"] = I("KVp", [768, KPAD + SEQ])
    dr["KVwin"] = I("KVwin", [768, SEQ])
    dr["Xown"] = I("Xown", [1024, NTOK])
    dr["Qown"] = I("Qown", [768, NTOK])
    dr["Gown"] = I("Gown", [36, NTOK])
    dr["Yown"] = I("Yown", [768, NTOK])
    dr["topc"] = P.dram("topc4", [4, 128, 1024], F32, kind="ExternalInput")
    dr["kval"] = P.dram("kval", [128, 16], F32, kind="ExternalInput")
    dr["out"] = P.dram("xT_own", [1024, NTOK], F32, kind="ExternalOutput")
    dr["kd"] = (nc.partition_id() % 4) * 512
    P.push_scope()
    FTok2(P, "L1", dr).run()
    P.pop_scope()
    extract_own(P, dr)
    for j in range(2):
        P.push_scope()
        FAttn2(P, dr, j).run()
        P.pop_scope()
        P.push_scope()
        FTok2(P, "L3" if j == 0 else "L5", dr).run()
        P.pop_scope()
    P.wait_all("sp", [dr["out"]])
    P.finalize()
    return nc, P


from concourse.bass_utils import run_bass_kernel_spmd


def attn_hg(inp, jlayer):
    hg = np.zeros((128, 8), np.float32)
    hg[:, 0] = np.tile(inp["b_q_gain"][jlayer], 2)
    for i in range(3):
        hg[:, 1 + i] = np.tile(inp["k_gain"][i], 2)
    return hg


def kernel(**inputs):
    inp = {k: np.ascontiguousarray(np.asarray(v, dtype=np.float32)) for k, v in inputs.items()}
    common = dict(host_consts())
    del common["topc"]
    common.update(gains=host_gains(inp), hg=host_hg(inp), convw=host_convw(inp), mem_w_kv=inp["mem_w_kv"],
                  w_out=inp["w_out"], w_up=inp["w_up"], w_down=inp["w_down"], b_w_in=inp["b_w_in"],
                  a_w_in=inp["a_w_in"], w_kv=inp["w_kv_shared"], ahg0=attn_hg(inp, 0), ahg1=attn_hg(inp, 1),
                  pos=np.ascontiguousarray(inp["cmp_pos"].transpose(0, 2, 1)), cmp_w1=inp["cmp_w1"],
                  cmp_w2=inp["cmp_w2"], rel_bias=inp["rel_bias"])
    per_b = []
    for b in range(2):
        xin = np.zeros((1024, XPAD), np.float32)
        xin[:, HALO:] = inp["x"][b].T
        per_b.append(dict(xin=xin, memT=np.ascontiguousarray(inp["mem"][b].T)))
    per_k = []
    for k in range(4):
        tc, kval = host_core_tables(k)
        per_k.append(dict(topc4=tc, kval=kval))
    maps = []
    for c in range(8):
        m = dict(common)
        m.update(per_b[c // 4])
        m.update(per_k[c % 4])
        maps.append(m)
    nc = build_fused2()[0]
    r = run_bass_kernel_spmd(nc, maps, core_ids=list(range(8))).results
    out = np.zeros((2, SEQ, 1024), np.float32)
    for c in range(8):
        b, k = c // 4, c % 4
        o = r[c]["xT_own"]
        for j in range(4):
            t0 = 512 * (4 * j + k)
            out[b, t0:t0 + 512, :] = o[:, j * 512:(j + 1) * 512].T
    return out
```

```python
import numpy as np
import concourse.bass as bass
import concourse.mybir as mybir
from contextlib import ExitStack

F32 = mybir.dt.float32
BF16 = mybir.dt.bfloat16
AF = mybir.ActivationFunctionType
ALU = mybir.AluOpType
AX = mybir.AxisListType

ENGS = ("pe", "act", "dve", "pool", "sp")


class T:
    __slots__ = ("h", "name", "lw", "rd", "dsem")

    def __init__(self, h, name):
        self.h = h
        self.name = name
        self.lw = None
        self.rd = []
        self.dsem = None

    def __getitem__(self, k):
        return self.h[k]


class Prog:
    def __init__(self, nc):
        self.nc = nc
        self.es = ExitStack()
        self.ins = {e: [] for e in ENGS}
        self.nsem = 0
        self.dsems = []
        self.free_dsems = []
        self.marked = set()
        self.rw = {e: {} for e in ENGS}
        self.cur = self.es
        self.uid = 0
        self.last_tok = {e: None for e in ENGS}
        self.epoch = 0
        self.ep_of = {}

    def _prune(self, eng, deps):
        out = []
        w = self.rw[eng]
        for d in deps:
            key = (d[0], d[1]); val = d[2]
            if w.get(key, -1) >= val:
                continue
            w[key] = val
            out.append(d)
        return out

    def sb(self, name, shape, dt):
        self.uid += 1
        name = "%s_%d" % (name, self.uid)
        return T(self.cur.enter_context(self.nc.sbuf_tensor(name, list(shape), dt)), name)

    def ps(self, name, shape, dt=F32):
        self.uid += 1
        name = "%s_%d" % (name, self.uid)
        return T(self.cur.enter_context(self.nc.psum_tensor(name, list(shape), dt)), name)

    def push_scope(self):
        self.cur = ExitStack()

    def pop_scope(self):
        self.barrier()
        self.cur.close()
        self.cur = self.es
        self.epoch += 1

    def barrier(self):
        for e in ENGS:
            deps = [t for e2, t in self.last_tok.items() if e2 != e and t is not None]
            deps += [("d", s, c[1]) for s, c in enumerate(self.dsems) if c[1] > 0]
            deps = self._prune(e, deps)
            self.ins[e].append(dict(deps=deps, fn=None, dma=None))

    def dram(self, name, shape, dt, kind="Internal"):
        return T(self.nc.dram_tensor(name, list(shape), dt, kind=kind), name)

    def _new_dsem(self):
        h = self.es.enter_context(self.nc.semaphore("d%d" % len(self.dsems)))
        self.dsems.append([h, 0])
        return len(self.dsems) - 1

    def _deps(self, eng, reads, writes):
        deps = []
        for t in reads:
            if t.lw is not None:
                deps.append(t.lw)
        for t in writes:
            if t.lw is not None:
                deps.append(t.lw)
            deps.extend(t.rd)
        out = []
        for d in deps:
            if d[0] == "e":
                if d[1] == eng:
                    continue
            out.append(d)
        if eng != "pe":
            for t in reads:
                if t.lw is not None and t.lw[0] == "e" and t.lw[1] == eng:
                    out.append(t.lw)
        return out

    def op(self, eng, fn, reads=(), writes=()):
        deps = self._prune(eng, self._deps(eng, reads, writes))
        idx = len(self.ins[eng])
        self.ins[eng].append(dict(deps=deps, fn=fn, dma=None))
        tok = ("e", eng, idx)
        self.last_tok[eng] = tok
        self.ep_of[(eng, idx)] = self.epoch
        for t in reads:
            t.rd = [r for r in t.rd if not (r[0] == "e" and r[1] == eng)]
            t.rd.append(tok)
        for t in writes:
            t.lw = tok
            t.rd = []
        return tok

    def dma(self, eng, out_t, out_ap, in_t, in_ap, nodep_out=False, **kw):
        reads = [in_t] if in_t is not None else []
        writes = [out_t] if out_t is not None else []
        deps = []
        for t in reads:
            if t.lw is not None:
                deps.append(t.lw)
        for t in writes:
            if nodep_out:
                continue
            if t.lw is not None:
                deps.append(t.lw)
            deps.extend(t.rd)
        owner = out_t if (out_t is not None and out_t.dsem is not None) else None
        if owner is None:
            owner = out_t if out_t is not None else in_t
        if owner.dsem is None:
            owner.dsem = self._new_dsem()
        s = owner.dsem
        self.dsems[s][1] += 16
        tok = ("d", s, self.dsems[s][1])
        deps = self._prune(eng, deps)
        self.ins[eng].append(dict(deps=deps, fn=None, dma=(out_ap, in_ap, s, kw)))
        for t in reads:
            t.rd.append(tok)
        for t in writes:
            t.lw = tok
            t.rd = []
        return tok

    def wait_all(self, eng, tiles):
        deps = []
        for t in tiles:
            if t.lw is not None:
                deps.append(t.lw)
            deps.extend(t.rd)
        deps = self._prune(eng, deps)
        self.ins[eng].append(dict(deps=deps, fn=None, dma=None))

    def finalize(self):
        nc = self.nc
        for e in ENGS:
            for rec in self.ins[e]:
                for d in rec["deps"]:
                    if d[0] == "e":
                        self.marked.add((d[1], d[2]))
        count_at = {}
        for e in ENGS:
            c = {}
            for i, rec in enumerate(self.ins[e]):
                if (e, i) in self.marked:
                    ep = self.ep_of[(e, i)]
                    c[ep] = c.get(ep, 0) + 1
                    count_at[(e, i)] = c[ep]
        esem = {(e, ep): self.es.enter_context(nc.semaphore("s_%s_%d" % (e, ep)))
                for e in ENGS for ep in range(self.epoch + 1)}
        engobj = {"pe": "tensor", "act": "scalar", "dve": "vector", "pool": "gpsimd", "sp": "sync"}
        block = self.es.enter_context(nc.Block())
        stats = {}

        def make_body(e):
            def body(eng):
                waited = {}
                nwait = 0
                for i, rec in enumerate(self.ins[e]):
                    need = {}
                    for d in rec["deps"]:
                        if d[0] == "e":
                            key = ("e", (d[1], self.ep_of[(d[1], d[2])])); val = count_at[(d[1], d[2])]
                        else:
                            key = ("d", d[1]); val = d[2]
                        if waited.get(key, 0) >= val:
                            continue
                        if need.get(key, 0) < val:
                            need[key] = val
                    for key, val in need.items():
                        h = esem[key[1]] if key[0] == "e" else self.dsems[key[1]][0]
                        eng.wait_ge(h, val)
                        waited[key] = val
                        nwait += 1
                    if rec.get("cc") is not None:
                        rec["cc"](eng).then_inc(self.dsems[rec["ccsem"]][0], 16)
                    elif rec["dma"] is not None:
                        out_ap, in_ap, s, kw = rec["dma"]
                        try:
                            eng.dma_start(out=out_ap, in_=in_ap, **kw).then_inc(self.dsems[s][0], 16)
                        except Exception:
                            print("DMA FAILED:", e, "out=", out_ap, "in=", in_ap)
                            raise
                    elif rec["fn"] is not None:
                        ins = rec["fn"](eng)
                        if (e, i) in self.marked:
                            ins.then_inc(esem[(e, self.ep_of[(e, i)])], 1)
                stats[e] = (len(self.ins[e]), nwait)
            return body

        for e in ENGS:
            if self.ins[e]:
                getattr(block, engobj[e])(make_body(e))
        self.stats = stats
        self.es.close()


NTOK = 2048
HALO = 4
NCOL = NTOK + HALO
RANGES = [(0, 1028), (1028, 1024)]
EPS = 1e-6


def tiles_of(n):
    out = []
    o = 0
    while o < n:
        w = min(512, n - o)
        out.append((o, w))
        o += w
    return out


class TokStage:
    def __init__(self, P, mode):
        self.P = P
        self.mode = mode
        nc = P.nc
        D = lambda n, s: P.dram(n, s, F32, kind="ExternalInput")
        O = lambda n, s: P.dram(n, s, F32, kind="ExternalOutput")
        self.xT = D("xT", [1024, NCOL])
        self.memT = D("memT", [1024, 256])
        self.gains = D("gains", [128, 8 * 16])
        self.hg = D("hg", [128, 16])
        self.flag = D("flag", [128, 1])
        self.mem_w_kv = D("mem_w_kv", [4, 1024, 512])
        self.w_out = D("w_out", [4, 1024, 1024])
        self.w_up = D("w_up", [4, 1024, 4096])
        self.w_down = D("w_down", [4, 4096, 1024])
        self.b_w_in = D("b_w_in", [2, 1024, 1060])
        if mode == "L1":
            self.a_w_in = D("a_w_in", [2, 1024, 2560])
            self.convw = D("convw", [128, 2 * 6 * 3])
            self.w_kv = D("w_kv", [1024, 768])
            self.kvT_o = O("kvT_o", [768, NCOL])
        else:
            self.ytokT = D("ytokT", [768, NCOL])
        self.xT_o = O("xT_o", [1024, NCOL])
        if mode != "L5":
            self.qT_o = O("qT_o", [768, NCOL])
            self.gT_o = O("gT_o", [36, NCOL])

        self.alloc()

    def alloc(self):
        P = self.P
        sb = P.sb
        self.x = sb("x", [128, 8, NCOL], F32)
        self.h = sb("h", [128, 8, 1028], BF16)
        self.y = sb("y", [128, 8, 1028], BF16)
        self.act = sb("act", [128, 8, 1028], BF16)
        self.u = sb("u", [128, 2, 1028], F32)
        self.v = sb("v", [128, 2, 1030], F32)
        self.qm = sb("qm", [128, 2, 1028], F32)
        self.carry = sb("carry", [128, 6, 2], F32)
        self.rstd = sb("rstd", [128, 512], F32)
        self.sq = [sb("sq%d" % i, [128, 512], BF16) for i in range(2)]
        self.tmp = [sb("tmp%d" % i, [128, 512], F32) for i in range(3)]
        self.ost = [sb("ost%d" % i, [128, 512], F32) for i in range(2)]
        self.wb = [sb("wb%d" % i, [128, 8, 256], BF16) for i in range(4)]
        self.memh = sb("memh", [128, 8, 256], BF16)
        self.kraw = sb("kraw", [128, 2, 256], F32)
        self.memk = sb("memk", [128, 2, 256], BF16)
        self.vaug = sb("vaug", [128, 2, 4, 128], BF16)
        self.qn = sb("qn", [128, 512], BF16)
        self.eT = [sb("eT%d" % i, [128, 512], BF16) for i in range(2)]
        self.rd = sb("rd", [64, 512], F32)
        self.g = sb("g", [128, 8 * 16], F32)
        self.hgs = sb("hgs", [128, 16], F32)
        self.flg = sb("flg", [128, 1], F32)
        self.cw = sb("cw", [128, 36], F32)
        self.ones = sb("ones", [128, 128], BF16)
        self.bd = sb("bd", [128, 128], BF16)
        self.epsT = sb("epsT", [128, 1], F32)
        self.pp = [P.ps("pp%d" % i, [128, 512]) for i in range(4)]
        self.pl = [P.ps("pl%d" % i, [128, 512]) for i in range(2)]
        self.po = P.ps("po", [128, 512])
        self.pn = P.ps("pn", [128, 512])
        self.cnt = dict(wf=0, wb=0, pp=0, pl=0, sq=0, tmp=0, ost=0, eT=0)

    def rot(self, name, lst):
        i = self.cnt[name]
        self.cnt[name] += 1
        return lst[i % len(lst)]

    def stream(self, wT, w_ap, ncols):
        P = self.P
        wb = self.rot("wb", self.wb)
        P.dma("pool", wb, wb[:, :, 0:ncols], wT, w_ap.rearrange("(c p) n -> p c n", p=128))
        return wb

    def proj(self, wb, sub, msz, rhs_t, rhs_fn, tiles, consumer):
        P = self.P
        for (t0, w) in tiles:
            ps = self.rot("pp", self.pp)
            for kc in range(8):
                P.op("pe", lambda e, ps=ps, kc=kc, t0=t0, w=w: e.matmul(
                    ps[0:msz, 0:w], lhsT=wb[:, kc, sub * 128: sub * 128 + msz], rhs=rhs_fn(kc, t0, w),
                    start=(kc == 0), stop=(kc == 7)), [wb, rhs_t], [ps])
            consumer(ps, t0, w)

    def setup(self):
        P = self.P
        P.dma("sp", self.g, self.g[:], self.gains, self.gains[:])
        P.dma("sp", self.hgs, self.hgs[:], self.hg, self.hg[:])
        P.dma("sp", self.flg, self.flg[:], self.flag, self.flag[:])
        if self.mode == "L1":
            P.dma("sp", self.cw, self.cw[:], self.convw, self.convw[:])
        for c in range(8):
            P.dma("sp", self.x, self.x[:, c, :], self.xT, self.xT[c * 128:(c + 1) * 128, :], nodep_out=True)
            P.dma("sp", self.u, self.memx(c, 0, 256), self.memT, self.memT[c * 128:(c + 1) * 128, :], nodep_out=True)
        P.op("pool", lambda e: e.memset(self.ones[:], 1.0), [], [self.ones])
        P.op("pool", lambda e: e.memset(self.bd[:], 0.0), [], [self.bd])
        P.op("pool", lambda e: e.memset(self.bd[0:64, 0:64], 1.0), [], [self.bd])
        P.op("pool", lambda e: e.memset(self.bd[64:128, 64:128], 1.0), [], [self.bd])
        P.op("pool", lambda e: e.memset(self.epsT[:], EPS), [], [self.epsT])
        P.op("pool", lambda e: e.memset(self.vaug[:], 1.0), [], [self.vaug])
        P.op("pool", lambda e: e.memset(self.carry[:], 0.0), [], [self.carry])
        self.norm(self.u, self.memx, 256, [(0, 256)], 15, self.memh, 0)

    def gcol(self, which, c):
        return self.g[:, which * 8 + c: which * 8 + c + 1]

    def memx(self, c, t0, w):
        return self.u[:, c // 4, (c % 4) * 256 + t0: (c % 4) * 256 + t0 + w]

    def xs(self, c0):
        return lambda c, t0, w: self.x[:, c, c0 + t0: c0 + t0 + w]

    def norm(self, src, src_fn, n, tiles, which, dst, d0):
        P = self.P
        for (t0, w) in tiles:
            for c in range(8):
                sq = self.rot("sq", self.sq)
                P.op("act", lambda e, sq=sq, c=c, t0=t0, w=w: e.activation(
                    out=sq[:, 0:w], in_=src_fn(c, t0, w), func=AF.Square), [src], [sq])
                P.op("pe", lambda e, sq=sq, c=c, w=w: e.matmul(
                    self.pn[:, 0:w], lhsT=self.ones[:], rhs=sq[:, 0:w], start=(c == 0), stop=(c == 7)),
                    [self.ones, sq], [self.pn])
            P.op("act", lambda e, w=w: e.activation(out=self.rstd[:, 0:w], in_=self.pn[:, 0:w], func=AF.Sqrt,
                                                    bias=self.epsT[:], scale=1.0 / 1024.0),
                 [self.pn, self.epsT], [self.rstd])
            P.op("dve", lambda e, w=w: e.reciprocal(out=self.rstd[:, 0:w], in_=self.rstd[:, 0:w]),
                 [self.rstd], [self.rstd])
            for c in range(8):
                eng = "dve"
                P.op(eng, lambda e, c=c, t0=t0, w=w: e.scalar_tensor_tensor(
                    out=dst[:, c, d0 + t0: d0 + t0 + w], in0=src_fn(c, t0, w),
                    scalar=self.gcol(which, c), in1=self.rstd[:, 0:w], op0=ALU.mult, op1=ALU.mult),
                    [src, self.g, self.rstd], [dst])

    def headnorm(self, src_t, src_ap_fn, w, gain_ap, dst_t, dst_ap):
        P = self.P
        sq = self.rot("sq", self.sq)
        P.op("act", lambda e, sq=sq: e.activation(out=sq[:, 0:w], in_=src_ap_fn(), func=AF.Square), [src_t], [sq])
        P.op("pe", lambda e, sq=sq: e.matmul(self.pn[:, 0:w], lhsT=self.bd[:], rhs=sq[:, 0:w], start=True, stop=True),
             [self.bd, sq], [self.pn])
        P.op("act", lambda e: e.activation(out=self.rstd[:, 0:w], in_=self.pn[:, 0:w], func=AF.Sqrt,
                                           bias=self.epsT[:], scale=1.0 / 64.0), [self.pn, self.epsT], [self.rstd])
        P.op("dve", lambda e: e.reciprocal(out=self.rstd[:, 0:w], in_=self.rstd[:, 0:w]), [self.rstd], [self.rstd])
        P.op("dve", lambda e: e.scalar_tensor_tensor(out=dst_ap, in0=src_ap_fn(), scalar=gain_ap,
                                                     in1=self.rstd[:, 0:w], op0=ALU.mult, op1=ALU.mult),
             [src_t, self.hgs, self.rstd], [dst_t])

    def mem_kv(self, l):
        P = self.P
        wb = self.stream(self.mem_w_kv, self.mem_w_kv[l, :, 0:256], 256)
        for sub in range(2):
            def cons(ps, t0, w, sub=sub):
                P.op("act", lambda e: e.activation(out=self.kraw[:, sub, 0:256], in_=ps[:, 0:256], func=AF.Copy),
                     [ps], [self.kraw])
            self.proj(wb, sub, 128, self.memh, lambda kc, t0, w: self.memh[:, kc, t0:t0 + w], [(0, 256)], cons)
            self.headnorm(self.kraw, lambda sub=sub: self.kraw[:, sub, 0:256], 256,
                          self.hgs[:, 4 + l: 5 + l], self.memk, self.memk[:, sub, 0:256])
        wb = self.stream(self.mem_w_kv, self.mem_w_kv[l, :, 256:512], 256)
        for mc in range(2):
            ps = self.rot("pp", self.pp)
            for kc in range(8):
                P.op("pe", lambda e, ps=ps, kc=kc, mc=mc: e.matmul(
                    ps[:, 0:256], lhsT=self.memh[:, kc, mc * 128:(mc + 1) * 128], rhs=wb[:, kc, 0:256],
                    start=(kc == 0), stop=(kc == 7)), [self.memh, wb], [ps])
            for hh in range(4):
                P.op("act", lambda e, ps=ps, mc=mc, hh=hh: e.activation(
                    out=self.vaug[:, mc, hh, 0:64], in_=ps[:, hh * 64:(hh + 1) * 64], func=AF.Copy), [ps], [self.vaug])

    def mem_attn(self, l, n, tiles):
        P = self.P
        for (t0, w) in tiles:
            for j in range(2):
                self.headnorm(self.qm, lambda j=j, t0=t0, w=w: self.qm[:, j, t0:t0 + w], w,
                              self.hgs[:, l: l + 1], self.qn, self.qn[:, 0:w])
                for hh in range(2):
                    b0 = 64 * hh
                    hd = 2 * j + hh
                    ets = []
                    for mc in range(2):
                        pl = self.rot("pl", self.pl)
                        P.op("pe", lambda e, pl=pl, mc=mc, b0=b0, j=j, w=w: e.matmul(
                            pl[:, 0:w], lhsT=self.memk[b0:b0 + 64, j, mc * 128:(mc + 1) * 128],
                            rhs=self.qn[b0:b0 + 64, 0:w], start=True, stop=True), [self.memk, self.qn], [pl])
                        eT = self.rot("eT", self.eT)
                        P.op("act", lambda e, pl=pl, eT=eT, w=w: e.activation(
                            out=eT[:, 0:w], in_=pl[:, 0:w], func=AF.Exp, scale=0.125), [pl], [eT])
                        ets.append(eT)
                    for mc in range(2):
                        P.op("pe", lambda e, mc=mc, hd=hd, w=w, eT=ets[mc]: e.matmul(
                            self.po[:, 0:w], lhsT=self.vaug[:, mc, hd, :], rhs=eT[:, 0:w],
                            start=(mc == 0), stop=(mc == 1)), [self.vaug, ets[mc]], [self.po])
                    P.op("dve", lambda e, w=w: e.reciprocal(out=self.rd[0:64, 0:w], in_=self.po[64:128, 0:w]),
                         [self.po], [self.rd])
                    P.op("dve", lambda e, w=w, b0=b0, j=j, t0=t0: e.tensor_tensor(
                        out=self.y[b0:b0 + 64, 6 + j, t0:t0 + w], in0=self.po[0:64, 0:w], in1=self.rd[0:64, 0:w],
                        op=ALU.mult), [self.po, self.rd], [self.y])

    def add_into_x(self, c0):
        P = self.P

        def mk(ch):
            def cons(ps, t0, w):
                P.op("dve", lambda e: e.tensor_tensor(
                    out=self.x[:, ch, c0 + t0: c0 + t0 + w], in0=ps[:, 0:w], in1=self.x[:, ch, c0 + t0: c0 + t0 + w],
                    op=ALU.add), [ps, self.x], [self.x])
            return cons
        return mk

    def out_proj(self, l, c0, n, tiles):
        mk = self.add_into_x(c0)
        for blk in range(4):
            wb = self.stream(self.w_out, self.w_out[l, :, blk * 256:(blk + 1) * 256], 256)
            for sub in range(2):
                self.proj(wb, sub, 128, self.y, lambda kc, t0, w: self.y[:, kc, t0:t0 + w], tiles, mk(blk * 2 + sub))

    def mlp(self, l, c0, n, tiles):
        P = self.P
        self.norm(self.x, self.xs(c0), n, tiles, 4 + l, self.h, 0)
        mk = self.add_into_x(c0)
        for sg in range(4):
            for blk in range(4):
                wb = self.stream(self.w_up, self.w_up[l, :, sg * 1024 + blk * 256: sg * 1024 + (blk + 1) * 256], 256)
                for sub in range(2):
                    def cons(ps, t0, w, fc=blk * 2 + sub):
                        tmp = self.rot("tmp", self.tmp)
                        P.op("act", lambda e: e.activation(out=tmp[:, 0:w], in_=ps[:, 0:w], func=AF.Relu), [ps], [tmp])
                        P.op("dve", lambda e: e.tensor_tensor(out=self.act[:, fc, t0:t0 + w], in0=tmp[:, 0:w],
                                                              in1=tmp[:, 0:w], op=ALU.mult), [tmp], [self.act])
                    self.proj(wb, sub, 128, self.h, lambda kc, t0, w: self.h[:, kc, t0:t0 + w], tiles, cons)
            for blk in range(4):
                wb = self.stream(self.w_down, self.w_down[l, sg * 1024:(sg + 1) * 1024, blk * 256:(blk + 1) * 256], 256)
                for sub in range(2):
                    self.proj(wb, sub, 128, self.act, lambda kc, t0, w: self.act[:, kc, t0:t0 + w], tiles,
                              mk(blk * 2 + sub))

    def mix_a(self, l, ri, c0, n, tiles):
        P = self.P
        W = self.a_w_in
        hrhs = lambda kc, t0, w: self.h[:, kc, t0:t0 + w]
        for i in range(3):
            wb = self.stream(W, W[l, :, 1536 + 256 * i: 1536 + 256 * (i + 1)], 256)
            for sub in range(2):
                def cons(ps, t0, w, sub=sub):
                    P.op("act", lambda e: e.activation(out=self.u[:, sub, t0:t0 + w], in_=ps[:, 0:w], func=AF.Copy),
                         [ps], [self.u])
                self.proj(wb, sub, 128, self.h, hrhs, tiles, cons)
            wb = self.stream(W, W[l, :, 768 + 256 * i: 768 + 256 * (i + 1)], 256)
            for sub in range(2):
                def cons(ps, t0, w, sub=sub):
                    P.op("dve", lambda e: e.tensor_tensor(out=self.v[:, sub, 2 + t0: 2 + t0 + w], in0=ps[:, 0:w],
                                                          in1=self.u[:, sub, t0:t0 + w], op=ALU.mult),
                         [ps, self.u], [self.v])
                self.proj(wb, sub, 128, self.h, hrhs, tiles, cons)
            for sub in range(2):
                ch = 2 * i + sub
                if ri == 0:
                    P.op("pool", lambda e, sub=sub: e.memset(self.v[:, sub, 0:2], 0.0), [], [self.v])
                    self.halo_fix(sub)
                    P.op("dve", lambda e, sub=sub, ch=ch: e.tensor_copy(out=self.carry[:, ch, :],
                                                                        in_=self.v[:, sub, n:n + 2]),
                         [self.v], [self.carry])
                else:
                    P.op("dve", lambda e, sub=sub, ch=ch: e.tensor_copy(out=self.v[:, sub, 0:2],
                                                                        in_=self.carry[:, ch, :]),
                         [self.carry], [self.v])
                cwc = lambda k, ch=ch: self.cw[:, l * 18 + ch * 3 + k: l * 18 + ch * 3 + k + 1]
                P.op("dve", lambda e, sub=sub, cwc=cwc: e.tensor_scalar(
                    out=self.u[:, sub, 0:n], in0=self.v[:, sub, 0:n], scalar1=cwc(0), scalar2=None, op0=ALU.mult),
                    [self.v, self.cw], [self.u])
                for k in (1, 2):
                    P.op("dve", lambda e, sub=sub, cwc=cwc, k=k: e.scalar_tensor_tensor(
                        out=self.u[:, sub, 0:n], in0=self.v[:, sub, k:k + n], scalar=cwc(k), in1=self.u[:, sub, 0:n],
                        op0=ALU.mult, op1=ALU.add), [self.v, self.cw, self.u], [self.u])
            wb = self.stream(W, W[l, :, 256 * i: 256 * (i + 1)], 256)
            for sub in range(2):
                def cons(ps, t0, w, sub=sub, ch=2 * i + sub):
                    P.op("dve", lambda e: e.tensor_tensor(out=self.y[:, ch, t0:t0 + w], in0=ps[:, 0:w],
                                                          in1=self.u[:, sub, t0:t0 + w], op=ALU.mult),
                         [ps, self.u], [self.y])
                self.proj(wb, sub, 128, self.h, hrhs, tiles, cons)
        wb = self.stream(W, W[l, :, 2304:2560], 256)
        self.qmem_proj(wb, tiles)

    def halo_fix(self, sub):
        self.P.op("dve", lambda e: e.tensor_scalar(
            out=self.v[:, sub, 2:2 + HALO], in0=self.v[:, sub, 2:2 + HALO], scalar1=self.flg[:, 0:1],
            scalar2=None, op0=ALU.mult), [self.v, self.flg], [self.v])

    def store_tile(self, dstT, r, msz, c0, t0, w, ost):
        self.P.dma("sp", dstT, dstT[r:r + msz, c0 + t0: c0 + t0 + w], ost, ost[0:msz, 0:w], nodep_out=True)

    def ytok_src(self, ch, c0, t0, w):
        return self.ytokT[ch * 128:(ch + 1) * 128, c0 + t0: c0 + t0 + w]

    def qmem_proj(self, wb, tiles):
        P = self.P
        for sub in range(2):
            def cons(ps, t0, w, sub=sub):
                P.op("act", lambda e: e.activation(out=self.qm[:, sub, t0:t0 + w], in_=ps[:, 0:w], func=AF.Copy),
                     [ps], [self.qm])
            self.proj(wb, sub, 128, self.h, lambda kc, t0, w: self.h[:, kc, t0:t0 + w], tiles, cons)

    def store_proj(self, wT, w_ap, ncols, dstT, row0, c0, tiles, func=AF.Copy):
        P = self.P
        wb = self.stream(wT, w_ap, ncols)
        nsub = (ncols + 127) // 128
        for sub in range(nsub):
            msz = min(128, ncols - sub * 128)

            def cons(ps, t0, w, sub=sub, msz=msz):
                ost = self.rot("ost", self.ost)
                P.op("act", lambda e: e.activation(out=ost[0:msz, 0:w], in_=ps[0:msz, 0:w], func=func), [ps], [ost])
                r = row0 + sub * 128
                self.store_tile(dstT, r, msz, c0, t0, w, ost)
            self.proj(wb, sub, msz, self.h, lambda kc, t0, w: self.h[:, kc, t0:t0 + w], tiles, cons)

    def pre_b(self, j, c0, n, tiles):
        W = self.b_w_in
        self.norm(self.x, self.xs(c0), n, tiles, 2 + j, self.h, 0)
        for i in range(3):
            self.store_proj(W, W[j, :, 256 * i:256 * (i + 1)], 256, self.qT_o, 256 * i, c0, tiles)
        self.store_proj(W, W[j, :, 768:804], 36, self.gT_o, 0, c0, tiles, func=AF.Sigmoid)

    def post_b(self, j, c0, n, tiles):
        P = self.P
        W = self.b_w_in
        l = 2 + j
        self.norm(self.x, self.xs(c0), n, tiles, l, self.h, 0)
        wb = self.stream(W, W[j, :, 804:1060], 256)
        self.qmem_proj(wb, tiles)
        for ch in range(6):
            for (t0, w) in tiles:
                tmp = self.rot("tmp", self.tmp)
                P.dma("sp", tmp, tmp[:, 0:w], self.ytokT, self.ytok_src(ch, c0, t0, w))
                P.op("act", lambda e, tmp=tmp, ch=ch, t0=t0, w=w: e.activation(out=self.y[:, ch, t0:t0 + w],
                                                                               in_=tmp[:, 0:w], func=AF.Copy), [tmp], [self.y])
        self.mem_attn(l, n, tiles)
        self.out_proj(l, c0, n, tiles)
        self.mlp(l, c0, n, tiles)

    def store_x(self):
        P = self.P
        for c in range(8):
            P.dma("sp", self.xT_o, self.xT_o[c * 128:(c + 1) * 128, :], self.x, self.x[:, c, :], nodep_out=True)

    def build(self):
        P = self.P
        self.setup()
        if self.mode == "L1":
            for l in range(2):
                self.mem_kv(l)
                for ri, (c0, n) in enumerate(RANGES):
                    tiles = tiles_of(n)
                    self.norm(self.x, self.xs(c0), n, tiles, l, self.h, 0)
                    self.mix_a(l, ri, c0, n, tiles)
                    self.mem_attn(l, n, tiles)
                    self.out_proj(l, c0, n, tiles)
                    self.mlp(l, c0, n, tiles)
            for ri, (c0, n) in enumerate(RANGES):
                tiles = tiles_of(n)
                self.norm(self.x, self.xs(c0), n, tiles, 8, self.h, 0)
                for i in range(3):
                    self.store_proj(self.w_kv, self.w_kv[:, 256 * i:256 * (i + 1)], 256, self.kvT_o, 256 * i, c0, tiles)
                self.pre_b(0, c0, n, tiles)
            self.store_x()
            outs = [self.kvT_o, self.qT_o, self.gT_o, self.xT_o]
        elif self.mode == "L3":
            self.mem_kv(2)
            for ri, (c0, n) in enumerate(RANGES):
                tiles = tiles_of(n)
                self.post_b(0, c0, n, tiles)
            for ri, (c0, n) in enumerate(RANGES):
                tiles = tiles_of(n)
                self.pre_b(1, c0, n, tiles)
            self.store_x()
            outs = [self.qT_o, self.gT_o, self.xT_o]
        else:
            self.mem_kv(3)
            for ri, (c0, n) in enumerate(RANGES):
                tiles = tiles_of(n)
                self.post_b(1, c0, n, tiles)
            self.store_x()
            outs = [self.xT_o]
        P.wait_all("sp", outs)
        P.finalize()


def build_tok(mode):
    nc = bass.Bass("TRN2", target_bir_lowering=False)
    P = Prog(nc)
    st = TokStage(P, mode)
    st.build()
    return nc, P


def host_gains(inp):
    g = np.zeros((16, 1024), np.float32)
    g[0:4] = inp["mix_norm"]
    g[4:8] = inp["mlp_norm"]
    g[8] = inp["kv_norm"]
    g[15] = inp["mem_norm"]
    return np.ascontiguousarray(g.reshape(16, 8, 128).transpose(2, 0, 1).reshape(128, 128))


def host_hg(inp):
    hg = np.zeros((128, 16), np.float32)
    for l in range(4):
        hg[:, l] = np.tile(inp["mem_q_gain"][l], 2)
        hg[:, 4 + l] = np.tile(inp["mem_k_gain"][l], 2)
    return hg


def host_convw(inp):
    w = inp["a_conv_w"].reshape(2, 3, 6, 128)
    return np.ascontiguousarray(w.transpose(3, 0, 2, 1).reshape(128, 36))

import math

SEQ = 8192
QT = 512
NQT = SEQ // QT
DC, LC = 3040, 5104
DW, LW = 1023, 1536
DS, LS = 1407, 1920
LTOT = LC + LW + LS
LU, LTW, LTS = 3072, 1408, 1792
MASKV = -30000.0


def rel_bucket_np(d):
    n = np.maximum(d, 0)
    nf = np.maximum(n, 1).astype(np.float32)
    large = 16 + (np.log(nf / np.float32(16)) / np.float32(math.log(1024 / 16)) * np.float32(16)).astype(np.int32)
    large = np.minimum(large, 31)
    return np.where(n < 16, n, large)


def host_consts():
    c = {}
    oh = np.zeros((33, LTOT), np.float32)
    x = np.arange(LC); d = DC - x
    b = rel_bucket_np(d)
    oh[b[d >= 0], x[d >= 0]] = 1.0
    oh[32, x[d < 0]] = 1.0
    x = np.arange(LW); d = DW - x; ok = (d >= 0) & (d < 512)
    b = rel_bucket_np(d)
    oh[b[ok], LC + x[ok]] = 1.0
    oh[32, LC + x[~ok]] = 1.0
    x = np.arange(LS); d = DS - x; ok = d >= 0
    b = rel_bucket_np(d)
    oh[b[ok], LC + LW + x[ok]] = 1.0
    oh[32, LC + LW + x[~ok]] = 1.0
    c["OH"] = oh
    c["ident"] = np.eye(128, dtype=np.float32)
    k = np.arange(SEQ)
    r = 2 * ((k // 128) % 32) + ((k % 128) >= 64)
    ee = np.zeros((64, SEQ), np.float32)
    ee[r, k] = 30000.0
    c["EE"] = ee
    i = np.arange(512)[:, None] * 16
    s = np.arange(128)[None, :] * 64
    ov = np.clip(np.minimum(i + 32, s + 64) - np.maximum(i, s), 0, None).astype(np.float32) / 32.0
    ov[511] = 0.0
    c["ovl"] = np.ascontiguousarray(ov.reshape(4, 128, 128).transpose(1, 0, 2))
    tc = np.zeros((NQT, 128, 4, 2, 128), np.float32)
    j = np.arange(128)[None, :]
    for qt in range(NQT):
        for ss in range(4):
            t = qt * QT + 511 - (128 * ss + np.arange(128))
            back = (t // 64)[:, None] - j
            forced = (j == 0) | ((back >= 0) & (back < 2))
            A = ((back >= 0) & ~forced).astype(np.float32)
            B = np.where(back >= 0, np.where(forced, 1e4, 0.0), -1e30).astype(np.float32)
            tc[qt, :, ss, 0] = A
            tc[qt, :, ss, 1] = B
    c["topc"] = tc.reshape(NQT, 128, 1024)
    return c


class Attn:
    def __init__(self, P):
        self.P = P
        D = lambda n, s: P.dram(n, s, F32, kind="ExternalInput")
        self.qT = D("qT", [384, SEQ])
        self.gB = D("gB", [9, SEQ])
        self.kvT = D("kvT", [384, SEQ])
        self.vtok = D("vtok", [SEQ, 128])
        self.blk = D("blk", [2, 64, 32, 512])
        self.hg = D("hg", [128, 8])
        self.pos = D("pos", [2, 64, 32])
        self.w1 = D("w1", [2, 2048, 256])
        self.w2 = D("w2", [2, 256, 64])
        self.rb6 = D("rb6", [32, 6])
        self.OH = D("OH", [33, LTOT])
        self.identD = D("ident", [128, 128])
        self.EE = D("EE", [64, SEQ])
        self.ovlD = D("ovl", [128, 4, 128])
        self.topc = D("topc", [NQT, 128, 1024])
        self.yT = P.dram("yT", [192, SEQ], F32, kind="ExternalOutput")
        self.Rrow = P.dram("Rrow", [6, LTOT], F32, kind="Internal")
        self.alloc()

    def alloc(self):
        P = self.P
        sb = P.sb
        self.ksaug = sb("ksaug", [128, SEQ], BF16)
        self.kwT = sb("kwT", [64, SEQ], BF16)
        self.vsaug = sb("vsaug", [128, 64, 128], BF16)
        self.vwaug = sb("vwaug", [128, 64, 128], BF16)
        self.kcT = sb("kcT", [64, 512], BF16)
        self.vcaug = sb("vcaug", [128, 4, 128], BF16)
        self.Ubig = sb("Ubig", [128, 6 * LU], BF16)
        self.TTw = [sb("TTw%d" % h, [128, LTW], BF16) for h in range(3)]
        self.TTs = [sb("TTs%d" % h, [128, LTS], BF16) for h in range(3)]
        self.ident = sb("identb", [128, 128], BF16)
        self.ones = sb("ones", [128, 128], BF16)
        self.ovl = sb("ovlb", [128, 4, 128], BF16)
        self.hgs = sb("hgs", [128, 8], F32)
        self.cf = sb("cf", [128, 6], F32)
        self.tab = sb("tab", [33, 6], F32)
        self.epsT = sb("epsT", [128, 1], F32)
        self.st = [sb("st%d" % i, [128, 1024], F32) for i in range(2)]
        self.sq = [sb("sq%d" % i, [128, 512], BF16) for i in range(2)]
        self.rq = sb("rq", [128, 512], F32)
        self.qraw = sb("qraw", [64, 6, 512], F32)
        self.qn = sb("qn", [64, 6, 512], BF16)
        self.qaug = sb("qaug", [128, 3, 2, 512], BF16)
        self.pT = [sb("pT%d" % i, [128, 512], BF16) for i in range(6)]
        self.rdq = sb("rdq", [128, 4], F32)
        self.rdb = sb("rdb", [128, 512], F32)
        self.impacc = sb("impacc", [128, 512], F32)
        self.tcs = sb("tcs", [128, 1024], F32)
        self.sc = sb("sc", [128, 128], F32)
        self.sc2 = sb("sc2", [128, 128], F32)
        self.mx = sb("mx", [128, 16], F32)
        self.mneg = sb("mneg", [128, 128], BF16)
        self.gt = [sb("gt%d" % i, [64, 512], F32) for i in range(2)]
        self.rdt = sb("rdt", [64, 512], F32)
        self.wt = sb("wt", [64, 512], F32)
        self.tm = sb("tm", [64, 512], F32)
        self.y = sb("y", [64, 3, 512], F32)
        self.w1b = sb("w1b", [64, 4, 256], BF16)
        self.blkb = sb("blkb", [64, 4, 512], BF16)
        self.posT = sb("posT", [64, 32], F32)
        self.w2b = sb("w2b", [128, 2, 64], BF16)
        self.gx = self.rdb
        self.gy = self.impacc
        self.gT = [sb("gT%d" % i, [128, 512], BF16) for i in range(2)]
        self.A = [P.ps("A%d" % i, [128, 512]) for i in range(3)]
        self.S = [P.ps("S%d" % i, [128, 512]) for i in range(2)]
        self.X0 = P.ps("X0", [128, 512])
        self.X1 = P.ps("X1", [128, 512])
        self.XT = P.ps("XT", [128, 512], BF16)
        self.cnt = {}

    def uslice(self, h, mo):
        return self.Ubig[:, h * LU + mo: h * LU + mo + 512]

    def rot(self, name, lst):
        i = self.cnt.get(name, 0)
        self.cnt[name] = i + 1
        return lst[i % len(lst)]

    def norm64(self, src_t, src_ap, w, gain_ap, dst_t, dst_ap):
        P = self.P
        sq = self.rot("sq", self.sq)
        P.op("act", lambda e: e.activation(out=sq[0:64, 0:w], in_=src_ap, func=AF.Square), [src_t], [sq])
        P.op("pe", lambda e: e.matmul(self.X0[0:64, 0:w], lhsT=self.ones[0:64, 0:64], rhs=sq[0:64, 0:w],
                                      start=True, stop=True), [self.ones, sq], [self.X0])
        P.op("act", lambda e: e.activation(out=self.rq[0:64, 0:w], in_=self.X0[0:64, 0:w], func=AF.Sqrt,
                                           bias=self.epsT[0:64, :], scale=1.0 / 64.0), [self.X0, self.epsT], [self.rq])
        P.op("dve", lambda e: e.reciprocal(out=self.rq[0:64, 0:w], in_=self.rq[0:64, 0:w]), [self.rq], [self.rq])
        P.op("dve", lambda e: e.scalar_tensor_tensor(out=dst_ap, in0=src_ap, scalar=gain_ap, in1=self.rq[0:64, 0:w],
                                                     op0=ALU.mult, op1=ALU.mult), [src_t, self.hgs, self.rq], [dst_t])

    def setup(self):
        P = self.P
        st = self.st
        P.dma("sp", self.hgs, self.hgs[:], self.hg, self.hg[:])
        P.op("pool", lambda e: e.memset(self.ones[:], 1.0), [], [self.ones])
        P.op("pool", lambda e: e.memset(self.epsT[:], 1e-6), [], [self.epsT])
        P.op("pool", lambda e: e.memset(self.vsaug[:], 1.0), [], [self.vsaug])
        P.op("pool", lambda e: e.memset(self.vwaug[:], 1.0), [], [self.vwaug])
        P.op("pool", lambda e: e.memset(self.vcaug[:], 1.0), [], [self.vcaug])
        s = self.rot("st", st)
        P.dma("sp", s, s[:, 0:128], self.identD, self.identD[:])
        P.op("pool", lambda e, s=s: e.tensor_copy(out=self.ident[:], in_=s[:, 0:128]), [s], [self.ident])
        s = self.rot("st", st)
        P.dma("sp", s, s[:, 0:512], self.ovlD, self.ovlD[:].rearrange("p c j -> p (c j)"))
        P.op("pool", lambda e, s=s: e.tensor_copy(out=self.ovl[:].rearrange("p c j -> p (c j)"), in_=s[:, 0:512]),
             [s], [self.ovl])
        self.setup_tables()
        self.setup_kv()

    def load_rb(self):
        P = self.P
        rb = self.rb6[:]
        P.dma("sp", self.cf, self.cf[:], self.rb6, bass.AP(tensor=rb.tensor, offset=31 * 6, ap=[[0, 128], [1, 6]]))
        P.dma("sp", self.tab, self.tab[0:32, :], self.rb6, self.rb6[:])

    def setup_tables(self):
        P = self.P
        st = self.st
        P.op("pool", lambda e: e.memset(self.tab[:], MASKV), [], [self.tab])
        self.load_rb()
        P.op("dve", lambda e: e.tensor_scalar(out=self.tab[0:32, :], in0=self.tab[0:32, :], scalar1=8.0, scalar2=None,
                                              op0=ALU.mult), [self.tab], [self.tab])
        o = 0
        while o < LTOT:
            w = min(512, LTOT - o)
            s = self.rot("st", st)
            P.dma("sp", s, s[0:33, 0:w], self.OH, self.OH[:, o:o + w])
            P.op("pe", lambda e, s=s, w=w: e.matmul(self.X1[0:6, 0:w], lhsT=self.tab[:, :], rhs=s[0:33, 0:w],
                                                    start=True, stop=True), [self.tab, s], [self.X1])
            s2 = self.rot("st", st)
            P.op("act", lambda e, s2=s2, w=w: e.activation(out=s2[0:6, 0:w], in_=self.X1[0:6, 0:w], func=AF.Copy),
                 [self.X1], [s2])
            P.dma("sp", self.Rrow, self.Rrow[:, o:o + w], s2, s2[0:6, 0:w])
            o += w
        rr = self.Rrow[:]

        def expand(dst, d0, h, base, pstep, L):
            o = 0
            while o < L:
                w = min(1024, L - o)
                s = self.rot("st", st)
                P.dma("sp", s, s[:, 0:w], self.Rrow,
                      bass.AP(tensor=rr.tensor, offset=h * LTOT + base + o, ap=[[pstep, 128], [1, w]]))
                P.op("pool", lambda e, s=s, o=o, w=w: e.tensor_copy(out=dst[:, d0 + o:d0 + o + w], in_=s[:, 0:w]),
                     [s], [dst])
                o += w
        for h in range(6):
            expand(self.Ubig, h * LU, h, 0, 16, LU)
        for h in range(3):
            expand(self.TTw[h], 0, h, LC, 1, LTW)
            expand(self.TTs[h], 0, h, LC + LW, 1, LTS)

    def load_kv_piece(self, s, slot, c0):
        self.P.dma("sp", s, s[0:64, :], self.kvT, self.kvT[slot * 64:(slot + 1) * 64, c0:c0 + 1024])

    def load_v_piece(self, piece):
        P = self.P
        c0 = piece * 1024
        s = self.rot("st", self.st)
        P.dma("sp", s, s[:, :].rearrange("p (c f) -> p c f", f=128), self.vtok,
              self.vtok[c0:c0 + 1024, :].rearrange("(c p) f -> p c f", p=128))
        sv = s[:, :].rearrange("p (c f) -> p c f", f=128)
        P.op("pool", lambda e, sv=sv, piece=piece: e.tensor_copy(
            out=self.vsaug[:, piece * 8:(piece + 1) * 8, 0:64], in_=sv[:, :, 0:64]), [s], [self.vsaug])
        P.op("pool", lambda e, sv=sv, piece=piece: e.tensor_copy(
            out=self.vwaug[:, piece * 8:(piece + 1) * 8, 0:64], in_=sv[:, :, 64:128]), [s], [self.vwaug])

    def setup_kv(self):
        P = self.P
        st = self.st
        for piece in range(8):
            c0 = piece * 1024
            for slot, gcol, dst in ((2, 2, self.ksaug), (4, 3, self.kwT)):
                s = self.rot("st", st)
                self.load_kv_piece(s, slot, c0)
                for hf in range(2):
                    a = hf * 512
                    self.norm64(s, s[0:64, a:a + 512], 512, self.hgs[0:64, gcol:gcol + 1], dst,
                                dst[0:64, c0 + a:c0 + a + 512])
            s = self.rot("st", st)
            P.dma("sp", s, s[64:128, :], self.EE, self.EE[:, c0:c0 + 1024])
            P.op("pool", lambda e, s=s, c0=c0: e.tensor_copy(out=self.ksaug[64:128, c0:c0 + 1024], in_=s[64:128, :]),
                 [s], [self.ksaug])
            self.load_v_piece(piece)
        self.compress()

    def compress_rhs(self, kvi, tg):
        P = self.P
        for th in range(2):
            s = self.rot("st", self.st)
            t0 = tg * 4 + th * 2
            P.dma("sp", s, s[0:64, :].rearrange("p (t n) -> p t n", n=512), self.blk,
                  self.blk[kvi, :, t0:t0 + 2, :])
            for tt in range(2):
                t = t0 + tt
                P.op("dve", lambda e, s=s, tt=tt, th=th, t=t: e.tensor_scalar(
                    out=self.blkb[:, th * 2 + tt, :], in0=s[0:64, tt * 512:(tt + 1) * 512],
                    scalar1=self.posT[:, t:t + 1], scalar2=None, op0=ALU.add), [s, self.posT], [self.blkb])

    def compress_begin(self, kvi):
        pass

    def compress(self):
        P = self.P
        st = self.st
        H = [self.A[0], self.A[1]]
        for kvi in range(2):
            self.compress_begin(kvi)
            P.dma("sp", self.posT, self.posT[:], self.pos, self.pos[kvi])
            s = self.rot("st", st)
            P.dma("sp", s, s[:, 0:128].rearrange("p (c d) -> p c d", d=64), self.w2,
                  self.w2[kvi].rearrange("(c p) d -> p c d", p=128))
            P.op("pool", lambda e, s=s: e.tensor_copy(out=self.w2b[:].rearrange("p c d -> p (c d)"), in_=s[:, 0:128]),
                 [s], [self.w2b])
            for tg in range(8):
                s = self.rot("st", st)
                P.dma("sp", s, s[0:64, :].rearrange("p (t j) -> p t j", j=256), self.w1,
                      self.w1[kvi, tg * 256:(tg + 1) * 256, :].rearrange("(t d) j -> d t j", d=64))
                P.op("pool", lambda e, s=s: e.tensor_copy(out=self.w1b[:].rearrange("p t j -> p (t j)"),
                                                          in_=s[0:64, :]), [s], [self.w1b])
                self.compress_rhs(kvi, tg)
                for tl in range(4):
                    t = tg * 4 + tl
                    for jc in range(2):
                        P.op("pe", lambda e, tl=tl, jc=jc, t=t: e.matmul(
                            H[jc][:, :], lhsT=self.w1b[:, tl, jc * 128:(jc + 1) * 128], rhs=self.blkb[:, tl, :],
                            start=(t == 0), stop=(t == 31)), [self.w1b, self.blkb], [H[jc]])
            for jc in range(2):
                P.op("act", lambda e, jc=jc: e.activation(out=self.gx[:], in_=H[jc][:, :], func=AF.Copy),
                     [H[jc]], [self.gx])
                P.op("pool", lambda e: e.tensor_tensor(out=self.gy[:], in0=self.gx[:], in1=self.gx[:], op=ALU.mult),
                     [self.gx], [self.gy])
                P.op("dve", lambda e: e.tensor_scalar(out=self.gy[:], in0=self.gy[:], scalar1=0.044715, scalar2=1.0,
                                                      op0=ALU.mult, op1=ALU.add), [self.gy], [self.gy])
                P.op("pool", lambda e: e.tensor_tensor(out=self.gy[:], in0=self.gy[:], in1=self.gx[:], op=ALU.mult),
                     [self.gy, self.gx], [self.gy])
                P.op("act", lambda e: e.activation(out=self.gy[:], in_=self.gy[:], func=AF.Sigmoid,
                                                   scale=1.5957691216057308), [self.gy], [self.gy])
                P.op("dve", lambda e, jc=jc: e.tensor_tensor(out=self.gT[jc][:], in0=self.gx[:], in1=self.gy[:],
                                                             op=ALU.mult), [self.gx, self.gy], [self.gT[jc]])
            if kvi == 0:
                for jc in range(2):
                    P.op("pe", lambda e, jc=jc: e.matmul(self.X1[0:64, :], lhsT=self.w2b[:, jc, :], rhs=self.gT[jc][:],
                                                         start=(jc == 0), stop=(jc == 1)),
                         [self.w2b, self.gT[jc]], [self.X1])
                s = self.rot("st", st)
                P.op("act", lambda e, s=s: e.activation(out=s[0:64, 0:512], in_=self.X1[0:64, :], func=AF.Copy),
                     [self.X1], [s])
                self.norm64(s, s[0:64, 0:512], 512, self.hgs[0:64, 1:2], self.kcT, self.kcT[:, :])
            else:
                for cc in range(4):
                    for jc in range(2):
                        P.op("pe", lambda e, jc=jc, cc=cc: e.matmul(
                            self.X1[:, cc * 64:(cc + 1) * 64], lhsT=self.gT[jc][:, cc * 128:(cc + 1) * 128],
                            rhs=self.w2b[:, jc, :], start=(jc == 0), stop=(jc == 1)),
                            [self.w2b, self.gT[jc]], [self.X1])
                    P.op("act", lambda e, cc=cc: e.activation(out=self.vcaug[:, cc, 0:64],
                                                              in_=self.X1[:, cc * 64:(cc + 1) * 64], func=AF.Copy),
                         [self.X1], [self.vcaug])

    def combine(self, qt, h, br, first):
        P = self.P
        A = self.A[h]
        s0 = qt * QT
        gt = self.load_gate(qt, h, br)
        P.op("dve", lambda e: e.tensor_scalar(out=self.rdt[:], in0=A[64:128, :], scalar1=1e-30, scalar2=None,
                                              op0=ALU.max), [A], [self.rdt])
        P.op("dve", lambda e: e.reciprocal(out=self.rdt[:], in_=self.rdt[:]), [self.rdt], [self.rdt])
        P.op("pool", lambda e: e.tensor_tensor(out=self.wt[:], in0=self.rdt[:], in1=gt[:], op=ALU.mult),
             [self.rdt, gt], [self.wt])
        if first:
            P.op("dve", lambda e: e.tensor_tensor(out=self.y[:, h, :], in0=A[0:64, :], in1=self.wt[:], op=ALU.mult),
                 [A, self.wt], [self.y])
        else:
            P.op("dve", lambda e: e.tensor_tensor(out=self.tm[:], in0=A[0:64, :], in1=self.wt[:], op=ALU.mult),
                 [A, self.wt], [self.tm])
            P.op("pool", lambda e: e.tensor_tensor(out=self.y[:, h, :], in0=self.y[:, h, :], in1=self.tm[:],
                                                   op=ALU.add), [self.y, self.tm], [self.y])

    def exp(self, S, pT, bias):
        P = self.P
        if bias is None:
            P.op("act", lambda e: e.activation(out=pT[:], in_=S[:, :], func=AF.Exp, scale=0.125), [S], [pT])
        else:
            bap, bt = bias
            P.op("act", lambda e: e.activation(out=pT[:], in_=S[:, :], func=AF.Exp, scale=0.125, bias=bap),
                 [S, bt], [pT])

    def cmp_bias(self, cc, h, mixed):
        return None if mixed else (self.cf[:, h:h + 1], self.cf)

    def sel_bias(self, kc, h, near):
        return None if near else (self.cf[:, h:h + 1], self.cf)

    def win_bias(self, c):
        return None

    def topc_ap(self, qt):
        return self.topc[qt]

    def load_q(self, qt):
        s0 = qt * QT
        self.P.dma("sp", self.qraw, self.qraw[:], self.qT, self.qT[:, s0:s0 + QT].rearrange("(h d) t -> d h t", d=64))

    def load_gate(self, qt, h, br):
        s0 = qt * QT
        gt = self.rot("gt", self.gt)
        g = self.gB[:]
        self.P.dma("sp", gt, gt[:], self.gB,
                   bass.AP(tensor=g.tensor, offset=(h * 3 + br) * SEQ + s0, ap=[[0, 64], [1, 512]]))
        return gt

    def store_y(self, qt):
        s0 = qt * QT
        self.P.dma("sp", self.yT, self.yT[:, s0:s0 + QT].rearrange("(h d) t -> d h t", d=64), self.y, self.y[:],
                   nodep_out=True)

    def qtile(self, qt):
        P = self.P
        s0 = qt * QT
        self.load_q(qt)
        P.dma("sp", self.tcs, self.tcs[:], self.topc, self.topc_ap(qt))
        for h in range(6):
            self.norm64(self.qraw, self.qraw[:, h, :], 512, self.hgs[0:64, 0:1], self.qn, self.qn[:, h, :])
        for h in range(3):
            for half in range(2):
                P.op("pool", lambda e, h=h, half=half: e.tensor_copy(out=self.qaug[0:64, h, half, :],
                                                                     in_=self.qn[:, h, :]), [self.qn], [self.qaug])
        for h in range(6):
            chunks = [cc for cc in range(4) if qt - 4 * cc >= 0]
            pts = []
            for cc in chunks:
                k = qt - 4 * cc
                mixed = k <= 5
                S = self.rot("S", self.S)
                P.op("pe", lambda e, S=S, cc=cc, h=h, mixed=mixed: e.matmul(
                    S[:, :], lhsT=self.kcT[:, cc * 128:(cc + 1) * 128], rhs=self.qn[:, h, :], start=True,
                    stop=(not mixed)), [self.kcT, self.qn], [S])
                if mixed:
                    mo = 512 * (5 - k)
                    P.op("pe", lambda e, S=S, h=h, mo=mo: e.matmul(S[:, :], lhsT=self.ident[:],
                                                                   rhs=self.uslice(h, mo), start=False, stop=True),
                         [self.ident, self.Ubig], [S])
                pT = self.rot("pT", self.pT)
                self.exp(S, pT, self.cmp_bias(cc, h, mixed))
                pts.append(pT)
            n = len(chunks)
            if h < 3:
                for i, (cc, pT) in enumerate(zip(chunks, pts)):
                    P.op("pe", lambda e, pT=pT, i=i, cc=cc, h=h: e.matmul(
                        self.A[h][:, :], lhsT=self.vcaug[:, cc, :], rhs=pT[:], start=(i == 0), stop=(i == n - 1)),
                        [self.vcaug, pT], [self.A[h]])
            for ss in range(4):
                for i, (cc, pT) in enumerate(zip(chunks, pts)):
                    P.op("pe", lambda e, pT=pT, i=i, cc=cc, ss=ss: e.matmul(
                        self.X1[:, ss * 128:(ss + 1) * 128], lhsT=pT[:, ss * 128:(ss + 1) * 128], rhs=self.ovl[:, cc, :],
                        start=(i == 0), stop=(i == n - 1)), [pT, self.ovl], [self.X1])
            for ss in range(4):
                for i, pT in enumerate(pts):
                    P.op("pe", lambda e, pT=pT, i=i, ss=ss: e.matmul(
                        self.X0[:, ss:ss + 1], lhsT=pT[:, ss * 128:(ss + 1) * 128], rhs=self.ones[:, 0:1],
                        start=(i == 0), stop=(i == n - 1)), [pT, self.ones], [self.X0])
            P.op("dve", lambda e: e.tensor_scalar(out=self.rdq[:], in0=self.X0[:, 0:4], scalar1=1e-30, scalar2=None,
                                                  op0=ALU.max), [self.X0], [self.rdq])
            P.op("dve", lambda e: e.reciprocal(out=self.rdq[:], in_=self.rdq[:]), [self.rdq], [self.rdq])
            for ss in range(4):
                if h == 0:
                    P.op("dve", lambda e, ss=ss: e.tensor_scalar(
                        out=self.impacc[:, ss * 128:(ss + 1) * 128], in0=self.X1[:, ss * 128:(ss + 1) * 128],
                        scalar1=self.rdq[:, ss:ss + 1], scalar2=None, op0=ALU.mult), [self.X1, self.rdq], [self.impacc])
                else:
                    P.op("dve", lambda e, ss=ss: e.scalar_tensor_tensor(
                        out=self.impacc[:, ss * 128:(ss + 1) * 128], in0=self.X1[:, ss * 128:(ss + 1) * 128],
                        scalar=self.rdq[:, ss:ss + 1], in1=self.impacc[:, ss * 128:(ss + 1) * 128],
                        op0=ALU.mult, op1=ALU.add), [self.X1, self.rdq, self.impacc], [self.impacc])
            if h < 3:
                self.combine(qt, h, 0, True)
        for ss in range(4):
            tA = self.tcs[:, ss * 256: ss * 256 + 128]
            tB = self.tcs[:, ss * 256 + 128: ss * 256 + 256]
            P.op("dve", lambda e, ss=ss, tA=tA: e.tensor_tensor(out=self.sc[:], in0=self.impacc[:, ss * 128:(ss + 1) * 128],
                                                                in1=tA, op=ALU.mult), [self.impacc, self.tcs], [self.sc])
            P.op("dve", lambda e, tB=tB: e.tensor_tensor(out=self.sc[:], in0=self.sc[:], in1=tB, op=ALU.add),
                 [self.sc, self.tcs], [self.sc])
            P.op("dve", lambda e: e.max(out=self.mx[:, 0:8], in_=self.sc[:]), [self.sc], [self.mx])
            P.op("dve", lambda e: e.match_replace(out=self.sc2[:], in_to_replace=self.mx[:, 0:8], in_values=self.sc[:],
                                                  imm_value=-1e30), [self.sc, self.mx], [self.sc2])
            P.op("dve", lambda e: e.max(out=self.mx[:, 8:16], in_=self.sc2[:]), [self.sc2], [self.mx])
            P.op("dve", lambda e: e.tensor_scalar(out=self.mneg[:], in0=self.sc[:], scalar1=self.mx[:, 15:16], scalar2=1.0,
                                                  op0=ALU.is_ge, op1=ALU.subtract), [self.sc, self.mx], [self.mneg])
            P.op("pe", lambda e, ss=ss: e.transpose(self.XT[:, ss * 128:(ss + 1) * 128], self.mneg[:], self.ident[:]),
                 [self.mneg, self.ident], [self.XT])
        for h in range(3):
            for half in range(2):
                eng = "act" if (h + half) % 2 == 0 else "dve"
                if eng == "act":
                    P.op("act", lambda e, h=h, half=half: e.activation(out=self.qaug[64:128, h, half, :],
                                                                       in_=self.XT[half * 64:(half + 1) * 64, :],
                                                                       func=AF.Copy), [self.XT], [self.qaug])
                else:
                    P.op("dve", lambda e, h=h, half=half: e.tensor_copy(out=self.qaug[64:128, h, half, :],
                                                                        in_=self.XT[half * 64:(half + 1) * 64, :]),
                         [self.XT], [self.qaug])
        nkc = 4 * qt + 4

        def sel_qk(kc, h):
            k0 = 128 * kc
            ep = s0 + 511 - k0
            near = ep <= DS
            half = kc // 32
            S = self.rot("S", self.S)
            P.op("pe", lambda e: e.matmul(S[:, :], lhsT=self.ksaug[:, k0:k0 + 128], rhs=self.qaug[:, h, half, :],
                                          start=True, stop=(not near)), [self.ksaug, self.qaug], [S])
            if near:
                mo = DS - ep
                P.op("pe", lambda e: e.matmul(S[:, :], lhsT=self.ident[:], rhs=self.TTs[h][:, mo:mo + 512],
                                              start=False, stop=True), [self.ident, self.TTs[h]], [S])
            return (kc, h, near, S)

        def sel_rest(kc, h, near, S):
            pT = self.rot("pT", self.pT)
            self.exp(S, pT, self.sel_bias(kc, h, near))
            P.op("pe", lambda e: e.matmul(self.A[h][:, :], lhsT=self.vsaug[:, kc, :], rhs=pT[:],
                                          start=(kc == 0), stop=(kc == nkc - 1)), [self.vsaug, pT], [self.A[h]])

        prev = None
        for kc in range(nkc):
            for h in range(3):
                cur = sel_qk(kc, h)
                if prev is not None:
                    sel_rest(*prev)
                prev = cur
        sel_rest(*prev)
        for h in range(3):
            self.combine(qt, h, 1, False)
        rs = [r for r in range(8) if s0 - 512 + 128 * r >= 0]

        def win_qk(i, r, h):
            k0 = s0 - 512 + 128 * r
            mo = 128 * r
            S = self.rot("S", self.S)
            P.op("pe", lambda e: e.matmul(S[:, :], lhsT=self.kwT[:, k0:k0 + 128], rhs=self.qn[:, h, :],
                                          start=True, stop=False), [self.kwT, self.qn], [S])
            P.op("pe", lambda e: e.matmul(S[:, :], lhsT=self.ident[:], rhs=self.TTw[h][:, mo:mo + 512],
                                          start=False, stop=True), [self.ident, self.TTw[h]], [S])
            return (i, k0, h, S)

        def win_rest(i, k0, h, S):
            pT = self.rot("pT", self.pT)
            self.exp(S, pT, self.win_bias(k0 // 128))
            P.op("pe", lambda e: e.matmul(self.A[h][:, :], lhsT=self.vwaug[:, k0 // 128, :], rhs=pT[:],
                                          start=(i == 0), stop=(i == len(rs) - 1)), [self.vwaug, pT], [self.A[h]])

        prev = None
        for i, r in enumerate(rs):
            for h in range(3):
                cur = win_qk(i, r, h)
                if prev is not None:
                    win_rest(*prev)
                prev = cur
        win_rest(*prev)
        for h in range(3):
            self.combine(qt, h, 2, False)
        self.store_y(qt)

    def build(self, nqt=NQT):
        self.setup()
        for qt in range(nqt):
            self.qtile(qt)
        self.P.wait_all("sp", [self.yT])
        self.P.finalize()


def build_attn(nqt=NQT):
    nc = bass.Bass("TRN2", target_bir_lowering=False)
    P = Prog(nc)
    a = Attn(P)
    a.build(nqt)
    return nc, P


XPAD = SEQ + HALO


def rev_ap(ap, n):
    dims = [list(d) for d in ap.ap]
    dims[-1] = [-1, n]
    return bass.AP(tensor=ap.tensor, offset=ap.offset + (n - 1), ap=dims)


class FTok(TokStage):
    def __init__(self, P, mode, dr):
        self.P = P
        self.mode = mode
        self.dr = dr
        for k in ("memT", "gains", "hg", "mem_w_kv", "w_out", "w_up", "w_down", "b_w_in", "a_w_in", "convw", "w_kv"):
            setattr(self, k, dr[k])
        self.xsrc = dr["xin"] if mode == "L1" else dr["X"]
        self.kvT_o, self.qT_o, self.gT_o = dr["KV"], dr["Q"], dr["G"]
        self.ytokT = dr["Y"]
        self.chunk = 0
        self.alloc()

    def halo_fix(self, sub):
        if self.chunk == 0:
            self.P.op("pool", lambda e: e.memset(self.v[:, sub, 2:2 + HALO], 0.0), [], [self.v])

    def store_tile(self, dstT, r, msz, c0, t0, w, ost):
        a = c0 + t0
        lo = HALO if a == 0 else 0
        g0 = NTOK * self.chunk + a + lo - HALO
        self.P.dma("sp", dstT, dstT[r:r + msz, g0:g0 + (w - lo)], ost, ost[0:msz, lo:w], nodep_out=True)

    def ytok_src(self, ch, c0, t0, w):
        g0 = NTOK * self.chunk + c0 + t0
        return self.ytokT[ch * 128:(ch + 1) * 128, g0:g0 + w]

    def setup_consts(self):
        P = self.P
        P.dma("sp", self.g, self.g[:], self.gains, self.gains[:])
        P.dma("sp", self.hgs, self.hgs[:], self.hg, self.hg[:])
        if self.mode == "L1":
            P.dma("sp", self.cw, self.cw[:], self.convw, self.convw[:])
        for c in range(8):
            P.dma("sp", self.u, self.memx(c, 0, 256), self.memT, self.memT[c * 128:(c + 1) * 128, :], nodep_out=True)
        P.op("pool", lambda e: e.memset(self.ones[:], 1.0), [], [self.ones])
        P.op("pool", lambda e: e.memset(self.bd[:], 0.0), [], [self.bd])
        P.op("pool", lambda e: e.memset(self.bd[0:64, 0:64], 1.0), [], [self.bd])
        P.op("pool", lambda e: e.memset(self.bd[64:128, 64:128], 1.0), [], [self.bd])
        P.op("pool", lambda e: e.memset(self.epsT[:], EPS), [], [self.epsT])
        P.op("pool", lambda e: e.memset(self.vaug[:], 1.0), [], [self.vaug])
        self.norm(self.u, self.memx, 256, [(0, 256)], 15, self.memh, 0)

    def load_x(self):
        P = self.P
        g0 = NTOK * self.chunk
        for c in range(8):
            P.dma("sp", self.x, self.x[:, c, :], self.xsrc, self.xsrc[c * 128:(c + 1) * 128, g0:g0 + NCOL],
                  nodep_out=(c > 0))

    def store_x(self):
        P = self.P
        dst = self.dr["out"] if self.mode == "L5" else self.dr["X"]
        off = 0 if self.mode == "L5" else HALO
        g0 = NTOK * self.chunk + off
        for c in range(8):
            P.dma("sp", dst, dst[c * 128:(c + 1) * 128, g0:g0 + NTOK], self.x, self.x[:, c, HALO:NCOL], nodep_out=True)

    def run(self):
        self.setup_consts()
        for chunk in range(4):
            self.chunk = chunk
            self.load_x()
            if self.mode == "L1":
                for l in range(2):
                    self.mem_kv(l)
                    for ri, (c0, n) in enumerate(RANGES):
                        tiles = tiles_of(n)
                        self.norm(self.x, self.xs(c0), n, tiles, l, self.h, 0)
                        self.mix_a(l, ri, c0, n, tiles)
                        self.mem_attn(l, n, tiles)
                        self.out_proj(l, c0, n, tiles)
                        self.mlp(l, c0, n, tiles)
                for ri, (c0, n) in enumerate(RANGES):
                    tiles = tiles_of(n)
                    self.norm(self.x, self.xs(c0), n, tiles, 8, self.h, 0)
                    for i in range(3):
                        self.store_proj(self.w_kv, self.w_kv[:, 256 * i:256 * (i + 1)], 256, self.kvT_o, 256 * i, c0, tiles)
                    self.pre_b(0, c0, n, tiles)
            else:
                j = 0 if self.mode == "L3" else 1
                self.mem_kv(2 + j)
                for ri, (c0, n) in enumerate(RANGES):
                    self.post_b(j, c0, n, tiles_of(n))
                if j == 0:
                    for ri, (c0, n) in enumerate(RANGES):
                        self.pre_b(1, c0, n, tiles_of(n))
            self.store_x()


class FAttn(Attn):
    def __init__(self, P, dr, jl):
        self.P = P
        self.dr = dr
        self.jl = jl
        self.Q, self.G, self.KV, self.Y = dr["Q"], dr["G"], dr["KV"], dr["Y"]
        self.hg = dr["ahg%d" % jl]
        self.pos, self.w1, self.w2 = dr["pos"], dr["cmp_w1"], dr["cmp_w2"]
        self.rel = dr["rel_bias"]
        self.OH, self.identD, self.EE, self.ovlD, self.topc = dr["OH"], dr["ident"], dr["EE"], dr["ovl"], dr["topc"]
        self.Rrow = dr["Rrow"]
        self.alloc()

    def alloc(self):
        Attn.alloc(self)
        self.ub_view = self.Ubig[0:64, 0:16 * 513].rearrange("p (r n) -> p r n", n=513)

    def set_combo(self, g, tr):
        self.g, self.tr = g, tr
        own = [3 * tr + i for i in range(3)]
        oth = [3 * (1 - tr) + i for i in range(3)]
        self.heads = own + oth

    def load_rb(self):
        P = self.P
        rel = self.rel[:]
        for i, hh in enumerate(self.heads):
            col = self.g * 6 + hh
            P.dma("sp", self.cf, self.cf[:, i:i + 1], self.rel,
                  bass.AP(tensor=rel.tensor, offset=31 * 12 + col, ap=[[0, 128], [1, 1]]))
            P.dma("sp", self.tab, self.tab[0:32, i:i + 1], self.rel, self.rel[:, col:col + 1],
                  allow_slow_non_contiguous=True)

    def load_kv_piece(self, s, slot, c0):
        r0 = slot * 128 + self.g * 64
        self.P.dma("sp", s, s[0:64, :], self.KV, self.KV[r0:r0 + 64, c0:c0 + 1024])

    def load_v_piece(self, piece):
        P = self.P
        c0 = piece * 1024
        for slot, dst in ((3, self.vsaug), (5, self.vwaug)):
            s = self.rot("st", self.st)
            self.load_kv_piece(s, slot, c0)
            for hf in range(2):
                sq = self.rot("sq", self.sq)
                P.op("pool", lambda e, s=s, sq=sq, hf=hf: e.tensor_copy(out=sq[0:64, :], in_=s[0:64, hf * 512:(hf + 1) * 512]),
                     [s], [sq])
                for c in range(4):
                    P.op("pe", lambda e, sq=sq, c=c: e.transpose(self.XT[:, c * 64:(c + 1) * 64],
                                                                 sq[0:64, c * 128:(c + 1) * 128], self.ident[0:64, 0:64]),
                         [sq, self.ident], [self.XT])
                ch0 = piece * 8 + hf * 4
                P.op("act", lambda e, dst=dst, ch0=ch0: e.activation(
                    out=dst[:, ch0:ch0 + 4, 0:64], in_=self.XT[:, 0:256].rearrange("p (c d) -> p c d", d=64),
                    func=AF.Copy), [self.XT], [dst])

    def compress_begin(self, kvi):
        P = self.P
        P.op("pool", lambda e: e.memset(self.ub_view[:, :, 512:513], 0.0), [], [self.Ubig])
        for piece in range(8):
            s = self.rot("st", self.st)
            self.load_kv_piece(s, kvi, piece * 1024)
            P.op("pool", lambda e, s=s, piece=piece: e.tensor_copy(
                out=self.ub_view[:, :, piece * 64:(piece + 1) * 64],
                in_=s[0:64, :].rearrange("p (n r) -> p r n", r=16)), [s], [self.Ubig])

    def compress_rhs(self, kvi, tg):
        P = self.P
        for tl in range(4):
            t = tg * 4 + tl
            r, off = t % 16, t // 16
            P.op("dve", lambda e, tl=tl, t=t, r=r, off=off: e.tensor_scalar(
                out=self.blkb[:, tl, :], in0=self.ub_view[:, r, off:off + 512], scalar1=self.posT[:, t:t + 1],
                scalar2=None, op0=ALU.add), [self.Ubig, self.posT], [self.blkb])

    def load_q(self, qt):
        P = self.P
        s0 = qt * QT
        for i2 in range(3):
            s = self.rot("st", self.st)
            for k in range(2):
                hh = self.heads[2 * i2 + k]
                r0 = (self.g * 6 + hh) * 64
                P.dma("sp", s, s[0:64, k * 512:(k + 1) * 512], self.Q, self.Q[r0:r0 + 64, s0:s0 + QT], nodep_out=(k > 0))
            for k in range(2):
                P.op("dve", lambda e, s=s, k=k, i2=i2: e.tensor_copy(
                    out=self.qraw[:, 2 * i2 + k, :], in_=rev_ap(s[0:64, k * 512:(k + 1) * 512], 512)), [s], [self.qraw])

    def load_gate(self, qt, h, br):
        P = self.P
        s0 = qt * QT
        gt = self.rot("gt", self.gt)
        g = self.G[:]
        row = (self.g * 6 + 3 * self.tr + h) * 3 + br
        P.dma("sp", gt, gt[:], self.G, bass.AP(tensor=g.tensor, offset=row * SEQ + s0, ap=[[0, 64], [1, 512]]))
        gr = self.rot("gtr", self.gtr)
        P.op("dve", lambda e: e.tensor_copy(out=gr[:], in_=rev_ap(gt[:], 512)), [gt], [gr])
        return gr

    def store_y(self, qt):
        P = self.P
        s0 = qt * QT
        bufs = [self.rdt, self.wt, self.tm]
        for h in range(3):
            b = bufs[h]
            P.op("dve", lambda e, b=b, h=h: e.tensor_copy(out=b[:], in_=rev_ap(self.y[:, h, :], 512)), [self.y], [b])
            r0 = (self.g * 6 + 3 * self.tr + h) * 64
            P.dma("sp", self.Y, self.Y[r0:r0 + 64, HALO + s0:HALO + s0 + QT], b, b[:], nodep_out=True)

    def run(self):
        P = self.P
        st = self.st
        self.gtr = [P.sb("gtr%d" % i, [64, 512], F32) for i in range(2)]
        P.dma("sp", self.hgs, self.hgs[:], self.hg, self.hg[:])
        P.op("pool", lambda e: e.memset(self.ones[:], 1.0), [], [self.ones])
        P.op("pool", lambda e: e.memset(self.epsT[:], 1e-6), [], [self.epsT])
        P.op("pool", lambda e: e.memset(self.vsaug[:], 1.0), [], [self.vsaug])
        P.op("pool", lambda e: e.memset(self.vwaug[:], 1.0), [], [self.vwaug])
        P.op("pool", lambda e: e.memset(self.vcaug[:], 1.0), [], [self.vcaug])
        s = self.rot("st", st)
        P.dma("sp", s, s[:, 0:128], self.identD, self.identD[:])
        P.op("pool", lambda e, s=s: e.tensor_copy(out=self.ident[:], in_=s[:, 0:128]), [s], [self.ident])
        s = self.rot("st", st)
        P.dma("sp", s, s[:, 0:512], self.ovlD, self.ovlD[:].rearrange("p c j -> p (c j)"))
        P.op("pool", lambda e, s=s: e.tensor_copy(out=self.ovl[:].rearrange("p c j -> p (c j)"), in_=s[:, 0:512]),
             [s], [self.ovl])
        for g in range(2):
            self.set_combo(g, 0)
            P.barrier()
            self.setup_kv()
            for tr in range(2):
                self.set_combo(g, tr)
                P.barrier()
                self.setup_tables()
                for qt in range(NQT):
                    self.qtile(qt)


def fused_dram(P, r3=False):
    dr = {}
    D = lambda n, s: P.dram(n, s, F32, kind="ExternalInput")
    dr["xin"] = D("xin", [1024, XPAD])
    dr["memT"] = D("memT", [1024, 256])
    dr["gains"] = D("gains", [128, 128])
    dr["hg"] = D("hg", [128, 16])
    dr["convw"] = D("convw", [128, 36])
    dr["mem_w_kv"] = D("mem_w_kv", [4, 1024, 512])
    dr["w_out"] = D("w_out", [4, 1024, 1024])
    dr["w_up"] = D("w_up", [4, 1024, 4096])
    dr["w_down"] = D("w_down", [4, 4096, 1024])
    dr["b_w_in"] = D("b_w_in", [2, 1024, 1060])
    dr["a_w_in"] = D("a_w_in", [2, 1024, 2560])
    dr["w_kv"] = D("w_kv", [1024, 768])
    dr["ahg0"] = D("ahg0", [128, 8])
    dr["ahg1"] = D("ahg1", [128, 8])
    dr["pos"] = D("pos", [2, 64, 32])
    dr["cmp_w1"] = D("cmp_w1", [2, 2048, 256])
    dr["cmp_w2"] = D("cmp_w2", [2, 256, 64])
    dr["rel_bias"] = D("rel_bias", [32, 12])
    dr["OH"] = D("OH", [33, LTOT])
    dr["ident"] = D("ident", [128, 128])
    dr["EE"] = D("EE", [64, SEQ])
    dr["ovl"] = D("ovl", [128, 4, 128])
    if not r3:
        dr["topc"] = D("topc", [NQT, 128, 1024])
    if not r3:
        dr["out"] = P.dram("xT_out", [1024, SEQ], F32, kind="ExternalOutput")
    I = lambda n, s: P.dram(n, s, F32, kind="Internal")
    dr["X"] = I("Xs", [1024, XPAD])
    if not r3:
        dr["KV"] = I("KVs", [768, SEQ])
    dr["Q"] = I("Qs", [768, SEQ])
    dr["G"] = I("Gs", [36, SEQ])
    dr["Y"] = I("Ys", [768, XPAD])
    dr["Rrow"] = I("Rrow", [6, LTOT])
    return dr


def build_fused():
    nc = bass.Bass("TRN2", target_bir_lowering=False)
    P = Prog(nc)
    dr = fused_dram(P)
    P.push_scope()
    FTok(P, "L1", dr).run()
    P.pop_scope()
    for j in range(2):
        P.push_scope()
        FAttn(P, dr, j).run()
        P.pop_scope()
        P.push_scope()
        FTok(P, "L3" if j == 0 else "L5", dr).run()
        P.pop_scope()
    P.wait_all("sp", [dr["out"]])
    P.finalize()
    return nc, P


KPAD = 1536
QTILES = (3, 7, 11, 15)


def host_core_tables(k):
    tc = np.zeros((4, 128, 4, 2, 128), np.float32)
    jj = np.arange(128)[None, :]
    for j, qt in enumerate(QTILES):
        for ss in range(4):
            ta = qt * QT + 511 - (128 * ss + np.arange(128))
            back = (ta // 64)[:, None] - jj
            J = jj - 24 + 8 * k
            valid = (J >= 0) & (back >= 0)
            forced = (J == 0) | ((back >= 0) & (back < 2))
            tc[j, :, ss, 0] = (valid & ~forced).astype(np.float32)
            tc[j, :, ss, 1] = np.where(valid, np.where(forced, 1e4, 0.0), -1e30).astype(np.float32)
    kval = np.zeros((128, 16), np.float32)
    first = KPAD - 512 * k
    for c in range(12):
        a = 128 * c + np.arange(128)
        kval[:, c] = np.where(a >= first, 0.0, MASKV)
    kval[:, 12] = np.where(16 * np.arange(128) >= first, 0.0, MASKV)
    return tc.reshape(4, 128, 1024), kval


class FTok2(FTok):
    def __init__(self, P, mode, dr):
        FTok.__init__(self, P, mode, dr)
        if mode != "L1":
            self.xsrc = dr["Xown"]
            self.ytokT = dr["Yown"]
            self.qT_o, self.gT_o = dr["Qown"], dr["Gown"]

    def store_tile(self, dstT, r, msz, c0, t0, w, ost):
        if self.mode == "L1":
            off = KPAD if dstT is self.dr["KV"] else 0
            a = c0 + t0
            lo = HALO if a == 0 else 0
            g0 = NTOK * self.chunk + a + lo - HALO + off
            self.P.dma("sp", dstT, dstT[r:r + msz, g0:g0 + (w - lo)], ost, ost[0:msz, lo:w], nodep_out=True)
        else:
            self.P.dma("sp", dstT, dstT[r:r + msz, c0 + t0:c0 + t0 + w], ost, ost[0:msz, 0:w], nodep_out=True)

    def ytok_src(self, ch, c0, t0, w):
        if self.mode == "L1":
            return FTok.ytok_src(self, ch, c0, t0, w)
        return self.ytokT[ch * 128:(ch + 1) * 128, c0 + t0:c0 + t0 + w]

    def load_x(self):
        if self.mode == "L1":
            return FTok.load_x(self)
        P = self.P
        for c in range(8):
            P.dma("sp", self.x, self.x[:, c, 0:NTOK], self.xsrc, self.xsrc[c * 128:(c + 1) * 128, :], nodep_out=(c > 0))

    def store_x(self):
        if self.mode == "L1":
            return FTok.store_x(self)
        P = self.P
        dst = self.dr["out"] if self.mode == "L5" else self.dr["Xown"]
        for c in range(8):
            P.dma("sp", dst, dst[c * 128:(c + 1) * 128, :], self.x, self.x[:, c, 0:NTOK], nodep_out=True)

    def zero_kv_pad(self):
        P = self.P
        z = self.ost[0]
        P.op("pool", lambda e: e.memset(z[:], 0.0), [], [z])
        KV = self.dr["KV"]
        for r in range(6):
            for cb in range(3):
                P.dma("sp", KV, KV[r * 128:(r + 1) * 128, cb * 512:(cb + 1) * 512], z, z[:], nodep_out=True)

    def run(self):
        if self.mode == "L1":
            self.zero_kv_pad()
            return FTok.run(self)
        self.setup_consts()
        self.chunk = 0
        self.load_x()
        j = 0 if self.mode == "L3" else 1
        self.mem_kv(2 + j)
        R2 = [(0, 1024), (1024, 1024)]
        for (c0, n) in R2:
            self.post_b(j, c0, n, tiles_of(n))
        if j == 0:
            for (c0, n) in R2:
                self.pre_b(1, c0, n, tiles_of(n))
        self.store_x()


def extract_own(P, dr):
    kd = dr["kd"]

    def dyn3(src_t, r0, r1, pad, ncols):
        base = src_t[r0:r1, pad:pad + KPAD + 6144 + 512][:, bass.ds(kd, 6144 + 512)]
        return bass.AP(tensor=base.tensor, offset=base.offset, ap=[[ncols, r1 - r0], [2048, 4], [1, 512]])
    KVp, KVw = dr["KV"], dr["KVwin"]
    for rb in range(3):
        P.dma("sp", KVw, KVw[rb * 256:(rb + 1) * 256, :], KVp,
              KVp[rb * 256:(rb + 1) * 256, 0:KPAD + SEQ][:, bass.ds(kd, SEQ)], nodep_out=True)
    for rb in range(4):
        P.dma("sp", dr["Xown"], dr["Xown"][rb * 256:(rb + 1) * 256, :].rearrange("r (s c) -> r s c", c=512), dr["X"],
              dyn3(dr["X"], rb * 256, (rb + 1) * 256, HALO, XPAD), nodep_out=True)
    for rb in range(3):
        P.dma("sp", dr["Qown"], dr["Qown"][rb * 256:(rb + 1) * 256, :].rearrange("r (s c) -> r s c", c=512), dr["Q"],
              dyn3(dr["Q"], rb * 256, (rb + 1) * 256, 0, SEQ), nodep_out=True)
    P.dma("sp", dr["Gown"], dr["Gown"][:, :].rearrange("r (s c) -> r s c", c=512), dr["G"],
          dyn3(dr["G"], 0, 36, 0, SEQ), nodep_out=True)


class FAttn2(FAttn):
    def __init__(self, P, dr, jl):
        FAttn.__init__(self, P, dr, jl)
        self.KV = dr["KVwin"]
        self.Q, self.G, self.Y = dr["Qown"], dr["Gown"], dr["Yown"]
        self.kvalD = dr["kval"]

    def load_q(self, qt):
        P = self.P
        c0 = ((qt - 3) // 4) * 512
        for i2 in range(3):
            s = self.rot("st", self.st)
            for k in range(2):
                hh = self.heads[2 * i2 + k]
                r0 = (self.g * 6 + hh) * 64
                P.dma("sp", s, s[0:64, k * 512:(k + 1) * 512], self.Q, self.Q[r0:r0 + 64, c0:c0 + 512], nodep_out=(k > 0))
            for k in range(2):
                P.op("dve", lambda e, s=s, k=k, i2=i2: e.tensor_copy(
                    out=self.qraw[:, 2 * i2 + k, :], in_=rev_ap(s[0:64, k * 512:(k + 1) * 512], 512)), [s], [self.qraw])

    def load_gate(self, qt, h, br):
        P = self.P
        c0 = ((qt - 3) // 4) * 512
        gt = self.rot("gt", self.gt)
        g = self.G[:]
        row = (self.g * 6 + 3 * self.tr + h) * 3 + br
        P.dma("sp", gt, gt[:], self.G, bass.AP(tensor=g.tensor, offset=row * NTOK + c0, ap=[[0, 64], [1, 512]]))
        gr = self.rot("gtr", self.gtr)
        P.op("dve", lambda e: e.tensor_copy(out=gr[:], in_=rev_ap(gt[:], 512)), [gt], [gr])
        return gr

    def store_y(self, qt):
        P = self.P
        c0 = ((qt - 3) // 4) * 512
        bufs = [self.rdt, self.wt, self.tm]
        for h in range(3):
            b = bufs[h]
            P.op("dve", lambda e, b=b, h=h: e.tensor_copy(out=b[:], in_=rev_ap(self.y[:, h, :], 512)), [self.y], [b])
            r0 = (self.g * 6 + 3 * self.tr + h) * 64
            P.dma("sp", self.Y, self.Y[r0:r0 + 64, c0:c0 + 512], b, b[:], nodep_out=True)

    def topc_ap(self, qt):
        return self.topc[(qt - 3) // 4]

    def cmp_bias(self, cc, h, mixed):
        if cc == 0:
            return (self.kval[:, 12:13], self.kval) if mixed else (self.cfc[:, h:h + 1], self.cfc)
        return None if mixed else (self.cf[:, h:h + 1], self.cf)

    def sel_bias(self, kc, h, near):
        if kc < 12:
            return (self.kval[:, kc:kc + 1], self.kval) if near else (self.vbf[:, kc, h:h + 1], self.vbf)
        return None if near else (self.cf[:, h:h + 1], self.cf)

    def win_bias(self, c):
        return (self.kval[:, c:c + 1], self.kval) if c < 12 else None

    def setup_tables(self):
        FAttn.setup_tables(self)
        P = self.P
        for c in range(12):
            P.op("dve", lambda e, c=c: e.tensor_scalar(out=self.vbf[:, c, :], in0=self.cf[:, 0:3],
                                                       scalar1=self.kval[:, c:c + 1], scalar2=None, op0=ALU.add),
                 [self.cf, self.kval], [self.vbf])
        P.op("dve", lambda e: e.tensor_scalar(out=self.cfc[:], in0=self.cf[:], scalar1=self.kval[:, 12:13], scalar2=None,
                                              op0=ALU.add), [self.cf, self.kval], [self.cfc])

    def run(self):
        P = self.P
        st = self.st
        self.gtr = [P.sb("gtr%d" % i, [64, 512], F32) for i in range(2)]
        self.kval = P.sb("kval", [128, 16], F32)
        self.vbf = P.sb("vbf", [128, 12, 3], F32)
        self.cfc = P.sb("cfc", [128, 6], F32)
        P.dma("sp", self.kval, self.kval[:], self.kvalD, self.kvalD[:])
        P.dma("sp", self.hgs, self.hgs[:], self.hg, self.hg[:])
        P.op("pool", lambda e: e.memset(self.ones[:], 1.0), [], [self.ones])
        P.op("pool", lambda e: e.memset(self.epsT[:], 1e-6), [], [self.epsT])
        P.op("pool", lambda e: e.memset(self.vsaug[:], 1.0), [], [self.vsaug])
        P.op("pool", lambda e: e.memset(self.vwaug[:], 1.0), [], [self.vwaug])
        P.op("pool", lambda e: e.memset(self.vcaug[:], 1.0), [], [self.vcaug])
        s = self.rot("st", st)
        P.dma("sp", s, s[:, 0:128], self.identD, self.identD[:])
        P.op("pool", lambda e, s=s: e.tensor_copy(out=self.ident[:], in_=s[:, 0:128]), [s], [self.ident])
        s = self.rot("st", st)
        P.dma("sp", s, s[:, 0:512], self.ovlD, self.ovlD[:].rearrange("p c j -> p (c j)"))
        P.op("pool", lambda e, s=s: e.tensor_copy(out=self.ovl[:].rearrange("p c j -> p (c j)"), in_=s[:, 0:512]),
             [s], [self.ovl])
        for g in range(2):
            self.set_combo(g, 0)
            self.setup_kv()
            for tr in range(2):
                self.set_combo(g, tr)
                self.setup_tables()
                for qt in QTILES:
                    self.qtile(qt)


def build_fused2():
    nc = bass.Bass("TRN2", target_bir_lowering=False)
    P = Prog(nc)
    dr = fused_dram(P, r3=True)
    I = lambda n, s: P.dram(n, s, F32, kind="Internal")
    dr["KV"] = I("KVp", [768, KPAD + SEQ])
    dr["KVwin"] = I("KVwin", [768, SEQ])
    dr["Xown"] = I("Xown", [1024, NTOK])
    dr["Qown"] = I("Qown", [768, NTOK])
    dr["Gown"] = I("Gown", [36, NTOK])
    dr["Yown"] = I("Yown", [768, NTOK])
    dr["topc"] = P.dram("topc4", [4, 128, 1024], F32, kind="ExternalInput")
    dr["kval"] = P.dram("kval", [128, 16], F32, kind="ExternalInput")
    dr["out"] = P.dram("xT_own", [1024, NTOK], F32, kind="ExternalOutput")
    dr["kd"] = (nc.partition_id() % 4) * 512
    P.push_scope()
    FTok2(P, "L1", dr).run()
    P.pop_scope()
    extract_own(P, dr)
    for j in range(2):
        P.push_scope()
        FAttn2(P, dr, j).run()
        P.pop_scope()
        P.push_scope()
        FTok2(P, "L3" if j == 0 else "L5", dr).run()
        P.pop_scope()
    P.wait_all("sp", [dr["out"]])
    P.finalize()
    return nc, P


from concourse.bass_utils import run_bass_kernel_spmd


def attn_hg(inp, jlayer):
    hg = np.zeros((128, 8), np.float32)
    hg[:, 0] = np.tile(inp["b_q_gain"][jlayer], 2)
    for i in range(3):
        hg[:, 1 + i] = np.tile(inp["k_gain"][i], 2)
    return hg


def kernel(**inputs):
    inp = {k: np.ascontiguousarray(np.asarray(v, dtype=np.float32)) for k, v in inputs.items()}
    common = dict(host_consts())
    del common["topc"]
    common.update(gains=host_gains(inp), hg=host_hg(inp), convw=host_convw(inp), mem_w_kv=inp["mem_w_kv"],
                  w_out=inp["w_out"], w_up=inp["w_up"], w_down=inp["w_down"], b_w_in=inp["b_w_in"],
                  a_w_in=inp["a_w_in"], w_kv=inp["w_kv_shared"], ahg0=attn_hg(inp, 0), ahg1=attn_hg(inp, 1),
                  pos=np.ascontiguousarray(inp["cmp_pos"].transpose(0, 2, 1)), cmp_w1=inp["cmp_w1"],
                  cmp_w2=inp["cmp_w2"], rel_bias=inp["rel_bias"])
    per_b = []
    for b in range(2):
        xin = np.zeros((1024, XPAD), np.float32)
        xin[:, HALO:] = inp["x"][b].T
        per_b.append(dict(xin=xin, memT=np.ascontiguousarray(inp["mem"][b].T)))
    per_k = []
    for k in range(4):
        tc, kval = host_core_tables(k)
        per_k.append(dict(topc4=tc, kval=kval))
    maps = []
    for c in range(8):
        m = dict(common)
        m.update(per_b[c // 4])
        m.update(per_k[c % 4])
        maps.append(m)
    nc = build_fused2()[0]
    r = run_bass_kernel_spmd(nc, maps, core_ids=list(range(8))).results
    out = np.zeros((2, SEQ, 1024), np.float32)
    for c in range(8):
        b, k = c // 4, c % 4
        o = r[c]["xT_own"]
        for j in range(4):
            t0 = 512 * (4 * j + k)
            out[b, t0:t0 + 512, :] = o[:, j * 512:(j + 1) * 512].T
    return out
```

```python
import numpy as np
import concourse.bass as bass
import concourse.mybir as mybir
from contextlib import ExitStack

F32 = mybir.dt.float32
BF16 = mybir.dt.bfloat16
AF = mybir.ActivationFunctionType
ALU = mybir.AluOpType
AX = mybir.AxisListType

ENGS = ("pe", "act", "dve", "pool", "sp")


class T:
    __slots__ = ("h", "name", "lw", "rd", "dsem")

    def __init__(self, h, name):
        self.h = h
        self.name = name
        self.lw = None
        self.rd = []
        self.dsem = None

    def __getitem__(self, k):
        return self.h[k]


class Prog:
    def __init__(self, nc):
        self.nc = nc
        self.es = ExitStack()
        self.ins = {e: [] for e in ENGS}
        self.nsem = 0
        self.dsems = []
        self.free_dsems = []
        self.marked = set()
        self.rw = {e: {} for e in ENGS}
        self.cur = self.es
        self.uid = 0
        self.last_tok = {e: None for e in ENGS}
        self.epoch = 0
        self.ep_of = {}

    def _prune(self, eng, deps):
        out = []
        w = self.rw[eng]
        for d in deps:
            key = (d[0], d[1]); val = d[2]
            if w.get(key, -1) >= val:
                continue
            w[key] = val
            out.append(d)
        return out

    def sb(self, name, shape, dt):
        self.uid += 1
        name = "%s_%d" % (name, self.uid)
        return T(self.cur.enter_context(self.nc.sbuf_tensor(name, list(shape), dt)), name)

    def ps(self, name, shape, dt=F32):
        self.uid += 1
        name = "%s_%d" % (name, self.uid)
        return T(self.cur.enter_context(self.nc.psum_tensor(name, list(shape), dt)), name)

    def push_scope(self):
        self.cur = ExitStack()

    def pop_scope(self):
        self.barrier()
        self.cur.close()
        self.cur = self.es
        self.epoch += 1

    def barrier(self):
        for e in ENGS:
            deps = [t for e2, t in self.last_tok.items() if e2 != e and t is not None]
            deps += [("d", s, c[1]) for s, c in enumerate(self.dsems) if c[1] > 0]
            deps = self._prune(e, deps)
            self.ins[e].append(dict(deps=deps, fn=None, dma=None))

    def dram(self, name, shape, dt, kind="Internal"):
        return T(self.nc.dram_tensor(name, list(shape), dt, kind=kind), name)

    def _new_dsem(self):
        h = self.es.enter_context(self.nc.semaphore("d%d" % len(self.dsems)))
        self.dsems.append([h, 0])
        return len(self.dsems) - 1

    def _deps(self, eng, reads, writes):
        deps = []
        for t in reads:
            if t.lw is not None:
                deps.append(t.lw)
        for t in writes:
            if t.lw is not None:
                deps.append(t.lw)
            deps.extend(t.rd)
        out = []
        for d in deps:
            if d[0] == "e":
                if d[1] == eng:
                    continue
            out.append(d)
        if eng != "pe":
            for t in reads:
                if t.lw is not None and t.lw[0] == "e" and t.lw[1] == eng:
                    out.append(t.lw)
        return out

    def op(self, eng, fn, reads=(), writes=()):
        deps = self._prune(eng, self._deps(eng, reads, writes))
        idx = len(self.ins[eng])
        self.ins[eng].append(dict(deps=deps, fn=fn, dma=None))
        tok = ("e", eng, idx)
        self.last_tok[eng] = tok
        self.ep_of[(eng, idx)] = self.epoch
        for t in reads:
            t.rd = [r for r in t.rd if not (r[0] == "e" and r[1] == eng)]
            t.rd.append(tok)
        for t in writes:
            t.lw = tok
            t.rd = []
        return tok

    def dma(self, eng, out_t, out_ap, in_t, in_ap, nodep_out=False, **kw):
        reads = [in_t] if in_t is not None else []
        writes = [out_t] if out_t is not None else []
        deps = []
        for t in reads:
            if t.lw is not None:
                deps.append(t.lw)
        for t in writes:
            if nodep_out:
                continue
            if t.lw is not None:
                deps.append(t.lw)
            deps.extend(t.rd)
        owner = out_t if (out_t is not None and out_t.dsem is not None) else None
        if owner is None:
            owner = out_t if out_t is not None else in_t
        if owner.dsem is None:
            owner.dsem = self._new_dsem()
        s = owner.dsem
        self.dsems[s][1] += 16
        tok = ("d", s, self.dsems[s][1])
        deps = self._prune(eng, deps)
        self.ins[eng].append(dict(deps=deps, fn=None, dma=(out_ap, in_ap, s, kw)))
        for t in reads:
            t.rd.append(tok)
        for t in writes:
            t.lw = tok
            t.rd = []
        return tok

    def wait_all(self, eng, tiles):
        deps = []
        for t in tiles:
            if t.lw is not None:
                deps.append(t.lw)
            deps.extend(t.rd)
        deps = self._prune(eng, deps)
        self.ins[eng].append(dict(deps=deps, fn=None, dma=None))

    def finalize(self):
        nc = self.nc
        for e in ENGS:
            for rec in self.ins[e]:
                for d in rec["deps"]:
                    if d[0] == "e":
                        self.marked.add((d[1], d[2]))
        count_at = {}
        for e in ENGS:
            c = {}
            for i, rec in enumerate(self.ins[e]):
                if (e, i) in self.marked:
                    ep = self.ep_of[(e, i)]
                    c[ep] = c.get(ep, 0) + 1
                    count_at[(e, i)] = c[ep]
        esem = {(e, ep): self.es.enter_context(nc.semaphore("s_%s_%d" % (e, ep)))
                for e in ENGS for ep in range(self.epoch + 1)}
        engobj = {"pe": "tensor", "act": "scalar", "dve": "vector", "pool": "gpsimd", "sp": "sync"}
        block = self.es.enter_context(nc.Block())
        stats = {}

        def make_body(e):
            def body(eng):
                waited = {}
                nwait = 0
                for i, rec in enumerate(self.ins[e]):
                    need = {}
                    for d in rec["deps"]:
                        if d[0] == "e":
                            key = ("e", (d[1], self.ep_of[(d[1], d[2])])); val = count_at[(d[1], d[2])]
                        else:
                            key = ("d", d[1]); val = d[2]
                        if waited.get(key, 0) >= val:
                            continue
                        if need.get(key, 0) < val:
                            need[key] = val
                    for key, val in need.items():
                        h = esem[key[1]] if key[0] == "e" else self.dsems[key[1]][0]
                        eng.wait_ge(h, val)
                        waited[key] = val
                        nwait += 1
                    if rec.get("cc") is not None:
                        rec["cc"](eng).then_inc(self.dsems[rec["ccsem"]][0], 16)
                    elif rec["dma"] is not None:
                        out_ap, in_ap, s, kw = rec["dma"]
                        try:
                            eng.dma_start(out=out_ap, in_=in_ap, **kw).then_inc(self.dsems[s][0], 16)
                        except Exception:
                            print("DMA FAILED:", e, "out=", out_ap, "in=", in_ap)
                            raise
                    elif rec["fn"] is not None:
                        ins = rec["fn"](eng)
                        if (e, i) in self.marked:
                            ins.then_inc(esem[(e, self.ep_of[(e, i)])], 1)
                stats[e] = (len(self.ins[e]), nwait)
            return body

        for e in ENGS:
            if self.ins[e]:
                getattr(block, engobj[e])(make_body(e))
        self.stats = stats
        self.es.close()


NTOK = 2048
HALO = 4
NCOL = NTOK + HALO
RANGES = [(0, 1028), (1028, 1024)]
EPS = 1e-6


def tiles_of(n):
    out = []
    o = 0
    while o < n:
        w = min(512, n - o)
        out.append((o, w))
        o += w
    return out


class TokStage:
    def __init__(self, P, mode):
        self.P = P
        self.mode = mode
        nc = P.nc
        D = lambda n, s: P.dram(n, s, F32, kind="ExternalInput")
        O = lambda n, s: P.dram(n, s, F32, kind="ExternalOutput")
        self.xT = D("xT", [1024, NCOL])
        self.memT = D("memT", [1024, 256])
        self.gains = D("gains", [128, 8 * 16])
        self.hg = D("hg", [128, 16])
        self.flag = D("flag", [128, 1])
        self.mem_w_kv = D("mem_w_kv", [4, 1024, 512])
        self.w_out = D("w_out", [4, 1024, 1024])
        self.w_up = D("w_up", [4, 1024, 4096])
        self.w_down = D("w_down", [4, 4096, 1024])
        self.b_w_in = D("b_w_in", [2, 1024, 1060])
        if mode == "L1":
            self.a_w_in = D("a_w_in", [2, 1024, 2560])
            self.convw = D("convw", [128, 2 * 6 * 3])
            self.w_kv = D("w_kv", [1024, 768])
            self.kvT_o = O("kvT_o", [768, NCOL])
        else:
            self.ytokT = D("ytokT", [768, NCOL])
        self.xT_o = O("xT_o", [1024, NCOL])
        if mode != "L5":
            self.qT_o = O("qT_o", [768, NCOL])
            self.gT_o = O("gT_o", [36, NCOL])

        self.alloc()

    def alloc(self):
        P = self.P
        sb = P.sb
        self.x = sb("x", [128, 8, NCOL], F32)
        self.h = sb("h", [128, 8, 1028], BF16)
        self.y = sb("y", [128, 8, 1028], BF16)
        self.act = sb("act", [128, 8, 1028], BF16)
        self.u = sb("u", [128, 2, 1028], F32)
        self.v = sb("v", [128, 2, 1030], F32)
        self.qm = sb("qm", [128, 2, 1028], F32)
        self.carry = sb("carry", [128, 6, 2], F32)
        self.rstd = sb("rstd", [128, 512], F32)
        self.sq = [sb("sq%d" % i, [128, 512], BF16) for i in range(2)]
        self.tmp = [sb("tmp%d" % i, [128, 512], F32) for i in range(3)]
        self.ost = [sb("ost%d" % i, [128, 512], F32) for i in range(2)]
        self.wb = [sb("wb%d" % i, [128, 8, 256], BF16) for i in range(4)]
        self.memh = sb("memh", [128, 8, 256], BF16)
        self.kraw = sb("kraw", [128, 2, 256], F32)
        self.memk = sb("memk", [128, 2, 256], BF16)
        self.vaug = sb("vaug", [128, 2, 4, 128], BF16)
        self.qn = sb("qn", [128, 512], BF16)
        self.eT = [sb("eT%d" % i, [128, 512], BF16) for i in range(2)]
        self.rd = sb("rd", [64, 512], F32)
        self.g = sb("g", [128, 8 * 16], F32)
        self.hgs = sb("hgs", [128, 16], F32)
        self.flg = sb("flg", [128, 1], F32)
        self.cw = sb("cw", [128, 36], F32)
        self.ones = sb("ones", [128, 128], BF16)
        self.bd = sb("bd", [128, 128], BF16)
        self.epsT = sb("epsT", [128, 1], F32)
        self.pp = [P.ps("pp%d" % i, [128, 512]) for i in range(4)]
        self.pl = [P.ps("pl%d" % i, [128, 512]) for i in range(2)]
        self.po = P.ps("po", [128, 512])
        self.pn = P.ps("pn", [128, 512])
        self.cnt = dict(wf=0, wb=0, pp=0, pl=0, sq=0, tmp=0, ost=0, eT=0)

    def rot(self, name, lst):
        i = self.cnt[name]
        self.cnt[name] += 1
        return lst[i % len(lst)]

    def stream(self, wT, w_ap, ncols):
        P = self.P
        wb = self.rot("wb", self.wb)
        P.dma("pool", wb, wb[:, :, 0:ncols], wT, w_ap.rearrange("(c p) n -> p c n", p=128))
        return wb

    def proj(self, wb, sub, msz, rhs_t, rhs_fn, tiles, consumer):
        P = self.P
        for (t0, w) in tiles:
            ps = self.rot("pp", self.pp)
            for kc in range(8):
                P.op("pe", lambda e, ps=ps, kc=kc, t0=t0, w=w: e.matmul(
                    ps[0:msz, 0:w], lhsT=wb[:, kc, sub * 128: sub * 128 + msz], rhs=rhs_fn(kc, t0, w),
                    start=(kc == 0), stop=(kc == 7)), [wb, rhs_t], [ps])
            consumer(ps, t0, w)

    def setup(self):
        P = self.P
        P.dma("sp", self.g, self.g[:], self.gains, self.gains[:])
        P.dma("sp", self.hgs, self.hgs[:], self.hg, self.hg[:])
        P.dma("sp", self.flg, self.flg[:], self.flag, self.flag[:])
        if self.mode == "L1":
            P.dma("sp", self.cw, self.cw[:], self.convw, self.convw[:])
        for c in range(8):
            P.dma("sp", self.x, self.x[:, c, :], self.xT, self.xT[c * 128:(c + 1) * 128, :], nodep_out=True)
            P.dma("sp", self.u, self.memx(c, 0, 256), self.memT, self.memT[c * 128:(c + 1) * 128, :], nodep_out=True)
        P.op("pool", lambda e: e.memset(self.ones[:], 1.0), [], [self.ones])
        P.op("pool", lambda e: e.memset(self.bd[:], 0.0), [], [self.bd])
        P.op("pool", lambda e: e.memset(self.bd[0:64, 0:64], 1.0), [], [self.bd])
        P.op("pool", lambda e: e.memset(self.bd[64:128, 64:128], 1.0), [], [self.bd])
        P.op("pool", lambda e: e.memset(self.epsT[:], EPS), [], [self.epsT])
        P.op("pool", lambda e: e.memset(self.vaug[:], 1.0), [], [self.vaug])
        P.op("pool", lambda e: e.memset(self.carry[:], 0.0), [], [self.carry])
        self.norm(self.u, self.memx, 256, [(0, 256)], 15, self.memh, 0)

    def gcol(self, which, c):
        return self.g[:, which * 8 + c: which * 8 + c + 1]

    def memx(self, c, t0, w):
        return self.u[:, c // 4, (c % 4) * 256 + t0: (c % 4) * 256 + t0 + w]

    def xs(self, c0):
        return lambda c, t0, w: self.x[:, c, c0 + t0: c0 + t0 + w]

    def norm(self, src, src_fn, n, tiles, which, dst, d0):
        P = self.P
        for (t0, w) in tiles:
            for c in range(8):
                sq = self.rot("sq", self.sq)
                P.op("act", lambda e, sq=sq, c=c, t0=t0, w=w: e.activation(
                    out=sq[:, 0:w], in_=src_fn(c, t0, w), func=AF.Square), [src], [sq])
                P.op("pe", lambda e, sq=sq, c=c, w=w: e.matmul(
                    self.pn[:, 0:w], lhsT=self.ones[:], rhs=sq[:, 0:w], start=(c == 0), stop=(c == 7)),
                    [self.ones, sq], [self.pn])
            P.op("act", lambda e, w=w: e.activation(out=self.rstd[:, 0:w], in_=self.pn[:, 0:w], func=AF.Ln,
                                                    bias=self.epsT[:], scale=1.0 / 1024.0),
                 [self.pn, self.epsT], [self.rstd])
            P.op("act", lambda e, w=w: e.activation(out=self.rstd[:, 0:w], in_=self.rstd[:, 0:w], func=AF.Exp, scale=-0.5),
                 [self.rstd], [self.rstd])
            for c in range(8):
                eng = "dve"
                P.op(eng, lambda e, c=c, t0=t0, w=w: e.scalar_tensor_tensor(
                    out=dst[:, c, d0 + t0: d0 + t0 + w], in0=src_fn(c, t0, w),
                    scalar=self.gcol(which, c), in1=self.rstd[:, 0:w], op0=ALU.mult, op1=ALU.mult),
                    [src, self.g, self.rstd], [dst])

    def headnorm(self, src_t, src_ap_fn, w, gain_ap, dst_t, dst_ap):
        P = self.P
        sq = self.rot("sq", self.sq)
        P.op("act", lambda e, sq=sq: e.activation(out=sq[:, 0:w], in_=src_ap_fn(), func=AF.Square), [src_t], [sq])
        P.op("pe", lambda e, sq=sq: e.matmul(self.pn[:, 0:w], lhsT=self.bd[:], rhs=sq[:, 0:w], start=True, stop=True),
             [self.bd, sq], [self.pn])
        P.op("act", lambda e: e.activation(out=self.rstd[:, 0:w], in_=self.pn[:, 0:w], func=AF.Ln,
                                           bias=self.epsT[:], scale=1.0 / 64.0), [self.pn, self.epsT], [self.rstd])
        P.op("act", lambda e: e.activation(out=self.rstd[:, 0:w], in_=self.rstd[:, 0:w], func=AF.Exp, scale=-0.5),
             [self.rstd], [self.rstd])
        P.op("dve", lambda e: e.scalar_tensor_tensor(out=dst_ap, in0=src_ap_fn(), scalar=gain_ap,
                                                     in1=self.rstd[:, 0:w], op0=ALU.mult, op1=ALU.mult),
             [src_t, self.hgs, self.rstd], [dst_t])

    def mem_kv(self, l):
        P = self.P
        wb = self.stream(self.mem_w_kv, self.mem_w_kv[l, :, 0:256], 256)
        for sub in range(2):
            def cons(ps, t0, w, sub=sub):
                P.op("act", lambda e: e.activation(out=self.kraw[:, sub, 0:256], in_=ps[:, 0:256], func=AF.Copy),
                     [ps], [self.kraw])
            self.proj(wb, sub, 128, self.memh, lambda kc, t0, w: self.memh[:, kc, t0:t0 + w], [(0, 256)], cons)
            self.headnorm(self.kraw, lambda sub=sub: self.kraw[:, sub, 0:256], 256,
                          self.hgs[:, 4 + l: 5 + l], self.memk, self.memk[:, sub, 0:256])
        wb = self.stream(self.mem_w_kv, self.mem_w_kv[l, :, 256:512], 256)
        for mc in range(2):
            ps = self.rot("pp", self.pp)
            for kc in range(8):
                P.op("pe", lambda e, ps=ps, kc=kc, mc=mc: e.matmul(
                    ps[:, 0:256], lhsT=self.memh[:, kc, mc * 128:(mc + 1) * 128], rhs=wb[:, kc, 0:256],
                    start=(kc == 0), stop=(kc == 7)), [self.memh, wb], [ps])
            for hh in range(4):
                P.op("act", lambda e, ps=ps, mc=mc, hh=hh: e.activation(
                    out=self.vaug[:, mc, hh, 0:64], in_=ps[:, hh * 64:(hh + 1) * 64], func=AF.Copy), [ps], [self.vaug])

    def mem_attn(self, l, n, tiles):
        P = self.P
        for (t0, w) in tiles:
            for j in range(2):
                self.headnorm(self.qm, lambda j=j, t0=t0, w=w: self.qm[:, j, t0:t0 + w], w,
                              self.hgs[:, l: l + 1], self.qn, self.qn[:, 0:w])
                for hh in range(2):
                    b0 = 64 * hh
                    hd = 2 * j + hh
                    ets = []
                    for mc in range(2):
                        pl = self.rot("pl", self.pl)
                        P.op("pe", lambda e, pl=pl, mc=mc, b0=b0, j=j, w=w: e.matmul(
                            pl[:, 0:w], lhsT=self.memk[b0:b0 + 64, j, mc * 128:(mc + 1) * 128],
                            rhs=self.qn[b0:b0 + 64, 0:w], start=True, stop=True), [self.memk, self.qn], [pl])
                        eT = self.rot("eT", self.eT)
                        P.op("act", lambda e, pl=pl, eT=eT, w=w: e.activation(
                            out=eT[:, 0:w], in_=pl[:, 0:w], func=AF.Exp, scale=0.125), [pl], [eT])
                        ets.append(eT)
                    for mc in range(2):
                        P.op("pe", lambda e, mc=mc, hd=hd, w=w, eT=ets[mc]: e.matmul(
                            self.po[:, 0:w], lhsT=self.vaug[:, mc, hd, :], rhs=eT[:, 0:w],
                            start=(mc == 0), stop=(mc == 1)), [self.vaug, ets[mc]], [self.po])
                    P.op("dve", lambda e, w=w: e.reciprocal(out=self.rd[0:64, 0:w], in_=self.po[64:128, 0:w]),
                         [self.po], [self.rd])
                    P.op("dve", lambda e, w=w, b0=b0, j=j, t0=t0: e.tensor_tensor(
                        out=self.y[b0:b0 + 64, 6 + j, t0:t0 + w], in0=self.po[0:64, 0:w], in1=self.rd[0:64, 0:w],
                        op=ALU.mult), [self.po, self.rd], [self.y])

    def add_into_x(self, c0):
        P = self.P

        def mk(ch):
            def cons(ps, t0, w):
                P.op("dve", lambda e: e.tensor_tensor(
                    out=self.x[:, ch, c0 + t0: c0 + t0 + w], in0=ps[:, 0:w], in1=self.x[:, ch, c0 + t0: c0 + t0 + w],
                    op=ALU.add), [ps, self.x], [self.x])
            return cons
        return mk

    def out_proj(self, l, c0, n, tiles):
        mk = self.add_into_x(c0)
        for blk in range(4):
            wb = self.stream(self.w_out, self.w_out[l, :, blk * 256:(blk + 1) * 256], 256)
            for sub in range(2):
                self.proj(wb, sub, 128, self.y, lambda kc, t0, w: self.y[:, kc, t0:t0 + w], tiles, mk(blk * 2 + sub))

    def mlp(self, l, c0, n, tiles):
        P = self.P
        self.norm(self.x, self.xs(c0), n, tiles, 4 + l, self.h, 0)
        mk = self.add_into_x(c0)
        for sg in range(4):
            for blk in range(4):
                wb = self.stream(self.w_up, self.w_up[l, :, sg * 1024 + blk * 256: sg * 1024 + (blk + 1) * 256], 256)
                for sub in range(2):
                    def cons(ps, t0, w, fc=blk * 2 + sub):
                        tmp = self.rot("tmp", self.tmp)
                        P.op("act", lambda e: e.activation(out=tmp[:, 0:w], in_=ps[:, 0:w], func=AF.Relu), [ps], [tmp])
                        P.op("dve", lambda e: e.tensor_tensor(out=self.act[:, fc, t0:t0 + w], in0=tmp[:, 0:w],
                                                              in1=tmp[:, 0:w], op=ALU.mult), [tmp], [self.act])
                    self.proj(wb, sub, 128, self.h, lambda kc, t0, w: self.h[:, kc, t0:t0 + w], tiles, cons)
            for blk in range(4):
                wb = self.stream(self.w_down, self.w_down[l, sg * 1024:(sg + 1) * 1024, blk * 256:(blk + 1) * 256], 256)
                for sub in range(2):
                    self.proj(wb, sub, 128, self.act, lambda kc, t0, w: self.act[:, kc, t0:t0 + w], tiles,
                              mk(blk * 2 + sub))

    def mix_a(self, l, ri, c0, n, tiles):
        P = self.P
        W = self.a_w_in
        hrhs = lambda kc, t0, w: self.h[:, kc, t0:t0 + w]
        for i in range(3):
            wb = self.stream(W, W[l, :, 1536 + 256 * i: 1536 + 256 * (i + 1)], 256)
            for sub in range(2):
                def cons(ps, t0, w, sub=sub):
                    P.op("act", lambda e: e.activation(out=self.u[:, sub, t0:t0 + w], in_=ps[:, 0:w], func=AF.Copy),
                         [ps], [self.u])
                self.proj(wb, sub, 128, self.h, hrhs, tiles, cons)
            wb = self.stream(W, W[l, :, 768 + 256 * i: 768 + 256 * (i + 1)], 256)
            for sub in range(2):
                def cons(ps, t0, w, sub=sub):
                    P.op("dve", lambda e: e.tensor_tensor(out=self.v[:, sub, 2 + t0: 2 + t0 + w], in0=ps[:, 0:w],
                                                          in1=self.u[:, sub, t0:t0 + w], op=ALU.mult),
                         [ps, self.u], [self.v])
                self.proj(wb, sub, 128, self.h, hrhs, tiles, cons)
            for sub in range(2):
                ch = 2 * i + sub
                if ri == 0:
                    P.op("pool", lambda e, sub=sub: e.memset(self.v[:, sub, 0:2], 0.0), [], [self.v])
                    self.halo_fix(sub)
                    P.op("dve", lambda e, sub=sub, ch=ch: e.tensor_copy(out=self.carry[:, ch, :],
                                                                        in_=self.v[:, sub, n:n + 2]),
                         [self.v], [self.carry])
                else:
                    P.op("dve", lambda e, sub=sub, ch=ch: e.tensor_copy(out=self.v[:, sub, 0:2],
                                                                        in_=self.carry[:, ch, :]),
                         [self.carry], [self.v])
                cwc = lambda k, ch=ch: self.cw[:, l * 18 + ch * 3 + k: l * 18 + ch * 3 + k + 1]
                P.op("dve", lambda e, sub=sub, cwc=cwc: e.tensor_scalar(
                    out=self.u[:, sub, 0:n], in0=self.v[:, sub, 0:n], scalar1=cwc(0), scalar2=None, op0=ALU.mult),
                    [self.v, self.cw], [self.u])
                for k in (1, 2):
                    P.op("dve", lambda e, sub=sub, cwc=cwc, k=k: e.scalar_tensor_tensor(
                        out=self.u[:, sub, 0:n], in0=self.v[:, sub, k:k + n], scalar=cwc(k), in1=self.u[:, sub, 0:n],
                        op0=ALU.mult, op1=ALU.add), [self.v, self.cw, self.u], [self.u])
            wb = self.stream(W, W[l, :, 256 * i: 256 * (i + 1)], 256)
            for sub in range(2):
                def cons(ps, t0, w, sub=sub, ch=2 * i + sub):
                    P.op("dve", lambda e: e.tensor_tensor(out=self.y[:, ch, t0:t0 + w], in0=ps[:, 0:w],
                                                          in1=self.u[:, sub, t0:t0 + w], op=ALU.mult),
                         [ps, self.u], [self.y])
                self.proj(wb, sub, 128, self.h, hrhs, tiles, cons)
        wb = self.stream(W, W[l, :, 2304:2560], 256)
        self.qmem_proj(wb, tiles)

    def halo_fix(self, sub):
        self.P.op("dve", lambda e: e.tensor_scalar(
            out=self.v[:, sub, 2:2 + HALO], in0=self.v[:, sub, 2:2 + HALO], scalar1=self.flg[:, 0:1],
            scalar2=None, op0=ALU.mult), [self.v, self.flg], [self.v])

    def store_tile(self, dstT, r, msz, c0, t0, w, ost):
        self.P.dma("sp", dstT, dstT[r:r + msz, c0 + t0: c0 + t0 + w], ost, ost[0:msz, 0:w], nodep_out=True)

    def ytok_src(self, ch, c0, t0, w):
        return self.ytokT[ch * 128:(ch + 1) * 128, c0 + t0: c0 + t0 + w]

    def qmem_proj(self, wb, tiles):
        P = self.P
        for sub in range(2):
            def cons(ps, t0, w, sub=sub):
                P.op("act", lambda e: e.activation(out=self.qm[:, sub, t0:t0 + w], in_=ps[:, 0:w], func=AF.Copy),
                     [ps], [self.qm])
            self.proj(wb, sub, 128, self.h, lambda kc, t0, w: self.h[:, kc, t0:t0 + w], tiles, cons)

    def store_proj(self, wT, w_ap, ncols, dstT, row0, c0, tiles, func=AF.Copy):
        P = self.P
        wb = self.stream(wT, w_ap, ncols)
        nsub = (ncols + 127) // 128
        for sub in range(nsub):
            msz = min(128, ncols - sub * 128)

            def cons(ps, t0, w, sub=sub, msz=msz):
                ost = self.rot("ost", self.ost)
                P.op("act", lambda e: e.activation(out=ost[0:msz, 0:w], in_=ps[0:msz, 0:w], func=func), [ps], [ost])
                r = row0 + sub * 128
                self.store_tile(dstT, r, msz, c0, t0, w, ost)
            self.proj(wb, sub, msz, self.h, lambda kc, t0, w: self.h[:, kc, t0:t0 + w], tiles, cons)

    def pre_b(self, j, c0, n, tiles):
        W = self.b_w_in
        self.norm(self.x, self.xs(c0), n, tiles, 2 + j, self.h, 0)
        for i in range(3):
            self.store_proj(W, W[j, :, 256 * i:256 * (i + 1)], 256, self.qT_o, 256 * i, c0, tiles)
        self.store_proj(W, W[j, :, 768:804], 36, self.gT_o, 0, c0, tiles, func=AF.Sigmoid)

    def post_b(self, j, c0, n, tiles):
        P = self.P
        W = self.b_w_in
        l = 2 + j
        self.norm(self.x, self.xs(c0), n, tiles, l, self.h, 0)
        wb = self.stream(W, W[j, :, 804:1060], 256)
        self.qmem_proj(wb, tiles)
        for ch in range(6):
            for (t0, w) in tiles:
                tmp = self.rot("tmp", self.tmp)
                P.dma("sp", tmp, tmp[:, 0:w], self.ytokT, self.ytok_src(ch, c0, t0, w))
                P.op("act", lambda e, tmp=tmp, ch=ch, t0=t0, w=w: e.activation(out=self.y[:, ch, t0:t0 + w],
                                                                               in_=tmp[:, 0:w], func=AF.Copy), [tmp], [self.y])
        self.mem_attn(l, n, tiles)
        self.out_proj(l, c0, n, tiles)
        self.mlp(l, c0, n, tiles)

    def store_x(self):
        P = self.P
        for c in range(8):
            P.dma("sp", self.xT_o, self.xT_o[c * 128:(c + 1) * 128, :], self.x, self.x[:, c, :], nodep_out=True)

    def build(self):
        P = self.P
        self.setup()
        if self.mode == "L1":
            for l in range(2):
                self.mem_kv(l)
                for ri, (c0, n) in enumerate(RANGES):
                    tiles = tiles_of(n)
                    self.norm(self.x, self.xs(c0), n, tiles, l, self.h, 0)
                    self.mix_a(l, ri, c0, n, tiles)
                    self.mem_attn(l, n, tiles)
                    self.out_proj(l, c0, n, tiles)
                    self.mlp(l, c0, n, tiles)
            for ri, (c0, n) in enumerate(RANGES):
                tiles = tiles_of(n)
                self.norm(self.x, self.xs(c0), n, tiles, 8, self.h, 0)
                for i in range(3):
                    self.store_proj(self.w_kv, self.w_kv[:, 256 * i:256 * (i + 1)], 256, self.kvT_o, 256 * i, c0, tiles)
                self.pre_b(0, c0, n, tiles)
            self.store_x()
            outs = [self.kvT_o, self.qT_o, self.gT_o, self.xT_o]
        elif self.mode == "L3":
            self.mem_kv(2)
            for ri, (c0, n) in enumerate(RANGES):
                tiles = tiles_of(n)
                self.post_b(0, c0, n, tiles)
            for ri, (c0, n) in enumerate(RANGES):
                tiles = tiles_of(n)
                self.pre_b(1, c0, n, tiles)
            self.store_x()
            outs = [self.qT_o, self.gT_o, self.xT_o]
        else:
            self.mem_kv(3)
            for ri, (c0, n) in enumerate(RANGES):
                tiles = tiles_of(n)
                self.post_b(1, c0, n, tiles)
            self.store_x()
            outs = [self.xT_o]
        P.wait_all("sp", outs)
        P.finalize()


def build_tok(mode):
    nc = bass.Bass("TRN2", target_bir_lowering=False)
    P = Prog(nc)
    st = TokStage(P, mode)
    st.build()
    return nc, P


def host_gains(inp):
    g = np.zeros((16, 1024), np.float32)
    g[0:4] = inp["mix_norm"]
    g[4:8] = inp["mlp_norm"]
    g[8] = inp["kv_norm"]
    g[15] = inp["mem_norm"]
    return np.ascontiguousarray(g.reshape(16, 8, 128).transpose(2, 0, 1).reshape(128, 128))


def host_hg(inp):
    hg = np.zeros((128, 16), np.float32)
    for l in range(4):
        hg[:, l] = np.tile(inp["mem_q_gain"][l], 2)
        hg[:, 4 + l] = np.tile(inp["mem_k_gain"][l], 2)
    return hg


def host_convw(inp):
    w = inp["a_conv_w"].reshape(2, 3, 6, 128)
    return np.ascontiguousarray(w.transpose(3, 0, 2, 1).reshape(128, 36))

import math

SEQ = 8192
QT = 512
NQT = SEQ // QT
DC, LC = 3040, 5104
DW, LW = 1023, 1536
DS, LS = 1407, 1920
LTOT = LC + LW + LS
LU, LTW, LTS = 3072, 1408, 1792
MASKV = -30000.0


def rel_bucket_np(d):
    n = np.maximum(d, 0)
    nf = np.maximum(n, 1).astype(np.float32)
    large = 16 + (np.log(nf / np.float32(16)) / np.float32(math.log(1024 / 16)) * np.float32(16)).astype(np.int32)
    large = np.minimum(large, 31)
    return np.where(n < 16, n, large)


def host_consts():
    c = {}
    oh = np.zeros((33, LTOT), np.float32)
    x = np.arange(LC); d = DC - x
    b = rel_bucket_np(d)
    oh[b[d >= 0], x[d >= 0]] = 1.0
    oh[32, x[d < 0]] = 1.0
    x = np.arange(LW); d = DW - x; ok = (d >= 0) & (d < 512)
    b = rel_bucket_np(d)
    oh[b[ok], LC + x[ok]] = 1.0
    oh[32, LC + x[~ok]] = 1.0
    x = np.arange(LS); d = DS - x; ok = d >= 0
    b = rel_bucket_np(d)
    oh[b[ok], LC + LW + x[ok]] = 1.0
    oh[32, LC + LW + x[~ok]] = 1.0
    c["OH"] = oh
    c["ident"] = np.eye(128, dtype=np.float32)
    k = np.arange(SEQ)
    r = 2 * ((k // 128) % 32) + ((k % 128) >= 64)
    ee = np.zeros((64, SEQ), np.float32)
    ee[r, k] = 30000.0
    c["EE"] = ee
    i = np.arange(512)[:, None] * 16
    s = np.arange(128)[None, :] * 64
    ov = np.clip(np.minimum(i + 32, s + 64) - np.maximum(i, s), 0, None).astype(np.float32) / 32.0
    ov[511] = 0.0
    c["ovl"] = np.ascontiguousarray(ov.reshape(4, 128, 128).transpose(1, 0, 2))
    tc = np.zeros((NQT, 128, 4, 2, 128), np.float32)
    j = np.arange(128)[None, :]
    for qt in range(NQT):
        for ss in range(4):
            t = qt * QT + 511 - (128 * ss + np.arange(128))
            back = (t // 64)[:, None] - j
            forced = (j == 0) | ((back >= 0) & (back < 2))
            A = ((back >= 0) & ~forced).astype(np.float32)
            B = np.where(back >= 0, np.where(forced, 1e4, 0.0), -1e30).astype(np.float32)
            tc[qt, :, ss, 0] = A
            tc[qt, :, ss, 1] = B
    c["topc"] = tc.reshape(NQT, 128, 1024)
    return c


class Attn:
    def __init__(self, P):
        self.P = P
        D = lambda n, s: P.dram(n, s, F32, kind="ExternalInput")
        self.qT = D("qT", [384, SEQ])
        self.gB = D("gB", [9, SEQ])
        self.kvT = D("kvT", [384, SEQ])
        self.vtok = D("vtok", [SEQ, 128])
        self.blk = D("blk", [2, 64, 32, 512])
        self.hg = D("hg", [128, 8])
        self.pos = D("pos", [2, 64, 32])
        self.w1 = D("w1", [2, 2048, 256])
        self.w2 = D("w2", [2, 256, 64])
        self.rb6 = D("rb6", [32, 6])
        self.OH = D("OH", [33, LTOT])
        self.identD = D("ident", [128, 128])
        self.EE = D("EE", [64, SEQ])
        self.ovlD = D("ovl", [128, 4, 128])
        self.topc = D("topc", [NQT, 128, 1024])
        self.yT = P.dram("yT", [192, SEQ], F32, kind="ExternalOutput")
        self.Rrow = P.dram("Rrow", [6, LTOT], F32, kind="Internal")
        self.alloc()

    def alloc(self):
        P = self.P
        sb = P.sb
        self.ksaug = sb("ksaug", [128, SEQ], BF16)
        self.kwT = sb("kwT", [64, SEQ], BF16)
        self.vsaug = sb("vsaug", [128, 64, 128], BF16)
        self.vwaug = sb("vwaug", [128, 64, 128], BF16)
        self.kcT = sb("kcT", [64, 512], BF16)
        self.vcaug = sb("vcaug", [128, 4, 128], BF16)
        self.Ubig = sb("Ubig", [128, 6 * LU], BF16)
        self.TTw = [sb("TTw%d" % h, [128, LTW], BF16) for h in range(3)]
        self.TTs = [sb("TTs%d" % h, [128, LTS], BF16) for h in range(3)]
        self.ident = sb("identb", [128, 128], BF16)
        self.ones = sb("ones", [128, 128], BF16)
        self.ovl = sb("ovlb", [128, 4, 128], BF16)
        self.hgs = sb("hgs", [128, 8], F32)
        self.cf = sb("cf", [128, 6], F32)
        self.tab = sb("tab", [33, 6], F32)
        self.epsT = sb("epsT", [128, 1], F32)
        self.st = [sb("st%d" % i, [128, 1024], F32) for i in range(2)]
        self.sq = [sb("sq%d" % i, [128, 512], BF16) for i in range(2)]
        self.rq = sb("rq", [128, 512], F32)
        self.qraw = sb("qraw", [64, 6, 512], F32)
        self.qn = sb("qn", [64, 6, 512], BF16)
        self.qaug = sb("qaug", [128, 3, 2, 512], BF16)
        self.pT = [sb("pT%d" % i, [128, 512], BF16) for i in range(6)]
        self.rdq = sb("rdq", [128, 4], F32)
        self.rdb = sb("rdb", [128, 512], F32)
        self.impacc = sb("impacc", [128, 512], F32)
        self.tcs = sb("tcs", [128, 1024], F32)
        self.sc = sb("sc", [128, 128], F32)
        self.sc2 = sb("sc2", [128, 128], F32)
        self.mx = sb("mx", [128, 16], F32)
        self.mneg = sb("mneg", [128, 128], BF16)
        self.gt = [sb("gt%d" % i, [64, 512], F32) for i in range(2)]
        self.rdt = sb("rdt", [64, 512], F32)
        self.wt = sb("wt", [64, 512], F32)
        self.tm = sb("tm", [64, 512], F32)
        self.y = sb("y", [64, 3, 512], F32)
        self.w1b = sb("w1b", [64, 4, 256], BF16)
        self.blkb = sb("blkb", [64, 4, 512], BF16)
        self.posT = sb("posT", [64, 32], F32)
        self.w2b = sb("w2b", [128, 2, 64], BF16)
        self.gx = self.rdb
        self.gy = self.impacc
        self.gT = [sb("gT%d" % i, [128, 512], BF16) for i in range(2)]
        self.A = [P.ps("A%d" % i, [128, 512]) for i in range(3)]
        self.S = [P.ps("S%d" % i, [128, 512]) for i in range(2)]
        self.X0 = P.ps("X0", [128, 512])
        self.X1 = P.ps("X1", [128, 512])
        self.XT = P.ps("XT", [128, 512], BF16)
        self.cnt = {}

    def uslice(self, h, mo):
        return self.Ubig[:, h * LU + mo: h * LU + mo + 512]

    def rot(self, name, lst):
        i = self.cnt.get(name, 0)
        self.cnt[name] = i + 1
        return lst[i % len(lst)]

    def norm64(self, src_t, src_ap, w, gain_ap, dst_t, dst_ap):
        P = self.P
        sq = self.rot("sq", self.sq)
        P.op("act", lambda e: e.activation(out=sq[0:64, 0:w], in_=src_ap, func=AF.Square), [src_t], [sq])
        P.op("pe", lambda e: e.matmul(self.X0[0:64, 0:w], lhsT=self.ones[0:64, 0:64], rhs=sq[0:64, 0:w],
                                      start=True, stop=True), [self.ones, sq], [self.X0])
        P.op("act", lambda e: e.activation(out=self.rq[0:64, 0:w], in_=self.X0[0:64, 0:w], func=AF.Ln,
                                           bias=self.epsT[0:64, :], scale=1.0 / 64.0), [self.X0, self.epsT], [self.rq])
        P.op("act", lambda e: e.activation(out=self.rq[0:64, 0:w], in_=self.rq[0:64, 0:w], func=AF.Exp, scale=-0.5),
             [self.rq], [self.rq])
        P.op("dve", lambda e: e.scalar_tensor_tensor(out=dst_ap, in0=src_ap, scalar=gain_ap, in1=self.rq[0:64, 0:w],
                                                     op0=ALU.mult, op1=ALU.mult), [src_t, self.hgs, self.rq], [dst_t])

    def setup(self):
        P = self.P
        st = self.st
        P.dma("sp", self.hgs, self.hgs[:], self.hg, self.hg[:])
        P.op("pool", lambda e: e.memset(self.ones[:], 1.0), [], [self.ones])
        P.op("pool", lambda e: e.memset(self.epsT[:], 1e-6), [], [self.epsT])
        P.op("pool", lambda e: e.memset(self.vsaug[:], 1.0), [], [self.vsaug])
        P.op("pool", lambda e: e.memset(self.vwaug[:], 1.0), [], [self.vwaug])
        P.op("pool", lambda e: e.memset(self.vcaug[:], 1.0), [], [self.vcaug])
        s = self.rot("st", st)
        P.dma("sp", s, s[:, 0:128], self.identD, self.identD[:])
        P.op("pool", lambda e, s=s: e.tensor_copy(out=self.ident[:], in_=s[:, 0:128]), [s], [self.ident])
        s = self.rot("st", st)
        P.dma("sp", s, s[:, 0:512], self.ovlD, self.ovlD[:].rearrange("p c j -> p (c j)"))
        P.op("pool", lambda e, s=s: e.tensor_copy(out=self.ovl[:].rearrange("p c j -> p (c j)"), in_=s[:, 0:512]),
             [s], [self.ovl])
        self.setup_tables()
        self.setup_kv()

    def load_rb(self):
        P = self.P
        rb = self.rb6[:]
        P.dma("sp", self.cf, self.cf[:], self.rb6, bass.AP(tensor=rb.tensor, offset=31 * 6, ap=[[0, 128], [1, 6]]))
        P.dma("sp", self.tab, self.tab[0:32, :], self.rb6, self.rb6[:])

    def setup_tables(self):
        P = self.P
        st = self.st
        P.op("pool", lambda e: e.memset(self.tab[:], MASKV), [], [self.tab])
        self.load_rb()
        P.op("dve", lambda e: e.tensor_scalar(out=self.tab[0:32, :], in0=self.tab[0:32, :], scalar1=8.0, scalar2=None,
                                              op0=ALU.mult), [self.tab], [self.tab])
        o = 0
        while o < LTOT:
            w = min(512, LTOT - o)
            s = self.rot("st", st)
            P.dma("sp", s, s[0:33, 0:w], self.OH, self.OH[:, o:o + w])
            P.op("pe", lambda e, s=s, w=w: e.matmul(self.X1[0:6, 0:w], lhsT=self.tab[:, :], rhs=s[0:33, 0:w],
                                                    start=True, stop=True), [self.tab, s], [self.X1])
            s2 = self.rot("st", st)
            P.op("act", lambda e, s2=s2, w=w: e.activation(out=s2[0:6, 0:w], in_=self.X1[0:6, 0:w], func=AF.Copy),
                 [self.X1], [s2])
            P.dma("sp", self.Rrow, self.Rrow[:, o:o + w], s2, s2[0:6, 0:w])
            o += w
        rr = self.Rrow[:]

        def expand(dst, d0, h, base, pstep, L):
            o = 0
            while o < L:
                w = min(1024, L - o)
                s = self.rot("st", st)
                P.dma("sp", s, s[:, 0:w], self.Rrow,
                      bass.AP(tensor=rr.tensor, offset=h * LTOT + base + o, ap=[[pstep, 128], [1, w]]))
                P.op("pool", lambda e, s=s, o=o, w=w: e.tensor_copy(out=dst[:, d0 + o:d0 + o + w], in_=s[:, 0:w]),
                     [s], [dst])
                o += w
        for h in range(6):
            expand(self.Ubig, h * LU, h, 0, 16, LU)
        for h in range(3):
            expand(self.TTw[h], 0, h, LC, 1, LTW)
            expand(self.TTs[h], 0, h, LC + LW, 1, LTS)

    def load_kv_piece(self, s, slot, c0):
        self.P.dma("sp", s, s[0:64, :], self.kvT, self.kvT[slot * 64:(slot + 1) * 64, c0:c0 + 1024])

    def load_v_piece(self, piece):
        P = self.P
        c0 = piece * 1024
        s = self.rot("st", self.st)
        P.dma("sp", s, s[:, :].rearrange("p (c f) -> p c f", f=128), self.vtok,
              self.vtok[c0:c0 + 1024, :].rearrange("(c p) f -> p c f", p=128))
        sv = s[:, :].rearrange("p (c f) -> p c f", f=128)
        P.op("pool", lambda e, sv=sv, piece=piece: e.tensor_copy(
            out=self.vsaug[:, piece * 8:(piece + 1) * 8, 0:64], in_=sv[:, :, 0:64]), [s], [self.vsaug])
        P.op("pool", lambda e, sv=sv, piece=piece: e.tensor_copy(
            out=self.vwaug[:, piece * 8:(piece + 1) * 8, 0:64], in_=sv[:, :, 64:128]), [s], [self.vwaug])

    def setup_kv(self):
        P = self.P
        st = self.st
        for piece in range(8):
            c0 = piece * 1024
            for slot, gcol, dst in ((2, 2, self.ksaug), (4, 3, self.kwT)):
                s = self.rot("st", st)
                self.load_kv_piece(s, slot, c0)
                for hf in range(2):
                    a = hf * 512
                    self.norm64(s, s[0:64, a:a + 512], 512, self.hgs[0:64, gcol:gcol + 1], dst,
                                dst[0:64, c0 + a:c0 + a + 512])
            s = self.rot("st", st)
            P.dma("sp", s, s[64:128, :], self.EE, self.EE[:, c0:c0 + 1024])
            P.op("pool", lambda e, s=s, c0=c0: e.tensor_copy(out=self.ksaug[64:128, c0:c0 + 1024], in_=s[64:128, :]),
                 [s], [self.ksaug])
            self.load_v_piece(piece)
        self.compress()

    def compress_rhs(self, kvi, tg):
        P = self.P
        for th in range(2):
            s = self.rot("st", self.st)
            t0 = tg * 4 + th * 2
            P.dma("sp", s, s[0:64, :].rearrange("p (t n) -> p t n", n=512), self.blk,
                  self.blk[kvi, :, t0:t0 + 2, :])
            for tt in range(2):
                t = t0 + tt
                P.op("dve", lambda e, s=s, tt=tt, th=th, t=t: e.tensor_scalar(
                    out=self.blkb[:, th * 2 + tt, :], in0=s[0:64, tt * 512:(tt + 1) * 512],
                    scalar1=self.posT[:, t:t + 1], scalar2=None, op0=ALU.add), [s, self.posT], [self.blkb])

    def compress_begin(self, kvi):
        pass

    def compress(self):
        P = self.P
        st = self.st
        H = [self.A[0], self.A[1]]
        for kvi in range(2):
            self.compress_begin(kvi)
            P.dma("sp", self.posT, self.posT[:], self.pos, self.pos[kvi])
            s = self.rot("st", st)
            P.dma("sp", s, s[:, 0:128].rearrange("p (c d) -> p c d", d=64), self.w2,
                  self.w2[kvi].rearrange("(c p) d -> p c d", p=128))
            P.op("pool", lambda e, s=s: e.tensor_copy(out=self.w2b[:].rearrange("p c d -> p (c d)"), in_=s[:, 0:128]),
                 [s], [self.w2b])
            for tg in range(8):
                s = self.rot("st", st)
                P.dma("sp", s, s[0:64, :].rearrange("p (t j) -> p t j", j=256), self.w1,
                      self.w1[kvi, tg * 256:(tg + 1) * 256, :].rearrange("(t d) j -> d t j", d=64))
                P.op("pool", lambda e, s=s: e.tensor_copy(out=self.w1b[:].rearrange("p t j -> p (t j)"),
                                                          in_=s[0:64, :]), [s], [self.w1b])
                self.compress_rhs(kvi, tg)
                for tl in range(4):
                    t = tg * 4 + tl
                    for jc in range(2):
                        P.op("pe", lambda e, tl=tl, jc=jc, t=t: e.matmul(
                            H[jc][:, :], lhsT=self.w1b[:, tl, jc * 128:(jc + 1) * 128], rhs=self.blkb[:, tl, :],
                            start=(t == 0), stop=(t == 31)), [self.w1b, self.blkb], [H[jc]])
            for jc in range(2):
                P.op("act", lambda e, jc=jc: e.activation(out=self.gx[:], in_=H[jc][:, :], func=AF.Copy),
                     [H[jc]], [self.gx])
                P.op("pool", lambda e: e.tensor_tensor(out=self.gy[:], in0=self.gx[:], in1=self.gx[:], op=ALU.mult),
                     [self.gx], [self.gy])
                P.op("dve", lambda e: e.tensor_scalar(out=self.gy[:], in0=self.gy[:], scalar1=0.044715, scalar2=1.0,
                                                      op0=ALU.mult, op1=ALU.add), [self.gy], [self.gy])
                P.op("pool", lambda e: e.tensor_tensor(out=self.gy[:], in0=self.gy[:], in1=self.gx[:], op=ALU.mult),
                     [self.gy, self.gx], [self.gy])
                P.op("act", lambda e: e.activation(out=self.gy[:], in_=self.gy[:], func=AF.Sigmoid,
                                                   scale=1.5957691216057308), [self.gy], [self.gy])
                P.op("dve", lambda e, jc=jc: e.tensor_tensor(out=self.gT[jc][:], in0=self.gx[:], in1=self.gy[:],
                                                             op=ALU.mult), [self.gx, self.gy], [self.gT[jc]])
            if kvi == 0:
                for jc in range(2):
                    P.op("pe", lambda e, jc=jc: e.matmul(self.X1[0:64, :], lhsT=self.w2b[:, jc, :], rhs=self.gT[jc][:],
                                                         start=(jc == 0), stop=(jc == 1)),
                         [self.w2b, self.gT[jc]], [self.X1])
                s = self.rot("st", st)
                P.op("act", lambda e, s=s: e.activation(out=s[0:64, 0:512], in_=self.X1[0:64, :], func=AF.Copy),
                     [self.X1], [s])
                self.norm64(s, s[0:64, 0:512], 512, self.hgs[0:64, 1:2], self.kcT, self.kcT[:, :])
            else:
                for cc in range(4):
                    for jc in range(2):
                        P.op("pe", lambda e, jc=jc, cc=cc: e.matmul(
                            self.X1[:, cc * 64:(cc + 1) * 64], lhsT=self.gT[jc][:, cc * 128:(cc + 1) * 128],
                            rhs=self.w2b[:, jc, :], start=(jc == 0), stop=(jc == 1)),
                            [self.w2b, self.gT[jc]], [self.X1])
                    P.op("act", lambda e, cc=cc: e.activation(out=self.vcaug[:, cc, 0:64],
                                                              in_=self.X1[:, cc * 64:(cc + 1) * 64], func=AF.Copy),
                         [self.X1], [self.vcaug])

    def combine(self, qt, h, br, first):
        P = self.P
        A = self.A[h]
        s0 = qt * QT
        gt = self.load_gate(qt, h, br)
        P.op("dve", lambda e: e.tensor_scalar(out=self.rdt[:], in0=A[64:128, :], scalar1=1e-30, scalar2=None,
                                              op0=ALU.max), [A], [self.rdt])
        P.op("dve", lambda e: e.reciprocal(out=self.rdt[:], in_=self.rdt[:]), [self.rdt], [self.rdt])
        P.op("pool", lambda e: e.tensor_tensor(out=self.wt[:], in0=self.rdt[:], in1=gt[:], op=ALU.mult),
             [self.rdt, gt], [self.wt])
        if first:
            P.op("dve", lambda e: e.tensor_tensor(out=self.y[:, h, :], in0=A[0:64, :], in1=self.wt[:], op=ALU.mult),
                 [A, self.wt], [self.y])
        else:
            P.op("dve", lambda e: e.tensor_tensor(out=self.tm[:], in0=A[0:64, :], in1=self.wt[:], op=ALU.mult),
                 [A, self.wt], [self.tm])
            P.op("pool", lambda e: e.tensor_tensor(out=self.y[:, h, :], in0=self.y[:, h, :], in1=self.tm[:],
                                                   op=ALU.add), [self.y, self.tm], [self.y])

    def exp(self, S, pT, bias):
        P = self.P
        if bias is None:
            P.op("act", lambda e: e.activation(out=pT[:], in_=S[:, :], func=AF.Exp, scale=0.125), [S], [pT])
        else:
            bap, bt = bias
            P.op("act", lambda e: e.activation(out=pT[:], in_=S[:, :], func=AF.Exp, scale=0.125, bias=bap),
                 [S, bt], [pT])

    def cmp_bias(self, cc, h, mixed):
        return None if mixed else (self.cf[:, h:h + 1], self.cf)

    def sel_bias(self, kc, h, near):
        return None if near else (self.cf[:, h:h + 1], self.cf)

    def win_bias(self, c):
        return None

    def topc_ap(self, qt):
        return self.topc[qt]

    def load_q(self, qt):
        s0 = qt * QT
        self.P.dma("sp", self.qraw, self.qraw[:], self.qT, self.qT[:, s0:s0 + QT].rearrange("(h d) t -> d h t", d=64))

    def load_gate(self, qt, h, br):
        s0 = qt * QT
        gt = self.rot("gt", self.gt)
        g = self.gB[:]
        self.P.dma("sp", gt, gt[:], self.gB,
                   bass.AP(tensor=g.tensor, offset=(h * 3 + br) * SEQ + s0, ap=[[0, 64], [1, 512]]))
        return gt

    def store_y(self, qt):
        s0 = qt * QT
        self.P.dma("sp", self.yT, self.yT[:, s0:s0 + QT].rearrange("(h d) t -> d h t", d=64), self.y, self.y[:],
                   nodep_out=True)

    def qtile(self, qt):
        P = self.P
        s0 = qt * QT
        self.load_q(qt)
        P.dma("sp", self.tcs, self.tcs[:], self.topc, self.topc_ap(qt))
        for h in range(6):
            self.norm64(self.qraw, self.qraw[:, h, :], 512, self.hgs[0:64, 0:1], self.qn, self.qn[:, h, :])
        for h in range(3):
            for half in range(2):
                P.op("pool", lambda e, h=h, half=half: e.tensor_copy(out=self.qaug[0:64, h, half, :],
                                                                     in_=self.qn[:, h, :]), [self.qn], [self.qaug])
        for h in range(6):
            chunks = [cc for cc in range(4) if qt - 4 * cc >= 0]
            pts = []
            for cc in chunks:
                k = qt - 4 * cc
                mixed = k <= 5
                S = self.rot("S", self.S)
                P.op("pe", lambda e, S=S, cc=cc, h=h, mixed=mixed: e.matmul(
                    S[:, :], lhsT=self.kcT[:, cc * 128:(cc + 1) * 128], rhs=self.qn[:, h, :], start=True,
                    stop=(not mixed)), [self.kcT, self.qn], [S])
                if mixed:
                    mo = 512 * (5 - k)
                    P.op("pe", lambda e, S=S, h=h, mo=mo: e.matmul(S[:, :], lhsT=self.ident[:],
                                                                   rhs=self.uslice(h, mo), start=False, stop=True),
                         [self.ident, self.Ubig], [S])
                pT = self.rot("pT", self.pT)
                self.exp(S, pT, self.cmp_bias(cc, h, mixed))
                pts.append(pT)
            n = len(chunks)
            if h < 3:
                for i, (cc, pT) in enumerate(zip(chunks, pts)):
                    P.op("pe", lambda e, pT=pT, i=i, cc=cc, h=h: e.matmul(
                        self.A[h][:, :], lhsT=self.vcaug[:, cc, :], rhs=pT[:], start=(i == 0), stop=(i == n - 1)),
                        [self.vcaug, pT], [self.A[h]])
            for ss in range(4):
                for i, (cc, pT) in enumerate(zip(chunks, pts)):
                    P.op("pe", lambda e, pT=pT, i=i, cc=cc, ss=ss: e.matmul(
                        self.X1[:, ss * 128:(ss + 1) * 128], lhsT=pT[:, ss * 128:(ss + 1) * 128], rhs=self.ovl[:, cc, :],
                        start=(i == 0), stop=(i == n - 1)), [pT, self.ovl], [self.X1])
            for ss in range(4):
                for i, pT in enumerate(pts):
                    P.op("pe", lambda e, pT=pT, i=i, ss=ss: e.matmul(
                        self.X0[:, ss:ss + 1], lhsT=pT[:, ss * 128:(ss + 1) * 128], rhs=self.ones[:, 0:1],
                        start=(i == 0), stop=(i == n - 1)), [pT, self.ones], [self.X0])
            P.op("dve", lambda e: e.tensor_scalar(out=self.rdq[:], in0=self.X0[:, 0:4], scalar1=1e-30, scalar2=None,
                                                  op0=ALU.max), [self.X0], [self.rdq])
            P.op("dve", lambda e: e.reciprocal(out=self.rdq[:], in_=self.rdq[:]), [self.rdq], [self.rdq])
            for ss in range(4):
                if h == 0:
                    P.op("dve", lambda e, ss=ss: e.tensor_scalar(
                        out=self.impacc[:, ss * 128:(ss + 1) * 128], in0=self.X1[:, ss * 128:(ss + 1) * 128],
                        scalar1=self.rdq[:, ss:ss + 1], scalar2=None, op0=ALU.mult), [self.X1, self.rdq], [self.impacc])
                else:
                    P.op("dve", lambda e, ss=ss: e.scalar_tensor_tensor(
                        out=self.impacc[:, ss * 128:(ss + 1) * 128], in0=self.X1[:, ss * 128:(ss + 1) * 128],
                        scalar=self.rdq[:, ss:ss + 1], in1=self.impacc[:, ss * 128:(ss + 1) * 128],
                        op0=ALU.mult, op1=ALU.add), [self.X1, self.rdq, self.impacc], [self.impacc])
            if h < 3:
                self.combine(qt, h, 0, True)
        for ss in range(4):
            tA = self.tcs[:, ss * 256: ss * 256 + 128]
            tB = self.tcs[:, ss * 256 + 128: ss * 256 + 256]
            P.op("dve", lambda e, ss=ss, tA=tA: e.tensor_tensor(out=self.sc[:], in0=self.impacc[:, ss * 128:(ss + 1) * 128],
                                                                in1=tA, op=ALU.mult), [self.impacc, self.tcs], [self.sc])
            P.op("dve", lambda e, tB=tB: e.tensor_tensor(out=self.sc[:], in0=self.sc[:], in1=tB, op=ALU.add),
                 [self.sc, self.tcs], [self.sc])
            P.op("dve", lambda e: e.max(out=self.mx[:, 0:8], in_=self.sc[:]), [self.sc], [self.mx])
            P.op("dve", lambda e: e.match_replace(out=self.sc2[:], in_to_replace=self.mx[:, 0:8], in_values=self.sc[:],
                                                  imm_value=-1e30), [self.sc, self.mx], [self.sc2])
            P.op("dve", lambda e: e.max(out=self.mx[:, 8:16], in_=self.sc2[:]), [self.sc2], [self.mx])
            P.op("dve", lambda e: e.tensor_scalar(out=self.mneg[:], in0=self.sc[:], scalar1=self.mx[:, 15:16], scalar2=1.0,
                                                  op0=ALU.is_ge, op1=ALU.subtract), [self.sc, self.mx], [self.mneg])
            P.op("pe", lambda e, ss=ss: e.transpose(self.XT[:, ss * 128:(ss + 1) * 128], self.mneg[:], self.ident[:]),
                 [self.mneg, self.ident], [self.XT])
        for h in range(3):
            for half in range(2):
                eng = "act" if (h + half) % 2 == 0 else "dve"
                if eng == "act":
                    P.op("act", lambda e, h=h, half=half: e.activation(out=self.qaug[64:128, h, half, :],
                                                                       in_=self.XT[half * 64:(half + 1) * 64, :],
                                                                       func=AF.Copy), [self.XT], [self.qaug])
                else:
                    P.op("dve", lambda e, h=h, half=half: e.tensor_copy(out=self.qaug[64:128, h, half, :],
                                                                        in_=self.XT[half * 64:(half + 1) * 64, :]),
                         [self.XT], [self.qaug])
        nkc = 4 * qt + 4

        def sel_qk(kc, h):
            k0 = 128 * kc
            ep = s0 + 511 - k0
            near = ep <= DS
            half = kc // 32
            S = self.rot("S", self.S)
            P.op("pe", lambda e: e.matmul(S[:, :], lhsT=self.ksaug[:, k0:k0 + 128], rhs=self.qaug[:, h, half, :],
                                          start=True, stop=(not near)), [self.ksaug, self.qaug], [S])
            if near:
                mo = DS - ep
                P.op("pe", lambda e: e.matmul(S[:, :], lhsT=self.ident[:], rhs=self.TTs[h][:, mo:mo + 512],
                                              start=False, stop=True), [self.ident, self.TTs[h]], [S])
            return (kc, h, near, S)

        def sel_rest(kc, h, near, S):
            pT = self.rot("pT", self.pT)
            self.exp(S, pT, self.sel_bias(kc, h, near))
            P.op("pe", lambda e: e.matmul(self.A[h][:, :], lhsT=self.vsaug[:, kc, :], rhs=pT[:],
                                          start=(kc == 0), stop=(kc == nkc - 1)), [self.vsaug, pT], [self.A[h]])

        prev = None
        for kc in range(nkc):
            for h in range(3):
                cur = sel_qk(kc, h)
                if prev is not None:
                    sel_rest(*prev)
                prev = cur
        sel_rest(*prev)
        for h in range(3):
            self.combine(qt, h, 1, False)
        rs = [r for r in range(8) if s0 - 512 + 128 * r >= 0]

        def win_qk(i, r, h):
            k0 = s0 - 512 + 128 * r
            mo = 128 * r
            S = self.rot("S", self.S)
            P.op("pe", lambda e: e.matmul(S[:, :], lhsT=self.kwT[:, k0:k0 + 128], rhs=self.qn[:, h, :],
                                          start=True, stop=False), [self.kwT, self.qn], [S])
            P.op("pe", lambda e: e.matmul(S[:, :], lhsT=self.ident[:], rhs=self.TTw[h][:, mo:mo + 512],
                                          start=False, stop=True), [self.ident, self.TTw[h]], [S])
            return (i, k0, h, S)

        def win_rest(i, k0, h, S):
            pT = self.rot("pT", self.pT)
            self.exp(S, pT, self.win_bias(k0 // 128))
            P.op("pe", lambda e: e.matmul(self.A[h][:, :], lhsT=self.vwaug[:, k0 // 128, :], rhs=pT[:],
                                          start=(i == 0), stop=(i == len(rs) - 1)), [self.vwaug, pT], [self.A[h]])

        prev = None
        for i, r in enumerate(rs):
            for h in range(3):
                cur = win_qk(i, r, h)
                if prev is not None:
                    win_rest(*prev)
                prev = cur
        win_rest(*prev)
        for h in range(3):
            self.combine(qt, h, 2, False)
        self.store_y(qt)

    def build(self, nqt=NQT):
        self.setup()
        for qt in range(nqt):
            self.qtile(qt)
        self.P.wait_all("sp", [self.yT])
        self.P.finalize()


def build_attn(nqt=NQT):
    nc = bass.Bass("TRN2", target_bir_lowering=False)
    P = Prog(nc)
    a = Attn(P)
    a.build(nqt)
    return nc, P


XPAD = SEQ + HALO


def rev_ap(ap, n):
    dims = [list(d) for d in ap.ap]
    dims[-1] = [-1, n]
    return bass.AP(tensor=ap.tensor, offset=ap.offset + (n - 1), ap=dims)


class FTok(TokStage):
    def __init__(self, P, mode, dr):
        self.P = P
        self.mode = mode
        self.dr = dr
        for k in ("memT", "gains", "hg", "mem_w_kv", "w_out", "w_up", "w_down", "b_w_in", "a_w_in", "convw", "w_kv"):
            setattr(self, k, dr[k])
        self.xsrc = dr["xin"] if mode == "L1" else dr["X"]
        self.kvT_o, self.qT_o, self.gT_o = dr["KV"], dr["Q"], dr["G"]
        self.ytokT = dr["Y"]
        self.chunk = 0
        self.alloc()

    def halo_fix(self, sub):
        if self.chunk == 0:
            self.P.op("pool", lambda e: e.memset(self.v[:, sub, 2:2 + HALO], 0.0), [], [self.v])

    def store_tile(self, dstT, r, msz, c0, t0, w, ost):
        a = c0 + t0
        lo = HALO if a == 0 else 0
        g0 = NTOK * self.chunk + a + lo - HALO
        self.P.dma("sp", dstT, dstT[r:r + msz, g0:g0 + (w - lo)], ost, ost[0:msz, lo:w], nodep_out=True)

    def ytok_src(self, ch, c0, t0, w):
        g0 = NTOK * self.chunk + c0 + t0
        return self.ytokT[ch * 128:(ch + 1) * 128, g0:g0 + w]

    def setup_consts(self):
        P = self.P
        P.dma("sp", self.g, self.g[:], self.gains, self.gains[:])
        P.dma("sp", self.hgs, self.hgs[:], self.hg, self.hg[:])
        if self.mode == "L1":
            P.dma("sp", self.cw, self.cw[:], self.convw, self.convw[:])
        for c in range(8):
            P.dma("sp", self.u, self.memx(c, 0, 256), self.memT, self.memT[c * 128:(c + 1) * 128, :], nodep_out=True)
        P.op("pool", lambda e: e.memset(self.ones[:], 1.0), [], [self.ones])
        P.op("pool", lambda e: e.memset(self.bd[:], 0.0), [], [self.bd])
        P.op("pool", lambda e: e.memset(self.bd[0:64, 0:64], 1.0), [], [self.bd])
        P.op("pool", lambda e: e.memset(self.bd[64:128, 64:128], 1.0), [], [self.bd])
        P.op("pool", lambda e: e.memset(self.epsT[:], EPS), [], [self.epsT])
        P.op("pool", lambda e: e.memset(self.vaug[:], 1.0), [], [self.vaug])
        self.norm(self.u, self.memx, 256, [(0, 256)], 15, self.memh, 0)

    def load_x(self):
        P = self.P
        g0 = NTOK * self.chunk
        for c in range(8):
            P.dma("sp", self.x, self.x[:, c, :], self.xsrc, self.xsrc[c * 128:(c + 1) * 128, g0:g0 + NCOL],
                  nodep_out=(c > 0))

    def store_x(self):
        P = self.P
        dst = self.dr["out"] if self.mode == "L5" else self.dr["X"]
        off = 0 if self.mode == "L5" else HALO
        g0 = NTOK * self.chunk + off
        for c in range(8):
            P.dma("sp", dst, dst[c * 128:(c + 1) * 128, g0:g0 + NTOK], self.x, self.x[:, c, HALO:NCOL], nodep_out=True)

    def run(self):
        self.setup_consts()
        for chunk in range(4):
            self.chunk = chunk
            self.load_x()
            if self.mode == "L1":
                for l in range(2):
                    self.mem_kv(l)
                    for ri, (c0, n) in enumerate(RANGES):
                        tiles = tiles_of(n)
                        self.norm(self.x, self.xs(c0), n, tiles, l, self.h, 0)
                        self.mix_a(l, ri, c0, n, tiles)
                        self.mem_attn(l, n, tiles)
                        self.out_proj(l, c0, n, tiles)
                        self.mlp(l, c0, n, tiles)
                for ri, (c0, n) in enumerate(RANGES):
                    tiles = tiles_of(n)
                    self.norm(self.x, self.xs(c0), n, tiles, 8, self.h, 0)
                    for i in range(3):
                        self.store_proj(self.w_kv, self.w_kv[:, 256 * i:256 * (i + 1)], 256, self.kvT_o, 256 * i, c0, tiles)
                    self.pre_b(0, c0, n, tiles)
            else:
                j = 0 if self.mode == "L3" else 1
                self.mem_kv(2 + j)
                for ri, (c0, n) in enumerate(RANGES):
                    self.post_b(j, c0, n, tiles_of(n))
                if j == 0:
                    for ri, (c0, n) in enumerate(RANGES):
                        self.pre_b(1, c0, n, tiles_of(n))
            self.store_x()


class FAttn(Attn):
    def __init__(self, P, dr, jl):
        self.P = P
        self.dr = dr
        self.jl = jl
        self.Q, self.G, self.KV, self.Y = dr["Q"], dr["G"], dr["KV"], dr["Y"]
        self.hg = dr["ahg%d" % jl]
        self.pos, self.w1, self.w2 = dr["pos"], dr["cmp_w1"], dr["cmp_w2"]
        self.rel = dr["rel_bias"]
        self.OH, self.identD, self.EE, self.ovlD, self.topc = dr["OH"], dr["ident"], dr["EE"], dr["ovl"], dr["topc"]
        self.Rrow = dr["Rrow"]
        self.alloc()

    def alloc(self):
        Attn.alloc(self)
        self.ub_view = self.Ubig[0:64, 0:16 * 513].rearrange("p (r n) -> p r n", n=513)

    def set_combo(self, g, tr):
        self.g, self.tr = g, tr
        own = [3 * tr + i for i in range(3)]
        oth = [3 * (1 - tr) + i for i in range(3)]
        self.heads = own + oth

    def load_rb(self):
        P = self.P
        rel = self.rel[:]
        for i, hh in enumerate(self.heads):
            col = self.g * 6 + hh
            P.dma("sp", self.cf, self.cf[:, i:i + 1], self.rel,
                  bass.AP(tensor=rel.tensor, offset=31 * 12 + col, ap=[[0, 128], [1, 1]]))
            P.dma("sp", self.tab, self.tab[0:32, i:i + 1], self.rel, self.rel[:, col:col + 1],
                  allow_slow_non_contiguous=True)

    def load_kv_piece(self, s, slot, c0):
        r0 = slot * 128 + self.g * 64
        self.P.dma("sp", s, s[0:64, :], self.KV, self.KV[r0:r0 + 64, c0:c0 + 1024])

    def load_v_piece(self, piece):
        P = self.P
        c0 = piece * 1024
        for slot, dst in ((3, self.vsaug), (5, self.vwaug)):
            s = self.rot("st", self.st)
            self.load_kv_piece(s, slot, c0)
            for hf in range(2):
                sq = self.rot("sq", self.sq)
                P.op("pool", lambda e, s=s, sq=sq, hf=hf: e.tensor_copy(out=sq[0:64, :], in_=s[0:64, hf * 512:(hf + 1) * 512]),
                     [s], [sq])
                for c in range(4):
                    P.op("pe", lambda e, sq=sq, c=c: e.transpose(self.XT[:, c * 64:(c + 1) * 64],
                                                                 sq[0:64, c * 128:(c + 1) * 128], self.ident[0:64, 0:64]),
                         [sq, self.ident], [self.XT])
                ch0 = piece * 8 + hf * 4
                P.op("act", lambda e, dst=dst, ch0=ch0: e.activation(
                    out=dst[:, ch0:ch0 + 4, 0:64], in_=self.XT[:, 0:256].rearrange("p (c d) -> p c d", d=64),
                    func=AF.Copy), [self.XT], [dst])

    def compress_begin(self, kvi):
        P = self.P
        P.op("pool", lambda e: e.memset(self.ub_view[:, :, 512:513], 0.0), [], [self.Ubig])
        for piece in range(8):
            s = self.rot("st", self.st)
            self.load_kv_piece(s, kvi, piece * 1024)
            P.op("pool", lambda e, s=s, piece=piece: e.tensor_copy(
                out=self.ub_view[:, :, piece * 64:(piece + 1) * 64],
                in_=s[0:64, :].rearrange("p (n r) -> p r n", r=16)), [s], [self.Ubig])

    def compress_rhs(self, kvi, tg):
        P = self.P
        for tl in range(4):
            t = tg * 4 + tl
            r, off = t % 16, t // 16
            P.op("dve", lambda e, tl=tl, t=t, r=r, off=off: e.tensor_scalar(
                out=self.blkb[:, tl, :], in0=self.ub_view[:, r, off:off + 512], scalar1=self.posT[:, t:t + 1],
                scalar2=None, op0=ALU.add), [self.Ubig, self.posT], [self.blkb])

    def load_q(self, qt):
        P = self.P
        s0 = qt * QT
        for i2 in range(3):
            s = self.rot("st", self.st)
            for k in range(2):
                hh = self.heads[2 * i2 + k]
                r0 = (self.g * 6 + hh) * 64
                P.dma("sp", s, s[0:64, k * 512:(k + 1) * 512], self.Q, self.Q[r0:r0 + 64, s0:s0 + QT], nodep_out=(k > 0))
            for k in range(2):
                P.op("dve", lambda e, s=s, k=k, i2=i2: e.tensor_copy(
                    out=self.qraw[:, 2 * i2 + k, :], in_=rev_ap(s[0:64, k * 512:(k + 1) * 512], 512)), [s], [self.qraw])

    def load_gate(self, qt, h, br):
        P = self.P
        s0 = qt * QT
        gt = self.rot("gt", self.gt)
        g = self.G[:]
        row = (self.g * 6 + 3 * self.tr + h) * 3 + br
        P.dma("sp", gt, gt[:], self.G, bass.AP(tensor=g.tensor, offset=row * SEQ + s0, ap=[[0, 64], [1, 512]]))
        gr = self.rot("gtr", self.gtr)
        P.op("dve", lambda e: e.tensor_copy(out=gr[:], in_=rev_ap(gt[:], 512)), [gt], [gr])
        return gr

    def store_y(self, qt):
        P = self.P
        s0 = qt * QT
        bufs = [self.rdt, self.wt, self.tm]
        for h in range(3):
            b = bufs[h]
            P.op("dve", lambda e, b=b, h=h: e.tensor_copy(out=b[:], in_=rev_ap(self.y[:, h, :], 512)), [self.y], [b])
            r0 = (self.g * 6 + 3 * self.tr + h) * 64
            P.dma("sp", self.Y, self.Y[r0:r0 + 64, HALO + s0:HALO + s0 + QT], b, b[:], nodep_out=True)

    def run(self):
        P = self.P
        st = self.st
        self.gtr = [P.sb("gtr%d" % i, [64, 512], F32) for i in range(2)]
        P.dma("sp", self.hgs, self.hgs[:], self.hg, self.hg[:])
        P.op("pool", lambda e: e.memset(self.ones[:], 1.0), [], [self.ones])
        P.op("pool", lambda e: e.memset(self.epsT[:], 1e-6), [], [self.epsT])
        P.op("pool", lambda e: e.memset(self.vsaug[:], 1.0), [], [self.vsaug])
        P.op("pool", lambda e: e.memset(self.vwaug[:], 1.0), [], [self.vwaug])
        P.op("pool", lambda e: e.memset(self.vcaug[:], 1.0), [], [self.vcaug])
        s = self.rot("st", st)
        P.dma("sp", s, s[:, 0:128], self.identD, self.identD[:])
        P.op("pool", lambda e, s=s: e.tensor_copy(out=self.ident[:], in_=s[:, 0:128]), [s], [self.ident])
        s = self.rot("st", st)
        P.dma("sp", s, s[:, 0:512], self.ovlD, self.ovlD[:].rearrange("p c j -> p (c j)"))
        P.op("pool", lambda e, s=s: e.tensor_copy(out=self.ovl[:].rearrange("p c j -> p (c j)"), in_=s[:, 0:512]),
             [s], [self.ovl])
        for g in range(2):
            self.set_combo(g, 0)
            P.barrier()
            self.setup_kv()
            for tr in range(2):
                self.set_combo(g, tr)
                P.barrier()
                self.setup_tables()
                for qt in range(NQT):
                    self.qtile(qt)


def fused_dram(P, r3=False):
    dr = {}
    D = lambda n, s: P.dram(n, s, F32, kind="ExternalInput")
    dr["xin"] = D("xin", [1024, XPAD])
    dr["memT"] = D("memT", [1024, 256])
    dr["gains"] = D("gains", [128, 128])
    dr["hg"] = D("hg", [128, 16])
    dr["convw"] = D("convw", [128, 36])
    dr["mem_w_kv"] = D("mem_w_kv", [4, 1024, 512])
    dr["w_out"] = D("w_out", [4, 1024, 1024])
    dr["w_up"] = D("w_up", [4, 1024, 4096])
    dr["w_down"] = D("w_down", [4, 4096, 1024])
    dr["b_w_in"] = D("b_w_in", [2, 1024, 1060])
    dr["a_w_in"] = D("a_w_in", [2, 1024, 2560])
    dr["w_kv"] = D("w_kv", [1024, 768])
    dr["ahg0"] = D("ahg0", [128, 8])
    dr["ahg1"] = D("ahg1", [128, 8])
    dr["pos"] = D("pos", [2, 64, 32])
    dr["cmp_w1"] = D("cmp_w1", [2, 2048, 256])
    dr["cmp_w2"] = D("cmp_w2", [2, 256, 64])
    dr["rel_bias"] = D("rel_bias", [32, 12])
    dr["OH"] = D("OH", [33, LTOT])
    dr["ident"] = D("ident", [128, 128])
    dr["EE"] = D("EE", [64, SEQ])
    dr["ovl"] = D("ovl", [128, 4, 128])
    if not r3:
        dr["topc"] = D("topc", [NQT, 128, 1024])
    if not r3:
        dr["out"] = P.dram("xT_out", [1024, SEQ], F32, kind="ExternalOutput")
    I = lambda n, s: P.dram(n, s, F32, kind="Internal")
    dr["X"] = I("Xs", [1024, XPAD])
    if not r3:
        dr["KV"] = I("KVs", [768, SEQ])
    dr["Q"] = I("Qs", [768, SEQ])
    dr["G"] = I("Gs", [36, SEQ])
    dr["Y"] = I("Ys", [768, XPAD])
    dr["Rrow"] = I("Rrow", [6, LTOT])
    return dr


def build_fused():
    nc = bass.Bass("TRN2", target_bir_lowering=False)
    P = Prog(nc)
    dr = fused_dram(P)
    P.push_scope()
    FTok(P, "L1", dr).run()
    P.pop_scope()
    for j in range(2):
        P.push_scope()
        FAttn(P, dr, j).run()
        P.pop_scope()
        P.push_scope()
        FTok(P, "L3" if j == 0 else "L5", dr).run()
        P.pop_scope()
    P.wait_all("sp", [dr["out"]])
    P.finalize()
    return nc, P


KPAD = 1536
QTILES = (3, 7, 11, 15)


def host_core_tables(k):
    tc = np.zeros((4, 128, 4, 2, 128), np.float32)
    jj = np.arange(128)[None, :]
    for j, qt in enumerate(QTILES):
        for ss in range(4):
            ta = qt * QT + 511 - (128 * ss + np.arange(128))
            back = (ta // 64)[:, None] - jj
            J = jj - 24 + 8 * k
            valid = (J >= 0) & (back >= 0)
            forced = (J == 0) | ((back >= 0) & (back < 2))
            tc[j, :, ss, 0] = (valid & ~forced).astype(np.float32)
            tc[j, :, ss, 1] = np.where(valid, np.where(forced, 1e4, 0.0), -1e30).astype(np.float32)
    kval = np.zeros((128, 16), np.float32)
    first = KPAD - 512 * k
    for c in range(12):
        a = 128 * c + np.arange(128)
        kval[:, c] = np.where(a >= first, 0.0, MASKV)
    kval[:, 12] = np.where(16 * np.arange(128) >= first, 0.0, MASKV)
    return tc.reshape(4, 128, 1024), kval


class FTok2(FTok):
    def __init__(self, P, mode, dr):
        FTok.__init__(self, P, mode, dr)
        if mode != "L1":
            self.xsrc = dr["Xown"]
            self.ytokT = dr["Yown"]
            self.qT_o, self.gT_o = dr["Qown"], dr["Gown"]

    def store_tile(self, dstT, r, msz, c0, t0, w, ost):
        if self.mode == "L1":
            off = KPAD if dstT is self.dr["KV"] else 0
            a = c0 + t0
            lo = HALO if a == 0 else 0
            g0 = NTOK * self.chunk + a + lo - HALO + off
            self.P.dma("sp", dstT, dstT[r:r + msz, g0:g0 + (w - lo)], ost, ost[0:msz, lo:w], nodep_out=True)
        else:
            self.P.dma("sp", dstT, dstT[r:r + msz, c0 + t0:c0 + t0 + w], ost, ost[0:msz, 0:w], nodep_out=True)

    def ytok_src(self, ch, c0, t0, w):
        if self.mode == "L1":
            return FTok.ytok_src(self, ch, c0, t0, w)
        return self.ytokT[ch * 128:(ch + 1) * 128, c0 + t0:c0 + t0 + w]

    def load_x(self):
        if self.mode == "L1":
            return FTok.load_x(self)
        P = self.P
        for c in range(8):
            P.dma("sp", self.x, self.x[:, c, 0:NTOK], self.xsrc, self.xsrc[c * 128:(c + 1) * 128, :], nodep_out=(c > 0))

    def store_x(self):
        if self.mode == "L1":
            return FTok.store_x(self)
        P = self.P
        dst = self.dr["out"] if self.mode == "L5" else self.dr["Xown"]
        for c in range(8):
            P.dma("sp", dst, dst[c * 128:(c + 1) * 128, :], self.x, self.x[:, c, 0:NTOK], nodep_out=True)

    def zero_kv_pad(self):
        P = self.P
        z = self.ost[0]
        P.op("pool", lambda e: e.memset(z[:], 0.0), [], [z])
        KV = self.dr["KV"]
        for r in range(6):
            for cb in range(3):
                P.dma("sp", KV, KV[r * 128:(r + 1) * 128, cb * 512:(cb + 1) * 512], z, z[:], nodep_out=True)

    def run(self):
        if self.mode == "L1":
            self.zero_kv_pad()
            return FTok.run(self)
        self.setup_consts()
        self.chunk = 0
        self.load_x()
        j = 0 if self.mode == "L3" else 1
        self.mem_kv(2 + j)
        R2 = [(0, 1024), (1024, 1024)]
        for (c0, n) in R2:
            self.post_b(j, c0, n, tiles_of(n))
        if j == 0:
            for (c0, n) in R2:
                self.pre_b(1, c0, n, tiles_of(n))
        self.store_x()


def extract_own(P, dr):
    kd = dr["kd"]

    def dyn3(src_t, r0, r1, pad, ncols):
        base = src_t[r0:r1, pad:pad + KPAD + 6144 + 512][:, bass.ds(kd, 6144 + 512)]
        return bass.AP(tensor=base.tensor, offset=base.offset, ap=[[ncols, r1 - r0], [2048, 4], [1, 512]])
    KVp, KVw = dr["KV"], dr["KVwin"]
    for rb in range(3):
        P.dma("sp", KVw, KVw[rb * 256:(rb + 1) * 256, :], KVp,
              KVp[rb * 256:(rb + 1) * 256, 0:KPAD + SEQ][:, bass.ds(kd, SEQ)], nodep_out=True)
    for rb in range(4):
        P.dma("sp", dr["Xown"], dr["Xown"][rb * 256:(rb + 1) * 256, :].rearrange("r (s c) -> r s c", c=512), dr["X"],
              dyn3(dr["X"], rb * 256, (rb + 1) * 256, HALO, XPAD), nodep_out=True)
    for rb in range(3):
        P.dma("sp", dr["Qown"], dr["Qown"][rb * 256:(rb + 1) * 256, :].rearrange("r (s c) -> r s c", c=512), dr["Q"],
              dyn3(dr["Q"], rb * 256, (rb + 1) * 256, 0, SEQ), nodep_out=True)
    P.dma("sp", dr["Gown"], dr["Gown"][:, :].rearrange("r (s c) -> r s c", c=512), dr["G"],
          dyn3(dr["G"], 0, 36, 0, SEQ), nodep_out=True)


class FAttn2(FAttn):
    def __init__(self, P, dr, jl):
        FAttn.__init__(self, P, dr, jl)
        self.KV = dr["KVwin"]
        self.Q, self.G, self.Y = dr["Qown"], dr["Gown"], dr["Yown"]
        self.kvalD = dr["kval"]

    def load_q(self, qt):
        P = self.P
        c0 = ((qt - 3) // 4) * 512
        for i2 in range(3):
            s = self.rot("st", self.st)
            for k in range(2):
                hh = self.heads[2 * i2 + k]
                r0 = (self.g * 6 + hh) * 64
                P.dma("sp", s, s[0:64, k * 512:(k + 1) * 512], self.Q, self.Q[r0:r0 + 64, c0:c0 + 512], nodep_out=(k > 0))
            for k in range(2):
                P.op("dve", lambda e, s=s, k=k, i2=i2: e.tensor_copy(
                    out=self.qraw[:, 2 * i2 + k, :], in_=rev_ap(s[0:64, k * 512:(k + 1) * 512], 512)), [s], [self.qraw])

    def load_gate(self, qt, h, br):
        P = self.P
        c0 = ((qt - 3) // 4) * 512
        gt = self.rot("gt", self.gt)
        g = self.G[:]
        row = (self.g * 6 + 3 * self.tr + h) * 3 + br
        P.dma("sp", gt, gt[:], self.G, bass.AP(tensor=g.tensor, offset=row * NTOK + c0, ap=[[0, 64], [1, 512]]))
        gr = self.rot("gtr", self.gtr)
        P.op("dve", lambda e: e.tensor_copy(out=gr[:], in_=rev_ap(gt[:], 512)), [gt], [gr])
        return gr

    def store_y(self, qt):
        P = self.P
        c0 = ((qt - 3) // 4) * 512
        bufs = [self.rdt, self.wt, self.tm]
        for h in range(3):
            b = bufs[h]
            P.op("dve", lambda e, b=b, h=h: e.tensor_copy(out=b[:], in_=rev_ap(self.y[:, h, :], 512)), [self.y], [b])
            r0 = (self.g * 6 + 3 * self.tr + h) * 64
            P.dma("sp", self.Y, self.Y[r0:r0 + 64, c0:c0 + 512], b, b[:], nodep_out=True)

    def topc_ap(self, qt):
        return self.topc[(qt - 3) // 4]

    def cmp_bias(self, cc, h, mixed):
        if cc == 0:
            return (self.kval[:, 12:13], self.kval) if mixed else (self.cfc[:, h:h + 1], self.cfc)
        return None if mixed else (self.cf[:, h:h + 1], self.cf)

    def sel_bias(self, kc, h, near):
        if kc < 12:
            return (self.kval[:, kc:kc + 1], self.kval) if near else (self.vbf[:, kc, h:h + 1], self.vbf)
        return None if near else (self.cf[:, h:h + 1], self.cf)

    def win_bias(self, c):
        return (self.kval[:, c:c + 1], self.kval) if c < 12 else None

    def setup_tables(self):
        FAttn.setup_tables(self)
        P = self.P
        for c in range(12):
            P.op("dve", lambda e, c=c: e.tensor_scalar(out=self.vbf[:, c, :], in0=self.cf[:, 0:3],
                                                       scalar1=self.kval[:, c:c + 1], scalar2=None, op0=ALU.add),
                 [self.cf, self.kval], [self.vbf])
        P.op("dve", lambda e: e.tensor_scalar(out=self.cfc[:], in0=self.cf[:], scalar1=self.kval[:, 12:13], scalar2=None,
                                              op0=ALU.add), [self.cf, self.kval], [self.cfc])

    def run(self):
        P = self.P
        st = self.st
        self.gtr = [P.sb("gtr%d" % i, [64, 512], F32) for i in range(2)]
        self.kval = P.sb("kval", [128, 16], F32)
        self.vbf = P.sb("vbf", [128, 12, 3], F32)
        self.cfc = P.sb("cfc", [128, 6], F32)
        P.dma("sp", self.kval, self.kval[:], self.kvalD, self.kvalD[:])
        P.dma("sp", self.hgs, self.hgs[:], self.hg, self.hg[:])
        P.op("pool", lambda e: e.memset(self.ones[:], 1.0), [], [self.ones])
        P.op("pool", lambda e: e.memset(self.epsT[:], 1e-6), [], [self.epsT])
        P.op("pool", lambda e: e.memset(self.vsaug[:], 1.0), [], [self.vsaug])
        P.op("pool", lambda e: e.memset(self.vwaug[:], 1.0), [], [self.vwaug])
        P.op("pool", lambda e: e.memset(self.vcaug[:], 1.0), [], [self.vcaug])
        s = self.rot("st", st)
        P.dma("sp", s, s[:, 0:128], self.identD, self.identD[:])
        P.op("pool", lambda e, s=s: e.tensor_copy(out=self.ident[:], in_=s[:, 0:128]), [s], [self.ident])
        s = self.rot("st", st)
        P.dma("sp", s, s[:, 0:512], self.ovlD, self.ovlD[:].rearrange("p c j -> p (c j)"))
        P.op("pool", lambda e, s=s: e.tensor_copy(out=self.ovl[:].rearrange("p c j -> p (c j)"), in_=s[:, 0:512]),
             [s], [self.ovl])
        for g in range(2):
            self.set_combo(g, 0)
            self.setup_kv()
            for tr in range(2):
                self.set_combo(g, tr)
                self.setup_tables()
                for qt in QTILES:
                    self.qtile(qt)


def build_fused2():
    nc = bass.Bass("TRN2", target_bir_lowering=False)
    P = Prog(nc)
    dr = fused_dram(P, r3=True)
    I = lambda n, s: P.dram(n, s, F32, kind="Internal")
    dr["KV"] = I("KVp", [768, KPAD + SEQ])
    dr["KVwin"] = I("KVwin", [768, SEQ])
    dr["Xown"] = I("Xown", [1024, NTOK])
    dr["Qown"] = I("Qown", [768, NTOK])
    dr["Gown"] = I("Gown", [36, NTOK])
    dr["Yown"] = I("Yown", [768, NTOK])
    dr["topc"] = P.dram("topc4", [4, 128, 1024], F32, kind="ExternalInput")
    dr["kval"] = P.dram("kval", [128, 16], F32, kind="ExternalInput")
    dr["out"] = P.dram("xT_own", [1024, NTOK], F32, kind="ExternalOutput")
    dr["kd"] = (nc.partition_id() % 4) * 512
    P.push_scope()
    FTok2(P, "L1", dr).run()
    P.pop_scope()
    extract_own(P, dr)
    for j in range(2):
        P.push_scope()
        FAttn2(P, dr, j).run()
        P.pop_scope()
        P.push_scope()
        FTok2(P, "L3" if j == 0 else "L5", dr).run()
        P.pop_scope()
    P.wait_all("sp", [dr["out"]])
    P.finalize()
    return nc, P


from concourse.bass_utils import run_bass_kernel_spmd


def attn_hg(inp, jlayer):
    hg = np.zeros((128, 8), np.float32)
    hg[:, 0] = np.tile(inp["b_q_gain"][jlayer], 2)
    for i in range(3):
        hg[:, 1 + i] = np.tile(inp["k_gain"][i], 2)
    return hg


def kernel(**inputs):
    inp = {k: np.ascontiguousarray(np.asarray(v, dtype=np.float32)) for k, v in inputs.items()}
    common = dict(host_consts())
    del common["topc"]
    common.update(gains=host_gains(inp), hg=host_hg(inp), convw=host_convw(inp), mem_w_kv=inp["mem_w_kv"],
                  w_out=inp["w_out"], w_up=inp["w_up"], w_down=inp["w_down"], b_w_in=inp["b_w_in"],
                  a_w_in=inp["a_w_in"], w_kv=inp["w_kv_shared"], ahg0=attn_hg(inp, 0), ahg1=attn_hg(inp, 1),
                  pos=np.ascontiguousarray(inp["cmp_pos"].transpose(0, 2, 1)), cmp_w1=inp["cmp_w1"],
                  cmp_w2=inp["cmp_w2"], rel_bias=inp["rel_bias"])
    per_b = []
    for b in range(2):
        xin = np.zeros((1024, XPAD), np.float32)
        xin[:, HALO:] = inp["x"][b].T
        per_b.append(dict(xin=xin, memT=np.ascontiguousarray(inp["mem"][b].T)))
    per_k = []
    for k in range(4):
        tc, kval = host_core_tables(k)
        per_k.append(dict(topc4=tc, kval=kval))
    maps = []
    for c in range(8):
        m = dict(common)
        m.update(per_b[c // 4])
        m.update(per_k[c % 4])
        maps.append(m)
    nc = build_fused2()[0]
    r = run_bass_kernel_spmd(nc, maps, core_ids=list(range(8))).results
    out = np.zeros((2, SEQ, 1024), np.float32)
    for c in range(8):
        b, k = c // 4, c % 4
        o = r[c]["xT_own"]
        for j in range(4):
            t0 = 512 * (4 * j + k)
            out[b, t0:t0 + 512, :] = o[:, j * 512:(j + 1) * 512].T
    return out
```

```python
import numpy as np
import concourse.bass as bass
import concourse.mybir as mybir
from contextlib import ExitStack

F32 = mybir.dt.float32
BF16 = mybir.dt.bfloat16
AF = mybir.ActivationFunctionType
ALU = mybir.AluOpType
AX = mybir.AxisListType

ENGS = ("pe", "act", "dve", "pool", "sp")


class T:
    __slots__ = ("h", "name", "lw", "rd", "dsem", "scoped")

    def __init__(self, h, name):
        self.h = h
        self.name = name
        self.lw = None
        self.rd = []
        self.dsem = None
        self.scoped = False

    def __getitem__(self, k):
        return self.h[k]


class Prog:
    def __init__(self, nc):
        self.nc = nc
        self.es = ExitStack()
        self.ins = {e: [] for e in ENGS}
        self.nsem = 0
        self.dsems = []
        self.free_dsems = []
        self.scope_dsems = []
        self.marked = set()
        self.rw = {e: {} for e in ENGS}
        self.cur = self.es
        self.uid = 0
        self.last_tok = {e: None for e in ENGS}
        self.epoch = 0
        self.ep_of = {}

    def _prune(self, eng, deps):
        out = []
        w = self.rw[eng]
        for d in deps:
            key = (d[0], d[1]); val = d[2]
            if w.get(key, -1) >= val:
                continue
            w[key] = val
            out.append(d)
        return out

    def sb(self, name, shape, dt):
        self.uid += 1
        name = "%s_%d" % (name, self.uid)
        t = T(self.cur.enter_context(self.nc.sbuf_tensor(name, list(shape), dt)), name)
        t.scoped = self.cur is not self.es
        return t

    def ps(self, name, shape, dt=F32):
        self.uid += 1
        name = "%s_%d" % (name, self.uid)
        return T(self.cur.enter_context(self.nc.psum_tensor(name, list(shape), dt)), name)

    def push_scope(self):
        self.cur = ExitStack()

    def pop_scope(self):
        self.barrier()
        self.cur.close()
        self.cur = self.es
        self.free_dsems.extend(self.scope_dsems)
        self.scope_dsems = []

    def barrier(self):
        for e in ENGS:
            deps = [t for e2, t in self.last_tok.items() if e2 != e and t is not None]
            deps += [("d", s, c[1]) for s, c in enumerate(self.dsems) if c[1] > 0]
            deps = self._prune(e, deps)
            self.ins[e].append(dict(deps=deps, fn=None, dma=None))

    def dram(self, name, shape, dt, kind="Internal"):
        return T(self.nc.dram_tensor(name, list(shape), dt, kind=kind), name)

    def _new_dsem(self, scoped=False):
        if scoped and self.free_dsems:
            i = self.free_dsems.pop()
        else:
            h = self.es.enter_context(self.nc.semaphore("d%d" % len(self.dsems)))
            self.dsems.append([h, 0])
            i = len(self.dsems) - 1
        if scoped:
            self.scope_dsems.append(i)
        return i

    def _deps(self, eng, reads, writes):
        deps = []
        for t in reads:
            if t.lw is not None:
                deps.append(t.lw)
        for t in writes:
            if t.lw is not None:
                deps.append(t.lw)
            deps.extend(t.rd)
        out = []
        for d in deps:
            if d[0] == "e":
                if d[1] == eng:
                    continue
            out.append(d)
        if eng != "pe":
            for t in reads:
                if t.lw is not None and t.lw[0] == "e" and t.lw[1] == eng:
                    out.append(t.lw)
        return out

    def op(self, eng, fn, reads=(), writes=()):
        deps = self._prune(eng, self._deps(eng, reads, writes))
        idx = len(self.ins[eng])
        self.ins[eng].append(dict(deps=deps, fn=fn, dma=None))
        tok = ("e", eng, idx)
        self.last_tok[eng] = tok
        self.ep_of[(eng, idx)] = self.epoch
        for t in reads:
            t.rd = [r for r in t.rd if not (r[0] == "e" and r[1] == eng)]
            t.rd.append(tok)
        for t in writes:
            t.lw = tok
            t.rd = []
        return tok

    def dma(self, eng, out_t, out_ap, in_t, in_ap, nodep_out=False, **kw):
        reads = [in_t] if in_t is not None else []
        writes = [out_t] if out_t is not None else []
        deps = []
        for t in reads:
            if t.lw is not None:
                deps.append(t.lw)
        for t in writes:
            if nodep_out:
                continue
            if t.lw is not None:
                deps.append(t.lw)
            deps.extend(t.rd)
        owner = out_t if (out_t is not None and out_t.dsem is not None) else None
        if owner is None:
            owner = out_t if out_t is not None else in_t
        if owner.dsem is None:
            owner.dsem = self._new_dsem(scoped=getattr(owner, "scoped", False))
        s = owner.dsem
        self.dsems[s][1] += 16
        tok = ("d", s, self.dsems[s][1])
        deps = self._prune(eng, deps)
        self.ins[eng].append(dict(deps=deps, fn=None, dma=(out_ap, in_ap, s, kw)))
        for t in reads:
            t.rd.append(tok)
        for t in writes:
            t.lw = tok
            t.rd = []
        return tok

    def wait_all(self, eng, tiles):
        deps = []
        for t in tiles:
            if t.lw is not None:
                deps.append(t.lw)
            deps.extend(t.rd)
        deps = self._prune(eng, deps)
        self.ins[eng].append(dict(deps=deps, fn=None, dma=None))

    def finalize(self):
        nc = self.nc
        for e in ENGS:
            for rec in self.ins[e]:
                for d in rec["deps"]:
                    if d[0] == "e":
                        self.marked.add((d[1], d[2]))
        count_at = {}
        for e in ENGS:
            c = {}
            for i, rec in enumerate(self.ins[e]):
                if (e, i) in self.marked:
                    ep = self.ep_of[(e, i)]
                    c[ep] = c.get(ep, 0) + 1
                    count_at[(e, i)] = c[ep]
        esem = {(e, ep): self.es.enter_context(nc.semaphore("s_%s_%d" % (e, ep)))
                for e in ENGS for ep in range(self.epoch + 1)}
        engobj = {"pe": "tensor", "act": "scalar", "dve": "vector", "pool": "gpsimd", "sp": "sync"}
        block = self.es.enter_context(nc.Block())
        stats = {}

        def make_body(e):
            def body(eng):
                waited = {}
                nwait = 0
                for i, rec in enumerate(self.ins[e]):
                    need = {}
                    for d in rec["deps"]:
                        if d[0] == "e":
                            key = ("e", (d[1], self.ep_of[(d[1], d[2])])); val = count_at[(d[1], d[2])]
                        else:
                            key = ("d", d[1]); val = d[2]
                        if waited.get(key, 0) >= val:
                            continue
                        if need.get(key, 0) < val:
                            need[key] = val
                    for key, val in need.items():
                        h = esem[key[1]] if key[0] == "e" else self.dsems[key[1]][0]
                        eng.wait_ge(h, val)
                        waited[key] = val
                        nwait += 1
                    if rec.get("cc") is not None:
                        rec["cc"](eng).then_inc(self.dsems[rec["ccsem"]][0], 16)
                    elif rec["dma"] is not None:
                        out_ap, in_ap, s, kw = rec["dma"]
                        try:
                            eng.dma_start(out=out_ap, in_=in_ap, **kw).then_inc(self.dsems[s][0], 16)
                        except Exception:
                            print("DMA FAILED:", e, "out=", out_ap, "in=", in_ap)
                            raise
                    elif rec["fn"] is not None:
                        ins = rec["fn"](eng)
                        if (e, i) in self.marked:
                            ins.then_inc(esem[(e, self.ep_of[(e, i)])], 1)
                stats[e] = (len(self.ins[e]), nwait)
            return body

        for e in ENGS:
            if self.ins[e]:
                getattr(block, engobj[e])(make_body(e))
        self.stats = stats
        self.es.close()


NTOK = 2048
HALO = 4
NCOL = NTOK + HALO
RANGES = [(0, 1028), (1028, 1024)]
EPS = 1e-6


def tiles_of(n):
    out = []
    o = 0
    while o < n:
        w = min(512, n - o)
        out.append((o, w))
        o += w
    return out


class TokStage:
    def __init__(self, P, mode):
        self.P = P
        self.mode = mode
        nc = P.nc
        D = lambda n, s: P.dram(n, s, F32, kind="ExternalInput")
        O = lambda n, s: P.dram(n, s, F32, kind="ExternalOutput")
        self.xT = D("xT", [1024, NCOL])
        self.memT = D("memT", [1024, 256])
        self.gains = D("gains", [128, 8 * 16])
        self.hg = D("hg", [128, 16])
        self.flag = D("flag", [128, 1])
        self.mem_w_kv = D("mem_w_kv", [4, 1024, 512])
        self.w_out = D("w_out", [4, 1024, 1024])
        self.w_up = D("w_up", [4, 1024, 4096])
        self.w_down = D("w_down", [4, 4096, 1024])
        self.b_w_in = D("b_w_in", [2, 1024, 1060])
        if mode == "L1":
            self.a_w_in = D("a_w_in", [2, 1024, 2560])
            self.convw = D("convw", [128, 2 * 6 * 3])
            self.w_kv = D("w_kv", [1024, 768])
            self.kvT_o = O("kvT_o", [768, NCOL])
        else:
            self.ytokT = D("ytokT", [768, NCOL])
        self.xT_o = O("xT_o", [1024, NCOL])
        if mode != "L5":
            self.qT_o = O("qT_o", [768, NCOL])
            self.gT_o = O("gT_o", [36, NCOL])

        self.alloc()

    def alloc(self):
        P = self.P
        sb = P.sb
        self.x = sb("x", [128, 8, NCOL], F32)
        self.h = sb("h", [128, 8, 1028], BF16)
        self.y = sb("y", [128, 8, 1028], BF16)
        self.act = sb("act", [128, 8, 1028], BF16)
        self.u = sb("u", [128, 2, 1028], F32)
        self.v = sb("v", [128, 2, 1030], F32)
        self.qm = sb("qm", [128, 2, 1028], F32)
        self.carry = sb("carry", [128, 6, 2], F32)
        self.rstd = sb("rstd", [128, 512], F32)
        self.sq = [sb("sq%d" % i, [128, 512], BF16) for i in range(2)]
        self.tmp = [sb("tmp%d" % i, [128, 512], F32) for i in range(3)]
        self.ost = [sb("ost%d" % i, [128, 512], F32) for i in range(2)]
        self.wb = [sb("wb%d" % i, [128, 8, 256], BF16) for i in range(4)]
        self.memh = sb("memh", [128, 8, 256], BF16)
        self.kraw = sb("kraw", [128, 2, 256], F32)
        self.memk = sb("memk", [128, 2, 256], BF16)
        self.vaug = sb("vaug", [128, 2, 4, 128], BF16)
        self.qn = sb("qn", [128, 512], BF16)
        self.eT = [sb("eT%d" % i, [128, 512], BF16) for i in range(2)]
        self.rd = sb("rd", [64, 512], F32)
        self.g = sb("g", [128, 8 * 16], F32)
        self.hgs = sb("hgs", [128, 16], F32)
        self.flg = sb("flg", [128, 1], F32)
        self.cw = sb("cw", [128, 36], F32)
        self.ones = sb("ones", [128, 128], BF16)
        self.bd = sb("bd", [128, 128], BF16)
        self.epsT = sb("epsT", [128, 1], F32)
        self.pp = [P.ps("pp%d" % i, [128, 512]) for i in range(4)]
        self.pl = [P.ps("pl%d" % i, [128, 512]) for i in range(2)]
        self.po = P.ps("po", [128, 512])
        self.pn = P.ps("pn", [128, 512])
        self.cnt = dict(wf=0, wb=0, pp=0, pl=0, sq=0, tmp=0, ost=0, eT=0)

    def rot(self, name, lst):
        i = self.cnt[name]
        self.cnt[name] += 1
        return lst[i % len(lst)]

    def stream(self, wT, w_ap, ncols):
        P = self.P
        wb = self.rot("wb", self.wb)
        P.dma("pool", wb, wb[:, :, 0:ncols], wT, w_ap.rearrange("(c p) n -> p c n", p=128))
        return wb

    def proj(self, wb, sub, msz, rhs_t, rhs_fn, tiles, consumer):
        P = self.P
        for (t0, w) in tiles:
            ps = self.rot("pp", self.pp)
            for kc in range(8):
                P.op("pe", lambda e, ps=ps, kc=kc, t0=t0, w=w: e.matmul(
                    ps[0:msz, 0:w], lhsT=wb[:, kc, sub * 128: sub * 128 + msz], rhs=rhs_fn(kc, t0, w),
                    start=(kc == 0), stop=(kc == 7)), [wb, rhs_t], [ps])
            consumer(ps, t0, w)

    def setup(self):
        P = self.P
        P.dma("sp", self.g, self.g[:], self.gains, self.gains[:])
        P.dma("sp", self.hgs, self.hgs[:], self.hg, self.hg[:])
        P.dma("sp", self.flg, self.flg[:], self.flag, self.flag[:])
        if self.mode == "L1":
            P.dma("sp", self.cw, self.cw[:], self.convw, self.convw[:])
        for c in range(8):
            P.dma("sp", self.x, self.x[:, c, :], self.xT, self.xT[c * 128:(c + 1) * 128, :], nodep_out=True)
            P.dma("sp", self.u, self.memx(c, 0, 256), self.memT, self.memT[c * 128:(c + 1) * 128, :], nodep_out=True)
        P.op("pool", lambda e: e.memset(self.ones[:], 1.0), [], [self.ones])
        P.op("pool", lambda e: e.memset(self.bd[:], 0.0), [], [self.bd])
        P.op("pool", lambda e: e.memset(self.bd[0:64, 0:64], 1.0), [], [self.bd])
        P.op("pool", lambda e: e.memset(self.bd[64:128, 64:128], 1.0), [], [self.bd])
        P.op("pool", lambda e: e.memset(self.epsT[:], EPS), [], [self.epsT])
        P.op("pool", lambda e: e.memset(self.vaug[:], 1.0), [], [self.vaug])
        P.op("pool", lambda e: e.memset(self.carry[:], 0.0), [], [self.carry])
        self.norm(self.u, self.memx, 256, [(0, 256)], 15, self.memh, 0)

    def gcol(self, which, c):
        return self.g[:, which * 8 + c: which * 8 + c + 1]

    def memx(self, c, t0, w):
        return self.u[:, c // 4, (c % 4) * 256 + t0: (c % 4) * 256 + t0 + w]

    def xs(self, c0):
        return lambda c, t0, w: self.x[:, c, c0 + t0: c0 + t0 + w]

    def norm(self, src, src_fn, n, tiles, which, dst, d0):
        P = self.P
        for (t0, w) in tiles:
            for c in range(8):
                sq = self.rot("sq", self.sq)
                P.op("act", lambda e, sq=sq, c=c, t0=t0, w=w: e.activation(
                    out=sq[:, 0:w], in_=src_fn(c, t0, w), func=AF.Square), [src], [sq])
                P.op("pe", lambda e, sq=sq, c=c, w=w: e.matmul(
                    self.pn[:, 0:w], lhsT=self.ones[:], rhs=sq[:, 0:w], start=(c == 0), stop=(c == 7)),
                    [self.ones, sq], [self.pn])
            P.op("act", lambda e, w=w: e.activation(out=self.rstd[:, 0:w], in_=self.pn[:, 0:w], func=AF.Ln,
                                                    bias=self.epsT[:], scale=1.0 / 1024.0),
                 [self.pn, self.epsT], [self.rstd])
            P.op("act", lambda e, w=w: e.activation(out=self.rstd[:, 0:w], in_=self.rstd[:, 0:w], func=AF.Exp, scale=-0.5),
                 [self.rstd], [self.rstd])
            for c in range(8):
                eng = "dve"
                P.op(eng, lambda e, c=c, t0=t0, w=w: e.scalar_tensor_tensor(
                    out=dst[:, c, d0 + t0: d0 + t0 + w], in0=src_fn(c, t0, w),
                    scalar=self.gcol(which, c), in1=self.rstd[:, 0:w], op0=ALU.mult, op1=ALU.mult),
                    [src, self.g, self.rstd], [dst])

    def headnorm(self, src_t, src_ap_fn, w, gain_ap, dst_t, dst_ap):
        P = self.P
        sq = self.rot("sq", self.sq)
        P.op("act", lambda e, sq=sq: e.activation(out=sq[:, 0:w], in_=src_ap_fn(), func=AF.Square), [src_t], [sq])
        P.op("pe", lambda e, sq=sq: e.matmul(self.pn[:, 0:w], lhsT=self.bd[:], rhs=sq[:, 0:w], start=True, stop=True),
             [self.bd, sq], [self.pn])
        P.op("act", lambda e: e.activation(out=self.rstd[:, 0:w], in_=self.pn[:, 0:w], func=AF.Ln,
                                           bias=self.epsT[:], scale=1.0 / 64.0), [self.pn, self.epsT], [self.rstd])
        P.op("act", lambda e: e.activation(out=self.rstd[:, 0:w], in_=self.rstd[:, 0:w], func=AF.Exp, scale=-0.5),
             [self.rstd], [self.rstd])
        P.op("dve", lambda e: e.scalar_tensor_tensor(out=dst_ap, in0=src_ap_fn(), scalar=gain_ap,
                                                     in1=self.rstd[:, 0:w], op0=ALU.mult, op1=ALU.mult),
             [src_t, self.hgs, self.rstd], [dst_t])

    def mem_kv(self, l):
        P = self.P
        wb = self.stream(self.mem_w_kv, self.mem_w_kv[l, :, 0:256], 256)
        for sub in range(2):
            def cons(ps, t0, w, sub=sub):
                P.op("act", lambda e: e.activation(out=self.kraw[:, sub, 0:256], in_=ps[:, 0:256], func=AF.Copy),
                     [ps], [self.kraw])
            self.proj(wb, sub, 128, self.memh, lambda kc, t0, w: self.memh[:, kc, t0:t0 + w], [(0, 256)], cons)
            self.headnorm(self.kraw, lambda sub=sub: self.kraw[:, sub, 0:256], 256,
                          self.hgs[:, 4 + l: 5 + l], self.memk, self.memk[:, sub, 0:256])
        wb = self.stream(self.mem_w_kv, self.mem_w_kv[l, :, 256:512], 256)
        for mc in range(2):
            ps = self.rot("pp", self.pp)
            for kc in range(8):
                P.op("pe", lambda e, ps=ps, kc=kc, mc=mc: e.matmul(
                    ps[:, 0:256], lhsT=self.memh[:, kc, mc * 128:(mc + 1) * 128], rhs=wb[:, kc, 0:256],
                    start=(kc == 0), stop=(kc == 7)), [self.memh, wb], [ps])
            for hh in range(4):
                P.op("act", lambda e, ps=ps, mc=mc, hh=hh: e.activation(
                    out=self.vaug[:, mc, hh, 0:64], in_=ps[:, hh * 64:(hh + 1) * 64], func=AF.Copy), [ps], [self.vaug])

    def mem_attn(self, l, n, tiles):
        P = self.P
        for (t0, w) in tiles:
            for j in range(2):
                self.headnorm(self.qm, lambda j=j, t0=t0, w=w: self.qm[:, j, t0:t0 + w], w,
                              self.hgs[:, l: l + 1], self.qn, self.qn[:, 0:w])
                for hh in range(2):
                    b0 = 64 * hh
                    hd = 2 * j + hh
                    ets = []
                    for mc in range(2):
                        pl = self.rot("pl", self.pl)
                        P.op("pe", lambda e, pl=pl, mc=mc, b0=b0, j=j, w=w: e.matmul(
                            pl[:, 0:w], lhsT=self.memk[b0:b0 + 64, j, mc * 128:(mc + 1) * 128],
                            rhs=self.qn[b0:b0 + 64, 0:w], start=True, stop=True), [self.memk, self.qn], [pl])
                        eT = self.rot("eT", self.eT)
                        P.op("act", lambda e, pl=pl, eT=eT, w=w: e.activation(
                            out=eT[:, 0:w], in_=pl[:, 0:w], func=AF.Exp, scale=0.125), [pl], [eT])
                        ets.append(eT)
                    for mc in range(2):
                        P.op("pe", lambda e, mc=mc, hd=hd, w=w, eT=ets[mc]: e.matmul(
                            self.po[:, 0:w], lhsT=self.vaug[:, mc, hd, :], rhs=eT[:, 0:w],
                            start=(mc == 0), stop=(mc == 1)), [self.vaug, ets[mc]], [self.po])
                    P.op("act", lambda e, w=w: e.activation(out=self.rd[0:64, 0:w], in_=self.po[64:128, 0:w], func=AF.Ln),
                         [self.po], [self.rd])
                    P.op("act", lambda e, w=w: e.activation(out=self.rd[0:64, 0:w], in_=self.rd[0:64, 0:w], func=AF.Exp,
                                                            scale=-1.0), [self.rd], [self.rd])
                    P.op("dve", lambda e, w=w, b0=b0, j=j, t0=t0: e.tensor_tensor(
                        out=self.y[b0:b0 + 64, 6 + j, t0:t0 + w], in0=self.po[0:64, 0:w], in1=self.rd[0:64, 0:w],
                        op=ALU.mult), [self.po, self.rd], [self.y])

    def add_into_x(self, c0):
        P = self.P

        def mk(ch):
            def cons(ps, t0, w):
                P.op("dve", lambda e: e.tensor_tensor(
                    out=self.x[:, ch, c0 + t0: c0 + t0 + w], in0=ps[:, 0:w], in1=self.x[:, ch, c0 + t0: c0 + t0 + w],
                    op=ALU.add), [ps, self.x], [self.x])
            return cons
        return mk

    def out_proj(self, l, c0, n, tiles):
        mk = self.add_into_x(c0)
        for blk in range(4):
            wb = self.stream(self.w_out, self.w_out[l, :, blk * 256:(blk + 1) * 256], 256)
            for sub in range(2):
                self.proj(wb, sub, 128, self.y, lambda kc, t0, w: self.y[:, kc, t0:t0 + w], tiles, mk(blk * 2 + sub))

    def mlp(self, l, c0, n, tiles):
        P = self.P
        self.norm(self.x, self.xs(c0), n, tiles, 4 + l, self.h, 0)
        mk = self.add_into_x(c0)
        for sg in range(4):
            for blk in range(4):
                wb = self.stream(self.w_up, self.w_up[l, :, sg * 1024 + blk * 256: sg * 1024 + (blk + 1) * 256], 256)
                for sub in range(2):
                    def cons(ps, t0, w, fc=blk * 2 + sub):
                        tmp = self.rot("tmp", self.tmp)
                        P.op("act", lambda e: e.activation(out=tmp[:, 0:w], in_=ps[:, 0:w], func=AF.Relu), [ps], [tmp])
                        P.op("dve", lambda e: e.tensor_tensor(out=self.act[:, fc, t0:t0 + w], in0=tmp[:, 0:w],
                                                              in1=tmp[:, 0:w], op=ALU.mult), [tmp], [self.act])
                    self.proj(wb, sub, 128, self.h, lambda kc, t0, w: self.h[:, kc, t0:t0 + w], tiles, cons)
            for blk in range(4):
                wb = self.stream(self.w_down, self.w_down[l, sg * 1024:(sg + 1) * 1024, blk * 256:(blk + 1) * 256], 256)
                for sub in range(2):
                    self.proj(wb, sub, 128, self.act, lambda kc, t0, w: self.act[:, kc, t0:t0 + w], tiles,
                              mk(blk * 2 + sub))

    def mix_a(self, l, ri, c0, n, tiles):
        P = self.P
        W = self.a_w_in
        hrhs = lambda kc, t0, w: self.h[:, kc, t0:t0 + w]
        for i in range(3):
            wb = self.stream(W, W[l, :, 1536 + 256 * i: 1536 + 256 * (i + 1)], 256)
            for sub in range(2):
                def cons(ps, t0, w, sub=sub):
                    P.op("act", lambda e: e.activation(out=self.u[:, sub, t0:t0 + w], in_=ps[:, 0:w], func=AF.Copy),
                         [ps], [self.u])
                self.proj(wb, sub, 128, self.h, hrhs, tiles, cons)
            wb = self.stream(W, W[l, :, 768 + 256 * i: 768 + 256 * (i + 1)], 256)
            for sub in range(2):
                def cons(ps, t0, w, sub=sub):
                    P.op("dve", lambda e: e.tensor_tensor(out=self.v[:, sub, 2 + t0: 2 + t0 + w], in0=ps[:, 0:w],
                                                          in1=self.u[:, sub, t0:t0 + w], op=ALU.mult),
                         [ps, self.u], [self.v])
                self.proj(wb, sub, 128, self.h, hrhs, tiles, cons)
            for sub in range(2):
                ch = 2 * i + sub
                if ri == 0:
                    P.op("pool", lambda e, sub=sub: e.memset(self.v[:, sub, 0:2], 0.0), [], [self.v])
                    self.halo_fix(sub)
                    P.op("dve", lambda e, sub=sub, ch=ch: e.tensor_copy(out=self.carry[:, ch, :],
                                                                        in_=self.v[:, sub, n:n + 2]),
                         [self.v], [self.carry])
                else:
                    P.op("dve", lambda e, sub=sub, ch=ch: e.tensor_copy(out=self.v[:, sub, 0:2],
                                                                        in_=self.carry[:, ch, :]),
                         [self.carry], [self.v])
                cwc = lambda k, ch=ch: self.cw[:, l * 18 + ch * 3 + k: l * 18 + ch * 3 + k + 1]
                P.op("dve", lambda e, sub=sub, cwc=cwc: e.tensor_scalar(
                    out=self.u[:, sub, 0:n], in0=self.v[:, sub, 0:n], scalar1=cwc(0), scalar2=None, op0=ALU.mult),
                    [self.v, self.cw], [self.u])
                for k in (1, 2):
                    P.op("dve", lambda e, sub=sub, cwc=cwc, k=k: e.scalar_tensor_tensor(
                        out=self.u[:, sub, 0:n], in0=self.v[:, sub, k:k + n], scalar=cwc(k), in1=self.u[:, sub, 0:n],
                        op0=ALU.mult, op1=ALU.add), [self.v, self.cw, self.u], [self.u])
            wb = self.stream(W, W[l, :, 256 * i: 256 * (i + 1)], 256)
            for sub in range(2):
                def cons(ps, t0, w, sub=sub, ch=2 * i + sub):
                    P.op("dve", lambda e: e.tensor_tensor(out=self.y[:, ch, t0:t0 + w], in0=ps[:, 0:w],
                                                          in1=self.u[:, sub, t0:t0 + w], op=ALU.mult),
                         [ps, self.u], [self.y])
                self.proj(wb, sub, 128, self.h, hrhs, tiles, cons)
        wb = self.stream(W, W[l, :, 2304:2560], 256)
        self.qmem_proj(wb, tiles)

    def halo_fix(self, sub):
        self.P.op("dve", lambda e: e.tensor_scalar(
            out=self.v[:, sub, 2:2 + HALO], in0=self.v[:, sub, 2:2 + HALO], scalar1=self.flg[:, 0:1],
            scalar2=None, op0=ALU.mult), [self.v, self.flg], [self.v])

    def store_tile(self, dstT, r, msz, c0, t0, w, ost):
        self.P.dma("sp", dstT, dstT[r:r + msz, c0 + t0: c0 + t0 + w], ost, ost[0:msz, 0:w], nodep_out=True)

    def ytok_src(self, ch, c0, t0, w):
        return self.ytokT[ch * 128:(ch + 1) * 128, c0 + t0: c0 + t0 + w]

    def qmem_proj(self, wb, tiles):
        P = self.P
        for sub in range(2):
            def cons(ps, t0, w, sub=sub):
                P.op("act", lambda e: e.activation(out=self.qm[:, sub, t0:t0 + w], in_=ps[:, 0:w], func=AF.Copy),
                     [ps], [self.qm])
            self.proj(wb, sub, 128, self.h, lambda kc, t0, w: self.h[:, kc, t0:t0 + w], tiles, cons)

    def store_proj(self, wT, w_ap, ncols, dstT, row0, c0, tiles, func=AF.Copy):
        P = self.P
        wb = self.stream(wT, w_ap, ncols)
        nsub = (ncols + 127) // 128
        for sub in range(nsub):
            msz = min(128, ncols - sub * 128)

            def cons(ps, t0, w, sub=sub, msz=msz):
                ost = self.rot("ost", self.ost)
                P.op("act", lambda e: e.activation(out=ost[0:msz, 0:w], in_=ps[0:msz, 0:w], func=func), [ps], [ost])
                r = row0 + sub * 128
                self.store_tile(dstT, r, msz, c0, t0, w, ost)
            self.proj(wb, sub, msz, self.h, lambda kc, t0, w: self.h[:, kc, t0:t0 + w], tiles, cons)

    def pre_b(self, j, c0, n, tiles):
        W = self.b_w_in
        self.norm(self.x, self.xs(c0), n, tiles, 2 + j, self.h, 0)
        for i in range(3):
            self.store_proj(W, W[j, :, 256 * i:256 * (i + 1)], 256, self.qT_o, 256 * i, c0, tiles)
        self.store_proj(W, W[j, :, 768:804], 36, self.gT_o, 0, c0, tiles, func=AF.Sigmoid)

    def post_b(self, j, c0, n, tiles):
        P = self.P
        W = self.b_w_in
        l = 2 + j
        self.norm(self.x, self.xs(c0), n, tiles, l, self.h, 0)
        wb = self.stream(W, W[j, :, 804:1060], 256)
        self.qmem_proj(wb, tiles)
        for ch in range(6):
            for (t0, w) in tiles:
                tmp = self.rot("tmp", self.tmp)
                P.dma("sp", tmp, tmp[:, 0:w], self.ytokT, self.ytok_src(ch, c0, t0, w))
                P.op("act", lambda e, tmp=tmp, ch=ch, t0=t0, w=w: e.activation(out=self.y[:, ch, t0:t0 + w],
                                                                               in_=tmp[:, 0:w], func=AF.Copy), [tmp], [self.y])
        self.mem_attn(l, n, tiles)
        self.out_proj(l, c0, n, tiles)
        self.mlp(l, c0, n, tiles)

    def store_x(self):
        P = self.P
        for c in range(8):
            P.dma("sp", self.xT_o, self.xT_o[c * 128:(c + 1) * 128, :], self.x, self.x[:, c, :], nodep_out=True)

    def build(self):
        P = self.P
        self.setup()
        if self.mode == "L1":
            for l in range(2):
                self.mem_kv(l)
                for ri, (c0, n) in enumerate(RANGES):
                    tiles = tiles_of(n)
                    self.norm(self.x, self.xs(c0), n, tiles, l, self.h, 0)
                    self.mix_a(l, ri, c0, n, tiles)
                    self.mem_attn(l, n, tiles)
                    self.out_proj(l, c0, n, tiles)
                    self.mlp(l, c0, n, tiles)
            for ri, (c0, n) in enumerate(RANGES):
                tiles = tiles_of(n)
                self.norm(self.x, self.xs(c0), n, tiles, 8, self.h, 0)
                for i in range(3):
                    self.store_proj(self.w_kv, self.w_kv[:, 256 * i:256 * (i + 1)], 256, self.kvT_o, 256 * i, c0, tiles)
                self.pre_b(0, c0, n, tiles)
            self.store_x()
            outs = [self.kvT_o, self.qT_o, self.gT_o, self.xT_o]
        elif self.mode == "L3":
            self.mem_kv(2)
            for ri, (c0, n) in enumerate(RANGES):
                tiles = tiles_of(n)
                self.post_b(0, c0, n, tiles)
            for ri, (c0, n) in enumerate(RANGES):
                tiles = tiles_of(n)
                self.pre_b(1, c0, n, tiles)
            self.store_x()
            outs = [self.qT_o, self.gT_o, self.xT_o]
        else:
            self.mem_kv(3)
            for ri, (c0, n) in enumerate(RANGES):
                tiles = tiles_of(n)
                self.post_b(1, c0, n, tiles)
            self.store_x()
            outs = [self.xT_o]
        P.wait_all("sp", outs)
        P.finalize()


def build_tok(mode):
    nc = bass.Bass("TRN2", target_bir_lowering=False)
    P = Prog(nc)
    st = TokStage(P, mode)
    st.build()
    return nc, P


def host_gains(inp):
    g = np.zeros((16, 1024), np.float32)
    g[0:4] = inp["mix_norm"]
    g[4:8] = inp["mlp_norm"]
    g[8] = inp["kv_norm"]
    g[15] = inp["mem_norm"]
    return np.ascontiguousarray(g.reshape(16, 8, 128).transpose(2, 0, 1).reshape(128, 128))


def host_hg(inp):
    hg = np.zeros((128, 16), np.float32)
    for l in range(4):
        hg[:, l] = np.tile(inp["mem_q_gain"][l], 2)
        hg[:, 4 + l] = np.tile(inp["mem_k_gain"][l], 2)
    return hg


def host_convw(inp):
    w = inp["a_conv_w"].reshape(2, 3, 6, 128)
    return np.ascontiguousarray(w.transpose(3, 0, 2, 1).reshape(128, 36))

import math

SEQ = 8192
QT = 512
NQT = SEQ // QT
DC, LC = 3040, 5104
DW, LW = 1023, 1536
DS, LS = 1407, 1920
LTOT = LC + LW + LS
LU, LTW, LTS = 3072, 1408, 1792
MASKV = -30000.0


def rel_bucket_np(d):
    n = np.maximum(d, 0)
    nf = np.maximum(n, 1).astype(np.float32)
    large = 16 + (np.log(nf / np.float32(16)) / np.float32(math.log(1024 / 16)) * np.float32(16)).astype(np.int32)
    large = np.minimum(large, 31)
    return np.where(n < 16, n, large)


def host_consts():
    c = {}
    oh = np.zeros((33, LTOT), np.float32)
    x = np.arange(LC); d = DC - x
    b = rel_bucket_np(d)
    oh[b[d >= 0], x[d >= 0]] = 1.0
    oh[32, x[d < 0]] = 1.0
    x = np.arange(LW); d = DW - x; ok = (d >= 0) & (d < 512)
    b = rel_bucket_np(d)
    oh[b[ok], LC + x[ok]] = 1.0
    oh[32, LC + x[~ok]] = 1.0
    x = np.arange(LS); d = DS - x; ok = d >= 0
    b = rel_bucket_np(d)
    oh[b[ok], LC + LW + x[ok]] = 1.0
    oh[32, LC + LW + x[~ok]] = 1.0
    c["OH"] = oh
    c["ident"] = np.eye(128, dtype=np.float32)
    k = np.arange(SEQ)
    r = 2 * ((k // 128) % 32) + ((k % 128) >= 64)
    ee = np.zeros((64, SEQ), np.float32)
    ee[r, k] = 30000.0
    c["EE"] = ee
    i = np.arange(512)[:, None] * 16
    s = np.arange(128)[None, :] * 64
    ov = np.clip(np.minimum(i + 32, s + 64) - np.maximum(i, s), 0, None).astype(np.float32) / 32.0
    ov[511] = 0.0
    c["ovl"] = np.ascontiguousarray(ov.reshape(4, 128, 128).transpose(1, 0, 2))
    tc = np.zeros((NQT, 128, 4, 2, 128), np.float32)
    j = np.arange(128)[None, :]
    for qt in range(NQT):
        for ss in range(4):
            t = qt * QT + 511 - (128 * ss + np.arange(128))
            back = (t // 64)[:, None] - j
            forced = (j == 0) | ((back >= 0) & (back < 2))
            A = ((back >= 0) & ~forced).astype(np.float32)
            B = np.where(back >= 0, np.where(forced, 1e4, 0.0), -1e30).astype(np.float32)
            tc[qt, :, ss, 0] = A
            tc[qt, :, ss, 1] = B
    c["topc"] = tc.reshape(NQT, 128, 1024)
    return c


class Attn:
    def __init__(self, P):
        self.P = P
        D = lambda n, s: P.dram(n, s, F32, kind="ExternalInput")
        self.qT = D("qT", [384, SEQ])
        self.gB = D("gB", [9, SEQ])
        self.kvT = D("kvT", [384, SEQ])
        self.vtok = D("vtok", [SEQ, 128])
        self.blk = D("blk", [2, 64, 32, 512])
        self.hg = D("hg", [128, 8])
        self.pos = D("pos", [2, 64, 32])
        self.w1 = D("w1", [2, 2048, 256])
        self.w2 = D("w2", [2, 256, 64])
        self.rb6 = D("rb6", [32, 6])
        self.OH = D("OH", [33, LTOT])
        self.identD = D("ident", [128, 128])
        self.EE = D("EE", [64, SEQ])
        self.ovlD = D("ovl", [128, 4, 128])
        self.topc = D("topc", [NQT, 128, 1024])
        self.yT = P.dram("yT", [192, SEQ], F32, kind="ExternalOutput")
        self.Rrow = P.dram("Rrow", [6, LTOT], F32, kind="Internal")
        self.alloc()

    def alloc(self):
        P = self.P
        sb = P.sb
        self.ksaug = sb("ksaug", [128, SEQ], BF16)
        self.kwT = sb("kwT", [64, SEQ], BF16)
        self.vsaug = sb("vsaug", [128, 64, 128], BF16)
        self.vwaug = sb("vwaug", [128, 64, 128], BF16)
        self.kcT = sb("kcT", [64, 512], BF16)
        self.vcaug = sb("vcaug", [128, 4, 128], BF16)
        self.Ubig = sb("Ubig", [128, 6 * LU], BF16)
        self.TTw = [sb("TTw%d" % h, [128, LTW], BF16) for h in range(3)]
        self.TTs = [sb("TTs%d" % h, [128, LTS], BF16) for h in range(3)]
        self.ident = sb("identb", [128, 128], BF16)
        self.ones = sb("ones", [128, 128], BF16)
        self.ovl = sb("ovlb", [128, 4, 128], BF16)
        self.hgs = sb("hgs", [128, 8], F32)
        self.cf = sb("cf", [128, 6], F32)
        self.tab = sb("tab", [33, 6], F32)
        self.epsT = sb("epsT", [128, 1], F32)
        self.tinyT = sb("tinyT", [128, 1], F32)
        self.st = [sb("st%d" % i, [128, 1024], F32) for i in range(2)]
        self.sq = [sb("sq%d" % i, [128, 512], BF16) for i in range(2)]
        self.rq = sb("rq", [128, 512], F32)
        self.qraw = sb("qraw", [64, 6, 512], F32)
        self.qn = sb("qn", [64, 6, 512], BF16)
        self.qaug = sb("qaug", [128, 3, 2, 512], BF16)
        self.pT = [sb("pT%d" % i, [128, 512], BF16) for i in range(6)]
        self.rdq = sb("rdq", [128, 4], F32)
        self.rdb = sb("rdb", [128, 512], F32)
        self.impacc = sb("impacc", [128, 512], F32)
        self.tcs = sb("tcs", [128, 1024], F32)
        self.sc = sb("sc", [128, 128], F32)
        self.sc2 = sb("sc2", [128, 128], F32)
        self.mx = sb("mx", [128, 16], F32)
        self.mneg = sb("mneg", [128, 128], BF16)
        self.gt = [sb("gt%d" % i, [64, 512], F32) for i in range(2)]
        self.rdt = sb("rdt", [64, 512], F32)
        self.wt = sb("wt", [64, 512], F32)
        self.tm = sb("tm", [64, 512], F32)
        self.y = sb("y", [64, 3, 512], F32)
        self.w1b = sb("w1b", [64, 4, 256], BF16)
        self.blkb = sb("blkb", [64, 4, 512], BF16)
        self.posT = sb("posT", [64, 32], F32)
        self.w2b = sb("w2b", [128, 2, 64], BF16)
        self.gx = self.rdb
        self.gy = self.impacc
        self.gT = [sb("gT%d" % i, [128, 512], BF16) for i in range(2)]
        self.A = [P.ps("A%d" % i, [128, 512]) for i in range(3)]
        self.S = [P.ps("S%d" % i, [128, 512]) for i in range(2)]
        self.X0 = P.ps("X0", [128, 512])
        self.X1 = P.ps("X1", [128, 512])
        self.XT = P.ps("XT", [128, 512], BF16)
        self.cnt = {}

    def uslice(self, h, mo):
        return self.Ubig[:, h * LU + mo: h * LU + mo + 512]

    def rot(self, name, lst):
        i = self.cnt.get(name, 0)
        self.cnt[name] = i + 1
        return lst[i % len(lst)]

    def norm64(self, src_t, src_ap, w, gain_ap, dst_t, dst_ap):
        P = self.P
        sq = self.rot("sq", self.sq)
        P.op("act", lambda e: e.activation(out=sq[0:64, 0:w], in_=src_ap, func=AF.Square), [src_t], [sq])
        P.op("pe", lambda e: e.matmul(self.X0[0:64, 0:w], lhsT=self.ones[0:64, 0:64], rhs=sq[0:64, 0:w],
                                      start=True, stop=True), [self.ones, sq], [self.X0])
        P.op("act", lambda e: e.activation(out=self.rq[0:64, 0:w], in_=self.X0[0:64, 0:w], func=AF.Ln,
                                           bias=self.epsT[0:64, :], scale=1.0 / 64.0), [self.X0, self.epsT], [self.rq])
        P.op("act", lambda e: e.activation(out=self.rq[0:64, 0:w], in_=self.rq[0:64, 0:w], func=AF.Exp, scale=-0.5),
             [self.rq], [self.rq])
        P.op("dve", lambda e: e.scalar_tensor_tensor(out=dst_ap, in0=src_ap, scalar=gain_ap, in1=self.rq[0:64, 0:w],
                                                     op0=ALU.mult, op1=ALU.mult), [src_t, self.hgs, self.rq], [dst_t])

    def setup(self):
        P = self.P
        st = self.st
        P.dma("sp", self.hgs, self.hgs[:], self.hg, self.hg[:])
        P.op("pool", lambda e: e.memset(self.ones[:], 1.0), [], [self.ones])
        P.op("pool", lambda e: e.memset(self.epsT[:], 1e-6), [], [self.epsT])
        P.op("pool", lambda e: e.memset(self.tinyT[:], 1e-30), [], [self.tinyT])
        P.op("pool", lambda e: e.memset(self.vsaug[:], 1.0), [], [self.vsaug])
        P.op("pool", lambda e: e.memset(self.vwaug[:], 1.0), [], [self.vwaug])
        P.op("pool", lambda e: e.memset(self.vcaug[:], 1.0), [], [self.vcaug])
        s = self.rot("st", st)
        P.dma("sp", s, s[:, 0:128], self.identD, self.identD[:])
        P.op("pool", lambda e, s=s: e.tensor_copy(out=self.ident[:], in_=s[:, 0:128]), [s], [self.ident])
        s = self.rot("st", st)
        P.dma("sp", s, s[:, 0:512], self.ovlD, self.ovlD[:].rearrange("p c j -> p (c j)"))
        P.op("pool", lambda e, s=s: e.tensor_copy(out=self.ovl[:].rearrange("p c j -> p (c j)"), in_=s[:, 0:512]),
             [s], [self.ovl])
        self.setup_tables()
        self.setup_kv()

    def load_rb(self):
        P = self.P
        rb = self.rb6[:]
        P.dma("sp", self.cf, self.cf[:], self.rb6, bass.AP(tensor=rb.tensor, offset=31 * 6, ap=[[0, 128], [1, 6]]))
        P.dma("sp", self.tab, self.tab[0:32, :], self.rb6, self.rb6[:])

    def setup_tables(self):
        P = self.P
        st = self.st
        P.op("pool", lambda e: e.memset(self.tab[:], MASKV), [], [self.tab])
        self.load_rb()
        P.op("dve", lambda e: e.tensor_scalar(out=self.tab[0:32, :], in0=self.tab[0:32, :], scalar1=8.0, scalar2=None,
                                              op0=ALU.mult), [self.tab], [self.tab])
        o = 0
        while o < LTOT:
            w = min(512, LTOT - o)
            s = self.rot("st", st)
            P.dma("sp", s, s[0:33, 0:w], self.OH, self.OH[:, o:o + w])
            P.op("pe", lambda e, s=s, w=w: e.matmul(self.X1[0:6, 0:w], lhsT=self.tab[:, :], rhs=s[0:33, 0:w],
                                                    start=True, stop=True), [self.tab, s], [self.X1])
            s2 = self.rot("st", st)
            P.op("act", lambda e, s2=s2, w=w: e.activation(out=s2[0:6, 0:w], in_=self.X1[0:6, 0:w], func=AF.Copy),
                 [self.X1], [s2])
            P.dma("sp", self.Rrow, self.Rrow[:, o:o + w], s2, s2[0:6, 0:w])
            o += w
        rr = self.Rrow[:]

        def expand(dst, d0, h, base, pstep, L):
            o = 0
            while o < L:
                w = min(1024, L - o)
                s = self.rot("st", st)
                P.dma("sp", s, s[:, 0:w], self.Rrow,
                      bass.AP(tensor=rr.tensor, offset=h * LTOT + base + o, ap=[[pstep, 128], [1, w]]))
                P.op("pool", lambda e, s=s, o=o, w=w: e.tensor_copy(out=dst[:, d0 + o:d0 + o + w], in_=s[:, 0:w]),
                     [s], [dst])
                o += w
        for h in range(6):
            expand(self.Ubig, h * LU, h, 0, 16, LU)
        for h in range(3):
            expand(self.TTw[h], 0, h, LC, 1, LTW)
            expand(self.TTs[h], 0, h, LC + LW, 1, LTS)

    def load_kv_piece(self, s, slot, c0):
        self.P.dma("sp", s, s[0:64, :], self.kvT, self.kvT[slot * 64:(slot + 1) * 64, c0:c0 + 1024])

    def load_v_piece(self, piece):
        P = self.P
        c0 = piece * 1024
        s = self.rot("st", self.st)
        P.dma("sp", s, s[:, :].rearrange("p (c f) -> p c f", f=128), self.vtok,
              self.vtok[c0:c0 + 1024, :].rearrange("(c p) f -> p c f", p=128))
        sv = s[:, :].rearrange("p (c f) -> p c f", f=128)
        P.op("pool", lambda e, sv=sv, piece=piece: e.tensor_copy(
            out=self.vsaug[:, piece * 8:(piece + 1) * 8, 0:64], in_=sv[:, :, 0:64]), [s], [self.vsaug])
        P.op("pool", lambda e, sv=sv, piece=piece: e.tensor_copy(
            out=self.vwaug[:, piece * 8:(piece + 1) * 8, 0:64], in_=sv[:, :, 64:128]), [s], [self.vwaug])

    def setup_kv(self):
        P = self.P
        st = self.st
        for piece in range(8):
            c0 = piece * 1024
            for slot, gcol, dst in ((2, 2, self.ksaug), (4, 3, self.kwT)):
                s = self.rot("st", st)
                self.load_kv_piece(s, slot, c0)
                for hf in range(2):
                    a = hf * 512
                    self.norm64(s, s[0:64, a:a + 512], 512, self.hgs[0:64, gcol:gcol + 1], dst,
                                dst[0:64, c0 + a:c0 + a + 512])
            s = self.rot("st", st)
            P.dma("sp", s, s[64:128, :], self.EE, self.EE[:, c0:c0 + 1024])
            P.op("pool", lambda e, s=s, c0=c0: e.tensor_copy(out=self.ksaug[64:128, c0:c0 + 1024], in_=s[64:128, :]),
                 [s], [self.ksaug])
            self.load_v_piece(piece)
        self.compress()

    def compress_rhs(self, kvi, tg):
        P = self.P
        for th in range(2):
            s = self.rot("st", self.st)
            t0 = tg * 4 + th * 2
            P.dma("sp", s, s[0:64, :].rearrange("p (t n) -> p t n", n=512), self.blk,
                  self.blk[kvi, :, t0:t0 + 2, :])
            for tt in range(2):
                t = t0 + tt
                P.op("dve", lambda e, s=s, tt=tt, th=th, t=t: e.tensor_scalar(
                    out=self.blkb[:, th * 2 + tt, :], in0=s[0:64, tt * 512:(tt + 1) * 512],
                    scalar1=self.posT[:, t:t + 1], scalar2=None, op0=ALU.add), [s, self.posT], [self.blkb])

    def compress_begin(self, kvi):
        pass

    def compress(self):
        P = self.P
        st = self.st
        H = [self.A[0], self.A[1]]
        for kvi in range(2):
            self.compress_begin(kvi)
            P.dma("sp", self.posT, self.posT[:], self.pos, self.pos[kvi])
            s = self.rot("st", st)
            P.dma("sp", s, s[:, 0:128].rearrange("p (c d) -> p c d", d=64), self.w2,
                  self.w2[kvi].rearrange("(c p) d -> p c d", p=128))
            P.op("pool", lambda e, s=s: e.tensor_copy(out=self.w2b[:].rearrange("p c d -> p (c d)"), in_=s[:, 0:128]),
                 [s], [self.w2b])
            for tg in range(8):
                s = self.rot("st", st)
                P.dma("sp", s, s[0:64, :].rearrange("p (t j) -> p t j", j=256), self.w1,
                      self.w1[kvi, tg * 256:(tg + 1) * 256, :].rearrange("(t d) j -> d t j", d=64))
                P.op("pool", lambda e, s=s: e.tensor_copy(out=self.w1b[:].rearrange("p t j -> p (t j)"),
                                                          in_=s[0:64, :]), [s], [self.w1b])
                self.compress_rhs(kvi, tg)
                for tl in range(4):
                    t = tg * 4 + tl
                    for jc in range(2):
                        P.op("pe", lambda e, tl=tl, jc=jc, t=t: e.matmul(
                            H[jc][:, :], lhsT=self.w1b[:, tl, jc * 128:(jc + 1) * 128], rhs=self.blkb[:, tl, :],
                            start=(t == 0), stop=(t == 31)), [self.w1b, self.blkb], [H[jc]])
            for jc in range(2):
                P.op("act", lambda e, jc=jc: e.activation(out=self.gx[:], in_=H[jc][:, :], func=AF.Copy),
                     [H[jc]], [self.gx])
                P.op("pool", lambda e: e.tensor_tensor(out=self.gy[:], in0=self.gx[:], in1=self.gx[:], op=ALU.mult),
                     [self.gx], [self.gy])
                P.op("dve", lambda e: e.tensor_scalar(out=self.gy[:], in0=self.gy[:], scalar1=0.044715, scalar2=1.0,
                                                      op0=ALU.mult, op1=ALU.add), [self.gy], [self.gy])
                P.op("pool", lambda e: e.tensor_tensor(out=self.gy[:], in0=self.gy[:], in1=self.gx[:], op=ALU.mult),
                     [self.gy, self.gx], [self.gy])
                P.op("act", lambda e: e.activation(out=self.gy[:], in_=self.gy[:], func=AF.Sigmoid,
                                                   scale=1.5957691216057308), [self.gy], [self.gy])
                P.op("dve", lambda e, jc=jc: e.tensor_tensor(out=self.gT[jc][:], in0=self.gx[:], in1=self.gy[:],
                                                             op=ALU.mult), [self.gx, self.gy], [self.gT[jc]])
            if kvi == 0:
                for jc in range(2):
                    P.op("pe", lambda e, jc=jc: e.matmul(self.X1[0:64, :], lhsT=self.w2b[:, jc, :], rhs=self.gT[jc][:],
                                                         start=(jc == 0), stop=(jc == 1)),
                         [self.w2b, self.gT[jc]], [self.X1])
                s = self.rot("st", st)
                P.op("act", lambda e, s=s: e.activation(out=s[0:64, 0:512], in_=self.X1[0:64, :], func=AF.Copy),
                     [self.X1], [s])
                self.norm64(s, s[0:64, 0:512], 512, self.hgs[0:64, 1:2], self.kcT, self.kcT[:, :])
            else:
                for cc in range(4):
                    for jc in range(2):
                        P.op("pe", lambda e, jc=jc, cc=cc: e.matmul(
                            self.X1[:, cc * 64:(cc + 1) * 64], lhsT=self.gT[jc][:, cc * 128:(cc + 1) * 128],
                            rhs=self.w2b[:, jc, :], start=(jc == 0), stop=(jc == 1)),
                            [self.w2b, self.gT[jc]], [self.X1])
                    P.op("act", lambda e, cc=cc: e.activation(out=self.vcaug[:, cc, 0:64],
                                                              in_=self.X1[:, cc * 64:(cc + 1) * 64], func=AF.Copy),
                         [self.X1], [self.vcaug])

    def combine(self, qt, h, br, first):
        P = self.P
        A = self.A[h]
        s0 = qt * QT
        gt = self.load_gate(qt, h, br)
        P.op("act", lambda e: e.activation(out=self.rdt[:], in_=A[64:128, :], func=AF.Ln, bias=self.tinyT[0:64, :]),
             [A, self.tinyT], [self.rdt])
        P.op("act", lambda e: e.activation(out=self.rdt[:], in_=self.rdt[:], func=AF.Exp, scale=-1.0),
             [self.rdt], [self.rdt])
        P.op("pool", lambda e: e.tensor_tensor(out=self.wt[:], in0=self.rdt[:], in1=gt[:], op=ALU.mult),
             [self.rdt, gt], [self.wt])
        if first:
            P.op("dve", lambda e: e.tensor_tensor(out=self.y[:, h, :], in0=A[0:64, :], in1=self.wt[:], op=ALU.mult),
                 [A, self.wt], [self.y])
        else:
            P.op("dve", lambda e: e.tensor_tensor(out=self.tm[:], in0=A[0:64, :], in1=self.wt[:], op=ALU.mult),
                 [A, self.wt], [self.tm])
            P.op("pool", lambda e: e.tensor_tensor(out=self.y[:, h, :], in0=self.y[:, h, :], in1=self.tm[:],
                                                   op=ALU.add), [self.y, self.tm], [self.y])

    def exp(self, S, pT, bias):
        P = self.P
        if bias is None:
            P.op("act", lambda e: e.activation(out=pT[:], in_=S[:, :], func=AF.Exp, scale=0.125), [S], [pT])
        else:
            bap, bt = bias
            P.op("act", lambda e: e.activation(out=pT[:], in_=S[:, :], func=AF.Exp, scale=0.125, bias=bap),
                 [S, bt], [pT])

    def cmp_bias(self, cc, h, mixed):
        return None if mixed else (self.cf[:, h:h + 1], self.cf)

    def sel_bias(self, kc, h, near):
        return None if near else (self.cf[:, h:h + 1], self.cf)

    def win_bias(self, c):
        return None

    def topc_ap(self, qt):
        return self.topc[qt]

    def load_q(self, qt):
        s0 = qt * QT
        self.P.dma("sp", self.qraw, self.qraw[:], self.qT, self.qT[:, s0:s0 + QT].rearrange("(h d) t -> d h t", d=64))

    def load_gate(self, qt, h, br):
        s0 = qt * QT
        gt = self.rot("gt", self.gt)
        g = self.gB[:]
        self.P.dma("sp", gt, gt[:], self.gB,
                   bass.AP(tensor=g.tensor, offset=(h * 3 + br) * SEQ + s0, ap=[[0, 64], [1, 512]]))
        return gt

    def store_y(self, qt):
        s0 = qt * QT
        self.P.dma("sp", self.yT, self.yT[:, s0:s0 + QT].rearrange("(h d) t -> d h t", d=64), self.y, self.y[:],
                   nodep_out=True)

    def qtile(self, qt):
        P = self.P
        s0 = qt * QT
        self.load_q(qt)
        P.dma("sp", self.tcs, self.tcs[:], self.topc, self.topc_ap(qt))
        for h in range(6):
            self.norm64(self.qraw, self.qraw[:, h, :], 512, self.hgs[0:64, 0:1], self.qn, self.qn[:, h, :])
        for h in range(3):
            for half in range(2):
                P.op("pool", lambda e, h=h, half=half: e.tensor_copy(out=self.qaug[0:64, h, half, :],
                                                                     in_=self.qn[:, h, :]), [self.qn], [self.qaug])
        for h in range(6):
            chunks = [cc for cc in range(4) if qt - 4 * cc >= 0]
            pts = []
            for cc in chunks:
                k = qt - 4 * cc
                mixed = k <= 5
                S = self.rot("S", self.S)
                P.op("pe", lambda e, S=S, cc=cc, h=h, mixed=mixed: e.matmul(
                    S[:, :], lhsT=self.kcT[:, cc * 128:(cc + 1) * 128], rhs=self.qn[:, h, :], start=True,
                    stop=(not mixed)), [self.kcT, self.qn], [S])
                if mixed:
                    mo = 512 * (5 - k)
                    P.op("pe", lambda e, S=S, h=h, mo=mo: e.matmul(S[:, :], lhsT=self.ident[:],
                                                                   rhs=self.uslice(h, mo), start=False, stop=True),
                         [self.ident, self.Ubig], [S])
                pT = self.rot("pT", self.pT)
                self.exp(S, pT, self.cmp_bias(cc, h, mixed))
                pts.append(pT)
            n = len(chunks)
            if h < 3:
                for i, (cc, pT) in enumerate(zip(chunks, pts)):
                    P.op("pe", lambda e, pT=pT, i=i, cc=cc, h=h: e.matmul(
                        self.A[h][:, :], lhsT=self.vcaug[:, cc, :], rhs=pT[:], start=(i == 0), stop=(i == n - 1)),
                        [self.vcaug, pT], [self.A[h]])
            for ss in range(4):
                for i, (cc, pT) in enumerate(zip(chunks, pts)):
                    P.op("pe", lambda e, pT=pT, i=i, cc=cc, ss=ss: e.matmul(
                        self.X1[:, ss * 128:(ss + 1) * 128], lhsT=pT[:, ss * 128:(ss + 1) * 128], rhs=self.ovl[:, cc, :],
                        start=(i == 0), stop=(i == n - 1)), [pT, self.ovl], [self.X1])
            for ss in range(4):
                for i, pT in enumerate(pts):
                    P.op("pe", lambda e, pT=pT, i=i, ss=ss: e.matmul(
                        self.X0[:, ss:ss + 1], lhsT=pT[:, ss * 128:(ss + 1) * 128], rhs=self.ones[:, 0:1],
                        start=(i == 0), stop=(i == n - 1)), [pT, self.ones], [self.X0])
            P.op("dve", lambda e: e.tensor_scalar(out=self.rdq[:], in0=self.X0[:, 0:4], scalar1=1e-30, scalar2=None,
                                                  op0=ALU.max), [self.X0], [self.rdq])
            P.op("dve", lambda e: e.reciprocal(out=self.rdq[:], in_=self.rdq[:]), [self.rdq], [self.rdq])
            for ss in range(4):
                if h == 0:
                    P.op("dve", lambda e, ss=ss: e.tensor_scalar(
                        out=self.impacc[:, ss * 128:(ss + 1) * 128], in0=self.X1[:, ss * 128:(ss + 1) * 128],
                        scalar1=self.rdq[:, ss:ss + 1], scalar2=None, op0=ALU.mult), [self.X1, self.rdq], [self.impacc])
                else:
                    P.op("dve", lambda e, ss=ss: e.scalar_tensor_tensor(
                        out=self.impacc[:, ss * 128:(ss + 1) * 128], in0=self.X1[:, ss * 128:(ss + 1) * 128],
                        scalar=self.rdq[:, ss:ss + 1], in1=self.impacc[:, ss * 128:(ss + 1) * 128],
                        op0=ALU.mult, op1=ALU.add), [self.X1, self.rdq, self.impacc], [self.impacc])
            if h < 3:
                self.combine(qt, h, 0, True)
        for ss in range(4):
            tA = self.tcs[:, ss * 256: ss * 256 + 128]
            tB = self.tcs[:, ss * 256 + 128: ss * 256 + 256]
            P.op("dve", lambda e, ss=ss, tA=tA: e.tensor_tensor(out=self.sc[:], in0=self.impacc[:, ss * 128:(ss + 1) * 128],
                                                                in1=tA, op=ALU.mult), [self.impacc, self.tcs], [self.sc])
            P.op("dve", lambda e, tB=tB: e.tensor_tensor(out=self.sc[:], in0=self.sc[:], in1=tB, op=ALU.add),
                 [self.sc, self.tcs], [self.sc])
            P.op("dve", lambda e: e.max(out=self.mx[:, 0:8], in_=self.sc[:]), [self.sc], [self.mx])
            P.op("dve", lambda e: e.match_replace(out=self.sc2[:], in_to_replace=self.mx[:, 0:8], in_values=self.sc[:],
                                                  imm_value=-1e30), [self.sc, self.mx], [self.sc2])
            P.op("dve", lambda e: e.max(out=self.mx[:, 8:16], in_=self.sc2[:]), [self.sc2], [self.mx])
            P.op("dve", lambda e: e.tensor_scalar(out=self.mneg[:], in0=self.sc[:], scalar1=self.mx[:, 15:16], scalar2=1.0,
                                                  op0=ALU.is_ge, op1=ALU.subtract), [self.sc, self.mx], [self.mneg])
            P.op("pe", lambda e, ss=ss: e.transpose(self.XT[:, ss * 128:(ss + 1) * 128], self.mneg[:], self.ident[:]),
                 [self.mneg, self.ident], [self.XT])
        for h in range(3):
            for half in range(2):
                eng = "act" if (h + half) % 2 == 0 else "dve"
                if eng == "act":
                    P.op("act", lambda e, h=h, half=half: e.activation(out=self.qaug[64:128, h, half, :],
                                                                       in_=self.XT[half * 64:(half + 1) * 64, :],
                                                                       func=AF.Copy), [self.XT], [self.qaug])
                else:
                    P.op("dve", lambda e, h=h, half=half: e.tensor_copy(out=self.qaug[64:128, h, half, :],
                                                                        in_=self.XT[half * 64:(half + 1) * 64, :]),
                         [self.XT], [self.qaug])
        nkc = 4 * qt + 4

        def sel_qk(kc, h):
            k0 = 128 * kc
            ep = s0 + 511 - k0
            near = ep <= DS
            half = kc // 32
            S = self.rot("S", self.S)
            P.op("pe", lambda e: e.matmul(S[:, :], lhsT=self.ksaug[:, k0:k0 + 128], rhs=self.qaug[:, h, half, :],
                                          start=True, stop=(not near)), [self.ksaug, self.qaug], [S])
            if near:
                mo = DS - ep
                P.op("pe", lambda e: e.matmul(S[:, :], lhsT=self.ident[:], rhs=self.TTs[h][:, mo:mo + 512],
                                              start=False, stop=True), [self.ident, self.TTs[h]], [S])
            return (kc, h, near, S)

        def sel_rest(kc, h, near, S):
            pT = self.rot("pT", self.pT)
            self.exp(S, pT, self.sel_bias(kc, h, near))
            P.op("pe", lambda e: e.matmul(self.A[h][:, :], lhsT=self.vsaug[:, kc, :], rhs=pT[:],
                                          start=(kc == 0), stop=(kc == nkc - 1)), [self.vsaug, pT], [self.A[h]])

        prev = None
        for kc in range(nkc):
            for h in range(3):
                cur = sel_qk(kc, h)
                if prev is not None:
                    sel_rest(*prev)
                prev = cur
        sel_rest(*prev)
        for h in range(3):
            self.combine(qt, h, 1, False)
        rs = [r for r in range(8) if s0 - 512 + 128 * r >= 0]

        def win_qk(i, r, h):
            k0 = s0 - 512 + 128 * r
            mo = 128 * r
            S = self.rot("S", self.S)
            P.op("pe", lambda e: e.matmul(S[:, :], lhsT=self.kwT[:, k0:k0 + 128], rhs=self.qn[:, h, :],
                                          start=True, stop=False), [self.kwT, self.qn], [S])
            P.op("pe", lambda e: e.matmul(S[:, :], lhsT=self.ident[:], rhs=self.TTw[h][:, mo:mo + 512],
                                          start=False, stop=True), [self.ident, self.TTw[h]], [S])
            return (i, k0, h, S)

        def win_rest(i, k0, h, S):
            pT = self.rot("pT", self.pT)
            self.exp(S, pT, self.win_bias(k0 // 128))
            P.op("pe", lambda e: e.matmul(self.A[h][:, :], lhsT=self.vwaug[:, k0 // 128, :], rhs=pT[:],
                                          start=(i == 0), stop=(i == len(rs) - 1)), [self.vwaug, pT], [self.A[h]])

        prev = None
        for i, r in enumerate(rs):
            for h in range(3):
                cur = win_qk(i, r, h)
                if prev is not None:
                    win_rest(*prev)
                prev = cur
        win_rest(*prev)
        for h in range(3):
            self.combine(qt, h, 2, False)
        self.store_y(qt)

    def build(self, nqt=NQT):
        self.setup()
        for qt in range(nqt):
            self.qtile(qt)
        self.P.wait_all("sp", [self.yT])
        self.P.finalize()


def build_attn(nqt=NQT):
    nc = bass.Bass("TRN2", target_bir_lowering=False)
    P = Prog(nc)
    a = Attn(P)
    a.build(nqt)
    return nc, P


XPAD = SEQ + HALO


def rev_ap(ap, n):
    dims = [list(d) for d in ap.ap]
    dims[-1] = [-1, n]
    return bass.AP(tensor=ap.tensor, offset=ap.offset + (n - 1), ap=dims)


class FTok(TokStage):
    def __init__(self, P, mode, dr):
        self.P = P
        self.mode = mode
        self.dr = dr
        for k in ("memT", "gains", "hg", "mem_w_kv", "w_out", "w_up", "w_down", "b_w_in", "a_w_in", "convw", "w_kv"):
            setattr(self, k, dr[k])
        self.xsrc = dr["xin"] if mode == "L1" else dr["X"]
        self.kvT_o, self.qT_o, self.gT_o = dr["KV"], dr["Q"], dr["G"]
        self.ytokT = dr["Y"]
        self.chunk = 0
        self.alloc()

    def halo_fix(self, sub):
        if self.chunk == 0:
            self.P.op("pool", lambda e: e.memset(self.v[:, sub, 2:2 + HALO], 0.0), [], [self.v])

    def store_tile(self, dstT, r, msz, c0, t0, w, ost):
        a = c0 + t0
        lo = HALO if a == 0 else 0
        g0 = NTOK * self.chunk + a + lo - HALO
        self.P.dma("sp", dstT, dstT[r:r + msz, g0:g0 + (w - lo)], ost, ost[0:msz, lo:w], nodep_out=True)

    def ytok_src(self, ch, c0, t0, w):
        g0 = NTOK * self.chunk + c0 + t0
        return self.ytokT[ch * 128:(ch + 1) * 128, g0:g0 + w]

    def setup_consts(self):
        P = self.P
        P.dma("sp", self.g, self.g[:], self.gains, self.gains[:])
        P.dma("sp", self.hgs, self.hgs[:], self.hg, self.hg[:])
        if self.mode == "L1":
            P.dma("sp", self.cw, self.cw[:], self.convw, self.convw[:])
        for c in range(8):
            P.dma("sp", self.u, self.memx(c, 0, 256), self.memT, self.memT[c * 128:(c + 1) * 128, :], nodep_out=True)
        P.op("pool", lambda e: e.memset(self.ones[:], 1.0), [], [self.ones])
        P.op("pool", lambda e: e.memset(self.bd[:], 0.0), [], [self.bd])
        P.op("pool", lambda e: e.memset(self.bd[0:64, 0:64], 1.0), [], [self.bd])
        P.op("pool", lambda e: e.memset(self.bd[64:128, 64:128], 1.0), [], [self.bd])
        P.op("pool", lambda e: e.memset(self.epsT[:], EPS), [], [self.epsT])
        P.op("pool", lambda e: e.memset(self.vaug[:], 1.0), [], [self.vaug])
        self.norm(self.u, self.memx, 256, [(0, 256)], 15, self.memh, 0)

    def load_x(self):
        P = self.P
        g0 = NTOK * self.chunk
        for c in range(8):
            P.dma("sp", self.x, self.x[:, c, :], self.xsrc, self.xsrc[c * 128:(c + 1) * 128, g0:g0 + NCOL],
                  nodep_out=(c > 0))

    def store_x(self):
        P = self.P
        dst = self.dr["out"] if self.mode == "L5" else self.dr["X"]
        off = 0 if self.mode == "L5" else HALO
        g0 = NTOK * self.chunk + off
        for c in range(8):
            P.dma("sp", dst, dst[c * 128:(c + 1) * 128, g0:g0 + NTOK], self.x, self.x[:, c, HALO:NCOL], nodep_out=True)

    def run(self):
        self.setup_consts()
        for chunk in range(4):
            self.chunk = chunk
            self.load_x()
            if self.mode == "L1":
                for l in range(2):
                    self.mem_kv(l)
                    for ri, (c0, n) in enumerate(RANGES):
                        tiles = tiles_of(n)
                        self.norm(self.x, self.xs(c0), n, tiles, l, self.h, 0)
                        self.mix_a(l, ri, c0, n, tiles)
                        self.mem_attn(l, n, tiles)
                        self.out_proj(l, c0, n, tiles)
                        self.mlp(l, c0, n, tiles)
                for ri, (c0, n) in enumerate(RANGES):
                    tiles = tiles_of(n)
                    self.norm(self.x, self.xs(c0), n, tiles, 8, self.h, 0)
                    for i in range(3):
                        self.store_proj(self.w_kv, self.w_kv[:, 256 * i:256 * (i + 1)], 256, self.kvT_o, 256 * i, c0, tiles)
                    if not getattr(self, "skip_pre0", False):
                        self.pre_b(0, c0, n, tiles)
            else:
                j = 0 if self.mode == "L3" else 1
                self.mem_kv(2 + j)
                for ri, (c0, n) in enumerate(RANGES):
                    self.post_b(j, c0, n, tiles_of(n))
                if j == 0:
                    for ri, (c0, n) in enumerate(RANGES):
                        self.pre_b(1, c0, n, tiles_of(n))
            self.store_x()


class FAttn(Attn):
    def __init__(self, P, dr, jl):
        self.P = P
        self.dr = dr
        self.jl = jl
        self.Q, self.G, self.KV, self.Y = dr["Q"], dr["G"], dr["KV"], dr["Y"]
        self.hg = dr["ahg%d" % jl]
        self.pos, self.w1, self.w2 = dr["pos"], dr["cmp_w1"], dr["cmp_w2"]
        self.rel = dr["rel_bias"]
        self.OH, self.identD, self.EE, self.ovlD, self.topc = dr["OH"], dr["ident"], dr["EE"], dr["ovl"], dr["topc"]
        self.Rrow = dr["Rrow"]
        self.alloc()

    def alloc(self):
        Attn.alloc(self)
        self.ub_view = self.Ubig[0:64, 0:16 * 513].rearrange("p (r n) -> p r n", n=513)

    def set_combo(self, g, tr):
        self.g, self.tr = g, tr
        own = [3 * tr + i for i in range(3)]
        oth = [3 * (1 - tr) + i for i in range(3)]
        self.heads = own + oth

    def load_rb(self):
        P = self.P
        rel = self.rel[:]
        for i, hh in enumerate(self.heads):
            col = self.g * 6 + hh
            P.dma("sp", self.cf, self.cf[:, i:i + 1], self.rel,
                  bass.AP(tensor=rel.tensor, offset=31 * 12 + col, ap=[[0, 128], [1, 1]]))
            P.dma("sp", self.tab, self.tab[0:32, i:i + 1], self.rel, self.rel[:, col:col + 1],
                  allow_slow_non_contiguous=True)

    def load_kv_piece(self, s, slot, c0):
        r0 = slot * 128 + self.g * 64
        self.P.dma("sp", s, s[0:64, :], self.KV, self.KV[r0:r0 + 64, c0:c0 + 1024])

    def load_v_piece(self, piece):
        P = self.P
        c0 = piece * 1024
        for slot, dst in ((3, self.vsaug), (5, self.vwaug)):
            s = self.rot("st", self.st)
            self.load_kv_piece(s, slot, c0)
            for hf in range(2):
                sq = self.rot("sq", self.sq)
                P.op("pool", lambda e, s=s, sq=sq, hf=hf: e.tensor_copy(out=sq[0:64, :], in_=s[0:64, hf * 512:(hf + 1) * 512]),
                     [s], [sq])
                for c in range(4):
                    P.op("pe", lambda e, sq=sq, c=c: e.transpose(self.XT[:, c * 64:(c + 1) * 64],
                                                                 sq[0:64, c * 128:(c + 1) * 128], self.ident[0:64, 0:64]),
                         [sq, self.ident], [self.XT])
                ch0 = piece * 8 + hf * 4
                P.op("act", lambda e, dst=dst, ch0=ch0: e.activation(
                    out=dst[:, ch0:ch0 + 4, 0:64], in_=self.XT[:, 0:256].rearrange("p (c d) -> p c d", d=64),
                    func=AF.Copy), [self.XT], [dst])

    def compress_begin(self, kvi):
        P = self.P
        P.op("pool", lambda e: e.memset(self.ub_view[:, :, 512:513], 0.0), [], [self.Ubig])
        for piece in range(8):
            s = self.rot("st", self.st)
            self.load_kv_piece(s, kvi, piece * 1024)
            P.op("pool", lambda e, s=s, piece=piece: e.tensor_copy(
                out=self.ub_view[:, :, piece * 64:(piece + 1) * 64],
                in_=s[0:64, :].rearrange("p (n r) -> p r n", r=16)), [s], [self.Ubig])

    def compress_rhs(self, kvi, tg):
        P = self.P
        for tl in range(4):
            t = tg * 4 + tl
            r, off = t % 16, t // 16
            P.op("dve", lambda e, tl=tl, t=t, r=r, off=off: e.tensor_scalar(
                out=self.blkb[:, tl, :], in0=self.ub_view[:, r, off:off + 512], scalar1=self.posT[:, t:t + 1],
                scalar2=None, op0=ALU.add), [self.Ubig, self.posT], [self.blkb])

    def load_q(self, qt):
        P = self.P
        s0 = qt * QT
        for i2 in range(3):
            s = self.rot("st", self.st)
            for k in range(2):
                hh = self.heads[2 * i2 + k]
                r0 = (self.g * 6 + hh) * 64
                P.dma("sp", s, s[0:64, k * 512:(k + 1) * 512], self.Q, self.Q[r0:r0 + 64, s0:s0 + QT], nodep_out=(k > 0))
            for k in range(2):
                P.op("dve", lambda e, s=s, k=k, i2=i2: e.tensor_copy(
                    out=self.qraw[:, 2 * i2 + k, :], in_=rev_ap(s[0:64, k * 512:(k + 1) * 512], 512)), [s], [self.qraw])

    def load_gate(self, qt, h, br):
        P = self.P
        s0 = qt * QT
        gt = self.rot("gt", self.gt)
        g = self.G[:]
        row = (self.g * 6 + 3 * self.tr + h) * 3 + br
        P.dma("sp", gt, gt[:], self.G, bass.AP(tensor=g.tensor, offset=row * SEQ + s0, ap=[[0, 64], [1, 512]]))
        gr = self.rot("gtr", self.gtr)
        P.op("dve", lambda e: e.tensor_copy(out=gr[:], in_=rev_ap(gt[:], 512)), [gt], [gr])
        return gr

    def store_y(self, qt):
        P = self.P
        s0 = qt * QT
        bufs = [self.rdt, self.wt, self.tm]
        for h in range(3):
            b = bufs[h]
            P.op("dve", lambda e, b=b, h=h: e.tensor_copy(out=b[:], in_=rev_ap(self.y[:, h, :], 512)), [self.y], [b])
            r0 = (self.g * 6 + 3 * self.tr + h) * 64
            P.dma("sp", self.Y, self.Y[r0:r0 + 64, HALO + s0:HALO + s0 + QT], b, b[:], nodep_out=True)

    def run(self):
        P = self.P
        st = self.st
        self.gtr = [P.sb("gtr%d" % i, [64, 512], F32) for i in range(2)]
        P.dma("sp", self.hgs, self.hgs[:], self.hg, self.hg[:])
        P.op("pool", lambda e: e.memset(self.ones[:], 1.0), [], [self.ones])
        P.op("pool", lambda e: e.memset(self.epsT[:], 1e-6), [], [self.epsT])
        P.op("pool", lambda e: e.memset(self.tinyT[:], 1e-30), [], [self.tinyT])
        P.op("pool", lambda e: e.memset(self.vsaug[:], 1.0), [], [self.vsaug])
        P.op("pool", lambda e: e.memset(self.vwaug[:], 1.0), [], [self.vwaug])
        P.op("pool", lambda e: e.memset(self.vcaug[:], 1.0), [], [self.vcaug])
        s = self.rot("st", st)
        P.dma("sp", s, s[:, 0:128], self.identD, self.identD[:])
        P.op("pool", lambda e, s=s: e.tensor_copy(out=self.ident[:], in_=s[:, 0:128]), [s], [self.ident])
        s = self.rot("st", st)
        P.dma("sp", s, s[:, 0:512], self.ovlD, self.ovlD[:].rearrange("p c j -> p (c j)"))
        P.op("pool", lambda e, s=s: e.tensor_copy(out=self.ovl[:].rearrange("p c j -> p (c j)"), in_=s[:, 0:512]),
             [s], [self.ovl])
        for g in range(2):
            self.set_combo(g, 0)
            P.barrier()
            self.setup_kv()
            for tr in range(2):
                self.set_combo(g, tr)
                P.barrier()
                self.setup_tables()
                for qt in range(NQT):
                    self.qtile(qt)


def fused_dram(P, r3=False):
    dr = {}
    D = lambda n, s: P.dram(n, s, F32, kind="ExternalInput")
    dr["xin"] = D("xin", [1024, XPAD])
    dr["memT"] = D("memT", [1024, 256])
    dr["gains"] = D("gains", [128, 128])
    dr["hg"] = D("hg", [128, 16])
    dr["convw"] = D("convw", [128, 36])
    dr["mem_w_kv"] = D("mem_w_kv", [4, 1024, 512])
    dr["w_out"] = D("w_out", [4, 1024, 1024])
    dr["w_up"] = D("w_up", [4, 1024, 4096])
    dr["w_down"] = D("w_down", [4, 4096, 1024])
    dr["b_w_in"] = D("b_w_in", [2, 1024, 1060])
    dr["a_w_in"] = D("a_w_in", [2, 1024, 2560])
    dr["w_kv"] = D("w_kv", [1024, 768])
    dr["ahg0"] = D("ahg0", [128, 8])
    dr["ahg1"] = D("ahg1", [128, 8])
    dr["pos"] = D("pos", [2, 64, 32])
    dr["cmp_w1"] = D("cmp_w1", [2, 2048, 256])
    dr["cmp_w2"] = D("cmp_w2", [2, 256, 64])
    dr["rel_bias"] = D("rel_bias", [32, 12])
    dr["OH"] = D("OH", [33, LTOT])
    dr["ident"] = D("ident", [128, 128])
    dr["EE"] = D("EE", [64, SEQ])
    dr["ovl"] = D("ovl", [128, 4, 128])
    if not r3:
        dr["topc"] = D("topc", [NQT, 128, 1024])
    if not r3:
        dr["out"] = P.dram("xT_out", [1024, SEQ], F32, kind="ExternalOutput")
    I = lambda n, s: P.dram(n, s, F32, kind="Internal")
    dr["X"] = I("Xs", [1024, XPAD])
    if not r3:
        dr["KV"] = I("KVs", [768, SEQ])
    dr["Q"] = I("Qs", [768, SEQ])
    dr["G"] = I("Gs", [36, SEQ])
    dr["Y"] = I("Ys", [768, XPAD])
    dr["Rrow"] = I("Rrow", [6, LTOT])
    return dr


def build_fused():
    nc = bass.Bass("TRN2", target_bir_lowering=False)
    P = Prog(nc)
    dr = fused_dram(P)
    P.push_scope()
    FTok(P, "L1", dr).run()
    P.pop_scope()
    for j in range(2):
        P.push_scope()
        FAttn(P, dr, j).run()
        P.pop_scope()
        P.push_scope()
        FTok(P, "L3" if j == 0 else "L5", dr).run()
        P.pop_scope()
    P.wait_all("sp", [dr["out"]])
    P.finalize()
    return nc, P


KPAD = 1536
QTILES = (3, 7, 11, 15)


def host_core_tables(k):
    tc = np.zeros((4, 128, 4, 2, 128), np.float32)
    jj = np.arange(128)[None, :]
    for j, qt in enumerate(QTILES):
        for ss in range(4):
            ta = qt * QT + 511 - (128 * ss + np.arange(128))
            back = (ta // 64)[:, None] - jj
            J = jj - 24 + 8 * k
            valid = (J >= 0) & (back >= 0)
            forced = (J == 0) | ((back >= 0) & (back < 2))
            tc[j, :, ss, 0] = (valid & ~forced).astype(np.float32)
            tc[j, :, ss, 1] = np.where(valid, np.where(forced, 1e4, 0.0), -1e30).astype(np.float32)
    kval = np.zeros((128, 16), np.float32)
    first = KPAD - 512 * k
    for c in range(12):
        a = 128 * c + np.arange(128)
        kval[:, c] = np.where(a >= first, 0.0, MASKV)
    kval[:, 12] = np.where(16 * np.arange(128) >= first, 0.0, MASKV)
    return tc.reshape(4, 128, 1024), kval


class FTok2(FTok):
    def __init__(self, P, mode, dr):
        FTok.__init__(self, P, mode, dr)
        if mode != "L1":
            self.xsrc = dr["Xown"]
            self.ytokT = dr["Yown"]
            self.qT_o, self.gT_o = dr["Qown"], dr["Gown"]

    def store_tile(self, dstT, r, msz, c0, t0, w, ost):
        if self.mode == "L1":
            off = KPAD if dstT is self.dr["KV"] else 0
            a = c0 + t0
            lo = HALO if a == 0 else 0
            g0 = NTOK * self.chunk + a + lo - HALO + off
            self.P.dma("sp", dstT, dstT[r:r + msz, g0:g0 + (w - lo)], ost, ost[0:msz, lo:w], nodep_out=True)
        else:
            self.P.dma("sp", dstT, dstT[r:r + msz, c0 + t0:c0 + t0 + w], ost, ost[0:msz, 0:w], nodep_out=True)

    def ytok_src(self, ch, c0, t0, w):
        if self.mode == "L1":
            return FTok.ytok_src(self, ch, c0, t0, w)
        return self.ytokT[ch * 128:(ch + 1) * 128, c0 + t0:c0 + t0 + w]

    def load_x(self):
        if self.mode == "L1":
            return FTok.load_x(self)
        P = self.P
        for c in range(8):
            P.dma("sp", self.x, self.x[:, c, 0:NTOK], self.xsrc, self.xsrc[c * 128:(c + 1) * 128, :], nodep_out=(c > 0))

    def store_x(self):
        if self.mode == "L1":
            return FTok.store_x(self)
        P = self.P
        dst = self.dr["out"] if self.mode == "L5" else self.dr["Xown"]
        for c in range(8):
            P.dma("sp", dst, dst[c * 128:(c + 1) * 128, :], self.x, self.x[:, c, 0:NTOK], nodep_out=True)

    def zero_kv_pad(self):
        P = self.P
        z = self.ost[0]
        P.op("pool", lambda e: e.memset(z[:], 0.0), [], [z])
        KV = self.dr["KV"]
        for r in range(6):
            for cb in range(3):
                P.dma("sp", KV, KV[r * 128:(r + 1) * 128, cb * 512:(cb + 1) * 512], z, z[:], nodep_out=True)

    def run(self):
        if self.mode == "L1":
            self.skip_pre0 = True
            self.zero_kv_pad()
            return FTok.run(self)
        self.setup_consts()
        self.chunk = 0
        self.load_x()
        if self.mode == "P0":
            for (c0, n) in [(0, 1024), (1024, 1024)]:
                self.pre_b(0, c0, n, tiles_of(n))
            return
        j = 0 if self.mode == "L3" else 1
        self.mem_kv(2 + j)
        R2 = [(0, 1024), (1024, 1024)]
        for (c0, n) in R2:
            self.post_b(j, c0, n, tiles_of(n))
        if j == 0:
            for (c0, n) in R2:
                self.pre_b(1, c0, n, tiles_of(n))
        self.store_x()


def extract_own(P, dr):
    kd = dr["kd"]

    def dyn3(src_t, r0, r1, pad, ncols):
        base = src_t[r0:r1, pad:pad + KPAD + 6144 + 512][:, bass.ds(kd, 6144 + 512)]
        return bass.AP(tensor=base.tensor, offset=base.offset, ap=[[ncols, r1 - r0], [2048, 4], [1, 512]])
    KVp, KVw = dr["KV"], dr["KVwin"]
    for rb in range(3):
        P.dma("sp", KVw, KVw[rb * 256:(rb + 1) * 256, :], KVp,
              KVp[rb * 256:(rb + 1) * 256, 0:KPAD + SEQ][:, bass.ds(kd, SEQ)], nodep_out=True)
    for rb in range(4):
        P.dma("sp", dr["Xown"], dr["Xown"][rb * 256:(rb + 1) * 256, :].rearrange("r (s c) -> r s c", c=512), dr["X"],
              dyn3(dr["X"], rb * 256, (rb + 1) * 256, HALO, XPAD), nodep_out=True)


class FAttn2(FAttn):
    def __init__(self, P, dr, jl):
        FAttn.__init__(self, P, dr, jl)
        self.KV = dr["KVwin"]
        self.Q, self.G, self.Y = dr["Qown"], dr["Gown"], dr["Yown"]
        self.kvalD = dr["kval"]

    def load_q(self, qt):
        P = self.P
        c0 = ((qt - 3) // 4) * 512
        for i2 in range(3):
            s = self.rot("st", self.st)
            for k in range(2):
                hh = self.heads[2 * i2 + k]
                r0 = (self.g * 6 + hh) * 64
                P.dma("sp", s, s[0:64, k * 512:(k + 1) * 512], self.Q, self.Q[r0:r0 + 64, c0:c0 + 512], nodep_out=(k > 0))
            for k in range(2):
                P.op("dve", lambda e, s=s, k=k, i2=i2: e.tensor_copy(
                    out=self.qraw[:, 2 * i2 + k, :], in_=rev_ap(s[0:64, k * 512:(k + 1) * 512], 512)), [s], [self.qraw])

    def load_gate(self, qt, h, br):
        P = self.P
        c0 = ((qt - 3) // 4) * 512
        gt = self.rot("gt", self.gt)
        g = self.G[:]
        row = (self.g * 6 + 3 * self.tr + h) * 3 + br
        P.dma("sp", gt, gt[:], self.G, bass.AP(tensor=g.tensor, offset=row * NTOK + c0, ap=[[0, 64], [1, 512]]))
        gr = self.rot("gtr", self.gtr)
        P.op("dve", lambda e: e.tensor_copy(out=gr[:], in_=rev_ap(gt[:], 512)), [gt], [gr])
        return gr

    def store_y(self, qt):
        P = self.P
        c0 = ((qt - 3) // 4) * 512
        bufs = [self.rdt, self.wt, self.tm]
        for h in range(3):
            b = bufs[h]
            P.op("dve", lambda e, b=b, h=h: e.tensor_copy(out=b[:], in_=rev_ap(self.y[:, h, :], 512)), [self.y], [b])
            r0 = (self.g * 6 + 3 * self.tr + h) * 64
            P.dma("sp", self.Y, self.Y[r0:r0 + 64, c0:c0 + 512], b, b[:], nodep_out=True)

    def topc_ap(self, qt):
        return self.topc[(qt - 3) // 4]

    def cmp_bias(self, cc, h, mixed):
        if cc == 0:
            return (self.kval[:, 12:13], self.kval) if mixed else (self.cfc[:, h:h + 1], self.cfc)
        return None if mixed else (self.cf[:, h:h + 1], self.cf)

    def sel_bias(self, kc, h, near):
        if kc < 12:
            return (self.kval[:, kc:kc + 1], self.kval) if near else (self.vbf[:, kc, h:h + 1], self.vbf)
        return None if near else (self.cf[:, h:h + 1], self.cf)

    def win_bias(self, c):
        return (self.kval[:, c:c + 1], self.kval) if c < 12 else None

    def setup_tables(self):
        FAttn.setup_tables(self)
        P = self.P
        for c in range(12):
            P.op("dve", lambda e, c=c: e.tensor_scalar(out=self.vbf[:, c, :], in0=self.cf[:, 0:3],
                                                       scalar1=self.kval[:, c:c + 1], scalar2=None, op0=ALU.add),
                 [self.cf, self.kval], [self.vbf])
        P.op("dve", lambda e: e.tensor_scalar(out=self.cfc[:], in0=self.cf[:], scalar1=self.kval[:, 12:13], scalar2=None,
                                              op0=ALU.add), [self.cf, self.kval], [self.cfc])

    def run(self):
        P = self.P
        st = self.st
        self.gtr = [P.sb("gtr%d" % i, [64, 512], F32) for i in range(2)]
        self.kval = P.sb("kval", [128, 16], F32)
        self.vbf = P.sb("vbf", [128, 12, 3], F32)
        self.cfc = P.sb("cfc", [128, 6], F32)
        P.dma("sp", self.kval, self.kval[:], self.kvalD, self.kvalD[:])
        P.dma("sp", self.hgs, self.hgs[:], self.hg, self.hg[:])
        P.op("pool", lambda e: e.memset(self.ones[:], 1.0), [], [self.ones])
        P.op("pool", lambda e: e.memset(self.epsT[:], 1e-6), [], [self.epsT])
        P.op("pool", lambda e: e.memset(self.tinyT[:], 1e-30), [], [self.tinyT])
        P.op("pool", lambda e: e.memset(self.vsaug[:], 1.0), [], [self.vsaug])
        P.op("pool", lambda e: e.memset(self.vwaug[:], 1.0), [], [self.vwaug])
        P.op("pool", lambda e: e.memset(self.vcaug[:], 1.0), [], [self.vcaug])
        s = self.rot("st", st)
        P.dma("sp", s, s[:, 0:128], self.identD, self.identD[:])
        P.op("pool", lambda e, s=s: e.tensor_copy(out=self.ident[:], in_=s[:, 0:128]), [s], [self.ident])
        s = self.rot("st", st)
        P.dma("sp", s, s[:, 0:512], self.ovlD, self.ovlD[:].rearrange("p c j -> p (c j)"))
        P.op("pool", lambda e, s=s: e.tensor_copy(out=self.ovl[:].rearrange("p c j -> p (c j)"), in_=s[:, 0:512]),
             [s], [self.ovl])
        kvt = [("ksaug", self.ksaug, [128, SEQ]), ("kwT", self.kwT, [64, SEQ]), ("vsaug", self.vsaug, [128, 64 * 128]),
               ("vwaug", self.vwaug, [128, 64 * 128]), ("kcT", self.kcT, [64, 512]), ("vcaug", self.vcaug, [128, 4 * 128])]
        for g in range(2):
            self.set_combo(g, 0)
            if self.jl == 0:
                self.setup_kv()
                for nm, tl, shp in kvt:
                    key = "kvc_%s_%d" % (nm, g)
                    if key not in self.dr:
                        self.dr[key] = P.dram(key, shp, BF16, kind="Internal")
                    c = self.dr[key]
                    src = tl[:] if len(tl[:].shape) == 2 else tl[:].rearrange("p a b -> p (a b)")
                    P.dma("sp", c, c[:], tl, src)
            else:
                for nm, tl, shp in kvt:
                    c = self.dr["kvc_%s_%d" % (nm, g)]
                    dst = tl[:] if len(tl[:].shape) == 2 else tl[:].rearrange("p a b -> p (a b)")
                    P.dma("sp", tl, dst, c, c[:])
            for tr in range(2):
                self.set_combo(g, tr)
                self.setup_tables()
                for qt in QTILES:
                    self.qtile(qt)


def build_fused2():
    nc = bass.Bass("TRN2", target_bir_lowering=False)
    P = Prog(nc)
    dr = fused_dram(P, r3=True)
    I = lambda n, s: P.dram(n, s, F32, kind="Internal")
    dr["KV"] = I("KVp", [768, KPAD + SEQ])
    dr["KVwin"] = I("KVwin", [768, SEQ])
    dr["Xown"] = I("Xown", [1024, NTOK])
    dr["Qown"] = I("Qown", [768, NTOK])
    dr["Gown"] = I("Gown", [36, NTOK])
    dr["Yown"] = I("Yown", [768, NTOK])
    dr["topc"] = P.dram("topc4", [4, 128, 1024], F32, kind="ExternalInput")
    dr["kval"] = P.dram("kval", [128, 16], F32, kind="ExternalInput")
    dr["out"] = P.dram("xT_own", [1024, NTOK], F32, kind="ExternalOutput")
    dr["kd"] = (nc.partition_id() % 4) * 512
    P.push_scope()
    FTok2(P, "L1", dr).run()
    P.pop_scope()
    extract_own(P, dr)
    P.push_scope()
    FTok2(P, "P0", dr).run()
    P.pop_scope()
    for j in range(2):
        P.push_scope()
        FAttn2(P, dr, j).run()
        P.pop_scope()
        P.push_scope()
        FTok2(P, "L3" if j == 0 else "L5", dr).run()
        P.pop_scope()
    P.wait_all("sp", [dr["out"]])
    P.finalize()
    return nc, P


from concourse.bass_utils import run_bass_kernel_spmd


def attn_hg(inp, jlayer):
    hg = np.zeros((128, 8), np.float32)
    hg[:, 0] = np.tile(inp["b_q_gain"][jlayer], 2)
    for i in range(3):
        hg[:, 1 + i] = np.tile(inp["k_gain"][i], 2)
    return hg


def kernel(**inputs):
    inp = {k: np.ascontiguousarray(np.asarray(v, dtype=np.float32)) for k, v in inputs.items()}
    common = dict(host_consts())
    del common["topc"]
    common.update(gains=host_gains(inp), hg=host_hg(inp), convw=host_convw(inp), mem_w_kv=inp["mem_w_kv"],
                  w_out=inp["w_out"], w_up=inp["w_up"], w_down=inp["w_down"], b_w_in=inp["b_w_in"],
                  a_w_in=inp["a_w_in"], w_kv=inp["w_kv_shared"], ahg0=attn_hg(inp, 0), ahg1=attn_hg(inp, 1),
                  pos=np.ascontiguousarray(inp["cmp_pos"].transpose(0, 2, 1)), cmp_w1=inp["cmp_w1"],
                  cmp_w2=inp["cmp_w2"], rel_bias=inp["rel_bias"])
    per_b = []
    for b in range(2):
        xin = np.zeros((1024, XPAD), np.float32)
        xin[:, HALO:] = inp["x"][b].T
        per_b.append(dict(xin=xin, memT=np.ascontiguousarray(inp["mem"][b].T)))
    per_k = []
    for k in range(4):
        tc, kval = host_core_tables(k)
        per_k.append(dict(topc4=tc, kval=kval))
    maps = []
    for c in range(8):
        m = dict(common)
        m.update(per_b[c // 4])
        m.update(per_k[c % 4])
        maps.append(m)
    nc = build_fused2()[0]
    r = run_bass_kernel_spmd(nc, maps, core_ids=list(range(8))).results
    out = np.zeros((2, SEQ, 1024), np.float32)
    for c in range(8):
        b, k = c // 4, c % 4
        o = r[c]["xT_own"]
        for j in range(4):
            t0 = 512 * (4 * j + k)
            out[b, t0:t0 + 512, :] = o[:, j * 512:(j + 1) * 512].T
    return out
```

```python
import numpy as np
import concourse.bass as bass
import concourse.mybir as mybir
from contextlib import ExitStack

F32 = mybir.dt.float32
BF16 = mybir.dt.bfloat16
AF = mybir.ActivationFunctionType
ALU = mybir.AluOpType
AX = mybir.AxisListType

ENGS = ("pe", "act", "dve", "pool", "sp")


class T:
    __slots__ = ("h", "name", "lw", "rd", "dsem", "scoped")

    def __init__(self, h, name):
        self.h = h
        self.name = name
        self.lw = None
        self.rd = []
        self.dsem = None
        self.scoped = False

    def __getitem__(self, k):
        return self.h[k]


class Prog:
    def __init__(self, nc):
        self.nc = nc
        self.es = ExitStack()
        self.ins = {e: [] for e in ENGS}
        self.nsem = 0
        self.dsems = []
        self.free_dsems = []
        self.scope_dsems = []
        self.marked = set()
        self.rw = {e: {} for e in ENGS}
        self.cur = self.es
        self.uid = 0
        self.last_tok = {e: None for e in ENGS}
        self.epoch = 0
        self.ep_of = {}

    def _prune(self, eng, deps):
        out = []
        w = self.rw[eng]
        for d in deps:
            key = (d[0], d[1]); val = d[2]
            if w.get(key, -1) >= val:
                continue
            w[key] = val
            out.append(d)
        return out

    def sb(self, name, shape, dt):
        self.uid += 1
        name = "%s_%d" % (name, self.uid)
        t = T(self.cur.enter_context(self.nc.sbuf_tensor(name, list(shape), dt)), name)
        t.scoped = self.cur is not self.es
        return t

    def ps(self, name, shape, dt=F32):
        self.uid += 1
        name = "%s_%d" % (name, self.uid)
        return T(self.cur.enter_context(self.nc.psum_tensor(name, list(shape), dt)), name)

    def push_scope(self):
        self.cur = ExitStack()

    def pop_scope(self):
        self.barrier()
        self.cur.close()
        self.cur = self.es
        self.free_dsems.extend(self.scope_dsems)
        self.scope_dsems = []

    def barrier(self):
        for e in ENGS:
            deps = [t for e2, t in self.last_tok.items() if e2 != e and t is not None]
            deps += [("d", s, c[1]) for s, c in enumerate(self.dsems) if c[1] > 0]
            deps = self._prune(e, deps)
            self.ins[e].append(dict(deps=deps, fn=None, dma=None))

    def dram(self, name, shape, dt, kind="Internal"):
        return T(self.nc.dram_tensor(name, list(shape), dt, kind=kind), name)

    def _new_dsem(self, scoped=False):
        if scoped and self.free_dsems:
            i = self.free_dsems.pop()
        else:
            h = self.es.enter_context(self.nc.semaphore("d%d" % len(self.dsems)))
            self.dsems.append([h, 0])
            i = len(self.dsems) - 1
        if scoped:
            self.scope_dsems.append(i)
        return i

    def _deps(self, eng, reads, writes):
        deps = []
        for t in reads:
            if t.lw is not None:
                deps.append(t.lw)
        for t in writes:
            if t.lw is not None:
                deps.append(t.lw)
            deps.extend(t.rd)
        out = []
        for d in deps:
            if d[0] == "e":
                if d[1] == eng:
                    continue
            out.append(d)
        if eng != "pe":
            for t in reads:
                if t.lw is not None and t.lw[0] == "e" and t.lw[1] == eng:
                    out.append(t.lw)
        return out

    def op(self, eng, fn, reads=(), writes=()):
        deps = self._prune(eng, self._deps(eng, reads, writes))
        idx = len(self.ins[eng])
        self.ins[eng].append(dict(deps=deps, fn=fn, dma=None))
        tok = ("e", eng, idx)
        self.last_tok[eng] = tok
        self.ep_of[(eng, idx)] = self.epoch
        for t in reads:
            t.rd = [r for r in t.rd if not (r[0] == "e" and r[1] == eng)]
            t.rd.append(tok)
        for t in writes:
            t.lw = tok
            t.rd = []
        return tok

    def dma(self, eng, out_t, out_ap, in_t, in_ap, nodep_out=False, **kw):
        reads = [in_t] if in_t is not None else []
        writes = [out_t] if out_t is not None else []
        deps = []
        for t in reads:
            if t.lw is not None:
                deps.append(t.lw)
        for t in writes:
            if nodep_out:
                continue
            if t.lw is not None:
                deps.append(t.lw)
            deps.extend(t.rd)
        owner = out_t if (out_t is not None and out_t.dsem is not None) else None
        if owner is None:
            owner = out_t if out_t is not None else in_t
        if owner.dsem is None:
            owner.dsem = self._new_dsem(scoped=getattr(owner, "scoped", False))
        s = owner.dsem
        self.dsems[s][1] += 16
        tok = ("d", s, self.dsems[s][1])
        deps = self._prune(eng, deps)
        self.ins[eng].append(dict(deps=deps, fn=None, dma=(out_ap, in_ap, s, kw)))
        for t in reads:
            t.rd.append(tok)
        for t in writes:
            t.lw = tok
            t.rd = []
        return tok

    def wait_all(self, eng, tiles):
        deps = []
        for t in tiles:
            if t.lw is not None:
                deps.append(t.lw)
            deps.extend(t.rd)
        deps = self._prune(eng, deps)
        self.ins[eng].append(dict(deps=deps, fn=None, dma=None))

    def finalize(self):
        nc = self.nc
        for e in ENGS:
            for rec in self.ins[e]:
                for d in rec["deps"]:
                    if d[0] == "e":
                        self.marked.add((d[1], d[2]))
        count_at = {}
        for e in ENGS:
            c = {}
            for i, rec in enumerate(self.ins[e]):
                if (e, i) in self.marked:
                    ep = self.ep_of[(e, i)]
                    c[ep] = c.get(ep, 0) + 1
                    count_at[(e, i)] = c[ep]
        esem = {(e, ep): self.es.enter_context(nc.semaphore("s_%s_%d" % (e, ep)))
                for e in ENGS for ep in range(self.epoch + 1)}
        engobj = {"pe": "tensor", "act": "scalar", "dve": "vector", "pool": "gpsimd", "sp": "sync"}
        block = self.es.enter_context(nc.Block())
        stats = {}

        def make_body(e):
            def body(eng):
                waited = {}
                nwait = 0
                for i, rec in enumerate(self.ins[e]):
                    need = {}
                    for d in rec["deps"]:
                        if d[0] == "e":
                            key = ("e", (d[1], self.ep_of[(d[1], d[2])])); val = count_at[(d[1], d[2])]
                        else:
                            key = ("d", d[1]); val = d[2]
                        if waited.get(key, 0) >= val:
                            continue
                        if need.get(key, 0) < val:
                            need[key] = val
                    for key, val in need.items():
                        h = esem[key[1]] if key[0] == "e" else self.dsems[key[1]][0]
                        eng.wait_ge(h, val)
                        waited[key] = val
                        nwait += 1
                    if rec.get("cc") is not None:
                        rec["cc"](eng).then_inc(self.dsems[rec["ccsem"]][0], 16)
                    elif rec["dma"] is not None:
                        out_ap, in_ap, s, kw = rec["dma"]
                        try:
                            eng.dma_start(out=out_ap, in_=in_ap, **kw).then_inc(self.dsems[s][0], 16)
                        except Exception:
                            print("DMA FAILED:", e, "out=", out_ap, "in=", in_ap)
                            raise
                    elif rec["fn"] is not None:
                        ins = rec["fn"](eng)
                        if (e, i) in self.marked:
                            ins.then_inc(esem[(e, self.ep_of[(e, i)])], 1)
                stats[e] = (len(self.ins[e]), nwait)
            return body

        for e in ENGS:
            if self.ins[e]:
                getattr(block, engobj[e])(make_body(e))
        self.stats = stats
        self.es.close()


NTOK = 2048
HALO = 4
NCOL = NTOK + HALO
RANGES = [(0, 1028), (1028, 1024)]
EPS = 1e-6


def tiles_of(n):
    k = (n + 511) // 512
    out = []
    o = 0
    for i in range(k):
        w = (n - o + (k - i) - 1) // (k - i)
        out.append((o, w))
        o += w
    return out


class TokStage:
    def __init__(self, P, mode):
        self.P = P
        self.mode = mode
        nc = P.nc
        D = lambda n, s: P.dram(n, s, F32, kind="ExternalInput")
        O = lambda n, s: P.dram(n, s, F32, kind="ExternalOutput")
        self.xT = D("xT", [1024, NCOL])
        self.memT = D("memT", [1024, 256])
        self.gains = D("gains", [128, 8 * 16])
        self.hg = D("hg", [128, 16])
        self.flag = D("flag", [128, 1])
        self.mem_w_kv = D("mem_w_kv", [4, 1024, 512])
        self.w_out = D("w_out", [4, 1024, 1024])
        self.w_up = D("w_up", [4, 1024, 4096])
        self.w_down = D("w_down", [4, 4096, 1024])
        self.b_w_in = D("b_w_in", [2, 1024, 1060])
        if mode == "L1":
            self.a_w_in = D("a_w_in", [2, 1024, 2560])
            self.convw = D("convw", [128, 2 * 6 * 3])
            self.w_kv = D("w_kv", [1024, 768])
            self.kvT_o = O("kvT_o", [768, NCOL])
        else:
            self.ytokT = D("ytokT", [768, NCOL])
        self.xT_o = O("xT_o", [1024, NCOL])
        if mode != "L5":
            self.qT_o = O("qT_o", [768, NCOL])
            self.gT_o = O("gT_o", [36, NCOL])

        self.alloc()

    def alloc(self):
        P = self.P
        sb = P.sb
        self.x = sb("x", [128, 8, NCOL], F32)
        self.h = sb("h", [128, 8, 1028], BF16)
        self.y = sb("y", [128, 8, 1028], BF16)
        self.act = sb("act", [128, 8, 1028], BF16)
        self.u = sb("u", [128, 2, 1028], F32)
        self.v = sb("v", [128, 2, 1030], F32)
        self.qm = sb("qm", [128, 2, 1028], F32)
        self.carry = sb("carry", [128, 6, 2], F32)
        self.rstd = sb("rstd", [128, 512], F32)
        self.sq = [sb("sq%d" % i, [128, 512], BF16) for i in range(2)]
        self.tmp = [sb("tmp%d" % i, [128, 512], F32) for i in range(3)]
        self.ost = [sb("ost%d" % i, [128, 512], F32) for i in range(2)]
        self.wb = [sb("wb%d" % i, [128, 8, 256], BF16) for i in range(4)]
        self.memh = sb("memh", [128, 8, 256], BF16)
        self.kraw = sb("kraw", [128, 2, 256], F32)
        self.memk = sb("memk", [128, 2, 256], BF16)
        self.vaug = sb("vaug", [128, 2, 4, 128], BF16)
        self.qn = sb("qn", [128, 512], BF16)
        self.eT = [sb("eT%d" % i, [128, 512], BF16) for i in range(2)]
        self.rd = sb("rd", [64, 512], F32)
        self.g = sb("g", [128, 8 * 16], F32)
        self.hgs = sb("hgs", [128, 16], F32)
        self.flg = sb("flg", [128, 1], F32)
        self.cw = sb("cw", [128, 36], F32)
        self.ones = sb("ones", [128, 128], BF16)
        self.bd = sb("bd", [128, 128], BF16)
        self.epsT = sb("epsT", [128, 1], F32)
        self.pp = [P.ps("pp%d" % i, [128, 512]) for i in range(4)]
        self.pl = [P.ps("pl%d" % i, [128, 512]) for i in range(2)]
        self.po = P.ps("po", [128, 512])
        self.pn = P.ps("pn", [128, 512])
        self.cnt = dict(wf=0, wb=0, pp=0, pl=0, sq=0, tmp=0, ost=0, eT=0)

    def rot(self, name, lst):
        i = self.cnt[name]
        self.cnt[name] += 1
        return lst[i % len(lst)]

    def stream(self, wT, w_ap, ncols):
        P = self.P
        wb = self.rot("wb", self.wb)
        P.dma("pool", wb, wb[:, :, 0:ncols], wT, w_ap.rearrange("(c p) n -> p c n", p=128))
        return wb

    def proj(self, wb, sub, msz, rhs_t, rhs_fn, tiles, consumer):
        P = self.P
        for (t0, w) in tiles:
            ps = self.rot("pp", self.pp)
            for kc in range(8):
                P.op("pe", lambda e, ps=ps, kc=kc, t0=t0, w=w: e.matmul(
                    ps[0:msz, 0:w], lhsT=wb[:, kc, sub * 128: sub * 128 + msz], rhs=rhs_fn(kc, t0, w),
                    start=(kc == 0), stop=(kc == 7)), [wb, rhs_t], [ps])
            consumer(ps, t0, w)

    def setup(self):
        P = self.P
        P.dma("sp", self.g, self.g[:], self.gains, self.gains[:])
        P.dma("sp", self.hgs, self.hgs[:], self.hg, self.hg[:])
        P.dma("sp", self.flg, self.flg[:], self.flag, self.flag[:])
        if self.mode == "L1":
            P.dma("sp", self.cw, self.cw[:], self.convw, self.convw[:])
        for c in range(8):
            P.dma("sp", self.x, self.x[:, c, :], self.xT, self.xT[c * 128:(c + 1) * 128, :], nodep_out=True)
            P.dma("sp", self.u, self.memx(c, 0, 256), self.memT, self.memT[c * 128:(c + 1) * 128, :], nodep_out=True)
        P.op("pool", lambda e: e.memset(self.ones[:], 1.0), [], [self.ones])
        P.op("pool", lambda e: e.memset(self.bd[:], 0.0), [], [self.bd])
        P.op("pool", lambda e: e.memset(self.bd[0:64, 0:64], 1.0), [], [self.bd])
        P.op("pool", lambda e: e.memset(self.bd[64:128, 64:128], 1.0), [], [self.bd])
        P.op("pool", lambda e: e.memset(self.epsT[:], EPS), [], [self.epsT])
        P.op("pool", lambda e: e.memset(self.vaug[:], 1.0), [], [self.vaug])
        P.op("pool", lambda e: e.memset(self.carry[:], 0.0), [], [self.carry])
        self.norm(self.u, self.memx, 256, [(0, 256)], 15, self.memh, 0)

    def gcol(self, which, c):
        return self.g[:, which * 8 + c: which * 8 + c + 1]

    def memx(self, c, t0, w):
        return self.u[:, c // 4, (c % 4) * 256 + t0: (c % 4) * 256 + t0 + w]

    def xs(self, c0):
        return lambda c, t0, w: self.x[:, c, c0 + t0: c0 + t0 + w]

    def norm(self, src, src_fn, n, tiles, which, dst, d0):
        P = self.P
        for (t0, w) in tiles:
            for c in range(8):
                sq = self.rot("sq", self.sq)
                P.op("act", lambda e, sq=sq, c=c, t0=t0, w=w: e.activation(
                    out=sq[:, 0:w], in_=src_fn(c, t0, w), func=AF.Square), [src], [sq])
                P.op("pe", lambda e, sq=sq, c=c, w=w: e.matmul(
                    self.pn[:, 0:w], lhsT=self.ones[:], rhs=sq[:, 0:w], start=(c == 0), stop=(c == 7)),
                    [self.ones, sq], [self.pn])
            P.op("act", lambda e, w=w: e.activation(out=self.rstd[:, 0:w], in_=self.pn[:, 0:w], func=AF.Ln,
                                                    bias=self.epsT[:], scale=1.0 / 1024.0),
                 [self.pn, self.epsT], [self.rstd])
            P.op("act", lambda e, w=w: e.activation(out=self.rstd[:, 0:w], in_=self.rstd[:, 0:w], func=AF.Exp, scale=-0.5),
                 [self.rstd], [self.rstd])
            for c in range(8):
                eng = "dve"
                P.op(eng, lambda e, c=c, t0=t0, w=w: e.scalar_tensor_tensor(
                    out=dst[:, c, d0 + t0: d0 + t0 + w], in0=src_fn(c, t0, w),
                    scalar=self.gcol(which, c), in1=self.rstd[:, 0:w], op0=ALU.mult, op1=ALU.mult),
                    [src, self.g, self.rstd], [dst])

    def headnorm(self, src_t, src_ap_fn, w, gain_ap, dst_t, dst_ap):
        P = self.P
        sq = self.rot("sq", self.sq)
        P.op("act", lambda e, sq=sq: e.activation(out=sq[:, 0:w], in_=src_ap_fn(), func=AF.Square), [src_t], [sq])
        P.op("pe", lambda e, sq=sq: e.matmul(self.pn[:, 0:w], lhsT=self.bd[:], rhs=sq[:, 0:w], start=True, stop=True),
             [self.bd, sq], [self.pn])
        P.op("act", lambda e: e.activation(out=self.rstd[:, 0:w], in_=self.pn[:, 0:w], func=AF.Ln,
                                           bias=self.epsT[:], scale=1.0 / 64.0), [self.pn, self.epsT], [self.rstd])
        P.op("act", lambda e: e.activation(out=self.rstd[:, 0:w], in_=self.rstd[:, 0:w], func=AF.Exp, scale=-0.5),
             [self.rstd], [self.rstd])
        P.op("dve", lambda e: e.scalar_tensor_tensor(out=dst_ap, in0=src_ap_fn(), scalar=gain_ap,
                                                     in1=self.rstd[:, 0:w], op0=ALU.mult, op1=ALU.mult),
             [src_t, self.hgs, self.rstd], [dst_t])

    def mem_kv(self, l):
        P = self.P
        wb = self.stream(self.mem_w_kv, self.mem_w_kv[l, :, 0:256], 256)
        for sub in range(2):
            def cons(ps, t0, w, sub=sub):
                P.op("act", lambda e: e.activation(out=self.kraw[:, sub, 0:256], in_=ps[:, 0:256], func=AF.Copy),
                     [ps], [self.kraw])
            self.proj(wb, sub, 128, self.memh, lambda kc, t0, w: self.memh[:, kc, t0:t0 + w], [(0, 256)], cons)
            self.headnorm(self.kraw, lambda sub=sub: self.kraw[:, sub, 0:256], 256,
                          self.hgs[:, 4 + l: 5 + l], self.memk, self.memk[:, sub, 0:256])
        wb = self.stream(self.mem_w_kv, self.mem_w_kv[l, :, 256:512], 256)
        for mc in range(2):
            ps = self.rot("pp", self.pp)
            for kc in range(8):
                P.op("pe", lambda e, ps=ps, kc=kc, mc=mc: e.matmul(
                    ps[:, 0:256], lhsT=self.memh[:, kc, mc * 128:(mc + 1) * 128], rhs=wb[:, kc, 0:256],
                    start=(kc == 0), stop=(kc == 7)), [self.memh, wb], [ps])
            for hh in range(4):
                P.op("act", lambda e, ps=ps, mc=mc, hh=hh: e.activation(
                    out=self.vaug[:, mc, hh, 0:64], in_=ps[:, hh * 64:(hh + 1) * 64], func=AF.Copy), [ps], [self.vaug])

    def mem_attn(self, l, n, tiles):
        P = self.P
        for (t0, w) in tiles:
            for j in range(2):
                self.headnorm(self.qm, lambda j=j, t0=t0, w=w: self.qm[:, j, t0:t0 + w], w,
                              self.hgs[:, l: l + 1], self.qn, self.qn[:, 0:w])
                for hh in range(2):
                    b0 = 64 * hh
                    hd = 2 * j + hh
                    ets = []
                    for mc in range(2):
                        pl = self.rot("pl", self.pl)
                        P.op("pe", lambda e, pl=pl, mc=mc, b0=b0, j=j, w=w: e.matmul(
                            pl[:, 0:w], lhsT=self.memk[b0:b0 + 64, j, mc * 128:(mc + 1) * 128],
                            rhs=self.qn[b0:b0 + 64, 0:w], start=True, stop=True), [self.memk, self.qn], [pl])
                        eT = self.rot("eT", self.eT)
                        P.op("act", lambda e, pl=pl, eT=eT, w=w: e.activation(
                            out=eT[:, 0:w], in_=pl[:, 0:w], func=AF.Exp, scale=0.125), [pl], [eT])
                        ets.append(eT)
                    for mc in range(2):
                        P.op("pe", lambda e, mc=mc, hd=hd, w=w, eT=ets[mc]: e.matmul(
                            self.po[:, 0:w], lhsT=self.vaug[:, mc, hd, :], rhs=eT[:, 0:w],
                            start=(mc == 0), stop=(mc == 1)), [self.vaug, ets[mc]], [self.po])
                    P.op("act", lambda e, w=w: e.activation(out=self.rd[0:64, 0:w], in_=self.po[64:128, 0:w], func=AF.Ln),
                         [self.po], [self.rd])
                    P.op("act", lambda e, w=w: e.activation(out=self.rd[0:64, 0:w], in_=self.rd[0:64, 0:w], func=AF.Exp,
                                                            scale=-1.0), [self.rd], [self.rd])
                    P.op("dve", lambda e, w=w, b0=b0, j=j, t0=t0: e.tensor_tensor(
                        out=self.y[b0:b0 + 64, 6 + j, t0:t0 + w], in0=self.po[0:64, 0:w], in1=self.rd[0:64, 0:w],
                        op=ALU.mult), [self.po, self.rd], [self.y])

    def add_into_x(self, c0):
        P = self.P

        def mk(ch):
            def cons(ps, t0, w):
                P.op("dve", lambda e: e.tensor_tensor(
                    out=self.x[:, ch, c0 + t0: c0 + t0 + w], in0=ps[:, 0:w], in1=self.x[:, ch, c0 + t0: c0 + t0 + w],
                    op=ALU.add), [ps, self.x], [self.x])
            return cons
        return mk

    def out_proj(self, l, c0, n, tiles):
        mk = self.add_into_x(c0)
        for blk in range(4):
            wb = self.stream(self.w_out, self.w_out[l, :, blk * 256:(blk + 1) * 256], 256)
            for sub in range(2):
                self.proj(wb, sub, 128, self.y, lambda kc, t0, w: self.y[:, kc, t0:t0 + w], tiles, mk(blk * 2 + sub))

    def mlp(self, l, c0, n, tiles):
        P = self.P
        self.norm(self.x, self.xs(c0), n, tiles, 4 + l, self.h, 0)
        mk = self.add_into_x(c0)
        for sg in range(4):
            for blk in range(4):
                wb = self.stream(self.w_up, self.w_up[l, :, sg * 1024 + blk * 256: sg * 1024 + (blk + 1) * 256], 256)
                for sub in range(2):
                    def cons(ps, t0, w, fc=blk * 2 + sub):
                        tmp = self.rot("tmp", self.tmp)
                        P.op("act", lambda e: e.activation(out=tmp[:, 0:w], in_=ps[:, 0:w], func=AF.Relu), [ps], [tmp])
                        P.op("dve", lambda e: e.tensor_tensor(out=self.act[:, fc, t0:t0 + w], in0=tmp[:, 0:w],
                                                              in1=tmp[:, 0:w], op=ALU.mult), [tmp], [self.act])
                    self.proj(wb, sub, 128, self.h, lambda kc, t0, w: self.h[:, kc, t0:t0 + w], tiles, cons)
            for blk in range(4):
                wb = self.stream(self.w_down, self.w_down[l, sg * 1024:(sg + 1) * 1024, blk * 256:(blk + 1) * 256], 256)
                for sub in range(2):
                    self.proj(wb, sub, 128, self.act, lambda kc, t0, w: self.act[:, kc, t0:t0 + w], tiles,
                              mk(blk * 2 + sub))

    def mix_a(self, l, ri, c0, n, tiles):
        P = self.P
        W = self.a_w_in
        hrhs = lambda kc, t0, w: self.h[:, kc, t0:t0 + w]
        for i in range(3):
            wb = self.stream(W, W[l, :, 1536 + 256 * i: 1536 + 256 * (i + 1)], 256)
            for sub in range(2):
                def cons(ps, t0, w, sub=sub):
                    P.op("act", lambda e: e.activation(out=self.u[:, sub, t0:t0 + w], in_=ps[:, 0:w], func=AF.Copy),
                         [ps], [self.u])
                self.proj(wb, sub, 128, self.h, hrhs, tiles, cons)
            wb = self.stream(W, W[l, :, 768 + 256 * i: 768 + 256 * (i + 1)], 256)
            for sub in range(2):
                def cons(ps, t0, w, sub=sub):
                    P.op("dve", lambda e: e.tensor_tensor(out=self.v[:, sub, 2 + t0: 2 + t0 + w], in0=ps[:, 0:w],
                                                          in1=self.u[:, sub, t0:t0 + w], op=ALU.mult),
                         [ps, self.u], [self.v])
                self.proj(wb, sub, 128, self.h, hrhs, tiles, cons)
            for sub in range(2):
                ch = 2 * i + sub
                if ri == 0:
                    P.op("pool", lambda e, sub=sub: e.memset(self.v[:, sub, 0:2], 0.0), [], [self.v])
                    self.halo_fix(sub)
                    P.op("dve", lambda e, sub=sub, ch=ch: e.tensor_copy(out=self.carry[:, ch, :],
                                                                        in_=self.v[:, sub, n:n + 2]),
                         [self.v], [self.carry])
                else:
                    P.op("dve", lambda e, sub=sub, ch=ch: e.tensor_copy(out=self.v[:, sub, 0:2],
                                                                        in_=self.carry[:, ch, :]),
                         [self.carry], [self.v])
                cwc = lambda k, ch=ch: self.cw[:, l * 18 + ch * 3 + k: l * 18 + ch * 3 + k + 1]
                P.op("dve", lambda e, sub=sub, cwc=cwc: e.tensor_scalar(
                    out=self.u[:, sub, 0:n], in0=self.v[:, sub, 0:n], scalar1=cwc(0), scalar2=None, op0=ALU.mult),
                    [self.v, self.cw], [self.u])
                for k in (1, 2):
                    P.op("dve", lambda e, sub=sub, cwc=cwc, k=k: e.scalar_tensor_tensor(
                        out=self.u[:, sub, 0:n], in0=self.v[:, sub, k:k + n], scalar=cwc(k), in1=self.u[:, sub, 0:n],
                        op0=ALU.mult, op1=ALU.add), [self.v, self.cw, self.u], [self.u])
            wb = self.stream(W, W[l, :, 256 * i: 256 * (i + 1)], 256)
            for sub in range(2):
                def cons(ps, t0, w, sub=sub, ch=2 * i + sub):
                    P.op("dve", lambda e: e.tensor_tensor(out=self.y[:, ch, t0:t0 + w], in0=ps[:, 0:w],
                                                          in1=self.u[:, sub, t0:t0 + w], op=ALU.mult),
                         [ps, self.u], [self.y])
                self.proj(wb, sub, 128, self.h, hrhs, tiles, cons)
        wb = self.stream(W, W[l, :, 2304:2560], 256)
        self.qmem_proj(wb, tiles)

    def halo_fix(self, sub):
        self.P.op("dve", lambda e: e.tensor_scalar(
            out=self.v[:, sub, 2:2 + HALO], in0=self.v[:, sub, 2:2 + HALO], scalar1=self.flg[:, 0:1],
            scalar2=None, op0=ALU.mult), [self.v, self.flg], [self.v])

    def store_tile(self, dstT, r, msz, c0, t0, w, ost):
        self.P.dma("sp", dstT, dstT[r:r + msz, c0 + t0: c0 + t0 + w], ost, ost[0:msz, 0:w], nodep_out=True)

    def ytok_src(self, ch, c0, t0, w):
        return self.ytokT[ch * 128:(ch + 1) * 128, c0 + t0: c0 + t0 + w]

    def qmem_proj(self, wb, tiles):
        P = self.P
        for sub in range(2):
            def cons(ps, t0, w, sub=sub):
                P.op("act", lambda e: e.activation(out=self.qm[:, sub, t0:t0 + w], in_=ps[:, 0:w], func=AF.Copy),
                     [ps], [self.qm])
            self.proj(wb, sub, 128, self.h, lambda kc, t0, w: self.h[:, kc, t0:t0 + w], tiles, cons)

    def store_proj(self, wT, w_ap, ncols, dstT, row0, c0, tiles, func=AF.Copy):
        P = self.P
        wb = self.stream(wT, w_ap, ncols)
        nsub = (ncols + 127) // 128
        for sub in range(nsub):
            msz = min(128, ncols - sub * 128)

            def cons(ps, t0, w, sub=sub, msz=msz):
                ost = self.rot("ost", self.ost)
                P.op("act", lambda e: e.activation(out=ost[0:msz, 0:w], in_=ps[0:msz, 0:w], func=func), [ps], [ost])
                r = row0 + sub * 128
                self.store_tile(dstT, r, msz, c0, t0, w, ost)
            self.proj(wb, sub, msz, self.h, lambda kc, t0, w: self.h[:, kc, t0:t0 + w], tiles, cons)

    def pre_b(self, j, c0, n, tiles):
        W = self.b_w_in
        self.norm(self.x, self.xs(c0), n, tiles, 2 + j, self.h, 0)
        for i in range(3):
            self.store_proj(W, W[j, :, 256 * i:256 * (i + 1)], 256, self.qT_o, 256 * i, c0, tiles)
        self.store_proj(W, W[j, :, 768:804], 36, self.gT_o, 0, c0, tiles, func=AF.Sigmoid)

    def post_b(self, j, c0, n, tiles):
        P = self.P
        W = self.b_w_in
        l = 2 + j
        self.norm(self.x, self.xs(c0), n, tiles, l, self.h, 0)
        wb = self.stream(W, W[j, :, 804:1060], 256)
        self.qmem_proj(wb, tiles)
        for ch in range(6):
            for (t0, w) in tiles:
                tmp = self.rot("tmp", self.tmp)
                P.dma("sp", tmp, tmp[:, 0:w], self.ytokT, self.ytok_src(ch, c0, t0, w))
                P.op("act", lambda e, tmp=tmp, ch=ch, t0=t0, w=w: e.activation(out=self.y[:, ch, t0:t0 + w],
                                                                               in_=tmp[:, 0:w], func=AF.Copy), [tmp], [self.y])
        self.mem_attn(l, n, tiles)
        self.out_proj(l, c0, n, tiles)
        self.mlp(l, c0, n, tiles)

    def store_x(self):
        P = self.P
        for c in range(8):
            P.dma("sp", self.xT_o, self.xT_o[c * 128:(c + 1) * 128, :], self.x, self.x[:, c, :], nodep_out=True)

    def build(self):
        P = self.P
        self.setup()
        if self.mode == "L1":
            for l in range(2):
                self.mem_kv(l)
                for ri, (c0, n) in enumerate(RANGES):
                    tiles = tiles_of(n)
                    self.norm(self.x, self.xs(c0), n, tiles, l, self.h, 0)
                    self.mix_a(l, ri, c0, n, tiles)
                    self.mem_attn(l, n, tiles)
                    self.out_proj(l, c0, n, tiles)
                    self.mlp(l, c0, n, tiles)
            for ri, (c0, n) in enumerate(RANGES):
                tiles = tiles_of(n)
                self.norm(self.x, self.xs(c0), n, tiles, 8, self.h, 0)
                for i in range(3):
                    self.store_proj(self.w_kv, self.w_kv[:, 256 * i:256 * (i + 1)], 256, self.kvT_o, 256 * i, c0, tiles)
                self.pre_b(0, c0, n, tiles)
            self.store_x()
            outs = [self.kvT_o, self.qT_o, self.gT_o, self.xT_o]
        elif self.mode == "L3":
            self.mem_kv(2)
            for ri, (c0, n) in enumerate(RANGES):
                tiles = tiles_of(n)
                self.post_b(0, c0, n, tiles)
            for ri, (c0, n) in enumerate(RANGES):
                tiles = tiles_of(n)
                self.pre_b(1, c0, n, tiles)
            self.store_x()
            outs = [self.qT_o, self.gT_o, self.xT_o]
        else:
            self.mem_kv(3)
            for ri, (c0, n) in enumerate(RANGES):
                tiles = tiles_of(n)
                self.post_b(1, c0, n, tiles)
            self.store_x()
            outs = [self.xT_o]
        P.wait_all("sp", outs)
        P.finalize()


def build_tok(mode):
    nc = bass.Bass("TRN2", target_bir_lowering=False)
    P = Prog(nc)
    st = TokStage(P, mode)
    st.build()
    return nc, P


def host_gains(inp):
    g = np.zeros((16, 1024), np.float32)
    g[0:4] = inp["mix_norm"]
    g[4:8] = inp["mlp_norm"]
    g[8] = inp["kv_norm"]
    g[15] = inp["mem_norm"]
    return np.ascontiguousarray(g.reshape(16, 8, 128).transpose(2, 0, 1).reshape(128, 128))


def host_hg(inp):
    hg = np.zeros((128, 16), np.float32)
    for l in range(4):
        hg[:, l] = np.tile(inp["mem_q_gain"][l], 2)
        hg[:, 4 + l] = np.tile(inp["mem_k_gain"][l], 2)
    return hg


def host_convw(inp):
    w = inp["a_conv_w"].reshape(2, 3, 6, 128)
    return np.ascontiguousarray(w.transpose(3, 0, 2, 1).reshape(128, 36))

import math

SEQ = 8192
QT = 512
NQT = SEQ // QT
DC, LC = 3040, 5104
DW, LW = 1023, 1536
DS, LS = 1407, 1920
LTOT = LC + LW + LS
LU, LTW, LTS = 3072, 1408, 1792
MASKV = -30000.0


def rel_bucket_np(d):
    n = np.maximum(d, 0)
    nf = np.maximum(n, 1).astype(np.float32)
    large = 16 + (np.log(nf / np.float32(16)) / np.float32(math.log(1024 / 16)) * np.float32(16)).astype(np.int32)
    large = np.minimum(large, 31)
    return np.where(n < 16, n, large)


def host_consts():
    c = {}
    oh = np.zeros((33, LTOT), np.float32)
    x = np.arange(LC); d = DC - x
    b = rel_bucket_np(d)
    oh[b[d >= 0], x[d >= 0]] = 1.0
    oh[32, x[d < 0]] = 1.0
    x = np.arange(LW); d = DW - x; ok = (d >= 0) & (d < 512)
    b = rel_bucket_np(d)
    oh[b[ok], LC + x[ok]] = 1.0
    oh[32, LC + x[~ok]] = 1.0
    x = np.arange(LS); d = DS - x; ok = d >= 0
    b = rel_bucket_np(d)
    oh[b[ok], LC + LW + x[ok]] = 1.0
    oh[32, LC + LW + x[~ok]] = 1.0
    c["OH"] = oh
    c["ident"] = np.eye(128, dtype=np.float32)
    k = np.arange(SEQ)
    r = 2 * ((k // 128) % 32) + ((k % 128) >= 64)
    ee = np.zeros((64, SEQ), np.float32)
    ee[r, k] = 30000.0
    c["EE"] = ee
    i = np.arange(512)[:, None] * 16
    s = np.arange(128)[None, :] * 64
    ov = np.clip(np.minimum(i + 32, s + 64) - np.maximum(i, s), 0, None).astype(np.float32) / 32.0
    ov[511] = 0.0
    c["ovl"] = np.ascontiguousarray(ov.reshape(4, 128, 128).transpose(1, 0, 2))
    tc = np.zeros((NQT, 128, 4, 2, 128), np.float32)
    j = np.arange(128)[None, :]
    for qt in range(NQT):
        for ss in range(4):
            t = qt * QT + 511 - (128 * ss + np.arange(128))
            back = (t // 64)[:, None] - j
            forced = (j == 0) | ((back >= 0) & (back < 2))
            A = ((back >= 0) & ~forced).astype(np.float32)
            B = np.where(back >= 0, np.where(forced, 1e4, 0.0), -1e30).astype(np.float32)
            tc[qt, :, ss, 0] = A
            tc[qt, :, ss, 1] = B
    c["topc"] = tc.reshape(NQT, 128, 1024)
    return c


class Attn:
    def __init__(self, P):
        self.P = P
        D = lambda n, s: P.dram(n, s, F32, kind="ExternalInput")
        self.qT = D("qT", [384, SEQ])
        self.gB = D("gB", [9, SEQ])
        self.kvT = D("kvT", [384, SEQ])
        self.vtok = D("vtok", [SEQ, 128])
        self.blk = D("blk", [2, 64, 32, 512])
        self.hg = D("hg", [128, 8])
        self.pos = D("pos", [2, 64, 32])
        self.w1 = D("w1", [2, 2048, 256])
        self.w2 = D("w2", [2, 256, 64])
        self.rb6 = D("rb6", [32, 6])
        self.OH = D("OH", [33, LTOT])
        self.identD = D("ident", [128, 128])
        self.EE = D("EE", [64, SEQ])
        self.ovlD = D("ovl", [128, 4, 128])
        self.topc = D("topc", [NQT, 128, 1024])
        self.yT = P.dram("yT", [192, SEQ], F32, kind="ExternalOutput")
        self.Rrow = P.dram("Rrow", [6, LTOT], F32, kind="Internal")
        self.alloc()

    def alloc(self):
        P = self.P
        sb = P.sb
        self.ksaug = sb("ksaug", [128, SEQ], BF16)
        self.kwT = sb("kwT", [64, SEQ], BF16)
        self.vsaug = sb("vsaug", [128, 64, 128], BF16)
        self.vwaug = sb("vwaug", [128, 64, 128], BF16)
        self.kcT = sb("kcT", [64, 512], BF16)
        self.vcaug = sb("vcaug", [128, 4, 128], BF16)
        self.Ubig = sb("Ubig", [128, 6 * LU], BF16)
        self.TTw = [sb("TTw%d" % h, [128, LTW], BF16) for h in range(3)]
        self.TTs = [sb("TTs%d" % h, [128, LTS], BF16) for h in range(3)]
        self.ident = sb("identb", [128, 128], BF16)
        self.ones = sb("ones", [128, 128], BF16)
        self.ovl = sb("ovlb", [128, 4, 128], BF16)
        self.hgs = sb("hgs", [128, 8], F32)
        self.cf = sb("cf", [128, 6], F32)
        self.tab = sb("tab", [33, 6], F32)
        self.epsT = sb("epsT", [128, 1], F32)
        self.tinyT = sb("tinyT", [128, 1], F32)
        self.st = [sb("st%d" % i, [128, 1024], F32) for i in range(2)]
        self.sq = [sb("sq%d" % i, [128, 512], BF16) for i in range(2)]
        self.rq = sb("rq", [128, 512], F32)
        self.qraw = sb("qraw", [64, 6, 512], F32)
        self.qn = sb("qn", [64, 6, 512], BF16)
        self.qaug = sb("qaug", [128, 3, 2, 512], BF16)
        self.pT = [sb("pT%d" % i, [128, 512], BF16) for i in range(9)]
        self.rdq = sb("rdq", [128, 4], F32)
        self.rdb = sb("rdb", [128, 512], F32)
        self.impacc = sb("impacc", [128, 512], F32)
        self.tcs = sb("tcs", [128, 1024], F32)
        self.sc = sb("sc", [128, 128], F32)
        self.sc2 = sb("sc2", [128, 128], F32)
        self.mx = sb("mx", [128, 16], F32)
        self.mneg = sb("mneg", [128, 128], BF16)
        self.gt = [sb("gt%d" % i, [64, 512], F32) for i in range(2)]
        self.rdt = sb("rdt", [64, 512], F32)
        self.wt = sb("wt", [64, 512], F32)
        self.tm = sb("tm", [64, 512], F32)
        self.y = sb("y", [64, 3, 512], F32)
        self.w1b = sb("w1b", [64, 4, 256], BF16)
        self.blkb = sb("blkb", [64, 4, 512], BF16)
        self.posT = sb("posT", [64, 32], F32)
        self.w2b = sb("w2b", [128, 2, 64], BF16)
        self.gx = self.rdb
        self.gy = self.impacc
        self.gT = [sb("gT%d" % i, [128, 512], BF16) for i in range(2)]
        self.A = [P.ps("A%d" % i, [128, 512]) for i in range(3)]
        self.S = [P.ps("S%d" % i, [128, 512]) for i in range(2)]
        self.X0 = P.ps("X0", [128, 512])
        self.X1 = P.ps("X1", [128, 512])
        self.XT = P.ps("XT", [128, 512], BF16)
        self.cnt = {}

    def uslice(self, h, mo):
        return self.Ubig[:, h * LU + mo: h * LU + mo + 512]

    def rot(self, name, lst):
        i = self.cnt.get(name, 0)
        self.cnt[name] = i + 1
        return lst[i % len(lst)]

    def norm64(self, src_t, src_ap, w, gain_ap, dst_t, dst_ap):
        P = self.P
        sq = self.rot("sq", self.sq)
        P.op("act", lambda e: e.activation(out=sq[0:64, 0:w], in_=src_ap, func=AF.Square), [src_t], [sq])
        P.op("pe", lambda e: e.matmul(self.X0[0:64, 0:w], lhsT=self.ones[0:64, 0:64], rhs=sq[0:64, 0:w],
                                      start=True, stop=True), [self.ones, sq], [self.X0])
        P.op("act", lambda e: e.activation(out=self.rq[0:64, 0:w], in_=self.X0[0:64, 0:w], func=AF.Ln,
                                           bias=self.epsT[0:64, :], scale=1.0 / 64.0), [self.X0, self.epsT], [self.rq])
        P.op("act", lambda e: e.activation(out=self.rq[0:64, 0:w], in_=self.rq[0:64, 0:w], func=AF.Exp, scale=-0.5),
             [self.rq], [self.rq])
        P.op("dve", lambda e: e.scalar_tensor_tensor(out=dst_ap, in0=src_ap, scalar=gain_ap, in1=self.rq[0:64, 0:w],
                                                     op0=ALU.mult, op1=ALU.mult), [src_t, self.hgs, self.rq], [dst_t])

    def setup(self):
        P = self.P
        st = self.st
        P.dma("sp", self.hgs, self.hgs[:], self.hg, self.hg[:])
        P.op("pool", lambda e: e.memset(self.ones[:], 1.0), [], [self.ones])
        P.op("pool", lambda e: e.memset(self.epsT[:], 1e-6), [], [self.epsT])
        P.op("pool", lambda e: e.memset(self.tinyT[:], 1e-30), [], [self.tinyT])
        P.op("pool", lambda e: e.memset(self.vsaug[:], 1.0), [], [self.vsaug])
        P.op("pool", lambda e: e.memset(self.vwaug[:], 1.0), [], [self.vwaug])
        P.op("pool", lambda e: e.memset(self.vcaug[:], 1.0), [], [self.vcaug])
        s = self.rot("st", st)
        P.dma("sp", s, s[:, 0:128], self.identD, self.identD[:])
        P.op("pool", lambda e, s=s: e.tensor_copy(out=self.ident[:], in_=s[:, 0:128]), [s], [self.ident])
        s = self.rot("st", st)
        P.dma("sp", s, s[:, 0:512], self.ovlD, self.ovlD[:].rearrange("p c j -> p (c j)"))
        P.op("pool", lambda e, s=s: e.tensor_copy(out=self.ovl[:].rearrange("p c j -> p (c j)"), in_=s[:, 0:512]),
             [s], [self.ovl])
        self.setup_tables()
        self.setup_kv()

    def load_rb(self):
        P = self.P
        rb = self.rb6[:]
        P.dma("sp", self.cf, self.cf[:], self.rb6, bass.AP(tensor=rb.tensor, offset=31 * 6, ap=[[0, 128], [1, 6]]))
        P.dma("sp", self.tab, self.tab[0:32, :], self.rb6, self.rb6[:])

    def setup_tables(self):
        P = self.P
        st = self.st
        P.op("pool", lambda e: e.memset(self.tab[:], MASKV), [], [self.tab])
        self.load_rb()
        P.op("dve", lambda e: e.tensor_scalar(out=self.tab[0:32, :], in0=self.tab[0:32, :], scalar1=8.0, scalar2=None,
                                              op0=ALU.mult), [self.tab], [self.tab])
        o = 0
        while o < LTOT:
            w = min(512, LTOT - o)
            s = self.rot("st", st)
            P.dma("sp", s, s[0:33, 0:w], self.OH, self.OH[:, o:o + w])
            P.op("pe", lambda e, s=s, w=w: e.matmul(self.X1[0:6, 0:w], lhsT=self.tab[:, :], rhs=s[0:33, 0:w],
                                                    start=True, stop=True), [self.tab, s], [self.X1])
            s2 = self.rot("st", st)
            P.op("act", lambda e, s2=s2, w=w: e.activation(out=s2[0:6, 0:w], in_=self.X1[0:6, 0:w], func=AF.Copy),
                 [self.X1], [s2])
            P.dma("sp", self.Rrow, self.Rrow[:, o:o + w], s2, s2[0:6, 0:w])
            o += w
        rr = self.Rrow[:]

        def expand(dst, d0, h, base, pstep, L):
            o = 0
            while o < L:
                w = min(1024, L - o)
                s = self.rot("st", st)
                P.dma("sp", s, s[:, 0:w], self.Rrow,
                      bass.AP(tensor=rr.tensor, offset=h * LTOT + base + o, ap=[[pstep, 128], [1, w]]))
                P.op("pool", lambda e, s=s, o=o, w=w: e.tensor_copy(out=dst[:, d0 + o:d0 + o + w], in_=s[:, 0:w]),
                     [s], [dst])
                o += w
        for h in range(6):
            expand(self.Ubig, h * LU, h, 0, 16, LU)
        for h in range(3):
            expand(self.TTw[h], 0, h, LC, 1, LTW)
            expand(self.TTs[h], 0, h, LC + LW, 1, LTS)

    def load_kv_piece(self, s, slot, c0):
        self.P.dma("sp", s, s[0:64, :], self.kvT, self.kvT[slot * 64:(slot + 1) * 64, c0:c0 + 1024])

    def load_v_piece(self, piece):
        P = self.P
        c0 = piece * 1024
        s = self.rot("st", self.st)
        P.dma("sp", s, s[:, :].rearrange("p (c f) -> p c f", f=128), self.vtok,
              self.vtok[c0:c0 + 1024, :].rearrange("(c p) f -> p c f", p=128))
        sv = s[:, :].rearrange("p (c f) -> p c f", f=128)
        P.op("pool", lambda e, sv=sv, piece=piece: e.tensor_copy(
            out=self.vsaug[:, piece * 8:(piece + 1) * 8, 0:64], in_=sv[:, :, 0:64]), [s], [self.vsaug])
        P.op("pool", lambda e, sv=sv, piece=piece: e.tensor_copy(
            out=self.vwaug[:, piece * 8:(piece + 1) * 8, 0:64], in_=sv[:, :, 64:128]), [s], [self.vwaug])

    def setup_kv(self):
        P = self.P
        st = self.st
        for piece in range(8):
            c0 = piece * 1024
            for slot, gcol, dst in ((2, 2, self.ksaug), (4, 3, self.kwT)):
                s = self.rot("st", st)
                self.load_kv_piece(s, slot, c0)
                for hf in range(2):
                    a = hf * 512
                    self.norm64(s, s[0:64, a:a + 512], 512, self.hgs[0:64, gcol:gcol + 1], dst,
                                dst[0:64, c0 + a:c0 + a + 512])
            s = self.rot("st", st)
            P.dma("sp", s, s[64:128, :], self.EE, self.EE[:, c0:c0 + 1024])
            P.op("pool", lambda e, s=s, c0=c0: e.tensor_copy(out=self.ksaug[64:128, c0:c0 + 1024], in_=s[64:128, :]),
                 [s], [self.ksaug])
            self.load_v_piece(piece)
        self.compress()

    def compress_rhs(self, kvi, tg):
        P = self.P
        for th in range(2):
            s = self.rot("st", self.st)
            t0 = tg * 4 + th * 2
            P.dma("sp", s, s[0:64, :].rearrange("p (t n) -> p t n", n=512), self.blk,
                  self.blk[kvi, :, t0:t0 + 2, :])
            for tt in range(2):
                t = t0 + tt
                P.op("dve", lambda e, s=s, tt=tt, th=th, t=t: e.tensor_scalar(
                    out=self.blkb[:, th * 2 + tt, :], in0=s[0:64, tt * 512:(tt + 1) * 512],
                    scalar1=self.posT[:, t:t + 1], scalar2=None, op0=ALU.add), [s, self.posT], [self.blkb])

    def compress_begin(self, kvi):
        pass

    def compress(self):
        P = self.P
        st = self.st
        H = [self.A[0], self.A[1]]
        for kvi in range(2):
            self.compress_begin(kvi)
            P.dma("sp", self.posT, self.posT[:], self.pos, self.pos[kvi])
            s = self.rot("st", st)
            P.dma("sp", s, s[:, 0:128].rearrange("p (c d) -> p c d", d=64), self.w2,
                  self.w2[kvi].rearrange("(c p) d -> p c d", p=128))
            P.op("pool", lambda e, s=s: e.tensor_copy(out=self.w2b[:].rearrange("p c d -> p (c d)"), in_=s[:, 0:128]),
                 [s], [self.w2b])
            for tg in range(8):
                s = self.rot("st", st)
                P.dma("sp", s, s[0:64, :].rearrange("p (t j) -> p t j", j=256), self.w1,
                      self.w1[kvi, tg * 256:(tg + 1) * 256, :].rearrange("(t d) j -> d t j", d=64))
                P.op("pool", lambda e, s=s: e.tensor_copy(out=self.w1b[:].rearrange("p t j -> p (t j)"),
                                                          in_=s[0:64, :]), [s], [self.w1b])
                self.compress_rhs(kvi, tg)
                for tl in range(4):
                    t = tg * 4 + tl
                    for jc in range(2):
                        P.op("pe", lambda e, tl=tl, jc=jc, t=t: e.matmul(
                            H[jc][:, :], lhsT=self.w1b[:, tl, jc * 128:(jc + 1) * 128], rhs=self.blkb[:, tl, :],
                            start=(t == 0), stop=(t == 31)), [self.w1b, self.blkb], [H[jc]])
            for jc in range(2):
                P.op("act", lambda e, jc=jc: e.activation(out=self.gx[:], in_=H[jc][:, :], func=AF.Copy),
                     [H[jc]], [self.gx])
                P.op("pool", lambda e: e.tensor_tensor(out=self.gy[:], in0=self.gx[:], in1=self.gx[:], op=ALU.mult),
                     [self.gx], [self.gy])
                P.op("dve", lambda e: e.tensor_scalar(out=self.gy[:], in0=self.gy[:], scalar1=0.044715, scalar2=1.0,
                                                      op0=ALU.mult, op1=ALU.add), [self.gy], [self.gy])
                P.op("pool", lambda e: e.tensor_tensor(out=self.gy[:], in0=self.gy[:], in1=self.gx[:], op=ALU.mult),
                     [self.gy, self.gx], [self.gy])
                P.op("act", lambda e: e.activation(out=self.gy[:], in_=self.gy[:], func=AF.Sigmoid,
                                                   scale=1.5957691216057308), [self.gy], [self.gy])
                P.op("dve", lambda e, jc=jc: e.tensor_tensor(out=self.gT[jc][:], in0=self.gx[:], in1=self.gy[:],
                                                             op=ALU.mult), [self.gx, self.gy], [self.gT[jc]])
            if kvi == 0:
                for jc in range(2):
                    P.op("pe", lambda e, jc=jc: e.matmul(self.X1[0:64, :], lhsT=self.w2b[:, jc, :], rhs=self.gT[jc][:],
                                                         start=(jc == 0), stop=(jc == 1)),
                         [self.w2b, self.gT[jc]], [self.X1])
                s = self.rot("st", st)
                P.op("act", lambda e, s=s: e.activation(out=s[0:64, 0:512], in_=self.X1[0:64, :], func=AF.Copy),
                     [self.X1], [s])
                self.norm64(s, s[0:64, 0:512], 512, self.hgs[0:64, 1:2], self.kcT, self.kcT[:, :])
            else:
                for cc in range(4):
                    for jc in range(2):
                        P.op("pe", lambda e, jc=jc, cc=cc: e.matmul(
                            self.X1[:, cc * 64:(cc + 1) * 64], lhsT=self.gT[jc][:, cc * 128:(cc + 1) * 128],
                            rhs=self.w2b[:, jc, :], start=(jc == 0), stop=(jc == 1)),
                            [self.w2b, self.gT[jc]], [self.X1])
                    P.op("act", lambda e, cc=cc: e.activation(out=self.vcaug[:, cc, 0:64],
                                                              in_=self.X1[:, cc * 64:(cc + 1) * 64], func=AF.Copy),
                         [self.X1], [self.vcaug])

    def combine(self, qt, h, br, first):
        P = self.P
        A = self.A[h]
        s0 = qt * QT
        gt = self.load_gate(qt, h, br)
        P.op("act", lambda e: e.activation(out=self.rdt[:], in_=A[64:128, :], func=AF.Ln, bias=self.tinyT[0:64, :]),
             [A, self.tinyT], [self.rdt])
        P.op("act", lambda e: e.activation(out=self.rdt[:], in_=self.rdt[:], func=AF.Exp, scale=-1.0),
             [self.rdt], [self.rdt])
        P.op("pool", lambda e: e.tensor_tensor(out=self.wt[:], in0=self.rdt[:], in1=gt[:], op=ALU.mult),
             [self.rdt, gt], [self.wt])
        if first:
            P.op("dve", lambda e: e.tensor_tensor(out=self.y[:, h, :], in0=A[0:64, :], in1=self.wt[:], op=ALU.mult),
                 [A, self.wt], [self.y])
        else:
            P.op("dve", lambda e: e.tensor_tensor(out=self.tm[:], in0=A[0:64, :], in1=self.wt[:], op=ALU.mult),
                 [A, self.wt], [self.tm])
            P.op("pool", lambda e: e.tensor_tensor(out=self.y[:, h, :], in0=self.y[:, h, :], in1=self.tm[:],
                                                   op=ALU.add), [self.y, self.tm], [self.y])

    def exp(self, S, pT, bias):
        P = self.P
        if bias is None:
            P.op("act", lambda e: e.activation(out=pT[:], in_=S[:, :], func=AF.Exp, scale=0.125), [S], [pT])
        else:
            bap, bt = bias
            P.op("act", lambda e: e.activation(out=pT[:], in_=S[:, :], func=AF.Exp, scale=0.125, bias=bap),
                 [S, bt], [pT])

    def cmp_bias(self, cc, h, mixed):
        return None if mixed else (self.cf[:, h:h + 1], self.cf)

    def sel_bias(self, kc, h, near):
        return None if near else (self.cf[:, h:h + 1], self.cf)

    def win_bias(self, c):
        return None

    def topc_ap(self, qt):
        return self.topc[qt]

    def load_q(self, qt):
        s0 = qt * QT
        self.P.dma("sp", self.qraw, self.qraw[:], self.qT, self.qT[:, s0:s0 + QT].rearrange("(h d) t -> d h t", d=64))

    def load_gate(self, qt, h, br):
        s0 = qt * QT
        gt = self.rot("gt", self.gt)
        g = self.gB[:]
        self.P.dma("sp", gt, gt[:], self.gB,
                   bass.AP(tensor=g.tensor, offset=(h * 3 + br) * SEQ + s0, ap=[[0, 64], [1, 512]]))
        return gt

    def store_y(self, qt):
        s0 = qt * QT
        self.P.dma("sp", self.yT, self.yT[:, s0:s0 + QT].rearrange("(h d) t -> d h t", d=64), self.y, self.y[:],
                   nodep_out=True)

    def qtile(self, qt):
        P = self.P
        s0 = qt * QT
        self.load_q(qt)
        P.dma("sp", self.tcs, self.tcs[:], self.topc, self.topc_ap(qt))
        for h in range(6):
            self.norm64(self.qraw, self.qraw[:, h, :], 512, self.hgs[0:64, 0:1], self.qn, self.qn[:, h, :])
        for h in range(3):
            for half in range(2):
                P.op("pool", lambda e, h=h, half=half: e.tensor_copy(out=self.qaug[0:64, h, half, :],
                                                                     in_=self.qn[:, h, :]), [self.qn], [self.qaug])
        for h in range(6):
            chunks = [cc for cc in range(4) if qt - 4 * cc >= 0]
            pts = []
            for cc in chunks:
                k = qt - 4 * cc
                mixed = k <= 5
                S = self.rot("S", self.S)
                P.op("pe", lambda e, S=S, cc=cc, h=h, mixed=mixed: e.matmul(
                    S[:, :], lhsT=self.kcT[:, cc * 128:(cc + 1) * 128], rhs=self.qn[:, h, :], start=True,
                    stop=(not mixed)), [self.kcT, self.qn], [S])
                if mixed:
                    mo = 512 * (5 - k)
                    P.op("pe", lambda e, S=S, h=h, mo=mo: e.matmul(S[:, :], lhsT=self.ident[:],
                                                                   rhs=self.uslice(h, mo), start=False, stop=True),
                         [self.ident, self.Ubig], [S])
                pT = self.rot("pT", self.pT)
                self.exp(S, pT, self.cmp_bias(cc, h, mixed))
                pts.append(pT)
            n = len(chunks)
            if h < 3:
                for i, (cc, pT) in enumerate(zip(chunks, pts)):
                    P.op("pe", lambda e, pT=pT, i=i, cc=cc, h=h: e.matmul(
                        self.A[h][:, :], lhsT=self.vcaug[:, cc, :], rhs=pT[:], start=(i == 0), stop=(i == n - 1)),
                        [self.vcaug, pT], [self.A[h]])
            for ss in range(4):
                for i, (cc, pT) in enumerate(zip(chunks, pts)):
                    P.op("pe", lambda e, pT=pT, i=i, cc=cc, ss=ss: e.matmul(
                        self.X1[:, ss * 128:(ss + 1) * 128], lhsT=pT[:, ss * 128:(ss + 1) * 128], rhs=self.ovl[:, cc, :],
                        start=(i == 0), stop=(i == n - 1)), [pT, self.ovl], [self.X1])
            for ss in range(4):
                for i, pT in enumerate(pts):
                    P.op("pe", lambda e, pT=pT, i=i, ss=ss: e.matmul(
                        self.X0[:, ss:ss + 1], lhsT=pT[:, ss * 128:(ss + 1) * 128], rhs=self.ones[:, 0:1],
                        start=(i == 0), stop=(i == n - 1)), [pT, self.ones], [self.X0])
            P.op("dve", lambda e: e.tensor_scalar(out=self.rdq[:], in0=self.X0[:, 0:4], scalar1=1e-30, scalar2=None,
                                                  op0=ALU.max), [self.X0], [self.rdq])
            P.op("dve", lambda e: e.reciprocal(out=self.rdq[:], in_=self.rdq[:]), [self.rdq], [self.rdq])
            for ss in range(4):
                if h == 0:
                    P.op("dve", lambda e, ss=ss: e.tensor_scalar(
                        out=self.impacc[:, ss * 128:(ss + 1) * 128], in0=self.X1[:, ss * 128:(ss + 1) * 128],
                        scalar1=self.rdq[:, ss:ss + 1], scalar2=None, op0=ALU.mult), [self.X1, self.rdq], [self.impacc])
                else:
                    P.op("dve", lambda e, ss=ss: e.scalar_tensor_tensor(
                        out=self.impacc[:, ss * 128:(ss + 1) * 128], in0=self.X1[:, ss * 128:(ss + 1) * 128],
                        scalar=self.rdq[:, ss:ss + 1], in1=self.impacc[:, ss * 128:(ss + 1) * 128],
                        op0=ALU.mult, op1=ALU.add), [self.X1, self.rdq, self.impacc], [self.impacc])
            if h < 3:
                self.combine(qt, h, 0, True)
        for ss in range(4):
            tA = self.tcs[:, ss * 256: ss * 256 + 128]
            tB = self.tcs[:, ss * 256 + 128: ss * 256 + 256]
            P.op("dve", lambda e, ss=ss, tA=tA: e.tensor_tensor(out=self.sc[:], in0=self.impacc[:, ss * 128:(ss + 1) * 128],
                                                                in1=tA, op=ALU.mult), [self.impacc, self.tcs], [self.sc])
            P.op("dve", lambda e, tB=tB: e.tensor_tensor(out=self.sc[:], in0=self.sc[:], in1=tB, op=ALU.add),
                 [self.sc, self.tcs], [self.sc])
            P.op("dve", lambda e: e.max(out=self.mx[:, 0:8], in_=self.sc[:]), [self.sc], [self.mx])
            P.op("dve", lambda e: e.match_replace(out=self.sc2[:], in_to_replace=self.mx[:, 0:8], in_values=self.sc[:],
                                                  imm_value=-1e30), [self.sc, self.mx], [self.sc2])
            P.op("dve", lambda e: e.max(out=self.mx[:, 8:16], in_=self.sc2[:]), [self.sc2], [self.mx])
            P.op("dve", lambda e: e.tensor_scalar(out=self.mneg[:], in0=self.sc[:], scalar1=self.mx[:, 15:16], scalar2=1.0,
                                                  op0=ALU.is_ge, op1=ALU.subtract), [self.sc, self.mx], [self.mneg])
            P.op("pe", lambda e, ss=ss: e.transpose(self.XT[:, ss * 128:(ss + 1) * 128], self.mneg[:], self.ident[:]),
                 [self.mneg, self.ident], [self.XT])
        for h in range(3):
            for half in range(2):
                eng = "act" if (h + half) % 2 == 0 else "dve"
                if eng == "act":
                    P.op("act", lambda e, h=h, half=half: e.activation(out=self.qaug[64:128, h, half, :],
                                                                       in_=self.XT[half * 64:(half + 1) * 64, :],
                                                                       func=AF.Copy), [self.XT], [self.qaug])
                else:
                    P.op("dve", lambda e, h=h, half=half: e.tensor_copy(out=self.qaug[64:128, h, half, :],
                                                                        in_=self.XT[half * 64:(half + 1) * 64, :]),
                         [self.XT], [self.qaug])
        nkc = 4 * qt + 4

        def sel_qk(kc, h):
            k0 = 128 * kc
            ep = s0 + 511 - k0
            near = ep <= DS
            half = kc // 32
            S = self.rot("S", self.S)
            P.op("pe", lambda e: e.matmul(S[:, :], lhsT=self.ksaug[:, k0:k0 + 128], rhs=self.qaug[:, h, half, :],
                                          start=True, stop=(not near)), [self.ksaug, self.qaug], [S])
            if near:
                mo = DS - ep
                P.op("pe", lambda e: e.matmul(S[:, :], lhsT=self.ident[:], rhs=self.TTs[h][:, mo:mo + 512],
                                              start=False, stop=True), [self.ident, self.TTs[h]], [S])
            return (kc, h, near, S)

        def sel_rest(kc, h, near, S):
            pT = self.rot("pT", self.pT)
            self.exp(S, pT, self.sel_bias(kc, h, near))
            P.op("pe", lambda e: e.matmul(self.A[h][:, :], lhsT=self.vsaug[:, kc, :], rhs=pT[:],
                                          start=(kc == 0), stop=(kc == nkc - 1)), [self.vsaug, pT], [self.A[h]])

        prev = None
        for kc in range(nkc):
            for h in range(3):
                cur = sel_qk(kc, h)
                if prev is not None:
                    sel_rest(*prev)
                prev = cur
        sel_rest(*prev)
        for h in range(3):
            self.combine(qt, h, 1, False)
        rs = [r for r in range(8) if s0 - 512 + 128 * r >= 0]

        def win_qk(i, r, h):
            k0 = s0 - 512 + 128 * r
            mo = 128 * r
            S = self.rot("S", self.S)
            P.op("pe", lambda e: e.matmul(S[:, :], lhsT=self.kwT[:, k0:k0 + 128], rhs=self.qn[:, h, :],
                                          start=True, stop=False), [self.kwT, self.qn], [S])
            P.op("pe", lambda e: e.matmul(S[:, :], lhsT=self.ident[:], rhs=self.TTw[h][:, mo:mo + 512],
                                          start=False, stop=True), [self.ident, self.TTw[h]], [S])
            return (i, k0, h, S)

        def win_rest(i, k0, h, S):
            pT = self.rot("pT", self.pT)
            self.exp(S, pT, self.win_bias(k0 // 128))
            P.op("pe", lambda e: e.matmul(self.A[h][:, :], lhsT=self.vwaug[:, k0 // 128, :], rhs=pT[:],
                                          start=(i == 0), stop=(i == len(rs) - 1)), [self.vwaug, pT], [self.A[h]])

        prev = None
        for i, r in enumerate(rs):
            for h in range(3):
                cur = win_qk(i, r, h)
                if prev is not None:
                    win_rest(*prev)
                prev = cur
        win_rest(*prev)
        for h in range(3):
            self.combine(qt, h, 2, False)
        self.store_y(qt)

    def build(self, nqt=NQT):
        self.setup()
        for qt in range(nqt):
            self.qtile(qt)
        self.P.wait_all("sp", [self.yT])
        self.P.finalize()


def build_attn(nqt=NQT):
    nc = bass.Bass("TRN2", target_bir_lowering=False)
    P = Prog(nc)
    a = Attn(P)
    a.build(nqt)
    return nc, P


XPAD = SEQ + HALO


def rev_ap(ap, n):
    dims = [list(d) for d in ap.ap]
    dims[-1] = [-1, n]
    return bass.AP(tensor=ap.tensor, offset=ap.offset + (n - 1), ap=dims)


class FTok(TokStage):
    def __init__(self, P, mode, dr):
        self.P = P
        self.mode = mode
        self.dr = dr
        for k in ("memT", "gains", "hg", "mem_w_kv", "w_out", "w_up", "w_down", "b_w_in", "a_w_in", "convw", "w_kv"):
            setattr(self, k, dr[k])
        self.xsrc = dr["xin"] if mode == "L1" else dr["X"]
        self.kvT_o, self.qT_o, self.gT_o = dr["KV"], dr["Q"], dr["G"]
        self.ytokT = dr["Y"]
        self.chunk = 0
        self.alloc()

    def halo_fix(self, sub):
        if self.chunk == 0:
            self.P.op("pool", lambda e: e.memset(self.v[:, sub, 2:2 + HALO], 0.0), [], [self.v])

    def store_tile(self, dstT, r, msz, c0, t0, w, ost):
        a = c0 + t0
        lo = HALO if a == 0 else 0
        g0 = NTOK * self.chunk + a + lo - HALO
        self.P.dma("sp", dstT, dstT[r:r + msz, g0:g0 + (w - lo)], ost, ost[0:msz, lo:w], nodep_out=True)

    def ytok_src(self, ch, c0, t0, w):
        g0 = NTOK * self.chunk + c0 + t0
        return self.ytokT[ch * 128:(ch + 1) * 128, g0:g0 + w]

    def setup_consts(self):
        P = self.P
        P.dma("sp", self.g, self.g[:], self.gains, self.gains[:])
        P.dma("sp", self.hgs, self.hgs[:], self.hg, self.hg[:])
        if self.mode == "L1":
            P.dma("sp", self.cw, self.cw[:], self.convw, self.convw[:])
        for c in range(8):
            P.dma("sp", self.u, self.memx(c, 0, 256), self.memT, self.memT[c * 128:(c + 1) * 128, :], nodep_out=True)
        P.op("pool", lambda e: e.memset(self.ones[:], 1.0), [], [self.ones])
        P.op("pool", lambda e: e.memset(self.bd[:], 0.0), [], [self.bd])
        P.op("pool", lambda e: e.memset(self.bd[0:64, 0:64], 1.0), [], [self.bd])
        P.op("pool", lambda e: e.memset(self.bd[64:128, 64:128], 1.0), [], [self.bd])
        P.op("pool", lambda e: e.memset(self.epsT[:], EPS), [], [self.epsT])
        P.op("pool", lambda e: e.memset(self.vaug[:], 1.0), [], [self.vaug])
        self.norm(self.u, self.memx, 256, [(0, 256)], 15, self.memh, 0)

    def load_x(self):
        P = self.P
        g0 = NTOK * self.chunk
        for c in range(8):
            P.dma("sp", self.x, self.x[:, c, :], self.xsrc, self.xsrc[c * 128:(c + 1) * 128, g0:g0 + NCOL],
                  nodep_out=(c > 0))

    def store_x(self):
        P = self.P
        dst = self.dr["out"] if self.mode == "L5" else self.dr["X"]
        off = 0 if self.mode == "L5" else HALO
        g0 = NTOK * self.chunk + off
        for c in range(8):
            P.dma("sp", dst, dst[c * 128:(c + 1) * 128, g0:g0 + NTOK], self.x, self.x[:, c, HALO:NCOL], nodep_out=True)

    def run(self):
        self.setup_consts()
        for chunk in range(4):
            self.chunk = chunk
            self.load_x()
            if self.mode == "L1":
                for l in range(2):
                    self.mem_kv(l)
                    for ri, (c0, n) in enumerate(RANGES):
                        tiles = tiles_of(n)
                        self.norm(self.x, self.xs(c0), n, tiles, l, self.h, 0)
                        self.mix_a(l, ri, c0, n, tiles)
                        self.mem_attn(l, n, tiles)
                        self.out_proj(l, c0, n, tiles)
                        self.mlp(l, c0, n, tiles)
                for ri, (c0, n) in enumerate(RANGES):
                    tiles = tiles_of(n)
                    self.norm(self.x, self.xs(c0), n, tiles, 8, self.h, 0)
                    for i in range(3):
                        self.store_proj(self.w_kv, self.w_kv[:, 256 * i:256 * (i + 1)], 256, self.kvT_o, 256 * i, c0, tiles)
                    if not getattr(self, "skip_pre0", False):
                        self.pre_b(0, c0, n, tiles)
            else:
                j = 0 if self.mode == "L3" else 1
                self.mem_kv(2 + j)
                for ri, (c0, n) in enumerate(RANGES):
                    self.post_b(j, c0, n, tiles_of(n))
                if j == 0:
                    for ri, (c0, n) in enumerate(RANGES):
                        self.pre_b(1, c0, n, tiles_of(n))
            self.store_x()


class FAttn(Attn):
    def __init__(self, P, dr, jl):
        self.P = P
        self.dr = dr
        self.jl = jl
        self.Q, self.G, self.KV, self.Y = dr["Q"], dr["G"], dr["KV"], dr["Y"]
        self.hg = dr["ahg%d" % jl]
        self.pos, self.w1, self.w2 = dr["pos"], dr["cmp_w1"], dr["cmp_w2"]
        self.rel = dr["rel_bias"]
        self.OH, self.identD, self.EE, self.ovlD, self.topc = dr["OH"], dr["ident"], dr["EE"], dr["ovl"], dr["topc"]
        self.Rrow = dr["Rrow"]
        self.alloc()

    def alloc(self):
        Attn.alloc(self)
        self.ub_view = self.Ubig[0:64, 0:16 * 513].rearrange("p (r n) -> p r n", n=513)

    def set_combo(self, g, tr):
        self.g, self.tr = g, tr
        own = [3 * tr + i for i in range(3)]
        oth = [3 * (1 - tr) + i for i in range(3)]
        self.heads = own + oth

    def load_rb(self):
        P = self.P
        rel = self.rel[:]
        for i, hh in enumerate(self.heads):
            col = self.g * 6 + hh
            P.dma("sp", self.cf, self.cf[:, i:i + 1], self.rel,
                  bass.AP(tensor=rel.tensor, offset=31 * 12 + col, ap=[[0, 128], [1, 1]]))
            P.dma("sp", self.tab, self.tab[0:32, i:i + 1], self.rel, self.rel[:, col:col + 1],
                  allow_slow_non_contiguous=True)

    def load_kv_piece(self, s, slot, c0):
        r0 = slot * 128 + self.g * 64
        self.P.dma("sp", s, s[0:64, :], self.KV, self.KV[r0:r0 + 64, c0:c0 + 1024])

    def load_v_piece(self, piece):
        P = self.P
        c0 = piece * 1024
        for slot, dst in ((3, self.vsaug), (5, self.vwaug)):
            s = self.rot("st", self.st)
            self.load_kv_piece(s, slot, c0)
            for hf in range(2):
                sq = self.rot("sq", self.sq)
                P.op("pool", lambda e, s=s, sq=sq, hf=hf: e.tensor_copy(out=sq[0:64, :], in_=s[0:64, hf * 512:(hf + 1) * 512]),
                     [s], [sq])
                for c in range(4):
                    P.op("pe", lambda e, sq=sq, c=c: e.transpose(self.XT[:, c * 64:(c + 1) * 64],
                                                                 sq[0:64, c * 128:(c + 1) * 128], self.ident[0:64, 0:64]),
                         [sq, self.ident], [self.XT])
                ch0 = piece * 8 + hf * 4
                P.op("act", lambda e, dst=dst, ch0=ch0: e.activation(
                    out=dst[:, ch0:ch0 + 4, 0:64], in_=self.XT[:, 0:256].rearrange("p (c d) -> p c d", d=64),
                    func=AF.Copy), [self.XT], [dst])

    def compress_begin(self, kvi):
        P = self.P
        P.op("pool", lambda e: e.memset(self.ub_view[:, :, 512:513], 0.0), [], [self.Ubig])
        for piece in range(8):
            s = self.rot("st", self.st)
            self.load_kv_piece(s, kvi, piece * 1024)
            P.op("pool", lambda e, s=s, piece=piece: e.tensor_copy(
                out=self.ub_view[:, :, piece * 64:(piece + 1) * 64],
                in_=s[0:64, :].rearrange("p (n r) -> p r n", r=16)), [s], [self.Ubig])

    def compress_rhs(self, kvi, tg):
        P = self.P
        for tl in range(4):
            t = tg * 4 + tl
            r, off = t % 16, t // 16
            P.op("dve", lambda e, tl=tl, t=t, r=r, off=off: e.tensor_scalar(
                out=self.blkb[:, tl, :], in0=self.ub_view[:, r, off:off + 512], scalar1=self.posT[:, t:t + 1],
                scalar2=None, op0=ALU.add), [self.Ubig, self.posT], [self.blkb])

    def load_q(self, qt):
        P = self.P
        s0 = qt * QT
        for i2 in range(3):
            s = self.rot("st", self.st)
            for k in range(2):
                hh = self.heads[2 * i2 + k]
                r0 = (self.g * 6 + hh) * 64
                P.dma("sp", s, s[0:64, k * 512:(k + 1) * 512], self.Q, self.Q[r0:r0 + 64, s0:s0 + QT], nodep_out=(k > 0))
            for k in range(2):
                P.op("dve", lambda e, s=s, k=k, i2=i2: e.tensor_copy(
                    out=self.qraw[:, 2 * i2 + k, :], in_=rev_ap(s[0:64, k * 512:(k + 1) * 512], 512)), [s], [self.qraw])

    def load_gate(self, qt, h, br):
        P = self.P
        s0 = qt * QT
        gt = self.rot("gt", self.gt)
        g = self.G[:]
        row = (self.g * 6 + 3 * self.tr + h) * 3 + br
        P.dma("sp", gt, gt[:], self.G, bass.AP(tensor=g.tensor, offset=row * SEQ + s0, ap=[[0, 64], [1, 512]]))
        gr = self.rot("gtr", self.gtr)
        P.op("dve", lambda e: e.tensor_copy(out=gr[:], in_=rev_ap(gt[:], 512)), [gt], [gr])
        return gr

    def store_y(self, qt):
        P = self.P
        s0 = qt * QT
        bufs = [self.rdt, self.wt, self.tm]
        for h in range(3):
            b = bufs[h]
            P.op("dve", lambda e, b=b, h=h: e.tensor_copy(out=b[:], in_=rev_ap(self.y[:, h, :], 512)), [self.y], [b])
            r0 = (self.g * 6 + 3 * self.tr + h) * 64
            P.dma("sp", self.Y, self.Y[r0:r0 + 64, HALO + s0:HALO + s0 + QT], b, b[:], nodep_out=True)

    def run(self):
        P = self.P
        st = self.st
        self.gtr = [P.sb("gtr%d" % i, [64, 512], F32) for i in range(2)]
        P.dma("sp", self.hgs, self.hgs[:], self.hg, self.hg[:])
        P.op("pool", lambda e: e.memset(self.ones[:], 1.0), [], [self.ones])
        P.op("pool", lambda e: e.memset(self.epsT[:], 1e-6), [], [self.epsT])
        P.op("pool", lambda e: e.memset(self.tinyT[:], 1e-30), [], [self.tinyT])
        P.op("pool", lambda e: e.memset(self.vsaug[:], 1.0), [], [self.vsaug])
        P.op("pool", lambda e: e.memset(self.vwaug[:], 1.0), [], [self.vwaug])
        P.op("pool", lambda e: e.memset(self.vcaug[:], 1.0), [], [self.vcaug])
        s = self.rot("st", st)
        P.dma("sp", s, s[:, 0:128], self.identD, self.identD[:])
        P.op("pool", lambda e, s=s: e.tensor_copy(out=self.ident[:], in_=s[:, 0:128]), [s], [self.ident])
        s = self.rot("st", st)
        P.dma("sp", s, s[:, 0:512], self.ovlD, self.ovlD[:].rearrange("p c j -> p (c j)"))
        P.op("pool", lambda e, s=s: e.tensor_copy(out=self.ovl[:].rearrange("p c j -> p (c j)"), in_=s[:, 0:512]),
             [s], [self.ovl])
        for g in range(2):
            self.set_combo(g, 0)
            P.barrier()
            self.setup_kv()
            for tr in range(2):
                self.set_combo(g, tr)
                P.barrier()
                self.setup_tables()
                for qt in range(NQT):
                    self.qtile(qt)


def fused_dram(P, r3=False):
    dr = {}
    D = lambda n, s: P.dram(n, s, F32, kind="ExternalInput")
    dr["xin"] = D("xin", [1024, XPAD])
    dr["memT"] = D("memT", [1024, 256])
    dr["gains"] = D("gains", [128, 128])
    dr["hg"] = D("hg", [128, 16])
    dr["convw"] = D("convw", [128, 36])
    dr["mem_w_kv"] = D("mem_w_kv", [4, 1024, 512])
    dr["w_out"] = D("w_out", [4, 1024, 1024])
    dr["w_up"] = D("w_up", [4, 1024, 4096])
    dr["w_down"] = D("w_down", [4, 4096, 1024])
    dr["b_w_in"] = D("b_w_in", [2, 1024, 1060])
    dr["a_w_in"] = D("a_w_in", [2, 1024, 2560])
    dr["w_kv"] = D("w_kv", [1024, 768])
    dr["ahg0"] = D("ahg0", [128, 8])
    dr["ahg1"] = D("ahg1", [128, 8])
    dr["pos"] = D("pos", [2, 64, 32])
    dr["cmp_w1"] = D("cmp_w1", [2, 2048, 256])
    dr["cmp_w2"] = D("cmp_w2", [2, 256, 64])
    dr["rel_bias"] = D("rel_bias", [32, 12])
    dr["OH"] = D("OH", [33, LTOT])
    dr["ident"] = D("ident", [128, 128])
    dr["EE"] = D("EE", [64, SEQ])
    dr["ovl"] = D("ovl", [128, 4, 128])
    if not r3:
        dr["topc"] = D("topc", [NQT, 128, 1024])
    if not r3:
        dr["out"] = P.dram("xT_out", [1024, SEQ], F32, kind="ExternalOutput")
    I = lambda n, s: P.dram(n, s, F32, kind="Internal")
    dr["X"] = I("Xs", [1024, XPAD])
    if not r3:
        dr["KV"] = I("KVs", [768, SEQ])
    dr["Q"] = I("Qs", [768, SEQ])
    dr["G"] = I("Gs", [36, SEQ])
    dr["Y"] = I("Ys", [768, XPAD])
    dr["Rrow"] = I("Rrow", [6, LTOT])
    return dr


def build_fused():
    nc = bass.Bass("TRN2", target_bir_lowering=False)
    P = Prog(nc)
    dr = fused_dram(P)
    P.push_scope()
    FTok(P, "L1", dr).run()
    P.pop_scope()
    for j in range(2):
        P.push_scope()
        FAttn(P, dr, j).run()
        P.pop_scope()
        P.push_scope()
        FTok(P, "L3" if j == 0 else "L5", dr).run()
        P.pop_scope()
    P.wait_all("sp", [dr["out"]])
    P.finalize()
    return nc, P


KPAD = 1536
QTILES = (3, 7, 11, 15)


def host_core_tables(k):
    tc = np.zeros((4, 128, 4, 2, 128), np.float32)
    jj = np.arange(128)[None, :]
    for j, qt in enumerate(QTILES):
        for ss in range(4):
            ta = qt * QT + 511 - (128 * ss + np.arange(128))
            back = (ta // 64)[:, None] - jj
            J = jj - 24 + 8 * k
            valid = (J >= 0) & (back >= 0)
            forced = (J == 0) | ((back >= 0) & (back < 2))
            tc[j, :, ss, 0] = (valid & ~forced).astype(np.float32)
            tc[j, :, ss, 1] = np.where(valid, np.where(forced, 1e4, 0.0), -1e30).astype(np.float32)
    kval = np.zeros((128, 16), np.float32)
    first = KPAD - 512 * k
    for c in range(12):
        a = 128 * c + np.arange(128)
        kval[:, c] = np.where(a >= first, 0.0, MASKV)
    kval[:, 12] = np.where(16 * np.arange(128) >= first, 0.0, MASKV)
    return tc.reshape(4, 128, 1024), kval


class FTok2(FTok):
    def __init__(self, P, mode, dr):
        FTok.__init__(self, P, mode, dr)
        if mode != "L1":
            self.xsrc = dr["Xown"]
            self.ytokT = dr["Yown"]
            self.qT_o, self.gT_o = dr["Qown"], dr["Gown"]

    def store_tile(self, dstT, r, msz, c0, t0, w, ost):
        if self.mode == "L1":
            off = KPAD if dstT is self.dr["KV"] else 0
            a = c0 + t0
            lo = HALO if a == 0 else 0
            g0 = NTOK * self.chunk + a + lo - HALO + off
            self.P.dma("sp", dstT, dstT[r:r + msz, g0:g0 + (w - lo)], ost, ost[0:msz, lo:w], nodep_out=True)
        else:
            self.P.dma("sp", dstT, dstT[r:r + msz, c0 + t0:c0 + t0 + w], ost, ost[0:msz, 0:w], nodep_out=True)

    def ytok_src(self, ch, c0, t0, w):
        if self.mode == "L1":
            return FTok.ytok_src(self, ch, c0, t0, w)
        return self.ytokT[ch * 128:(ch + 1) * 128, c0 + t0:c0 + t0 + w]

    def load_x(self):
        if self.mode == "L1":
            return FTok.load_x(self)
        P = self.P
        for c in range(8):
            P.dma("sp", self.x, self.x[:, c, 0:NTOK], self.xsrc, self.xsrc[c * 128:(c + 1) * 128, :], nodep_out=(c > 0))

    def store_x(self):
        if self.mode == "L1":
            return FTok.store_x(self)
        P = self.P
        dst = self.dr["out"] if self.mode == "L5" else self.dr["Xown"]
        for c in range(8):
            P.dma("sp", dst, dst[c * 128:(c + 1) * 128, :], self.x, self.x[:, c, 0:NTOK], nodep_out=True)

    def zero_kv_pad(self):
        P = self.P
        z = self.ost[0]
        P.op("pool", lambda e: e.memset(z[:], 0.0), [], [z])
        KV = self.dr["KV"]
        for r in range(6):
            for cb in range(3):
                P.dma("sp", KV, KV[r * 128:(r + 1) * 128, cb * 512:(cb + 1) * 512], z, z[:], nodep_out=True)

    def run(self):
        if self.mode == "L1":
            self.skip_pre0 = True
            self.zero_kv_pad()
            return FTok.run(self)
        self.setup_consts()
        self.chunk = 0
        self.load_x()
        if self.mode == "P0":
            for (c0, n) in [(0, 1024), (1024, 1024)]:
                self.pre_b(0, c0, n, tiles_of(n))
            return
        j = 0 if self.mode == "L3" else 1
        self.mem_kv(2 + j)
        R2 = [(0, 1024), (1024, 1024)]
        for (c0, n) in R2:
            self.post_b(j, c0, n, tiles_of(n))
        if j == 0:
            for (c0, n) in R2:
                self.pre_b(1, c0, n, tiles_of(n))
        self.store_x()


def extract_own(P, dr):
    kd = dr["kd"]

    def dyn3(src_t, r0, r1, pad, ncols):
        base = src_t[r0:r1, pad:pad + KPAD + 6144 + 512][:, bass.ds(kd, 6144 + 512)]
        return bass.AP(tensor=base.tensor, offset=base.offset, ap=[[ncols, r1 - r0], [2048, 4], [1, 512]])
    KVp, KVw = dr["KV"], dr["KVwin"]
    for rb in range(3):
        P.dma("sp", KVw, KVw[rb * 256:(rb + 1) * 256, :], KVp,
              KVp[rb * 256:(rb + 1) * 256, 0:KPAD + SEQ][:, bass.ds(kd, SEQ)], nodep_out=True)
    for rb in range(4):
        P.dma("sp", dr["Xown"], dr["Xown"][rb * 256:(rb + 1) * 256, :].rearrange("r (s c) -> r s c", c=512), dr["X"],
              dyn3(dr["X"], rb * 256, (rb + 1) * 256, HALO, XPAD), nodep_out=True)


class FAttn2(FAttn):
    def __init__(self, P, dr, jl):
        FAttn.__init__(self, P, dr, jl)
        self.KV = dr["KVwin"]
        self.Q, self.G, self.Y = dr["Qown"], dr["Gown"], dr["Yown"]
        self.kvalD = dr["kval"]

    def load_q(self, qt):
        P = self.P
        c0 = ((qt - 3) // 4) * 512
        for i2 in range(3):
            s = self.rot("st", self.st)
            for k in range(2):
                hh = self.heads[2 * i2 + k]
                r0 = (self.g * 6 + hh) * 64
                P.dma("sp", s, s[0:64, k * 512:(k + 1) * 512], self.Q, self.Q[r0:r0 + 64, c0:c0 + 512], nodep_out=(k > 0))
            for k in range(2):
                P.op("dve", lambda e, s=s, k=k, i2=i2: e.tensor_copy(
                    out=self.qraw[:, 2 * i2 + k, :], in_=rev_ap(s[0:64, k * 512:(k + 1) * 512], 512)), [s], [self.qraw])

    def load_gate(self, qt, h, br):
        P = self.P
        c0 = ((qt - 3) // 4) * 512
        gt = self.rot("gt", self.gt)
        g = self.G[:]
        row = (self.g * 6 + 3 * self.tr + h) * 3 + br
        P.dma("sp", gt, gt[:], self.G, bass.AP(tensor=g.tensor, offset=row * NTOK + c0, ap=[[0, 64], [1, 512]]))
        gr = self.rot("gtr", self.gtr)
        P.op("dve", lambda e: e.tensor_copy(out=gr[:], in_=rev_ap(gt[:], 512)), [gt], [gr])
        return gr

    def store_y(self, qt):
        P = self.P
        c0 = ((qt - 3) // 4) * 512
        bufs = [self.rdt, self.wt, self.tm]
        for h in range(3):
            b = bufs[h]
            P.op("dve", lambda e, b=b, h=h: e.tensor_copy(out=b[:], in_=rev_ap(self.y[:, h, :], 512)), [self.y], [b])
            r0 = (self.g * 6 + 3 * self.tr + h) * 64
            P.dma("sp", self.Y, self.Y[r0:r0 + 64, c0:c0 + 512], b, b[:], nodep_out=True)

    def topc_ap(self, qt):
        return self.topc[(qt - 3) // 4]

    def cmp_bias(self, cc, h, mixed):
        if cc == 0:
            return (self.kval[:, 12:13], self.kval) if mixed else (self.cfc[:, h:h + 1], self.cfc)
        return None if mixed else (self.cf[:, h:h + 1], self.cf)

    def sel_bias(self, kc, h, near):
        if kc < 12:
            return (self.kval[:, kc:kc + 1], self.kval) if near else (self.vbf[:, kc, h:h + 1], self.vbf)
        return None if near else (self.cf[:, h:h + 1], self.cf)

    def win_bias(self, c):
        return (self.kval[:, c:c + 1], self.kval) if c < 12 else None

    def setup_tables(self):
        FAttn.setup_tables(self)
        P = self.P
        for c in range(12):
            P.op("dve", lambda e, c=c: e.tensor_scalar(out=self.vbf[:, c, :], in0=self.cf[:, 0:3],
                                                       scalar1=self.kval[:, c:c + 1], scalar2=None, op0=ALU.add),
                 [self.cf, self.kval], [self.vbf])
        P.op("dve", lambda e: e.tensor_scalar(out=self.cfc[:], in0=self.cf[:], scalar1=self.kval[:, 12:13], scalar2=None,
                                              op0=ALU.add), [self.cf, self.kval], [self.cfc])

    def run(self):
        P = self.P
        st = self.st
        self.gtr = [P.sb("gtr%d" % i, [64, 512], F32) for i in range(2)]
        self.kval = P.sb("kval", [128, 16], F32)
        self.vbf = P.sb("vbf", [128, 12, 3], F32)
        self.cfc = P.sb("cfc", [128, 6], F32)
        P.dma("sp", self.kval, self.kval[:], self.kvalD, self.kvalD[:])
        P.dma("sp", self.hgs, self.hgs[:], self.hg, self.hg[:])
        P.op("pool", lambda e: e.memset(self.ones[:], 1.0), [], [self.ones])
        P.op("pool", lambda e: e.memset(self.epsT[:], 1e-6), [], [self.epsT])
        P.op("pool", lambda e: e.memset(self.tinyT[:], 1e-30), [], [self.tinyT])
        P.op("pool", lambda e: e.memset(self.vsaug[:], 1.0), [], [self.vsaug])
        P.op("pool", lambda e: e.memset(self.vwaug[:], 1.0), [], [self.vwaug])
        P.op("pool", lambda e: e.memset(self.vcaug[:], 1.0), [], [self.vcaug])
        s = self.rot("st", st)
        P.dma("sp", s, s[:, 0:128], self.identD, self.identD[:])
        P.op("pool", lambda e, s=s: e.tensor_copy(out=self.ident[:], in_=s[:, 0:128]), [s], [self.ident])
        s = self.rot("st", st)
        P.dma("sp", s, s[:, 0:512], self.ovlD, self.ovlD[:].rearrange("p c j -> p (c j)"))
        P.op("pool", lambda e, s=s: e.tensor_copy(out=self.ovl[:].rearrange("p c j -> p (c j)"), in_=s[:, 0:512]),
             [s], [self.ovl])
        kvt = [("ksaug", self.ksaug, [128, SEQ]), ("kwT", self.kwT, [64, SEQ]), ("vsaug", self.vsaug, [128, 64 * 128]),
               ("vwaug", self.vwaug, [128, 64 * 128]), ("kcT", self.kcT, [64, 512]), ("vcaug", self.vcaug, [128, 4 * 128])]
        for g in range(2):
            self.set_combo(g, 0)
            if self.jl == 0:
                self.setup_kv()
                for nm, tl, shp in kvt:
                    key = "kvc_%s_%d" % (nm, g)
                    if key not in self.dr:
                        self.dr[key] = P.dram(key, shp, BF16, kind="Internal")
                    c = self.dr[key]
                    src = tl[:] if len(tl[:].shape) == 2 else tl[:].rearrange("p a b -> p (a b)")
                    P.dma("sp", c, c[:], tl, src)
            else:
                for nm, tl, shp in kvt:
                    c = self.dr["kvc_%s_%d" % (nm, g)]
                    dst = tl[:] if len(tl[:].shape) == 2 else tl[:].rearrange("p a b -> p (a b)")
                    P.dma("sp", tl, dst, c, c[:])
            for tr in range(2):
                self.set_combo(g, tr)
                self.setup_tables()
                for qt in QTILES:
                    self.qtile(qt)


def build_fused2():
    nc = bass.Bass("TRN2", target_bir_lowering=False)
    P = Prog(nc)
    dr = fused_dram(P, r3=True)
    I = lambda n, s: P.dram(n, s, F32, kind="Internal")
    dr["KV"] = I("KVp", [768, KPAD + SEQ])
    dr["KVwin"] = I("KVwin", [768, SEQ])
    dr["Xown"] = I("Xown", [1024, NTOK])
    dr["Qown"] = I("Qown", [768, NTOK])
    dr["Gown"] = I("Gown", [36, NTOK])
    dr["Yown"] = I("Yown", [768, NTOK])
    dr["topc"] = P.dram("topc4", [4, 128, 1024], F32, kind="ExternalInput")
    dr["kval"] = P.dram("kval", [128, 16], F32, kind="ExternalInput")
    dr["out"] = P.dram("xT_own", [1024, NTOK], F32, kind="ExternalOutput")
    dr["kd"] = (nc.partition_id() % 4) * 512
    P.push_scope()
    FTok2(P, "L1", dr).run()
    P.pop_scope()
    extract_own(P, dr)
    P.push_scope()
    FTok2(P, "P0", dr).run()
    P.pop_scope()
    for j in range(2):
        P.push_scope()
        FAttn2(P, dr, j).run()
        P.pop_scope()
        P.push_scope()
        FTok2(P, "L3" if j == 0 else "L5", dr).run()
        P.pop_scope()
    P.wait_all("sp", [dr["out"]])
    P.finalize()
    return nc, P


from concourse.bass_utils import run_bass_kernel_spmd


def attn_hg(inp, jlayer):
    hg = np.zeros((128, 8), np.float32)
    hg[:, 0] = np.tile(inp["b_q_gain"][jlayer], 2)
    for i in range(3):
        hg[:, 1 + i] = np.tile(inp["k_gain"][i], 2)
    return hg


def kernel(**inputs):
    inp = {k: np.ascontiguousarray(np.asarray(v, dtype=np.float32)) for k, v in inputs.items()}
    common = dict(host_consts())
    del common["topc"]
    common.update(gains=host_gains(inp), hg=host_hg(inp), convw=host_convw(inp), mem_w_kv=inp["mem_w_kv"],
                  w_out=inp["w_out"], w_up=inp["w_up"], w_down=inp["w_down"], b_w_in=inp["b_w_in"],
                  a_w_in=inp["a_w_in"], w_kv=inp["w_kv_shared"], ahg0=attn_hg(inp, 0), ahg1=attn_hg(inp, 1),
                  pos=np.ascontiguousarray(inp["cmp_pos"].transpose(0, 2, 1)), cmp_w1=inp["cmp_w1"],
                  cmp_w2=inp["cmp_w2"], rel_bias=inp["rel_bias"])
    per_b = []
    for b in range(2):
        xin = np.zeros((1024, XPAD), np.float32)
        xin[:, HALO:] = inp["x"][b].T
        per_b.append(dict(xin=xin, memT=np.ascontiguousarray(inp["mem"][b].T)))
    per_k = []
    for k in range(4):
        tc, kval = host_core_tables(k)
        per_k.append(dict(topc4=tc, kval=kval))
    maps = []
    for c in range(8):
        m = dict(common)
        m.update(per_b[c // 4])
        m.update(per_k[c % 4])
        maps.append(m)
    nc = build_fused2()[0]
    r = run_bass_kernel_spmd(nc, maps, core_ids=list(range(8))).results
    out = np.zeros((2, SEQ, 1024), np.float32)
    for c in range(8):
        b, k = c // 4, c % 4
        o = r[c]["xT_own"]
        for j in range(4):
            t0 = 512 * (4 * j + k)
            out[b, t0:t0 + 512, :] = o[:, j * 512:(j + 1) * 512].T
    return out
```

```python
import numpy as np
import concourse.bass as bass
import concourse.mybir as mybir
from contextlib import ExitStack

F32 = mybir.dt.float32
BF16 = mybir.dt.bfloat16
AF = mybir.ActivationFunctionType
ALU = mybir.AluOpType
AX = mybir.AxisListType

ENGS = ("pe", "act", "dve", "pool", "sp")


class T:
    __slots__ = ("h", "name", "lw", "rd", "dsem", "scoped")

    def __init__(self, h, name):
        self.h = h
        self.name = name
        self.lw = None
        self.rd = []
        self.dsem = None
        self.scoped = False

    def __getitem__(self, k):
        return self.h[k]


class Prog:
    def __init__(self, nc):
        self.nc = nc
        self.es = ExitStack()
        self.ins = {e: [] for e in ENGS}
        self.nsem = 0
        self.dsems = []
        self.free_dsems = []
        self.scope_dsems = []
        self.marked = set()
        self.rw = {e: {} for e in ENGS}
        self.cur = self.es
        self.uid = 0
        self.last_tok = {e: None for e in ENGS}
        self.epoch = 0
        self.ep_of = {}

    def _prune(self, eng, deps):
        out = []
        w = self.rw[eng]
        for d in deps:
            key = (d[0], d[1]); val = d[2]
            if w.get(key, -1) >= val:
                continue
            w[key] = val
            out.append(d)
        return out

    def sb(self, name, shape, dt):
        self.uid += 1
        name = "%s_%d" % (name, self.uid)
        t = T(self.cur.enter_context(self.nc.sbuf_tensor(name, list(shape), dt)), name)
        t.scoped = self.cur is not self.es
        return t

    def ps(self, name, shape, dt=F32):
        self.uid += 1
        name = "%s_%d" % (name, self.uid)
        return T(self.cur.enter_context(self.nc.psum_tensor(name, list(shape), dt)), name)

    def push_scope(self):
        self.cur = ExitStack()

    def pop_scope(self):
        self.barrier()
        self.cur.close()
        self.cur = self.es
        self.free_dsems.extend(self.scope_dsems)
        self.scope_dsems = []

    def barrier(self):
        for e in ENGS:
            deps = [t for e2, t in self.last_tok.items() if e2 != e and t is not None]
            deps += [("d", s, c[1]) for s, c in enumerate(self.dsems) if c[1] > 0]
            deps = self._prune(e, deps)
            self.ins[e].append(dict(deps=deps, fn=None, dma=None))

    def dram(self, name, shape, dt, kind="Internal"):
        return T(self.nc.dram_tensor(name, list(shape), dt, kind=kind), name)

    def _new_dsem(self, scoped=False):
        if scoped and self.free_dsems:
            i = self.free_dsems.pop()
        else:
            h = self.es.enter_context(self.nc.semaphore("d%d" % len(self.dsems)))
            self.dsems.append([h, 0])
            i = len(self.dsems) - 1
        if scoped:
            self.scope_dsems.append(i)
        return i

    def _deps(self, eng, reads, writes):
        deps = []
        for t in reads:
            if t.lw is not None:
                deps.append(t.lw)
        for t in writes:
            if t.lw is not None:
                deps.append(t.lw)
            deps.extend(t.rd)
        out = []
        for d in deps:
            if d[0] == "e":
                if d[1] == eng:
                    continue
            out.append(d)
        if eng != "pe":
            for t in reads:
                if t.lw is not None and t.lw[0] == "e" and t.lw[1] == eng:
                    out.append(t.lw)
        return out

    def op(self, eng, fn, reads=(), writes=()):
        deps = self._prune(eng, self._deps(eng, reads, writes))
        idx = len(self.ins[eng])
        self.ins[eng].append(dict(deps=deps, fn=fn, dma=None))
        tok = ("e", eng, idx)
        self.last_tok[eng] = tok
        self.ep_of[(eng, idx)] = self.epoch
        for t in reads:
            t.rd = [r for r in t.rd if not (r[0] == "e" and r[1] == eng)]
            t.rd.append(tok)
        for t in writes:
            t.lw = tok
            t.rd = []
        return tok

    def dma(self, eng, out_t, out_ap, in_t, in_ap, nodep_out=False, **kw):
        reads = [in_t] if in_t is not None else []
        writes = [out_t] if out_t is not None else []
        deps = []
        for t in reads:
            if t.lw is not None:
                deps.append(t.lw)
        for t in writes:
            if nodep_out:
                continue
            if t.lw is not None:
                deps.append(t.lw)
            deps.extend(t.rd)
        owner = out_t if (out_t is not None and out_t.dsem is not None) else None
        if owner is None:
            owner = out_t if out_t is not None else in_t
        if owner.dsem is None:
            owner.dsem = self._new_dsem(scoped=getattr(owner, "scoped", False))
        s = owner.dsem
        self.dsems[s][1] += 16
        tok = ("d", s, self.dsems[s][1])
        deps = self._prune(eng, deps)
        self.ins[eng].append(dict(deps=deps, fn=None, dma=(out_ap, in_ap, s, kw)))
        for t in reads:
            t.rd.append(tok)
        for t in writes:
            t.lw = tok
            t.rd = []
        return tok

    def wait_all(self, eng, tiles):
        deps = []
        for t in tiles:
            if t.lw is not None:
                deps.append(t.lw)
            deps.extend(t.rd)
        deps = self._prune(eng, deps)
        self.ins[eng].append(dict(deps=deps, fn=None, dma=None))

    def finalize(self):
        nc = self.nc
        for e in ENGS:
            for rec in self.ins[e]:
                for d in rec["deps"]:
                    if d[0] == "e":
                        self.marked.add((d[1], d[2]))
        count_at = {}
        for e in ENGS:
            c = {}
            for i, rec in enumerate(self.ins[e]):
                if (e, i) in self.marked:
                    ep = self.ep_of[(e, i)]
                    c[ep] = c.get(ep, 0) + 1
                    count_at[(e, i)] = c[ep]
        esem = {(e, ep): self.es.enter_context(nc.semaphore("s_%s_%d" % (e, ep)))
                for e in ENGS for ep in range(self.epoch + 1)}
        engobj = {"pe": "tensor", "act": "scalar", "dve": "vector", "pool": "gpsimd", "sp": "sync"}
        block = self.es.enter_context(nc.Block())
        stats = {}

        def make_body(e):
            def body(eng):
                waited = {}
                nwait = 0
                for i, rec in enumerate(self.ins[e]):
                    need = {}
                    for d in rec["deps"]:
                        if d[0] == "e":
                            key = ("e", (d[1], self.ep_of[(d[1], d[2])])); val = count_at[(d[1], d[2])]
                        else:
                            key = ("d", d[1]); val = d[2]
                        if waited.get(key, 0) >= val:
                            continue
                        if need.get(key, 0) < val:
                            need[key] = val
                    for key, val in need.items():
                        h = esem[key[1]] if key[0] == "e" else self.dsems[key[1]][0]
                        eng.wait_ge(h, val)
                        waited[key] = val
                        nwait += 1
                    if rec.get("cc") is not None:
                        rec["cc"](eng).then_inc(self.dsems[rec["ccsem"]][0], 16)
                    elif rec["dma"] is not None:
                        out_ap, in_ap, s, kw = rec["dma"]
                        try:
                            eng.dma_start(out=out_ap, in_=in_ap, **kw).then_inc(self.dsems[s][0], 16)
                        except Exception:
                            print("DMA FAILED:", e, "out=", out_ap, "in=", in_ap)
                            raise
                    elif rec["fn"] is not None:
                        ins = rec["fn"](eng)
                        if (e, i) in self.marked:
                            ins.then_inc(esem[(e, self.ep_of[(e, i)])], 1)
                stats[e] = (len(self.ins[e]), nwait)
            return body

        for e in ENGS:
            if self.ins[e]:
                getattr(block, engobj[e])(make_body(e))
        self.stats = stats
        self.es.close()


NTOK = 2048
HALO = 4
NCOL = NTOK + HALO
RANGES = [(0, 1028), (1028, 1024)]
EPS = 1e-6


def tiles_of(n):
    k = (n + 511) // 512
    out = []
    o = 0
    for i in range(k):
        w = (n - o + (k - i) - 1) // (k - i)
        out.append((o, w))
        o += w
    return out


class TokStage:
    def __init__(self, P, mode):
        self.P = P
        self.mode = mode
        nc = P.nc
        D = lambda n, s: P.dram(n, s, F32, kind="ExternalInput")
        O = lambda n, s: P.dram(n, s, F32, kind="ExternalOutput")
        self.xT = D("xT", [1024, NCOL])
        self.memT = D("memT", [1024, 256])
        self.gains = D("gains", [128, 8 * 16])
        self.hg = D("hg", [128, 16])
        self.flag = D("flag", [128, 1])
        self.mem_w_kv = D("mem_w_kv", [4, 1024, 512])
        self.w_out = D("w_out", [4, 1024, 1024])
        self.w_up = D("w_up", [4, 1024, 4096])
        self.w_down = D("w_down", [4, 4096, 1024])
        self.b_w_in = D("b_w_in", [2, 1024, 1060])
        if mode == "L1":
            self.a_w_in = D("a_w_in", [2, 1024, 2560])
            self.convw = D("convw", [128, 2 * 6 * 3])
            self.w_kv = D("w_kv", [1024, 768])
            self.kvT_o = O("kvT_o", [768, NCOL])
        else:
            self.ytokT = D("ytokT", [768, NCOL])
        self.xT_o = O("xT_o", [1024, NCOL])
        if mode != "L5":
            self.qT_o = O("qT_o", [768, NCOL])
            self.gT_o = O("gT_o", [36, NCOL])

        self.alloc()

    def alloc(self):
        P = self.P
        sb = P.sb
        self.x = sb("x", [128, 8, NCOL], F32)
        self.h = sb("h", [128, 8, 1028], BF16)
        self.y = sb("y", [128, 8, 1028], BF16)
        self.act = sb("act", [128, 8, 1028], BF16)
        self.u = sb("u", [128, 2, 1028], F32)
        self.v = sb("v", [128, 2, 1030], F32)
        self.qm = sb("qm", [128, 2, 1028], F32)
        self.carry = sb("carry", [128, 6, 2], F32)
        self.rstd = sb("rstd", [128, 512], F32)
        self.sq = [sb("sq%d" % i, [128, 512], BF16) for i in range(2)]
        self.tmp = [sb("tmp%d" % i, [128, 512], F32) for i in range(3)]
        self.ost = [sb("ost%d" % i, [128, 512], F32) for i in range(2)]
        self.wb = [sb("wb%d" % i, [128, 8, 256], BF16) for i in range(4)]
        self.memh = sb("memh", [128, 8, 256], BF16)
        self.kraw = sb("kraw", [128, 2, 256], F32)
        self.memk = sb("memk", [128, 2, 256], BF16)
        self.vaug = sb("vaug", [128, 2, 4, 128], BF16)
        self.qn = sb("qn", [128, 512], BF16)
        self.eT = [sb("eT%d" % i, [128, 512], BF16) for i in range(2)]
        self.rd = sb("rd", [64, 512], F32)
        self.g = sb("g", [128, 8 * 16], F32)
        self.hgs = sb("hgs", [128, 16], F32)
        self.flg = sb("flg", [128, 1], F32)
        self.cw = sb("cw", [128, 36], F32)
        self.ones = sb("ones", [128, 128], BF16)
        self.bd = sb("bd", [128, 128], BF16)
        self.epsT = sb("epsT", [128, 1], F32)
        self.pp = [P.ps("pp%d" % i, [128, 512]) for i in range(4)]
        self.pl = [P.ps("pl%d" % i, [128, 512]) for i in range(2)]
        self.po = P.ps("po", [128, 512])
        self.pn = P.ps("pn", [128, 512])
        self.cnt = dict(wf=0, wb=0, pp=0, pl=0, sq=0, tmp=0, ost=0, eT=0)

    def rot(self, name, lst):
        i = self.cnt[name]
        self.cnt[name] += 1
        return lst[i % len(lst)]

    def stream(self, wT, w_ap, ncols):
        P = self.P
        wb = self.rot("wb", self.wb)
        P.dma("pool", wb, wb[:, :, 0:ncols], wT, w_ap.rearrange("(c p) n -> p c n", p=128))
        return wb

    def proj(self, wb, sub, msz, rhs_t, rhs_fn, tiles, consumer):
        P = self.P
        for (t0, w) in tiles:
            ps = self.rot("pp", self.pp)
            for kc in range(8):
                P.op("pe", lambda e, ps=ps, kc=kc, t0=t0, w=w: e.matmul(
                    ps[0:msz, 0:w], lhsT=wb[:, kc, sub * 128: sub * 128 + msz], rhs=rhs_fn(kc, t0, w),
                    start=(kc == 0), stop=(kc == 7)), [wb, rhs_t], [ps])
            consumer(ps, t0, w)

    def setup(self):
        P = self.P
        P.dma("sp", self.g, self.g[:], self.gains, self.gains[:])
        P.dma("sp", self.hgs, self.hgs[:], self.hg, self.hg[:])
        P.dma("sp", self.flg, self.flg[:], self.flag, self.flag[:])
        if self.mode == "L1":
            P.dma("sp", self.cw, self.cw[:], self.convw, self.convw[:])
        for c in range(8):
            P.dma("sp", self.x, self.x[:, c, :], self.xT, self.xT[c * 128:(c + 1) * 128, :], nodep_out=True)
            P.dma("sp", self.u, self.memx(c, 0, 256), self.memT, self.memT[c * 128:(c + 1) * 128, :], nodep_out=True)
        P.op("pool", lambda e: e.memset(self.ones[:], 1.0), [], [self.ones])
        P.op("pool", lambda e: e.memset(self.bd[:], 0.0), [], [self.bd])
        P.op("pool", lambda e: e.memset(self.bd[0:64, 0:64], 1.0), [], [self.bd])
        P.op("pool", lambda e: e.memset(self.bd[64:128, 64:128], 1.0), [], [self.bd])
        P.op("pool", lambda e: e.memset(self.epsT[:], EPS), [], [self.epsT])
        P.op("pool", lambda e: e.memset(self.vaug[:], 1.0), [], [self.vaug])
        P.op("pool", lambda e: e.memset(self.carry[:], 0.0), [], [self.carry])
        self.norm(self.u, self.memx, 256, [(0, 256)], 15, self.memh, 0)

    def gcol(self, which, c):
        return self.g[:, which * 8 + c: which * 8 + c + 1]

    def memx(self, c, t0, w):
        return self.u[:, c // 4, (c % 4) * 256 + t0: (c % 4) * 256 + t0 + w]

    def xs(self, c0):
        return lambda c, t0, w: self.x[:, c, c0 + t0: c0 + t0 + w]

    def norm(self, src, src_fn, n, tiles, which, dst, d0):
        P = self.P
        for (t0, w) in tiles:
            for c in range(8):
                sq = self.rot("sq", self.sq)
                P.op("act", lambda e, sq=sq, c=c, t0=t0, w=w: e.activation(
                    out=sq[:, 0:w], in_=src_fn(c, t0, w), func=AF.Square), [src], [sq])
                P.op("pe", lambda e, sq=sq, c=c, w=w: e.matmul(
                    self.pn[:, 0:w], lhsT=self.ones[:], rhs=sq[:, 0:w], start=(c == 0), stop=(c == 7)),
                    [self.ones, sq], [self.pn])
            P.op("act", lambda e, w=w: e.activation(out=self.rstd[:, 0:w], in_=self.pn[:, 0:w], func=AF.Ln,
                                                    bias=self.epsT[:], scale=1.0 / 1024.0),
                 [self.pn, self.epsT], [self.rstd])
            P.op("act", lambda e, w=w: e.activation(out=self.rstd[:, 0:w], in_=self.rstd[:, 0:w], func=AF.Exp, scale=-0.5),
                 [self.rstd], [self.rstd])
            for c in range(8):
                eng = "dve"
                P.op(eng, lambda e, c=c, t0=t0, w=w: e.scalar_tensor_tensor(
                    out=dst[:, c, d0 + t0: d0 + t0 + w], in0=src_fn(c, t0, w),
                    scalar=self.gcol(which, c), in1=self.rstd[:, 0:w], op0=ALU.mult, op1=ALU.mult),
                    [src, self.g, self.rstd], [dst])

    def headnorm(self, src_t, src_ap_fn, w, gain_ap, dst_t, dst_ap):
        P = self.P
        sq = self.rot("sq", self.sq)
        P.op("act", lambda e, sq=sq: e.activation(out=sq[:, 0:w], in_=src_ap_fn(), func=AF.Square), [src_t], [sq])
        P.op("pe", lambda e, sq=sq: e.matmul(self.pn[:, 0:w], lhsT=self.bd[:], rhs=sq[:, 0:w], start=True, stop=True),
             [self.bd, sq], [self.pn])
        P.op("act", lambda e: e.activation(out=self.rstd[:, 0:w], in_=self.pn[:, 0:w], func=AF.Ln,
                                           bias=self.epsT[:], scale=1.0 / 64.0), [self.pn, self.epsT], [self.rstd])
        P.op("act", lambda e: e.activation(out=self.rstd[:, 0:w], in_=self.rstd[:, 0:w], func=AF.Exp, scale=-0.5),
             [self.rstd], [self.rstd])
        P.op("dve", lambda e: e.scalar_tensor_tensor(out=dst_ap, in0=src_ap_fn(), scalar=gain_ap,
                                                     in1=self.rstd[:, 0:w], op0=ALU.mult, op1=ALU.mult),
             [src_t, self.hgs, self.rstd], [dst_t])

    def mem_kv(self, l):
        P = self.P
        wb = self.stream(self.mem_w_kv, self.mem_w_kv[l, :, 0:256], 256)
        for sub in range(2):
            def cons(ps, t0, w, sub=sub):
                P.op("act", lambda e: e.activation(out=self.kraw[:, sub, 0:256], in_=ps[:, 0:256], func=AF.Copy),
                     [ps], [self.kraw])
            self.proj(wb, sub, 128, self.memh, lambda kc, t0, w: self.memh[:, kc, t0:t0 + w], [(0, 256)], cons)
            self.headnorm(self.kraw, lambda sub=sub: self.kraw[:, sub, 0:256], 256,
                          self.hgs[:, 4 + l: 5 + l], self.memk, self.memk[:, sub, 0:256])
        wb = self.stream(self.mem_w_kv, self.mem_w_kv[l, :, 256:512], 256)
        for mc in range(2):
            ps = self.rot("pp", self.pp)
            for kc in range(8):
                P.op("pe", lambda e, ps=ps, kc=kc, mc=mc: e.matmul(
                    ps[:, 0:256], lhsT=self.memh[:, kc, mc * 128:(mc + 1) * 128], rhs=wb[:, kc, 0:256],
                    start=(kc == 0), stop=(kc == 7)), [self.memh, wb], [ps])
            for hh in range(4):
                P.op("act", lambda e, ps=ps, mc=mc, hh=hh: e.activation(
                    out=self.vaug[:, mc, hh, 0:64], in_=ps[:, hh * 64:(hh + 1) * 64], func=AF.Copy), [ps], [self.vaug])

    def mem_attn(self, l, n, tiles):
        P = self.P
        for (t0, w) in tiles:
            for j in range(2):
                self.headnorm(self.qm, lambda j=j, t0=t0, w=w: self.qm[:, j, t0:t0 + w], w,
                              self.hgs[:, l: l + 1], self.qn, self.qn[:, 0:w])
                for hh in range(2):
                    b0 = 64 * hh
                    hd = 2 * j + hh
                    ets = []
                    for mc in range(2):
                        pl = self.rot("pl", self.pl)
                        P.op("pe", lambda e, pl=pl, mc=mc, b0=b0, j=j, w=w: e.matmul(
                            pl[:, 0:w], lhsT=self.memk[b0:b0 + 64, j, mc * 128:(mc + 1) * 128],
                            rhs=self.qn[b0:b0 + 64, 0:w], start=True, stop=True), [self.memk, self.qn], [pl])
                        eT = self.rot("eT", self.eT)
                        P.op("act", lambda e, pl=pl, eT=eT, w=w: e.activation(
                            out=eT[:, 0:w], in_=pl[:, 0:w], func=AF.Exp, scale=0.125), [pl], [eT])
                        ets.append(eT)
                    for mc in range(2):
                        P.op("pe", lambda e, mc=mc, hd=hd, w=w, eT=ets[mc]: e.matmul(
                            self.po[:, 0:w], lhsT=self.vaug[:, mc, hd, :], rhs=eT[:, 0:w],
                            start=(mc == 0), stop=(mc == 1)), [self.vaug, ets[mc]], [self.po])
                    P.op("act", lambda e, w=w: e.activation(out=self.rd[0:64, 0:w], in_=self.po[64:128, 0:w], func=AF.Ln),
                         [self.po], [self.rd])
                    P.op("act", lambda e, w=w: e.activation(out=self.rd[0:64, 0:w], in_=self.rd[0:64, 0:w], func=AF.Exp,
                                                            scale=-1.0), [self.rd], [self.rd])
                    P.op("dve", lambda e, w=w, b0=b0, j=j, t0=t0: e.tensor_tensor(
                        out=self.y[b0:b0 + 64, 6 + j, t0:t0 + w], in0=self.po[0:64, 0:w], in1=self.rd[0:64, 0:w],
                        op=ALU.mult), [self.po, self.rd], [self.y])

    def add_into_x(self, c0):
        P = self.P

        def mk(ch):
            def cons(ps, t0, w):
                P.op("dve", lambda e: e.tensor_tensor(
                    out=self.x[:, ch, c0 + t0: c0 + t0 + w], in0=ps[:, 0:w], in1=self.x[:, ch, c0 + t0: c0 + t0 + w],
                    op=ALU.add), [ps, self.x], [self.x])
            return cons
        return mk

    def out_proj(self, l, c0, n, tiles):
        mk = self.add_into_x(c0)
        for blk in range(4):
            wb = self.stream(self.w_out, self.w_out[l, :, blk * 256:(blk + 1) * 256], 256)
            for sub in range(2):
                self.proj(wb, sub, 128, self.y, lambda kc, t0, w: self.y[:, kc, t0:t0 + w], tiles, mk(blk * 2 + sub))

    def mlp(self, l, c0, n, tiles):
        P = self.P
        self.norm(self.x, self.xs(c0), n, tiles, 4 + l, self.h, 0)
        mk = self.add_into_x(c0)
        for sg in range(4):
            for blk in range(4):
                wb = self.stream(self.w_up, self.w_up[l, :, sg * 1024 + blk * 256: sg * 1024 + (blk + 1) * 256], 256)
                for sub in range(2):
                    def cons(ps, t0, w, fc=blk * 2 + sub):
                        tmp = self.rot("tmp", self.tmp)
                        P.op("act", lambda e: e.activation(out=tmp[:, 0:w], in_=ps[:, 0:w], func=AF.Relu), [ps], [tmp])
                        P.op("dve", lambda e: e.tensor_tensor(out=self.act[:, fc, t0:t0 + w], in0=tmp[:, 0:w],
                                                              in1=tmp[:, 0:w], op=ALU.mult), [tmp], [self.act])
                    self.proj(wb, sub, 128, self.h, lambda kc, t0, w: self.h[:, kc, t0:t0 + w], tiles, cons)
            for blk in range(4):
                wb = self.stream(self.w_down, self.w_down[l, sg * 1024:(sg + 1) * 1024, blk * 256:(blk + 1) * 256], 256)
                for sub in range(2):
                    self.proj(wb, sub, 128, self.act, lambda kc, t0, w: self.act[:, kc, t0:t0 + w], tiles,
                              mk(blk * 2 + sub))

    def mix_a(self, l, ri, c0, n, tiles):
        P = self.P
        W = self.a_w_in
        hrhs = lambda kc, t0, w: self.h[:, kc, t0:t0 + w]
        for i in range(3):
            wb = self.stream(W, W[l, :, 1536 + 256 * i: 1536 + 256 * (i + 1)], 256)
            for sub in range(2):
                def cons(ps, t0, w, sub=sub):
                    P.op("act", lambda e: e.activation(out=self.u[:, sub, t0:t0 + w], in_=ps[:, 0:w], func=AF.Copy),
                         [ps], [self.u])
                self.proj(wb, sub, 128, self.h, hrhs, tiles, cons)
            wb = self.stream(W, W[l, :, 768 + 256 * i: 768 + 256 * (i + 1)], 256)
            for sub in range(2):
                def cons(ps, t0, w, sub=sub):
                    P.op("dve", lambda e: e.tensor_tensor(out=self.v[:, sub, 2 + t0: 2 + t0 + w], in0=ps[:, 0:w],
                                                          in1=self.u[:, sub, t0:t0 + w], op=ALU.mult),
                         [ps, self.u], [self.v])
                self.proj(wb, sub, 128, self.h, hrhs, tiles, cons)
            for sub in range(2):
                ch = 2 * i + sub
                if ri == 0:
                    P.op("pool", lambda e, sub=sub: e.memset(self.v[:, sub, 0:2], 0.0), [], [self.v])
                    self.halo_fix(sub)
                    P.op("dve", lambda e, sub=sub, ch=ch: e.tensor_copy(out=self.carry[:, ch, :],
                                                                        in_=self.v[:, sub, n:n + 2]),
                         [self.v], [self.carry])
                else:
                    P.op("dve", lambda e, sub=sub, ch=ch: e.tensor_copy(out=self.v[:, sub, 0:2],
                                                                        in_=self.carry[:, ch, :]),
                         [self.carry], [self.v])
                cwc = lambda k, ch=ch: self.cw[:, l * 18 + ch * 3 + k: l * 18 + ch * 3 + k + 1]
                P.op("dve", lambda e, sub=sub, cwc=cwc: e.tensor_scalar(
                    out=self.u[:, sub, 0:n], in0=self.v[:, sub, 0:n], scalar1=cwc(0), scalar2=None, op0=ALU.mult),
                    [self.v, self.cw], [self.u])
                for k in (1, 2):
                    P.op("dve", lambda e, sub=sub, cwc=cwc, k=k: e.scalar_tensor_tensor(
                        out=self.u[:, sub, 0:n], in0=self.v[:, sub, k:k + n], scalar=cwc(k), in1=self.u[:, sub, 0:n],
                        op0=ALU.mult, op1=ALU.add), [self.v, self.cw, self.u], [self.u])
            wb = self.stream(W, W[l, :, 256 * i: 256 * (i + 1)], 256)
            for sub in range(2):
                def cons(ps, t0, w, sub=sub, ch=2 * i + sub):
                    P.op("dve", lambda e: e.tensor_tensor(out=self.y[:, ch, t0:t0 + w], in0=ps[:, 0:w],
                                                          in1=self.u[:, sub, t0:t0 + w], op=ALU.mult),
                         [ps, self.u], [self.y])
                self.proj(wb, sub, 128, self.h, hrhs, tiles, cons)
        wb = self.stream(W, W[l, :, 2304:2560], 256)
        self.qmem_proj(wb, tiles)

    def halo_fix(self, sub):
        self.P.op("dve", lambda e: e.tensor_scalar(
            out=self.v[:, sub, 2:2 + HALO], in0=self.v[:, sub, 2:2 + HALO], scalar1=self.flg[:, 0:1],
            scalar2=None, op0=ALU.mult), [self.v, self.flg], [self.v])

    def store_tile(self, dstT, r, msz, c0, t0, w, ost):
        self.P.dma("sp", dstT, dstT[r:r + msz, c0 + t0: c0 + t0 + w], ost, ost[0:msz, 0:w], nodep_out=True)

    def ytok_src(self, ch, c0, t0, w):
        return self.ytokT[ch * 128:(ch + 1) * 128, c0 + t0: c0 + t0 + w]

    def qmem_proj(self, wb, tiles):
        P = self.P
        for sub in range(2):
            def cons(ps, t0, w, sub=sub):
                P.op("act", lambda e: e.activation(out=self.qm[:, sub, t0:t0 + w], in_=ps[:, 0:w], func=AF.Copy),
                     [ps], [self.qm])
            self.proj(wb, sub, 128, self.h, lambda kc, t0, w: self.h[:, kc, t0:t0 + w], tiles, cons)

    def store_proj(self, wT, w_ap, ncols, dstT, row0, c0, tiles, func=AF.Copy):
        P = self.P
        wb = self.stream(wT, w_ap, ncols)
        nsub = (ncols + 127) // 128
        for sub in range(nsub):
            msz = min(128, ncols - sub * 128)

            def cons(ps, t0, w, sub=sub, msz=msz):
                ost = self.rot("ost", self.ost)
                P.op("act", lambda e: e.activation(out=ost[0:msz, 0:w], in_=ps[0:msz, 0:w], func=func), [ps], [ost])
                r = row0 + sub * 128
                self.store_tile(dstT, r, msz, c0, t0, w, ost)
            self.proj(wb, sub, msz, self.h, lambda kc, t0, w: self.h[:, kc, t0:t0 + w], tiles, cons)

    def pre_b(self, j, c0, n, tiles):
        W = self.b_w_in
        self.norm(self.x, self.xs(c0), n, tiles, 2 + j, self.h, 0)
        for i in range(3):
            self.store_proj(W, W[j, :, 256 * i:256 * (i + 1)], 256, self.qT_o, 256 * i, c0, tiles)
        self.store_proj(W, W[j, :, 768:804], 36, self.gT_o, 0, c0, tiles, func=AF.Sigmoid)

    def post_b(self, j, c0, n, tiles):
        P = self.P
        W = self.b_w_in
        l = 2 + j
        self.norm(self.x, self.xs(c0), n, tiles, l, self.h, 0)
        wb = self.stream(W, W[j, :, 804:1060], 256)
        self.qmem_proj(wb, tiles)
        for ch in range(6):
            for (t0, w) in tiles:
                tmp = self.rot("tmp", self.tmp)
                P.dma("sp", tmp, tmp[:, 0:w], self.ytokT, self.ytok_src(ch, c0, t0, w))
                P.op("act", lambda e, tmp=tmp, ch=ch, t0=t0, w=w: e.activation(out=self.y[:, ch, t0:t0 + w],
                                                                               in_=tmp[:, 0:w], func=AF.Copy), [tmp], [self.y])
        self.mem_attn(l, n, tiles)
        self.out_proj(l, c0, n, tiles)
        self.mlp(l, c0, n, tiles)

    def store_x(self):
        P = self.P
        for c in range(8):
            P.dma("sp", self.xT_o, self.xT_o[c * 128:(c + 1) * 128, :], self.x, self.x[:, c, :], nodep_out=True)

    def build(self):
        P = self.P
        self.setup()
        if self.mode == "L1":
            for l in range(2):
                self.mem_kv(l)
                for ri, (c0, n) in enumerate(RANGES):
                    tiles = tiles_of(n)
                    self.norm(self.x, self.xs(c0), n, tiles, l, self.h, 0)
                    self.mix_a(l, ri, c0, n, tiles)
                    self.mem_attn(l, n, tiles)
                    self.out_proj(l, c0, n, tiles)
                    self.mlp(l, c0, n, tiles)
            for ri, (c0, n) in enumerate(RANGES):
                tiles = tiles_of(n)
                self.norm(self.x, self.xs(c0), n, tiles, 8, self.h, 0)
                for i in range(3):
                    self.store_proj(self.w_kv, self.w_kv[:, 256 * i:256 * (i + 1)], 256, self.kvT_o, 256 * i, c0, tiles)
                self.pre_b(0, c0, n, tiles)
            self.store_x()
            outs = [self.kvT_o, self.qT_o, self.gT_o, self.xT_o]
        elif self.mode == "L3":
            self.mem_kv(2)
            for ri, (c0, n) in enumerate(RANGES):
                tiles = tiles_of(n)
                self.post_b(0, c0, n, tiles)
            for ri, (c0, n) in enumerate(RANGES):
                tiles = tiles_of(n)
                self.pre_b(1, c0, n, tiles)
            self.store_x()
            outs = [self.qT_o, self.gT_o, self.xT_o]
        else:
            self.mem_kv(3)
            for ri, (c0, n) in enumerate(RANGES):
                tiles = tiles_of(n)
                self.post_b(1, c0, n, tiles)
            self.store_x()
            outs = [self.xT_o]
        P.wait_all("sp", outs)
        P.finalize()


def build_tok(mode):
    nc = bass.Bass("TRN2", target_bir_lowering=False)
    P = Prog(nc)
    st = TokStage(P, mode)
    st.build()
    return nc, P


def host_gains(inp):
    g = np.zeros((16, 1024), np.float32)
    g[0:4] = inp["mix_norm"]
    g[4:8] = inp["mlp_norm"]
    g[8] = inp["kv_norm"]
    g[15] = inp["mem_norm"]
    return np.ascontiguousarray(g.reshape(16, 8, 128).transpose(2, 0, 1).reshape(128, 128))


def host_hg(inp):
    hg = np.zeros((128, 16), np.float32)
    for l in range(4):
        hg[:, l] = np.tile(inp["mem_q_gain"][l], 2)
        hg[:, 4 + l] = np.tile(inp["mem_k_gain"][l], 2)
    return hg


def host_convw(inp):
    w = inp["a_conv_w"].reshape(2, 3, 6, 128)
    return np.ascontiguousarray(w.transpose(3, 0, 2, 1).reshape(128, 36))

import math

SEQ = 8192
QT = 512
NQT = SEQ // QT
DC, LC = 3040, 5104
DW, LW = 1023, 1536
DS, LS = 1407, 1920
LTOT = LC + LW + LS
LU, LTW, LTS = 3072, 1408, 1792
MASKV = -30000.0


def rel_bucket_np(d):
    n = np.maximum(d, 0)
    nf = np.maximum(n, 1).astype(np.float32)
    large = 16 + (np.log(nf / np.float32(16)) / np.float32(math.log(1024 / 16)) * np.float32(16)).astype(np.int32)
    large = np.minimum(large, 31)
    return np.where(n < 16, n, large)


def host_consts():
    c = {}
    oh = np.zeros((33, LTOT), np.float32)
    x = np.arange(LC); d = DC - x
    b = rel_bucket_np(d)
    oh[b[d >= 0], x[d >= 0]] = 1.0
    oh[32, x[d < 0]] = 1.0
    x = np.arange(LW); d = DW - x; ok = (d >= 0) & (d < 512)
    b = rel_bucket_np(d)
    oh[b[ok], LC + x[ok]] = 1.0
    oh[32, LC + x[~ok]] = 1.0
    x = np.arange(LS); d = DS - x; ok = d >= 0
    b = rel_bucket_np(d)
    oh[b[ok], LC + LW + x[ok]] = 1.0
    oh[32, LC + LW + x[~ok]] = 1.0
    c["OH"] = oh
    c["ident"] = np.eye(128, dtype=np.float32)
    k = np.arange(SEQ)
    r = 2 * ((k // 128) % 32) + ((k % 128) >= 64)
    ee = np.zeros((64, SEQ), np.float32)
    ee[r, k] = 30000.0
    c["EE"] = ee
    i = np.arange(512)[:, None] * 16
    s = np.arange(128)[None, :] * 64
    ov = np.clip(np.minimum(i + 32, s + 64) - np.maximum(i, s), 0, None).astype(np.float32) / 32.0
    ov[511] = 0.0
    c["ovl"] = np.ascontiguousarray(ov.reshape(4, 128, 128).transpose(1, 0, 2))
    tc = np.zeros((NQT, 128, 4, 2, 128), np.float32)
    j = np.arange(128)[None, :]
    for qt in range(NQT):
        for ss in range(4):
            t = qt * QT + 511 - (128 * ss + np.arange(128))
            back = (t // 64)[:, None] - j
            forced = (j == 0) | ((back >= 0) & (back < 2))
            A = ((back >= 0) & ~forced).astype(np.float32)
            B = np.where(back >= 0, np.where(forced, 1e4, 0.0), -1e30).astype(np.float32)
            tc[qt, :, ss, 0] = A
            tc[qt, :, ss, 1] = B
    c["topc"] = tc.reshape(NQT, 128, 1024)
    return c


class Attn:
    def __init__(self, P):
        self.P = P
        D = lambda n, s: P.dram(n, s, F32, kind="ExternalInput")
        self.qT = D("qT", [384, SEQ])
        self.gB = D("gB", [9, SEQ])
        self.kvT = D("kvT", [384, SEQ])
        self.vtok = D("vtok", [SEQ, 128])
        self.blk = D("blk", [2, 64, 32, 512])
        self.hg = D("hg", [128, 8])
        self.pos = D("pos", [2, 64, 32])
        self.w1 = D("w1", [2, 2048, 256])
        self.w2 = D("w2", [2, 256, 64])
        self.rb6 = D("rb6", [32, 6])
        self.OH = D("OH", [33, LTOT])
        self.identD = D("ident", [128, 128])
        self.EE = D("EE", [64, SEQ])
        self.ovlD = D("ovl", [128, 4, 128])
        self.topc = D("topc", [NQT, 128, 1024])
        self.yT = P.dram("yT", [192, SEQ], F32, kind="ExternalOutput")
        self.Rrow = P.dram("Rrow", [6, LTOT], F32, kind="Internal")
        self.alloc()

    def alloc(self):
        P = self.P
        sb = P.sb
        self.ksaug = sb("ksaug", [128, SEQ], BF16)
        self.kwT = sb("kwT", [64, SEQ], BF16)
        self.vsaug = sb("vsaug", [128, 64, 128], BF16)
        self.vwaug = sb("vwaug", [128, 64, 128], BF16)
        self.kcT = sb("kcT", [64, 512], BF16)
        self.vcaug = sb("vcaug", [128, 4, 128], BF16)
        self.Ubig = sb("Ubig", [128, 6 * LU], BF16)
        self.TTw = [sb("TTw%d" % h, [128, LTW], BF16) for h in range(3)]
        self.TTs = [sb("TTs%d" % h, [128, LTS], BF16) for h in range(3)]
        self.ident = sb("identb", [128, 128], BF16)
        self.ones = sb("ones", [128, 128], BF16)
        self.ovl = sb("ovlb", [128, 4, 128], BF16)
        self.hgs = sb("hgs", [128, 8], F32)
        self.cf = sb("cf", [128, 6], F32)
        self.tab = sb("tab", [33, 6], F32)
        self.epsT = sb("epsT", [128, 1], F32)
        self.tinyT = sb("tinyT", [128, 1], F32)
        self.st = [sb("st%d" % i, [128, 1024], F32) for i in range(2)]
        self.sq = [sb("sq%d" % i, [128, 512], BF16) for i in range(2)]
        self.rq = sb("rq", [128, 512], F32)
        self.qraw = sb("qraw", [64, 6, 512], F32)
        self.qn = sb("qn", [64, 6, 512], BF16)
        self.qaug = sb("qaug", [128, 3, 2, 512], BF16)
        self.pT = [sb("pT%d" % i, [128, 512], BF16) for i in range(9)]
        self.rdq = sb("rdq", [128, 4], F32)
        self.rdb = sb("rdb", [128, 512], F32)
        self.impacc = sb("impacc", [128, 512], F32)
        self.tcs = sb("tcs", [128, 1024], F32)
        self.sc = sb("sc", [128, 128], F32)
        self.sc2 = sb("sc2", [128, 128], F32)
        self.mx = sb("mx", [128, 16], F32)
        self.mneg = sb("mneg", [128, 128], BF16)
        self.gt = [sb("gt%d" % i, [64, 512], F32) for i in range(2)]
        self.rdt = sb("rdt", [64, 512], F32)
        self.wt = sb("wt", [64, 512], F32)
        self.tm = sb("tm", [64, 512], F32)
        self.y = sb("y", [64, 3, 512], F32)
        self.w1b = sb("w1b", [64, 4, 256], BF16)
        self.blkb = sb("blkb", [64, 4, 512], BF16)
        self.posT = sb("posT", [64, 32], F32)
        self.w2b = sb("w2b", [128, 2, 64], BF16)
        self.gx = self.rdb
        self.gy = self.impacc
        self.gT = [sb("gT%d" % i, [128, 512], BF16) for i in range(2)]
        self.A = [P.ps("A%d" % i, [128, 512]) for i in range(3)]
        self.S = [P.ps("S%d" % i, [128, 512]) for i in range(2)]
        self.X0 = P.ps("X0", [128, 512])
        self.X1 = P.ps("X1", [128, 512])
        self.XT = P.ps("XT", [128, 512], BF16)
        self.cnt = {}

    def uslice(self, h, mo):
        return self.Ubig[:, h * LU + mo: h * LU + mo + 512]

    def rot(self, name, lst):
        i = self.cnt.get(name, 0)
        self.cnt[name] = i + 1
        return lst[i % len(lst)]

    def norm64(self, src_t, src_ap, w, gain_ap, dst_t, dst_ap):
        P = self.P
        sq = self.rot("sq", self.sq)
        P.op("act", lambda e: e.activation(out=sq[0:64, 0:w], in_=src_ap, func=AF.Square), [src_t], [sq])
        P.op("pe", lambda e: e.matmul(self.X0[0:64, 0:w], lhsT=self.ones[0:64, 0:64], rhs=sq[0:64, 0:w],
                                      start=True, stop=True), [self.ones, sq], [self.X0])
        P.op("act", lambda e: e.activation(out=self.rq[0:64, 0:w], in_=self.X0[0:64, 0:w], func=AF.Ln,
                                           bias=self.epsT[0:64, :], scale=1.0 / 64.0), [self.X0, self.epsT], [self.rq])
        P.op("act", lambda e: e.activation(out=self.rq[0:64, 0:w], in_=self.rq[0:64, 0:w], func=AF.Exp, scale=-0.5),
             [self.rq], [self.rq])
        P.op("dve", lambda e: e.scalar_tensor_tensor(out=dst_ap, in0=src_ap, scalar=gain_ap, in1=self.rq[0:64, 0:w],
                                                     op0=ALU.mult, op1=ALU.mult), [src_t, self.hgs, self.rq], [dst_t])

    def setup(self):
        P = self.P
        st = self.st
        P.dma("sp", self.hgs, self.hgs[:], self.hg, self.hg[:])
        P.op("pool", lambda e: e.memset(self.ones[:], 1.0), [], [self.ones])
        P.op("pool", lambda e: e.memset(self.epsT[:], 1e-6), [], [self.epsT])
        P.op("pool", lambda e: e.memset(self.tinyT[:], 1e-30), [], [self.tinyT])
        P.op("pool", lambda e: e.memset(self.vsaug[:], 1.0), [], [self.vsaug])
        P.op("pool", lambda e: e.memset(self.vwaug[:], 1.0), [], [self.vwaug])
        P.op("pool", lambda e: e.memset(self.vcaug[:], 1.0), [], [self.vcaug])
        s = self.rot("st", st)
        P.dma("sp", s, s[:, 0:128], self.identD, self.identD[:])
        P.op("pool", lambda e, s=s: e.tensor_copy(out=self.ident[:], in_=s[:, 0:128]), [s], [self.ident])
        s = self.rot("st", st)
        P.dma("sp", s, s[:, 0:512], self.ovlD, self.ovlD[:].rearrange("p c j -> p (c j)"))
        P.op("pool", lambda e, s=s: e.tensor_copy(out=self.ovl[:].rearrange("p c j -> p (c j)"), in_=s[:, 0:512]),
             [s], [self.ovl])
        self.setup_tables()
        self.setup_kv()

    def load_rb(self):
        P = self.P
        rb = self.rb6[:]
        P.dma("sp", self.cf, self.cf[:], self.rb6, bass.AP(tensor=rb.tensor, offset=31 * 6, ap=[[0, 128], [1, 6]]))
        P.dma("sp", self.tab, self.tab[0:32, :], self.rb6, self.rb6[:])

    def setup_tables(self):
        P = self.P
        st = self.st
        P.op("pool", lambda e: e.memset(self.tab[:], MASKV), [], [self.tab])
        self.load_rb()
        P.op("dve", lambda e: e.tensor_scalar(out=self.tab[0:32, :], in0=self.tab[0:32, :], scalar1=8.0, scalar2=None,
                                              op0=ALU.mult), [self.tab], [self.tab])
        o = 0
        while o < LTOT:
            w = min(512, LTOT - o)
            s = self.rot("st", st)
            P.dma("sp", s, s[0:33, 0:w], self.OH, self.OH[:, o:o + w])
            P.op("pe", lambda e, s=s, w=w: e.matmul(self.X1[0:6, 0:w], lhsT=self.tab[:, :], rhs=s[0:33, 0:w],
                                                    start=True, stop=True), [self.tab, s], [self.X1])
            s2 = self.rot("st", st)
            P.op("act", lambda e, s2=s2, w=w: e.activation(out=s2[0:6, 0:w], in_=self.X1[0:6, 0:w], func=AF.Copy),
                 [self.X1], [s2])
            P.dma("sp", self.Rrow, self.Rrow[:, o:o + w], s2, s2[0:6, 0:w])
            o += w
        rr = self.Rrow[:]

        def expand(dst, d0, h, base, pstep, L, nodep=False):
            P.dma("pool", dst, dst[:, d0:d0 + L], self.Rrow,
                  bass.AP(tensor=rr.tensor, offset=h * LTOT + base, ap=[[pstep, 128], [1, L]]), nodep_out=nodep)
        for h in range(6):
            expand(self.Ubig, h * LU, h, 0, 16, LU, nodep=(h > 0))
        for h in range(3):
            expand(self.TTw[h], 0, h, LC, 1, LTW)
            expand(self.TTs[h], 0, h, LC + LW, 1, LTS)

    def load_kv_piece(self, s, slot, c0):
        self.P.dma("sp", s, s[0:64, :], self.kvT, self.kvT[slot * 64:(slot + 1) * 64, c0:c0 + 1024])

    def load_v_piece(self, piece):
        P = self.P
        c0 = piece * 1024
        s = self.rot("st", self.st)
        P.dma("sp", s, s[:, :].rearrange("p (c f) -> p c f", f=128), self.vtok,
              self.vtok[c0:c0 + 1024, :].rearrange("(c p) f -> p c f", p=128))
        sv = s[:, :].rearrange("p (c f) -> p c f", f=128)
        P.op("pool", lambda e, sv=sv, piece=piece: e.tensor_copy(
            out=self.vsaug[:, piece * 8:(piece + 1) * 8, 0:64], in_=sv[:, :, 0:64]), [s], [self.vsaug])
        P.op("pool", lambda e, sv=sv, piece=piece: e.tensor_copy(
            out=self.vwaug[:, piece * 8:(piece + 1) * 8, 0:64], in_=sv[:, :, 64:128]), [s], [self.vwaug])

    def setup_kv(self):
        P = self.P
        st = self.st
        for piece in range(8):
            c0 = piece * 1024
            for slot, gcol, dst in ((2, 2, self.ksaug), (4, 3, self.kwT)):
                s = self.rot("st", st)
                self.load_kv_piece(s, slot, c0)
                for hf in range(2):
                    a = hf * 512
                    self.norm64(s, s[0:64, a:a + 512], 512, self.hgs[0:64, gcol:gcol + 1], dst,
                                dst[0:64, c0 + a:c0 + a + 512])
            P.dma("pool", self.ksaug, self.ksaug[64:128, c0:c0 + 1024], self.EE, self.EE[:, c0:c0 + 1024])
            self.load_v_piece(piece)
        self.compress()

    def compress_rhs(self, kvi, tg):
        P = self.P
        for th in range(2):
            s = self.rot("st", self.st)
            t0 = tg * 4 + th * 2
            P.dma("sp", s, s[0:64, :].rearrange("p (t n) -> p t n", n=512), self.blk,
                  self.blk[kvi, :, t0:t0 + 2, :])
            for tt in range(2):
                t = t0 + tt
                P.op("dve", lambda e, s=s, tt=tt, th=th, t=t: e.tensor_scalar(
                    out=self.blkb[:, th * 2 + tt, :], in0=s[0:64, tt * 512:(tt + 1) * 512],
                    scalar1=self.posT[:, t:t + 1], scalar2=None, op0=ALU.add), [s, self.posT], [self.blkb])

    def compress_begin(self, kvi):
        pass

    def compress(self):
        P = self.P
        st = self.st
        H = [self.A[0], self.A[1]]
        for kvi in range(2):
            self.compress_begin(kvi)
            P.dma("sp", self.posT, self.posT[:], self.pos, self.pos[kvi])
            s = self.rot("st", st)
            P.dma("sp", s, s[:, 0:128].rearrange("p (c d) -> p c d", d=64), self.w2,
                  self.w2[kvi].rearrange("(c p) d -> p c d", p=128))
            P.op("pool", lambda e, s=s: e.tensor_copy(out=self.w2b[:].rearrange("p c d -> p (c d)"), in_=s[:, 0:128]),
                 [s], [self.w2b])
            for tg in range(8):
                s = self.rot("st", st)
                P.dma("sp", s, s[0:64, :].rearrange("p (t j) -> p t j", j=256), self.w1,
                      self.w1[kvi, tg * 256:(tg + 1) * 256, :].rearrange("(t d) j -> d t j", d=64))
                P.op("pool", lambda e, s=s: e.tensor_copy(out=self.w1b[:].rearrange("p t j -> p (t j)"),
                                                          in_=s[0:64, :]), [s], [self.w1b])
                self.compress_rhs(kvi, tg)
                for tl in range(4):
                    t = tg * 4 + tl
                    for jc in range(2):
                        P.op("pe", lambda e, tl=tl, jc=jc, t=t: e.matmul(
                            H[jc][:, :], lhsT=self.w1b[:, tl, jc * 128:(jc + 1) * 128], rhs=self.blkb[:, tl, :],
                            start=(t == 0), stop=(t == 31)), [self.w1b, self.blkb], [H[jc]])
            for jc in range(2):
                P.op("act", lambda e, jc=jc: e.activation(out=self.gx[:], in_=H[jc][:, :], func=AF.Copy),
                     [H[jc]], [self.gx])
                P.op("pool", lambda e: e.tensor_tensor(out=self.gy[:], in0=self.gx[:], in1=self.gx[:], op=ALU.mult),
                     [self.gx], [self.gy])
                P.op("dve", lambda e: e.tensor_scalar(out=self.gy[:], in0=self.gy[:], scalar1=0.044715, scalar2=1.0,
                                                      op0=ALU.mult, op1=ALU.add), [self.gy], [self.gy])
                P.op("pool", lambda e: e.tensor_tensor(out=self.gy[:], in0=self.gy[:], in1=self.gx[:], op=ALU.mult),
                     [self.gy, self.gx], [self.gy])
                P.op("act", lambda e: e.activation(out=self.gy[:], in_=self.gy[:], func=AF.Sigmoid,
                                                   scale=1.5957691216057308), [self.gy], [self.gy])
                P.op("dve", lambda e, jc=jc: e.tensor_tensor(out=self.gT[jc][:], in0=self.gx[:], in1=self.gy[:],
                                                             op=ALU.mult), [self.gx, self.gy], [self.gT[jc]])
            if kvi == 0:
                for jc in range(2):
                    P.op("pe", lambda e, jc=jc: e.matmul(self.X1[0:64, :], lhsT=self.w2b[:, jc, :], rhs=self.gT[jc][:],
                                                         start=(jc == 0), stop=(jc == 1)),
                         [self.w2b, self.gT[jc]], [self.X1])
                s = self.rot("st", st)
                P.op("act", lambda e, s=s: e.activation(out=s[0:64, 0:512], in_=self.X1[0:64, :], func=AF.Copy),
                     [self.X1], [s])
                self.norm64(s, s[0:64, 0:512], 512, self.hgs[0:64, 1:2], self.kcT, self.kcT[:, :])
            else:
                for cc in range(4):
                    for jc in range(2):
                        P.op("pe", lambda e, jc=jc, cc=cc: e.matmul(
                            self.X1[:, cc * 64:(cc + 1) * 64], lhsT=self.gT[jc][:, cc * 128:(cc + 1) * 128],
                            rhs=self.w2b[:, jc, :], start=(jc == 0), stop=(jc == 1)),
                            [self.w2b, self.gT[jc]], [self.X1])
                    P.op("act", lambda e, cc=cc: e.activation(out=self.vcaug[:, cc, 0:64],
                                                              in_=self.X1[:, cc * 64:(cc + 1) * 64], func=AF.Copy),
                         [self.X1], [self.vcaug])

    def combine(self, qt, h, br, first):
        P = self.P
        A = self.A[h]
        s0 = qt * QT
        gt = self.load_gate(qt, h, br)
        P.op("act", lambda e: e.activation(out=self.rdt[:], in_=A[64:128, :], func=AF.Ln, bias=self.tinyT[0:64, :]),
             [A, self.tinyT], [self.rdt])
        P.op("act", lambda e: e.activation(out=self.rdt[:], in_=self.rdt[:], func=AF.Exp, scale=-1.0),
             [self.rdt], [self.rdt])
        P.op("pool", lambda e: e.tensor_tensor(out=self.wt[:], in0=self.rdt[:], in1=gt[:], op=ALU.mult),
             [self.rdt, gt], [self.wt])
        if first:
            P.op("dve", lambda e: e.tensor_tensor(out=self.y[:, h, :], in0=A[0:64, :], in1=self.wt[:], op=ALU.mult),
                 [A, self.wt], [self.y])
        else:
            P.op("dve", lambda e: e.tensor_tensor(out=self.tm[:], in0=A[0:64, :], in1=self.wt[:], op=ALU.mult),
                 [A, self.wt], [self.tm])
            P.op("pool", lambda e: e.tensor_tensor(out=self.y[:, h, :], in0=self.y[:, h, :], in1=self.tm[:],
                                                   op=ALU.add), [self.y, self.tm], [self.y])

    def exp(self, S, pT, bias):
        P = self.P
        if bias is None:
            P.op("act", lambda e: e.activation(out=pT[:], in_=S[:, :], func=AF.Exp, scale=0.125), [S], [pT])
        else:
            bap, bt = bias
            P.op("act", lambda e: e.activation(out=pT[:], in_=S[:, :], func=AF.Exp, scale=0.125, bias=bap),
                 [S, bt], [pT])

    def cmp_bias(self, cc, h, mixed):
        return None if mixed else (self.cf[:, h:h + 1], self.cf)

    def sel_bias(self, kc, h, near):
        return None if near else (self.cf[:, h:h + 1], self.cf)

    def win_bias(self, c):
        return None

    def topc_ap(self, qt):
        return self.topc[qt]

    def load_q(self, qt):
        s0 = qt * QT
        self.P.dma("sp", self.qraw, self.qraw[:], self.qT, self.qT[:, s0:s0 + QT].rearrange("(h d) t -> d h t", d=64))

    def load_gate(self, qt, h, br):
        s0 = qt * QT
        gt = self.rot("gt", self.gt)
        g = self.gB[:]
        self.P.dma("sp", gt, gt[:], self.gB,
                   bass.AP(tensor=g.tensor, offset=(h * 3 + br) * SEQ + s0, ap=[[0, 64], [1, 512]]))
        return gt

    def store_y(self, qt):
        s0 = qt * QT
        self.P.dma("sp", self.yT, self.yT[:, s0:s0 + QT].rearrange("(h d) t -> d h t", d=64), self.y, self.y[:],
                   nodep_out=True)

    def qtile(self, qt):
        P = self.P
        s0 = qt * QT
        self.load_q(qt)
        P.dma("sp", self.tcs, self.tcs[:], self.topc, self.topc_ap(qt))
        for h in range(6):
            self.norm64(self.qraw, self.qraw[:, h, :], 512, self.hgs[0:64, 0:1], self.qn, self.qn[:, h, :])
        for h in range(3):
            for half in range(2):
                P.op("pool", lambda e, h=h, half=half: e.tensor_copy(out=self.qaug[0:64, h, half, :],
                                                                     in_=self.qn[:, h, :]), [self.qn], [self.qaug])
        for h in range(6):
            chunks = [cc for cc in range(4) if qt - 4 * cc >= 0]
            pts = []
            for cc in chunks:
                k = qt - 4 * cc
                mixed = k <= 5
                S = self.rot("S", self.S)
                P.op("pe", lambda e, S=S, cc=cc, h=h, mixed=mixed: e.matmul(
                    S[:, :], lhsT=self.kcT[:, cc * 128:(cc + 1) * 128], rhs=self.qn[:, h, :], start=True,
                    stop=(not mixed)), [self.kcT, self.qn], [S])
                if mixed:
                    mo = 512 * (5 - k)
                    P.op("pe", lambda e, S=S, h=h, mo=mo: e.matmul(S[:, :], lhsT=self.ident[:],
                                                                   rhs=self.uslice(h, mo), start=False, stop=True),
                         [self.ident, self.Ubig], [S])
                pT = self.rot("pT", self.pT)
                self.exp(S, pT, self.cmp_bias(cc, h, mixed))
                pts.append(pT)
            n = len(chunks)
            if h < 3:
                for i, (cc, pT) in enumerate(zip(chunks, pts)):
                    P.op("pe", lambda e, pT=pT, i=i, cc=cc, h=h: e.matmul(
                        self.A[h][:, :], lhsT=self.vcaug[:, cc, :], rhs=pT[:], start=(i == 0), stop=(i == n - 1)),
                        [self.vcaug, pT], [self.A[h]])
            for ss in range(4):
                for i, (cc, pT) in enumerate(zip(chunks, pts)):
                    P.op("pe", lambda e, pT=pT, i=i, cc=cc, ss=ss: e.matmul(
                        self.X1[:, ss * 128:(ss + 1) * 128], lhsT=pT[:, ss * 128:(ss + 1) * 128], rhs=self.ovl[:, cc, :],
                        start=(i == 0), stop=(i == n - 1)), [pT, self.ovl], [self.X1])
            for ss in range(4):
                for i, pT in enumerate(pts):
                    P.op("pe", lambda e, pT=pT, i=i, ss=ss: e.matmul(
                        self.X0[:, ss:ss + 1], lhsT=pT[:, ss * 128:(ss + 1) * 128], rhs=self.ones[:, 0:1],
                        start=(i == 0), stop=(i == n - 1)), [pT, self.ones], [self.X0])
            P.op("dve", lambda e: e.tensor_scalar(out=self.rdq[:], in0=self.X0[:, 0:4], scalar1=1e-30, scalar2=None,
                                                  op0=ALU.max), [self.X0], [self.rdq])
            P.op("dve", lambda e: e.reciprocal(out=self.rdq[:], in_=self.rdq[:]), [self.rdq], [self.rdq])
            for ss in range(4):
                if h == 0:
                    P.op("dve", lambda e, ss=ss: e.tensor_scalar(
                        out=self.impacc[:, ss * 128:(ss + 1) * 128], in0=self.X1[:, ss * 128:(ss + 1) * 128],
                        scalar1=self.rdq[:, ss:ss + 1], scalar2=None, op0=ALU.mult), [self.X1, self.rdq], [self.impacc])
                else:
                    P.op("dve", lambda e, ss=ss: e.scalar_tensor_tensor(
                        out=self.impacc[:, ss * 128:(ss + 1) * 128], in0=self.X1[:, ss * 128:(ss + 1) * 128],
                        scalar=self.rdq[:, ss:ss + 1], in1=self.impacc[:, ss * 128:(ss + 1) * 128],
                        op0=ALU.mult, op1=ALU.add), [self.X1, self.rdq, self.impacc], [self.impacc])
            if h < 3:
                self.combine(qt, h, 0, True)
        for ss in range(4):
            tA = self.tcs[:, ss * 256: ss * 256 + 128]
            tB = self.tcs[:, ss * 256 + 128: ss * 256 + 256]
            P.op("dve", lambda e, ss=ss, tA=tA: e.tensor_tensor(out=self.sc[:], in0=self.impacc[:, ss * 128:(ss + 1) * 128],
                                                                in1=tA, op=ALU.mult), [self.impacc, self.tcs], [self.sc])
            P.op("dve", lambda e, tB=tB: e.tensor_tensor(out=self.sc[:], in0=self.sc[:], in1=tB, op=ALU.add),
                 [self.sc, self.tcs], [self.sc])
            P.op("dve", lambda e: e.max(out=self.mx[:, 0:8], in_=self.sc[:]), [self.sc], [self.mx])
            P.op("dve", lambda e: e.match_replace(out=self.sc2[:], in_to_replace=self.mx[:, 0:8], in_values=self.sc[:],
                                                  imm_value=-1e30), [self.sc, self.mx], [self.sc2])
            P.op("dve", lambda e: e.max(out=self.mx[:, 8:16], in_=self.sc2[:]), [self.sc2], [self.mx])
            P.op("dve", lambda e: e.tensor_scalar(out=self.mneg[:], in0=self.sc[:], scalar1=self.mx[:, 15:16], scalar2=1.0,
                                                  op0=ALU.is_ge, op1=ALU.subtract), [self.sc, self.mx], [self.mneg])
            P.op("pe", lambda e, ss=ss: e.transpose(self.XT[:, ss * 128:(ss + 1) * 128], self.mneg[:], self.ident[:]),
                 [self.mneg, self.ident], [self.XT])
        for h in range(3):
            for half in range(2):
                eng = "act" if (h + half) % 2 == 0 else "dve"
                if eng == "act":
                    P.op("act", lambda e, h=h, half=half: e.activation(out=self.qaug[64:128, h, half, :],
                                                                       in_=self.XT[half * 64:(half + 1) * 64, :],
                                                                       func=AF.Copy), [self.XT], [self.qaug])
                else:
                    P.op("dve", lambda e, h=h, half=half: e.tensor_copy(out=self.qaug[64:128, h, half, :],
                                                                        in_=self.XT[half * 64:(half + 1) * 64, :]),
                         [self.XT], [self.qaug])
        nkc = 4 * qt + 4

        def sel_qk(kc, h):
            k0 = 128 * kc
            ep = s0 + 511 - k0
            near = ep <= DS
            half = kc // 32
            S = self.rot("S", self.S)
            P.op("pe", lambda e: e.matmul(S[:, :], lhsT=self.ksaug[:, k0:k0 + 128], rhs=self.qaug[:, h, half, :],
                                          start=True, stop=(not near)), [self.ksaug, self.qaug], [S])
            if near:
                mo = DS - ep
                P.op("pe", lambda e: e.matmul(S[:, :], lhsT=self.ident[:], rhs=self.TTs[h][:, mo:mo + 512],
                                              start=False, stop=True), [self.ident, self.TTs[h]], [S])
            return (kc, h, near, S)

        def sel_rest(kc, h, near, S):
            pT = self.rot("pT", self.pT)
            self.exp(S, pT, self.sel_bias(kc, h, near))
            P.op("pe", lambda e: e.matmul(self.A[h][:, :], lhsT=self.vsaug[:, kc, :], rhs=pT[:],
                                          start=(kc == 0), stop=(kc == nkc - 1)), [self.vsaug, pT], [self.A[h]])

        prev = None
        for kc in range(nkc):
            for h in range(3):
                cur = sel_qk(kc, h)
                if prev is not None:
                    sel_rest(*prev)
                prev = cur
        sel_rest(*prev)
        for h in range(3):
            self.combine(qt, h, 1, False)
        rs = [r for r in range(8) if s0 - 512 + 128 * r >= 0]

        def win_qk(i, r, h):
            k0 = s0 - 512 + 128 * r
            mo = 128 * r
            S = self.rot("S", self.S)
            P.op("pe", lambda e: e.matmul(S[:, :], lhsT=self.kwT[:, k0:k0 + 128], rhs=self.qn[:, h, :],
                                          start=True, stop=False), [self.kwT, self.qn], [S])
            P.op("pe", lambda e: e.matmul(S[:, :], lhsT=self.ident[:], rhs=self.TTw[h][:, mo:mo + 512],
                                          start=False, stop=True), [self.ident, self.TTw[h]], [S])
            return (i, k0, h, S)

        def win_rest(i, k0, h, S):
            pT = self.rot("pT", self.pT)
            self.exp(S, pT, self.win_bias(k0 // 128))
            P.op("pe", lambda e: e.matmul(self.A[h][:, :], lhsT=self.vwaug[:, k0 // 128, :], rhs=pT[:],
                                          start=(i == 0), stop=(i == len(rs) - 1)), [self.vwaug, pT], [self.A[h]])

        prev = None
        for i, r in enumerate(rs):
            for h in range(3):
                cur = win_qk(i, r, h)
                if prev is not None:
                    win_rest(*prev)
                prev = cur
        win_rest(*prev)
        for h in range(3):
            self.combine(qt, h, 2, False)
        self.store_y(qt)

    def build(self, nqt=NQT):
        self.setup()
        for qt in range(nqt):
            self.qtile(qt)
        self.P.wait_all("sp", [self.yT])
        self.P.finalize()


def build_attn(nqt=NQT):
    nc = bass.Bass("TRN2", target_bir_lowering=False)
    P = Prog(nc)
    a = Attn(P)
    a.build(nqt)
    return nc, P


XPAD = SEQ + HALO


def rev_ap(ap, n):
    dims = [list(d) for d in ap.ap]
    dims[-1] = [-1, n]
    return bass.AP(tensor=ap.tensor, offset=ap.offset + (n - 1), ap=dims)


class FTok(TokStage):
    def __init__(self, P, mode, dr):
        self.P = P
        self.mode = mode
        self.dr = dr
        for k in ("memT", "gains", "hg", "mem_w_kv", "w_out", "w_up", "w_down", "b_w_in", "a_w_in", "convw", "w_kv"):
            setattr(self, k, dr[k])
        self.xsrc = dr["xin"] if mode == "L1" else dr["X"]
        self.kvT_o, self.qT_o, self.gT_o = dr["KV"], dr["Q"], dr["G"]
        self.ytokT = dr["Y"]
        self.chunk = 0
        self.alloc()

    def halo_fix(self, sub):
        if self.chunk == 0:
            self.P.op("pool", lambda e: e.memset(self.v[:, sub, 2:2 + HALO], 0.0), [], [self.v])

    def store_tile(self, dstT, r, msz, c0, t0, w, ost):
        a = c0 + t0
        lo = HALO if a == 0 else 0
        g0 = NTOK * self.chunk + a + lo - HALO
        self.P.dma("sp", dstT, dstT[r:r + msz, g0:g0 + (w - lo)], ost, ost[0:msz, lo:w], nodep_out=True)

    def ytok_src(self, ch, c0, t0, w):
        g0 = NTOK * self.chunk + c0 + t0
        return self.ytokT[ch * 128:(ch + 1) * 128, g0:g0 + w]

    def setup_consts(self):
        P = self.P
        P.dma("sp", self.g, self.g[:], self.gains, self.gains[:])
        P.dma("sp", self.hgs, self.hgs[:], self.hg, self.hg[:])
        if self.mode == "L1":
            P.dma("sp", self.cw, self.cw[:], self.convw, self.convw[:])
        for c in range(8):
            P.dma("sp", self.u, self.memx(c, 0, 256), self.memT, self.memT[c * 128:(c + 1) * 128, :], nodep_out=True)
        P.op("pool", lambda e: e.memset(self.ones[:], 1.0), [], [self.ones])
        P.op("pool", lambda e: e.memset(self.bd[:], 0.0), [], [self.bd])
        P.op("pool", lambda e: e.memset(self.bd[0:64, 0:64], 1.0), [], [self.bd])
        P.op("pool", lambda e: e.memset(self.bd[64:128, 64:128], 1.0), [], [self.bd])
        P.op("pool", lambda e: e.memset(self.epsT[:], EPS), [], [self.epsT])
        P.op("pool", lambda e: e.memset(self.vaug[:], 1.0), [], [self.vaug])
        self.norm(self.u, self.memx, 256, [(0, 256)], 15, self.memh, 0)

    def load_x(self):
        P = self.P
        g0 = NTOK * self.chunk
        for c in range(8):
            P.dma("sp", self.x, self.x[:, c, :], self.xsrc, self.xsrc[c * 128:(c + 1) * 128, g0:g0 + NCOL],
                  nodep_out=(c > 0))

    def store_x(self):
        P = self.P
        dst = self.dr["out"] if self.mode == "L5" else self.dr["X"]
        off = 0 if self.mode == "L5" else HALO
        g0 = NTOK * self.chunk + off
        for c in range(8):
            P.dma("sp", dst, dst[c * 128:(c + 1) * 128, g0:g0 + NTOK], self.x, self.x[:, c, HALO:NCOL], nodep_out=True)

    def run(self):
        self.setup_consts()
        for chunk in range(4):
            self.chunk = chunk
            self.load_x()
            if self.mode == "L1":
                for l in range(2):
                    self.mem_kv(l)
                    for ri, (c0, n) in enumerate(RANGES):
                        tiles = tiles_of(n)
                        self.norm(self.x, self.xs(c0), n, tiles, l, self.h, 0)
                        self.mix_a(l, ri, c0, n, tiles)
                        self.mem_attn(l, n, tiles)
                        self.out_proj(l, c0, n, tiles)
                        self.mlp(l, c0, n, tiles)
                for ri, (c0, n) in enumerate(RANGES):
                    tiles = tiles_of(n)
                    self.norm(self.x, self.xs(c0), n, tiles, 8, self.h, 0)
                    for i in range(3):
                        self.store_proj(self.w_kv, self.w_kv[:, 256 * i:256 * (i + 1)], 256, self.kvT_o, 256 * i, c0, tiles)
                    if not getattr(self, "skip_pre0", False):
                        self.pre_b(0, c0, n, tiles)
            else:
                j = 0 if self.mode == "L3" else 1
                self.mem_kv(2 + j)
                for ri, (c0, n) in enumerate(RANGES):
                    self.post_b(j, c0, n, tiles_of(n))
                if j == 0:
                    for ri, (c0, n) in enumerate(RANGES):
                        self.pre_b(1, c0, n, tiles_of(n))
            self.store_x()


class FAttn(Attn):
    def __init__(self, P, dr, jl):
        self.P = P
        self.dr = dr
        self.jl = jl
        self.Q, self.G, self.KV, self.Y = dr["Q"], dr["G"], dr["KV"], dr["Y"]
        self.hg = dr["ahg%d" % jl]
        self.pos, self.w1, self.w2 = dr["pos"], dr["cmp_w1"], dr["cmp_w2"]
        self.rel = dr["rel_bias"]
        self.OH, self.identD, self.EE, self.ovlD, self.topc = dr["OH"], dr["ident"], dr["EE"], dr["ovl"], dr["topc"]
        self.Rrow = dr["Rrow"]
        self.alloc()

    def alloc(self):
        Attn.alloc(self)
        self.ub_view = self.Ubig[0:64, 0:16 * 513].rearrange("p (r n) -> p r n", n=513)

    def set_combo(self, g, tr):
        self.g, self.tr = g, tr
        own = [3 * tr + i for i in range(3)]
        oth = [3 * (1 - tr) + i for i in range(3)]
        self.heads = own + oth

    def load_rb(self):
        P = self.P
        rel = self.rel[:]
        for i, hh in enumerate(self.heads):
            col = self.g * 6 + hh
            P.dma("sp", self.cf, self.cf[:, i:i + 1], self.rel,
                  bass.AP(tensor=rel.tensor, offset=31 * 12 + col, ap=[[0, 128], [1, 1]]))
            P.dma("sp", self.tab, self.tab[0:32, i:i + 1], self.rel, self.rel[:, col:col + 1],
                  allow_slow_non_contiguous=True)

    def load_kv_piece(self, s, slot, c0):
        r0 = slot * 128 + self.g * 64
        self.P.dma("sp", s, s[0:64, :], self.KV, self.KV[r0:r0 + 64, c0:c0 + 1024])

    def load_v_piece(self, piece):
        P = self.P
        c0 = piece * 1024
        for slot, dst in ((3, self.vsaug), (5, self.vwaug)):
            r0 = slot * 128 + self.g * 64
            for hf in range(2):
                sq = self.rot("sq", self.sq)
                a = c0 + hf * 512
                P.dma("pool", sq, sq[0:64, :], self.KV, self.KV[r0:r0 + 64, a:a + 512])
                for c in range(4):
                    P.op("pe", lambda e, sq=sq, c=c: e.transpose(self.XT[:, c * 64:(c + 1) * 64],
                                                                 sq[0:64, c * 128:(c + 1) * 128], self.ident[0:64, 0:64]),
                         [sq, self.ident], [self.XT])
                ch0 = piece * 8 + hf * 4
                P.op("act", lambda e, dst=dst, ch0=ch0: e.activation(
                    out=dst[:, ch0:ch0 + 4, 0:64], in_=self.XT[:, 0:256].rearrange("p (c d) -> p c d", d=64),
                    func=AF.Copy), [self.XT], [dst])

    def compress_begin(self, kvi):
        P = self.P
        P.op("pool", lambda e: e.memset(self.ub_view[:, :, 512:513], 0.0), [], [self.Ubig])
        for piece in range(8):
            s = self.rot("st", self.st)
            self.load_kv_piece(s, kvi, piece * 1024)
            P.op("pool", lambda e, s=s, piece=piece: e.tensor_copy(
                out=self.ub_view[:, :, piece * 64:(piece + 1) * 64],
                in_=s[0:64, :].rearrange("p (n r) -> p r n", r=16)), [s], [self.Ubig])

    def compress_rhs(self, kvi, tg):
        P = self.P
        for tl in range(4):
            t = tg * 4 + tl
            r, off = t % 16, t // 16
            P.op("dve", lambda e, tl=tl, t=t, r=r, off=off: e.tensor_scalar(
                out=self.blkb[:, tl, :], in0=self.ub_view[:, r, off:off + 512], scalar1=self.posT[:, t:t + 1],
                scalar2=None, op0=ALU.add), [self.Ubig, self.posT], [self.blkb])

    def load_q(self, qt):
        P = self.P
        s0 = qt * QT
        for i2 in range(3):
            s = self.rot("st", self.st)
            for k in range(2):
                hh = self.heads[2 * i2 + k]
                r0 = (self.g * 6 + hh) * 64
                P.dma("sp", s, s[0:64, k * 512:(k + 1) * 512], self.Q, self.Q[r0:r0 + 64, s0:s0 + QT], nodep_out=(k > 0))
            for k in range(2):
                P.op("dve", lambda e, s=s, k=k, i2=i2: e.tensor_copy(
                    out=self.qraw[:, 2 * i2 + k, :], in_=rev_ap(s[0:64, k * 512:(k + 1) * 512], 512)), [s], [self.qraw])

    def load_gate(self, qt, h, br):
        P = self.P
        s0 = qt * QT
        gt = self.rot("gt", self.gt)
        g = self.G[:]
        row = (self.g * 6 + 3 * self.tr + h) * 3 + br
        P.dma("sp", gt, gt[:], self.G, bass.AP(tensor=g.tensor, offset=row * SEQ + s0, ap=[[0, 64], [1, 512]]))
        gr = self.rot("gtr", self.gtr)
        P.op("dve", lambda e: e.tensor_copy(out=gr[:], in_=rev_ap(gt[:], 512)), [gt], [gr])
        return gr

    def store_y(self, qt):
        P = self.P
        s0 = qt * QT
        bufs = [self.rdt, self.wt, self.tm]
        for h in range(3):
            b = bufs[h]
            P.op("dve", lambda e, b=b, h=h: e.tensor_copy(out=b[:], in_=rev_ap(self.y[:, h, :], 512)), [self.y], [b])
            r0 = (self.g * 6 + 3 * self.tr + h) * 64
            P.dma("sp", self.Y, self.Y[r0:r0 + 64, HALO + s0:HALO + s0 + QT], b, b[:], nodep_out=True)

    def run(self):
        P = self.P
        st = self.st
        self.gtr = [P.sb("gtr%d" % i, [64, 512], F32) for i in range(2)]
        P.dma("sp", self.hgs, self.hgs[:], self.hg, self.hg[:])
        P.op("pool", lambda e: e.memset(self.ones[:], 1.0), [], [self.ones])
        P.op("pool", lambda e: e.memset(self.epsT[:], 1e-6), [], [self.epsT])
        P.op("pool", lambda e: e.memset(self.tinyT[:], 1e-30), [], [self.tinyT])
        P.op("pool", lambda e: e.memset(self.vsaug[:], 1.0), [], [self.vsaug])
        P.op("pool", lambda e: e.memset(self.vwaug[:], 1.0), [], [self.vwaug])
        P.op("pool", lambda e: e.memset(self.vcaug[:], 1.0), [], [self.vcaug])
        s = self.rot("st", st)
        P.dma("sp", s, s[:, 0:128], self.identD, self.identD[:])
        P.op("pool", lambda e, s=s: e.tensor_copy(out=self.ident[:], in_=s[:, 0:128]), [s], [self.ident])
        s = self.rot("st", st)
        P.dma("sp", s, s[:, 0:512], self.ovlD, self.ovlD[:].rearrange("p c j -> p (c j)"))
        P.op("pool", lambda e, s=s: e.tensor_copy(out=self.ovl[:].rearrange("p c j -> p (c j)"), in_=s[:, 0:512]),
             [s], [self.ovl])
        for g in range(2):
            self.set_combo(g, 0)
            P.barrier()
            self.setup_kv()
            for tr in range(2):
                self.set_combo(g, tr)
                P.barrier()
                self.setup_tables()
                for qt in range(NQT):
                    self.qtile(qt)


def fused_dram(P, r3=False):
    dr = {}
    D = lambda n, s: P.dram(n, s, F32, kind="ExternalInput")
    dr["xin"] = D("xin", [1024, XPAD])
    dr["memT"] = D("memT", [1024, 256])
    dr["gains"] = D("gains", [128, 128])
    dr["hg"] = D("hg", [128, 16])
    dr["convw"] = D("convw", [128, 36])
    dr["mem_w_kv"] = D("mem_w_kv", [4, 1024, 512])
    dr["w_out"] = D("w_out", [4, 1024, 1024])
    dr["w_up"] = D("w_up", [4, 1024, 4096])
    dr["w_down"] = D("w_down", [4, 4096, 1024])
    dr["b_w_in"] = D("b_w_in", [2, 1024, 1060])
    dr["a_w_in"] = D("a_w_in", [2, 1024, 2560])
    dr["w_kv"] = D("w_kv", [1024, 768])
    dr["ahg0"] = D("ahg0", [128, 8])
    dr["ahg1"] = D("ahg1", [128, 8])
    dr["pos"] = D("pos", [2, 64, 32])
    dr["cmp_w1"] = D("cmp_w1", [2, 2048, 256])
    dr["cmp_w2"] = D("cmp_w2", [2, 256, 64])
    dr["rel_bias"] = D("rel_bias", [32, 12])
    dr["OH"] = D("OH", [33, LTOT])
    dr["ident"] = D("ident", [128, 128])
    dr["EE"] = D("EE", [64, SEQ])
    dr["ovl"] = D("ovl", [128, 4, 128])
    if not r3:
        dr["topc"] = D("topc", [NQT, 128, 1024])
    if not r3:
        dr["out"] = P.dram("xT_out", [1024, SEQ], F32, kind="ExternalOutput")
    I = lambda n, s: P.dram(n, s, F32, kind="Internal")
    dr["X"] = I("Xs", [1024, XPAD])
    if not r3:
        dr["KV"] = I("KVs", [768, SEQ])
    dr["Q"] = I("Qs", [768, SEQ])
    dr["G"] = I("Gs", [36, SEQ])
    dr["Y"] = I("Ys", [768, XPAD])
    dr["Rrow"] = I("Rrow", [6, LTOT])
    return dr


def build_fused():
    nc = bass.Bass("TRN2", target_bir_lowering=False)
    P = Prog(nc)
    dr = fused_dram(P)
    P.push_scope()
    FTok(P, "L1", dr).run()
    P.pop_scope()
    for j in range(2):
        P.push_scope()
        FAttn(P, dr, j).run()
        P.pop_scope()
        P.push_scope()
        FTok(P, "L3" if j == 0 else "L5", dr).run()
        P.pop_scope()
    P.wait_all("sp", [dr["out"]])
    P.finalize()
    return nc, P


KPAD = 1536
QTILES = (3, 7, 11, 15)


def host_core_tables(k):
    tc = np.zeros((4, 128, 4, 2, 128), np.float32)
    jj = np.arange(128)[None, :]
    for j, qt in enumerate(QTILES):
        for ss in range(4):
            ta = qt * QT + 511 - (128 * ss + np.arange(128))
            back = (ta // 64)[:, None] - jj
            J = jj - 24 + 8 * k
            valid = (J >= 0) & (back >= 0)
            forced = (J == 0) | ((back >= 0) & (back < 2))
            tc[j, :, ss, 0] = (valid & ~forced).astype(np.float32)
            tc[j, :, ss, 1] = np.where(valid, np.where(forced, 1e4, 0.0), -1e30).astype(np.float32)
    kval = np.zeros((128, 16), np.float32)
    first = KPAD - 512 * k
    for c in range(12):
        a = 128 * c + np.arange(128)
        kval[:, c] = np.where(a >= first, 0.0, MASKV)
    kval[:, 12] = np.where(16 * np.arange(128) >= first, 0.0, MASKV)
    return tc.reshape(4, 128, 1024), kval


class FTok2(FTok):
    def __init__(self, P, mode, dr):
        FTok.__init__(self, P, mode, dr)
        if mode != "L1":
            self.xsrc = dr["Xown"]
            self.ytokT = dr["Yown"]
            self.qT_o, self.gT_o = dr["Qown"], dr["Gown"]

    def store_tile(self, dstT, r, msz, c0, t0, w, ost):
        if self.mode == "L1":
            off = KPAD if dstT is self.dr["KV"] else 0
            a = c0 + t0
            lo = HALO if a == 0 else 0
            g0 = NTOK * self.chunk + a + lo - HALO + off
            self.P.dma("sp", dstT, dstT[r:r + msz, g0:g0 + (w - lo)], ost, ost[0:msz, lo:w], nodep_out=True)
        else:
            self.P.dma("sp", dstT, dstT[r:r + msz, c0 + t0:c0 + t0 + w], ost, ost[0:msz, 0:w], nodep_out=True)

    def ytok_src(self, ch, c0, t0, w):
        if self.mode == "L1":
            return FTok.ytok_src(self, ch, c0, t0, w)
        return self.ytokT[ch * 128:(ch + 1) * 128, c0 + t0:c0 + t0 + w]

    def load_x(self):
        if self.mode == "L1":
            return FTok.load_x(self)
        P = self.P
        for c in range(8):
            P.dma("sp", self.x, self.x[:, c, 0:NTOK], self.xsrc, self.xsrc[c * 128:(c + 1) * 128, :], nodep_out=(c > 0))

    def store_x(self):
        if self.mode == "L1":
            return FTok.store_x(self)
        P = self.P
        dst = self.dr["out"] if self.mode == "L5" else self.dr["Xown"]
        for c in range(8):
            P.dma("sp", dst, dst[c * 128:(c + 1) * 128, :], self.x, self.x[:, c, 0:NTOK], nodep_out=True)

    def zero_kv_pad(self):
        P = self.P
        z = self.ost[0]
        P.op("pool", lambda e: e.memset(z[:], 0.0), [], [z])
        KV = self.dr["KV"]
        for r in range(6):
            for cb in range(3):
                P.dma("sp", KV, KV[r * 128:(r + 1) * 128, cb * 512:(cb + 1) * 512], z, z[:], nodep_out=True)

    def run(self):
        if self.mode == "L1":
            self.skip_pre0 = True
            self.zero_kv_pad()
            return FTok.run(self)
        self.setup_consts()
        self.chunk = 0
        self.load_x()
        if self.mode == "P0":
            for (c0, n) in [(0, 1024), (1024, 1024)]:
                self.pre_b(0, c0, n, tiles_of(n))
            return
        j = 0 if self.mode == "L3" else 1
        self.mem_kv(2 + j)
        R2 = [(0, 1024), (1024, 1024)]
        for (c0, n) in R2:
            self.post_b(j, c0, n, tiles_of(n))
        if j == 0:
            for (c0, n) in R2:
                self.pre_b(1, c0, n, tiles_of(n))
        self.store_x()


def extract_own(P, dr):
    kd = dr["kd"]

    def dyn3(src_t, r0, r1, pad, ncols):
        base = src_t[r0:r1, pad:pad + KPAD + 6144 + 512][:, bass.ds(kd, 6144 + 512)]
        return bass.AP(tensor=base.tensor, offset=base.offset, ap=[[ncols, r1 - r0], [2048, 4], [1, 512]])
    KVp, KVw = dr["KV"], dr["KVwin"]
    for rb in range(3):
        P.dma("sp", KVw, KVw[rb * 256:(rb + 1) * 256, :], KVp,
              KVp[rb * 256:(rb + 1) * 256, 0:KPAD + SEQ][:, bass.ds(kd, SEQ)], nodep_out=True)
    for rb in range(4):
        P.dma("sp", dr["Xown"], dr["Xown"][rb * 256:(rb + 1) * 256, :].rearrange("r (s c) -> r s c", c=512), dr["X"],
              dyn3(dr["X"], rb * 256, (rb + 1) * 256, HALO, XPAD), nodep_out=True)


class FAttn2(FAttn):
    def __init__(self, P, dr, jl):
        FAttn.__init__(self, P, dr, jl)
        self.KV = dr["KVwin"]
        self.Q, self.G, self.Y = dr["Qown"], dr["Gown"], dr["Yown"]
        self.kvalD = dr["kval"]

    def load_q(self, qt):
        P = self.P
        c0 = ((qt - 3) // 4) * 512
        for i2 in range(3):
            s = self.rot("st", self.st)
            for k in range(2):
                hh = self.heads[2 * i2 + k]
                r0 = (self.g * 6 + hh) * 64
                P.dma("sp", s, s[0:64, k * 512:(k + 1) * 512], self.Q, self.Q[r0:r0 + 64, c0:c0 + 512], nodep_out=(k > 0))
            for k in range(2):
                P.op("dve", lambda e, s=s, k=k, i2=i2: e.tensor_copy(
                    out=self.qraw[:, 2 * i2 + k, :], in_=rev_ap(s[0:64, k * 512:(k + 1) * 512], 512)), [s], [self.qraw])

    def load_gate(self, qt, h, br):
        P = self.P
        c0 = ((qt - 3) // 4) * 512
        gt = self.rot("gt", self.gt)
        g = self.G[:]
        row = (self.g * 6 + 3 * self.tr + h) * 3 + br
        P.dma("sp", gt, gt[:], self.G, bass.AP(tensor=g.tensor, offset=row * NTOK + c0, ap=[[0, 64], [1, 512]]))
        gr = self.rot("gtr", self.gtr)
        P.op("dve", lambda e: e.tensor_copy(out=gr[:], in_=rev_ap(gt[:], 512)), [gt], [gr])
        return gr

    def store_y(self, qt):
        P = self.P
        c0 = ((qt - 3) // 4) * 512
        bufs = [self.rdt, self.wt, self.tm]
        for h in range(3):
            b = bufs[h]
            P.op("dve", lambda e, b=b, h=h: e.tensor_copy(out=b[:], in_=rev_ap(self.y[:, h, :], 512)), [self.y], [b])
            r0 = (self.g * 6 + 3 * self.tr + h) * 64
            P.dma("sp", self.Y, self.Y[r0:r0 + 64, c0:c0 + 512], b, b[:], nodep_out=True)

    def topc_ap(self, qt):
        return self.topc[(qt - 3) // 4]

    def cmp_bias(self, cc, h, mixed):
        if cc == 0:
            return (self.kval[:, 12:13], self.kval) if mixed else (self.cfc[:, h:h + 1], self.cfc)
        return None if mixed else (self.cf[:, h:h + 1], self.cf)

    def sel_bias(self, kc, h, near):
        if kc < 12:
            return (self.kval[:, kc:kc + 1], self.kval) if near else (self.vbf[:, kc, h:h + 1], self.vbf)
        return None if near else (self.cf[:, h:h + 1], self.cf)

    def win_bias(self, c):
        return (self.kval[:, c:c + 1], self.kval) if c < 12 else None

    def setup_tables(self):
        FAttn.setup_tables(self)
        P = self.P
        for c in range(12):
            P.op("dve", lambda e, c=c: e.tensor_scalar(out=self.vbf[:, c, :], in0=self.cf[:, 0:3],
                                                       scalar1=self.kval[:, c:c + 1], scalar2=None, op0=ALU.add),
                 [self.cf, self.kval], [self.vbf])
        P.op("dve", lambda e: e.tensor_scalar(out=self.cfc[:], in0=self.cf[:], scalar1=self.kval[:, 12:13], scalar2=None,
                                              op0=ALU.add), [self.cf, self.kval], [self.cfc])

    def run(self):
        P = self.P
        st = self.st
        self.gtr = [P.sb("gtr%d" % i, [64, 512], F32) for i in range(2)]
        self.kval = P.sb("kval", [128, 16], F32)
        self.vbf = P.sb("vbf", [128, 12, 3], F32)
        self.cfc = P.sb("cfc", [128, 6], F32)
        P.dma("sp", self.kval, self.kval[:], self.kvalD, self.kvalD[:])
        P.dma("sp", self.hgs, self.hgs[:], self.hg, self.hg[:])
        P.op("pool", lambda e: e.memset(self.ones[:], 1.0), [], [self.ones])
        P.op("pool", lambda e: e.memset(self.epsT[:], 1e-6), [], [self.epsT])
        P.op("pool", lambda e: e.memset(self.tinyT[:], 1e-30), [], [self.tinyT])
        P.op("pool", lambda e: e.memset(self.vsaug[:], 1.0), [], [self.vsaug])
        P.op("pool", lambda e: e.memset(self.vwaug[:], 1.0), [], [self.vwaug])
        P.op("pool", lambda e: e.memset(self.vcaug[:], 1.0), [], [self.vcaug])
        s = self.rot("st", st)
        P.dma("sp", s, s[:, 0:128], self.identD, self.identD[:])
        P.op("pool", lambda e, s=s: e.tensor_copy(out=self.ident[:], in_=s[:, 0:128]), [s], [self.ident])
        s = self.rot("st", st)
        P.dma("sp", s, s[:, 0:512], self.ovlD, self.ovlD[:].rearrange("p c j -> p (c j)"))
        P.op("pool", lambda e, s=s: e.tensor_copy(out=self.ovl[:].rearrange("p c j -> p (c j)"), in_=s[:, 0:512]),
             [s], [self.ovl])
        kvt = [("ksaug", self.ksaug, [128, SEQ]), ("kwT", self.kwT, [64, SEQ]), ("vsaug", self.vsaug, [128, 64 * 128]),
               ("vwaug", self.vwaug, [128, 64 * 128]), ("kcT", self.kcT, [64, 512]), ("vcaug", self.vcaug, [128, 4 * 128])]
        for g in range(2):
            self.set_combo(g, 0)
            if self.jl == 0:
                self.setup_kv()
                for nm, tl, shp in kvt:
                    key = "kvc_%s_%d" % (nm, g)
                    if key not in self.dr:
                        self.dr[key] = P.dram(key, shp, BF16, kind="Internal")
                    c = self.dr[key]
                    src = tl[:] if len(tl[:].shape) == 2 else tl[:].rearrange("p a b -> p (a b)")
                    P.dma("sp", c, c[:], tl, src)
            else:
                for nm, tl, shp in kvt:
                    c = self.dr["kvc_%s_%d" % (nm, g)]
                    dst = tl[:] if len(tl[:].shape) == 2 else tl[:].rearrange("p a b -> p (a b)")
                    P.dma("sp", tl, dst, c, c[:])
            for tr in range(2):
                self.set_combo(g, tr)
                self.setup_tables()
                for qt in QTILES:
                    self.qtile(qt)


def build_fused2():
    nc = bass.Bass("TRN2", target_bir_lowering=False)
    P = Prog(nc)
    dr = fused_dram(P, r3=True)
    I = lambda n, s: P.dram(n, s, F32, kind="Internal")
    dr["KV"] = I("KVp", [768, KPAD + SEQ])
    dr["KVwin"] = I("KVwin", [768, SEQ])
    dr["Xown"] = I("Xown", [1024, NTOK])
    dr["Qown"] = I("Qown", [768, NTOK])
    dr["Gown"] = I("Gown", [36, NTOK])
    dr["Yown"] = I("Yown", [768, NTOK])
    dr["topc"] = P.dram("topc4", [4, 128, 1024], F32, kind="ExternalInput")
    dr["kval"] = P.dram("kval", [128, 16], F32, kind="ExternalInput")
    dr["out"] = P.dram("xT_own", [1024, NTOK], F32, kind="ExternalOutput")
    dr["kd"] = (nc.partition_id() % 4) * 512
    P.push_scope()
    FTok2(P, "L1", dr).run()
    P.pop_scope()
    extract_own(P, dr)
    P.push_scope()
    FTok2(P, "P0", dr).run()
    P.pop_scope()
    for j in range(2):
        P.push_scope()
        FAttn2(P, dr, j).run()
        P.pop_scope()
        P.push_scope()
        FTok2(P, "L3" if j == 0 else "L5", dr).run()
        P.pop_scope()
    P.wait_all("sp", [dr["out"]])
    P.finalize()
    return nc, P


from concourse.bass_utils import run_bass_kernel_spmd


def attn_hg(inp, jlayer):
    hg = np.zeros((128, 8), np.float32)
    hg[:, 0] = np.tile(inp["b_q_gain"][jlayer], 2)
    for i in range(3):
        hg[:, 1 + i] = np.tile(inp["k_gain"][i], 2)
    return hg


def kernel(**inputs):
    inp = {k: np.ascontiguousarray(np.asarray(v, dtype=np.float32)) for k, v in inputs.items()}
    common = dict(host_consts())
    del common["topc"]
    common.update(gains=host_gains(inp), hg=host_hg(inp), convw=host_convw(inp), mem_w_kv=inp["mem_w_kv"],
                  w_out=inp["w_out"], w_up=inp["w_up"], w_down=inp["w_down"], b_w_in=inp["b_w_in"],
                  a_w_in=inp["a_w_in"], w_kv=inp["w_kv_shared"], ahg0=attn_hg(inp, 0), ahg1=attn_hg(inp, 1),
                  pos=np.ascontiguousarray(inp["cmp_pos"].transpose(0, 2, 1)), cmp_w1=inp["cmp_w1"],
                  cmp_w2=inp["cmp_w2"], rel_bias=inp["rel_bias"])
    per_b = []
    for b in range(2):
        xin = np.zeros((1024, XPAD), np.float32)
        xin[:, HALO:] = inp["x"][b].T
        per_b.append(dict(xin=xin, memT=np.ascontiguousarray(inp["mem"][b].T)))
    per_k = []
    for k in range(4):
        tc, kval = host_core_tables(k)
        per_k.append(dict(topc4=tc, kval=kval))
    maps = []
    for c in range(8):
        m = dict(common)
        m.update(per_b[c // 4])
        m.update(per_k[c % 4])
        maps.append(m)
    nc = build_fused2()[0]
    r = run_bass_kernel_spmd(nc, maps, core_ids=list(range(8))).results
    out = np.zeros((2, SEQ, 1024), np.float32)
    for c in range(8):
        b, k = c // 4, c % 4
        o = r[c]["xT_own"]
        for j in range(4):
            t0 = 512 * (4 * j + k)
            out[b, t0:t0 + 512, :] = o[:, j * 512:(j + 1) * 512].T
    return out
```

```python
import numpy as np
import concourse.bass as bass
import concourse.mybir as mybir
from contextlib import ExitStack

F32 = mybir.dt.float32
BF16 = mybir.dt.bfloat16
AF = mybir.ActivationFunctionType
ALU = mybir.AluOpType
AX = mybir.AxisListType

ENGS = ("pe", "act", "dve", "pool", "sp")


class T:
    __slots__ = ("h", "name", "lw", "rd", "dsem", "scoped")

    def __init__(self, h, name):
        self.h = h
        self.name = name
        self.lw = None
        self.rd = []
        self.dsem = None
        self.scoped = False

    def __getitem__(self, k):
        return self.h[k]


class Prog:
    def __init__(self, nc):
        self.nc = nc
        self.es = ExitStack()
        self.ins = {e: [] for e in ENGS}
        self.nsem = 0
        self.dsems = []
        self.free_dsems = []
        self.scope_dsems = []
        self.marked = set()
        self.rw = {e: {} for e in ENGS}
        self.cur = self.es
        self.uid = 0
        self.last_tok = {e: None for e in ENGS}
        self.epoch = 0
        self.ep_of = {}

    def _prune(self, eng, deps):
        out = []
        w = self.rw[eng]
        for d in deps:
            key = (d[0], d[1]); val = d[2]
            if w.get(key, -1) >= val:
                continue
            w[key] = val
            out.append(d)
        return out

    def sb(self, name, shape, dt):
        self.uid += 1
        name = "%s_%d" % (name, self.uid)
        t = T(self.cur.enter_context(self.nc.sbuf_tensor(name, list(shape), dt)), name)
        t.scoped = self.cur is not self.es
        return t

    def ps(self, name, shape, dt=F32):
        self.uid += 1
        name = "%s_%d" % (name, self.uid)
        return T(self.cur.enter_context(self.nc.psum_tensor(name, list(shape), dt)), name)

    def push_scope(self):
        self.cur = ExitStack()

    def pop_scope(self):
        self.barrier()
        self.cur.close()
        self.cur = self.es
        self.free_dsems.extend(self.scope_dsems)
        self.scope_dsems = []

    def barrier(self):
        for e in ENGS:
            deps = [t for e2, t in self.last_tok.items() if e2 != e and t is not None]
            deps += [("d", s, c[1]) for s, c in enumerate(self.dsems) if c[1] > 0]
            deps = self._prune(e, deps)
            self.ins[e].append(dict(deps=deps, fn=None, dma=None))

    def dram(self, name, shape, dt, kind="Internal"):
        return T(self.nc.dram_tensor(name, list(shape), dt, kind=kind), name)

    def _new_dsem(self, scoped=False):
        if scoped and self.free_dsems:
            i = self.free_dsems.pop()
        else:
            h = self.es.enter_context(self.nc.semaphore("d%d" % len(self.dsems)))
            self.dsems.append([h, 0])
            i = len(self.dsems) - 1
        if scoped:
            self.scope_dsems.append(i)
        return i

    def _deps(self, eng, reads, writes):
        deps = []
        for t in reads:
            if t.lw is not None:
                deps.append(t.lw)
        for t in writes:
            if t.lw is not None:
                deps.append(t.lw)
            deps.extend(t.rd)
        out = []
        for d in deps:
            if d[0] == "e":
                if d[1] == eng:
                    continue
            out.append(d)
        if eng != "pe":
            for t in reads:
                if t.lw is not None and t.lw[0] == "e" and t.lw[1] == eng:
                    out.append(t.lw)
        return out

    def op(self, eng, fn, reads=(), writes=()):
        deps = self._prune(eng, self._deps(eng, reads, writes))
        idx = len(self.ins[eng])
        self.ins[eng].append(dict(deps=deps, fn=fn, dma=None))
        tok = ("e", eng, idx)
        self.last_tok[eng] = tok
        self.ep_of[(eng, idx)] = self.epoch
        for t in reads:
            t.rd = [r for r in t.rd if not (r[0] == "e" and r[1] == eng)]
            t.rd.append(tok)
        for t in writes:
            t.lw = tok
            t.rd = []
        return tok

    def dma(self, eng, out_t, out_ap, in_t, in_ap, nodep_out=False, **kw):
        reads = [in_t] if in_t is not None else []
        writes = [out_t] if out_t is not None else []
        deps = []
        for t in reads:
            if t.lw is not None:
                deps.append(t.lw)
        for t in writes:
            if nodep_out:
                continue
            if t.lw is not None:
                deps.append(t.lw)
            deps.extend(t.rd)
        owner = out_t if (out_t is not None and out_t.dsem is not None) else None
        if owner is None:
            owner = out_t if out_t is not None else in_t
        if owner.dsem is None:
            owner.dsem = self._new_dsem(scoped=getattr(owner, "scoped", False))
        s = owner.dsem
        self.dsems[s][1] += 16
        tok = ("d", s, self.dsems[s][1])
        deps = self._prune(eng, deps)
        self.ins[eng].append(dict(deps=deps, fn=None, dma=(out_ap, in_ap, s, kw)))
        for t in reads:
            t.rd.append(tok)
        for t in writes:
            t.lw = tok
            t.rd = []
        return tok

    def wait_all(self, eng, tiles):
        deps = []
        for t in tiles:
            if t.lw is not None:
                deps.append(t.lw)
            deps.extend(t.rd)
        deps = self._prune(eng, deps)
        self.ins[eng].append(dict(deps=deps, fn=None, dma=None))

    def finalize(self):
        nc = self.nc
        for e in ENGS:
            for rec in self.ins[e]:
                for d in rec["deps"]:
                    if d[0] == "e":
                        self.marked.add((d[1], d[2]))
        count_at = {}
        for e in ENGS:
            c = {}
            for i, rec in enumerate(self.ins[e]):
                if (e, i) in self.marked:
                    ep = self.ep_of[(e, i)]
                    c[ep] = c.get(ep, 0) + 1
                    count_at[(e, i)] = c[ep]
        esem = {(e, ep): self.es.enter_context(nc.semaphore("s_%s_%d" % (e, ep)))
                for e in ENGS for ep in range(self.epoch + 1)}
        engobj = {"pe": "tensor", "act": "scalar", "dve": "vector", "pool": "gpsimd", "sp": "sync"}
        block = self.es.enter_context(nc.Block())
        stats = {}

        def make_body(e):
            def body(eng):
                waited = {}
                nwait = 0
                for i, rec in enumerate(self.ins[e]):
                    need = {}
                    for d in rec["deps"]:
                        if d[0] == "e":
                            key = ("e", (d[1], self.ep_of[(d[1], d[2])])); val = count_at[(d[1], d[2])]
                        else:
                            key = ("d", d[1]); val = d[2]
                        if waited.get(key, 0) >= val:
                            continue
                        if need.get(key, 0) < val:
                            need[key] = val
                    for key, val in need.items():
                        h = esem[key[1]] if key[0] == "e" else self.dsems[key[1]][0]
                        eng.wait_ge(h, val)
                        waited[key] = val
                        nwait += 1
                    if rec.get("cc") is not None:
                        rec["cc"](eng).then_inc(self.dsems[rec["ccsem"]][0], 16)
                    elif rec["dma"] is not None:
                        out_ap, in_ap, s, kw = rec["dma"]
                        try:
                            eng.dma_start(out=out_ap, in_=in_ap, **kw).then_inc(self.dsems[s][0], 16)
                        except Exception:
                            print("DMA FAILED:", e, "out=", out_ap, "in=", in_ap)
                            raise
                    elif rec["fn"] is not None:
                        ins = rec["fn"](eng)
                        if (e, i) in self.marked:
                            ins.then_inc(esem[(e, self.ep_of[(e, i)])], 1)
                stats[e] = (len(self.ins[e]), nwait)
            return body

        for e in ENGS:
            if self.ins[e]:
                getattr(block, engobj[e])(make_body(e))
        self.stats = stats
        self.es.close()


NTOK = 2048
HALO = 4
NCOL = NTOK + HALO
RANGES = [(0, 1028), (1028, 1024)]
EPS = 1e-6


def tiles_of(n):
    k = (n + 511) // 512
    out = []
    o = 0
    for i in range(k):
        w = (n - o + (k - i) - 1) // (k - i)
        out.append((o, w))
        o += w
    return out


class TokStage:
    def __init__(self, P, mode):
        self.P = P
        self.mode = mode
        nc = P.nc
        D = lambda n, s: P.dram(n, s, F32, kind="ExternalInput")
        O = lambda n, s: P.dram(n, s, F32, kind="ExternalOutput")
        self.xT = D("xT", [1024, NCOL])
        self.memT = D("memT", [1024, 256])
        self.gains = D("gains", [128, 8 * 16])
        self.hg = D("hg", [128, 16])
        self.flag = D("flag", [128, 1])
        self.mem_w_kv = D("mem_w_kv", [4, 1024, 512])
        self.w_out = D("w_out", [4, 1024, 1024])
        self.w_up = D("w_up", [4, 1024, 4096])
        self.w_down = D("w_down", [4, 4096, 1024])
        self.b_w_in = D("b_w_in", [2, 1024, 1060])
        if mode == "L1":
            self.a_w_in = D("a_w_in", [2, 1024, 2560])
            self.convw = D("convw", [128, 2 * 6 * 3])
            self.w_kv = D("w_kv", [1024, 768])
            self.kvT_o = O("kvT_o", [768, NCOL])
        else:
            self.ytokT = D("ytokT", [768, NCOL])
        self.xT_o = O("xT_o", [1024, NCOL])
        if mode != "L5":
            self.qT_o = O("qT_o", [768, NCOL])
            self.gT_o = O("gT_o", [36, NCOL])

        self.alloc()

    def alloc(self):
        P = self.P
        sb = P.sb
        self.x = sb("x", [128, 8, NCOL], F32)
        self.h = sb("h", [128, 8, 1028], BF16)
        self.y = sb("y", [128, 8, 1028], BF16)
        self.act = sb("act", [128, 8, 1028], BF16)
        self.u = sb("u", [128, 2, 1028], F32)
        self.v = sb("v", [128, 2, 1030], F32)
        self.qm = sb("qm", [128, 2, 1028], F32)
        self.carry = sb("carry", [128, 6, 2], F32)
        self.rstd = sb("rstd", [128, 512], F32)
        self.sq = [sb("sq%d" % i, [128, 512], BF16) for i in range(2)]
        self.tmp = [sb("tmp%d" % i, [128, 512], F32) for i in range(3)]
        self.ost = [sb("ost%d" % i, [128, 512], F32) for i in range(2)]
        self.wb = [sb("wb%d" % i, [128, 8, 256], BF16) for i in range(4)]
        self.memh = sb("memh", [128, 8, 256], BF16)
        self.kraw = sb("kraw", [128, 2, 256], F32)
        self.memk = sb("memk", [128, 2, 256], BF16)
        self.vaug = sb("vaug", [128, 2, 4, 128], BF16)
        self.qn = sb("qn", [128, 512], BF16)
        self.eT = [sb("eT%d" % i, [128, 512], BF16) for i in range(2)]
        self.rd = sb("rd", [64, 512], F32)
        self.g = sb("g", [128, 8 * 16], F32)
        self.hgs = sb("hgs", [128, 16], F32)
        self.flg = sb("flg", [128, 1], F32)
        self.cw = sb("cw", [128, 36], F32)
        self.ones = sb("ones", [128, 128], BF16)
        self.bd = sb("bd", [128, 128], BF16)
        self.epsT = sb("epsT", [128, 1], F32)
        self.pp = [P.ps("pp%d" % i, [128, 512]) for i in range(4)]
        self.pl = [P.ps("pl%d" % i, [128, 512]) for i in range(2)]
        self.po = P.ps("po", [128, 512])
        self.pn = P.ps("pn", [128, 512])
        self.cnt = dict(wf=0, wb=0, pp=0, pl=0, sq=0, tmp=0, ost=0, eT=0)

    def rot(self, name, lst):
        i = self.cnt[name]
        self.cnt[name] += 1
        return lst[i % len(lst)]

    def stream(self, wT, w_ap, ncols):
        P = self.P
        wb = self.rot("wb", self.wb)
        P.dma("pool", wb, wb[:, :, 0:ncols], wT, w_ap.rearrange("(c p) n -> p c n", p=128))
        return wb

    def proj(self, wb, sub, msz, rhs_t, rhs_fn, tiles, consumer):
        P = self.P
        for (t0, w) in tiles:
            ps = self.rot("pp", self.pp)
            for kc in range(8):
                P.op("pe", lambda e, ps=ps, kc=kc, t0=t0, w=w: e.matmul(
                    ps[0:msz, 0:w], lhsT=wb[:, kc, sub * 128: sub * 128 + msz], rhs=rhs_fn(kc, t0, w),
                    start=(kc == 0), stop=(kc == 7)), [wb, rhs_t], [ps])
            consumer(ps, t0, w)

    def setup(self):
        P = self.P
        P.dma("sp", self.g, self.g[:], self.gains, self.gains[:])
        P.dma("sp", self.hgs, self.hgs[:], self.hg, self.hg[:])
        P.dma("sp", self.flg, self.flg[:], self.flag, self.flag[:])
        if self.mode == "L1":
            P.dma("sp", self.cw, self.cw[:], self.convw, self.convw[:])
        for c in range(8):
            P.dma("sp", self.x, self.x[:, c, :], self.xT, self.xT[c * 128:(c + 1) * 128, :], nodep_out=True)
            P.dma("sp", self.u, self.memx(c, 0, 256), self.memT, self.memT[c * 128:(c + 1) * 128, :], nodep_out=True)
        P.op("pool", lambda e: e.memset(self.ones[:], 1.0), [], [self.ones])
        P.op("pool", lambda e: e.memset(self.bd[:], 0.0), [], [self.bd])
        P.op("pool", lambda e: e.memset(self.bd[0:64, 0:64], 1.0), [], [self.bd])
        P.op("pool", lambda e: e.memset(self.bd[64:128, 64:128], 1.0), [], [self.bd])
        P.op("pool", lambda e: e.memset(self.epsT[:], EPS), [], [self.epsT])
        P.op("pool", lambda e: e.memset(self.vaug[:], 1.0), [], [self.vaug])
        P.op("pool", lambda e: e.memset(self.carry[:], 0.0), [], [self.carry])
        self.norm(self.u, self.memx, 256, [(0, 256)], 15, self.memh, 0)

    def gcol(self, which, c):
        return self.g[:, which * 8 + c: which * 8 + c + 1]

    def memx(self, c, t0, w):
        return self.u[:, c // 4, (c % 4) * 256 + t0: (c % 4) * 256 + t0 + w]

    def xs(self, c0):
        return lambda c, t0, w: self.x[:, c, c0 + t0: c0 + t0 + w]

    def norm(self, src, src_fn, n, tiles, which, dst, d0):
        P = self.P
        for (t0, w) in tiles:
            for c in range(8):
                sq = self.rot("sq", self.sq)
                P.op("act", lambda e, sq=sq, c=c, t0=t0, w=w: e.activation(
                    out=sq[:, 0:w], in_=src_fn(c, t0, w), func=AF.Square), [src], [sq])
                P.op("pe", lambda e, sq=sq, c=c, w=w: e.matmul(
                    self.pn[:, 0:w], lhsT=self.ones[:], rhs=sq[:, 0:w], start=(c == 0), stop=(c == 7)),
                    [self.ones, sq], [self.pn])
            P.op("act", lambda e, w=w: e.activation(out=self.rstd[:, 0:w], in_=self.pn[:, 0:w], func=AF.Ln,
                                                    bias=self.epsT[:], scale=1.0 / 1024.0),
                 [self.pn, self.epsT], [self.rstd])
            P.op("act", lambda e, w=w: e.activation(out=self.rstd[:, 0:w], in_=self.rstd[:, 0:w], func=AF.Exp, scale=-0.5),
                 [self.rstd], [self.rstd])
            for c in range(8):
                eng = "dve"
                P.op(eng, lambda e, c=c, t0=t0, w=w: e.scalar_tensor_tensor(
                    out=dst[:, c, d0 + t0: d0 + t0 + w], in0=src_fn(c, t0, w),
                    scalar=self.gcol(which, c), in1=self.rstd[:, 0:w], op0=ALU.mult, op1=ALU.mult),
                    [src, self.g, self.rstd], [dst])

    def headnorm(self, src_t, src_ap_fn, w, gain_ap, dst_t, dst_ap):
        P = self.P
        sq = self.rot("sq", self.sq)
        P.op("act", lambda e, sq=sq: e.activation(out=sq[:, 0:w], in_=src_ap_fn(), func=AF.Square), [src_t], [sq])
        P.op("pe", lambda e, sq=sq: e.matmul(self.pn[:, 0:w], lhsT=self.bd[:], rhs=sq[:, 0:w], start=True, stop=True),
             [self.bd, sq], [self.pn])
        P.op("act", lambda e: e.activation(out=self.rstd[:, 0:w], in_=self.pn[:, 0:w], func=AF.Ln,
                                           bias=self.epsT[:], scale=1.0 / 64.0), [self.pn, self.epsT], [self.rstd])
        P.op("act", lambda e: e.activation(out=self.rstd[:, 0:w], in_=self.rstd[:, 0:w], func=AF.Exp, scale=-0.5),
             [self.rstd], [self.rstd])
        P.op("dve", lambda e: e.scalar_tensor_tensor(out=dst_ap, in0=src_ap_fn(), scalar=gain_ap,
                                                     in1=self.rstd[:, 0:w], op0=ALU.mult, op1=ALU.mult),
             [src_t, self.hgs, self.rstd], [dst_t])

    def mem_kv(self, l):
        P = self.P
        wb = self.stream(self.mem_w_kv, self.mem_w_kv[l, :, 0:256], 256)
        for sub in range(2):
            def cons(ps, t0, w, sub=sub):
                P.op("act", lambda e: e.activation(out=self.kraw[:, sub, 0:256], in_=ps[:, 0:256], func=AF.Copy),
                     [ps], [self.kraw])
            self.proj(wb, sub, 128, self.memh, lambda kc, t0, w: self.memh[:, kc, t0:t0 + w], [(0, 256)], cons)
            self.headnorm(self.kraw, lambda sub=sub: self.kraw[:, sub, 0:256], 256,
                          self.hgs[:, 4 + l: 5 + l], self.memk, self.memk[:, sub, 0:256])
        wb = self.stream(self.mem_w_kv, self.mem_w_kv[l, :, 256:512], 256)
        for mc in range(2):
            ps = self.rot("pp", self.pp)
            for kc in range(8):
                P.op("pe", lambda e, ps=ps, kc=kc, mc=mc: e.matmul(
                    ps[:, 0:256], lhsT=self.memh[:, kc, mc * 128:(mc + 1) * 128], rhs=wb[:, kc, 0:256],
                    start=(kc == 0), stop=(kc == 7)), [self.memh, wb], [ps])
            for hh in range(4):
                P.op("act", lambda e, ps=ps, mc=mc, hh=hh: e.activation(
                    out=self.vaug[:, mc, hh, 0:64], in_=ps[:, hh * 64:(hh + 1) * 64], func=AF.Copy), [ps], [self.vaug])

    def mem_attn(self, l, n, tiles):
        P = self.P
        for (t0, w) in tiles:
            for j in range(2):
                self.headnorm(self.qm, lambda j=j, t0=t0, w=w: self.qm[:, j, t0:t0 + w], w,
                              self.hgs[:, l: l + 1], self.qn, self.qn[:, 0:w])
                for hh in range(2):
                    b0 = 64 * hh
                    hd = 2 * j + hh
                    ets = []
                    for mc in range(2):
                        pl = self.rot("pl", self.pl)
                        P.op("pe", lambda e, pl=pl, mc=mc, b0=b0, j=j, w=w: e.matmul(
                            pl[:, 0:w], lhsT=self.memk[b0:b0 + 64, j, mc * 128:(mc + 1) * 128],
                            rhs=self.qn[b0:b0 + 64, 0:w], start=True, stop=True), [self.memk, self.qn], [pl])
                        eT = self.rot("eT", self.eT)
                        P.op("act", lambda e, pl=pl, eT=eT, w=w: e.activation(
                            out=eT[:, 0:w], in_=pl[:, 0:w], func=AF.Exp, scale=0.125), [pl], [eT])
                        ets.append(eT)
                    for mc in range(2):
                        P.op("pe", lambda e, mc=mc, hd=hd, w=w, eT=ets[mc]: e.matmul(
                            self.po[:, 0:w], lhsT=self.vaug[:, mc, hd, :], rhs=eT[:, 0:w],
                            start=(mc == 0), stop=(mc == 1)), [self.vaug, ets[mc]], [self.po])
                    P.op("act", lambda e, w=w: e.activation(out=self.rd[0:64, 0:w], in_=self.po[64:128, 0:w], func=AF.Ln),
                         [self.po], [self.rd])
                    P.op("act", lambda e, w=w: e.activation(out=self.rd[0:64, 0:w], in_=self.rd[0:64, 0:w], func=AF.Exp,
                                                            scale=-1.0), [self.rd], [self.rd])
                    P.op("dve", lambda e, w=w, b0=b0, j=j, t0=t0: e.tensor_tensor(
                        out=self.y[b0:b0 + 64, 6 + j, t0:t0 + w], in0=self.po[0:64, 0:w], in1=self.rd[0:64, 0:w],
                        op=ALU.mult), [self.po, self.rd], [self.y])

    def add_into_x(self, c0):
        P = self.P

        def mk(ch):
            def cons(ps, t0, w):
                P.op("dve", lambda e: e.tensor_tensor(
                    out=self.x[:, ch, c0 + t0: c0 + t0 + w], in0=ps[:, 0:w], in1=self.x[:, ch, c0 + t0: c0 + t0 + w],
                    op=ALU.add), [ps, self.x], [self.x])
            return cons
        return mk

    def out_proj(self, l, c0, n, tiles):
        mk = self.add_into_x(c0)
        for blk in range(4):
            wb = self.stream(self.w_out, self.w_out[l, :, blk * 256:(blk + 1) * 256], 256)
            for sub in range(2):
                self.proj(wb, sub, 128, self.y, lambda kc, t0, w: self.y[:, kc, t0:t0 + w], tiles, mk(blk * 2 + sub))

    def mlp(self, l, c0, n, tiles):
        P = self.P
        self.norm(self.x, self.xs(c0), n, tiles, 4 + l, self.h, 0)
        mk = self.add_into_x(c0)
        for sg in range(4):
            for blk in range(4):
                wb = self.stream(self.w_up, self.w_up[l, :, sg * 1024 + blk * 256: sg * 1024 + (blk + 1) * 256], 256)
                for sub in range(2):
                    def cons(ps, t0, w, fc=blk * 2 + sub):
                        tmp = self.rot("tmp", self.tmp)
                        P.op("act", lambda e: e.activation(out=tmp[:, 0:w], in_=ps[:, 0:w], func=AF.Relu), [ps], [tmp])
                        P.op("dve", lambda e: e.tensor_tensor(out=self.act[:, fc, t0:t0 + w], in0=tmp[:, 0:w],
                                                              in1=tmp[:, 0:w], op=ALU.mult), [tmp], [self.act])
                    self.proj(wb, sub, 128, self.h, lambda kc, t0, w: self.h[:, kc, t0:t0 + w], tiles, cons)
            for blk in range(4):
                wb = self.stream(self.w_down, self.w_down[l, sg * 1024:(sg + 1) * 1024, blk * 256:(blk + 1) * 256], 256)
                for sub in range(2):
                    self.proj(wb, sub, 128, self.act, lambda kc, t0, w: self.act[:, kc, t0:t0 + w], tiles,
                              mk(blk * 2 + sub))

    def mix_a(self, l, ri, c0, n, tiles):
        P = self.P
        W = self.a_w_in
        hrhs = lambda kc, t0, w: self.h[:, kc, t0:t0 + w]
        for i in range(3):
            wb = self.stream(W, W[l, :, 1536 + 256 * i: 1536 + 256 * (i + 1)], 256)
            for sub in range(2):
                def cons(ps, t0, w, sub=sub):
                    P.op("act", lambda e: e.activation(out=self.u[:, sub, t0:t0 + w], in_=ps[:, 0:w], func=AF.Copy),
                         [ps], [self.u])
                self.proj(wb, sub, 128, self.h, hrhs, tiles, cons)
            wb = self.stream(W, W[l, :, 768 + 256 * i: 768 + 256 * (i + 1)], 256)
            for sub in range(2):
                def cons(ps, t0, w, sub=sub):
                    P.op("dve", lambda e: e.tensor_tensor(out=self.v[:, sub, 2 + t0: 2 + t0 + w], in0=ps[:, 0:w],
                                                          in1=self.u[:, sub, t0:t0 + w], op=ALU.mult),
                         [ps, self.u], [self.v])
                self.proj(wb, sub, 128, self.h, hrhs, tiles, cons)
            for sub in range(2):
                ch = 2 * i + sub
                self.conv_lead(l, ri, sub, ch, n)
                cwc = lambda k, ch=ch: self.cw[:, l * 18 + ch * 3 + k: l * 18 + ch * 3 + k + 1]
                P.op("dve", lambda e, sub=sub, cwc=cwc: e.tensor_scalar(
                    out=self.u[:, sub, 0:n], in0=self.v[:, sub, 0:n], scalar1=cwc(0), scalar2=None, op0=ALU.mult),
                    [self.v, self.cw], [self.u])
                for k in (1, 2):
                    P.op("dve", lambda e, sub=sub, cwc=cwc, k=k: e.scalar_tensor_tensor(
                        out=self.u[:, sub, 0:n], in0=self.v[:, sub, k:k + n], scalar=cwc(k), in1=self.u[:, sub, 0:n],
                        op0=ALU.mult, op1=ALU.add), [self.v, self.cw, self.u], [self.u])
            wb = self.stream(W, W[l, :, 256 * i: 256 * (i + 1)], 256)
            for sub in range(2):
                def cons(ps, t0, w, sub=sub, ch=2 * i + sub):
                    P.op("dve", lambda e: e.tensor_tensor(out=self.y[:, ch, t0:t0 + w], in0=ps[:, 0:w],
                                                          in1=self.u[:, sub, t0:t0 + w], op=ALU.mult),
                         [ps, self.u], [self.y])
                self.proj(wb, sub, 128, self.h, hrhs, tiles, cons)
        wb = self.stream(W, W[l, :, 2304:2560], 256)
        self.qmem_proj(wb, tiles)

    def conv_lead(self, l, ri, sub, ch, n):
        P = self.P
        if ri == 0:
            P.op("pool", lambda e: e.memset(self.v[:, sub, 0:2], 0.0), [], [self.v])
            self.halo_fix(sub)
            P.op("dve", lambda e: e.tensor_copy(out=self.carry[:, ch, :], in_=self.v[:, sub, n:n + 2]),
                 [self.v], [self.carry])
        else:
            P.op("dve", lambda e: e.tensor_copy(out=self.v[:, sub, 0:2], in_=self.carry[:, ch, :]),
                 [self.carry], [self.v])

    def halo_fix(self, sub):
        self.P.op("dve", lambda e: e.tensor_scalar(
            out=self.v[:, sub, 2:2 + HALO], in0=self.v[:, sub, 2:2 + HALO], scalar1=self.flg[:, 0:1],
            scalar2=None, op0=ALU.mult), [self.v, self.flg], [self.v])

    def store_tile(self, dstT, r, msz, c0, t0, w, ost):
        self.P.dma("sp", dstT, dstT[r:r + msz, c0 + t0: c0 + t0 + w], ost, ost[0:msz, 0:w], nodep_out=True)

    def ytok_src(self, ch, c0, t0, w):
        return self.ytokT[ch * 128:(ch + 1) * 128, c0 + t0: c0 + t0 + w]

    def qmem_proj(self, wb, tiles):
        P = self.P
        for sub in range(2):
            def cons(ps, t0, w, sub=sub):
                P.op("act", lambda e: e.activation(out=self.qm[:, sub, t0:t0 + w], in_=ps[:, 0:w], func=AF.Copy),
                     [ps], [self.qm])
            self.proj(wb, sub, 128, self.h, lambda kc, t0, w: self.h[:, kc, t0:t0 + w], tiles, cons)

    def store_proj(self, wT, w_ap, ncols, dstT, row0, c0, tiles, func=AF.Copy):
        P = self.P
        wb = self.stream(wT, w_ap, ncols)
        nsub = (ncols + 127) // 128
        for sub in range(nsub):
            msz = min(128, ncols - sub * 128)

            def cons(ps, t0, w, sub=sub, msz=msz):
                ost = self.rot("ost", self.ost)
                P.op("act", lambda e: e.activation(out=ost[0:msz, 0:w], in_=ps[0:msz, 0:w], func=func), [ps], [ost])
                r = row0 + sub * 128
                self.store_tile(dstT, r, msz, c0, t0, w, ost)
            self.proj(wb, sub, msz, self.h, lambda kc, t0, w: self.h[:, kc, t0:t0 + w], tiles, cons)

    def pre_b(self, j, c0, n, tiles):
        W = self.b_w_in
        self.norm(self.x, self.xs(c0), n, tiles, 2 + j, self.h, 0)
        for i in range(3):
            self.store_proj(W, W[j, :, 256 * i:256 * (i + 1)], 256, self.qT_o, 256 * i, c0, tiles)
        self.store_proj(W, W[j, :, 768:804], 36, self.gT_o, 0, c0, tiles, func=AF.Sigmoid)

    def post_b(self, j, c0, n, tiles):
        P = self.P
        W = self.b_w_in
        l = 2 + j
        self.norm(self.x, self.xs(c0), n, tiles, l, self.h, 0)
        wb = self.stream(W, W[j, :, 804:1060], 256)
        self.qmem_proj(wb, tiles)
        for ch in range(6):
            for (t0, w) in tiles:
                tmp = self.rot("tmp", self.tmp)
                P.dma("sp", tmp, tmp[:, 0:w], self.ytokT, self.ytok_src(ch, c0, t0, w))
                P.op("act", lambda e, tmp=tmp, ch=ch, t0=t0, w=w: e.activation(out=self.y[:, ch, t0:t0 + w],
                                                                               in_=tmp[:, 0:w], func=AF.Copy), [tmp], [self.y])
        self.mem_attn(l, n, tiles)
        self.out_proj(l, c0, n, tiles)
        self.mlp(l, c0, n, tiles)

    def store_x(self):
        P = self.P
        for c in range(8):
            P.dma("sp", self.xT_o, self.xT_o[c * 128:(c + 1) * 128, :], self.x, self.x[:, c, :], nodep_out=True)

    def build(self):
        P = self.P
        self.setup()
        if self.mode == "L1":
            for l in range(2):
                self.mem_kv(l)
                for ri, (c0, n) in enumerate(RANGES):
                    tiles = tiles_of(n)
                    self.norm(self.x, self.xs(c0), n, tiles, l, self.h, 0)
                    self.mix_a(l, ri, c0, n, tiles)
                    self.mem_attn(l, n, tiles)
                    self.out_proj(l, c0, n, tiles)
                    self.mlp(l, c0, n, tiles)
            for ri, (c0, n) in enumerate(RANGES):
                tiles = tiles_of(n)
                self.norm(self.x, self.xs(c0), n, tiles, 8, self.h, 0)
                for i in range(3):
                    self.store_proj(self.w_kv, self.w_kv[:, 256 * i:256 * (i + 1)], 256, self.kvT_o, 256 * i, c0, tiles)
                self.pre_b(0, c0, n, tiles)
            self.store_x()
            outs = [self.kvT_o, self.qT_o, self.gT_o, self.xT_o]
        elif self.mode == "L3":
            self.mem_kv(2)
            for ri, (c0, n) in enumerate(RANGES):
                tiles = tiles_of(n)
                self.post_b(0, c0, n, tiles)
            for ri, (c0, n) in enumerate(RANGES):
                tiles = tiles_of(n)
                self.pre_b(1, c0, n, tiles)
            self.store_x()
            outs = [self.qT_o, self.gT_o, self.xT_o]
        else:
            self.mem_kv(3)
            for ri, (c0, n) in enumerate(RANGES):
                tiles = tiles_of(n)
                self.post_b(1, c0, n, tiles)
            self.store_x()
            outs = [self.xT_o]
        P.wait_all("sp", outs)
        P.finalize()


def build_tok(mode):
    nc = bass.Bass("TRN2", target_bir_lowering=False)
    P = Prog(nc)
    st = TokStage(P, mode)
    st.build()
    return nc, P


def host_gains(inp):
    g = np.zeros((16, 1024), np.float32)
    g[0:4] = inp["mix_norm"]
    g[4:8] = inp["mlp_norm"]
    g[8] = inp["kv_norm"]
    g[15] = inp["mem_norm"]
    return np.ascontiguousarray(g.reshape(16, 8, 128).transpose(2, 0, 1).reshape(128, 128))


def host_hg(inp):
    hg = np.zeros((128, 16), np.float32)
    for l in range(4):
        hg[:, l] = np.tile(inp["mem_q_gain"][l], 2)
        hg[:, 4 + l] = np.tile(inp["mem_k_gain"][l], 2)
    return hg


def host_convw(inp):
    w = inp["a_conv_w"].reshape(2, 3, 6, 128)
    return np.ascontiguousarray(w.transpose(3, 0, 2, 1).reshape(128, 36))

import math

SEQ = 8192
QT = 512
NQT = SEQ // QT
DC, LC = 3040, 5104
DW, LW = 1023, 1536
DS, LS = 1407, 1920
LTOT = LC + LW + LS
LU, LTW, LTS = 3072, 1408, 1792
MASKV = -30000.0
PIPE = 3


def rel_bucket_np(d):
    n = np.maximum(d, 0)
    nf = np.maximum(n, 1).astype(np.float32)
    large = 16 + (np.log(nf / np.float32(16)) / np.float32(math.log(1024 / 16)) * np.float32(16)).astype(np.int32)
    large = np.minimum(large, 31)
    return np.where(n < 16, n, large)


def host_consts():
    c = {}
    oh = np.zeros((33, LTOT), np.float32)
    x = np.arange(LC); d = DC - x
    b = rel_bucket_np(d)
    oh[b[d >= 0], x[d >= 0]] = 1.0
    oh[32, x[d < 0]] = 1.0
    x = np.arange(LW); d = DW - x; ok = (d >= 0) & (d < 512)
    b = rel_bucket_np(d)
    oh[b[ok], LC + x[ok]] = 1.0
    oh[32, LC + x[~ok]] = 1.0
    x = np.arange(LS); d = DS - x; ok = d >= 0
    b = rel_bucket_np(d)
    oh[b[ok], LC + LW + x[ok]] = 1.0
    oh[32, LC + LW + x[~ok]] = 1.0
    c["OH"] = oh
    c["ident"] = np.eye(128, dtype=np.float32)
    k = np.arange(SEQ)
    r = 2 * ((k // 128) % 32) + ((k % 128) >= 64)
    ee = np.zeros((64, SEQ), np.float32)
    ee[r, k] = 30000.0
    c["EE"] = ee
    i = np.arange(512)[:, None] * 16
    s = np.arange(128)[None, :] * 64
    ov = np.clip(np.minimum(i + 32, s + 64) - np.maximum(i, s), 0, None).astype(np.float32) / 32.0
    ov[511] = 0.0
    c["ovl"] = np.ascontiguousarray(ov.reshape(4, 128, 128).transpose(1, 0, 2))
    tc = np.zeros((NQT, 128, 4, 2, 128), np.float32)
    j = np.arange(128)[None, :]
    for qt in range(NQT):
        for ss in range(4):
            t = qt * QT + 511 - (128 * ss + np.arange(128))
            back = (t // 64)[:, None] - j
            forced = (j == 0) | ((back >= 0) & (back < 2))
            A = ((back >= 0) & ~forced).astype(np.float32)
            B = np.where(back >= 0, np.where(forced, 1e4, 0.0), -1e30).astype(np.float32)
            tc[qt, :, ss, 0] = A
            tc[qt, :, ss, 1] = B
    c["topc"] = tc.reshape(NQT, 128, 1024)
    return c


class Attn:
    def __init__(self, P):
        self.P = P
        D = lambda n, s: P.dram(n, s, F32, kind="ExternalInput")
        self.qT = D("qT", [384, SEQ])
        self.gB = D("gB", [9, SEQ])
        self.kvT = D("kvT", [384, SEQ])
        self.vtok = D("vtok", [SEQ, 128])
        self.blk = D("blk", [2, 64, 32, 512])
        self.hg = D("hg", [128, 8])
        self.pos = D("pos", [2, 64, 32])
        self.w1 = D("w1", [2, 2048, 256])
        self.w2 = D("w2", [2, 256, 64])
        self.rb6 = D("rb6", [32, 6])
        self.OH = D("OH", [33, LTOT])
        self.identD = D("ident", [128, 128])
        self.EE = D("EE", [64, SEQ])
        self.ovlD = D("ovl", [128, 4, 128])
        self.topc = D("topc", [NQT, 128, 1024])
        self.yT = P.dram("yT", [192, SEQ], F32, kind="ExternalOutput")
        self.Rrow = P.dram("Rrow", [6, LTOT], F32, kind="Internal")
        self.alloc()

    def alloc(self):
        P = self.P
        sb = P.sb
        self.ksaug = sb("ksaug", [128, SEQ], BF16)
        self.kwT = sb("kwT", [64, SEQ], BF16)
        self.vsaug = sb("vsaug", [128, 64, 128], BF16)
        self.vwaug = sb("vwaug", [128, 64, 128], BF16)
        self.kcT = sb("kcT", [64, 512], BF16)
        self.vcaug = sb("vcaug", [128, 4, 128], BF16)
        self.Ubig = sb("Ubig", [128, 6 * LU], BF16)
        self.TTw = [sb("TTw%d" % h, [128, LTW], BF16) for h in range(3)]
        self.TTs = [sb("TTs%d" % h, [128, LTS], BF16) for h in range(3)]
        self.ident = sb("identb", [128, 128], BF16)
        self.ones = sb("ones", [128, 128], BF16)
        self.ovl = sb("ovlb", [128, 4, 128], BF16)
        self.hgs = sb("hgs", [128, 8], F32)
        self.cf = sb("cf", [128, 6], F32)
        self.tab = sb("tab", [33, 6], F32)
        self.epsT = sb("epsT", [128, 1], F32)
        self.tinyT = sb("tinyT", [128, 1], F32)
        self.st = [sb("st%d" % i, [128, 1024], F32) for i in range(2)]
        self.sq = [sb("sq%d" % i, [128, 512], BF16) for i in range(2)]
        self.rq = sb("rq", [128, 512], F32)
        self.qraw = sb("qraw", [64, 6, 512], F32)
        self.qn = sb("qn", [64, 6, 512], BF16)
        self.qaug = sb("qaug", [128, 3, 2, 512], BF16)
        self.pT = [sb("pT%d" % i, [128, 512], BF16) for i in range(9)]
        self.rdq = sb("rdq", [128, 4], F32)
        self.rdb = sb("rdb", [128, 512], F32)
        self.impacc = sb("impacc", [128, 512], F32)
        self.tcs = sb("tcs", [128, 1024], F32)
        self.sc = sb("sc", [128, 128], F32)
        self.sc2 = sb("sc2", [128, 128], F32)
        self.mx = sb("mx", [128, 16], F32)
        self.mneg = sb("mneg", [128, 128], BF16)
        self.gt = [sb("gt%d" % i, [64, 512], F32) for i in range(2)]
        self.rdt = sb("rdt", [64, 512], F32)
        self.wt = sb("wt", [64, 512], F32)
        self.tm = sb("tm", [64, 512], F32)
        self.y = sb("y", [64, 3, 512], F32)
        self.w1b = sb("w1b", [64, 4, 256], BF16)
        self.blkb = sb("blkb", [64, 4, 512], BF16)
        self.posT = sb("posT", [64, 32], F32)
        self.w2b = sb("w2b", [128, 2, 64], BF16)
        self.gx = self.rdb
        self.gy = self.impacc
        self.gT = [sb("gT%d" % i, [128, 512], BF16) for i in range(2)]
        self.A = [P.ps("A%d" % i, [128, 512]) for i in range(3)]
        self.S = [P.ps("S%d" % i, [128, 512]) for i in range(2)]
        self.X0 = P.ps("X0", [128, 512])
        self.X1 = P.ps("X1", [128, 512])
        self.XT = P.ps("XT", [128, 512], BF16)
        self.S4 = self.S + [self.X0, self.X1]
        self.cnt = {}

    def uslice(self, h, mo):
        return self.Ubig[:, h * LU + mo: h * LU + mo + 512]

    def rot(self, name, lst):
        i = self.cnt.get(name, 0)
        self.cnt[name] = i + 1
        return lst[i % len(lst)]

    def norm64(self, src_t, src_ap, w, gain_ap, dst_t, dst_ap):
        P = self.P
        sq = self.rot("sq", self.sq)
        P.op("act", lambda e: e.activation(out=sq[0:64, 0:w], in_=src_ap, func=AF.Square), [src_t], [sq])
        P.op("pe", lambda e: e.matmul(self.X0[0:64, 0:w], lhsT=self.ones[0:64, 0:64], rhs=sq[0:64, 0:w],
                                      start=True, stop=True), [self.ones, sq], [self.X0])
        P.op("act", lambda e: e.activation(out=self.rq[0:64, 0:w], in_=self.X0[0:64, 0:w], func=AF.Ln,
                                           bias=self.epsT[0:64, :], scale=1.0 / 64.0), [self.X0, self.epsT], [self.rq])
        P.op("act", lambda e: e.activation(out=self.rq[0:64, 0:w], in_=self.rq[0:64, 0:w], func=AF.Exp, scale=-0.5),
             [self.rq], [self.rq])
        P.op("dve", lambda e: e.scalar_tensor_tensor(out=dst_ap, in0=src_ap, scalar=gain_ap, in1=self.rq[0:64, 0:w],
                                                     op0=ALU.mult, op1=ALU.mult), [src_t, self.hgs, self.rq], [dst_t])

    def setup(self):
        P = self.P
        st = self.st
        P.dma("sp", self.hgs, self.hgs[:], self.hg, self.hg[:])
        P.op("pool", lambda e: e.memset(self.ones[:], 1.0), [], [self.ones])
        P.op("pool", lambda e: e.memset(self.epsT[:], 1e-6), [], [self.epsT])
        P.op("pool", lambda e: e.memset(self.tinyT[:], 1e-30), [], [self.tinyT])
        P.op("pool", lambda e: e.memset(self.vsaug[:], 1.0), [], [self.vsaug])
        P.op("pool", lambda e: e.memset(self.vwaug[:], 1.0), [], [self.vwaug])
        P.op("pool", lambda e: e.memset(self.vcaug[:], 1.0), [], [self.vcaug])
        s = self.rot("st", st)
        P.dma("sp", s, s[:, 0:128], self.identD, self.identD[:])
        P.op("pool", lambda e, s=s: e.tensor_copy(out=self.ident[:], in_=s[:, 0:128]), [s], [self.ident])
        s = self.rot("st", st)
        P.dma("sp", s, s[:, 0:512], self.ovlD, self.ovlD[:].rearrange("p c j -> p (c j)"))
        P.op("pool", lambda e, s=s: e.tensor_copy(out=self.ovl[:].rearrange("p c j -> p (c j)"), in_=s[:, 0:512]),
             [s], [self.ovl])
        self.setup_tables()
        self.setup_kv()

    def load_rb(self):
        P = self.P
        rb = self.rb6[:]
        P.dma("sp", self.cf, self.cf[:], self.rb6, bass.AP(tensor=rb.tensor, offset=31 * 6, ap=[[0, 128], [1, 6]]))
        P.dma("sp", self.tab, self.tab[0:32, :], self.rb6, self.rb6[:])

    def setup_tables(self):
        P = self.P
        st = self.st
        P.op("pool", lambda e: e.memset(self.tab[:], MASKV), [], [self.tab])
        self.load_rb()
        P.op("dve", lambda e: e.tensor_scalar(out=self.tab[0:32, :], in0=self.tab[0:32, :], scalar1=8.0, scalar2=None,
                                              op0=ALU.mult), [self.tab], [self.tab])
        o = 0
        while o < LTOT:
            w = min(512, LTOT - o)
            s = self.rot("st", st)
            P.dma("sp", s, s[0:33, 0:w], self.OH, self.OH[:, o:o + w])
            P.op("pe", lambda e, s=s, w=w: e.matmul(self.X1[0:6, 0:w], lhsT=self.tab[:, :], rhs=s[0:33, 0:w],
                                                    start=True, stop=True), [self.tab, s], [self.X1])
            s2 = self.rot("st", st)
            P.op("act", lambda e, s2=s2, w=w: e.activation(out=s2[0:6, 0:w], in_=self.X1[0:6, 0:w], func=AF.Copy),
                 [self.X1], [s2])
            P.dma("sp", self.Rrow, self.Rrow[:, o:o + w], s2, s2[0:6, 0:w])
            o += w
        rr = self.Rrow[:]

        def expand(dst, d0, h, base, pstep, L, nodep=False):
            P.dma("pool", dst, dst[:, d0:d0 + L], self.Rrow,
                  bass.AP(tensor=rr.tensor, offset=h * LTOT + base, ap=[[pstep, 128], [1, L]]), nodep_out=nodep)
        for h in range(6):
            expand(self.Ubig, h * LU, h, 0, 16, LU, nodep=(h > 0))
        for h in range(3):
            expand(self.TTw[h], 0, h, LC, 1, LTW)
            expand(self.TTs[h], 0, h, LC + LW, 1, LTS)

    def load_kv_piece(self, s, slot, c0):
        self.P.dma("sp", s, s[0:64, :], self.kvT, self.kvT[slot * 64:(slot + 1) * 64, c0:c0 + 1024])

    def load_v_piece(self, piece):
        P = self.P
        c0 = piece * 1024
        s = self.rot("st", self.st)
        P.dma("sp", s, s[:, :].rearrange("p (c f) -> p c f", f=128), self.vtok,
              self.vtok[c0:c0 + 1024, :].rearrange("(c p) f -> p c f", p=128))
        sv = s[:, :].rearrange("p (c f) -> p c f", f=128)
        P.op("pool", lambda e, sv=sv, piece=piece: e.tensor_copy(
            out=self.vsaug[:, piece * 8:(piece + 1) * 8, 0:64], in_=sv[:, :, 0:64]), [s], [self.vsaug])
        P.op("pool", lambda e, sv=sv, piece=piece: e.tensor_copy(
            out=self.vwaug[:, piece * 8:(piece + 1) * 8, 0:64], in_=sv[:, :, 64:128]), [s], [self.vwaug])

    def setup_kv(self):
        P = self.P
        st = self.st
        for piece in range(8):
            c0 = piece * 1024
            for slot, gcol, dst in ((2, 2, self.ksaug), (4, 3, self.kwT)):
                s = self.rot("st", st)
                self.load_kv_piece(s, slot, c0)
                for hf in range(2):
                    a = hf * 512
                    self.norm64(s, s[0:64, a:a + 512], 512, self.hgs[0:64, gcol:gcol + 1], dst,
                                dst[0:64, c0 + a:c0 + a + 512])
            P.dma("pool", self.ksaug, self.ksaug[64:128, c0:c0 + 1024], self.EE, self.EE[:, c0:c0 + 1024])
            self.load_v_piece(piece)
        self.compress()

    def compress_rhs(self, kvi, tg):
        P = self.P
        for th in range(2):
            s = self.rot("st", self.st)
            t0 = tg * 4 + th * 2
            P.dma("sp", s, s[0:64, :].rearrange("p (t n) -> p t n", n=512), self.blk,
                  self.blk[kvi, :, t0:t0 + 2, :])
            for tt in range(2):
                t = t0 + tt
                P.op("dve", lambda e, s=s, tt=tt, th=th, t=t: e.tensor_scalar(
                    out=self.blkb[:, th * 2 + tt, :], in0=s[0:64, tt * 512:(tt + 1) * 512],
                    scalar1=self.posT[:, t:t + 1], scalar2=None, op0=ALU.add), [s, self.posT], [self.blkb])

    def compress_begin(self, kvi):
        pass

    def compress(self):
        P = self.P
        st = self.st
        H = [self.A[0], self.A[1]]
        for kvi in range(2):
            self.compress_begin(kvi)
            P.dma("sp", self.posT, self.posT[:], self.pos, self.pos[kvi])
            s = self.rot("st", st)
            P.dma("sp", s, s[:, 0:128].rearrange("p (c d) -> p c d", d=64), self.w2,
                  self.w2[kvi].rearrange("(c p) d -> p c d", p=128))
            P.op("pool", lambda e, s=s: e.tensor_copy(out=self.w2b[:].rearrange("p c d -> p (c d)"), in_=s[:, 0:128]),
                 [s], [self.w2b])
            for tg in range(8):
                s = self.rot("st", st)
                P.dma("sp", s, s[0:64, :].rearrange("p (t j) -> p t j", j=256), self.w1,
                      self.w1[kvi, tg * 256:(tg + 1) * 256, :].rearrange("(t d) j -> d t j", d=64))
                P.op("pool", lambda e, s=s: e.tensor_copy(out=self.w1b[:].rearrange("p t j -> p (t j)"),
                                                          in_=s[0:64, :]), [s], [self.w1b])
                self.compress_rhs(kvi, tg)
                for tl in range(4):
                    t = tg * 4 + tl
                    for jc in range(2):
                        P.op("pe", lambda e, tl=tl, jc=jc, t=t: e.matmul(
                            H[jc][:, :], lhsT=self.w1b[:, tl, jc * 128:(jc + 1) * 128], rhs=self.blkb[:, tl, :],
                            start=(t == 0), stop=(t == 31)), [self.w1b, self.blkb], [H[jc]])
            for jc in range(2):
                P.op("act", lambda e, jc=jc: e.activation(out=self.gx[:], in_=H[jc][:, :], func=AF.Copy),
                     [H[jc]], [self.gx])
                P.op("pool", lambda e: e.tensor_tensor(out=self.gy[:], in0=self.gx[:], in1=self.gx[:], op=ALU.mult),
                     [self.gx], [self.gy])
                P.op("dve", lambda e: e.tensor_scalar(out=self.gy[:], in0=self.gy[:], scalar1=0.044715, scalar2=1.0,
                                                      op0=ALU.mult, op1=ALU.add), [self.gy], [self.gy])
                P.op("pool", lambda e: e.tensor_tensor(out=self.gy[:], in0=self.gy[:], in1=self.gx[:], op=ALU.mult),
                     [self.gy, self.gx], [self.gy])
                P.op("act", lambda e: e.activation(out=self.gy[:], in_=self.gy[:], func=AF.Sigmoid,
                                                   scale=1.5957691216057308), [self.gy], [self.gy])
                P.op("dve", lambda e, jc=jc: e.tensor_tensor(out=self.gT[jc][:], in0=self.gx[:], in1=self.gy[:],
                                                             op=ALU.mult), [self.gx, self.gy], [self.gT[jc]])
            if kvi == 0:
                for jc in range(2):
                    P.op("pe", lambda e, jc=jc: e.matmul(self.X1[0:64, :], lhsT=self.w2b[:, jc, :], rhs=self.gT[jc][:],
                                                         start=(jc == 0), stop=(jc == 1)),
                         [self.w2b, self.gT[jc]], [self.X1])
                s = self.rot("st", st)
                P.op("act", lambda e, s=s: e.activation(out=s[0:64, 0:512], in_=self.X1[0:64, :], func=AF.Copy),
                     [self.X1], [s])
                self.norm64(s, s[0:64, 0:512], 512, self.hgs[0:64, 1:2], self.kcT, self.kcT[:, :])
            else:
                for cc in range(4):
                    for jc in range(2):
                        P.op("pe", lambda e, jc=jc, cc=cc: e.matmul(
                            self.X1[:, cc * 64:(cc + 1) * 64], lhsT=self.gT[jc][:, cc * 128:(cc + 1) * 128],
                            rhs=self.w2b[:, jc, :], start=(jc == 0), stop=(jc == 1)),
                            [self.w2b, self.gT[jc]], [self.X1])
                    P.op("act", lambda e, cc=cc: e.activation(out=self.vcaug[:, cc, 0:64],
                                                              in_=self.X1[:, cc * 64:(cc + 1) * 64], func=AF.Copy),
                         [self.X1], [self.vcaug])

    def combine(self, qt, h, br, first):
        P = self.P
        A = self.A[h]
        s0 = qt * QT
        gt = self.load_gate(qt, h, br)
        P.op("act", lambda e: e.activation(out=self.rdt[:], in_=A[64:128, :], func=AF.Ln, bias=self.tinyT[0:64, :]),
             [A, self.tinyT], [self.rdt])
        P.op("act", lambda e: e.activation(out=self.rdt[:], in_=self.rdt[:], func=AF.Exp, scale=-1.0),
             [self.rdt], [self.rdt])
        P.op("pool", lambda e: e.tensor_tensor(out=self.wt[:], in0=self.rdt[:], in1=gt[:], op=ALU.mult),
             [self.rdt, gt], [self.wt])
        if first:
            P.op("dve", lambda e: e.tensor_tensor(out=self.y[:, h, :], in0=A[0:64, :], in1=self.wt[:], op=ALU.mult),
                 [A, self.wt], [self.y])
        else:
            P.op("dve", lambda e: e.tensor_tensor(out=self.tm[:], in0=A[0:64, :], in1=self.wt[:], op=ALU.mult),
                 [A, self.wt], [self.tm])
            P.op("pool", lambda e: e.tensor_tensor(out=self.y[:, h, :], in0=self.y[:, h, :], in1=self.tm[:],
                                                   op=ALU.add), [self.y, self.tm], [self.y])

    def exp(self, S, pT, bias):
        P = self.P
        if bias is None:
            P.op("act", lambda e: e.activation(out=pT[:], in_=S[:, :], func=AF.Exp, scale=0.125), [S], [pT])
        else:
            bap, bt = bias
            P.op("act", lambda e: e.activation(out=pT[:], in_=S[:, :], func=AF.Exp, scale=0.125, bias=bap),
                 [S, bt], [pT])

    def cmp_bias(self, cc, h, mixed):
        return None if mixed else (self.cf[:, h:h + 1], self.cf)

    def sel_bias(self, kc, h, near):
        return None if near else (self.cf[:, h:h + 1], self.cf)

    def win_bias(self, c):
        return None

    def topc_ap(self, qt):
        return self.topc[qt]

    def load_q(self, qt):
        s0 = qt * QT
        self.P.dma("sp", self.qraw, self.qraw[:], self.qT, self.qT[:, s0:s0 + QT].rearrange("(h d) t -> d h t", d=64))

    def load_gate(self, qt, h, br):
        s0 = qt * QT
        gt = self.rot("gt", self.gt)
        g = self.gB[:]
        self.P.dma("sp", gt, gt[:], self.gB,
                   bass.AP(tensor=g.tensor, offset=(h * 3 + br) * SEQ + s0, ap=[[0, 64], [1, 512]]))
        return gt

    def store_y(self, qt):
        s0 = qt * QT
        self.P.dma("sp", self.yT, self.yT[:, s0:s0 + QT].rearrange("(h d) t -> d h t", d=64), self.y, self.y[:],
                   nodep_out=True)

    def qtile(self, qt):
        P = self.P
        s0 = qt * QT
        self.load_q(qt)
        P.dma("sp", self.tcs, self.tcs[:], self.topc, self.topc_ap(qt))
        for h in range(6):
            self.norm64(self.qraw, self.qraw[:, h, :], 512, self.hgs[0:64, 0:1], self.qn, self.qn[:, h, :])
        for h in range(3):
            for half in range(2):
                P.op("pool", lambda e, h=h, half=half: e.tensor_copy(out=self.qaug[0:64, h, half, :],
                                                                     in_=self.qn[:, h, :]), [self.qn], [self.qaug])
        for h in range(6):
            chunks = [cc for cc in range(4) if qt - 4 * cc >= 0]
            pts = []
            for cc in chunks:
                k = qt - 4 * cc
                mixed = k <= 5
                S = self.rot("S", self.S)
                P.op("pe", lambda e, S=S, cc=cc, h=h, mixed=mixed: e.matmul(
                    S[:, :], lhsT=self.kcT[:, cc * 128:(cc + 1) * 128], rhs=self.qn[:, h, :], start=True,
                    stop=(not mixed)), [self.kcT, self.qn], [S])
                if mixed:
                    mo = 512 * (5 - k)
                    P.op("pe", lambda e, S=S, h=h, mo=mo: e.matmul(S[:, :], lhsT=self.ident[:],
                                                                   rhs=self.uslice(h, mo), start=False, stop=True),
                         [self.ident, self.Ubig], [S])
                pT = self.rot("pT", self.pT)
                self.exp(S, pT, self.cmp_bias(cc, h, mixed))
                pts.append(pT)
            n = len(chunks)
            if h < 3:
                for i, (cc, pT) in enumerate(zip(chunks, pts)):
                    P.op("pe", lambda e, pT=pT, i=i, cc=cc, h=h: e.matmul(
                        self.A[h][:, :], lhsT=self.vcaug[:, cc, :], rhs=pT[:], start=(i == 0), stop=(i == n - 1)),
                        [self.vcaug, pT], [self.A[h]])
            for ss in range(4):
                for i, (cc, pT) in enumerate(zip(chunks, pts)):
                    P.op("pe", lambda e, pT=pT, i=i, cc=cc, ss=ss: e.matmul(
                        self.X1[:, ss * 128:(ss + 1) * 128], lhsT=pT[:, ss * 128:(ss + 1) * 128], rhs=self.ovl[:, cc, :],
                        start=(i == 0), stop=(i == n - 1)), [pT, self.ovl], [self.X1])
            for ss in range(4):
                for i, pT in enumerate(pts):
                    P.op("pe", lambda e, pT=pT, i=i, ss=ss: e.matmul(
                        self.X0[:, ss:ss + 1], lhsT=pT[:, ss * 128:(ss + 1) * 128], rhs=self.ones[:, 0:1],
                        start=(i == 0), stop=(i == n - 1)), [pT, self.ones], [self.X0])
            P.op("dve", lambda e: e.tensor_scalar(out=self.rdq[:], in0=self.X0[:, 0:4], scalar1=1e-30, scalar2=None,
                                                  op0=ALU.max), [self.X0], [self.rdq])
            P.op("dve", lambda e: e.reciprocal(out=self.rdq[:], in_=self.rdq[:]), [self.rdq], [self.rdq])
            for ss in range(4):
                if h == 0:
                    P.op("dve", lambda e, ss=ss: e.tensor_scalar(
                        out=self.impacc[:, ss * 128:(ss + 1) * 128], in0=self.X1[:, ss * 128:(ss + 1) * 128],
                        scalar1=self.rdq[:, ss:ss + 1], scalar2=None, op0=ALU.mult), [self.X1, self.rdq], [self.impacc])
                else:
                    P.op("dve", lambda e, ss=ss: e.scalar_tensor_tensor(
                        out=self.impacc[:, ss * 128:(ss + 1) * 128], in0=self.X1[:, ss * 128:(ss + 1) * 128],
                        scalar=self.rdq[:, ss:ss + 1], in1=self.impacc[:, ss * 128:(ss + 1) * 128],
                        op0=ALU.mult, op1=ALU.add), [self.X1, self.rdq, self.impacc], [self.impacc])
            if h < 3:
                self.combine(qt, h, 0, True)
        for ss in range(4):
            tA = self.tcs[:, ss * 256: ss * 256 + 128]
            tB = self.tcs[:, ss * 256 + 128: ss * 256 + 256]
            P.op("dve", lambda e, ss=ss, tA=tA: e.tensor_tensor(out=self.sc[:], in0=self.impacc[:, ss * 128:(ss + 1) * 128],
                                                                in1=tA, op=ALU.mult), [self.impacc, self.tcs], [self.sc])
            P.op("dve", lambda e, tB=tB: e.tensor_tensor(out=self.sc[:], in0=self.sc[:], in1=tB, op=ALU.add),
                 [self.sc, self.tcs], [self.sc])
            P.op("dve", lambda e: e.max(out=self.mx[:, 0:8], in_=self.sc[:]), [self.sc], [self.mx])
            P.op("dve", lambda e: e.match_replace(out=self.sc2[:], in_to_replace=self.mx[:, 0:8], in_values=self.sc[:],
                                                  imm_value=-1e30), [self.sc, self.mx], [self.sc2])
            P.op("dve", lambda e: e.max(out=self.mx[:, 8:16], in_=self.sc2[:]), [self.sc2], [self.mx])
            P.op("dve", lambda e: e.tensor_scalar(out=self.mneg[:], in0=self.sc[:], scalar1=self.mx[:, 15:16], scalar2=1.0,
                                                  op0=ALU.is_ge, op1=ALU.subtract), [self.sc, self.mx], [self.mneg])
            P.op("pe", lambda e, ss=ss: e.transpose(self.XT[:, ss * 128:(ss + 1) * 128], self.mneg[:], self.ident[:]),
                 [self.mneg, self.ident], [self.XT])
        for h in range(3):
            for half in range(2):
                eng = "act" if (h + half) % 2 == 0 else "dve"
                if eng == "act":
                    P.op("act", lambda e, h=h, half=half: e.activation(out=self.qaug[64:128, h, half, :],
                                                                       in_=self.XT[half * 64:(half + 1) * 64, :],
                                                                       func=AF.Copy), [self.XT], [self.qaug])
                else:
                    P.op("dve", lambda e, h=h, half=half: e.tensor_copy(out=self.qaug[64:128, h, half, :],
                                                                        in_=self.XT[half * 64:(half + 1) * 64, :]),
                         [self.XT], [self.qaug])
        nkc = 4 * qt + 4

        def sel_qk(kc, h):
            k0 = 128 * kc
            ep = s0 + 511 - k0
            near = ep <= DS
            half = kc // 32
            S = self.rot("S4", self.S4)
            P.op("pe", lambda e: e.matmul(S[:, :], lhsT=self.ksaug[:, k0:k0 + 128], rhs=self.qaug[:, h, half, :],
                                          start=True, stop=(not near)), [self.ksaug, self.qaug], [S])
            if near:
                mo = DS - ep
                P.op("pe", lambda e: e.matmul(S[:, :], lhsT=self.ident[:], rhs=self.TTs[h][:, mo:mo + 512],
                                              start=False, stop=True), [self.ident, self.TTs[h]], [S])
            return (kc, h, near, S)

        def sel_rest(kc, h, near, S):
            pT = self.rot("pT", self.pT)
            self.exp(S, pT, self.sel_bias(kc, h, near))
            P.op("pe", lambda e: e.matmul(self.A[h][:, :], lhsT=self.vsaug[:, kc, :], rhs=pT[:],
                                          start=(kc == 0), stop=(kc == nkc - 1)), [self.vsaug, pT], [self.A[h]])

        pend = []
        for kc in range(nkc):
            for h in range(3):
                pend.append(sel_qk(kc, h))
                if len(pend) > PIPE:
                    sel_rest(*pend.pop(0))
        while pend:
            sel_rest(*pend.pop(0))
        for h in range(3):
            self.combine(qt, h, 1, False)
        rs = [r for r in range(8) if s0 - 512 + 128 * r >= 0]

        def win_qk(i, r, h):
            k0 = s0 - 512 + 128 * r
            mo = 128 * r
            S = self.rot("S4", self.S4)
            P.op("pe", lambda e: e.matmul(S[:, :], lhsT=self.kwT[:, k0:k0 + 128], rhs=self.qn[:, h, :],
                                          start=True, stop=False), [self.kwT, self.qn], [S])
            P.op("pe", lambda e: e.matmul(S[:, :], lhsT=self.ident[:], rhs=self.TTw[h][:, mo:mo + 512],
                                          start=False, stop=True), [self.ident, self.TTw[h]], [S])
            return (i, k0, h, S)

        def win_rest(i, k0, h, S):
            pT = self.rot("pT", self.pT)
            self.exp(S, pT, self.win_bias(k0 // 128))
            P.op("pe", lambda e: e.matmul(self.A[h][:, :], lhsT=self.vwaug[:, k0 // 128, :], rhs=pT[:],
                                          start=(i == 0), stop=(i == len(rs) - 1)), [self.vwaug, pT], [self.A[h]])

        pend = []
        for i, r in enumerate(rs):
            for h in range(3):
                pend.append(win_qk(i, r, h))
                if len(pend) > PIPE:
                    win_rest(*pend.pop(0))
        while pend:
            win_rest(*pend.pop(0))
        for h in range(3):
            self.combine(qt, h, 2, False)
        self.store_y(qt)

    def build(self, nqt=NQT):
        self.setup()
        for qt in range(nqt):
            self.qtile(qt)
        self.P.wait_all("sp", [self.yT])
        self.P.finalize()


def build_attn(nqt=NQT):
    nc = bass.Bass("TRN2", target_bir_lowering=False)
    P = Prog(nc)
    a = Attn(P)
    a.build(nqt)
    return nc, P


XPAD = SEQ + HALO


def rev_ap(ap, n):
    dims = [list(d) for d in ap.ap]
    dims[-1] = [-1, n]
    return bass.AP(tensor=ap.tensor, offset=ap.offset + (n - 1), ap=dims)


class FTok(TokStage):
    def __init__(self, P, mode, dr):
        self.P = P
        self.mode = mode
        self.dr = dr
        for k in ("memT", "gains", "hg", "mem_w_kv", "w_out", "w_up", "w_down", "b_w_in", "a_w_in", "convw", "w_kv"):
            setattr(self, k, dr[k])
        self.xsrc = dr["xin"] if mode == "L1" else dr["X"]
        self.kvT_o, self.qT_o, self.gT_o = dr["KV"], dr["Q"], dr["G"]
        self.ytokT = dr["Y"]
        self.chunk = 0
        self.alloc()

    def halo_fix(self, sub):
        if self.chunk == 0:
            self.P.op("pool", lambda e: e.memset(self.v[:, sub, 2:2 + HALO], 0.0), [], [self.v])

    def store_tile(self, dstT, r, msz, c0, t0, w, ost):
        a = c0 + t0
        lo = HALO if a == 0 else 0
        g0 = NTOK * self.chunk + a + lo - HALO
        self.P.dma("sp", dstT, dstT[r:r + msz, g0:g0 + (w - lo)], ost, ost[0:msz, lo:w], nodep_out=True)

    def ytok_src(self, ch, c0, t0, w):
        g0 = NTOK * self.chunk + c0 + t0
        return self.ytokT[ch * 128:(ch + 1) * 128, g0:g0 + w]

    def setup_consts(self):
        P = self.P
        P.dma("sp", self.g, self.g[:], self.gains, self.gains[:])
        P.dma("sp", self.hgs, self.hgs[:], self.hg, self.hg[:])
        if self.mode == "L1":
            P.dma("sp", self.cw, self.cw[:], self.convw, self.convw[:])
        for c in range(8):
            P.dma("sp", self.u, self.memx(c, 0, 256), self.memT, self.memT[c * 128:(c + 1) * 128, :], nodep_out=True)
        P.op("pool", lambda e: e.memset(self.ones[:], 1.0), [], [self.ones])
        P.op("pool", lambda e: e.memset(self.bd[:], 0.0), [], [self.bd])
        P.op("pool", lambda e: e.memset(self.bd[0:64, 0:64], 1.0), [], [self.bd])
        P.op("pool", lambda e: e.memset(self.bd[64:128, 64:128], 1.0), [], [self.bd])
        P.op("pool", lambda e: e.memset(self.epsT[:], EPS), [], [self.epsT])
        P.op("pool", lambda e: e.memset(self.vaug[:], 1.0), [], [self.vaug])
        self.norm(self.u, self.memx, 256, [(0, 256)], 15, self.memh, 0)

    def load_x(self):
        P = self.P
        g0 = NTOK * self.chunk
        for c in range(8):
            P.dma("sp", self.x, self.x[:, c, :], self.xsrc, self.xsrc[c * 128:(c + 1) * 128, g0:g0 + NCOL],
                  nodep_out=(c > 0))

    def store_x(self):
        P = self.P
        dst = self.dr["out"] if self.mode == "L5" else self.dr["X"]
        off = 0 if self.mode == "L5" else HALO
        g0 = NTOK * self.chunk + off
        for c in range(8):
            P.dma("sp", dst, dst[c * 128:(c + 1) * 128, g0:g0 + NTOK], self.x, self.x[:, c, HALO:NCOL], nodep_out=True)

    def run(self):
        self.setup_consts()
        for chunk in range(4):
            self.chunk = chunk
            self.load_x()
            if self.mode == "L1":
                for l in range(2):
                    self.mem_kv(l)
                    for ri, (c0, n) in enumerate(RANGES):
                        tiles = tiles_of(n)
                        self.norm(self.x, self.xs(c0), n, tiles, l, self.h, 0)
                        self.mix_a(l, ri, c0, n, tiles)
                        self.mem_attn(l, n, tiles)
                        self.out_proj(l, c0, n, tiles)
                        self.mlp(l, c0, n, tiles)
                for ri, (c0, n) in enumerate(RANGES):
                    tiles = tiles_of(n)
                    self.norm(self.x, self.xs(c0), n, tiles, 8, self.h, 0)
                    for i in range(3):
                        self.store_proj(self.w_kv, self.w_kv[:, 256 * i:256 * (i + 1)], 256, self.kvT_o, 256 * i, c0, tiles)
                    if not getattr(self, "skip_pre0", False):
                        self.pre_b(0, c0, n, tiles)
            else:
                j = 0 if self.mode == "L3" else 1
                self.mem_kv(2 + j)
                for ri, (c0, n) in enumerate(RANGES):
                    self.post_b(j, c0, n, tiles_of(n))
                if j == 0:
                    for ri, (c0, n) in enumerate(RANGES):
                        self.pre_b(1, c0, n, tiles_of(n))
            self.store_x()


class FAttn(Attn):
    def __init__(self, P, dr, jl):
        self.P = P
        self.dr = dr
        self.jl = jl
        self.Q, self.G, self.KV, self.Y = dr["Q"], dr["G"], dr["KV"], dr["Y"]
        self.hg = dr["ahg%d" % jl]
        self.pos, self.w1, self.w2 = dr["pos"], dr["cmp_w1"], dr["cmp_w2"]
        self.rel = dr["rel_bias"]
        self.OH, self.identD, self.EE, self.ovlD, self.topc = dr["OH"], dr["ident"], dr["EE"], dr["ovl"], dr["topc"]
        self.Rrow = dr["Rrow"]
        self.alloc()

    def alloc(self):
        Attn.alloc(self)
        self.ub_view = self.Ubig[0:64, 0:16 * 513].rearrange("p (r n) -> p r n", n=513)

    def set_combo(self, g, tr):
        self.g, self.tr = g, tr
        own = [3 * tr + i for i in range(3)]
        oth = [3 * (1 - tr) + i for i in range(3)]
        self.heads = own + oth

    def load_rb(self):
        P = self.P
        rel = self.rel[:]
        for i, hh in enumerate(self.heads):
            col = self.g * 6 + hh
            P.dma("sp", self.cf, self.cf[:, i:i + 1], self.rel,
                  bass.AP(tensor=rel.tensor, offset=31 * 12 + col, ap=[[0, 128], [1, 1]]))
            P.dma("sp", self.tab, self.tab[0:32, i:i + 1], self.rel, self.rel[:, col:col + 1],
                  allow_slow_non_contiguous=True)

    def load_kv_piece(self, s, slot, c0):
        r0 = slot * 128 + self.g * 64
        self.P.dma("sp", s, s[0:64, :], self.KV, self.KV[r0:r0 + 64, c0:c0 + 1024])

    def load_v_piece(self, piece):
        P = self.P
        c0 = piece * 1024
        for slot, dst in ((3, self.vsaug), (5, self.vwaug)):
            r0 = slot * 128 + self.g * 64
            for hf in range(2):
                sq = self.rot("sq", self.sq)
                a = c0 + hf * 512
                P.dma("pool", sq, sq[0:64, :], self.KV, self.KV[r0:r0 + 64, a:a + 512])
                for c in range(4):
                    P.op("pe", lambda e, sq=sq, c=c: e.transpose(self.XT[:, c * 64:(c + 1) * 64],
                                                                 sq[0:64, c * 128:(c + 1) * 128], self.ident[0:64, 0:64]),
                         [sq, self.ident], [self.XT])
                ch0 = piece * 8 + hf * 4
                P.op("act", lambda e, dst=dst, ch0=ch0: e.activation(
                    out=dst[:, ch0:ch0 + 4, 0:64], in_=self.XT[:, 0:256].rearrange("p (c d) -> p c d", d=64),
                    func=AF.Copy), [self.XT], [dst])

    def compress_begin(self, kvi):
        P = self.P
        P.op("pool", lambda e: e.memset(self.ub_view[:, :, 512:513], 0.0), [], [self.Ubig])
        for piece in range(8):
            s = self.rot("st", self.st)
            self.load_kv_piece(s, kvi, piece * 1024)
            P.op("pool", lambda e, s=s, piece=piece: e.tensor_copy(
                out=self.ub_view[:, :, piece * 64:(piece + 1) * 64],
                in_=s[0:64, :].rearrange("p (n r) -> p r n", r=16)), [s], [self.Ubig])

    def compress_rhs(self, kvi, tg):
        P = self.P
        for tl in range(4):
            t = tg * 4 + tl
            r, off = t % 16, t // 16
            P.op("dve", lambda e, tl=tl, t=t, r=r, off=off: e.tensor_scalar(
                out=self.blkb[:, tl, :], in0=self.ub_view[:, r, off:off + 512], scalar1=self.posT[:, t:t + 1],
                scalar2=None, op0=ALU.add), [self.Ubig, self.posT], [self.blkb])

    def load_q(self, qt):
        P = self.P
        s0 = qt * QT
        for i2 in range(3):
            s = self.rot("st", self.st)
            for k in range(2):
                hh = self.heads[2 * i2 + k]
                r0 = (self.g * 6 + hh) * 64
                P.dma("sp", s, s[0:64, k * 512:(k + 1) * 512], self.Q, self.Q[r0:r0 + 64, s0:s0 + QT], nodep_out=(k > 0))
            for k in range(2):
                P.op("dve", lambda e, s=s, k=k, i2=i2: e.tensor_copy(
                    out=self.qraw[:, 2 * i2 + k, :], in_=rev_ap(s[0:64, k * 512:(k + 1) * 512], 512)), [s], [self.qraw])

    def load_gate(self, qt, h, br):
        P = self.P
        s0 = qt * QT
        gt = self.rot("gt", self.gt)
        g = self.G[:]
        row = (self.g * 6 + 3 * self.tr + h) * 3 + br
        P.dma("sp", gt, gt[:], self.G, bass.AP(tensor=g.tensor, offset=row * SEQ + s0, ap=[[0, 64], [1, 512]]))
        gr = self.rot("gtr", self.gtr)
        P.op("dve", lambda e: e.tensor_copy(out=gr[:], in_=rev_ap(gt[:], 512)), [gt], [gr])
        return gr

    def store_y(self, qt):
        P = self.P
        s0 = qt * QT
        bufs = [self.rdt, self.wt, self.tm]
        for h in range(3):
            b = bufs[h]
            P.op("dve", lambda e, b=b, h=h: e.tensor_copy(out=b[:], in_=rev_ap(self.y[:, h, :], 512)), [self.y], [b])
            r0 = (self.g * 6 + 3 * self.tr + h) * 64
            P.dma("sp", self.Y, self.Y[r0:r0 + 64, HALO + s0:HALO + s0 + QT], b, b[:], nodep_out=True)

    def run(self):
        P = self.P
        st = self.st
        self.gtr = [P.sb("gtr%d" % i, [64, 512], F32) for i in range(2)]
        P.dma("sp", self.hgs, self.hgs[:], self.hg, self.hg[:])
        P.op("pool", lambda e: e.memset(self.ones[:], 1.0), [], [self.ones])
        P.op("pool", lambda e: e.memset(self.epsT[:], 1e-6), [], [self.epsT])
        P.op("pool", lambda e: e.memset(self.tinyT[:], 1e-30), [], [self.tinyT])
        P.op("pool", lambda e: e.memset(self.vsaug[:], 1.0), [], [self.vsaug])
        P.op("pool", lambda e: e.memset(self.vwaug[:], 1.0), [], [self.vwaug])
        P.op("pool", lambda e: e.memset(self.vcaug[:], 1.0), [], [self.vcaug])
        s = self.rot("st", st)
        P.dma("sp", s, s[:, 0:128], self.identD, self.identD[:])
        P.op("pool", lambda e, s=s: e.tensor_copy(out=self.ident[:], in_=s[:, 0:128]), [s], [self.ident])
        s = self.rot("st", st)
        P.dma("sp", s, s[:, 0:512], self.ovlD, self.ovlD[:].rearrange("p c j -> p (c j)"))
        P.op("pool", lambda e, s=s: e.tensor_copy(out=self.ovl[:].rearrange("p c j -> p (c j)"), in_=s[:, 0:512]),
             [s], [self.ovl])
        for g in range(2):
            self.set_combo(g, 0)
            P.barrier()
            self.setup_kv()
            for tr in range(2):
                self.set_combo(g, tr)
                P.barrier()
                self.setup_tables()
                for qt in range(NQT):
                    self.qtile(qt)


def fused_dram(P, r3=False):
    dr = {}
    D = lambda n, s: P.dram(n, s, F32, kind="ExternalInput")
    dr["xin"] = D("xin", [1024, XPAD])
    dr["memT"] = D("memT", [1024, 256])
    dr["gains"] = D("gains", [128, 128])
    dr["hg"] = D("hg", [128, 16])
    dr["convw"] = D("convw", [128, 36])
    dr["mem_w_kv"] = D("mem_w_kv", [4, 1024, 512])
    dr["w_out"] = D("w_out", [4, 1024, 1024])
    dr["w_up"] = D("w_up", [4, 1024, 4096])
    dr["w_down"] = D("w_down", [4, 4096, 1024])
    dr["b_w_in"] = D("b_w_in", [2, 1024, 1060])
    dr["a_w_in"] = D("a_w_in", [2, 1024, 2560])
    dr["w_kv"] = D("w_kv", [1024, 768])
    dr["ahg0"] = D("ahg0", [128, 8])
    dr["ahg1"] = D("ahg1", [128, 8])
    dr["pos"] = D("pos", [2, 64, 32])
    dr["cmp_w1"] = D("cmp_w1", [2, 2048, 256])
    dr["cmp_w2"] = D("cmp_w2", [2, 256, 64])
    dr["rel_bias"] = D("rel_bias", [32, 12])
    dr["OH"] = D("OH", [33, LTOT])
    dr["ident"] = D("ident", [128, 128])
    dr["EE"] = D("EE", [64, SEQ])
    dr["ovl"] = D("ovl", [128, 4, 128])
    if not r3:
        dr["topc"] = D("topc", [NQT, 128, 1024])
    if not r3:
        dr["out"] = P.dram("xT_out", [1024, SEQ], F32, kind="ExternalOutput")
    I = lambda n, s: P.dram(n, s, F32, kind="Internal")
    dr["X"] = I("Xs", [1024, XPAD])
    if not r3:
        dr["KV"] = I("KVs", [768, SEQ])
    dr["Q"] = I("Qs", [768, SEQ])
    dr["G"] = I("Gs", [36, SEQ])
    dr["Y"] = I("Ys", [768, XPAD])
    dr["Rrow"] = I("Rrow", [6, LTOT])
    return dr


def build_fused():
    nc = bass.Bass("TRN2", target_bir_lowering=False)
    P = Prog(nc)
    dr = fused_dram(P)
    P.push_scope()
    FTok(P, "L1", dr).run()
    P.pop_scope()
    for j in range(2):
        P.push_scope()
        FAttn(P, dr, j).run()
        P.pop_scope()
        P.push_scope()
        FTok(P, "L3" if j == 0 else "L5", dr).run()
        P.pop_scope()
    P.wait_all("sp", [dr["out"]])
    P.finalize()
    return nc, P


KPAD = 1536
QTILES = (3, 7, 11, 15)


def host_core_tables(k):
    tc = np.zeros((4, 128, 4, 2, 128), np.float32)
    jj = np.arange(128)[None, :]
    for j, qt in enumerate(QTILES):
        for ss in range(4):
            ta = qt * QT + 511 - (128 * ss + np.arange(128))
            back = (ta // 64)[:, None] - jj
            J = jj - 24 + 8 * k
            valid = (J >= 0) & (back >= 0)
            forced = (J == 0) | ((back >= 0) & (back < 2))
            tc[j, :, ss, 0] = (valid & ~forced).astype(np.float32)
            tc[j, :, ss, 1] = np.where(valid, np.where(forced, 1e4, 0.0), -1e30).astype(np.float32)
    kval = np.zeros((128, 16), np.float32)
    first = KPAD - 512 * k
    for c in range(12):
        a = 128 * c + np.arange(128)
        kval[:, c] = np.where(a >= first, 0.0, MASKV)
    kval[:, 12] = np.where(16 * np.arange(128) >= first, 0.0, MASKV)
    return tc.reshape(4, 128, 1024), kval


class FTok2(FTok):
    def __init__(self, P, mode, dr):
        FTok.__init__(self, P, mode, dr)
        if mode != "L1":
            self.xsrc = dr["Xown"]
            self.ytokT = dr["Yown"]
            self.qT_o, self.gT_o = dr["Qown"], dr["Gown"]

    def conv_lead(self, l, ri, sub, ch, n):
        P = self.P
        cr = self.carry2[l]
        if self.chunk == 0 and ri == 0:
            P.op("pool", lambda e: e.memset(self.v[:, sub, 0:2], 0.0), [], [self.v])
        else:
            P.op("dve", lambda e: e.tensor_copy(out=self.v[:, sub, 0:2], in_=cr[:, ch, :]), [cr], [self.v])
        P.op("dve", lambda e: e.tensor_copy(out=cr[:, ch, :], in_=self.v[:, sub, n:n + 2]), [self.v], [cr])

    def store_tile(self, dstT, r, msz, c0, t0, w, ost):
        if self.mode == "L1":
            off = KPAD if dstT is self.dr["KV"] else 0
            g0 = NTOK * self.chunk + c0 + t0 + off
            self.P.dma("sp", dstT, dstT[r:r + msz, g0:g0 + w], ost, ost[0:msz, 0:w], nodep_out=True)
        else:
            self.P.dma("sp", dstT, dstT[r:r + msz, c0 + t0:c0 + t0 + w], ost, ost[0:msz, 0:w], nodep_out=True)

    def ytok_src(self, ch, c0, t0, w):
        if self.mode == "L1":
            return FTok.ytok_src(self, ch, c0, t0, w)
        return self.ytokT[ch * 128:(ch + 1) * 128, c0 + t0:c0 + t0 + w]

    def load_x(self):
        P = self.P
        if self.mode == "L1":
            g0 = NTOK * self.chunk + HALO
            for c in range(8):
                P.dma("sp", self.x, self.x[:, c, 0:NTOK], self.xsrc, self.xsrc[c * 128:(c + 1) * 128, g0:g0 + NTOK],
                      nodep_out=(c > 0))
            return
        for c in range(8):
            P.dma("sp", self.x, self.x[:, c, 0:NTOK], self.xsrc, self.xsrc[c * 128:(c + 1) * 128, :], nodep_out=(c > 0))

    def store_x(self):
        P = self.P
        if self.mode == "L1":
            X = self.dr["X"]
            g0 = NTOK * self.chunk + HALO
            for c in range(8):
                P.dma("sp", X, X[c * 128:(c + 1) * 128, g0:g0 + NTOK], self.x, self.x[:, c, 0:NTOK], nodep_out=True)
            return
        dst = self.dr["out"] if self.mode == "L5" else self.dr["Xown"]
        for c in range(8):
            P.dma("sp", dst, dst[c * 128:(c + 1) * 128, :], self.x, self.x[:, c, 0:NTOK], nodep_out=True)

    def zero_kv_pad(self):
        P = self.P
        z = self.ost[0]
        P.op("pool", lambda e: e.memset(z[:], 0.0), [], [z])
        KV = self.dr["KV"]
        for r in range(6):
            for cb in range(3):
                P.dma("sp", KV, KV[r * 128:(r + 1) * 128, cb * 512:(cb + 1) * 512], z, z[:], nodep_out=True)

    def run(self):
        if self.mode == "L1":
            self.carry2 = [self.P.sb("carry2_%d" % l, [128, 6, 2], F32) for l in range(2)]
            self.zero_kv_pad()
            self.setup_consts()
            R2 = [(0, 1024), (1024, 1024)]
            for chunk in range(4):
                self.chunk = chunk
                self.load_x()
                for l in range(2):
                    self.mem_kv(l)
                    for ri, (c0, n) in enumerate(R2):
                        tiles = tiles_of(n)
                        self.norm(self.x, self.xs(c0), n, tiles, l, self.h, 0)
                        self.mix_a(l, ri, c0, n, tiles)
                        self.mem_attn(l, n, tiles)
                        self.out_proj(l, c0, n, tiles)
                        self.mlp(l, c0, n, tiles)
                for ri, (c0, n) in enumerate(R2):
                    tiles = tiles_of(n)
                    self.norm(self.x, self.xs(c0), n, tiles, 8, self.h, 0)
                    for i in range(3):
                        self.store_proj(self.w_kv, self.w_kv[:, 256 * i:256 * (i + 1)], 256, self.kvT_o, 256 * i, c0, tiles)
                self.store_x()
            return
        self.setup_consts()
        self.chunk = 0
        self.load_x()
        if self.mode == "P0":
            for (c0, n) in [(0, 1024), (1024, 1024)]:
                self.pre_b(0, c0, n, tiles_of(n))
            return
        j = 0 if self.mode == "L3" else 1
        self.mem_kv(2 + j)
        R2 = [(0, 1024), (1024, 1024)]
        for (c0, n) in R2:
            self.post_b(j, c0, n, tiles_of(n))
        if j == 0:
            for (c0, n) in R2:
                self.pre_b(1, c0, n, tiles_of(n))
        self.store_x()


def extract_own(P, dr):
    kd = dr["kd"]

    def dyn3(src_t, r0, r1, pad, ncols):
        base = src_t[r0:r1, pad:pad + KPAD + 6144 + 512][:, bass.ds(kd, 6144 + 512)]
        return bass.AP(tensor=base.tensor, offset=base.offset, ap=[[ncols, r1 - r0], [2048, 4], [1, 512]])
    KVp, KVw = dr["KV"], dr["KVwin"]
    for rb in range(3):
        P.dma("sp", KVw, KVw[rb * 256:(rb + 1) * 256, :], KVp,
              KVp[rb * 256:(rb + 1) * 256, 0:KPAD + SEQ][:, bass.ds(kd, SEQ)], nodep_out=True)
    for rb in range(4):
        P.dma("sp", dr["Xown"], dr["Xown"][rb * 256:(rb + 1) * 256, :].rearrange("r (s c) -> r s c", c=512), dr["X"],
              dyn3(dr["X"], rb * 256, (rb + 1) * 256, HALO, XPAD), nodep_out=True)


class FAttn2(FAttn):
    def __init__(self, P, dr, jl):
        FAttn.__init__(self, P, dr, jl)
        self.KV = dr["KVwin"]
        self.Q, self.G, self.Y = dr["Qown"], dr["Gown"], dr["Yown"]
        self.kvalD = dr["kval"]

    def load_q(self, qt):
        P = self.P
        c0 = ((qt - 3) // 4) * 512
        for i2 in range(3):
            s = self.rot("st", self.st)
            for k in range(2):
                hh = self.heads[2 * i2 + k]
                r0 = (self.g * 6 + hh) * 64
                P.dma("sp", s, s[0:64, k * 512:(k + 1) * 512], self.Q, self.Q[r0:r0 + 64, c0:c0 + 512], nodep_out=(k > 0))
            for k in range(2):
                P.op("dve", lambda e, s=s, k=k, i2=i2: e.tensor_copy(
                    out=self.qraw[:, 2 * i2 + k, :], in_=rev_ap(s[0:64, k * 512:(k + 1) * 512], 512)), [s], [self.qraw])

    def load_gate(self, qt, h, br):
        P = self.P
        c0 = ((qt - 3) // 4) * 512
        gt = self.rot("gt", self.gt)
        g = self.G[:]
        row = (self.g * 6 + 3 * self.tr + h) * 3 + br
        P.dma("sp", gt, gt[:], self.G, bass.AP(tensor=g.tensor, offset=row * NTOK + c0, ap=[[0, 64], [1, 512]]))
        gr = self.rot("gtr", self.gtr)
        P.op("dve", lambda e: e.tensor_copy(out=gr[:], in_=rev_ap(gt[:], 512)), [gt], [gr])
        return gr

    def store_y(self, qt):
        P = self.P
        c0 = ((qt - 3) // 4) * 512
        bufs = [self.rdt, self.wt, self.tm]
        for h in range(3):
            b = bufs[h]
            P.op("dve", lambda e, b=b, h=h: e.tensor_copy(out=b[:], in_=rev_ap(self.y[:, h, :], 512)), [self.y], [b])
            r0 = (self.g * 6 + 3 * self.tr + h) * 64
            P.dma("sp", self.Y, self.Y[r0:r0 + 64, c0:c0 + 512], b, b[:], nodep_out=True)

    def topc_ap(self, qt):
        return self.topc[(qt - 3) // 4]

    def cmp_bias(self, cc, h, mixed):
        if cc == 0:
            return (self.kval[:, 12:13], self.kval) if mixed else (self.cfc[:, h:h + 1], self.cfc)
        return None if mixed else (self.cf[:, h:h + 1], self.cf)

    def sel_bias(self, kc, h, near):
        if kc < 12:
            return (self.kval[:, kc:kc + 1], self.kval) if near else (self.vbf[:, kc, h:h + 1], self.vbf)
        return None if near else (self.cf[:, h:h + 1], self.cf)

    def win_bias(self, c):
        return (self.kval[:, c:c + 1], self.kval) if c < 12 else None

    def setup_tables(self):
        FAttn.setup_tables(self)
        P = self.P
        for c in range(12):
            P.op("dve", lambda e, c=c: e.tensor_scalar(out=self.vbf[:, c, :], in0=self.cf[:, 0:3],
                                                       scalar1=self.kval[:, c:c + 1], scalar2=None, op0=ALU.add),
                 [self.cf, self.kval], [self.vbf])
        P.op("dve", lambda e: e.tensor_scalar(out=self.cfc[:], in0=self.cf[:], scalar1=self.kval[:, 12:13], scalar2=None,
                                              op0=ALU.add), [self.cf, self.kval], [self.cfc])

    def run(self):
        P = self.P
        st = self.st
        self.gtr = [P.sb("gtr%d" % i, [64, 512], F32) for i in range(2)]
        self.kval = P.sb("kval", [128, 16], F32)
        self.vbf = P.sb("vbf", [128, 12, 3], F32)
        self.cfc = P.sb("cfc", [128, 6], F32)
        P.dma("sp", self.kval, self.kval[:], self.kvalD, self.kvalD[:])
        P.dma("sp", self.hgs, self.hgs[:], self.hg, self.hg[:])
        P.op("pool", lambda e: e.memset(self.ones[:], 1.0), [], [self.ones])
        P.op("pool", lambda e: e.memset(self.epsT[:], 1e-6), [], [self.epsT])
        P.op("pool", lambda e: e.memset(self.tinyT[:], 1e-30), [], [self.tinyT])
        P.op("pool", lambda e: e.memset(self.vsaug[:], 1.0), [], [self.vsaug])
        P.op("pool", lambda e: e.memset(self.vwaug[:], 1.0), [], [self.vwaug])
        P.op("pool", lambda e: e.memset(self.vcaug[:], 1.0), [], [self.vcaug])
        s = self.rot("st", st)
        P.dma("sp", s, s[:, 0:128], self.identD, self.identD[:])
        P.op("pool", lambda e, s=s: e.tensor_copy(out=self.ident[:], in_=s[:, 0:128]), [s], [self.ident])
        s = self.rot("st", st)
        P.dma("sp", s, s[:, 0:512], self.ovlD, self.ovlD[:].rearrange("p c j -> p (c j)"))
        P.op("pool", lambda e, s=s: e.tensor_copy(out=self.ovl[:].rearrange("p c j -> p (c j)"), in_=s[:, 0:512]),
             [s], [self.ovl])
        kvt = [("ksaug", self.ksaug, [128, SEQ]), ("kwT", self.kwT, [64, SEQ]), ("vsaug", self.vsaug, [128, 64 * 128]),
               ("vwaug", self.vwaug, [128, 64 * 128]), ("kcT", self.kcT, [64, 512]), ("vcaug", self.vcaug, [128, 4 * 128])]
        for g in range(2):
            self.set_combo(g, 0)
            if self.jl == 0:
                self.setup_kv()
                for nm, tl, shp in kvt:
                    key = "kvc_%s_%d" % (nm, g)
                    if key not in self.dr:
                        self.dr[key] = P.dram(key, shp, BF16, kind="Internal")
                    c = self.dr[key]
                    src = tl[:] if len(tl[:].shape) == 2 else tl[:].rearrange("p a b -> p (a b)")
                    P.dma("sp", c, c[:], tl, src)
            else:
                for nm, tl, shp in kvt:
                    c = self.dr["kvc_%s_%d" % (nm, g)]
                    dst = tl[:] if len(tl[:].shape) == 2 else tl[:].rearrange("p a b -> p (a b)")
                    P.dma("sp", tl, dst, c, c[:])
            for tr in range(2):
                self.set_combo(g, tr)
                self.setup_tables()
                for qt in QTILES:
                    self.qtile(qt)


def build_fused2():
    nc = bass.Bass("TRN2", target_bir_lowering=False)
    P = Prog(nc)
    dr = fused_dram(P, r3=True)
    I = lambda n, s: P.dram(n, s, F32, kind="Internal")
    dr["KV"] = I("KVp", [768, KPAD + SEQ])
    dr["KVwin"] = I("KVwin", [768, SEQ])
    dr["Xown"] = I("Xown", [1024, NTOK])
    dr["Qown"] = I("Qown", [768, NTOK])
    dr["Gown"] = I("Gown", [36, NTOK])
    dr["Yown"] = I("Yown", [768, NTOK])
    dr["topc"] = P.dram("topc4", [4, 128, 1024], F32, kind="ExternalInput")
    dr["kval"] = P.dram("kval", [128, 16], F32, kind="ExternalInput")
    dr["out"] = P.dram("xT_own", [1024, NTOK], F32, kind="ExternalOutput")
    dr["kd"] = (nc.partition_id() % 4) * 512
    P.push_scope()
    FTok2(P, "L1", dr).run()
    P.pop_scope()
    extract_own(P, dr)
    P.push_scope()
    FTok2(P, "P0", dr).run()
    P.pop_scope()
    for j in range(2):
        P.push_scope()
        FAttn2(P, dr, j).run()
        P.pop_scope()
        P.push_scope()
        FTok2(P, "L3" if j == 0 else "L5", dr).run()
        P.pop_scope()
    P.wait_all("sp", [dr["out"]])
    P.finalize()
    return nc, P


from concourse.bass_utils import run_bass_kernel_spmd


def attn_hg(inp, jlayer):
    hg = np.zeros((128, 8), np.float32)
    hg[:, 0] = np.tile(inp["b_q_gain"][jlayer], 2)
    for i in range(3):
        hg[:, 1 + i] = np.tile(inp["k_gain"][i], 2)
    return hg


def kernel(**inputs):
    inp = {k: np.ascontiguousarray(np.asarray(v, dtype=np.float32)) for k, v in inputs.items()}
    common = dict(host_consts())
    del common["topc"]
    common.update(gains=host_gains(inp), hg=host_hg(inp), convw=host_convw(inp), mem_w_kv=inp["mem_w_kv"],
                  w_out=inp["w_out"], w_up=inp["w_up"], w_down=inp["w_down"], b_w_in=inp["b_w_in"],
                  a_w_in=inp["a_w_in"], w_kv=inp["w_kv_shared"], ahg0=attn_hg(inp, 0), ahg1=attn_hg(inp, 1),
                  pos=np.ascontiguousarray(inp["cmp_pos"].transpose(0, 2, 1)), cmp_w1=inp["cmp_w1"],
                  cmp_w2=inp["cmp_w2"], rel_bias=inp["rel_bias"])
    per_b = []
    for b in range(2):
        xin = np.zeros((1024, XPAD), np.float32)
        xin[:, HALO:] = inp["x"][b].T
        per_b.append(dict(xin=xin, memT=np.ascontiguousarray(inp["mem"][b].T)))
    per_k = []
    for k in range(4):
        tc, kval = host_core_tables(k)
        per_k.append(dict(topc4=tc, kval=kval))
    maps = []
    for c in range(8):
        m = dict(common)
        m.update(per_b[c // 4])
        m.update(per_k[c % 4])
        maps.append(m)
    nc = build_fused2()[0]
    r = run_bass_kernel_spmd(nc, maps, core_ids=list(range(8))).results
    out = np.zeros((2, SEQ, 1024), np.float32)
    for c in range(8):
        b, k = c // 4, c % 4
        o = r[c]["xT_own"]
        for j in range(4):
            t0 = 512 * (4 * j + k)
            out[b, t0:t0 + 512, :] = o[:, j * 512:(j + 1) * 512].T
    return out
```

```python
import numpy as np
import concourse.bass as bass
import concourse.mybir as mybir
from contextlib import ExitStack

F32 = mybir.dt.float32
BF16 = mybir.dt.bfloat16
AF = mybir.ActivationFunctionType
ALU = mybir.AluOpType
AX = mybir.AxisListType

ENGS = ("pe", "act", "dve", "pool", "sp")


class T:
    __slots__ = ("h", "name", "lw", "rd", "dsem", "scoped")

    def __init__(self, h, name):
        self.h = h
        self.name = name
        self.lw = None
        self.rd = []
        self.dsem = None
        self.scoped = False

    def __getitem__(self, k):
        return self.h[k]


class Prog:
    def __init__(self, nc):
        self.nc = nc
        self.es = ExitStack()
        self.ins = {e: [] for e in ENGS}
        self.nsem = 0
        self.dsems = []
        self.free_dsems = []
        self.scope_dsems = []
        self.marked = set()
        self.rw = {e: {} for e in ENGS}
        self.cur = self.es
        self.uid = 0
        self.last_tok = {e: None for e in ENGS}
        self.epoch = 0
        self.ep_of = {}

    def _prune(self, eng, deps):
        out = []
        w = self.rw[eng]
        for d in deps:
            key = (d[0], d[1]); val = d[2]
            if w.get(key, -1) >= val:
                continue
            w[key] = val
            out.append(d)
        return out

    def sb(self, name, shape, dt):
        self.uid += 1
        name = "%s_%d" % (name, self.uid)
        t = T(self.cur.enter_context(self.nc.sbuf_tensor(name, list(shape), dt)), name)
        t.scoped = self.cur is not self.es
        return t

    def ps(self, name, shape, dt=F32):
        self.uid += 1
        name = "%s_%d" % (name, self.uid)
        return T(self.cur.enter_context(self.nc.psum_tensor(name, list(shape), dt)), name)

    def push_scope(self):
        self.cur = ExitStack()

    def pop_scope(self):
        self.barrier()
        self.cur.close()
        self.cur = self.es
        self.free_dsems.extend(self.scope_dsems)
        self.scope_dsems = []

    def barrier(self):
        for e in ENGS:
            deps = [t for e2, t in self.last_tok.items() if e2 != e and t is not None]
            deps += [("d", s, c[1]) for s, c in enumerate(self.dsems) if c[1] > 0]
            deps = self._prune(e, deps)
            self.ins[e].append(dict(deps=deps, fn=None, dma=None))

    def dram(self, name, shape, dt, kind="Internal"):
        return T(self.nc.dram_tensor(name, list(shape), dt, kind=kind), name)

    def _new_dsem(self, scoped=False):
        if scoped and self.free_dsems:
            i = self.free_dsems.pop()
        else:
            h = self.es.enter_context(self.nc.semaphore("d%d" % len(self.dsems)))
            self.dsems.append([h, 0])
            i = len(self.dsems) - 1
        if scoped:
            self.scope_dsems.append(i)
        return i

    def _deps(self, eng, reads, writes):
        deps = []
        for t in reads:
            if t.lw is not None:
                deps.append(t.lw)
        for t in writes:
            if t.lw is not None:
                deps.append(t.lw)
            deps.extend(t.rd)
        out = []
        for d in deps:
            if d[0] == "e":
                if d[1] == eng:
                    continue
            out.append(d)
        if eng != "pe":
            for t in reads:
                if t.lw is not None and t.lw[0] == "e" and t.lw[1] == eng:
                    out.append(t.lw)
        return out

    def op(self, eng, fn, reads=(), writes=()):
        deps = self._prune(eng, self._deps(eng, reads, writes))
        idx = len(self.ins[eng])
        self.ins[eng].append(dict(deps=deps, fn=fn, dma=None))
        tok = ("e", eng, idx)
        self.last_tok[eng] = tok
        self.ep_of[(eng, idx)] = self.epoch
        for t in reads:
            t.rd = [r for r in t.rd if not (r[0] == "e" and r[1] == eng)]
            t.rd.append(tok)
        for t in writes:
            t.lw = tok
            t.rd = []
        return tok

    def dma(self, eng, out_t, out_ap, in_t, in_ap, nodep_out=False, **kw):
        reads = [in_t] if in_t is not None else []
        writes = [out_t] if out_t is not None else []
        deps = []
        for t in reads:
            if t.lw is not None:
                deps.append(t.lw)
        for t in writes:
            if nodep_out:
                continue
            if t.lw is not None:
                deps.append(t.lw)
            deps.extend(t.rd)
        owner = out_t if (out_t is not None and out_t.dsem is not None) else None
        if owner is None:
            owner = out_t if out_t is not None else in_t
        if owner.dsem is None:
            owner.dsem = self._new_dsem(scoped=getattr(owner, "scoped", False))
        s = owner.dsem
        self.dsems[s][1] += 16
        tok = ("d", s, self.dsems[s][1])
        deps = self._prune(eng, deps)
        self.ins[eng].append(dict(deps=deps, fn=None, dma=(out_ap, in_ap, s, kw)))
        for t in reads:
            t.rd.append(tok)
        for t in writes:
            t.lw = tok
            t.rd = []
        return tok

    def wait_all(self, eng, tiles):
        deps = []
        for t in tiles:
            if t.lw is not None:
                deps.append(t.lw)
            deps.extend(t.rd)
        deps = self._prune(eng, deps)
        self.ins[eng].append(dict(deps=deps, fn=None, dma=None))

    def finalize(self):
        nc = self.nc
        for e in ENGS:
            for rec in self.ins[e]:
                for d in rec["deps"]:
                    if d[0] == "e":
                        self.marked.add((d[1], d[2]))
        count_at = {}
        for e in ENGS:
            c = {}
            for i, rec in enumerate(self.ins[e]):
                if (e, i) in self.marked:
                    ep = self.ep_of[(e, i)]
                    c[ep] = c.get(ep, 0) + 1
                    count_at[(e, i)] = c[ep]
        esem = {(e, ep): self.es.enter_context(nc.semaphore("s_%s_%d" % (e, ep)))
                for e in ENGS for ep in range(self.epoch + 1)}
        engobj = {"pe": "tensor", "act": "scalar", "dve": "vector", "pool": "gpsimd", "sp": "sync"}
        block = self.es.enter_context(nc.Block())
        stats = {}

        def make_body(e):
            def body(eng):
                waited = {}
                nwait = 0
                for i, rec in enumerate(self.ins[e]):
                    need = {}
                    for d in rec["deps"]:
                        if d[0] == "e":
                            key = ("e", (d[1], self.ep_of[(d[1], d[2])])); val = count_at[(d[1], d[2])]
                        else:
                            key = ("d", d[1]); val = d[2]
                        if waited.get(key, 0) >= val:
                            continue
                        if need.get(key, 0) < val:
                            need[key] = val
                    for key, val in need.items():
                        h = esem[key[1]] if key[0] == "e" else self.dsems[key[1]][0]
                        eng.wait_ge(h, val)
                        waited[key] = val
                        nwait += 1
                    if rec.get("cc") is not None:
                        rec["cc"](eng).then_inc(self.dsems[rec["ccsem"]][0], 16)
                    elif rec["dma"] is not None:
                        out_ap, in_ap, s, kw = rec["dma"]
                        try:
                            eng.dma_start(out=out_ap, in_=in_ap, **kw).then_inc(self.dsems[s][0], 16)
                        except Exception:
                            print("DMA FAILED:", e, "out=", out_ap, "in=", in_ap)
                            raise
                    elif rec["fn"] is not None:
                        ins = rec["fn"](eng)
                        if (e, i) in self.marked:
                            ins.then_inc(esem[(e, self.ep_of[(e, i)])], 1)
                stats[e] = (len(self.ins[e]), nwait)
            return body

        for e in ENGS:
            if self.ins[e]:
                getattr(block, engobj[e])(make_body(e))
        self.stats = stats
        self.es.close()


NTOK = 2048
HALO = 4
NCOL = NTOK + HALO
RANGES = [(0, 1028), (1028, 1024)]
EPS = 1e-6


def tiles_of(n):
    k = (n + 511) // 512
    out = []
    o = 0
    for i in range(k):
        w = (n - o + (k - i) - 1) // (k - i)
        out.append((o, w))
        o += w
    return out


class TokStage:
    def __init__(self, P, mode):
        self.P = P
        self.mode = mode
        nc = P.nc
        D = lambda n, s: P.dram(n, s, F32, kind="ExternalInput")
        O = lambda n, s: P.dram(n, s, F32, kind="ExternalOutput")
        self.xT = D("xT", [1024, NCOL])
        self.memT = D("memT", [1024, 256])
        self.gains = D("gains", [128, 8 * 16])
        self.hg = D("hg", [128, 16])
        self.flag = D("flag", [128, 1])
        self.mem_w_kv = D("mem_w_kv", [4, 1024, 512])
        self.w_out = D("w_out", [4, 1024, 1024])
        self.w_up = D("w_up", [4, 1024, 4096])
        self.w_down = D("w_down", [4, 4096, 1024])
        self.b_w_in = D("b_w_in", [2, 1024, 1060])
        if mode == "L1":
            self.a_w_in = D("a_w_in", [2, 1024, 2560])
            self.convw = D("convw", [128, 2 * 6 * 3])
            self.w_kv = D("w_kv", [1024, 768])
            self.kvT_o = O("kvT_o", [768, NCOL])
        else:
            self.ytokT = D("ytokT", [768, NCOL])
        self.xT_o = O("xT_o", [1024, NCOL])
        if mode != "L5":
            self.qT_o = O("qT_o", [768, NCOL])
            self.gT_o = O("gT_o", [36, NCOL])

        self.alloc()

    def alloc(self):
        P = self.P
        sb = P.sb
        self.x = sb("x", [128, 8, NCOL], F32)
        self.h = sb("h", [128, 8, 1028], BF16)
        self.y = sb("y", [128, 8, 1028], BF16)
        self.act = sb("act", [128, 8, 1028], BF16)
        self.u = sb("u", [128, 2, 1028], F32)
        self.v = sb("v", [128, 2, 1030], F32)
        self.qm = sb("qm", [128, 2, 1028], F32)
        self.carry = sb("carry", [128, 6, 2], F32)
        self.rstd = sb("rstd", [128, 512], F32)
        self.sq = [sb("sq%d" % i, [128, 512], BF16) for i in range(2)]
        self.tmp = [sb("tmp%d" % i, [128, 512], F32) for i in range(3)]
        self.ost = [sb("ost%d" % i, [128, 512], F32) for i in range(2)]
        self.wb = [sb("wb%d" % i, [128, 8, 256], BF16) for i in range(4)]
        self.memh = sb("memh", [128, 8, 256], BF16)
        self.kraw = sb("kraw", [128, 2, 256], F32)
        self.memk = sb("memk", [128, 2, 256], BF16)
        self.vaug = sb("vaug", [128, 2, 4, 128], BF16)
        self.qn = sb("qn", [128, 512], BF16)
        self.eT = [sb("eT%d" % i, [128, 512], BF16) for i in range(2)]
        self.rd = sb("rd", [64, 512], F32)
        self.g = sb("g", [128, 8 * 16], F32)
        self.hgs = sb("hgs", [128, 16], F32)
        self.flg = sb("flg", [128, 1], F32)
        self.cw = sb("cw", [128, 36], F32)
        self.ones = sb("ones", [128, 128], BF16)
        self.bd = sb("bd", [128, 128], BF16)
        self.epsT = sb("epsT", [128, 1], F32)
        self.pp = [P.ps("pp%d" % i, [128, 512]) for i in range(4)]
        self.pl = [P.ps("pl%d" % i, [128, 512]) for i in range(2)]
        self.po = P.ps("po", [128, 512])
        self.pn = P.ps("pn", [128, 512])
        self.cnt = dict(wf=0, wb=0, pp=0, pl=0, sq=0, tmp=0, ost=0, eT=0)

    def rot(self, name, lst):
        i = self.cnt[name]
        self.cnt[name] += 1
        return lst[i % len(lst)]

    def stream(self, wT, w_ap, ncols):
        P = self.P
        wb = self.rot("wb", self.wb)
        P.dma("pool", wb, wb[:, :, 0:ncols], wT, w_ap.rearrange("(c p) n -> p c n", p=128))
        return wb

    def proj(self, wb, sub, msz, rhs_t, rhs_fn, tiles, consumer):
        P = self.P
        for (t0, w) in tiles:
            ps = self.rot("pp", self.pp)
            for kc in range(8):
                P.op("pe", lambda e, ps=ps, kc=kc, t0=t0, w=w: e.matmul(
                    ps[0:msz, 0:w], lhsT=wb[:, kc, sub * 128: sub * 128 + msz], rhs=rhs_fn(kc, t0, w),
                    start=(kc == 0), stop=(kc == 7)), [wb, rhs_t], [ps])
            consumer(ps, t0, w)

    def setup(self):
        P = self.P
        P.dma("sp", self.g, self.g[:], self.gains, self.gains[:])
        P.dma("sp", self.hgs, self.hgs[:], self.hg, self.hg[:])
        P.dma("sp", self.flg, self.flg[:], self.flag, self.flag[:])
        if self.mode == "L1":
            P.dma("sp", self.cw, self.cw[:], self.convw, self.convw[:])
        for c in range(8):
            P.dma("sp", self.x, self.x[:, c, :], self.xT, self.xT[c * 128:(c + 1) * 128, :], nodep_out=True)
            P.dma("sp", self.u, self.memx(c, 0, 256), self.memT, self.memT[c * 128:(c + 1) * 128, :], nodep_out=True)
        P.op("pool", lambda e: e.memset(self.ones[:], 1.0), [], [self.ones])
        P.op("pool", lambda e: e.memset(self.bd[:], 0.0), [], [self.bd])
        P.op("pool", lambda e: e.memset(self.bd[0:64, 0:64], 1.0), [], [self.bd])
        P.op("pool", lambda e: e.memset(self.bd[64:128, 64:128], 1.0), [], [self.bd])
        P.op("pool", lambda e: e.memset(self.epsT[:], EPS), [], [self.epsT])
        P.op("pool", lambda e: e.memset(self.vaug[:], 1.0), [], [self.vaug])
        P.op("pool", lambda e: e.memset(self.carry[:], 0.0), [], [self.carry])
        self.norm(self.u, self.memx, 256, [(0, 256)], 15, self.memh, 0)

    def gcol(self, which, c):
        return self.g[:, which * 8 + c: which * 8 + c + 1]

    def memx(self, c, t0, w):
        return self.u[:, c // 4, (c % 4) * 256 + t0: (c % 4) * 256 + t0 + w]

    def xs(self, c0):
        return lambda c, t0, w: self.x[:, c, c0 + t0: c0 + t0 + w]

    def norm(self, src, src_fn, n, tiles, which, dst, d0):
        P = self.P
        for (t0, w) in tiles:
            for c in range(8):
                sq = self.rot("sq", self.sq)
                P.op("act", lambda e, sq=sq, c=c, t0=t0, w=w: e.activation(
                    out=sq[:, 0:w], in_=src_fn(c, t0, w), func=AF.Square), [src], [sq])
                P.op("pe", lambda e, sq=sq, c=c, w=w: e.matmul(
                    self.pn[:, 0:w], lhsT=self.ones[:], rhs=sq[:, 0:w], start=(c == 0), stop=(c == 7)),
                    [self.ones, sq], [self.pn])
            P.op("act", lambda e, w=w: e.activation(out=self.rstd[:, 0:w], in_=self.pn[:, 0:w], func=AF.Ln,
                                                    bias=self.epsT[:], scale=1.0 / 1024.0),
                 [self.pn, self.epsT], [self.rstd])
            P.op("act", lambda e, w=w: e.activation(out=self.rstd[:, 0:w], in_=self.rstd[:, 0:w], func=AF.Exp, scale=-0.5),
                 [self.rstd], [self.rstd])
            for c in range(8):
                eng = "dve"
                P.op(eng, lambda e, c=c, t0=t0, w=w: e.scalar_tensor_tensor(
                    out=dst[:, c, d0 + t0: d0 + t0 + w], in0=src_fn(c, t0, w),
                    scalar=self.gcol(which, c), in1=self.rstd[:, 0:w], op0=ALU.mult, op1=ALU.mult),
                    [src, self.g, self.rstd], [dst])

    def headnorm(self, src_t, src_ap_fn, w, gain_ap, dst_t, dst_ap):
        P = self.P
        sq = self.rot("sq", self.sq)
        P.op("act", lambda e, sq=sq: e.activation(out=sq[:, 0:w], in_=src_ap_fn(), func=AF.Square), [src_t], [sq])
        P.op("pe", lambda e, sq=sq: e.matmul(self.pn[:, 0:w], lhsT=self.bd[:], rhs=sq[:, 0:w], start=True, stop=True),
             [self.bd, sq], [self.pn])
        P.op("act", lambda e: e.activation(out=self.rstd[:, 0:w], in_=self.pn[:, 0:w], func=AF.Ln,
                                           bias=self.epsT[:], scale=1.0 / 64.0), [self.pn, self.epsT], [self.rstd])
        P.op("act", lambda e: e.activation(out=self.rstd[:, 0:w], in_=self.rstd[:, 0:w], func=AF.Exp, scale=-0.5),
             [self.rstd], [self.rstd])
        P.op("dve", lambda e: e.scalar_tensor_tensor(out=dst_ap, in0=src_ap_fn(), scalar=gain_ap,
                                                     in1=self.rstd[:, 0:w], op0=ALU.mult, op1=ALU.mult),
             [src_t, self.hgs, self.rstd], [dst_t])

    def mem_kv(self, l):
        P = self.P
        wb = self.stream(self.mem_w_kv, self.mem_w_kv[l, :, 0:256], 256)
        for sub in range(2):
            def cons(ps, t0, w, sub=sub):
                P.op("act", lambda e: e.activation(out=self.kraw[:, sub, 0:256], in_=ps[:, 0:256], func=AF.Copy),
                     [ps], [self.kraw])
            self.proj(wb, sub, 128, self.memh, lambda kc, t0, w: self.memh[:, kc, t0:t0 + w], [(0, 256)], cons)
            self.headnorm(self.kraw, lambda sub=sub: self.kraw[:, sub, 0:256], 256,
                          self.hgs[:, 4 + l: 5 + l], self.memk, self.memk[:, sub, 0:256])
        wb = self.stream(self.mem_w_kv, self.mem_w_kv[l, :, 256:512], 256)
        for mc in range(2):
            ps = self.rot("pp", self.pp)
            for kc in range(8):
                P.op("pe", lambda e, ps=ps, kc=kc, mc=mc: e.matmul(
                    ps[:, 0:256], lhsT=self.memh[:, kc, mc * 128:(mc + 1) * 128], rhs=wb[:, kc, 0:256],
                    start=(kc == 0), stop=(kc == 7)), [self.memh, wb], [ps])
            for hh in range(4):
                P.op("act", lambda e, ps=ps, mc=mc, hh=hh: e.activation(
                    out=self.vaug[:, mc, hh, 0:64], in_=ps[:, hh * 64:(hh + 1) * 64], func=AF.Copy), [ps], [self.vaug])

    def mem_attn(self, l, n, tiles):
        P = self.P
        for (t0, w) in tiles:
            for j in range(2):
                self.headnorm(self.qm, lambda j=j, t0=t0, w=w: self.qm[:, j, t0:t0 + w], w,
                              self.hgs[:, l: l + 1], self.qn, self.qn[:, 0:w])
                for hh in range(2):
                    b0 = 64 * hh
                    hd = 2 * j + hh
                    ets = []
                    for mc in range(2):
                        pl = self.rot("pl", self.pl)
                        P.op("pe", lambda e, pl=pl, mc=mc, b0=b0, j=j, w=w: e.matmul(
                            pl[:, 0:w], lhsT=self.memk[b0:b0 + 64, j, mc * 128:(mc + 1) * 128],
                            rhs=self.qn[b0:b0 + 64, 0:w], start=True, stop=True), [self.memk, self.qn], [pl])
                        eT = self.rot("eT", self.eT)
                        P.op("act", lambda e, pl=pl, eT=eT, w=w: e.activation(
                            out=eT[:, 0:w], in_=pl[:, 0:w], func=AF.Exp, scale=0.125), [pl], [eT])
                        ets.append(eT)
                    for mc in range(2):
                        P.op("pe", lambda e, mc=mc, hd=hd, w=w, eT=ets[mc]: e.matmul(
                            self.po[:, 0:w], lhsT=self.vaug[:, mc, hd, :], rhs=eT[:, 0:w],
                            start=(mc == 0), stop=(mc == 1)), [self.vaug, ets[mc]], [self.po])
                    P.op("act", lambda e, w=w: e.activation(out=self.rd[0:64, 0:w], in_=self.po[64:128, 0:w], func=AF.Ln),
                         [self.po], [self.rd])
                    P.op("act", lambda e, w=w: e.activation(out=self.rd[0:64, 0:w], in_=self.rd[0:64, 0:w], func=AF.Exp,
                                                            scale=-1.0), [self.rd], [self.rd])
                    P.op("dve", lambda e, w=w, b0=b0, j=j, t0=t0: e.tensor_tensor(
                        out=self.y[b0:b0 + 64, 6 + j, t0:t0 + w], in0=self.po[0:64, 0:w], in1=self.rd[0:64, 0:w],
                        op=ALU.mult), [self.po, self.rd], [self.y])

    def add_into_x(self, c0):
        P = self.P

        def mk(ch):
            def cons(ps, t0, w):
                P.op("dve", lambda e: e.tensor_tensor(
                    out=self.x[:, ch, c0 + t0: c0 + t0 + w], in0=ps[:, 0:w], in1=self.x[:, ch, c0 + t0: c0 + t0 + w],
                    op=ALU.add), [ps, self.x], [self.x])
            return cons
        return mk

    def out_proj(self, l, c0, n, tiles):
        mk = self.add_into_x(c0)
        for blk in range(4):
            wb = self.stream(self.w_out, self.w_out[l, :, blk * 256:(blk + 1) * 256], 256)
            for sub in range(2):
                self.proj(wb, sub, 128, self.y, lambda kc, t0, w: self.y[:, kc, t0:t0 + w], tiles, mk(blk * 2 + sub))

    def mlp(self, l, c0, n, tiles):
        P = self.P
        self.norm(self.x, self.xs(c0), n, tiles, 4 + l, self.h, 0)
        mk = self.add_into_x(c0)
        for sg in range(4):
            for blk in range(4):
                wb = self.stream(self.w_up, self.w_up[l, :, sg * 1024 + blk * 256: sg * 1024 + (blk + 1) * 256], 256)
                for sub in range(2):
                    def cons(ps, t0, w, fc=blk * 2 + sub):
                        tmp = self.rot("tmp", self.tmp)
                        P.op("act", lambda e: e.activation(out=tmp[:, 0:w], in_=ps[:, 0:w], func=AF.Relu), [ps], [tmp])
                        P.op("dve", lambda e: e.tensor_tensor(out=self.act[:, fc, t0:t0 + w], in0=tmp[:, 0:w],
                                                              in1=tmp[:, 0:w], op=ALU.mult), [tmp], [self.act])
                    self.proj(wb, sub, 128, self.h, lambda kc, t0, w: self.h[:, kc, t0:t0 + w], tiles, cons)
            for blk in range(4):
                wb = self.stream(self.w_down, self.w_down[l, sg * 1024:(sg + 1) * 1024, blk * 256:(blk + 1) * 256], 256)
                for sub in range(2):
                    self.proj(wb, sub, 128, self.act, lambda kc, t0, w: self.act[:, kc, t0:t0 + w], tiles,
                              mk(blk * 2 + sub))

    def mix_a(self, l, ri, c0, n, tiles):
        P = self.P
        W = self.a_w_in
        hrhs = lambda kc, t0, w: self.h[:, kc, t0:t0 + w]
        for i in range(3):
            wb = self.stream(W, W[l, :, 1536 + 256 * i: 1536 + 256 * (i + 1)], 256)
            for sub in range(2):
                def cons(ps, t0, w, sub=sub):
                    P.op("act", lambda e: e.activation(out=self.u[:, sub, t0:t0 + w], in_=ps[:, 0:w], func=AF.Copy),
                         [ps], [self.u])
                self.proj(wb, sub, 128, self.h, hrhs, tiles, cons)
            wb = self.stream(W, W[l, :, 768 + 256 * i: 768 + 256 * (i + 1)], 256)
            for sub in range(2):
                def cons(ps, t0, w, sub=sub):
                    P.op("dve", lambda e: e.tensor_tensor(out=self.v[:, sub, 2 + t0: 2 + t0 + w], in0=ps[:, 0:w],
                                                          in1=self.u[:, sub, t0:t0 + w], op=ALU.mult),
                         [ps, self.u], [self.v])
                self.proj(wb, sub, 128, self.h, hrhs, tiles, cons)
            for sub in range(2):
                ch = 2 * i + sub
                self.conv_lead(l, ri, sub, ch, n)
                cwc = lambda k, ch=ch: self.cw[:, l * 18 + ch * 3 + k: l * 18 + ch * 3 + k + 1]
                P.op("dve", lambda e, sub=sub, cwc=cwc: e.tensor_scalar(
                    out=self.u[:, sub, 0:n], in0=self.v[:, sub, 0:n], scalar1=cwc(0), scalar2=None, op0=ALU.mult),
                    [self.v, self.cw], [self.u])
                for k in (1, 2):
                    P.op("dve", lambda e, sub=sub, cwc=cwc, k=k: e.scalar_tensor_tensor(
                        out=self.u[:, sub, 0:n], in0=self.v[:, sub, k:k + n], scalar=cwc(k), in1=self.u[:, sub, 0:n],
                        op0=ALU.mult, op1=ALU.add), [self.v, self.cw, self.u], [self.u])
            wb = self.stream(W, W[l, :, 256 * i: 256 * (i + 1)], 256)
            for sub in range(2):
                def cons(ps, t0, w, sub=sub, ch=2 * i + sub):
                    P.op("dve", lambda e: e.tensor_tensor(out=self.y[:, ch, t0:t0 + w], in0=ps[:, 0:w],
                                                          in1=self.u[:, sub, t0:t0 + w], op=ALU.mult),
                         [ps, self.u], [self.y])
                self.proj(wb, sub, 128, self.h, hrhs, tiles, cons)
        wb = self.stream(W, W[l, :, 2304:2560], 256)
        self.qmem_proj(wb, tiles)

    def conv_lead(self, l, ri, sub, ch, n):
        P = self.P
        if ri == 0:
            P.op("pool", lambda e: e.memset(self.v[:, sub, 0:2], 0.0), [], [self.v])
            self.halo_fix(sub)
            P.op("dve", lambda e: e.tensor_copy(out=self.carry[:, ch, :], in_=self.v[:, sub, n:n + 2]),
                 [self.v], [self.carry])
        else:
            P.op("dve", lambda e: e.tensor_copy(out=self.v[:, sub, 0:2], in_=self.carry[:, ch, :]),
                 [self.carry], [self.v])

    def halo_fix(self, sub):
        self.P.op("dve", lambda e: e.tensor_scalar(
            out=self.v[:, sub, 2:2 + HALO], in0=self.v[:, sub, 2:2 + HALO], scalar1=self.flg[:, 0:1],
            scalar2=None, op0=ALU.mult), [self.v, self.flg], [self.v])

    def store_tile(self, dstT, r, msz, c0, t0, w, ost):
        self.P.dma("sp", dstT, dstT[r:r + msz, c0 + t0: c0 + t0 + w], ost, ost[0:msz, 0:w], nodep_out=True)

    def ytok_src(self, ch, c0, t0, w):
        return self.ytokT[ch * 128:(ch + 1) * 128, c0 + t0: c0 + t0 + w]

    def qmem_proj(self, wb, tiles):
        P = self.P
        for sub in range(2):
            def cons(ps, t0, w, sub=sub):
                P.op("act", lambda e: e.activation(out=self.qm[:, sub, t0:t0 + w], in_=ps[:, 0:w], func=AF.Copy),
                     [ps], [self.qm])
            self.proj(wb, sub, 128, self.h, lambda kc, t0, w: self.h[:, kc, t0:t0 + w], tiles, cons)

    def store_proj(self, wT, w_ap, ncols, dstT, row0, c0, tiles, func=AF.Copy):
        P = self.P
        wb = self.stream(wT, w_ap, ncols)
        nsub = (ncols + 127) // 128
        for sub in range(nsub):
            msz = min(128, ncols - sub * 128)

            def cons(ps, t0, w, sub=sub, msz=msz):
                ost = self.rot("ost", self.ost)
                P.op("act", lambda e: e.activation(out=ost[0:msz, 0:w], in_=ps[0:msz, 0:w], func=func), [ps], [ost])
                r = row0 + sub * 128
                self.store_tile(dstT, r, msz, c0, t0, w, ost)
            self.proj(wb, sub, msz, self.h, lambda kc, t0, w: self.h[:, kc, t0:t0 + w], tiles, cons)

    def pre_b(self, j, c0, n, tiles):
        W = self.b_w_in
        self.norm(self.x, self.xs(c0), n, tiles, 2 + j, self.h, 0)
        for i in range(3):
            self.store_proj(W, W[j, :, 256 * i:256 * (i + 1)], 256, self.qT_o, 256 * i, c0, tiles)
        self.store_proj(W, W[j, :, 768:804], 36, self.gT_o, 0, c0, tiles, func=AF.Sigmoid)

    def post_b(self, j, c0, n, tiles):
        P = self.P
        W = self.b_w_in
        l = 2 + j
        self.norm(self.x, self.xs(c0), n, tiles, l, self.h, 0)
        wb = self.stream(W, W[j, :, 804:1060], 256)
        self.qmem_proj(wb, tiles)
        for ch in range(6):
            for (t0, w) in tiles:
                tmp = self.rot("tmp", self.tmp)
                P.dma("sp", tmp, tmp[:, 0:w], self.ytokT, self.ytok_src(ch, c0, t0, w))
                P.op("act", lambda e, tmp=tmp, ch=ch, t0=t0, w=w: e.activation(out=self.y[:, ch, t0:t0 + w],
                                                                               in_=tmp[:, 0:w], func=AF.Copy), [tmp], [self.y])
        self.mem_attn(l, n, tiles)
        self.out_proj(l, c0, n, tiles)
        self.mlp(l, c0, n, tiles)

    def store_x(self):
        P = self.P
        for c in range(8):
            P.dma("sp", self.xT_o, self.xT_o[c * 128:(c + 1) * 128, :], self.x, self.x[:, c, :], nodep_out=True)

    def build(self):
        P = self.P
        self.setup()
        if self.mode == "L1":
            for l in range(2):
                self.mem_kv(l)
                for ri, (c0, n) in enumerate(RANGES):
                    tiles = tiles_of(n)
                    self.norm(self.x, self.xs(c0), n, tiles, l, self.h, 0)
                    self.mix_a(l, ri, c0, n, tiles)
                    self.mem_attn(l, n, tiles)
                    self.out_proj(l, c0, n, tiles)
                    self.mlp(l, c0, n, tiles)
            for ri, (c0, n) in enumerate(RANGES):
                tiles = tiles_of(n)
                self.norm(self.x, self.xs(c0), n, tiles, 8, self.h, 0)
                for i in range(3):
                    self.store_proj(self.w_kv, self.w_kv[:, 256 * i:256 * (i + 1)], 256, self.kvT_o, 256 * i, c0, tiles)
                self.pre_b(0, c0, n, tiles)
            self.store_x()
            outs = [self.kvT_o, self.qT_o, self.gT_o, self.xT_o]
        elif self.mode == "L3":
            self.mem_kv(2)
            for ri, (c0, n) in enumerate(RANGES):
                tiles = tiles_of(n)
                self.post_b(0, c0, n, tiles)
            for ri, (c0, n) in enumerate(RANGES):
                tiles = tiles_of(n)
                self.pre_b(1, c0, n, tiles)
            self.store_x()
            outs = [self.qT_o, self.gT_o, self.xT_o]
        else:
            self.mem_kv(3)
            for ri, (c0, n) in enumerate(RANGES):
                tiles = tiles_of(n)
                self.post_b(1, c0, n, tiles)
            self.store_x()
            outs = [self.xT_o]
        P.wait_all("sp", outs)
        P.finalize()


def build_tok(mode):
    nc = bass.Bass("TRN2", target_bir_lowering=False)
    P = Prog(nc)
    st = TokStage(P, mode)
    st.build()
    return nc, P


def host_gains(inp):
    g = np.zeros((16, 1024), np.float32)
    g[0:4] = inp["mix_norm"]
    g[4:8] = inp["mlp_norm"]
    g[8] = inp["kv_norm"]
    g[15] = inp["mem_norm"]
    return np.ascontiguousarray(g.reshape(16, 8, 128).transpose(2, 0, 1).reshape(128, 128))


def host_hg(inp):
    hg = np.zeros((128, 16), np.float32)
    for l in range(4):
        hg[:, l] = np.tile(inp["mem_q_gain"][l], 2)
        hg[:, 4 + l] = np.tile(inp["mem_k_gain"][l], 2)
    return hg


def host_convw(inp):
    w = inp["a_conv_w"].reshape(2, 3, 6, 128)
    return np.ascontiguousarray(w.transpose(3, 0, 2, 1).reshape(128, 36))

import math

SEQ = 8192
QT = 512
NQT = SEQ // QT
DC, LC = 3040, 5104
DW, LW = 1023, 1536
DS, LS = 1407, 1920
LTOT = LC + LW + LS
LU, LTW, LTS = 3072, 1408, 1792
MASKV = -30000.0
PIPE = 3


def rel_bucket_np(d):
    n = np.maximum(d, 0)
    nf = np.maximum(n, 1).astype(np.float32)
    large = 16 + (np.log(nf / np.float32(16)) / np.float32(math.log(1024 / 16)) * np.float32(16)).astype(np.int32)
    large = np.minimum(large, 31)
    return np.where(n < 16, n, large)


def host_consts():
    c = {}
    oh = np.zeros((33, LTOT), np.float32)
    x = np.arange(LC); d = DC - x
    b = rel_bucket_np(d)
    oh[b[d >= 0], x[d >= 0]] = 1.0
    oh[32, x[d < 0]] = 1.0
    x = np.arange(LW); d = DW - x; ok = (d >= 0) & (d < 512)
    b = rel_bucket_np(d)
    oh[b[ok], LC + x[ok]] = 1.0
    oh[32, LC + x[~ok]] = 1.0
    x = np.arange(LS); d = DS - x; ok = d >= 0
    b = rel_bucket_np(d)
    oh[b[ok], LC + LW + x[ok]] = 1.0
    oh[32, LC + LW + x[~ok]] = 1.0
    c["OH"] = oh
    c["ident"] = np.eye(128, dtype=np.float32)
    k = np.arange(SEQ)
    r = 2 * ((k // 128) % 32) + ((k % 128) >= 64)
    ee = np.zeros((64, SEQ), np.float32)
    ee[r, k] = 30000.0
    c["EE"] = ee
    i = np.arange(512)[:, None] * 16
    s = np.arange(128)[None, :] * 64
    ov = np.clip(np.minimum(i + 32, s + 64) - np.maximum(i, s), 0, None).astype(np.float32) / 32.0
    ov[511] = 0.0
    c["ovl"] = np.ascontiguousarray(ov.reshape(4, 128, 128).transpose(1, 0, 2))
    tc = np.zeros((NQT, 128, 4, 2, 128), np.float32)
    j = np.arange(128)[None, :]
    for qt in range(NQT):
        for ss in range(4):
            t = qt * QT + 511 - (128 * ss + np.arange(128))
            back = (t // 64)[:, None] - j
            forced = (j == 0) | ((back >= 0) & (back < 2))
            A = ((back >= 0) & ~forced).astype(np.float32)
            B = np.where(back >= 0, np.where(forced, 1e4, 0.0), -1e30).astype(np.float32)
            tc[qt, :, ss, 0] = A
            tc[qt, :, ss, 1] = B
    c["topc"] = tc.reshape(NQT, 128, 1024)
    return c


class Attn:
    def __init__(self, P):
        self.P = P
        D = lambda n, s: P.dram(n, s, F32, kind="ExternalInput")
        self.qT = D("qT", [384, SEQ])
        self.gB = D("gB", [9, SEQ])
        self.kvT = D("kvT", [384, SEQ])
        self.vtok = D("vtok", [SEQ, 128])
        self.blk = D("blk", [2, 64, 32, 512])
        self.hg = D("hg", [128, 8])
        self.pos = D("pos", [2, 64, 32])
        self.w1 = D("w1", [2, 2048, 256])
        self.w2 = D("w2", [2, 256, 64])
        self.rb6 = D("rb6", [32, 6])
        self.OH = D("OH", [33, LTOT])
        self.identD = D("ident", [128, 128])
        self.EE = D("EE", [64, SEQ])
        self.ovlD = D("ovl", [128, 4, 128])
        self.topc = D("topc", [NQT, 128, 1024])
        self.yT = P.dram("yT", [192, SEQ], F32, kind="ExternalOutput")
        self.Rrow = P.dram("Rrow", [6, LTOT], F32, kind="Internal")
        self.alloc()

    def alloc(self):
        P = self.P
        sb = P.sb
        self.ksaug = sb("ksaug", [128, SEQ], BF16)
        self.kwT = sb("kwT", [128, SEQ], BF16)
        self.vsaug = sb("vsaug", [128, 64, 128], BF16)
        self.vwaug = sb("vwaug", [128, 64, 128], BF16)
        self.kcT = sb("kcT", [128, 512], BF16)
        self.vcaug = sb("vcaug", [128, 4, 128], BF16)
        self.Ubig = sb("Ubig", [128, 6 * LU], BF16)
        self.TTw = [sb("TTw%d" % h, [128, LTW], BF16) for h in range(3)]
        self.TTs = [sb("TTs%d" % h, [128, LTS], BF16) for h in range(3)]
        self.ident = sb("identb", [128, 128], BF16)
        self.ones = sb("ones", [128, 128], BF16)
        self.ovl = sb("ovlb", [128, 4, 128], BF16)
        self.hgs = sb("hgs", [128, 8], F32)
        self.cf = sb("cf", [128, 6], F32)
        self.tab = sb("tab", [33, 6], F32)
        self.epsT = sb("epsT", [128, 1], F32)
        self.tinyT = sb("tinyT", [128, 1], F32)
        self.st = [sb("st%d" % i, [128, 1024], F32) for i in range(2)]
        self.sq = [sb("sq%d" % i, [128, 512], BF16) for i in range(2)]
        self.rq = sb("rq", [128, 512], F32)
        self.qraw = sb("qraw", [64, 6, 512], F32)
        self.qn = sb("qn", [128, 6, 512], BF16)
        self.qaug = sb("qaug", [128, 3, 2, 512], BF16)
        self.pT = [sb("pT%d" % i, [128, 512], BF16) for i in range(9)]
        self.rdq = sb("rdq", [128, 4], F32)
        self.rdb = sb("rdb", [128, 512], F32)
        self.impacc = sb("impacc", [128, 512], F32)
        self.tcs = sb("tcs", [128, 1024], F32)
        self.sc = sb("sc", [128, 128], F32)
        self.sc2 = sb("sc2", [128, 128], F32)
        self.mx = sb("mx", [128, 16], F32)
        self.mneg = sb("mneg", [128, 128], BF16)
        self.gt = [sb("gt%d" % i, [64, 512], F32) for i in range(2)]
        self.rdt = sb("rdt", [64, 512], F32)
        self.wt = sb("wt", [64, 512], F32)
        self.tm = sb("tm", [64, 512], F32)
        self.y = sb("y", [64, 3, 512], F32)
        self.w1b = sb("w1b", [64, 4, 256], BF16)
        self.blkb = sb("blkb", [64, 4, 512], BF16)
        self.posT = sb("posT", [64, 32], F32)
        self.w2b = sb("w2b", [128, 2, 64], BF16)
        self.gx = self.rdb
        self.gy = self.impacc
        self.gT = [sb("gT%d" % i, [128, 512], BF16) for i in range(2)]
        self.A = [P.ps("A%d" % i, [128, 512]) for i in range(3)]
        self.S = [P.ps("S%d" % i, [128, 512]) for i in range(2)]
        self.X0 = P.ps("X0", [128, 512])
        self.X1 = P.ps("X1", [128, 512])
        self.XT = P.ps("XT", [128, 512], BF16)
        self.S4 = self.S + [self.X0, self.X1]
        self.cnt = {}

    def uslice(self, h, mo):
        return self.Ubig[:, h * LU + mo: h * LU + mo + 512]

    def rot(self, name, lst):
        i = self.cnt.get(name, 0)
        self.cnt[name] = i + 1
        return lst[i % len(lst)]

    def norm64(self, src_t, src_ap, w, gain_ap, dst_t, dst_ap):
        P = self.P
        sq = self.rot("sq", self.sq)
        P.op("act", lambda e: e.activation(out=sq[0:64, 0:w], in_=src_ap, func=AF.Square), [src_t], [sq])
        P.op("pe", lambda e: e.matmul(self.X0[0:64, 0:w], lhsT=self.ones[0:64, 0:64], rhs=sq[0:64, 0:w],
                                      start=True, stop=True), [self.ones, sq], [self.X0])
        P.op("act", lambda e: e.activation(out=self.rq[0:64, 0:w], in_=self.X0[0:64, 0:w], func=AF.Ln,
                                           bias=self.epsT[0:64, :], scale=1.0 / 64.0), [self.X0, self.epsT], [self.rq])
        P.op("act", lambda e: e.activation(out=self.rq[0:64, 0:w], in_=self.rq[0:64, 0:w], func=AF.Exp, scale=-0.5),
             [self.rq], [self.rq])
        P.op("dve", lambda e: e.scalar_tensor_tensor(out=dst_ap, in0=src_ap, scalar=gain_ap, in1=self.rq[0:64, 0:w],
                                                     op0=ALU.mult, op1=ALU.mult), [src_t, self.hgs, self.rq], [dst_t])

    def setup(self):
        P = self.P
        st = self.st
        P.dma("sp", self.hgs, self.hgs[:], self.hg, self.hg[:])
        P.op("pool", lambda e: e.memset(self.ones[:], 1.0), [], [self.ones])
        P.op("pool", lambda e: e.memset(self.epsT[:], 1e-6), [], [self.epsT])
        P.op("pool", lambda e: e.memset(self.tinyT[:], 1e-30), [], [self.tinyT])
        P.op("pool", lambda e: e.memset(self.vsaug[:], 1.0), [], [self.vsaug])
        P.op("pool", lambda e: e.memset(self.vwaug[:], 1.0), [], [self.vwaug])
        P.op("pool", lambda e: e.memset(self.vcaug[:], 1.0), [], [self.vcaug])
        s = self.rot("st", st)
        P.dma("sp", s, s[:, 0:128], self.identD, self.identD[:])
        P.op("pool", lambda e, s=s: e.tensor_copy(out=self.ident[:], in_=s[:, 0:128]), [s], [self.ident])
        s = self.rot("st", st)
        P.dma("sp", s, s[:, 0:512], self.ovlD, self.ovlD[:].rearrange("p c j -> p (c j)"))
        P.op("pool", lambda e, s=s: e.tensor_copy(out=self.ovl[:].rearrange("p c j -> p (c j)"), in_=s[:, 0:512]),
             [s], [self.ovl])
        self.setup_tables()
        self.setup_kv()

    def load_rb(self):
        P = self.P
        rb = self.rb6[:]
        P.dma("sp", self.cf, self.cf[:], self.rb6, bass.AP(tensor=rb.tensor, offset=31 * 6, ap=[[0, 128], [1, 6]]))
        P.dma("sp", self.tab, self.tab[0:32, :], self.rb6, self.rb6[:])

    def setup_tables(self):
        P = self.P
        st = self.st
        P.op("pool", lambda e: e.memset(self.tab[:], MASKV), [], [self.tab])
        self.load_rb()
        P.op("dve", lambda e: e.tensor_scalar(out=self.tab[0:32, :], in0=self.tab[0:32, :], scalar1=8.0, scalar2=None,
                                              op0=ALU.mult), [self.tab], [self.tab])
        o = 0
        while o < LTOT:
            w = min(512, LTOT - o)
            s = self.rot("st", st)
            P.dma("sp", s, s[0:33, 0:w], self.OH, self.OH[:, o:o + w])
            P.op("pe", lambda e, s=s, w=w: e.matmul(self.X1[0:6, 0:w], lhsT=self.tab[:, :], rhs=s[0:33, 0:w],
                                                    start=True, stop=True), [self.tab, s], [self.X1])
            s2 = self.rot("st", st)
            P.op("act", lambda e, s2=s2, w=w: e.activation(out=s2[0:6, 0:w], in_=self.X1[0:6, 0:w], func=AF.Copy),
                 [self.X1], [s2])
            P.dma("sp", self.Rrow, self.Rrow[:, o:o + w], s2, s2[0:6, 0:w])
            o += w
        rr = self.Rrow[:]

        def expand(dst, d0, h, base, pstep, L, nodep=False):
            P.dma("pool", dst, dst[:, d0:d0 + L], self.Rrow,
                  bass.AP(tensor=rr.tensor, offset=h * LTOT + base, ap=[[pstep, 128], [1, L]]), nodep_out=nodep)
        for h in range(6):
            expand(self.Ubig, h * LU, h, 0, 16, LU, nodep=(h > 0))
        for h in range(3):
            expand(self.TTw[h], 0, h, LC, 1, LTW)
            expand(self.TTs[h], 0, h, LC + LW, 1, LTS)

    def load_kv_piece(self, s, slot, c0):
        self.P.dma("sp", s, s[0:64, :], self.kvT, self.kvT[slot * 64:(slot + 1) * 64, c0:c0 + 1024])

    def load_v_piece(self, piece):
        P = self.P
        c0 = piece * 1024
        s = self.rot("st", self.st)
        P.dma("sp", s, s[:, :].rearrange("p (c f) -> p c f", f=128), self.vtok,
              self.vtok[c0:c0 + 1024, :].rearrange("(c p) f -> p c f", p=128))
        sv = s[:, :].rearrange("p (c f) -> p c f", f=128)
        P.op("pool", lambda e, sv=sv, piece=piece: e.tensor_copy(
            out=self.vsaug[:, piece * 8:(piece + 1) * 8, 0:64], in_=sv[:, :, 0:64]), [s], [self.vsaug])
        P.op("pool", lambda e, sv=sv, piece=piece: e.tensor_copy(
            out=self.vwaug[:, piece * 8:(piece + 1) * 8, 0:64], in_=sv[:, :, 64:128]), [s], [self.vwaug])

    def setup_kv(self):
        P = self.P
        st = self.st
        for piece in range(8):
            c0 = piece * 1024
            for slot, gcol, dst in ((2, 2, self.ksaug), (4, 3, self.kwT)):
                s = self.rot("st", st)
                self.load_kv_piece(s, slot, c0)
                for hf in range(2):
                    a = hf * 512
                    self.norm64(s, s[0:64, a:a + 512], 512, self.hgs[0:64, gcol:gcol + 1], dst,
                                dst[0:64, c0 + a:c0 + a + 512])
            P.dma("pool", self.ksaug, self.ksaug[64:128, c0:c0 + 1024], self.EE, self.EE[:, c0:c0 + 1024])
            self.load_v_piece(piece)
        self.compress()

    def compress_rhs(self, kvi, tg):
        P = self.P
        for th in range(2):
            s = self.rot("st", self.st)
            t0 = tg * 4 + th * 2
            P.dma("sp", s, s[0:64, :].rearrange("p (t n) -> p t n", n=512), self.blk,
                  self.blk[kvi, :, t0:t0 + 2, :])
            for tt in range(2):
                t = t0 + tt
                P.op("dve", lambda e, s=s, tt=tt, th=th, t=t: e.tensor_scalar(
                    out=self.blkb[:, th * 2 + tt, :], in0=s[0:64, tt * 512:(tt + 1) * 512],
                    scalar1=self.posT[:, t:t + 1], scalar2=None, op0=ALU.add), [s, self.posT], [self.blkb])

    def compress_begin(self, kvi):
        pass

    def compress(self):
        P = self.P
        st = self.st
        H = [self.A[0], self.A[1]]
        for kvi in range(2):
            self.compress_begin(kvi)
            P.dma("sp", self.posT, self.posT[:], self.pos, self.pos[kvi])
            s = self.rot("st", st)
            P.dma("sp", s, s[:, 0:128].rearrange("p (c d) -> p c d", d=64), self.w2,
                  self.w2[kvi].rearrange("(c p) d -> p c d", p=128))
            P.op("pool", lambda e, s=s: e.tensor_copy(out=self.w2b[:].rearrange("p c d -> p (c d)"), in_=s[:, 0:128]),
                 [s], [self.w2b])
            for tg in range(8):
                s = self.rot("st", st)
                P.dma("sp", s, s[0:64, :].rearrange("p (t j) -> p t j", j=256), self.w1,
                      self.w1[kvi, tg * 256:(tg + 1) * 256, :].rearrange("(t d) j -> d t j", d=64))
                P.op("pool", lambda e, s=s: e.tensor_copy(out=self.w1b[:].rearrange("p t j -> p (t j)"),
                                                          in_=s[0:64, :]), [s], [self.w1b])
                self.compress_rhs(kvi, tg)
                for tl in range(4):
                    t = tg * 4 + tl
                    for jc in range(2):
                        P.op("pe", lambda e, tl=tl, jc=jc, t=t: e.matmul(
                            H[jc][:, :], lhsT=self.w1b[:, tl, jc * 128:(jc + 1) * 128], rhs=self.blkb[:, tl, :],
                            start=(t == 0), stop=(t == 31)), [self.w1b, self.blkb], [H[jc]])
            for jc in range(2):
                P.op("act", lambda e, jc=jc: e.activation(out=self.gx[:], in_=H[jc][:, :], func=AF.Copy),
                     [H[jc]], [self.gx])
                P.op("pool", lambda e: e.tensor_tensor(out=self.gy[:], in0=self.gx[:], in1=self.gx[:], op=ALU.mult),
                     [self.gx], [self.gy])
                P.op("dve", lambda e: e.tensor_scalar(out=self.gy[:], in0=self.gy[:], scalar1=0.044715, scalar2=1.0,
                                                      op0=ALU.mult, op1=ALU.add), [self.gy], [self.gy])
                P.op("pool", lambda e: e.tensor_tensor(out=self.gy[:], in0=self.gy[:], in1=self.gx[:], op=ALU.mult),
                     [self.gy, self.gx], [self.gy])
                P.op("act", lambda e: e.activation(out=self.gy[:], in_=self.gy[:], func=AF.Sigmoid,
                                                   scale=1.5957691216057308), [self.gy], [self.gy])
                P.op("dve", lambda e, jc=jc: e.tensor_tensor(out=self.gT[jc][:], in0=self.gx[:], in1=self.gy[:],
                                                             op=ALU.mult), [self.gx, self.gy], [self.gT[jc]])
            if kvi == 0:
                for jc in range(2):
                    P.op("pe", lambda e, jc=jc: e.matmul(self.X1[0:64, :], lhsT=self.w2b[:, jc, :], rhs=self.gT[jc][:],
                                                         start=(jc == 0), stop=(jc == 1)),
                         [self.w2b, self.gT[jc]], [self.X1])
                s = self.rot("st", st)
                P.op("act", lambda e, s=s: e.activation(out=s[0:64, 0:512], in_=self.X1[0:64, :], func=AF.Copy),
                     [self.X1], [s])
                self.norm64(s, s[0:64, 0:512], 512, self.hgs[0:64, 1:2], self.kcT, self.kcT[0:64, :])
            else:
                for cc in range(4):
                    for jc in range(2):
                        P.op("pe", lambda e, jc=jc, cc=cc: e.matmul(
                            self.X1[:, cc * 64:(cc + 1) * 64], lhsT=self.gT[jc][:, cc * 128:(cc + 1) * 128],
                            rhs=self.w2b[:, jc, :], start=(jc == 0), stop=(jc == 1)),
                            [self.w2b, self.gT[jc]], [self.X1])
                    P.op("act", lambda e, cc=cc: e.activation(out=self.vcaug[:, cc, 0:64],
                                                              in_=self.X1[:, cc * 64:(cc + 1) * 64], func=AF.Copy),
                         [self.X1], [self.vcaug])

    def combine(self, qt, h, br, first):
        P = self.P
        A = self.A[h]
        s0 = qt * QT
        gt = self.load_gate(qt, h, br)
        P.op("act", lambda e: e.activation(out=self.rdt[:], in_=A[64:128, :], func=AF.Ln, bias=self.tinyT[0:64, :]),
             [A, self.tinyT], [self.rdt])
        P.op("act", lambda e: e.activation(out=self.rdt[:], in_=self.rdt[:], func=AF.Exp, scale=-1.0),
             [self.rdt], [self.rdt])
        P.op("pool", lambda e: e.tensor_tensor(out=self.wt[:], in0=self.rdt[:], in1=gt[:], op=ALU.mult),
             [self.rdt, gt], [self.wt])
        if first:
            P.op("dve", lambda e: e.tensor_tensor(out=self.y[:, h, :], in0=A[0:64, :], in1=self.wt[:], op=ALU.mult),
                 [A, self.wt], [self.y])
        else:
            P.op("dve", lambda e: e.tensor_tensor(out=self.tm[:], in0=A[0:64, :], in1=self.wt[:], op=ALU.mult),
                 [A, self.wt], [self.tm])
            P.op("pool", lambda e: e.tensor_tensor(out=self.y[:, h, :], in0=self.y[:, h, :], in1=self.tm[:],
                                                   op=ALU.add), [self.y, self.tm], [self.y])

    def exp(self, S, pT, bias):
        P = self.P
        if bias is None:
            P.op("act", lambda e: e.activation(out=pT[:], in_=S[:, :], func=AF.Exp, scale=0.125), [S], [pT])
        else:
            bap, bt = bias
            P.op("act", lambda e: e.activation(out=pT[:], in_=S[:, :], func=AF.Exp, scale=0.125, bias=bap),
                 [S, bt], [pT])

    def cmp_bias(self, cc, h, mixed):
        return None if mixed else (self.cf[:, h:h + 1], self.cf)

    def sel_bias(self, kc, h, near):
        return None if near else (self.cf[:, h:h + 1], self.cf)

    def win_bias(self, c):
        return None

    def topc_ap(self, qt):
        return self.topc[qt]

    def load_q(self, qt):
        s0 = qt * QT
        self.P.dma("sp", self.qraw, self.qraw[:], self.qT, self.qT[:, s0:s0 + QT].rearrange("(h d) t -> d h t", d=64))

    def load_gate(self, qt, h, br):
        s0 = qt * QT
        gt = self.rot("gt", self.gt)
        g = self.gB[:]
        self.P.dma("sp", gt, gt[:], self.gB,
                   bass.AP(tensor=g.tensor, offset=(h * 3 + br) * SEQ + s0, ap=[[0, 64], [1, 512]]))
        return gt

    def store_y(self, qt):
        s0 = qt * QT
        self.P.dma("sp", self.yT, self.yT[:, s0:s0 + QT].rearrange("(h d) t -> d h t", d=64), self.y, self.y[:],
                   nodep_out=True)

    def qtile(self, qt):
        P = self.P
        s0 = qt * QT
        self.load_q(qt)
        P.dma("sp", self.tcs, self.tcs[:], self.topc, self.topc_ap(qt))
        for h in range(6):
            self.norm64(self.qraw, self.qraw[:, h, :], 512, self.hgs[0:64, 0:1], self.qn, self.qn[0:64, h, :])
        for h in range(3):
            for half in range(2):
                P.op("pool", lambda e, h=h, half=half: e.tensor_copy(out=self.qaug[0:64, h, half, :],
                                                                     in_=self.qn[0:64, h, :]), [self.qn], [self.qaug])
        for h in range(6):
            chunks = [cc for cc in range(4) if qt - 4 * cc >= 0]
            pts = []
            for cc in chunks:
                k = qt - 4 * cc
                mixed = k <= 5
                S = self.rot("S", self.S)
                P.op("pe", lambda e, S=S, cc=cc, h=h, mixed=mixed: e.matmul(
                    S[:, :], lhsT=self.kcT[:, cc * 128:(cc + 1) * 128], rhs=self.qn[:, h, :], start=True,
                    stop=(not mixed)), [self.kcT, self.qn], [S])
                if mixed:
                    mo = 512 * (5 - k)
                    P.op("pe", lambda e, S=S, h=h, mo=mo: e.matmul(S[:, :], lhsT=self.ident[:],
                                                                   rhs=self.uslice(h, mo), start=False, stop=True),
                         [self.ident, self.Ubig], [S])
                pT = self.rot("pT", self.pT)
                self.exp(S, pT, self.cmp_bias(cc, h, mixed))
                pts.append(pT)
            n = len(chunks)
            if h < 3:
                for i, (cc, pT) in enumerate(zip(chunks, pts)):
                    P.op("pe", lambda e, pT=pT, i=i, cc=cc, h=h: e.matmul(
                        self.A[h][:, :], lhsT=self.vcaug[:, cc, :], rhs=pT[:], start=(i == 0), stop=(i == n - 1)),
                        [self.vcaug, pT], [self.A[h]])
            for ss in range(4):
                for i, (cc, pT) in enumerate(zip(chunks, pts)):
                    P.op("pe", lambda e, pT=pT, i=i, cc=cc, ss=ss: e.matmul(
                        self.X1[:, ss * 128:(ss + 1) * 128], lhsT=pT[:, ss * 128:(ss + 1) * 128], rhs=self.ovl[:, cc, :],
                        start=(i == 0), stop=(i == n - 1)), [pT, self.ovl], [self.X1])
            for ss in range(4):
                for i, pT in enumerate(pts):
                    P.op("pe", lambda e, pT=pT, i=i, ss=ss: e.matmul(
                        self.X0[:, ss:ss + 1], lhsT=pT[:, ss * 128:(ss + 1) * 128], rhs=self.ones[:, 0:1],
                        start=(i == 0), stop=(i == n - 1)), [pT, self.ones], [self.X0])
            P.op("dve", lambda e: e.tensor_scalar(out=self.rdq[:], in0=self.X0[:, 0:4], scalar1=1e-30, scalar2=None,
                                                  op0=ALU.max), [self.X0], [self.rdq])
            P.op("dve", lambda e: e.reciprocal(out=self.rdq[:], in_=self.rdq[:]), [self.rdq], [self.rdq])
            for ss in range(4):
                if h == 0:
                    P.op("dve", lambda e, ss=ss: e.tensor_scalar(
                        out=self.impacc[:, ss * 128:(ss + 1) * 128], in0=self.X1[:, ss * 128:(ss + 1) * 128],
                        scalar1=self.rdq[:, ss:ss + 1], scalar2=None, op0=ALU.mult), [self.X1, self.rdq], [self.impacc])
                else:
                    P.op("dve", lambda e, ss=ss: e.scalar_tensor_tensor(
                        out=self.impacc[:, ss * 128:(ss + 1) * 128], in0=self.X1[:, ss * 128:(ss + 1) * 128],
                        scalar=self.rdq[:, ss:ss + 1], in1=self.impacc[:, ss * 128:(ss + 1) * 128],
                        op0=ALU.mult, op1=ALU.add), [self.X1, self.rdq, self.impacc], [self.impacc])
            if h < 3:
                self.combine(qt, h, 0, True)
        for ss in range(4):
            tA = self.tcs[:, ss * 256: ss * 256 + 128]
            tB = self.tcs[:, ss * 256 + 128: ss * 256 + 256]
            P.op("dve", lambda e, ss=ss, tA=tA: e.tensor_tensor(out=self.sc[:], in0=self.impacc[:, ss * 128:(ss + 1) * 128],
                                                                in1=tA, op=ALU.mult), [self.impacc, self.tcs], [self.sc])
            P.op("dve", lambda e, tB=tB: e.tensor_tensor(out=self.sc[:], in0=self.sc[:], in1=tB, op=ALU.add),
                 [self.sc, self.tcs], [self.sc])
            P.op("dve", lambda e: e.max(out=self.mx[:, 0:8], in_=self.sc[:]), [self.sc], [self.mx])
            P.op("dve", lambda e: e.match_replace(out=self.sc2[:], in_to_replace=self.mx[:, 0:8], in_values=self.sc[:],
                                                  imm_value=-1e30), [self.sc, self.mx], [self.sc2])
            P.op("dve", lambda e: e.max(out=self.mx[:, 8:16], in_=self.sc2[:]), [self.sc2], [self.mx])
            P.op("dve", lambda e: e.tensor_scalar(out=self.mneg[:], in0=self.sc[:], scalar1=self.mx[:, 15:16], scalar2=1.0,
                                                  op0=ALU.is_ge, op1=ALU.subtract), [self.sc, self.mx], [self.mneg])
            P.op("pe", lambda e, ss=ss: e.transpose(self.XT[:, ss * 128:(ss + 1) * 128], self.mneg[:], self.ident[:]),
                 [self.mneg, self.ident], [self.XT])
        for h in range(3):
            for half in range(2):
                eng = "act" if (h + half) % 2 == 0 else "dve"
                if eng == "act":
                    P.op("act", lambda e, h=h, half=half: e.activation(out=self.qaug[64:128, h, half, :],
                                                                       in_=self.XT[half * 64:(half + 1) * 64, :],
                                                                       func=AF.Copy), [self.XT], [self.qaug])
                else:
                    P.op("dve", lambda e, h=h, half=half: e.tensor_copy(out=self.qaug[64:128, h, half, :],
                                                                        in_=self.XT[half * 64:(half + 1) * 64, :]),
                         [self.XT], [self.qaug])
        nkc = 4 * qt + 4

        def sel_qk(kc, h):
            k0 = 128 * kc
            ep = s0 + 511 - k0
            near = ep <= DS
            half = kc // 32
            S = self.rot("S4", self.S4)
            P.op("pe", lambda e: e.matmul(S[:, :], lhsT=self.ksaug[:, k0:k0 + 128], rhs=self.qaug[:, h, half, :],
                                          start=True, stop=(not near)), [self.ksaug, self.qaug], [S])
            if near:
                mo = DS - ep
                P.op("pe", lambda e: e.matmul(S[:, :], lhsT=self.ident[:], rhs=self.TTs[h][:, mo:mo + 512],
                                              start=False, stop=True), [self.ident, self.TTs[h]], [S])
            return (kc, h, near, S)

        def sel_rest(kc, h, near, S):
            pT = self.rot("pT", self.pT)
            self.exp(S, pT, self.sel_bias(kc, h, near))
            P.op("pe", lambda e: e.matmul(self.A[h][:, :], lhsT=self.vsaug[:, kc, :], rhs=pT[:],
                                          start=(kc == 0), stop=(kc == nkc - 1)), [self.vsaug, pT], [self.A[h]])

        pend = []
        for kc in range(nkc):
            for h in range(3):
                pend.append(sel_qk(kc, h))
                if len(pend) > PIPE:
                    sel_rest(*pend.pop(0))
        while pend:
            sel_rest(*pend.pop(0))
        for h in range(3):
            self.combine(qt, h, 1, False)
        rs = [r for r in range(8) if s0 - 512 + 128 * r >= 0]

        def win_qk(i, r, h):
            k0 = s0 - 512 + 128 * r
            mo = 128 * r
            S = self.rot("S4", self.S4)
            P.op("pe", lambda e: e.matmul(S[:, :], lhsT=self.kwT[:, k0:k0 + 128], rhs=self.qn[:, h, :],
                                          start=True, stop=False), [self.kwT, self.qn], [S])
            P.op("pe", lambda e: e.matmul(S[:, :], lhsT=self.ident[:], rhs=self.TTw[h][:, mo:mo + 512],
                                          start=False, stop=True), [self.ident, self.TTw[h]], [S])
            return (i, k0, h, S)

        def win_rest(i, k0, h, S):
            pT = self.rot("pT", self.pT)
            self.exp(S, pT, self.win_bias(k0 // 128))
            P.op("pe", lambda e: e.matmul(self.A[h][:, :], lhsT=self.vwaug[:, k0 // 128, :], rhs=pT[:],
                                          start=(i == 0), stop=(i == len(rs) - 1)), [self.vwaug, pT], [self.A[h]])

        pend = []
        for i, r in enumerate(rs):
            for h in range(3):
                pend.append(win_qk(i, r, h))
                if len(pend) > PIPE:
                    win_rest(*pend.pop(0))
        while pend:
            win_rest(*pend.pop(0))
        for h in range(3):
            self.combine(qt, h, 2, False)
        self.store_y(qt)

    def build(self, nqt=NQT):
        self.setup()
        for qt in range(nqt):
            self.qtile(qt)
        self.P.wait_all("sp", [self.yT])
        self.P.finalize()


def build_attn(nqt=NQT):
    nc = bass.Bass("TRN2", target_bir_lowering=False)
    P = Prog(nc)
    a = Attn(P)
    a.build(nqt)
    return nc, P


XPAD = SEQ + HALO


def rev_ap(ap, n):
    dims = [list(d) for d in ap.ap]
    dims[-1] = [-1, n]
    return bass.AP(tensor=ap.tensor, offset=ap.offset + (n - 1), ap=dims)


class FTok(TokStage):
    def __init__(self, P, mode, dr):
        self.P = P
        self.mode = mode
        self.dr = dr
        for k in ("memT", "gains", "hg", "mem_w_kv", "w_out", "w_up", "w_down", "b_w_in", "a_w_in", "convw", "w_kv"):
            setattr(self, k, dr[k])
        self.xsrc = dr["xin"] if mode == "L1" else dr["X"]
        self.kvT_o, self.qT_o, self.gT_o = dr["KV"], dr["Q"], dr["G"]
        self.ytokT = dr["Y"]
        self.chunk = 0
        self.alloc()

    def halo_fix(self, sub):
        if self.chunk == 0:
            self.P.op("pool", lambda e: e.memset(self.v[:, sub, 2:2 + HALO], 0.0), [], [self.v])

    def store_tile(self, dstT, r, msz, c0, t0, w, ost):
        a = c0 + t0
        lo = HALO if a == 0 else 0
        g0 = NTOK * self.chunk + a + lo - HALO
        self.P.dma("sp", dstT, dstT[r:r + msz, g0:g0 + (w - lo)], ost, ost[0:msz, lo:w], nodep_out=True)

    def ytok_src(self, ch, c0, t0, w):
        g0 = NTOK * self.chunk + c0 + t0
        return self.ytokT[ch * 128:(ch + 1) * 128, g0:g0 + w]

    def setup_consts(self):
        P = self.P
        P.dma("sp", self.g, self.g[:], self.gains, self.gains[:])
        P.dma("sp", self.hgs, self.hgs[:], self.hg, self.hg[:])
        if self.mode == "L1":
            P.dma("sp", self.cw, self.cw[:], self.convw, self.convw[:])
        for c in range(8):
            P.dma("sp", self.u, self.memx(c, 0, 256), self.memT, self.memT[c * 128:(c + 1) * 128, :], nodep_out=True)
        P.op("pool", lambda e: e.memset(self.ones[:], 1.0), [], [self.ones])
        P.op("pool", lambda e: e.memset(self.bd[:], 0.0), [], [self.bd])
        P.op("pool", lambda e: e.memset(self.bd[0:64, 0:64], 1.0), [], [self.bd])
        P.op("pool", lambda e: e.memset(self.bd[64:128, 64:128], 1.0), [], [self.bd])
        P.op("pool", lambda e: e.memset(self.epsT[:], EPS), [], [self.epsT])
        P.op("pool", lambda e: e.memset(self.vaug[:], 1.0), [], [self.vaug])
        self.norm(self.u, self.memx, 256, [(0, 256)], 15, self.memh, 0)

    def load_x(self):
        P = self.P
        g0 = NTOK * self.chunk
        for c in range(8):
            P.dma("sp", self.x, self.x[:, c, :], self.xsrc, self.xsrc[c * 128:(c + 1) * 128, g0:g0 + NCOL],
                  nodep_out=(c > 0))

    def store_x(self):
        P = self.P
        dst = self.dr["out"] if self.mode == "L5" else self.dr["X"]
        off = 0 if self.mode == "L5" else HALO
        g0 = NTOK * self.chunk + off
        for c in range(8):
            P.dma("sp", dst, dst[c * 128:(c + 1) * 128, g0:g0 + NTOK], self.x, self.x[:, c, HALO:NCOL], nodep_out=True)

    def run(self):
        self.setup_consts()
        for chunk in range(4):
            self.chunk = chunk
            self.load_x()
            if self.mode == "L1":
                for l in range(2):
                    self.mem_kv(l)
                    for ri, (c0, n) in enumerate(RANGES):
                        tiles = tiles_of(n)
                        self.norm(self.x, self.xs(c0), n, tiles, l, self.h, 0)
                        self.mix_a(l, ri, c0, n, tiles)
                        self.mem_attn(l, n, tiles)
                        self.out_proj(l, c0, n, tiles)
                        self.mlp(l, c0, n, tiles)
                for ri, (c0, n) in enumerate(RANGES):
                    tiles = tiles_of(n)
                    self.norm(self.x, self.xs(c0), n, tiles, 8, self.h, 0)
                    for i in range(3):
                        self.store_proj(self.w_kv, self.w_kv[:, 256 * i:256 * (i + 1)], 256, self.kvT_o, 256 * i, c0, tiles)
                    if not getattr(self, "skip_pre0", False):
                        self.pre_b(0, c0, n, tiles)
            else:
                j = 0 if self.mode == "L3" else 1
                self.mem_kv(2 + j)
                for ri, (c0, n) in enumerate(RANGES):
                    self.post_b(j, c0, n, tiles_of(n))
                if j == 0:
                    for ri, (c0, n) in enumerate(RANGES):
                        self.pre_b(1, c0, n, tiles_of(n))
            self.store_x()


class FAttn(Attn):
    def __init__(self, P, dr, jl):
        self.P = P
        self.dr = dr
        self.jl = jl
        self.Q, self.G, self.KV, self.Y = dr["Q"], dr["G"], dr["KV"], dr["Y"]
        self.hg = dr["ahg%d" % jl]
        self.pos, self.w1, self.w2 = dr["pos"], dr["cmp_w1"], dr["cmp_w2"]
        self.rel = dr["rel_bias"]
        self.OH, self.identD, self.EE, self.ovlD, self.topc = dr["OH"], dr["ident"], dr["EE"], dr["ovl"], dr["topc"]
        self.Rrow = dr["Rrow"]
        self.alloc()

    def alloc(self):
        Attn.alloc(self)
        self.ub_view = self.Ubig[0:64, 0:16 * 513].rearrange("p (r n) -> p r n", n=513)

    def set_combo(self, g, tr):
        self.g, self.tr = g, tr
        own = [3 * tr + i for i in range(3)]
        oth = [3 * (1 - tr) + i for i in range(3)]
        self.heads = own + oth

    def load_rb(self):
        P = self.P
        rel = self.rel[:]
        for i, hh in enumerate(self.heads):
            col = self.g * 6 + hh
            P.dma("sp", self.cf, self.cf[:, i:i + 1], self.rel,
                  bass.AP(tensor=rel.tensor, offset=31 * 12 + col, ap=[[0, 128], [1, 1]]))
            P.dma("sp", self.tab, self.tab[0:32, i:i + 1], self.rel, self.rel[:, col:col + 1],
                  allow_slow_non_contiguous=True)

    def load_kv_piece(self, s, slot, c0):
        r0 = slot * 128 + self.g * 64
        self.P.dma("sp", s, s[0:64, :], self.KV, self.KV[r0:r0 + 64, c0:c0 + 1024])

    def load_v_piece(self, piece):
        P = self.P
        c0 = piece * 1024
        for slot, dst in ((3, self.vsaug), (5, self.vwaug)):
            r0 = slot * 128 + self.g * 64
            for hf in range(2):
                sq = self.rot("sq", self.sq)
                a = c0 + hf * 512
                P.dma("pool", sq, sq[0:64, :], self.KV, self.KV[r0:r0 + 64, a:a + 512])
                for c in range(4):
                    P.op("pe", lambda e, sq=sq, c=c: e.transpose(self.XT[:, c * 64:(c + 1) * 64],
                                                                 sq[0:64, c * 128:(c + 1) * 128], self.ident[0:64, 0:64]),
                         [sq, self.ident], [self.XT])
                ch0 = piece * 8 + hf * 4
                P.op("act", lambda e, dst=dst, ch0=ch0: e.activation(
                    out=dst[:, ch0:ch0 + 4, 0:64], in_=self.XT[:, 0:256].rearrange("p (c d) -> p c d", d=64),
                    func=AF.Copy), [self.XT], [dst])

    def compress_begin(self, kvi):
        P = self.P
        P.op("pool", lambda e: e.memset(self.ub_view[:, :, 512:513], 0.0), [], [self.Ubig])
        for piece in range(8):
            s = self.rot("st", self.st)
            self.load_kv_piece(s, kvi, piece * 1024)
            P.op("pool", lambda e, s=s, piece=piece: e.tensor_copy(
                out=self.ub_view[:, :, piece * 64:(piece + 1) * 64],
                in_=s[0:64, :].rearrange("p (n r) -> p r n", r=16)), [s], [self.Ubig])

    def compress_rhs(self, kvi, tg):
        P = self.P
        for tl in range(4):
            t = tg * 4 + tl
            r, off = t % 16, t // 16
            P.op("dve", lambda e, tl=tl, t=t, r=r, off=off: e.tensor_scalar(
                out=self.blkb[:, tl, :], in0=self.ub_view[:, r, off:off + 512], scalar1=self.posT[:, t:t + 1],
                scalar2=None, op0=ALU.add), [self.Ubig, self.posT], [self.blkb])

    def load_q(self, qt):
        P = self.P
        s0 = qt * QT
        for i2 in range(3):
            s = self.rot("st", self.st)
            for k in range(2):
                hh = self.heads[2 * i2 + k]
                r0 = (self.g * 6 + hh) * 64
                P.dma("sp", s, s[0:64, k * 512:(k + 1) * 512], self.Q, self.Q[r0:r0 + 64, s0:s0 + QT], nodep_out=(k > 0))
            for k in range(2):
                P.op("dve", lambda e, s=s, k=k, i2=i2: e.tensor_copy(
                    out=self.qraw[:, 2 * i2 + k, :], in_=rev_ap(s[0:64, k * 512:(k + 1) * 512], 512)), [s], [self.qraw])

    def load_gate(self, qt, h, br):
        P = self.P
        s0 = qt * QT
        gt = self.rot("gt", self.gt)
        g = self.G[:]
        row = (self.g * 6 + 3 * self.tr + h) * 3 + br
        P.dma("sp", gt, gt[:], self.G, bass.AP(tensor=g.tensor, offset=row * SEQ + s0, ap=[[0, 64], [1, 512]]))
        gr = self.rot("gtr", self.gtr)
        P.op("dve", lambda e: e.tensor_copy(out=gr[:], in_=rev_ap(gt[:], 512)), [gt], [gr])
        return gr

    def store_y(self, qt):
        P = self.P
        s0 = qt * QT
        bufs = [self.rdt, self.wt, self.tm]
        for h in range(3):
            b = bufs[h]
            P.op("dve", lambda e, b=b, h=h: e.tensor_copy(out=b[:], in_=rev_ap(self.y[:, h, :], 512)), [self.y], [b])
            r0 = (self.g * 6 + 3 * self.tr + h) * 64
            P.dma("sp", self.Y, self.Y[r0:r0 + 64, HALO + s0:HALO + s0 + QT], b, b[:], nodep_out=True)

    def run(self):
        P = self.P
        st = self.st
        self.gtr = [P.sb("gtr%d" % i, [64, 512], F32) for i in range(2)]
        P.dma("sp", self.hgs, self.hgs[:], self.hg, self.hg[:])
        P.op("pool", lambda e: e.memset(self.ones[:], 1.0), [], [self.ones])
        P.op("pool", lambda e: e.memset(self.epsT[:], 1e-6), [], [self.epsT])
        P.op("pool", lambda e: e.memset(self.tinyT[:], 1e-30), [], [self.tinyT])
        P.op("pool", lambda e: e.memset(self.vsaug[:], 1.0), [], [self.vsaug])
        P.op("pool", lambda e: e.memset(self.vwaug[:], 1.0), [], [self.vwaug])
        P.op("pool", lambda e: e.memset(self.vcaug[:], 1.0), [], [self.vcaug])
        s = self.rot("st", st)
        P.dma("sp", s, s[:, 0:128], self.identD, self.identD[:])
        P.op("pool", lambda e, s=s: e.tensor_copy(out=self.ident[:], in_=s[:, 0:128]), [s], [self.ident])
        s = self.rot("st", st)
        P.dma("sp", s, s[:, 0:512], self.ovlD, self.ovlD[:].rearrange("p c j -> p (c j)"))
        P.op("pool", lambda e, s=s: e.tensor_copy(out=self.ovl[:].rearrange("p c j -> p (c j)"), in_=s[:, 0:512]),
             [s], [self.ovl])
        for g in range(2):
            self.set_combo(g, 0)
            P.barrier()
            self.setup_kv()
            for tr in range(2):
                self.set_combo(g, tr)
                P.barrier()
                self.setup_tables()
                for qt in range(NQT):
                    self.qtile(qt)


def fused_dram(P, r3=False):
    dr = {}
    D = lambda n, s: P.dram(n, s, F32, kind="ExternalInput")
    dr["xin"] = D("xin", [1024, XPAD])
    dr["memT"] = D("memT", [1024, 256])
    dr["gains"] = D("gains", [128, 128])
    dr["hg"] = D("hg", [128, 16])
    dr["convw"] = D("convw", [128, 36])
    dr["mem_w_kv"] = D("mem_w_kv", [4, 1024, 512])
    dr["w_out"] = D("w_out", [4, 1024, 1024])
    dr["w_up"] = D("w_up", [4, 1024, 4096])
    dr["w_down"] = D("w_down", [4, 4096, 1024])
    dr["b_w_in"] = D("b_w_in", [2, 1024, 1060])
    dr["a_w_in"] = D("a_w_in", [2, 1024, 2560])
    dr["w_kv"] = D("w_kv", [1024, 768])
    dr["ahg0"] = D("ahg0", [128, 8])
    dr["ahg1"] = D("ahg1", [128, 8])
    dr["pos"] = D("pos", [2, 64, 32])
    dr["cmp_w1"] = D("cmp_w1", [2, 2048, 256])
    dr["cmp_w2"] = D("cmp_w2", [2, 256, 64])
    dr["rel_bias"] = D("rel_bias", [32, 12])
    dr["OH"] = D("OH", [33, LTOT])
    dr["ident"] = D("ident", [128, 128])
    dr["EE"] = D("EE", [64, SEQ])
    dr["ovl"] = D("ovl", [128, 4, 128])
    if not r3:
        dr["topc"] = D("topc", [NQT, 128, 1024])
    if not r3:
        dr["out"] = P.dram("xT_out", [1024, SEQ], F32, kind="ExternalOutput")
    I = lambda n, s: P.dram(n, s, F32, kind="Internal")
    dr["X"] = I("Xs", [1024, XPAD])
    if not r3:
        dr["KV"] = I("KVs", [768, SEQ])
    dr["Q"] = I("Qs", [768, SEQ])
    dr["G"] = I("Gs", [36, SEQ])
    dr["Y"] = I("Ys", [768, XPAD])
    dr["Rrow"] = I("Rrow", [6, LTOT])
    return dr


def build_fused():
    nc = bass.Bass("TRN2", target_bir_lowering=False)
    P = Prog(nc)
    dr = fused_dram(P)
    P.push_scope()
    FTok(P, "L1", dr).run()
    P.pop_scope()
    for j in range(2):
        P.push_scope()
        FAttn(P, dr, j).run()
        P.pop_scope()
        P.push_scope()
        FTok(P, "L3" if j == 0 else "L5", dr).run()
        P.pop_scope()
    P.wait_all("sp", [dr["out"]])
    P.finalize()
    return nc, P


KPAD = 1536
QTILES = (3, 7, 11, 15)


def host_core_tables(k):
    tc = np.zeros((4, 128, 4, 2, 128), np.float32)
    jj = np.arange(128)[None, :]
    for j, qt in enumerate(QTILES):
        for ss in range(4):
            ta = qt * QT + 511 - (128 * ss + np.arange(128))
            back = (ta // 64)[:, None] - jj
            J = jj - 24 + 8 * k
            valid = (J >= 0) & (back >= 0)
            forced = (J == 0) | ((back >= 0) & (back < 2))
            tc[j, :, ss, 0] = (valid & ~forced).astype(np.float32)
            tc[j, :, ss, 1] = np.where(valid, np.where(forced, 1e4, 0.0), -1e30).astype(np.float32)
    kval = np.zeros((128, 16), np.float32)
    first = KPAD - 512 * k
    for c in range(12):
        a = 128 * c + np.arange(128)
        kval[:, c] = np.where(a >= first, 0.0, MASKV)
    kval[:, 12] = np.where(16 * np.arange(128) >= first, 0.0, MASKV)
    return tc.reshape(4, 128, 1024), kval


class FTok2(FTok):
    def __init__(self, P, mode, dr):
        FTok.__init__(self, P, mode, dr)
        if mode != "L1":
            self.xsrc = dr["Xown"]
            self.ytokT = dr["Yown"]
            self.qT_o, self.gT_o = dr["Qown"], dr["Gown"]

    def conv_lead(self, l, ri, sub, ch, n):
        P = self.P
        cr = self.carry2[l]
        if self.chunk == 0 and ri == 0:
            P.op("pool", lambda e: e.memset(self.v[:, sub, 0:2], 0.0), [], [self.v])
        else:
            P.op("dve", lambda e: e.tensor_copy(out=self.v[:, sub, 0:2], in_=cr[:, ch, :]), [cr], [self.v])
        P.op("dve", lambda e: e.tensor_copy(out=cr[:, ch, :], in_=self.v[:, sub, n:n + 2]), [self.v], [cr])

    def store_tile(self, dstT, r, msz, c0, t0, w, ost):
        if self.mode == "L1":
            off = KPAD if dstT is self.dr["KV"] else 0
            g0 = NTOK * self.chunk + c0 + t0 + off
            self.P.dma("sp", dstT, dstT[r:r + msz, g0:g0 + w], ost, ost[0:msz, 0:w], nodep_out=True)
        else:
            self.P.dma("sp", dstT, dstT[r:r + msz, c0 + t0:c0 + t0 + w], ost, ost[0:msz, 0:w], nodep_out=True)

    def ytok_src(self, ch, c0, t0, w):
        if self.mode == "L1":
            return FTok.ytok_src(self, ch, c0, t0, w)
        return self.ytokT[ch * 128:(ch + 1) * 128, c0 + t0:c0 + t0 + w]

    def load_x(self):
        P = self.P
        if self.mode == "L1":
            g0 = NTOK * self.chunk + HALO
            for c in range(8):
                P.dma("sp", self.x, self.x[:, c, 0:NTOK], self.xsrc, self.xsrc[c * 128:(c + 1) * 128, g0:g0 + NTOK],
                      nodep_out=(c > 0))
            return
        for c in range(8):
            P.dma("sp", self.x, self.x[:, c, 0:NTOK], self.xsrc, self.xsrc[c * 128:(c + 1) * 128, :], nodep_out=(c > 0))

    def store_x(self):
        P = self.P
        if self.mode == "L1":
            X = self.dr["X"]
            g0 = NTOK * self.chunk + HALO
            for c in range(8):
                P.dma("sp", X, X[c * 128:(c + 1) * 128, g0:g0 + NTOK], self.x, self.x[:, c, 0:NTOK], nodep_out=True)
            return
        dst = self.dr["out"] if self.mode == "L5" else self.dr["Xown"]
        for c in range(8):
            P.dma("sp", dst, dst[c * 128:(c + 1) * 128, :], self.x, self.x[:, c, 0:NTOK], nodep_out=True)

    def zero_kv_pad(self):
        P = self.P
        z = self.ost[0]
        P.op("pool", lambda e: e.memset(z[:], 0.0), [], [z])
        KV = self.dr["KV"]
        for r in range(6):
            for cb in range(3):
                P.dma("sp", KV, KV[r * 128:(r + 1) * 128, cb * 512:(cb + 1) * 512], z, z[:], nodep_out=True)

    def run(self):
        if self.mode == "L1":
            self.carry2 = [self.P.sb("carry2_%d" % l, [128, 6, 2], F32) for l in range(2)]
            self.zero_kv_pad()
            self.setup_consts()
            R2 = [(0, 1024), (1024, 1024)]
            for chunk in range(4):
                self.chunk = chunk
                self.load_x()
                for l in range(2):
                    self.mem_kv(l)
                    for ri, (c0, n) in enumerate(R2):
                        tiles = tiles_of(n)
                        self.norm(self.x, self.xs(c0), n, tiles, l, self.h, 0)
                        self.mix_a(l, ri, c0, n, tiles)
                        self.mem_attn(l, n, tiles)
                        self.out_proj(l, c0, n, tiles)
                        self.mlp(l, c0, n, tiles)
                for ri, (c0, n) in enumerate(R2):
                    tiles = tiles_of(n)
                    self.norm(self.x, self.xs(c0), n, tiles, 8, self.h, 0)
                    for i in range(3):
                        self.store_proj(self.w_kv, self.w_kv[:, 256 * i:256 * (i + 1)], 256, self.kvT_o, 256 * i, c0, tiles)
                self.store_x()
            return
        self.setup_consts()
        self.chunk = 0
        self.load_x()
        if self.mode == "P0":
            for (c0, n) in [(0, 1024), (1024, 1024)]:
                self.pre_b(0, c0, n, tiles_of(n))
            return
        j = 0 if self.mode == "L3" else 1
        self.mem_kv(2 + j)
        R2 = [(0, 1024), (1024, 1024)]
        for (c0, n) in R2:
            self.post_b(j, c0, n, tiles_of(n))
        if j == 0:
            for (c0, n) in R2:
                self.pre_b(1, c0, n, tiles_of(n))
        self.store_x()


def extract_own(P, dr):
    kd = dr["kd"]

    def dyn3(src_t, r0, r1, pad, ncols):
        base = src_t[r0:r1, pad:pad + KPAD + 6144 + 512][:, bass.ds(kd, 6144 + 512)]
        return bass.AP(tensor=base.tensor, offset=base.offset, ap=[[ncols, r1 - r0], [2048, 4], [1, 512]])
    KVp, KVw = dr["KV"], dr["KVwin"]
    for rb in range(3):
        P.dma("sp", KVw, KVw[rb * 256:(rb + 1) * 256, :], KVp,
              KVp[rb * 256:(rb + 1) * 256, 0:KPAD + SEQ][:, bass.ds(kd, SEQ)], nodep_out=True)
    for rb in range(4):
        P.dma("sp", dr["Xown"], dr["Xown"][rb * 256:(rb + 1) * 256, :].rearrange("r (s c) -> r s c", c=512), dr["X"],
              dyn3(dr["X"], rb * 256, (rb + 1) * 256, HALO, XPAD), nodep_out=True)


class FAttn2(FAttn):
    def __init__(self, P, dr, jl):
        FAttn.__init__(self, P, dr, jl)
        self.KV = dr["KVwin"]
        self.Q, self.G, self.Y = dr["Qown"], dr["Gown"], dr["Yown"]
        self.kvalD = dr["kval"]

    def load_q(self, qt):
        P = self.P
        c0 = ((qt - 3) // 4) * 512
        for i2 in range(3):
            s = self.rot("st", self.st)
            for k in range(2):
                hh = self.heads[2 * i2 + k]
                r0 = (self.g * 6 + hh) * 64
                P.dma("sp", s, s[0:64, k * 512:(k + 1) * 512], self.Q, self.Q[r0:r0 + 64, c0:c0 + 512], nodep_out=(k > 0))
            for k in range(2):
                P.op("dve", lambda e, s=s, k=k, i2=i2: e.tensor_copy(
                    out=self.qraw[:, 2 * i2 + k, :], in_=rev_ap(s[0:64, k * 512:(k + 1) * 512], 512)), [s], [self.qraw])

    def load_gate(self, qt, h, br):
        P = self.P
        c0 = ((qt - 3) // 4) * 512
        gt = self.rot("gt", self.gt)
        g = self.G[:]
        row = (self.g * 6 + 3 * self.tr + h) * 3 + br
        P.dma("sp", gt, gt[:], self.G, bass.AP(tensor=g.tensor, offset=row * NTOK + c0, ap=[[0, 64], [1, 512]]))
        gr = self.rot("gtr", self.gtr)
        P.op("dve", lambda e: e.tensor_copy(out=gr[:], in_=rev_ap(gt[:], 512)), [gt], [gr])
        return gr

    def store_y(self, qt):
        P = self.P
        c0 = ((qt - 3) // 4) * 512
        bufs = [self.rdt, self.wt, self.tm]
        for h in range(3):
            b = bufs[h]
            P.op("dve", lambda e, b=b, h=h: e.tensor_copy(out=b[:], in_=rev_ap(self.y[:, h, :], 512)), [self.y], [b])
            r0 = (self.g * 6 + 3 * self.tr + h) * 64
            P.dma("sp", self.Y, self.Y[r0:r0 + 64, c0:c0 + 512], b, b[:], nodep_out=True)

    def topc_ap(self, qt):
        return self.topc[(qt - 3) // 4]

    def cmp_bias(self, cc, h, mixed):
        if cc == 0:
            return (self.kval[:, 12:13], self.kval) if mixed else (self.cfc[:, h:h + 1], self.cfc)
        return None if mixed else (self.cf[:, h:h + 1], self.cf)

    def sel_bias(self, kc, h, near):
        if kc < 12:
            return (self.kval[:, kc:kc + 1], self.kval) if near else (self.vbf[:, kc, h:h + 1], self.vbf)
        return None if near else (self.cf[:, h:h + 1], self.cf)

    def win_bias(self, c):
        return (self.kval[:, c:c + 1], self.kval) if c < 12 else None

    def setup_tables(self):
        FAttn.setup_tables(self)
        P = self.P
        for c in range(12):
            P.op("dve", lambda e, c=c: e.tensor_scalar(out=self.vbf[:, c, :], in0=self.cf[:, 0:3],
                                                       scalar1=self.kval[:, c:c + 1], scalar2=None, op0=ALU.add),
                 [self.cf, self.kval], [self.vbf])
        P.op("dve", lambda e: e.tensor_scalar(out=self.cfc[:], in0=self.cf[:], scalar1=self.kval[:, 12:13], scalar2=None,
                                              op0=ALU.add), [self.cf, self.kval], [self.cfc])

    def run(self):
        P = self.P
        st = self.st
        self.gtr = [P.sb("gtr%d" % i, [64, 512], F32) for i in range(2)]
        self.kval = P.sb("kval", [128, 16], F32)
        self.vbf = P.sb("vbf", [128, 12, 3], F32)
        self.cfc = P.sb("cfc", [128, 6], F32)
        P.dma("sp", self.kval, self.kval[:], self.kvalD, self.kvalD[:])
        P.dma("sp", self.hgs, self.hgs[:], self.hg, self.hg[:])
        P.op("pool", lambda e: e.memset(self.ones[:], 1.0), [], [self.ones])
        P.op("pool", lambda e: e.memset(self.epsT[:], 1e-6), [], [self.epsT])
        P.op("pool", lambda e: e.memset(self.tinyT[:], 1e-30), [], [self.tinyT])
        P.op("pool", lambda e: e.memset(self.vsaug[:], 1.0), [], [self.vsaug])
        P.op("pool", lambda e: e.memset(self.vwaug[:], 1.0), [], [self.vwaug])
        P.op("pool", lambda e: e.memset(self.vcaug[:], 1.0), [], [self.vcaug])
        s = self.rot("st", st)
        P.dma("sp", s, s[:, 0:128], self.identD, self.identD[:])
        P.op("pool", lambda e, s=s: e.tensor_copy(out=self.ident[:], in_=s[:, 0:128]), [s], [self.ident])
        s = self.rot("st", st)
        P.dma("sp", s, s[:, 0:512], self.ovlD, self.ovlD[:].rearrange("p c j -> p (c j)"))
        P.op("pool", lambda e, s=s: e.tensor_copy(out=self.ovl[:].rearrange("p c j -> p (c j)"), in_=s[:, 0:512]),
             [s], [self.ovl])
        for z in (self.kwT, self.kcT):
            P.op("pool", lambda e, z=z: e.memset(z[64:128, :], 0.0), [], [z])
        P.op("pool", lambda e: e.memset(self.qn[64:128, :, :], 0.0), [], [self.qn])
        kvt = [("ksaug", self.ksaug, [128, SEQ]), ("kwT", self.kwT, [64, SEQ]), ("vsaug", self.vsaug, [128, 64 * 128]),
               ("vwaug", self.vwaug, [128, 64 * 128]), ("kcT", self.kcT, [64, 512]), ("vcaug", self.vcaug, [128, 4 * 128])]
        for g in range(2):
            self.set_combo(g, 0)
            if self.jl == 0:
                self.setup_kv()
                for nm, tl, shp in kvt:
                    key = "kvc_%s_%d" % (nm, g)
                    if key not in self.dr:
                        self.dr[key] = P.dram(key, shp, BF16, kind="Internal")
                    c = self.dr[key]
                    src = tl[0:shp[0], :] if len(tl[:].shape) == 2 else tl[:].rearrange("p a b -> p (a b)")
                    P.dma("sp", c, c[:], tl, src)
            else:
                for nm, tl, shp in kvt:
                    c = self.dr["kvc_%s_%d" % (nm, g)]
                    dst = tl[0:shp[0], :] if len(tl[:].shape) == 2 else tl[:].rearrange("p a b -> p (a b)")
                    P.dma("sp", tl, dst, c, c[:])
            for tr in range(2):
                self.set_combo(g, tr)
                self.setup_tables()
                for qt in QTILES:
                    self.qtile(qt)


def build_fused2():
    nc = bass.Bass("TRN2", target_bir_lowering=False)
    P = Prog(nc)
    dr = fused_dram(P, r3=True)
    I = lambda n, s: P.dram(n, s, F32, kind="Internal")
    dr["KV"] = I("KVp", [768, KPAD + SEQ])
    dr["KVwin"] = I("KVwin", [768, SEQ])
    dr["Xown"] = I("Xown", [1024, NTOK])
    dr["Qown"] = I("Qown", [768, NTOK])
    dr["Gown"] = I("Gown", [36, NTOK])
    dr["Yown"] = I("Yown", [768, NTOK])
    dr["topc"] = P.dram("topc4", [4, 128, 1024], F32, kind="ExternalInput")
    dr["kval"] = P.dram("kval", [128, 16], F32, kind="ExternalInput")
    dr["out"] = P.dram("xT_own", [1024, NTOK], F32, kind="ExternalOutput")
    dr["kd"] = (nc.partition_id() % 4) * 512
    P.push_scope()
    FTok2(P, "L1", dr).run()
    P.pop_scope()
    extract_own(P, dr)
    P.push_scope()
    FTok2(P, "P0", dr).run()
    P.pop_scope()
    for j in range(2):
        P.push_scope()
        FAttn2(P, dr, j).run()
        P.pop_scope()
        P.push_scope()
        FTok2(P, "L3" if j == 0 else "L5", dr).run()
        P.pop_scope()
    P.wait_all("sp", [dr["out"]])
    P.finalize()
    return nc, P


from concourse.bass_utils import run_bass_kernel_spmd


def attn_hg(inp, jlayer):
    hg = np.zeros((128, 8), np.float32)
    hg[:, 0] = np.tile(inp["b_q_gain"][jlayer], 2)
    for i in range(3):
        hg[:, 1 + i] = np.tile(inp["k_gain"][i], 2)
    return hg


def kernel(**inputs):
    inp = {k: np.ascontiguousarray(np.asarray(v, dtype=np.float32)) for k, v in inputs.items()}
    common = dict(host_consts())
    del common["topc"]
    common.update(gains=host_gains(inp), hg=host_hg(inp), convw=host_convw(inp), mem_w_kv=inp["mem_w_kv"],
                  w_out=inp["w_out"], w_up=inp["w_up"], w_down=inp["w_down"], b_w_in=inp["b_w_in"],
                  a_w_in=inp["a_w_in"], w_kv=inp["w_kv_shared"], ahg0=attn_hg(inp, 0), ahg1=attn_hg(inp, 1),
                  pos=np.ascontiguousarray(inp["cmp_pos"].transpose(0, 2, 1)), cmp_w1=inp["cmp_w1"],
                  cmp_w2=inp["cmp_w2"], rel_bias=inp["rel_bias"])
    per_b = []
    for b in range(2):
        xin = np.zeros((1024, XPAD), np.float32)
        xin[:, HALO:] = inp["x"][b].T
        per_b.append(dict(xin=xin, memT=np.ascontiguousarray(inp["mem"][b].T)))
    per_k = []
    for k in range(4):
        tc, kval = host_core_tables(k)
        per_k.append(dict(topc4=tc, kval=kval))
    maps = []
    for c in range(8):
        m = dict(common)
        m.update(per_b[c // 4])
        m.update(per_k[c % 4])
        maps.append(m)
    nc = build_fused2()[0]
    r = run_bass_kernel_spmd(nc, maps, core_ids=list(range(8))).results
    out = np.zeros((2, SEQ, 1024), np.float32)
    for c in range(8):
        b, k = c // 4, c % 4
        o = r[c]["xT_own"]
        for j in range(4):
            t0 = 512 * (4 * j + k)
            out[b, t0:t0 + 512, :] = o[:, j * 512:(j + 1) * 512].T
    return out
```

```python
import numpy as np
import concourse.bass as bass
import concourse.mybir as mybir
from contextlib import ExitStack

F32 = mybir.dt.float32
BF16 = mybir.dt.bfloat16
AF = mybir.ActivationFunctionType
ALU = mybir.AluOpType
AX = mybir.AxisListType

ENGS = ("pe", "act", "dve", "pool", "sp")


class T:
    __slots__ = ("h", "name", "lw", "rd", "dsem", "scoped")

    def __init__(self, h, name):
        self.h = h
        self.name = name
        self.lw = None
        self.rd = []
        self.dsem = None
        self.scoped = False

    def __getitem__(self, k):
        return self.h[k]


class Prog:
    def __init__(self, nc):
        self.nc = nc
        self.es = ExitStack()
        self.ins = {e: [] for e in ENGS}
        self.nsem = 0
        self.dsems = []
        self.free_dsems = []
        self.scope_dsems = []
        self.marked = set()
        self.rw = {e: {} for e in ENGS}
        self.cur = self.es
        self.uid = 0
        self.last_tok = {e: None for e in ENGS}
        self.epoch = 0
        self.ep_of = {}

    def _prune(self, eng, deps):
        out = []
        w = self.rw[eng]
        for d in deps:
            key = (d[0], d[1]); val = d[2]
            if w.get(key, -1) >= val:
                continue
            w[key] = val
            out.append(d)
        return out

    def sb(self, name, shape, dt):
        self.uid += 1
        name = "%s_%d" % (name, self.uid)
        t = T(self.cur.enter_context(self.nc.sbuf_tensor(name, list(shape), dt)), name)
        t.scoped = self.cur is not self.es
        return t

    def ps(self, name, shape, dt=F32):
        self.uid += 1
        name = "%s_%d" % (name, self.uid)
        return T(self.cur.enter_context(self.nc.psum_tensor(name, list(shape), dt)), name)

    def push_scope(self):
        self.cur = ExitStack()

    def pop_scope(self):
        self.barrier()
        self.cur.close()
        self.cur = self.es
        self.free_dsems.extend(self.scope_dsems)
        self.scope_dsems = []

    def barrier(self):
        for e in ENGS:
            deps = [t for e2, t in self.last_tok.items() if e2 != e and t is not None]
            deps += [("d", s, c[1]) for s, c in enumerate(self.dsems) if c[1] > 0]
            deps = self._prune(e, deps)
            self.ins[e].append(dict(deps=deps, fn=None, dma=None))

    def dram(self, name, shape, dt, kind="Internal"):
        return T(self.nc.dram_tensor(name, list(shape), dt, kind=kind), name)

    def _new_dsem(self, scoped=False):
        if scoped and self.free_dsems:
            i = self.free_dsems.pop()
        else:
            h = self.es.enter_context(self.nc.semaphore("d%d" % len(self.dsems)))
            self.dsems.append([h, 0])
            i = len(self.dsems) - 1
        if scoped:
            self.scope_dsems.append(i)
        return i

    def _deps(self, eng, reads, writes):
        deps = []
        for t in reads:
            if t.lw is not None:
                deps.append(t.lw)
        for t in writes:
            if t.lw is not None:
                deps.append(t.lw)
            deps.extend(t.rd)
        out = []
        for d in deps:
            if d[0] == "e":
                if d[1] == eng:
                    continue
            out.append(d)
        if eng != "pe":
            for t in reads:
                if t.lw is not None and t.lw[0] == "e" and t.lw[1] == eng:
                    out.append(t.lw)
        return out

    def op(self, eng, fn, reads=(), writes=()):
        deps = self._prune(eng, self._deps(eng, reads, writes))
        idx = len(self.ins[eng])
        self.ins[eng].append(dict(deps=deps, fn=fn, dma=None))
        tok = ("e", eng, idx)
        self.last_tok[eng] = tok
        self.ep_of[(eng, idx)] = self.epoch
        for t in reads:
            t.rd = [r for r in t.rd if not (r[0] == "e" and r[1] == eng)]
            t.rd.append(tok)
        for t in writes:
            t.lw = tok
            t.rd = []
        return tok

    def dma(self, eng, out_t, out_ap, in_t, in_ap, nodep_out=False, **kw):
        reads = [in_t] if in_t is not None else []
        writes = [out_t] if out_t is not None else []
        deps = []
        for t in reads:
            if t.lw is not None:
                deps.append(t.lw)
        for t in writes:
            if nodep_out:
                continue
            if t.lw is not None:
                deps.append(t.lw)
            deps.extend(t.rd)
        owner = out_t if (out_t is not None and out_t.dsem is not None) else None
        if owner is None:
            owner = out_t if out_t is not None else in_t
        if owner.dsem is None:
            owner.dsem = self._new_dsem(scoped=getattr(owner, "scoped", False))
        s = owner.dsem
        self.dsems[s][1] += 16
        tok = ("d", s, self.dsems[s][1])
        deps = self._prune(eng, deps)
        self.ins[eng].append(dict(deps=deps, fn=None, dma=(out_ap, in_ap, s, kw)))
        for t in reads:
            t.rd.append(tok)
        for t in writes:
            t.lw = tok
            t.rd = []
        return tok

    def wait_all(self, eng, tiles):
        deps = []
        for t in tiles:
            if t.lw is not None:
                deps.append(t.lw)
            deps.extend(t.rd)
        deps = self._prune(eng, deps)
        self.ins[eng].append(dict(deps=deps, fn=None, dma=None))

    def finalize(self):
        nc = self.nc
        for e in ENGS:
            for rec in self.ins[e]:
                for d in rec["deps"]:
                    if d[0] == "e":
                        self.marked.add((d[1], d[2]))
        count_at = {}
        for e in ENGS:
            c = {}
            for i, rec in enumerate(self.ins[e]):
                if (e, i) in self.marked:
                    ep = self.ep_of[(e, i)]
                    c[ep] = c.get(ep, 0) + 1
                    count_at[(e, i)] = c[ep]
        esem = {(e, ep): self.es.enter_context(nc.semaphore("s_%s_%d" % (e, ep)))
                for e in ENGS for ep in range(self.epoch + 1)}
        engobj = {"pe": "tensor", "act": "scalar", "dve": "vector", "pool": "gpsimd", "sp": "sync"}
        block = self.es.enter_context(nc.Block())
        stats = {}

        def make_body(e):
            def body(eng):
                waited = {}
                nwait = 0
                for i, rec in enumerate(self.ins[e]):
                    need = {}
                    for d in rec["deps"]:
                        if d[0] == "e":
                            key = ("e", (d[1], self.ep_of[(d[1], d[2])])); val = count_at[(d[1], d[2])]
                        else:
                            key = ("d", d[1]); val = d[2]
                        if waited.get(key, 0) >= val:
                            continue
                        if need.get(key, 0) < val:
                            need[key] = val
                    for key, val in need.items():
                        h = esem[key[1]] if key[0] == "e" else self.dsems[key[1]][0]
                        eng.wait_ge(h, val)
                        waited[key] = val
                        nwait += 1
                    if rec.get("cc") is not None:
                        rec["cc"](eng).then_inc(self.dsems[rec["ccsem"]][0], 16)
                    elif rec["dma"] is not None:
                        out_ap, in_ap, s, kw = rec["dma"]
                        try:
                            eng.dma_start(out=out_ap, in_=in_ap, **kw).then_inc(self.dsems[s][0], 16)
                        except Exception:
                            print("DMA FAILED:", e, "out=", out_ap, "in=", in_ap)
                            raise
                    elif rec["fn"] is not None:
                        ins = rec["fn"](eng)
                        if (e, i) in self.marked:
                            ins.then_inc(esem[(e, self.ep_of[(e, i)])], 1)
                stats[e] = (len(self.ins[e]), nwait)
            return body

        for e in ENGS:
            if self.ins[e]:
                getattr(block, engobj[e])(make_body(e))
        self.stats = stats
        self.es.close()


NTOK = 2048
HALO = 4
NCOL = NTOK + HALO
RANGES = [(0, 1028), (1028, 1024)]
EPS = 1e-6


def tiles_of(n):
    k = (n + 511) // 512
    out = []
    o = 0
    for i in range(k):
        w = (n - o + (k - i) - 1) // (k - i)
        out.append((o, w))
        o += w
    return out


class TokStage:
    def __init__(self, P, mode):
        self.P = P
        self.mode = mode
        nc = P.nc
        D = lambda n, s: P.dram(n, s, F32, kind="ExternalInput")
        O = lambda n, s: P.dram(n, s, F32, kind="ExternalOutput")
        self.xT = D("xT", [1024, NCOL])
        self.memT = D("memT", [1024, 256])
        self.gains = D("gains", [128, 8 * 16])
        self.hg = D("hg", [128, 16])
        self.flag = D("flag", [128, 1])
        self.mem_w_kv = D("mem_w_kv", [4, 1024, 512])
        self.w_out = D("w_out", [4, 1024, 1024])
        self.w_up = D("w_up", [4, 1024, 4096])
        self.w_down = D("w_down", [4, 4096, 1024])
        self.b_w_in = D("b_w_in", [2, 1024, 1060])
        if mode == "L1":
            self.a_w_in = D("a_w_in", [2, 1024, 2560])
            self.convw = D("convw", [128, 2 * 6 * 3])
            self.w_kv = D("w_kv", [1024, 768])
            self.kvT_o = O("kvT_o", [768, NCOL])
        else:
            self.ytokT = D("ytokT", [768, NCOL])
        self.xT_o = O("xT_o", [1024, NCOL])
        if mode != "L5":
            self.qT_o = O("qT_o", [768, NCOL])
            self.gT_o = O("gT_o", [36, NCOL])

        self.alloc()

    def alloc(self):
        P = self.P
        sb = P.sb
        self.x = sb("x", [128, 8, NCOL], F32)
        self.h = sb("h", [128, 8, 1028], BF16)
        self.y = sb("y", [128, 8, 1028], BF16)
        self.act = sb("act", [128, 8, 1028], BF16)
        self.u = sb("u", [128, 2, 1028], F32)
        self.v = sb("v", [128, 2, 1030], F32)
        self.qm = sb("qm", [128, 2, 1028], F32)
        self.carry = sb("carry", [128, 6, 2], F32)
        self.rstd = sb("rstd", [128, 512], F32)
        self.sq = [sb("sq%d" % i, [128, 512], BF16) for i in range(2)]
        self.tmp = [sb("tmp%d" % i, [128, 512], F32) for i in range(3)]
        self.ost = [sb("ost%d" % i, [128, 512], F32) for i in range(2)]
        self.wb = [sb("wb%d" % i, [128, 8, 256], BF16) for i in range(4)]
        self.memh = sb("memh", [128, 8, 256], BF16)
        self.kraw = sb("kraw", [128, 2, 256], F32)
        self.memk = sb("memk", [128, 2, 256], BF16)
        self.vaug = sb("vaug", [128, 2, 4, 128], BF16)
        self.qn = sb("qn", [128, 512], BF16)
        self.eT = [sb("eT%d" % i, [128, 512], BF16) for i in range(2)]
        self.rd = sb("rd", [64, 512], F32)
        self.g = sb("g", [128, 8 * 16], F32)
        self.hgs = sb("hgs", [128, 16], F32)
        self.flg = sb("flg", [128, 1], F32)
        self.cw = sb("cw", [128, 36], F32)
        self.ones = sb("ones", [128, 128], BF16)
        self.bd = sb("bd", [128, 128], BF16)
        self.epsT = sb("epsT", [128, 1], F32)
        self.pp = [P.ps("pp%d" % i, [128, 512]) for i in range(4)]
        self.pl = [P.ps("pl%d" % i, [128, 512]) for i in range(2)]
        self.po = P.ps("po", [128, 512])
        self.pn = P.ps("pn", [128, 512])
        self.cnt = dict(wf=0, wb=0, pp=0, pl=0, sq=0, tmp=0, ost=0, eT=0)

    def rot(self, name, lst):
        i = self.cnt[name]
        self.cnt[name] += 1
        return lst[i % len(lst)]

    def stream(self, wT, w_ap, ncols):
        P = self.P
        wb = self.rot("wb", self.wb)
        P.dma("pool", wb, wb[:, :, 0:ncols], wT, w_ap.rearrange("(c p) n -> p c n", p=128))
        return wb

    def proj(self, wb, sub, msz, rhs_t, rhs_fn, tiles, consumer):
        P = self.P
        for (t0, w) in tiles:
            ps = self.rot("pp", self.pp)
            for kc in range(8):
                P.op("pe", lambda e, ps=ps, kc=kc, t0=t0, w=w: e.matmul(
                    ps[0:msz, 0:w], lhsT=wb[:, kc, sub * 128: sub * 128 + msz], rhs=rhs_fn(kc, t0, w),
                    start=(kc == 0), stop=(kc == 7)), [wb, rhs_t], [ps])
            consumer(ps, t0, w)

    def setup(self):
        P = self.P
        P.dma("sp", self.g, self.g[:], self.gains, self.gains[:])
        P.dma("sp", self.hgs, self.hgs[:], self.hg, self.hg[:])
        P.dma("sp", self.flg, self.flg[:], self.flag, self.flag[:])
        if self.mode == "L1":
            P.dma("sp", self.cw, self.cw[:], self.convw, self.convw[:])
        for c in range(8):
            P.dma("sp", self.x, self.x[:, c, :], self.xT, self.xT[c * 128:(c + 1) * 128, :], nodep_out=True)
            P.dma("sp", self.u, self.memx(c, 0, 256), self.memT, self.memT[c * 128:(c + 1) * 128, :], nodep_out=True)
        P.op("pool", lambda e: e.memset(self.ones[:], 1.0), [], [self.ones])
        P.op("pool", lambda e: e.memset(self.bd[:], 0.0), [], [self.bd])
        P.op("pool", lambda e: e.memset(self.bd[0:64, 0:64], 1.0), [], [self.bd])
        P.op("pool", lambda e: e.memset(self.bd[64:128, 64:128], 1.0), [], [self.bd])
        P.op("pool", lambda e: e.memset(self.epsT[:], EPS), [], [self.epsT])
        P.op("pool", lambda e: e.memset(self.vaug[:], 1.0), [], [self.vaug])
        P.op("pool", lambda e: e.memset(self.carry[:], 0.0), [], [self.carry])
        self.norm(self.u, self.memx, 256, [(0, 256)], 15, self.memh, 0)

    def gcol(self, which, c):
        return self.g[:, which * 8 + c: which * 8 + c + 1]

    def memx(self, c, t0, w):
        return self.u[:, c // 4, (c % 4) * 256 + t0: (c % 4) * 256 + t0 + w]

    def xs(self, c0):
        return lambda c, t0, w: self.x[:, c, c0 + t0: c0 + t0 + w]

    def norm(self, src, src_fn, n, tiles, which, dst, d0):
        P = self.P
        for (t0, w) in tiles:
            for c in range(8):
                sq = self.rot("sq", self.sq)
                P.op("act", lambda e, sq=sq, c=c, t0=t0, w=w: e.activation(
                    out=sq[:, 0:w], in_=src_fn(c, t0, w), func=AF.Square), [src], [sq])
                P.op("pe", lambda e, sq=sq, c=c, w=w: e.matmul(
                    self.pn[:, 0:w], lhsT=self.ones[:], rhs=sq[:, 0:w], start=(c == 0), stop=(c == 7)),
                    [self.ones, sq], [self.pn])
            P.op("act", lambda e, w=w: e.activation(out=self.rstd[:, 0:w], in_=self.pn[:, 0:w], func=AF.Ln,
                                                    bias=self.epsT[:], scale=1.0 / 1024.0),
                 [self.pn, self.epsT], [self.rstd])
            P.op("act", lambda e, w=w: e.activation(out=self.rstd[:, 0:w], in_=self.rstd[:, 0:w], func=AF.Exp, scale=-0.5),
                 [self.rstd], [self.rstd])
            for c in range(8):
                eng = "dve"
                P.op(eng, lambda e, c=c, t0=t0, w=w: e.scalar_tensor_tensor(
                    out=dst[:, c, d0 + t0: d0 + t0 + w], in0=src_fn(c, t0, w),
                    scalar=self.gcol(which, c), in1=self.rstd[:, 0:w], op0=ALU.mult, op1=ALU.mult),
                    [src, self.g, self.rstd], [dst])

    def headnorm(self, src_t, src_ap_fn, w, gain_ap, dst_t, dst_ap):
        P = self.P
        sq = self.rot("sq", self.sq)
        P.op("act", lambda e, sq=sq: e.activation(out=sq[:, 0:w], in_=src_ap_fn(), func=AF.Square), [src_t], [sq])
        P.op("pe", lambda e, sq=sq: e.matmul(self.pn[:, 0:w], lhsT=self.bd[:], rhs=sq[:, 0:w], start=True, stop=True),
             [self.bd, sq], [self.pn])
        P.op("act", lambda e: e.activation(out=self.rstd[:, 0:w], in_=self.pn[:, 0:w], func=AF.Ln,
                                           bias=self.epsT[:], scale=1.0 / 64.0), [self.pn, self.epsT], [self.rstd])
        P.op("act", lambda e: e.activation(out=self.rstd[:, 0:w], in_=self.rstd[:, 0:w], func=AF.Exp, scale=-0.5),
             [self.rstd], [self.rstd])
        P.op("dve", lambda e: e.scalar_tensor_tensor(out=dst_ap, in0=src_ap_fn(), scalar=gain_ap,
                                                     in1=self.rstd[:, 0:w], op0=ALU.mult, op1=ALU.mult),
             [src_t, self.hgs, self.rstd], [dst_t])

    def mem_kv(self, l):
        P = self.P
        wb = self.stream(self.mem_w_kv, self.mem_w_kv[l, :, 0:256], 256)
        for sub in range(2):
            def cons(ps, t0, w, sub=sub):
                P.op("act", lambda e: e.activation(out=self.kraw[:, sub, 0:256], in_=ps[:, 0:256], func=AF.Copy),
                     [ps], [self.kraw])
            self.proj(wb, sub, 128, self.memh, lambda kc, t0, w: self.memh[:, kc, t0:t0 + w], [(0, 256)], cons)
            self.headnorm(self.kraw, lambda sub=sub: self.kraw[:, sub, 0:256], 256,
                          self.hgs[:, 4 + l: 5 + l], self.memk, self.memk[:, sub, 0:256])
        wb = self.stream(self.mem_w_kv, self.mem_w_kv[l, :, 256:512], 256)
        for mc in range(2):
            ps = self.rot("pp", self.pp)
            for kc in range(8):
                P.op("pe", lambda e, ps=ps, kc=kc, mc=mc: e.matmul(
                    ps[:, 0:256], lhsT=self.memh[:, kc, mc * 128:(mc + 1) * 128], rhs=wb[:, kc, 0:256],
                    start=(kc == 0), stop=(kc == 7)), [self.memh, wb], [ps])
            for hh in range(4):
                P.op("act", lambda e, ps=ps, mc=mc, hh=hh: e.activation(
                    out=self.vaug[:, mc, hh, 0:64], in_=ps[:, hh * 64:(hh + 1) * 64], func=AF.Copy), [ps], [self.vaug])

    def mem_attn(self, l, n, tiles):
        P = self.P
        for (t0, w) in tiles:
            for j in range(2):
                self.headnorm(self.qm, lambda j=j, t0=t0, w=w: self.qm[:, j, t0:t0 + w], w,
                              self.hgs[:, l: l + 1], self.qn, self.qn[:, 0:w])
                for hh in range(2):
                    b0 = 64 * hh
                    hd = 2 * j + hh
                    ets = []
                    for mc in range(2):
                        pl = self.rot("pl", self.pl)
                        P.op("pe", lambda e, pl=pl, mc=mc, b0=b0, j=j, w=w: e.matmul(
                            pl[:, 0:w], lhsT=self.memk[b0:b0 + 64, j, mc * 128:(mc + 1) * 128],
                            rhs=self.qn[b0:b0 + 64, 0:w], start=True, stop=True), [self.memk, self.qn], [pl])
                        eT = self.rot("eT", self.eT)
                        P.op("act", lambda e, pl=pl, eT=eT, w=w: e.activation(
                            out=eT[:, 0:w], in_=pl[:, 0:w], func=AF.Exp, scale=0.125), [pl], [eT])
                        ets.append(eT)
                    for mc in range(2):
                        P.op("pe", lambda e, mc=mc, hd=hd, w=w, eT=ets[mc]: e.matmul(
                            self.po[:, 0:w], lhsT=self.vaug[:, mc, hd, :], rhs=eT[:, 0:w],
                            start=(mc == 0), stop=(mc == 1)), [self.vaug, ets[mc]], [self.po])
                    P.op("act", lambda e, w=w: e.activation(out=self.rd[0:64, 0:w], in_=self.po[64:128, 0:w], func=AF.Ln),
                         [self.po], [self.rd])
                    P.op("act", lambda e, w=w: e.activation(out=self.rd[0:64, 0:w], in_=self.rd[0:64, 0:w], func=AF.Exp,
                                                            scale=-1.0), [self.rd], [self.rd])
                    P.op("dve", lambda e, w=w, b0=b0, j=j, t0=t0: e.tensor_tensor(
                        out=self.y[b0:b0 + 64, 6 + j, t0:t0 + w], in0=self.po[0:64, 0:w], in1=self.rd[0:64, 0:w],
                        op=ALU.mult), [self.po, self.rd], [self.y])

    def add_into_x(self, c0):
        P = self.P

        def mk(ch):
            def cons(ps, t0, w):
                P.op("dve", lambda e: e.tensor_tensor(
                    out=self.x[:, ch, c0 + t0: c0 + t0 + w], in0=ps[:, 0:w], in1=self.x[:, ch, c0 + t0: c0 + t0 + w],
                    op=ALU.add), [ps, self.x], [self.x])
            return cons
        return mk

    def out_proj(self, l, c0, n, tiles):
        mk = self.add_into_x(c0)
        for blk in range(4):
            wb = self.stream(self.w_out, self.w_out[l, :, blk * 256:(blk + 1) * 256], 256)
            for sub in range(2):
                self.proj(wb, sub, 128, self.y, lambda kc, t0, w: self.y[:, kc, t0:t0 + w], tiles, mk(blk * 2 + sub))

    def mlp(self, l, c0, n, tiles):
        P = self.P
        self.norm(self.x, self.xs(c0), n, tiles, 4 + l, self.h, 0)
        mk = self.add_into_x(c0)
        for sg in range(4):
            for blk in range(4):
                wb = self.stream(self.w_up, self.w_up[l, :, sg * 1024 + blk * 256: sg * 1024 + (blk + 1) * 256], 256)
                for sub in range(2):
                    def cons(ps, t0, w, fc=blk * 2 + sub):
                        tmp = self.rot("tmp", self.tmp)
                        P.op("act", lambda e: e.activation(out=tmp[:, 0:w], in_=ps[:, 0:w], func=AF.Relu), [ps], [tmp])
                        P.op("dve", lambda e: e.tensor_tensor(out=self.act[:, fc, t0:t0 + w], in0=tmp[:, 0:w],
                                                              in1=tmp[:, 0:w], op=ALU.mult), [tmp], [self.act])
                    self.proj(wb, sub, 128, self.h, lambda kc, t0, w: self.h[:, kc, t0:t0 + w], tiles, cons)
            for blk in range(4):
                wb = self.stream(self.w_down, self.w_down[l, sg * 1024:(sg + 1) * 1024, blk * 256:(blk + 1) * 256], 256)
                for sub in range(2):
                    self.proj(wb, sub, 128, self.act, lambda kc, t0, w: self.act[:, kc, t0:t0 + w], tiles,
                              mk(blk * 2 + sub))

    def mix_a(self, l, ri, c0, n, tiles):
        P = self.P
        W = self.a_w_in
        hrhs = lambda kc, t0, w: self.h[:, kc, t0:t0 + w]
        for i in range(3):
            wb = self.stream(W, W[l, :, 1536 + 256 * i: 1536 + 256 * (i + 1)], 256)
            for sub in range(2):
                def cons(ps, t0, w, sub=sub):
                    P.op("act", lambda e: e.activation(out=self.u[:, sub, t0:t0 + w], in_=ps[:, 0:w], func=AF.Copy),
                         [ps], [self.u])
                self.proj(wb, sub, 128, self.h, hrhs, tiles, cons)
            wb = self.stream(W, W[l, :, 768 + 256 * i: 768 + 256 * (i + 1)], 256)
            for sub in range(2):
                def cons(ps, t0, w, sub=sub):
                    P.op("dve", lambda e: e.tensor_tensor(out=self.v[:, sub, 2 + t0: 2 + t0 + w], in0=ps[:, 0:w],
                                                          in1=self.u[:, sub, t0:t0 + w], op=ALU.mult),
                         [ps, self.u], [self.v])
                self.proj(wb, sub, 128, self.h, hrhs, tiles, cons)
            for sub in range(2):
                ch = 2 * i + sub
                self.conv_lead(l, ri, sub, ch, n)
                cwc = lambda k, ch=ch: self.cw[:, l * 18 + ch * 3 + k: l * 18 + ch * 3 + k + 1]
                P.op("dve", lambda e, sub=sub, cwc=cwc: e.tensor_scalar(
                    out=self.u[:, sub, 0:n], in0=self.v[:, sub, 0:n], scalar1=cwc(0), scalar2=None, op0=ALU.mult),
                    [self.v, self.cw], [self.u])
                for k in (1, 2):
                    P.op("dve", lambda e, sub=sub, cwc=cwc, k=k: e.scalar_tensor_tensor(
                        out=self.u[:, sub, 0:n], in0=self.v[:, sub, k:k + n], scalar=cwc(k), in1=self.u[:, sub, 0:n],
                        op0=ALU.mult, op1=ALU.add), [self.v, self.cw, self.u], [self.u])
            wb = self.stream(W, W[l, :, 256 * i: 256 * (i + 1)], 256)
            for sub in range(2):
                def cons(ps, t0, w, sub=sub, ch=2 * i + sub):
                    P.op("dve", lambda e: e.tensor_tensor(out=self.y[:, ch, t0:t0 + w], in0=ps[:, 0:w],
                                                          in1=self.u[:, sub, t0:t0 + w], op=ALU.mult),
                         [ps, self.u], [self.y])
                self.proj(wb, sub, 128, self.h, hrhs, tiles, cons)
        wb = self.stream(W, W[l, :, 2304:2560], 256)
        self.qmem_proj(wb, tiles)

    def conv_lead(self, l, ri, sub, ch, n):
        P = self.P
        if ri == 0:
            P.op("pool", lambda e: e.memset(self.v[:, sub, 0:2], 0.0), [], [self.v])
            self.halo_fix(sub)
            P.op("dve", lambda e: e.tensor_copy(out=self.carry[:, ch, :], in_=self.v[:, sub, n:n + 2]),
                 [self.v], [self.carry])
        else:
            P.op("dve", lambda e: e.tensor_copy(out=self.v[:, sub, 0:2], in_=self.carry[:, ch, :]),
                 [self.carry], [self.v])

    def halo_fix(self, sub):
        self.P.op("dve", lambda e: e.tensor_scalar(
            out=self.v[:, sub, 2:2 + HALO], in0=self.v[:, sub, 2:2 + HALO], scalar1=self.flg[:, 0:1],
            scalar2=None, op0=ALU.mult), [self.v, self.flg], [self.v])

    def store_tile(self, dstT, r, msz, c0, t0, w, ost):
        self.P.dma("sp", dstT, dstT[r:r + msz, c0 + t0: c0 + t0 + w], ost, ost[0:msz, 0:w], nodep_out=True)

    def ytok_src(self, ch, c0, t0, w):
        return self.ytokT[ch * 128:(ch + 1) * 128, c0 + t0: c0 + t0 + w]

    def qmem_proj(self, wb, tiles):
        P = self.P
        for sub in range(2):
            def cons(ps, t0, w, sub=sub):
                P.op("act", lambda e: e.activation(out=self.qm[:, sub, t0:t0 + w], in_=ps[:, 0:w], func=AF.Copy),
                     [ps], [self.qm])
            self.proj(wb, sub, 128, self.h, lambda kc, t0, w: self.h[:, kc, t0:t0 + w], tiles, cons)

    def store_proj(self, wT, w_ap, ncols, dstT, row0, c0, tiles, func=AF.Copy):
        P = self.P
        wb = self.stream(wT, w_ap, ncols)
        nsub = (ncols + 127) // 128
        for sub in range(nsub):
            msz = min(128, ncols - sub * 128)

            def cons(ps, t0, w, sub=sub, msz=msz):
                ost = self.rot("ost", self.ost)
                P.op("act", lambda e: e.activation(out=ost[0:msz, 0:w], in_=ps[0:msz, 0:w], func=func), [ps], [ost])
                r = row0 + sub * 128
                self.store_tile(dstT, r, msz, c0, t0, w, ost)
            self.proj(wb, sub, msz, self.h, lambda kc, t0, w: self.h[:, kc, t0:t0 + w], tiles, cons)

    def pre_b(self, j, c0, n, tiles):
        W = self.b_w_in
        self.norm(self.x, self.xs(c0), n, tiles, 2 + j, self.h, 0)
        for i in range(3):
            self.store_proj(W, W[j, :, 256 * i:256 * (i + 1)], 256, self.qT_o, 256 * i, c0, tiles)
        self.store_proj(W, W[j, :, 768:804], 36, self.gT_o, 0, c0, tiles, func=AF.Sigmoid)

    def post_b(self, j, c0, n, tiles):
        P = self.P
        W = self.b_w_in
        l = 2 + j
        self.norm(self.x, self.xs(c0), n, tiles, l, self.h, 0)
        wb = self.stream(W, W[j, :, 804:1060], 256)
        self.qmem_proj(wb, tiles)
        for ch in range(6):
            for (t0, w) in tiles:
                tmp = self.rot("tmp", self.tmp)
                P.dma("sp", tmp, tmp[:, 0:w], self.ytokT, self.ytok_src(ch, c0, t0, w))
                P.op("act", lambda e, tmp=tmp, ch=ch, t0=t0, w=w: e.activation(out=self.y[:, ch, t0:t0 + w],
                                                                               in_=tmp[:, 0:w], func=AF.Copy), [tmp], [self.y])
        self.mem_attn(l, n, tiles)
        self.out_proj(l, c0, n, tiles)
        self.mlp(l, c0, n, tiles)

    def store_x(self):
        P = self.P
        for c in range(8):
            P.dma("sp", self.xT_o, self.xT_o[c * 128:(c + 1) * 128, :], self.x, self.x[:, c, :], nodep_out=True)

    def build(self):
        P = self.P
        self.setup()
        if self.mode == "L1":
            for l in range(2):
                self.mem_kv(l)
                for ri, (c0, n) in enumerate(RANGES):
                    tiles = tiles_of(n)
                    self.norm(self.x, self.xs(c0), n, tiles, l, self.h, 0)
                    self.mix_a(l, ri, c0, n, tiles)
                    self.mem_attn(l, n, tiles)
                    self.out_proj(l, c0, n, tiles)
                    self.mlp(l, c0, n, tiles)
            for ri, (c0, n) in enumerate(RANGES):
                tiles = tiles_of(n)
                self.norm(self.x, self.xs(c0), n, tiles, 8, self.h, 0)
                for i in range(3):
                    self.store_proj(self.w_kv, self.w_kv[:, 256 * i:256 * (i + 1)], 256, self.kvT_o, 256 * i, c0, tiles)
                self.pre_b(0, c0, n, tiles)
            self.store_x()
            outs = [self.kvT_o, self.qT_o, self.gT_o, self.xT_o]
        elif self.mode == "L3":
            self.mem_kv(2)
            for ri, (c0, n) in enumerate(RANGES):
                tiles = tiles_of(n)
                self.post_b(0, c0, n, tiles)
            for ri, (c0, n) in enumerate(RANGES):
                tiles = tiles_of(n)
                self.pre_b(1, c0, n, tiles)
            self.store_x()
            outs = [self.qT_o, self.gT_o, self.xT_o]
        else:
            self.mem_kv(3)
            for ri, (c0, n) in enumerate(RANGES):
                tiles = tiles_of(n)
                self.post_b(1, c0, n, tiles)
            self.store_x()
            outs = [self.xT_o]
        P.wait_all("sp", outs)
        P.finalize()


def build_tok(mode):
    nc = bass.Bass("TRN2", target_bir_lowering=False)
    P = Prog(nc)
    st = TokStage(P, mode)
    st.build()
    return nc, P


def host_gains(inp):
    g = np.zeros((16, 1024), np.float32)
    g[0:4] = inp["mix_norm"]
    g[4:8] = inp["mlp_norm"]
    g[8] = inp["kv_norm"]
    g[15] = inp["mem_norm"]
    return np.ascontiguousarray(g.reshape(16, 8, 128).transpose(2, 0, 1).reshape(128, 128))


def host_hg(inp):
    hg = np.zeros((128, 16), np.float32)
    for l in range(4):
        hg[:, l] = np.tile(inp["mem_q_gain"][l], 2)
        hg[:, 4 + l] = np.tile(inp["mem_k_gain"][l], 2)
    return hg


def host_convw(inp):
    w = inp["a_conv_w"].reshape(2, 3, 6, 128)
    return np.ascontiguousarray(w.transpose(3, 0, 2, 1).reshape(128, 36))

import math

SEQ = 8192
QT = 512
NQT = SEQ // QT
DC, LC = 3040, 5104
DW, LW = 1023, 1536
DS, LS = 1407, 1920
LTOT = LC + LW + LS
LU, LTW, LTS = 3072, 1408, 1792
MASKV = -30000.0
PIPE = 3


def rel_bucket_np(d):
    n = np.maximum(d, 0)
    nf = np.maximum(n, 1).astype(np.float32)
    large = 16 + (np.log(nf / np.float32(16)) / np.float32(math.log(1024 / 16)) * np.float32(16)).astype(np.int32)
    large = np.minimum(large, 31)
    return np.where(n < 16, n, large)


def host_consts():
    c = {}
    oh = np.zeros((33, LTOT), np.float32)
    x = np.arange(LC); d = DC - x
    b = rel_bucket_np(d)
    oh[b[d >= 0], x[d >= 0]] = 1.0
    oh[32, x[d < 0]] = 1.0
    x = np.arange(LW); d = DW - x; ok = (d >= 0) & (d < 512)
    b = rel_bucket_np(d)
    oh[b[ok], LC + x[ok]] = 1.0
    oh[32, LC + x[~ok]] = 1.0
    x = np.arange(LS); d = DS - x; ok = d >= 0
    b = rel_bucket_np(d)
    oh[b[ok], LC + LW + x[ok]] = 1.0
    oh[32, LC + LW + x[~ok]] = 1.0
    c["OH"] = oh
    c["ident"] = np.eye(128, dtype=np.float32)
    k = np.arange(SEQ)
    r = 2 * ((k // 128) % 32) + ((k % 128) >= 64)
    ee = np.zeros((64, SEQ), np.float32)
    ee[r, k] = 30000.0
    c["EE"] = ee
    i = np.arange(512)[:, None] * 16
    s = np.arange(128)[None, :] * 64
    ov = np.clip(np.minimum(i + 32, s + 64) - np.maximum(i, s), 0, None).astype(np.float32) / 32.0
    ov[511] = 0.0
    c["ovl"] = np.ascontiguousarray(ov.reshape(4, 128, 128).transpose(1, 0, 2))
    tc = np.zeros((NQT, 128, 4, 2, 128), np.float32)
    j = np.arange(128)[None, :]
    for qt in range(NQT):
        for ss in range(4):
            t = qt * QT + 511 - (128 * ss + np.arange(128))
            back = (t // 64)[:, None] - j
            forced = (j == 0) | ((back >= 0) & (back < 2))
            A = ((back >= 0) & ~forced).astype(np.float32)
            B = np.where(back >= 0, np.where(forced, 1e4, 0.0), -1e30).astype(np.float32)
            tc[qt, :, ss, 0] = A
            tc[qt, :, ss, 1] = B
    c["topc"] = tc.reshape(NQT, 128, 1024)
    return c


class Attn:
    def __init__(self, P):
        self.P = P
        D = lambda n, s: P.dram(n, s, F32, kind="ExternalInput")
        self.qT = D("qT", [384, SEQ])
        self.gB = D("gB", [9, SEQ])
        self.kvT = D("kvT", [384, SEQ])
        self.vtok = D("vtok", [SEQ, 128])
        self.blk = D("blk", [2, 64, 32, 512])
        self.hg = D("hg", [128, 8])
        self.pos = D("pos", [2, 64, 32])
        self.w1 = D("w1", [2, 2048, 256])
        self.w2 = D("w2", [2, 256, 64])
        self.rb6 = D("rb6", [32, 6])
        self.OH = D("OH", [33, LTOT])
        self.identD = D("ident", [128, 128])
        self.EE = D("EE", [64, SEQ])
        self.ovlD = D("ovl", [128, 4, 128])
        self.topc = D("topc", [NQT, 128, 1024])
        self.yT = P.dram("yT", [192, SEQ], F32, kind="ExternalOutput")
        self.Rrow = P.dram("Rrow", [6, LTOT], F32, kind="Internal")
        self.alloc()

    def alloc(self):
        P = self.P
        sb = P.sb
        self.ksaug = sb("ksaug", [128, SEQ], BF16)
        self.kwT = sb("kwT", [128, SEQ], BF16)
        self.vsaug = sb("vsaug", [128, 64, 128], BF16)
        self.vwaug = sb("vwaug", [128, 64, 128], BF16)
        self.kcT = sb("kcT", [128, 512], BF16)
        self.vcaug = sb("vcaug", [128, 4, 128], BF16)
        self.Ubig = sb("Ubig", [128, 6 * LU], BF16)
        self.TTw = [sb("TTw%d" % h, [128, LTW], BF16) for h in range(3)]
        self.TTs = [sb("TTs%d" % h, [128, LTS], BF16) for h in range(3)]
        self.ident = sb("identb", [128, 128], BF16)
        self.ones = sb("ones", [128, 128], BF16)
        self.ovl = sb("ovlb", [128, 4, 128], BF16)
        self.hgs = sb("hgs", [128, 8], F32)
        self.cf = sb("cf", [128, 6], F32)
        self.tab = sb("tab", [33, 6], F32)
        self.epsT = sb("epsT", [128, 1], F32)
        self.tinyT = sb("tinyT", [128, 1], F32)
        self.st = [sb("st%d" % i, [128, 1024], F32) for i in range(2)]
        self.sq = [sb("sq%d" % i, [128, 512], BF16) for i in range(2)]
        self.rq = sb("rq", [128, 512], F32)
        self.qraw = sb("qraw", [64, 6, 512], F32)
        self.qn = sb("qn", [128, 6, 512], BF16)
        self.qaug = sb("qaug", [128, 3, 2, 512], BF16)
        self.pT = [sb("pT%d" % i, [128, 512], BF16) for i in range(9)]
        self.rdq = sb("rdq", [128, 4], F32)
        self.rdb = sb("rdb", [128, 512], F32)
        self.impacc = sb("impacc", [128, 512], F32)
        self.tcs = sb("tcs", [128, 1024], F32)
        self.sc = sb("sc", [128, 128], F32)
        self.sc2 = sb("sc2", [128, 128], F32)
        self.mx = sb("mx", [128, 16], F32)
        self.mneg = sb("mneg", [128, 128], BF16)
        self.gt = [sb("gt%d" % i, [64, 512], F32) for i in range(2)]
        self.rdt = sb("rdt", [64, 512], F32)
        self.wt = sb("wt", [64, 512], F32)
        self.tm = sb("tm", [64, 512], F32)
        self.y = sb("y", [64, 3, 512], F32)
        self.w1b = sb("w1b", [64, 4, 256], BF16)
        self.blkb = sb("blkb", [64, 4, 512], BF16)
        self.posT = sb("posT", [64, 32], F32)
        self.w2b = sb("w2b", [128, 2, 64], BF16)
        self.gx = self.rdb
        self.gy = self.impacc
        self.gT = [sb("gT%d" % i, [128, 512], BF16) for i in range(2)]
        self.A = [P.ps("A%d" % i, [128, 512]) for i in range(3)]
        self.S = [P.ps("S%d" % i, [128, 512]) for i in range(2)]
        self.X0 = P.ps("X0", [128, 512])
        self.X1 = P.ps("X1", [128, 512])
        self.XT = P.ps("XT", [128, 512], BF16)
        self.S4 = self.S + [self.X0, self.X1]
        self.cnt = {}

    def uslice(self, h, mo):
        return self.Ubig[:, h * LU + mo: h * LU + mo + 512]

    def rot(self, name, lst):
        i = self.cnt.get(name, 0)
        self.cnt[name] = i + 1
        return lst[i % len(lst)]

    def norm64(self, src_t, src_ap, w, gain_ap, dst_t, dst_ap):
        P = self.P
        sq = self.rot("sq", self.sq)
        P.op("act", lambda e: e.activation(out=sq[0:64, 0:w], in_=src_ap, func=AF.Square), [src_t], [sq])
        P.op("pe", lambda e: e.matmul(self.X0[0:64, 0:w], lhsT=self.ones[0:64, 0:64], rhs=sq[0:64, 0:w],
                                      start=True, stop=True), [self.ones, sq], [self.X0])
        P.op("act", lambda e: e.activation(out=self.rq[0:64, 0:w], in_=self.X0[0:64, 0:w], func=AF.Ln,
                                           bias=self.epsT[0:64, :], scale=1.0 / 64.0), [self.X0, self.epsT], [self.rq])
        P.op("act", lambda e: e.activation(out=self.rq[0:64, 0:w], in_=self.rq[0:64, 0:w], func=AF.Exp, scale=-0.5),
             [self.rq], [self.rq])
        P.op("dve", lambda e: e.scalar_tensor_tensor(out=dst_ap, in0=src_ap, scalar=gain_ap, in1=self.rq[0:64, 0:w],
                                                     op0=ALU.mult, op1=ALU.mult), [src_t, self.hgs, self.rq], [dst_t])

    def setup(self):
        P = self.P
        st = self.st
        P.dma("sp", self.hgs, self.hgs[:], self.hg, self.hg[:])
        P.op("pool", lambda e: e.memset(self.ones[:], 1.0), [], [self.ones])
        P.op("pool", lambda e: e.memset(self.epsT[:], 1e-6), [], [self.epsT])
        P.op("pool", lambda e: e.memset(self.tinyT[:], 1e-30), [], [self.tinyT])
        P.op("pool", lambda e: e.memset(self.vsaug[:], 1.0), [], [self.vsaug])
        P.op("pool", lambda e: e.memset(self.vwaug[:], 1.0), [], [self.vwaug])
        P.op("pool", lambda e: e.memset(self.vcaug[:], 1.0), [], [self.vcaug])
        s = self.rot("st", st)
        P.dma("sp", s, s[:, 0:128], self.identD, self.identD[:])
        P.op("pool", lambda e, s=s: e.tensor_copy(out=self.ident[:], in_=s[:, 0:128]), [s], [self.ident])
        s = self.rot("st", st)
        P.dma("sp", s, s[:, 0:512], self.ovlD, self.ovlD[:].rearrange("p c j -> p (c j)"))
        P.op("pool", lambda e, s=s: e.tensor_copy(out=self.ovl[:].rearrange("p c j -> p (c j)"), in_=s[:, 0:512]),
             [s], [self.ovl])
        self.setup_tables()
        self.setup_kv()

    def load_rb(self):
        P = self.P
        rb = self.rb6[:]
        P.dma("sp", self.cf, self.cf[:], self.rb6, bass.AP(tensor=rb.tensor, offset=31 * 6, ap=[[0, 128], [1, 6]]))
        P.dma("sp", self.tab, self.tab[0:32, :], self.rb6, self.rb6[:])

    def setup_tables(self):
        P = self.P
        st = self.st
        P.op("pool", lambda e: e.memset(self.tab[:], MASKV), [], [self.tab])
        self.load_rb()
        P.op("dve", lambda e: e.tensor_scalar(out=self.tab[0:32, :], in0=self.tab[0:32, :], scalar1=8.0, scalar2=None,
                                              op0=ALU.mult), [self.tab], [self.tab])
        o = 0
        while o < LTOT:
            w = min(512, LTOT - o)
            s = self.rot("st", st)
            P.dma("sp", s, s[0:33, 0:w], self.OH, self.OH[:, o:o + w])
            P.op("pe", lambda e, s=s, w=w: e.matmul(self.X1[0:6, 0:w], lhsT=self.tab[:, :], rhs=s[0:33, 0:w],
                                                    start=True, stop=True), [self.tab, s], [self.X1])
            s2 = self.rot("st", st)
            P.op("act", lambda e, s2=s2, w=w: e.activation(out=s2[0:6, 0:w], in_=self.X1[0:6, 0:w], func=AF.Copy),
                 [self.X1], [s2])
            P.dma("sp", self.Rrow, self.Rrow[:, o:o + w], s2, s2[0:6, 0:w])
            o += w
        rr = self.Rrow[:]

        def expand(dst, d0, h, base, pstep, L, nodep=False):
            P.dma("pool", dst, dst[:, d0:d0 + L], self.Rrow,
                  bass.AP(tensor=rr.tensor, offset=h * LTOT + base, ap=[[pstep, 128], [1, L]]), nodep_out=nodep)
        for h in range(6):
            expand(self.Ubig, h * LU, h, 0, 16, LU, nodep=(h > 0))
        for h in range(3):
            expand(self.TTw[h], 0, h, LC, 1, LTW)
            expand(self.TTs[h], 0, h, LC + LW, 1, LTS)

    def load_kv_piece(self, s, slot, c0):
        self.P.dma("sp", s, s[0:64, :], self.kvT, self.kvT[slot * 64:(slot + 1) * 64, c0:c0 + 1024])

    def load_v_piece(self, piece):
        P = self.P
        c0 = piece * 1024
        s = self.rot("st", self.st)
        P.dma("sp", s, s[:, :].rearrange("p (c f) -> p c f", f=128), self.vtok,
              self.vtok[c0:c0 + 1024, :].rearrange("(c p) f -> p c f", p=128))
        sv = s[:, :].rearrange("p (c f) -> p c f", f=128)
        P.op("pool", lambda e, sv=sv, piece=piece: e.tensor_copy(
            out=self.vsaug[:, piece * 8:(piece + 1) * 8, 0:64], in_=sv[:, :, 0:64]), [s], [self.vsaug])
        P.op("pool", lambda e, sv=sv, piece=piece: e.tensor_copy(
            out=self.vwaug[:, piece * 8:(piece + 1) * 8, 0:64], in_=sv[:, :, 64:128]), [s], [self.vwaug])

    def setup_kv(self):
        P = self.P
        st = self.st
        for piece in range(8):
            c0 = piece * 1024
            for slot, gcol, dst in ((2, 2, self.ksaug), (4, 3, self.kwT)):
                s = self.rot("st", st)
                self.load_kv_piece(s, slot, c0)
                for hf in range(2):
                    a = hf * 512
                    self.norm64(s, s[0:64, a:a + 512], 512, self.hgs[0:64, gcol:gcol + 1], dst,
                                dst[0:64, c0 + a:c0 + a + 512])
            P.dma("pool", self.ksaug, self.ksaug[64:128, c0:c0 + 1024], self.EE, self.EE[:, c0:c0 + 1024])
            self.load_v_piece(piece)
        self.compress()

    def compress_rhs(self, kvi, tg):
        P = self.P
        for th in range(2):
            s = self.rot("st", self.st)
            t0 = tg * 4 + th * 2
            P.dma("sp", s, s[0:64, :].rearrange("p (t n) -> p t n", n=512), self.blk,
                  self.blk[kvi, :, t0:t0 + 2, :])
            for tt in range(2):
                t = t0 + tt
                P.op("dve", lambda e, s=s, tt=tt, th=th, t=t: e.tensor_scalar(
                    out=self.blkb[:, th * 2 + tt, :], in0=s[0:64, tt * 512:(tt + 1) * 512],
                    scalar1=self.posT[:, t:t + 1], scalar2=None, op0=ALU.add), [s, self.posT], [self.blkb])

    def compress_begin(self, kvi):
        pass

    def compress(self):
        P = self.P
        st = self.st
        H = [self.A[0], self.A[1]]
        for kvi in range(2):
            self.compress_begin(kvi)
            P.dma("sp", self.posT, self.posT[:], self.pos, self.pos[kvi])
            s = self.rot("st", st)
            P.dma("sp", s, s[:, 0:128].rearrange("p (c d) -> p c d", d=64), self.w2,
                  self.w2[kvi].rearrange("(c p) d -> p c d", p=128))
            P.op("pool", lambda e, s=s: e.tensor_copy(out=self.w2b[:].rearrange("p c d -> p (c d)"), in_=s[:, 0:128]),
                 [s], [self.w2b])
            for tg in range(8):
                s = self.rot("st", st)
                P.dma("sp", s, s[0:64, :].rearrange("p (t j) -> p t j", j=256), self.w1,
                      self.w1[kvi, tg * 256:(tg + 1) * 256, :].rearrange("(t d) j -> d t j", d=64))
                P.op("pool", lambda e, s=s: e.tensor_copy(out=self.w1b[:].rearrange("p t j -> p (t j)"),
                                                          in_=s[0:64, :]), [s], [self.w1b])
                self.compress_rhs(kvi, tg)
                for tl in range(4):
                    t = tg * 4 + tl
                    for jc in range(2):
                        P.op("pe", lambda e, tl=tl, jc=jc, t=t: e.matmul(
                            H[jc][:, :], lhsT=self.w1b[:, tl, jc * 128:(jc + 1) * 128], rhs=self.blkb[:, tl, :],
                            start=(t == 0), stop=(t == 31)), [self.w1b, self.blkb], [H[jc]])
            for jc in range(2):
                P.op("act", lambda e, jc=jc: e.activation(out=self.gx[:], in_=H[jc][:, :], func=AF.Copy),
                     [H[jc]], [self.gx])
                P.op("pool", lambda e: e.tensor_tensor(out=self.gy[:], in0=self.gx[:], in1=self.gx[:], op=ALU.mult),
                     [self.gx], [self.gy])
                P.op("dve", lambda e: e.tensor_scalar(out=self.gy[:], in0=self.gy[:], scalar1=0.044715, scalar2=1.0,
                                                      op0=ALU.mult, op1=ALU.add), [self.gy], [self.gy])
                P.op("pool", lambda e: e.tensor_tensor(out=self.gy[:], in0=self.gy[:], in1=self.gx[:], op=ALU.mult),
                     [self.gy, self.gx], [self.gy])
                P.op("act", lambda e: e.activation(out=self.gy[:], in_=self.gy[:], func=AF.Sigmoid,
                                                   scale=1.5957691216057308), [self.gy], [self.gy])
                P.op("dve", lambda e, jc=jc: e.tensor_tensor(out=self.gT[jc][:], in0=self.gx[:], in1=self.gy[:],
                                                             op=ALU.mult), [self.gx, self.gy], [self.gT[jc]])
            if kvi == 0:
                for jc in range(2):
                    P.op("pe", lambda e, jc=jc: e.matmul(self.X1[0:64, :], lhsT=self.w2b[:, jc, :], rhs=self.gT[jc][:],
                                                         start=(jc == 0), stop=(jc == 1)),
                         [self.w2b, self.gT[jc]], [self.X1])
                s = self.rot("st", st)
                P.op("act", lambda e, s=s: e.activation(out=s[0:64, 0:512], in_=self.X1[0:64, :], func=AF.Copy),
                     [self.X1], [s])
                self.norm64(s, s[0:64, 0:512], 512, self.hgs[0:64, 1:2], self.kcT, self.kcT[0:64, :])
            else:
                for cc in range(4):
                    for jc in range(2):
                        P.op("pe", lambda e, jc=jc, cc=cc: e.matmul(
                            self.X1[:, cc * 64:(cc + 1) * 64], lhsT=self.gT[jc][:, cc * 128:(cc + 1) * 128],
                            rhs=self.w2b[:, jc, :], start=(jc == 0), stop=(jc == 1)),
                            [self.w2b, self.gT[jc]], [self.X1])
                    P.op("act", lambda e, cc=cc: e.activation(out=self.vcaug[:, cc, 0:64],
                                                              in_=self.X1[:, cc * 64:(cc + 1) * 64], func=AF.Copy),
                         [self.X1], [self.vcaug])

    def combine(self, qt, h, br, first):
        P = self.P
        A = self.A[h]
        s0 = qt * QT
        gt = self.load_gate(qt, h, br)
        P.op("act", lambda e: e.activation(out=self.rdt[:], in_=A[64:128, :], func=AF.Ln, bias=self.tinyT[0:64, :]),
             [A, self.tinyT], [self.rdt])
        P.op("act", lambda e: e.activation(out=self.rdt[:], in_=self.rdt[:], func=AF.Exp, scale=-1.0),
             [self.rdt], [self.rdt])
        P.op("pool", lambda e: e.tensor_tensor(out=self.wt[:], in0=self.rdt[:], in1=gt[:], op=ALU.mult),
             [self.rdt, gt], [self.wt])
        if first:
            P.op("dve", lambda e: e.tensor_tensor(out=self.y[:, h, :], in0=A[0:64, :], in1=self.wt[:], op=ALU.mult),
                 [A, self.wt], [self.y])
        else:
            P.op("dve", lambda e: e.tensor_tensor(out=self.tm[:], in0=A[0:64, :], in1=self.wt[:], op=ALU.mult),
                 [A, self.wt], [self.tm])
            P.op("pool", lambda e: e.tensor_tensor(out=self.y[:, h, :], in0=self.y[:, h, :], in1=self.tm[:],
                                                   op=ALU.add), [self.y, self.tm], [self.y])

    def exp(self, S, pT, bias):
        P = self.P
        if bias is None:
            P.op("act", lambda e: e.activation(out=pT[:], in_=S[:, :], func=AF.Exp, scale=0.125), [S], [pT])
        else:
            bap, bt = bias
            P.op("act", lambda e: e.activation(out=pT[:], in_=S[:, :], func=AF.Exp, scale=0.125, bias=bap),
                 [S, bt], [pT])

    def cmp_bias(self, cc, h, mixed):
        return None if mixed else (self.cf[:, h:h + 1], self.cf)

    def sel_bias(self, kc, h, near):
        return None if near else (self.cf[:, h:h + 1], self.cf)

    def win_bias(self, c):
        return None

    def topc_ap(self, qt):
        return self.topc[qt]

    def load_q(self, qt):
        s0 = qt * QT
        self.P.dma("sp", self.qraw, self.qraw[:], self.qT, self.qT[:, s0:s0 + QT].rearrange("(h d) t -> d h t", d=64))

    def load_gate(self, qt, h, br):
        s0 = qt * QT
        gt = self.rot("gt", self.gt)
        g = self.gB[:]
        self.P.dma("sp", gt, gt[:], self.gB,
                   bass.AP(tensor=g.tensor, offset=(h * 3 + br) * SEQ + s0, ap=[[0, 64], [1, 512]]))
        return gt

    def store_y(self, qt):
        s0 = qt * QT
        self.P.dma("sp", self.yT, self.yT[:, s0:s0 + QT].rearrange("(h d) t -> d h t", d=64), self.y, self.y[:],
                   nodep_out=True)

    def qtile(self, qt):
        P = self.P
        s0 = qt * QT
        self.load_q(qt)
        P.dma("sp", self.tcs, self.tcs[:], self.topc, self.topc_ap(qt))
        for h in range(6):
            self.norm64(self.qraw, self.qraw[:, h, :], 512, self.hgs[0:64, 0:1], self.qn, self.qn[0:64, h, :])
        for h in range(3):
            for half in range(2):
                P.op("pool", lambda e, h=h, half=half: e.tensor_copy(out=self.qaug[0:64, h, half, :],
                                                                     in_=self.qn[0:64, h, :]), [self.qn], [self.qaug])
        for h in range(6):
            chunks = [cc for cc in range(4) if qt - 4 * cc >= 0]
            pts = []
            for cc in chunks:
                k = qt - 4 * cc
                mixed = k <= 5
                S = self.rot("S", self.S)
                P.op("pe", lambda e, S=S, cc=cc, h=h, mixed=mixed: e.matmul(
                    S[:, :], lhsT=self.kcT[:, cc * 128:(cc + 1) * 128], rhs=self.qn[:, h, :], start=True,
                    stop=(not mixed)), [self.kcT, self.qn], [S])
                if mixed:
                    mo = 512 * (5 - k)
                    P.op("pe", lambda e, S=S, h=h, mo=mo: e.matmul(S[:, :], lhsT=self.ident[:],
                                                                   rhs=self.uslice(h, mo), start=False, stop=True),
                         [self.ident, self.Ubig], [S])
                pT = self.rot("pT", self.pT)
                self.exp(S, pT, self.cmp_bias(cc, h, mixed))
                pts.append(pT)
            n = len(chunks)
            if h < 3:
                for i, (cc, pT) in enumerate(zip(chunks, pts)):
                    P.op("pe", lambda e, pT=pT, i=i, cc=cc, h=h: e.matmul(
                        self.A[h][:, :], lhsT=self.vcaug[:, cc, :], rhs=pT[:], start=(i == 0), stop=(i == n - 1)),
                        [self.vcaug, pT], [self.A[h]])
            for ss in range(4):
                for i, (cc, pT) in enumerate(zip(chunks, pts)):
                    P.op("pe", lambda e, pT=pT, i=i, cc=cc, ss=ss: e.matmul(
                        self.X1[:, ss * 128:(ss + 1) * 128], lhsT=pT[:, ss * 128:(ss + 1) * 128], rhs=self.ovl[:, cc, :],
                        start=(i == 0), stop=(i == n - 1)), [pT, self.ovl], [self.X1])
            for ss in range(4):
                for i, pT in enumerate(pts):
                    P.op("pe", lambda e, pT=pT, i=i, ss=ss: e.matmul(
                        self.X0[:, ss:ss + 1], lhsT=pT[:, ss * 128:(ss + 1) * 128], rhs=self.ones[:, 0:1],
                        start=(i == 0), stop=(i == n - 1)), [pT, self.ones], [self.X0])
            P.op("dve", lambda e: e.tensor_scalar(out=self.rdq[:], in0=self.X0[:, 0:4], scalar1=1e-30, scalar2=None,
                                                  op0=ALU.max), [self.X0], [self.rdq])
            P.op("dve", lambda e: e.reciprocal(out=self.rdq[:], in_=self.rdq[:]), [self.rdq], [self.rdq])
            for ss in range(4):
                if h == 0:
                    P.op("dve", lambda e, ss=ss: e.tensor_scalar(
                        out=self.impacc[:, ss * 128:(ss + 1) * 128], in0=self.X1[:, ss * 128:(ss + 1) * 128],
                        scalar1=self.rdq[:, ss:ss + 1], scalar2=None, op0=ALU.mult), [self.X1, self.rdq], [self.impacc])
                else:
                    P.op("dve", lambda e, ss=ss: e.scalar_tensor_tensor(
                        out=self.impacc[:, ss * 128:(ss + 1) * 128], in0=self.X1[:, ss * 128:(ss + 1) * 128],
                        scalar=self.rdq[:, ss:ss + 1], in1=self.impacc[:, ss * 128:(ss + 1) * 128],
                        op0=ALU.mult, op1=ALU.add), [self.X1, self.rdq, self.impacc], [self.impacc])
            if h < 3:
                self.combine(qt, h, 0, True)
        for ss in range(4):
            tA = self.tcs[:, ss * 256: ss * 256 + 128]
            tB = self.tcs[:, ss * 256 + 128: ss * 256 + 256]
            P.op("dve", lambda e, ss=ss, tA=tA: e.tensor_tensor(out=self.sc[:], in0=self.impacc[:, ss * 128:(ss + 1) * 128],
                                                                in1=tA, op=ALU.mult), [self.impacc, self.tcs], [self.sc])
            P.op("dve", lambda e, tB=tB: e.tensor_tensor(out=self.sc[:], in0=self.sc[:], in1=tB, op=ALU.add),
                 [self.sc, self.tcs], [self.sc])
            P.op("dve", lambda e: e.max(out=self.mx[:, 0:8], in_=self.sc[:]), [self.sc], [self.mx])
            P.op("dve", lambda e: e.match_replace(out=self.sc2[:], in_to_replace=self.mx[:, 0:8], in_values=self.sc[:],
                                                  imm_value=-1e30), [self.sc, self.mx], [self.sc2])
            P.op("dve", lambda e: e.max(out=self.mx[:, 8:16], in_=self.sc2[:]), [self.sc2], [self.mx])
            P.op("dve", lambda e: e.tensor_scalar(out=self.mneg[:], in0=self.sc[:], scalar1=self.mx[:, 15:16], scalar2=1.0,
                                                  op0=ALU.is_ge, op1=ALU.subtract), [self.sc, self.mx], [self.mneg])
            P.op("pe", lambda e, ss=ss: e.transpose(self.XT[:, ss * 128:(ss + 1) * 128], self.mneg[:], self.ident[:]),
                 [self.mneg, self.ident], [self.XT])
        for h in range(3):
            for half in range(2):
                eng = "act" if (h + half) % 2 == 0 else "dve"
                if eng == "act":
                    P.op("act", lambda e, h=h, half=half: e.activation(out=self.qaug[64:128, h, half, :],
                                                                       in_=self.XT[half * 64:(half + 1) * 64, :],
                                                                       func=AF.Copy), [self.XT], [self.qaug])
                else:
                    P.op("dve", lambda e, h=h, half=half: e.tensor_copy(out=self.qaug[64:128, h, half, :],
                                                                        in_=self.XT[half * 64:(half + 1) * 64, :]),
                         [self.XT], [self.qaug])
        nkc = 4 * qt + 4

        def sel_qk(kc, h):
            k0 = 128 * kc
            ep = s0 + 511 - k0
            near = ep <= DS
            half = kc // 32
            S = self.rot("S4", self.S4)
            P.op("pe", lambda e: e.matmul(S[:, :], lhsT=self.ksaug[:, k0:k0 + 128], rhs=self.qaug[:, h, half, :],
                                          start=True, stop=(not near)), [self.ksaug, self.qaug], [S])
            if near:
                mo = DS - ep
                P.op("pe", lambda e: e.matmul(S[:, :], lhsT=self.ident[:], rhs=self.TTs[h][:, mo:mo + 512],
                                              start=False, stop=True), [self.ident, self.TTs[h]], [S])
            return (kc, h, near, S)

        def sel_rest(kc, h, near, S):
            pT = self.rot("pT", self.pT)
            self.exp(S, pT, self.sel_bias(kc, h, near))
            P.op("pe", lambda e: e.matmul(self.A[h][:, :], lhsT=self.vsaug[:, kc, :], rhs=pT[:],
                                          start=(kc == 0), stop=(kc == nkc - 1)), [self.vsaug, pT], [self.A[h]])

        pend = []
        for kc in range(nkc):
            for h in range(3):
                pend.append(sel_qk(kc, h))
                if len(pend) > PIPE:
                    sel_rest(*pend.pop(0))
        while pend:
            sel_rest(*pend.pop(0))
        for h in range(3):
            self.combine(qt, h, 1, False)
        rs = [r for r in range(8) if s0 - 512 + 128 * r >= 0]

        def win_qk(i, r, h):
            k0 = s0 - 512 + 128 * r
            mo = 128 * r
            S = self.rot("S4", self.S4)
            P.op("pe", lambda e: e.matmul(S[:, :], lhsT=self.kwT[:, k0:k0 + 128], rhs=self.qn[:, h, :],
                                          start=True, stop=False), [self.kwT, self.qn], [S])
            P.op("pe", lambda e: e.matmul(S[:, :], lhsT=self.ident[:], rhs=self.TTw[h][:, mo:mo + 512],
                                          start=False, stop=True), [self.ident, self.TTw[h]], [S])
            return (i, k0, h, S)

        def win_rest(i, k0, h, S):
            pT = self.rot("pT", self.pT)
            self.exp(S, pT, self.win_bias(k0 // 128))
            P.op("pe", lambda e: e.matmul(self.A[h][:, :], lhsT=self.vwaug[:, k0 // 128, :], rhs=pT[:],
                                          start=(i == 0), stop=(i == len(rs) - 1)), [self.vwaug, pT], [self.A[h]])

        pend = []
        for i, r in enumerate(rs):
            for h in range(3):
                pend.append(win_qk(i, r, h))
                if len(pend) > PIPE:
                    win_rest(*pend.pop(0))
        while pend:
            win_rest(*pend.pop(0))
        for h in range(3):
            self.combine(qt, h, 2, False)
        self.store_y(qt)

    def build(self, nqt=NQT):
        self.setup()
        for qt in range(nqt):
            self.qtile(qt)
        self.P.wait_all("sp", [self.yT])
        self.P.finalize()


def build_attn(nqt=NQT):
    nc = bass.Bass("TRN2", target_bir_lowering=False)
    P = Prog(nc)
    a = Attn(P)
    a.build(nqt)
    return nc, P


XPAD = SEQ + HALO


def rev_ap(ap, n):
    dims = [list(d) for d in ap.ap]
    dims[-1] = [-1, n]
    return bass.AP(tensor=ap.tensor, offset=ap.offset + (n - 1), ap=dims)


class FTok(TokStage):
    def __init__(self, P, mode, dr):
        self.P = P
        self.mode = mode
        self.dr = dr
        for k in ("memT", "gains", "hg", "mem_w_kv", "w_out", "w_up", "w_down", "b_w_in", "a_w_in", "convw", "w_kv"):
            setattr(self, k, dr[k])
        self.xsrc = dr["xin"] if mode == "L1" else dr["X"]
        self.kvT_o, self.qT_o, self.gT_o = dr["KV"], dr["Q"], dr["G"]
        self.ytokT = dr["Y"]
        self.chunk = 0
        self.alloc()

    def halo_fix(self, sub):
        if self.chunk == 0:
            self.P.op("pool", lambda e: e.memset(self.v[:, sub, 2:2 + HALO], 0.0), [], [self.v])

    def store_tile(self, dstT, r, msz, c0, t0, w, ost):
        a = c0 + t0
        lo = HALO if a == 0 else 0
        g0 = NTOK * self.chunk + a + lo - HALO
        self.P.dma("sp", dstT, dstT[r:r + msz, g0:g0 + (w - lo)], ost, ost[0:msz, lo:w], nodep_out=True)

    def ytok_src(self, ch, c0, t0, w):
        g0 = NTOK * self.chunk + c0 + t0
        return self.ytokT[ch * 128:(ch + 1) * 128, g0:g0 + w]

    def setup_consts(self):
        P = self.P
        P.dma("sp", self.g, self.g[:], self.gains, self.gains[:])
        P.dma("sp", self.hgs, self.hgs[:], self.hg, self.hg[:])
        if self.mode == "L1":
            P.dma("sp", self.cw, self.cw[:], self.convw, self.convw[:])
        for c in range(8):
            P.dma("sp", self.u, self.memx(c, 0, 256), self.memT, self.memT[c * 128:(c + 1) * 128, :], nodep_out=True)
        P.op("pool", lambda e: e.memset(self.ones[:], 1.0), [], [self.ones])
        P.op("pool", lambda e: e.memset(self.bd[:], 0.0), [], [self.bd])
        P.op("pool", lambda e: e.memset(self.bd[0:64, 0:64], 1.0), [], [self.bd])
        P.op("pool", lambda e: e.memset(self.bd[64:128, 64:128], 1.0), [], [self.bd])
        P.op("pool", lambda e: e.memset(self.epsT[:], EPS), [], [self.epsT])
        P.op("pool", lambda e: e.memset(self.vaug[:], 1.0), [], [self.vaug])
        self.norm(self.u, self.memx, 256, [(0, 256)], 15, self.memh, 0)

    def load_x(self):
        P = self.P
        g0 = NTOK * self.chunk
        for c in range(8):
            P.dma("sp", self.x, self.x[:, c, :], self.xsrc, self.xsrc[c * 128:(c + 1) * 128, g0:g0 + NCOL],
                  nodep_out=(c > 0))

    def store_x(self):
        P = self.P
        dst = self.dr["out"] if self.mode == "L5" else self.dr["X"]
        off = 0 if self.mode == "L5" else HALO
        g0 = NTOK * self.chunk + off
        for c in range(8):
            P.dma("sp", dst, dst[c * 128:(c + 1) * 128, g0:g0 + NTOK], self.x, self.x[:, c, HALO:NCOL], nodep_out=True)

    def run(self):
        self.setup_consts()
        for chunk in range(4):
            self.chunk = chunk
            self.load_x()
            if self.mode == "L1":
                for l in range(2):
                    self.mem_kv(l)
                    for ri, (c0, n) in enumerate(RANGES):
                        tiles = tiles_of(n)
                        self.norm(self.x, self.xs(c0), n, tiles, l, self.h, 0)
                        self.mix_a(l, ri, c0, n, tiles)
                        self.mem_attn(l, n, tiles)
                        self.out_proj(l, c0, n, tiles)
                        self.mlp(l, c0, n, tiles)
                for ri, (c0, n) in enumerate(RANGES):
                    tiles = tiles_of(n)
                    self.norm(self.x, self.xs(c0), n, tiles, 8, self.h, 0)
                    for i in range(3):
                        self.store_proj(self.w_kv, self.w_kv[:, 256 * i:256 * (i + 1)], 256, self.kvT_o, 256 * i, c0, tiles)
                    if not getattr(self, "skip_pre0", False):
                        self.pre_b(0, c0, n, tiles)
            else:
                j = 0 if self.mode == "L3" else 1
                self.mem_kv(2 + j)
                for ri, (c0, n) in enumerate(RANGES):
                    self.post_b(j, c0, n, tiles_of(n))
                if j == 0:
                    for ri, (c0, n) in enumerate(RANGES):
                        self.pre_b(1, c0, n, tiles_of(n))
            self.store_x()


class FAttn(Attn):
    def __init__(self, P, dr, jl):
        self.P = P
        self.dr = dr
        self.jl = jl
        self.Q, self.G, self.KV, self.Y = dr["Q"], dr["G"], dr["KV"], dr["Y"]
        self.hg = dr["ahg%d" % jl]
        self.pos, self.w1, self.w2 = dr["pos"], dr["cmp_w1"], dr["cmp_w2"]
        self.rel = dr["rel_bias"]
        self.OH, self.identD, self.EE, self.ovlD, self.topc = dr["OH"], dr["ident"], dr["EE"], dr["ovl"], dr["topc"]
        self.Rrow = dr["Rrow"]
        self.alloc()

    def alloc(self):
        Attn.alloc(self)
        self.ub_view = self.Ubig[0:64, 0:16 * 513].rearrange("p (r n) -> p r n", n=513)

    def set_combo(self, g, tr):
        self.g, self.tr = g, tr
        own = [3 * tr + i for i in range(3)]
        oth = [3 * (1 - tr) + i for i in range(3)]
        self.heads = own + oth

    def load_rb(self):
        P = self.P
        rel = self.rel[:]
        for i, hh in enumerate(self.heads):
            col = self.g * 6 + hh
            P.dma("sp", self.cf, self.cf[:, i:i + 1], self.rel,
                  bass.AP(tensor=rel.tensor, offset=31 * 12 + col, ap=[[0, 128], [1, 1]]))
            P.dma("sp", self.tab, self.tab[0:32, i:i + 1], self.rel, self.rel[:, col:col + 1],
                  allow_slow_non_contiguous=True)

    def load_kv_piece(self, s, slot, c0):
        r0 = slot * 128 + self.g * 64
        self.P.dma("sp", s, s[0:64, :], self.KV, self.KV[r0:r0 + 64, c0:c0 + 1024])

    def load_v_piece(self, piece):
        P = self.P
        c0 = piece * 1024
        for slot, dst in ((3, self.vsaug), (5, self.vwaug)):
            r0 = slot * 128 + self.g * 64
            for hf in range(2):
                sq = self.rot("sq", self.sq)
                a = c0 + hf * 512
                P.dma("pool", sq, sq[0:64, :], self.KV, self.KV[r0:r0 + 64, a:a + 512])
                for c in range(4):
                    P.op("pe", lambda e, sq=sq, c=c: e.transpose(self.XT[:, c * 64:(c + 1) * 64],
                                                                 sq[0:64, c * 128:(c + 1) * 128], self.ident[0:64, 0:64]),
                         [sq, self.ident], [self.XT])
                ch0 = piece * 8 + hf * 4
                P.op("act", lambda e, dst=dst, ch0=ch0: e.activation(
                    out=dst[:, ch0:ch0 + 4, 0:64], in_=self.XT[:, 0:256].rearrange("p (c d) -> p c d", d=64),
                    func=AF.Copy), [self.XT], [dst])

    def compress_begin(self, kvi):
        P = self.P
        P.op("pool", lambda e: e.memset(self.ub_view[:, :, 512:513], 0.0), [], [self.Ubig])
        for piece in range(8):
            s = self.rot("st", self.st)
            self.load_kv_piece(s, kvi, piece * 1024)
            P.op("pool", lambda e, s=s, piece=piece: e.tensor_copy(
                out=self.ub_view[:, :, piece * 64:(piece + 1) * 64],
                in_=s[0:64, :].rearrange("p (n r) -> p r n", r=16)), [s], [self.Ubig])

    def compress_rhs(self, kvi, tg):
        P = self.P
        for tl in range(4):
            t = tg * 4 + tl
            r, off = t % 16, t // 16
            P.op("dve", lambda e, tl=tl, t=t, r=r, off=off: e.tensor_scalar(
                out=self.blkb[:, tl, :], in0=self.ub_view[:, r, off:off + 512], scalar1=self.posT[:, t:t + 1],
                scalar2=None, op0=ALU.add), [self.Ubig, self.posT], [self.blkb])

    def load_q(self, qt):
        P = self.P
        s0 = qt * QT
        for i2 in range(3):
            s = self.rot("st", self.st)
            for k in range(2):
                hh = self.heads[2 * i2 + k]
                r0 = (self.g * 6 + hh) * 64
                P.dma("sp", s, s[0:64, k * 512:(k + 1) * 512], self.Q, self.Q[r0:r0 + 64, s0:s0 + QT], nodep_out=(k > 0))
            for k in range(2):
                P.op("dve", lambda e, s=s, k=k, i2=i2: e.tensor_copy(
                    out=self.qraw[:, 2 * i2 + k, :], in_=rev_ap(s[0:64, k * 512:(k + 1) * 512], 512)), [s], [self.qraw])

    def load_gate(self, qt, h, br):
        P = self.P
        s0 = qt * QT
        gt = self.rot("gt", self.gt)
        g = self.G[:]
        row = (self.g * 6 + 3 * self.tr + h) * 3 + br
        P.dma("sp", gt, gt[:], self.G, bass.AP(tensor=g.tensor, offset=row * SEQ + s0, ap=[[0, 64], [1, 512]]))
        gr = self.rot("gtr", self.gtr)
        P.op("dve", lambda e: e.tensor_copy(out=gr[:], in_=rev_ap(gt[:], 512)), [gt], [gr])
        return gr

    def store_y(self, qt):
        P = self.P
        s0 = qt * QT
        bufs = [self.rdt, self.wt, self.tm]
        for h in range(3):
            b = bufs[h]
            P.op("dve", lambda e, b=b, h=h: e.tensor_copy(out=b[:], in_=rev_ap(self.y[:, h, :], 512)), [self.y], [b])
            r0 = (self.g * 6 + 3 * self.tr + h) * 64
            P.dma("sp", self.Y, self.Y[r0:r0 + 64, HALO + s0:HALO + s0 + QT], b, b[:], nodep_out=True)

    def run(self):
        P = self.P
        st = self.st
        self.gtr = [P.sb("gtr%d" % i, [64, 512], F32) for i in range(2)]
        P.dma("sp", self.hgs, self.hgs[:], self.hg, self.hg[:])
        P.op("pool", lambda e: e.memset(self.ones[:], 1.0), [], [self.ones])
        P.op("pool", lambda e: e.memset(self.epsT[:], 1e-6), [], [self.epsT])
        P.op("pool", lambda e: e.memset(self.tinyT[:], 1e-30), [], [self.tinyT])
        P.op("pool", lambda e: e.memset(self.vsaug[:], 1.0), [], [self.vsaug])
        P.op("pool", lambda e: e.memset(self.vwaug[:], 1.0), [], [self.vwaug])
        P.op("pool", lambda e: e.memset(self.vcaug[:], 1.0), [], [self.vcaug])
        s = self.rot("st", st)
        P.dma("sp", s, s[:, 0:128], self.identD, self.identD[:])
        P.op("pool", lambda e, s=s: e.tensor_copy(out=self.ident[:], in_=s[:, 0:128]), [s], [self.ident])
        s = self.rot("st", st)
        P.dma("sp", s, s[:, 0:512], self.ovlD, self.ovlD[:].rearrange("p c j -> p (c j)"))
        P.op("pool", lambda e, s=s: e.tensor_copy(out=self.ovl[:].rearrange("p c j -> p (c j)"), in_=s[:, 0:512]),
             [s], [self.ovl])
        for g in range(2):
            self.set_combo(g, 0)
            P.barrier()
            self.setup_kv()
            for tr in range(2):
                self.set_combo(g, tr)
                P.barrier()
                self.setup_tables()
                for qt in range(NQT):
                    self.qtile(qt)


def fused_dram(P, r3=False):
    dr = {}
    D = lambda n, s: P.dram(n, s, F32, kind="ExternalInput")
    dr["xin"] = D("xin", [1024, XPAD])
    dr["memT"] = D("memT", [1024, 256])
    dr["gains"] = D("gains", [128, 128])
    dr["hg"] = D("hg", [128, 16])
    dr["convw"] = D("convw", [128, 36])
    dr["mem_w_kv"] = D("mem_w_kv", [4, 1024, 512])
    dr["w_out"] = D("w_out", [4, 1024, 1024])
    dr["w_up"] = D("w_up", [4, 1024, 4096])
    dr["w_down"] = D("w_down", [4, 4096, 1024])
    dr["b_w_in"] = D("b_w_in", [2, 1024, 1060])
    dr["a_w_in"] = D("a_w_in", [2, 1024, 2560])
    dr["w_kv"] = D("w_kv", [1024, 768])
    dr["ahg0"] = D("ahg0", [128, 8])
    dr["ahg1"] = D("ahg1", [128, 8])
    dr["pos"] = D("pos", [2, 64, 32])
    dr["cmp_w1"] = D("cmp_w1", [2, 2048, 256])
    dr["cmp_w2"] = D("cmp_w2", [2, 256, 64])
    dr["rel_bias"] = D("rel_bias", [32, 12])
    dr["OH"] = D("OH", [33, LTOT])
    dr["ident"] = D("ident", [128, 128])
    dr["EE"] = D("EE", [64, SEQ])
    dr["ovl"] = D("ovl", [128, 4, 128])
    if not r3:
        dr["topc"] = D("topc", [NQT, 128, 1024])
    if not r3:
        dr["out"] = P.dram("xT_out", [1024, SEQ], F32, kind="ExternalOutput")
    I = lambda n, s: P.dram(n, s, F32, kind="Internal")
    dr["X"] = I("Xs", [1024, XPAD])
    if not r3:
        dr["KV"] = I("KVs", [768, SEQ])
    dr["Q"] = I("Qs", [768, SEQ])
    dr["G"] = I("Gs", [36, SEQ])
    dr["Y"] = I("Ys", [768, XPAD])
    dr["Rrow"] = I("Rrow", [6, LTOT])
    return dr


def build_fused():
    nc = bass.Bass("TRN2", target_bir_lowering=False)
    P = Prog(nc)
    dr = fused_dram(P)
    P.push_scope()
    FTok(P, "L1", dr).run()
    P.pop_scope()
    for j in range(2):
        P.push_scope()
        FAttn(P, dr, j).run()
        P.pop_scope()
        P.push_scope()
        FTok(P, "L3" if j == 0 else "L5", dr).run()
        P.pop_scope()
    P.wait_all("sp", [dr["out"]])
    P.finalize()
    return nc, P


KPAD = 1536
QTILES = (3, 7, 11, 15)


def host_core_tables(k):
    tc = np.zeros((4, 128, 4, 2, 128), np.float32)
    jj = np.arange(128)[None, :]
    for j, qt in enumerate(QTILES):
        for ss in range(4):
            ta = qt * QT + 511 - (128 * ss + np.arange(128))
            back = (ta // 64)[:, None] - jj
            J = jj - 24 + 8 * k
            valid = (J >= 0) & (back >= 0)
            forced = (J == 0) | ((back >= 0) & (back < 2))
            tc[j, :, ss, 0] = (valid & ~forced).astype(np.float32)
            tc[j, :, ss, 1] = np.where(valid, np.where(forced, 1e4, 0.0), -1e30).astype(np.float32)
    kval = np.zeros((128, 16), np.float32)
    first = KPAD - 512 * k
    for c in range(12):
        a = 128 * c + np.arange(128)
        kval[:, c] = np.where(a >= first, 0.0, MASKV)
    kval[:, 12] = np.where(16 * np.arange(128) >= first, 0.0, MASKV)
    return tc.reshape(4, 128, 1024), kval


class FTok2(FTok):
    def __init__(self, P, mode, dr):
        FTok.__init__(self, P, mode, dr)
        if mode != "L1":
            self.xsrc = dr["Xown"]
            self.ytokT = dr["Yown"]
            self.qT_o, self.gT_o = dr["Qown"], dr["Gown"]

    def conv_lead(self, l, ri, sub, ch, n):
        P = self.P
        cr = self.carry2[l]
        if self.chunk == 0 and ri == 0:
            P.op("pool", lambda e: e.memset(self.v[:, sub, 0:2], 0.0), [], [self.v])
        else:
            P.op("dve", lambda e: e.tensor_copy(out=self.v[:, sub, 0:2], in_=cr[:, ch, :]), [cr], [self.v])
        P.op("dve", lambda e: e.tensor_copy(out=cr[:, ch, :], in_=self.v[:, sub, n:n + 2]), [self.v], [cr])

    def store_tile(self, dstT, r, msz, c0, t0, w, ost):
        if self.mode == "L1":
            off = KPAD if dstT is self.dr["KV"] else 0
            g0 = NTOK * self.chunk + c0 + t0 + off
            self.P.dma("sp", dstT, dstT[r:r + msz, g0:g0 + w], ost, ost[0:msz, 0:w], nodep_out=True)
        else:
            self.P.dma("sp", dstT, dstT[r:r + msz, c0 + t0:c0 + t0 + w], ost, ost[0:msz, 0:w], nodep_out=True)

    def ytok_src(self, ch, c0, t0, w):
        if self.mode == "L1":
            return FTok.ytok_src(self, ch, c0, t0, w)
        return self.ytokT[ch * 128:(ch + 1) * 128, c0 + t0:c0 + t0 + w]

    def load_x(self):
        P = self.P
        if self.mode == "L1":
            g0 = NTOK * self.chunk + HALO
            for c in range(8):
                P.dma("sp", self.x, self.x[:, c, 0:NTOK], self.xsrc, self.xsrc[c * 128:(c + 1) * 128, g0:g0 + NTOK],
                      nodep_out=(c > 0))
            return
        for c in range(8):
            P.dma("sp", self.x, self.x[:, c, 0:NTOK], self.xsrc, self.xsrc[c * 128:(c + 1) * 128, :], nodep_out=(c > 0))

    def store_x(self):
        P = self.P
        if self.mode == "L1":
            X = self.dr["X"]
            g0 = NTOK * self.chunk + HALO
            for c in range(8):
                P.dma("sp", X, X[c * 128:(c + 1) * 128, g0:g0 + NTOK], self.x, self.x[:, c, 0:NTOK], nodep_out=True)
            return
        dst = self.dr["out"] if self.mode == "L5" else self.dr["Xown"]
        for c in range(8):
            P.dma("sp", dst, dst[c * 128:(c + 1) * 128, :], self.x, self.x[:, c, 0:NTOK], nodep_out=True)

    def zero_kv_pad(self):
        P = self.P
        z = self.ost[0]
        P.op("pool", lambda e: e.memset(z[:], 0.0), [], [z])
        KV = self.dr["KV"]
        for r in range(6):
            for cb in range(3):
                P.dma("sp", KV, KV[r * 128:(r + 1) * 128, cb * 512:(cb + 1) * 512], z, z[:], nodep_out=True)

    def run(self):
        if self.mode == "L1":
            self.carry2 = [self.P.sb("carry2_%d" % l, [128, 6, 2], F32) for l in range(2)]
            self.zero_kv_pad()
            self.setup_consts()
            R2 = [(0, 1024), (1024, 1024)]
            for chunk in range(4):
                self.chunk = chunk
                self.load_x()
                for l in range(2):
                    self.mem_kv(l)
                    for ri, (c0, n) in enumerate(R2):
                        tiles = tiles_of(n)
                        self.norm(self.x, self.xs(c0), n, tiles, l, self.h, 0)
                        self.mix_a(l, ri, c0, n, tiles)
                        self.mem_attn(l, n, tiles)
                        self.out_proj(l, c0, n, tiles)
                        self.mlp(l, c0, n, tiles)
                for ri, (c0, n) in enumerate(R2):
                    tiles = tiles_of(n)
                    self.norm(self.x, self.xs(c0), n, tiles, 8, self.h, 0)
                    for i in range(3):
                        self.store_proj(self.w_kv, self.w_kv[:, 256 * i:256 * (i + 1)], 256, self.kvT_o, 256 * i, c0, tiles)
                self.store_x()
            return
        self.setup_consts()
        self.chunk = 0
        self.load_x()
        if self.mode == "P0":
            for (c0, n) in [(0, 1024), (1024, 1024)]:
                self.pre_b(0, c0, n, tiles_of(n))
            return
        j = 0 if self.mode == "L3" else 1
        self.mem_kv(2 + j)
        R2 = [(0, 1024), (1024, 1024)]
        for (c0, n) in R2:
            self.post_b(j, c0, n, tiles_of(n))
        if j == 0:
            for (c0, n) in R2:
                self.pre_b(1, c0, n, tiles_of(n))
        self.store_x()


def extract_own(P, dr):
    kd = dr["kd"]

    def dyn3(src_t, r0, r1, pad, ncols):
        base = src_t[r0:r1, pad:pad + KPAD + 6144 + 512][:, bass.ds(kd, 6144 + 512)]
        return bass.AP(tensor=base.tensor, offset=base.offset, ap=[[ncols, r1 - r0], [2048, 4], [1, 512]])
    KVp, KVw = dr["KV"], dr["KVwin"]
    for rb in range(3):
        P.dma("sp", KVw, KVw[rb * 256:(rb + 1) * 256, :], KVp,
              KVp[rb * 256:(rb + 1) * 256, 0:KPAD + SEQ][:, bass.ds(kd, SEQ)], nodep_out=True)
    for rb in range(4):
        P.dma("sp", dr["Xown"], dr["Xown"][rb * 256:(rb + 1) * 256, :].rearrange("r (s c) -> r s c", c=512), dr["X"],
              dyn3(dr["X"], rb * 256, (rb + 1) * 256, HALO, XPAD), nodep_out=True)


class FAttn2(FAttn):
    def __init__(self, P, dr, jl):
        FAttn.__init__(self, P, dr, jl)
        self.KV = dr["KVwin"]
        self.Q, self.G, self.Y = dr["Qown"], dr["Gown"], dr["Yown"]
        self.kvalD = dr["kval"]

    def load_q(self, qt):
        P = self.P
        c0 = ((qt - 3) // 4) * 512
        for i2 in range(3):
            s = self.rot("st", self.st)
            for k in range(2):
                hh = self.heads[2 * i2 + k]
                r0 = (self.g * 6 + hh) * 64
                P.dma("sp", s, s[0:64, k * 512:(k + 1) * 512], self.Q, self.Q[r0:r0 + 64, c0:c0 + 512], nodep_out=(k > 0))
            for k in range(2):
                P.op("dve", lambda e, s=s, k=k, i2=i2: e.tensor_copy(
                    out=self.qraw[:, 2 * i2 + k, :], in_=rev_ap(s[0:64, k * 512:(k + 1) * 512], 512)), [s], [self.qraw])

    def load_gate(self, qt, h, br):
        P = self.P
        c0 = ((qt - 3) // 4) * 512
        gt = self.rot("gt", self.gt)
        g = self.G[:]
        row = (self.g * 6 + 3 * self.tr + h) * 3 + br
        P.dma("sp", gt, gt[:], self.G, bass.AP(tensor=g.tensor, offset=row * NTOK + c0, ap=[[0, 64], [1, 512]]))
        gr = self.rot("gtr", self.gtr)
        P.op("dve", lambda e: e.tensor_copy(out=gr[:], in_=rev_ap(gt[:], 512)), [gt], [gr])
        return gr

    def store_y(self, qt):
        P = self.P
        c0 = ((qt - 3) // 4) * 512
        bufs = [self.rdt, self.wt, self.tm]
        for h in range(3):
            b = bufs[h]
            P.op("dve", lambda e, b=b, h=h: e.tensor_copy(out=b[:], in_=rev_ap(self.y[:, h, :], 512)), [self.y], [b])
            r0 = (self.g * 6 + 3 * self.tr + h) * 64
            P.dma("sp", self.Y, self.Y[r0:r0 + 64, c0:c0 + 512], b, b[:], nodep_out=True)

    def topc_ap(self, qt):
        return self.topc[(qt - 3) // 4]

    def cmp_bias(self, cc, h, mixed):
        if cc == 0:
            return (self.kval[:, 12:13], self.kval) if mixed else (self.cfc[:, h:h + 1], self.cfc)
        return None if mixed else (self.cf[:, h:h + 1], self.cf)

    def sel_bias(self, kc, h, near):
        if kc < 12:
            return (self.kval[:, kc:kc + 1], self.kval) if near else (self.vbf[:, kc, h:h + 1], self.vbf)
        return None if near else (self.cf[:, h:h + 1], self.cf)

    def win_bias(self, c):
        return (self.kval[:, c:c + 1], self.kval) if c < 12 else None

    def build_rrow(self):
        P = self.P
        st = self.st
        tab = self.tab12
        P.op("pool", lambda e: e.memset(tab[:], MASKV), [], [tab])
        P.dma("sp", tab, tab[0:32, :], self.rel, self.rel[:, :])
        P.op("dve", lambda e: e.tensor_scalar(out=tab[0:32, :], in0=tab[0:32, :], scalar1=8.0, scalar2=None, op0=ALU.mult),
             [tab], [tab])
        R = self.dr["Rrow12"]
        o = 0
        while o < LTOT:
            w = min(512, LTOT - o)
            s = self.rot("st", st)
            P.dma("sp", s, s[0:33, 0:w], self.OH, self.OH[:, o:o + w])
            P.op("pe", lambda e, s=s, w=w: e.matmul(self.X1[0:12, 0:w], lhsT=tab[:, :], rhs=s[0:33, 0:w],
                                                    start=True, stop=True), [tab, s], [self.X1])
            s2 = self.rot("st", st)
            P.op("act", lambda e, s2=s2, w=w: e.activation(out=s2[0:12, 0:w], in_=self.X1[0:12, 0:w], func=AF.Copy),
                 [self.X1], [s2])
            P.dma("sp", R, R[:, o:o + w], s2, s2[0:12, 0:w], nodep_out=True)
            o += w

    def setup_tables(self):
        P = self.P
        R = self.dr["Rrow12"]
        rr = R[:]
        rel = self.rel[:]
        for i, hh in enumerate(self.heads):
            col = self.g * 6 + hh
            P.dma("sp", self.cf, self.cf[:, i:i + 1], self.rel,
                  bass.AP(tensor=rel.tensor, offset=31 * 12 + col, ap=[[0, 128], [1, 1]]))

        def expand(dst, d0, row, base, pstep, L, nodep=False):
            P.dma("pool", dst, dst[:, d0:d0 + L], R,
                  bass.AP(tensor=rr.tensor, offset=row * LTOT + base, ap=[[pstep, 128], [1, L]]), nodep_out=nodep)
        for i, hh in enumerate(self.heads):
            row = self.g * 6 + hh
            expand(self.Ubig, i * LU, row, 0, 16, LU, nodep=(i > 0))
            if i < 3:
                expand(self.TTw[i], 0, row, LC, 1, LTW)
                expand(self.TTs[i], 0, row, LC + LW, 1, LTS)
        for c in range(12):
            P.op("dve", lambda e, c=c: e.tensor_scalar(out=self.vbf[:, c, :], in0=self.cf[:, 0:3],
                                                       scalar1=self.kval[:, c:c + 1], scalar2=None, op0=ALU.add),
                 [self.cf, self.kval], [self.vbf])
        P.op("dve", lambda e: e.tensor_scalar(out=self.cfc[:], in0=self.cf[:], scalar1=self.kval[:, 12:13], scalar2=None,
                                              op0=ALU.add), [self.cf, self.kval], [self.cfc])

    def run(self):
        P = self.P
        st = self.st
        self.gtr = [P.sb("gtr%d" % i, [64, 512], F32) for i in range(2)]
        self.kval = P.sb("kval", [128, 16], F32)
        self.tab12 = P.sb("tab12", [33, 12], F32)
        self.vbf = P.sb("vbf", [128, 12, 3], F32)
        self.cfc = P.sb("cfc", [128, 6], F32)
        P.dma("sp", self.kval, self.kval[:], self.kvalD, self.kvalD[:])
        P.dma("sp", self.hgs, self.hgs[:], self.hg, self.hg[:])
        P.op("pool", lambda e: e.memset(self.ones[:], 1.0), [], [self.ones])
        P.op("pool", lambda e: e.memset(self.epsT[:], 1e-6), [], [self.epsT])
        P.op("pool", lambda e: e.memset(self.tinyT[:], 1e-30), [], [self.tinyT])
        P.op("pool", lambda e: e.memset(self.vsaug[:], 1.0), [], [self.vsaug])
        P.op("pool", lambda e: e.memset(self.vwaug[:], 1.0), [], [self.vwaug])
        P.op("pool", lambda e: e.memset(self.vcaug[:], 1.0), [], [self.vcaug])
        s = self.rot("st", st)
        P.dma("sp", s, s[:, 0:128], self.identD, self.identD[:])
        P.op("pool", lambda e, s=s: e.tensor_copy(out=self.ident[:], in_=s[:, 0:128]), [s], [self.ident])
        s = self.rot("st", st)
        P.dma("sp", s, s[:, 0:512], self.ovlD, self.ovlD[:].rearrange("p c j -> p (c j)"))
        P.op("pool", lambda e, s=s: e.tensor_copy(out=self.ovl[:].rearrange("p c j -> p (c j)"), in_=s[:, 0:512]),
             [s], [self.ovl])
        for z in (self.kwT, self.kcT):
            P.op("pool", lambda e, z=z: e.memset(z[64:128, :], 0.0), [], [z])
        P.op("pool", lambda e: e.memset(self.qn[64:128, :, :], 0.0), [], [self.qn])
        kvt = [("ksaug", self.ksaug, [128, SEQ]), ("kwT", self.kwT, [64, SEQ]), ("vsaug", self.vsaug, [128, 64 * 128]),
               ("vwaug", self.vwaug, [128, 64 * 128]), ("kcT", self.kcT, [64, 512]), ("vcaug", self.vcaug, [128, 4 * 128])]
        if self.jl == 0:
            self.build_rrow()
        for g in range(2):
            self.set_combo(g, 0)
            if self.jl == 0:
                self.setup_kv()
                for nm, tl, shp in kvt:
                    key = "kvc_%s_%d" % (nm, g)
                    if key not in self.dr:
                        self.dr[key] = P.dram(key, shp, BF16, kind="Internal")
                    c = self.dr[key]
                    src = tl[0:shp[0], :] if len(tl[:].shape) == 2 else tl[:].rearrange("p a b -> p (a b)")
                    P.dma("sp", c, c[:], tl, src)
            else:
                for nm, tl, shp in kvt:
                    c = self.dr["kvc_%s_%d" % (nm, g)]
                    dst = tl[0:shp[0], :] if len(tl[:].shape) == 2 else tl[:].rearrange("p a b -> p (a b)")
                    P.dma("sp", tl, dst, c, c[:])
            for tr in range(2):
                self.set_combo(g, tr)
                self.setup_tables()
                for qt in QTILES:
                    self.qtile(qt)


def build_fused2():
    nc = bass.Bass("TRN2", target_bir_lowering=False)
    P = Prog(nc)
    dr = fused_dram(P, r3=True)
    I = lambda n, s: P.dram(n, s, F32, kind="Internal")
    dr["KV"] = I("KVp", [768, KPAD + SEQ])
    dr["KVwin"] = I("KVwin", [768, SEQ])
    dr["Xown"] = I("Xown", [1024, NTOK])
    dr["Qown"] = I("Qown", [768, NTOK])
    dr["Gown"] = I("Gown", [36, NTOK])
    dr["Yown"] = I("Yown", [768, NTOK])
    dr["Rrow12"] = I("Rrow12", [12, LTOT])
    dr["topc"] = P.dram("topc4", [4, 128, 1024], F32, kind="ExternalInput")
    dr["kval"] = P.dram("kval", [128, 16], F32, kind="ExternalInput")
    dr["out"] = P.dram("xT_own", [1024, NTOK], F32, kind="ExternalOutput")
    dr["kd"] = (nc.partition_id() % 4) * 512
    P.push_scope()
    FTok2(P, "L1", dr).run()
    P.pop_scope()
    extract_own(P, dr)
    P.push_scope()
    FTok2(P, "P0", dr).run()
    P.pop_scope()
    for j in range(2):
        P.push_scope()
        FAttn2(P, dr, j).run()
        P.pop_scope()
        P.push_scope()
        FTok2(P, "L3" if j == 0 else "L5", dr).run()
        P.pop_scope()
    P.wait_all("sp", [dr["out"]])
    P.finalize()
    return nc, P


from concourse.bass_utils import run_bass_kernel_spmd


def attn_hg(inp, jlayer):
    hg = np.zeros((128, 8), np.float32)
    hg[:, 0] = np.tile(inp["b_q_gain"][jlayer], 2)
    for i in range(3):
        hg[:, 1 + i] = np.tile(inp["k_gain"][i], 2)
    return hg


def kernel(**inputs):
    inp = {k: np.ascontiguousarray(np.asarray(v, dtype=np.float32)) for k, v in inputs.items()}
    common = dict(host_consts())
    del common["topc"]
    common.update(gains=host_gains(inp), hg=host_hg(inp), convw=host_convw(inp), mem_w_kv=inp["mem_w_kv"],
                  w_out=inp["w_out"], w_up=inp["w_up"], w_down=inp["w_down"], b_w_in=inp["b_w_in"],
                  a_w_in=inp["a_w_in"], w_kv=inp["w_kv_shared"], ahg0=attn_hg(inp, 0), ahg1=attn_hg(inp, 1),
                  pos=np.ascontiguousarray(inp["cmp_pos"].transpose(0, 2, 1)), cmp_w1=inp["cmp_w1"],
                  cmp_w2=inp["cmp_w2"], rel_bias=inp["rel_bias"])
    per_b = []
    for b in range(2):
        xin = np.zeros((1024, XPAD), np.float32)
        xin[:, HALO:] = inp["x"][b].T
        per_b.append(dict(xin=xin, memT=np.ascontiguousarray(inp["mem"][b].T)))
    per_k = []
    for k in range(4):
        tc, kval = host_core_tables(k)
        per_k.append(dict(topc4=tc, kval=kval))
    maps = []
    for c in range(8):
        m = dict(common)
        m.update(per_b[c // 4])
        m.update(per_k[c % 4])
        maps.append(m)
    nc = build_fused2()[0]
    r = run_bass_kernel_spmd(nc, maps, core_ids=list(range(8))).results
    out = np.zeros((2, SEQ, 1024), np.float32)
    for c in range(8):
        b, k = c // 4, c % 4
        o = r[c]["xT_own"]
        for j in range(4):
            t0 = 512 * (4 * j + k)
            out[b, t0:t0 + 512, :] = o[:, j * 512:(j + 1) * 512].T
    return out
```
